# Optimizing a Trainium2 kernel written in Bass

```python
import jax, jax.numpy as jnp
from jax import lax
import numpy as np

D_MODEL = 2048
BATCH = 2
SEQ = 4096
DEPTH = 1

ATT_HEADS = 8
ATT_HEAD_DIM = 128
ATT_WIDTH = ATT_HEADS * ATT_HEAD_DIM
RWKV_WIDTH = D_MODEL - ATT_WIDTH
RWKV_HEAD_DIM = 64
RWKV_HEADS = RWKV_WIDTH // RWKV_HEAD_DIM
IDX_HEADS = 16
IDX_HEAD_DIM = 64
TOPK_MAX = 256
ROPE_THETA = 500000.0
ROPE_FRACTION = 4
DECAY_LORA = 96
AAA_LORA = 96
GATE_LORA = 256
D_FF = ((8 * D_MODEL // 3 + 255) // 256) * 256
Q_BLOCK = 64
NORM_EPS = 1e-6
LNX_EPS = 64e-5

ATT_SPLITS = (ATT_WIDTH, ATT_WIDTH, ATT_WIDTH, IDX_HEADS * IDX_HEAD_DIM, IDX_HEAD_DIM, IDX_HEADS)
RWKV_SPLITS = (RWKV_WIDTH, RWKV_WIDTH, RWKV_WIDTH, DECAY_LORA, AAA_LORA, GATE_LORA)
ATT_COLS = sum(ATT_SPLITS)
RWKV_COLS = sum(RWKV_SPLITS)
IN_COLS = ATT_COLS + RWKV_COLS

kernel_name = "hymba_dsa_rwkv7_adaln_layer"


def split_cols(y, sizes):
    offs = [int(o) for o in np.cumsum(sizes)[:-1]]
    return jnp.split(y, offs, axis=-1)


def rms_norm(x, gain):
    xf = x.astype(jnp.float32)
    y = xf * lax.rsqrt(jnp.mean(xf * xf, axis=-1, keepdims=True) + NORM_EPS)
    return (y * gain.astype(jnp.float32)).astype(x.dtype)


def partial_rope(x, positions):
    d = x.shape[-1]
    rot = d // ROPE_FRACTION
    half = rot // 2
    inv_freq = ROPE_THETA ** (-jnp.arange(half, dtype=jnp.float32) / half)
    ang = positions.astype(jnp.float32)[..., None] * inv_freq
    cos = jnp.cos(ang)[:, :, None, :]
    sin = jnp.sin(ang)[:, :, None, :]
    xr = x[..., :rot].astype(jnp.float32)
    x1, x2 = xr[..., :half], xr[..., half:]
    rotated = jnp.concatenate([x1 * cos - x2 * sin, x2 * cos + x1 * sin], axis=-1)
    return jnp.concatenate([rotated.astype(x.dtype), x[..., rot:]], axis=-1)


def dsa_sparse_attention(q, k, v, iq, ik, iw):
    B, S, H, D = q.shape
    topk = min(TOPK_MAX, S // 4)
    nblk = S // Q_BLOCK
    key_pos = jnp.arange(S)
    bidx = jnp.arange(B)[:, None, None]

    def blocks(t):
        return jnp.moveaxis(t.reshape(B, nblk, Q_BLOCK, *t.shape[2:]), 1, 0)

    def one_block(args):
        bi, qb, iqb, iwb = args
        qpos = bi * Q_BLOCK + jnp.arange(Q_BLOCK)
        causal = key_pos[None, None, :] <= qpos[None, :, None]
        dots = jnp.einsum('bqhd,bsd->bqhs', iqb, ik, preferred_element_type=jnp.float32)
        score = jnp.einsum('bqhs,bqh->bqs', jax.nn.relu(dots), iwb.astype(jnp.float32))
        score = jnp.where(causal, score, -jnp.inf)
        _, idx = lax.top_k(score, topk)
        valid = idx <= qpos[None, :, None]
        k_sel = k[bidx, idx]
        v_sel = v[bidx, idx]
        logits = jnp.einsum('bqhd,bqkhd->bhqk', qb, k_sel,
                            preferred_element_type=jnp.float32) * (D ** -0.5)
        logits = jnp.where(valid[:, None], logits, -jnp.inf)
        p = jax.nn.softmax(logits, axis=-1).astype(v.dtype)
        return jnp.einsum('bhqk,bqkhd->bqhd', p, v_sel)

    out = lax.map(one_block, (jnp.arange(nblk), blocks(q), blocks(iq), blocks(iw)))
    return jnp.moveaxis(out, 0, 1).reshape(B, S, H * D)


def rwkv7_time_mix(y, mu, w0, w_up, a0, a_up, g_up, k_k, k_a, r_k, lnx_g, lnx_b):
    B, S, _ = y.shape
    H, N = RWKV_HEADS, RWKV_HEAD_DIM
    f32 = jnp.float32
    y_prev = jnp.pad(y, ((0, 0), (1, 0), (0, 0)))[:, :-1]
    y = y + (y_prev - y) * mu
    r, k, v, xw, xa, xg = split_cols(y, RWKV_SPLITS)
    w_raw = (w0 + jnp.tanh(xw) @ w_up).astype(f32)
    decay = jnp.exp(-jnp.exp(-jax.nn.softplus(-w_raw) - 0.5))
    a = jax.nn.sigmoid(a0 + xa @ a_up)
    g = jax.nn.sigmoid(xg) @ g_up

    def heads(t):
        return t.reshape(B, S, H, N)

    kk = heads(k * k_k).astype(f32)
    kk = kk / jnp.maximum(jnp.sqrt(jnp.sum(kk * kk, axis=-1, keepdims=True)), 1e-12)
    k = k * (1 + (a - 1) * k_a)
    r_h, k_h, v_h, a_h, w_h = heads(r), heads(k), heads(v), heads(a), heads(decay)

    def step(state, inp):
        r_t, w_t, k_t, v_t, kk_t, a_t = inp
        sa = jnp.einsum('bhvk,bhk->bhv', state, -kk_t)
        state = (state * w_t[:, :, None, :]
                 + sa[..., None] * (kk_t * a_t)[:, :, None, :]
                 + v_t[..., None] * k_t[:, :, None, :])
        out = jnp.einsum('bhvk,bhk->bhv', state, r_t)
        return state, out

    def seq_major(t):
        return jnp.moveaxis(t.astype(f32), 1, 0)

    state0 = jnp.zeros((B, H, N, N), f32)
    _, o = lax.scan(step, state0, (seq_major(r_h), seq_major(w_h), seq_major(k_h),
                                   seq_major(v_h), seq_major(kk), seq_major(a_h)))
    o = jnp.moveaxis(o, 0, 1)
    mean = jnp.mean(o, axis=-1, keepdims=True)
    var = jnp.mean(jnp.square(o - mean), axis=-1, keepdims=True)
    o = ((o - mean) * lax.rsqrt(var + LNX_EPS)).reshape(B, S, H * N) * lnx_g + lnx_b
    bonus = jnp.sum(r_h * k_h * r_k, axis=-1, keepdims=True) * v_h
    o = o + bonus.reshape(B, S, H * N)
    return (o * g).astype(y.dtype)


def setup_inputs(seed: int = 0) -> dict:
    key = jax.random.key(seed)
    ks = jax.random.split(key, 26)
    f32 = jnp.float32
    L = DEPTH

    def nrm(k, shape, scale):
        return jax.random.normal(k, shape, f32) * scale

    offset = jax.random.randint(ks[2], (BATCH, 1), 0, 1024, dtype=jnp.int32)
    return {
        "x": nrm(ks[0], (BATCH, SEQ, D_MODEL), 1.0),
        "c": nrm(ks[1], (BATCH, D_MODEL), 1.0),
        "positions": offset + jnp.arange(SEQ, dtype=jnp.int32)[None, :],
        "w_ada": nrm(ks[3], (L, D_MODEL, 6 * D_MODEL), D_MODEL ** -0.5),
        "b_ada": nrm(ks[4], (L, 6 * D_MODEL), 0.01),
        "norm1_g": 1.0 + nrm(ks[5], (L, D_MODEL), 0.02),
        "w_in": nrm(ks[6], (L, D_MODEL, IN_COLS), D_MODEL ** -0.5),
        "q_norm_g": 1.0 + nrm(ks[7], (L, ATT_HEAD_DIM), 0.02),
        "k_norm_g": 1.0 + nrm(ks[8], (L, ATT_HEAD_DIM), 0.02),
        "rwkv_mu": jax.random.uniform(ks[9], (L, RWKV_COLS), f32),
        "rwkv_w0": nrm(ks[10], (L, RWKV_WIDTH), 0.5),
        "rwkv_w_up": nrm(ks[11], (L, DECAY_LORA, RWKV_WIDTH), DECAY_LORA ** -0.5),
        "rwkv_a0": nrm(ks[12], (L, RWKV_WIDTH), 0.1),
        "rwkv_a_up": nrm(ks[13], (L, AAA_LORA, RWKV_WIDTH), AAA_LORA ** -0.5),
        "rwkv_g_up": nrm(ks[14], (L, GATE_LORA, RWKV_WIDTH), GATE_LORA ** -0.5),
        "rwkv_k_k": 0.85 + nrm(ks[15], (L, RWKV_WIDTH), 0.02),
        "rwkv_k_a": 1.0 + nrm(ks[16], (L, RWKV_WIDTH), 0.02),
        "rwkv_r_k": nrm(ks[17], (L, RWKV_HEADS, RWKV_HEAD_DIM), 0.1),
        "rwkv_lnx_g": 1.0 + nrm(ks[18], (L, RWKV_WIDTH), 0.02),
        "rwkv_lnx_b": nrm(ks[19], (L, RWKV_WIDTH), 0.01),
        "w_out": nrm(ks[20], (L, ATT_WIDTH + RWKV_WIDTH, D_MODEL), D_MODEL ** -0.5),
        "norm2_g": 1.0 + nrm(ks[21], (L, D_MODEL), 0.02),
        "w_ffn_gate": nrm(ks[22], (L, D_MODEL, D_FF), D_MODEL ** -0.5),
        "w_ffn_up": nrm(ks[23], (L, D_MODEL, D_FF), D_MODEL ** -0.5),
        "w_ffn_down": nrm(ks[24], (L, D_FF, D_MODEL), D_FF ** -0.5),
    }


def reference(x, c, positions, w_ada, b_ada, norm1_g, w_in, q_norm_g, k_norm_g,
              rwkv_mu, rwkv_w0, rwkv_w_up, rwkv_a0, rwkv_a_up, rwkv_g_up,
              rwkv_k_k, rwkv_k_a, rwkv_r_k, rwkv_lnx_g, rwkv_lnx_b, w_out,
              norm2_g, w_ffn_gate, w_ffn_up, w_ffn_down):
    B, S, _ = x.shape
    c_act = jax.nn.silu(c)
    for l in range(DEPTH):
        mod = c_act @ w_ada[l] + b_ada[l]
        sh1, sc1, gt1, sh2, sc2, gt2 = jnp.split(mod[:, None, :], 6, axis=-1)

        h = rms_norm(x, norm1_g[l]) * (1 + sc1) + sh1
        proj = h @ w_in[l]
        att_proj, rwkv_proj = proj[..., :ATT_COLS], proj[..., ATT_COLS:]
        q, k, v, iq, ik, iw = split_cols(att_proj, ATT_SPLITS)
        q = partial_rope(rms_norm(q.reshape(B, S, ATT_HEADS, ATT_HEAD_DIM), q_norm_g[l]), positions)
        k = partial_rope(rms_norm(k.reshape(B, S, ATT_HEADS, ATT_HEAD_DIM), k_norm_g[l]), positions)
        v = v.reshape(B, S, ATT_HEADS, ATT_HEAD_DIM)
        iq = partial_rope(iq.reshape(B, S, IDX_HEADS, IDX_HEAD_DIM), positions) * (IDX_HEAD_DIM ** -0.5)
        ik = partial_rope(ik[:, :, None, :], positions)[:, :, 0, :]
        iw = iw * (IDX_HEADS ** -0.5)
        att_out = dsa_sparse_attention(q, k, v, iq, ik, iw)
        rwkv_out = rwkv7_time_mix(rwkv_proj, rwkv_mu[l], rwkv_w0[l], rwkv_w_up[l],
                                  rwkv_a0[l], rwkv_a_up[l], rwkv_g_up[l], rwkv_k_k[l],
                                  rwkv_k_a[l], rwkv_r_k[l], rwkv_lnx_g[l], rwkv_lnx_b[l])
        mixed = jnp.concatenate([att_out, rwkv_out], axis=-1) @ w_out[l]
        x = x + gt1 * mixed

        h2 = rms_norm(x, norm2_g[l]) * (1 + sc2) + sh2
        ffn = (jax.nn.silu(h2 @ w_ffn_gate[l]) * (h2 @ w_ffn_up[l])) @ w_ffn_down[l]
        x = x + gt2 * ffn
    return x
```

```python
import contextlib
import numpy as np
import concourse.bass as bass
import concourse.mybir as mybir
from concourse.bass_utils import run_bass_kernel_spmd

F32 = mybir.dt.float32
BF16 = mybir.dt.bfloat16
I32 = mybir.dt.int32
AF = mybir.ActivationFunctionType
ALU = mybir.AluOpType
AX = mybir.AxisListType

D = 2048
S = 4096
NT = 32
OWN = 1024
ENGS = ("pe", "act", "dve", "pool", "sp")
DEBUG = {}


class _Op:
    __slots__ = ("eng", "fn", "deps", "needs_inc", "is_dma", "sem", "count", "idx", "prev_same_sem")

    def __init__(self, eng, fn, is_dma):
        self.eng = eng
        self.fn = fn
        self.deps = set()
        self.needs_inc = False
        self.is_dma = is_dma
        self.sem = None
        self.count = 0
        self.prev_same_sem = None


class Prog:
    def __init__(self, nc, stack, n_dma_sems=48):
        self.nc = nc
        self.n_dma_sems = n_dma_sems
        self.eng_sem = {e: stack.enter_context(nc.semaphore("s_" + e)) for e in ENGS}
        self.dma_sems = [stack.enter_context(nc.semaphore("d%d" % i)) for i in range(n_dma_sems)]
        self.bar_sem = stack.enter_context(nc.semaphore("bar"))
        self.cnt = {e: 0 for e in ENGS}
        self.dcnt = [0] * n_dma_sems
        self.rr = 0
        self.nbar = 0
        self._reset()

    def _reset(self):
        self.ops = []
        self.last_writer = {}
        self.readers = {}

    def _record(self, op, reads, writes):
        idx = len(self.ops)
        op.idx = idx
        deps = set()
        for k in reads:
            w = self.last_writer.get(k)
            if w is not None:
                deps.add(w)
        for k in writes:
            w = self.last_writer.get(k)
            if w is not None:
                deps.add(w)
            for r in self.readers.get(k, ()):
                deps.add(r)
        deps.discard(idx)
        op.deps = deps
        self.ops.append(op)
        for k in reads:
            self.readers.setdefault(k, []).append(idx)
        for k in writes:
            self.last_writer[k] = idx
            self.readers[k] = []
        return idx

    def op(self, eng, fn, reads=(), writes=()):
        return self._record(_Op(eng, fn, False), reads, writes)

    def dma(self, queue, out, in_, reads=(), writes=(), **kw):
        def fn(e, out=out, in_=in_, kw=kw):
            return e.dma_start(out=out, in_=in_, **kw)
        return self._record(_Op(queue, fn, True), reads, writes)

    def flush(self):
        nc = self.nc
        ops = self.ops
        for o in ops:
            nd = set()
            for d in o.deps:
                p = ops[d]
                if o.eng == "pe" and p.eng == "pe" and not p.is_dma and not o.is_dma:
                    continue
                nd.add(d)
                p.needs_inc = True
            o.deps = nd
        last_of = {}
        for o in ops:
            if not o.is_dma:
                last_of[o.eng] = o
        for o in last_of.values():
            o.needs_inc = True
        dlast = [None] * self.n_dma_sems
        for o in ops:
            if o.is_dma:
                s = self.rr % self.n_dma_sems
                self.rr += 1
                o.prev_same_sem = dlast[s]
                self.dcnt[s] += 16
                o.sem = self.dma_sems[s]
                o.count = self.dcnt[s]
                dlast[s] = o.idx
            elif o.needs_inc:
                self.cnt[o.eng] += 1
                o.sem = self.eng_sem[o.eng]
                o.count = self.cnt[o.eng]
        per_eng = {e: [o for o in ops if o.eng == e] for e in ENGS}
        final = [(self.dma_sems[s], self.dcnt[s]) for s in range(self.n_dma_sems) if self.dcnt[s] > 0]
        final += [(self.eng_sem[e], self.cnt[e]) for e in ENGS if self.cnt[e] > 0]
        self.nbar += 1
        nbar = self.nbar
        bar = self.bar_sem

        def run(e_name, eng):
            waited = {}
            for o in per_eng[e_name]:
                need = {}
                for d in o.deps:
                    p = ops[d]
                    if need.get(p.sem.num, (0, None))[0] < p.count:
                        need[p.sem.num] = (p.count, p.sem)
                if o.is_dma and o.prev_same_sem is not None:
                    p = ops[o.prev_same_sem]
                    if need.get(p.sem.num, (0, None))[0] < p.count:
                        need[p.sem.num] = (p.count, p.sem)
                for key, (c, s) in need.items():
                    if waited.get(key, 0) < c:
                        eng.wait_ge(s, c)
                        waited[key] = c
                ins = o.fn(eng)
                if o.is_dma:
                    ins.then_inc(o.sem, 16)
                elif o.needs_inc:
                    ins.then_inc(o.sem, 1)
            if e_name == "sp":
                for s, c in final:
                    eng.wait_ge(s, c)
                eng.sem_inc(bar, 1)
            eng.wait_ge(bar, nbar)

        with nc.Block() as block:
            @block.tensor
            def _(e):
                run("pe", e)

            @block.scalar
            def _(e):
                run("act", e)

            @block.vector
            def _(e):
                run("dve", e)

            @block.gpsimd
            def _(e):
                run("pool", e)

            @block.sync
            def _(e):
                run("sp", e)
        self._reset()


class Ctx:
    pass


def bcast_rows(ap1d, n):
    return bass.AP(ap1d.tensor, ap1d.offset, [[0, 128], [1, n]])


def dbg_out(K, name, shape, dtype=F32):
    t = K.nc.dram_tensor(name, list(shape), dtype, kind="ExternalOutput")
    K.dbg[name] = t
    return t.ap()


def make_ident(K, P, ident):
    P.op("pool", lambda e: e.memset(ident[:], 0.0), writes=["ident"])
    P.op("pool", lambda e: e.affine_select(out=ident[:], in_=ident[:], pattern=[[-1, 128]],
                                           compare_op=ALU.not_equal, fill=1.0, base=0,
                                           channel_multiplier=1),
         reads=["ident"], writes=["ident"])


def phase0_adaln(K):
    nc, P = K.nc, K.P
    with contextlib.ExitStack() as st:
        c_sb = st.enter_context(nc.sbuf_tensor("c_sb", [128, 16], F32))
        cact = st.enter_context(nc.sbuf_tensor("cact", [128, 16], F32))
        wst = [st.enter_context(nc.sbuf_tensor("wst%d" % i, [128, 16, 512], F32)) for i in range(2)]
        modrow = st.enter_context(nc.sbuf_tensor("modrow", [1, 12288], F32))
        brow = st.enter_context(nc.sbuf_tensor("brow", [1, 12288], F32))
        ps = [st.enter_context(nc.psum_tensor("ps0_%d" % i, [1, 512], F32)) for i in range(2)]
        P.dma("sp", c_sb[:], K.c_arr, writes=["c_sb"])
        P.dma("sp", brow[:], K.b_ada.rearrange("(o n) -> o n", o=1), writes=["brow"])
        P.op("act", lambda e: e.activation(out=cact[:], in_=c_sb[:], func=AF.Silu),
             reads=["c_sb"], writes=["cact"])
        wv = K.w_ada.rearrange("(k p) n -> p k n", p=128)
        for nt in range(24):
            sl = nt % 2
            for hh in range(2):
                P.dma("sp", wst[sl][:, hh * 8:(hh + 1) * 8, :],
                      wv[:, hh * 8:(hh + 1) * 8, nt * 512:(nt + 1) * 512],
                      writes=[("wst", sl, hh)])
            for k in range(16):
                P.op("pe", lambda e, k=k, sl=sl: e.matmul(ps[sl][:, :], lhsT=cact[:, k:k + 1],
                                                         rhs=wst[sl][:, k, :], start=(k == 0), stop=(k == 15)),
                     reads=["cact", ("wst", sl, k // 8)], writes=[("ps0", sl)])
            P.op("dve", lambda e, nt=nt, sl=sl: e.tensor_tensor(
                out=modrow[0:1, nt * 512:(nt + 1) * 512], in0=ps[sl][:, :],
                in1=brow[0:1, nt * 512:(nt + 1) * 512], op=ALU.add),
                reads=[("ps0", sl), "brow"], writes=[("modrow", nt)])
        P.dma("sp", K.mod_d.rearrange("(o n) -> o n", o=1), modrow[:],
              reads=[("modrow", nt) for nt in range(24)], writes=["mod_d"])
        P.flush()


def load_mod_rows(K, P, tile, which, gain_ap=None, key=None):
    src = K.mod_d[which * D:(which + 1) * D]
    P.dma("sp", tile[:], bcast_rows(src, D), writes=[key])


def bc(ap, shape):
    return ap.to_broadcast(list(shape))


def load_weight_bf16(K, P, st_tiles, dst, c_dst, src2d, ncols, tag):
    wv = src2d.rearrange("(k p) n -> p k n", p=128)
    nk = wv.shape[1]
    engs = ["pool", "dve", "act"]
    for c0 in range(0, ncols, 512):
        n = min(512, ncols - c0)
        for k0 in range(0, nk, 4):
            kn = min(4, nk - k0)
            i = K.wcnt
            K.wcnt += 1
            sl = i % 2
            stg = st_tiles[sl]
            P.dma("sp", stg[:, 0:kn, 0:n], wv[:, k0:k0 + kn, c0:c0 + n], writes=[("wstg", sl)])
            eng = engs[i % 3]
            o = dst[:, k0:k0 + kn, c_dst + c0:c_dst + c0 + n]
            if eng == "act":
                P.op("act", lambda e, o=o, stg=stg, kn=kn, n=n: e.copy(out=o, in_=stg[:, 0:kn, 0:n]),
                     reads=[("wstg", sl)], writes=[(tag, c0, k0)])
            else:
                P.op(eng, lambda e, o=o, stg=stg, kn=kn, n=n: e.tensor_copy(out=o, in_=stg[:, 0:kn, 0:n]),
                     reads=[("wstg", sl)], writes=[(tag, c0, k0)])
    return [(tag, c0, k0) for c0 in range(0, ncols, 512) for k0 in range(0, nk, 4)]


def rope_tables(K, P, st, pos_arr, ntile, invf_att, invf_idx, tag):
    nc = K.nc
    posi = st.enter_context(nc.sbuf_tensor(tag + "posi", [128, ntile], I32))
    posf = st.enter_context(nc.sbuf_tensor(tag + "posf", [128, ntile], F32))
    iva = st.enter_context(nc.sbuf_tensor(tag + "iva", [128, 16], F32))
    ivi = st.enter_context(nc.sbuf_tensor(tag + "ivi", [128, 8], F32))
    P.dma("sp", posi[:], pos_arr, writes=[tag + "posi"])
    P.dma("sp", iva[:], invf_att, writes=[tag + "iva"])
    P.dma("sp", ivi[:], invf_idx, writes=[tag + "ivi"])
    P.op("dve", lambda e: e.tensor_copy(out=posf[:], in_=posi[:]), reads=[tag + "posi"], writes=[tag + "posf"])
    out = {}
    for nm, iv, h in (("a", iva, 16), ("i", ivi, 8)):
        u = st.enter_context(nc.sbuf_tensor(tag + "u" + nm, [128, ntile, h], F32))
        ui = st.enter_context(nc.sbuf_tensor(tag + "ui" + nm, [128, ntile, h], I32))
        uf = st.enter_context(nc.sbuf_tensor(tag + "uf" + nm, [128, ntile, h], F32))
        for fn, off in (("sin", 0.0), ("cos", 0.25)):
            tb = st.enter_context(nc.sbuf_tensor(tag + fn + nm, [128, ntile, h], F32))
            kk = tag + fn + nm
            P.op("dve", lambda e, u=u, iv=iv, h=h: e.tensor_tensor(
                out=u[:], in0=bc(posf[:].unsqueeze(2), [128, ntile, h]),
                in1=bc(iv[:].unsqueeze(1), [128, ntile, h]), op=ALU.mult),
                reads=[tag + "posf", tag + "iv" + nm], writes=[tag + "U" + nm])
            P.op("dve", lambda e, u=u, off=off: e.tensor_scalar(
                out=u[:], in0=u[:], scalar1=float(1.0 / (2 * np.pi)), scalar2=off, op0=ALU.mult, op1=ALU.add),
                reads=[tag + "U" + nm], writes=[tag + "U" + nm])
            P.op("dve", lambda e, u=u, ui=ui: e.tensor_copy(out=ui[:], in_=u[:]), reads=[tag + "U" + nm], writes=[tag + "UI" + nm])
            P.op("dve", lambda e, uf=uf, ui=ui: e.tensor_copy(out=uf[:], in_=ui[:]), reads=[tag + "UI" + nm], writes=[tag + "UF" + nm])
            P.op("dve", lambda e, u=u, uf=uf: e.tensor_tensor(out=u[:], in0=u[:], in1=uf[:], op=ALU.subtract),
                 reads=[tag + "U" + nm, tag + "UF" + nm], writes=[tag + "U" + nm])
            P.op("dve", lambda e, u=u: e.tensor_scalar(out=u[:], in0=u[:], scalar1=-0.5, scalar2=0.5,
                                                        op0=ALU.max, op1=ALU.min),
                 reads=[tag + "U" + nm], writes=[tag + "U" + nm])
            P.op("act", lambda e, u=u, tb=tb: e.activation(out=tb[:], in_=u[:], func=AF.Sin,
                                                            scale=float(2 * np.pi)),
                 reads=[tag + "U" + nm], writes=[kk])
            out[fn + nm] = (tb, kk)
    return out


def apply_rope(P, eng, x4, cos, sin, t, half, tmp, rk, wk):
    ctb, ck = cos
    stb, sk = sin
    H = x4.shape[1]
    x1 = x4[:, :, 0:half]
    x2 = x4[:, :, half:2 * half]
    cb = bc(ctb[:, t, :].unsqueeze(1), [128, H, half])
    sb = bc(stb[:, t, :].unsqueeze(1), [128, H, half])
    a, b2, c, d = tmp
    P.op(eng, lambda e: e.tensor_tensor(out=a[:, 0:H, 0:half], in0=x1, in1=cb, op=ALU.mult), reads=rk + [ck], writes=["rtmpA"])
    P.op(eng, lambda e: e.tensor_tensor(out=b2[:, 0:H, 0:half], in0=x2, in1=sb, op=ALU.mult), reads=rk + [sk], writes=["rtmpB"])
    P.op(eng, lambda e: e.tensor_tensor(out=c[:, 0:H, 0:half], in0=x2, in1=cb, op=ALU.mult), reads=rk + [ck], writes=["rtmpC"])
    P.op(eng, lambda e: e.tensor_tensor(out=d[:, 0:H, 0:half], in0=x1, in1=sb, op=ALU.mult), reads=rk + [sk], writes=["rtmpD"])
    P.op(eng, lambda e: e.tensor_tensor(out=x1, in0=a[:, 0:H, 0:half], in1=b2[:, 0:H, 0:half], op=ALU.subtract),
         reads=["rtmpA", "rtmpB", "rtmpC", "rtmpD"] + rk, writes=rk)
    P.op(eng, lambda e: e.tensor_tensor(out=x2, in0=c[:, 0:H, 0:half], in1=d[:, 0:H, 0:half], op=ALU.add),
         reads=["rtmpC", "rtmpD"] + rk, writes=rk)


def head_rmsnorm(P, x3, gain, sq, ssum, rk, wk):
    P.op("pool", lambda e: e.tensor_tensor(out=sq[:], in0=x3, in1=x3, op=ALU.mult), reads=rk, writes=[wk + "sq"])
    P.op("dve", lambda e: e.tensor_reduce(out=ssum[:, 0:8], in_=sq[:], axis=AX.X, op=ALU.add),
         reads=[wk + "sq"], writes=[wk + "s0"])
    P.op("dve", lambda e: e.tensor_scalar(out=ssum[:, 8:16], in0=ssum[:, 0:8], scalar1=1.0 / 128, scalar2=1e-6,
                                           op0=ALU.mult, op1=ALU.add), reads=[wk + "s0"], writes=[wk + "s1"])
    P.op("act", lambda e: e.activation(out=ssum[:, 16:24], in_=ssum[:, 8:16], func=AF.Sqrt),
         reads=[wk + "s1"], writes=[wk + "s2"])
    P.op("dve", lambda e: e.reciprocal(out=ssum[:, 24:32], in_=ssum[:, 16:24]), reads=[wk + "s2"], writes=[wk + "s3"])
    P.op("dve", lambda e: e.tensor_tensor(out=x3, in0=x3, in1=bc(ssum[:, 24:32].unsqueeze(2), [128, 8, 128]),
                                           op=ALU.mult), reads=rk + [wk + "s3"], writes=rk)
    P.op("pool", lambda e: e.tensor_tensor(out=x3, in0=x3, in1=bc(gain[:].unsqueeze(1), [128, 8, 128]),
                                            op=ALU.mult), reads=rk + ["gain" + wk], writes=rk)


def norm_block(K, P, T, x_src, t, G1, SH1, ident, blk_hT, ti):
    xs = t % 2
    xt, hb, ss, junk, pT = T["xt"], T["hb"], T["ss"], T["junk"], T["pT"]
    P.dma("sp", xt[xs][:], x_src[t * 128:(t + 1) * 128, :], writes=[("xt", xs)])
    P.op("act", lambda e: e.activation(out=junk[:], in_=xt[xs][:], func=AF.Square, accum_out=ss[:, 0:1]),
         reads=[("xt", xs)], writes=["junk", "ss0"])
    P.op("dve", lambda e: e.tensor_scalar(out=ss[:, 1:2], in0=ss[:, 0:1], scalar1=1.0 / D, scalar2=1e-6,
                                           op0=ALU.mult, op1=ALU.add), reads=["ss0"], writes=["ss1"])
    P.op("act", lambda e: e.activation(out=ss[:, 2:3], in_=ss[:, 1:2], func=AF.Sqrt), reads=["ss1"], writes=["ss2"])
    P.op("dve", lambda e: e.reciprocal(out=ss[:, 3:4], in_=ss[:, 2:3]), reads=["ss2"], writes=["ss3"])
    P.op("dve", lambda e: e.scalar_tensor_tensor(out=xt[xs][:], in0=xt[xs][:], scalar=ss[:, 3:4], in1=G1[:],
                                                  op0=ALU.mult, op1=ALU.mult),
         reads=[("xt", xs), "ss3", "G"], writes=[("xt", xs)])
    P.op("pool", lambda e: e.tensor_tensor(out=hb[xs][:], in0=xt[xs][:], in1=SH1[:], op=ALU.add),
         reads=[("xt", xs), "SH"], writes=[("hb", xs)])
    for half in range(2):
        for kk in range(8):
            k = half * 8 + kk
            P.op("pe", lambda e, k=k, kk=kk, half=half: e.transpose(
                out=pT[half][:, kk, :], in_=hb[xs][:, k * 128:(k + 1) * 128], identity=ident[:]),
                reads=[("hb", xs), "ident"], writes=[("pT", half)])
        o = blk_hT[:, half * 8:(half + 1) * 8, ti * 128:(ti + 1) * 128]
        if half == 0:
            P.op("act", lambda e, o=o, half=half: e.copy(out=o, in_=pT[half][:]),
                 reads=[("pT", half)], writes=[("hT", ti, half)])
        else:
            P.op("dve", lambda e, o=o, half=half: e.tensor_copy(out=o, in_=pT[half][:]),
                 reads=[("pT", half)], writes=[("hT", ti, half)])


def norm_tiles_alloc(K, st, tag):
    nc = K.nc
    T = {}
    T["xt"] = [st.enter_context(nc.sbuf_tensor(tag + "xt%d" % i, [128, D], F32)) for i in range(2)]
    T["hb"] = [st.enter_context(nc.sbuf_tensor(tag + "hb%d" % i, [128, D], BF16)) for i in range(2)]
    T["ss"] = st.enter_context(nc.sbuf_tensor(tag + "ss", [128, 4], F32))
    T["junk"] = st.enter_context(nc.sbuf_tensor(tag + "junk", [128, D], BF16))
    T["pT"] = [st.enter_context(nc.psum_tensor(tag + "pT%d" % i, [128, 8, 128], BF16)) for i in range(2)]
    return T


def load_G_SH(K, P, st, which_sh, which_sc, gain_vec, tag):
    nc = K.nc
    G = st.enter_context(nc.sbuf_tensor(tag + "G", [128, D], F32))
    SH = st.enter_context(nc.sbuf_tensor(tag + "SH", [128, D], F32))
    gtmp = st.enter_context(nc.sbuf_tensor(tag + "gtmp", [128, D], F32))
    P.dma("sp", SH[:], bcast_rows(K.mod_d[which_sh * D:(which_sh + 1) * D], D), writes=["SH"])
    P.dma("sp", G[:], bcast_rows(K.mod_d[which_sc * D:(which_sc + 1) * D], D), writes=["G"])
    P.dma("sp", gtmp[:], bcast_rows(gain_vec, D), writes=["gtmp"])
    P.op("dve", lambda e: e.scalar_tensor_tensor(out=G[:], in0=G[:], scalar=1.0, in1=gtmp[:],
                                                  op0=ALU.add, op1=ALU.mult), reads=["G", "gtmp"], writes=["G"])
    return G, SH


def phase1_kv(K):
    nc, P = K.nc, K.P
    with contextlib.ExitStack() as st:
        ident = st.enter_context(nc.sbuf_tensor("ident", [128, 128], BF16))
        make_ident(K, P, ident)
        G1, SH1 = load_G_SH(K, P, st, 0, 1, K.norm1_g, "p1")
        T = norm_tiles_alloc(K, st, "p1")
        hT = [st.enter_context(nc.sbuf_tensor("hT%d" % i, [128, 16, 512], BF16)) for i in range(2)]
        W = st.enter_context(nc.sbuf_tensor("Wkv", [128, 16, 2112], BF16))
        stg = [st.enter_context(nc.sbuf_tensor("wstg%d" % i, [128, 4, 512], F32)) for i in range(2)]
        wk_k = load_weight_bf16(K, P, stg, W, 0, K.w_in[:, 1024:2048], 1024, "Wk")
        wk_v = load_weight_bf16(K, P, stg, W, 1024, K.w_in[:, 2048:3072], 1024, "Wv")
        wk_i = load_weight_bf16(K, P, stg, W, 2048, K.w_in[:, 4096:4160], 64, "Wi")
        rt = rope_tables(K, P, st, K.pos_full, 32, K.invf_att, K.invf_idx, "rf")
        gain = st.enter_context(nc.sbuf_tensor("kgain", [128, 128], F32))
        P.dma("sp", gain[:], bcast_rows(K.k_norm_g, 128), writes=["gainK"])
        ksb = st.enter_context(nc.sbuf_tensor("ksb", [128, 8, 128], F32))
        kbf = st.enter_context(nc.sbuf_tensor("kbf", [128, 8, 128], BF16))
        sq = st.enter_context(nc.sbuf_tensor("sq", [128, 8, 128], F32))
        ssum = st.enter_context(nc.sbuf_tensor("ssum", [128, 32], F32))
        rtmp = [st.enter_context(nc.sbuf_tensor("rtmp%d" % i, [128, 8, 16], F32)) for i in range(4)]
        vsb = st.enter_context(nc.sbuf_tensor("vsb", [128, 8, 129], BF16))
        iksb = st.enter_context(nc.sbuf_tensor("iksb", [128, 1, 64], F32))
        ikbf = st.enter_context(nc.sbuf_tensor("ikbf", [128, 64], BF16))
        kTs = st.enter_context(nc.sbuf_tensor("kTs", [128, 8, 128], BF16))
        ikTs = st.enter_context(nc.sbuf_tensor("ikTs", [64, 128], BF16))
        pm = [st.enter_context(nc.psum_tensor("pm%d" % i, [128, 512], F32)) for i in range(3)]
        pk = st.enter_context(nc.psum_tensor("pk", [128, 8, 128], BF16))
        P.op("pool", lambda e: e.memset(vsb[:], 1.0), writes=["vsb"])
        for blk in range(8):
            hs = blk % 2
            for ti in range(4):
                norm_block(K, P, T, K.x_full, blk * 4 + ti, G1, SH1, ident, hT[hs], ti)
            hkeys = [("hT", ti, half) for ti in range(4) for half in range(2)]
            P.dma("sp", K.hT_d.rearrange("k p t -> p k t")[:, :, blk * 512:(blk + 1) * 512], hT[hs][:],
                  reads=hkeys, writes=[("hT_d", blk)])
            for ti in range(4):
                t = blk * 4 + ti
                hk = [("hT", ti, 0), ("hT", ti, 1)]
                for gi, (c0, n, wkeys) in enumerate([(0, 512, wk_k), (512, 512, wk_k), (1024, 512, wk_v),
                                                     (1536, 512, wk_v), (2048, 64, wk_i)]):
                    pb = pm[gi % 3]
                    for k in range(16):
                        P.op("pe", lambda e, pb=pb, k=k, c0=c0, n=n, ti=ti, hs=hs: e.matmul(
                            pb[:, 0:n], lhsT=hT[hs][:, k, ti * 128:(ti + 1) * 128], rhs=W[:, k, c0:c0 + n],
                            start=(k == 0), stop=(k == 15)), reads=hk + wkeys, writes=[("pm", gi % 3)])
                    if gi < 2:
                        P.op("act", lambda e, pb=pb, gi=gi: e.copy(out=ksb[:, gi * 4:(gi + 1) * 4, :], in_=pb[:, 0:512]),
                             reads=[("pm", gi % 3)], writes=["ksb"])
                    elif gi < 4:
                        g2 = gi - 2
                        P.op("act", lambda e, pb=pb, g2=g2: e.copy(out=vsb[:, g2 * 4:(g2 + 1) * 4, 0:128], in_=pb[:, 0:512]),
                             reads=[("pm", gi % 3)], writes=["vsb"])
                    else:
                        P.op("act", lambda e, pb=pb: e.copy(out=iksb[:, 0, :], in_=pb[:, 0:64]),
                             reads=[("pm", gi % 3)], writes=["iksb"])
                P.dma("sp", K.v_d[t * 128:(t + 1) * 128, :], vsb[:].rearrange("p h d -> p (h d)"),
                      reads=["vsb"], writes=[("v_d", t)])
                head_rmsnorm(P, ksb[:], gain, sq, ssum, ["ksb"], "K")
                apply_rope(P, "dve", ksb[:], rt["cosa"], rt["sina"], t, 16, rtmp, ["ksb"], "rK")
                P.op("act", lambda e: e.copy(out=kbf[:], in_=ksb[:]), reads=["ksb"], writes=["kbf"])
                for h in range(8):
                    P.op("pe", lambda e, h=h: e.transpose(out=pk[:, h, :], in_=kbf[:, h, :], identity=ident[:]),
                         reads=["kbf", "ident"], writes=["pk"])
                P.op("dve", lambda e: e.tensor_copy(out=kTs[:], in_=pk[:]), reads=["pk"], writes=["kTs"])
                P.dma("sp", K.kT_d.rearrange("h p t -> p h t")[:, :, t * 128:(t + 1) * 128], kTs[:],
                      reads=["kTs"], writes=[("kT_d", t)])
                apply_rope(P, "pool", iksb[:], rt["cosi"], rt["sini"], t, 8, rtmp, ["iksb"], "rI")
                P.op("act", lambda e: e.copy(out=ikbf[:], in_=iksb[:, 0, :]), reads=["iksb"], writes=["ikbf"])
                P.op("pe", lambda e: e.transpose(out=pk[0:64, 0, :], in_=ikbf[:], identity=ident[:]),
                     reads=["ikbf", "ident"], writes=["pk"])
                P.op("dve", lambda e: e.tensor_copy(out=ikTs[:], in_=pk[0:64, 0, :]), reads=["pk"], writes=["ikTs"])
                P.dma("sp", K.ikT_d[:, t * 128:(t + 1) * 128], ikTs[:], reads=["ikTs"], writes=[("ikT_d", t)])
        P.flush()

RW0 = 4176
RW_GROUPS = [(i * 128, 128) for i in range(24)] + [(3072, 96), (3168, 96), (3264, 128), (3392, 128)]


def phase1b_rwkv_proj(K):
    nc, P = K.nc, K.P
    with contextlib.ExitStack() as st:
        W = st.enter_context(nc.sbuf_tensor("Wr", [128, 16, 3520], BF16))
        stg = [st.enter_context(nc.sbuf_tensor("wstgb%d" % i, [128, 4, 512], F32)) for i in range(2)]
        hT = [st.enter_context(nc.sbuf_tensor("hTb%d" % i, [128, 16, 512], BF16)) for i in range(2)]
        ost = [st.enter_context(nc.sbuf_tensor("ost%d" % i, [128, 512], F32)) for i in range(4)]
        pm = [st.enter_context(nc.psum_tensor("pmb%d" % i, [128, 512], F32)) for i in range(4)]
        wkeys = load_weight_bf16(K, P, stg, W, 0, K.w_in[:, RW0:RW0 + 3520], 3520, "Wr")
        cnt = 0
        for blk in range(8):
            hs = blk % 2
            P.dma("sp", hT[hs][:], K.hT_d.rearrange("k p t -> p k t")[:, :, blk * 512:(blk + 1) * 512],
                  writes=[("hTb", hs)])
            for (r0, m) in RW_GROUPS:
                s4 = cnt % 4
                cnt += 1
                for k in range(16):
                    P.op("pe", lambda e, k=k, r0=r0, m=m, hs=hs, s4=s4: e.matmul(
                        pm[s4][0:m, :], lhsT=W[:, k, r0:r0 + m], rhs=hT[hs][:, k, :],
                        start=(k == 0), stop=(k == 15)), reads=[("hTb", hs)] + wkeys, writes=[("pmb", s4)])
                if cnt % 2 == 0:
                    P.op("act", lambda e, m=m, s4=s4: e.copy(out=ost[s4][0:m, :], in_=pm[s4][0:m, :]),
                         reads=[("pmb", s4)], writes=[("ost", s4)])
                else:
                    P.op("dve", lambda e, m=m, s4=s4: e.tensor_copy(out=ost[s4][0:m, :], in_=pm[s4][0:m, :]),
                         reads=[("pmb", s4)], writes=[("ost", s4)])
                P.dma("sp", K.yT_d[r0:r0 + m, blk * 512:(blk + 1) * 512], ost[s4][0:m, :],
                      reads=[("ost", s4)], writes=[("yT_d", r0, blk)])
        P.flush()


def phase2_own_proj(K):
    nc, P = K.nc, K.P
    with contextlib.ExitStack() as st:
        ident = st.enter_context(nc.sbuf_tensor("ident2", [128, 128], BF16))
        make_ident(K, P, ident)
        G1, SH1 = load_G_SH(K, P, st, 0, 1, K.norm1_g, "p2")
        T = norm_tiles_alloc(K, st, "p2")
        hT = [st.enter_context(nc.sbuf_tensor("hTo%d" % i, [128, 16, 512], BF16)) for i in range(2)]
        W = st.enter_context(nc.sbuf_tensor("Wq", [128, 16, 2064], BF16))
        stg = [st.enter_context(nc.sbuf_tensor("wstgq%d" % i, [128, 4, 512], F32)) for i in range(2)]
        wk_q = load_weight_bf16(K, P, stg, W, 0, K.w_in[:, 0:1024], 1024, "Wq")
        wk_iq = load_weight_bf16(K, P, stg, W, 1024, K.w_in[:, 3072:4096], 1024, "Wiq")
        wk_iw = load_weight_bf16(K, P, stg, W, 2048, K.w_in[:, 4160:4176], 16, "Wiw")
        rt = rope_tables(K, P, st, K.pos_own, 8, K.invf_att, K.invf_idx, "ro")
        gain = st.enter_context(nc.sbuf_tensor("qgain", [128, 128], F32))
        P.dma("sp", gain[:], bcast_rows(K.q_norm_g, 128), writes=["gainQ"])
        qsb = st.enter_context(nc.sbuf_tensor("qsb", [128, 8, 128], F32))
        qbf = st.enter_context(nc.sbuf_tensor("qbf", [128, 8, 128], BF16))
        sq = st.enter_context(nc.sbuf_tensor("sq2", [128, 8, 128], F32))
        ssum = st.enter_context(nc.sbuf_tensor("ssum2", [128, 32], F32))
        rtmp = [st.enter_context(nc.sbuf_tensor("rtmpq%d" % i, [128, 16, 16], F32)) for i in range(4)]
        iqsb = st.enter_context(nc.sbuf_tensor("iqsb", [128, 16, 64], F32))
        iqbf = st.enter_context(nc.sbuf_tensor("iqbf", [128, 16, 64], BF16))
        iwsb = st.enter_context(nc.sbuf_tensor("iwsb", [128, 16], F32))
        qTs = st.enter_context(nc.sbuf_tensor("qTs", [128, 8, 128], BF16))
        iqTs = st.enter_context(nc.sbuf_tensor("iqTs", [64, 128, 16], BF16))
        pm = [st.enter_context(nc.psum_tensor("pmq%d" % i, [128, 512], F32)) for i in range(3)]
        pk = st.enter_context(nc.psum_tensor("pkq", [128, 8, 128], BF16))
        for blk in range(2):
            hs = blk % 2
            for ti in range(4):
                norm_block(K, P, T, K.x_own, blk * 4 + ti, G1, SH1, ident, hT[hs], ti)
            for ti in range(4):
                t = blk * 4 + ti
                hk = [("hT", ti, 0), ("hT", ti, 1)]
                for gi, (c0, n, wkeys) in enumerate([(0, 512, wk_q), (512, 512, wk_q), (1024, 512, wk_iq),
                                                     (1536, 512, wk_iq), (2048, 16, wk_iw)]):
                    pb = pm[gi % 3]
                    for k in range(16):
                        P.op("pe", lambda e, pb=pb, k=k, c0=c0, n=n, ti=ti, hs=hs: e.matmul(
                            pb[:, 0:n], lhsT=hT[hs][:, k, ti * 128:(ti + 1) * 128], rhs=W[:, k, c0:c0 + n],
                            start=(k == 0), stop=(k == 15)), reads=hk + wkeys, writes=[("pmq", gi % 3)])
                    if gi < 2:
                        P.op("act", lambda e, pb=pb, gi=gi: e.copy(out=qsb[:, gi * 4:(gi + 1) * 4, :], in_=pb[:, 0:512]),
                             reads=[("pmq", gi % 3)], writes=["qsb"])
                    elif gi < 4:
                        g2 = gi - 2
                        P.op("act", lambda e, pb=pb, g2=g2: e.copy(out=iqsb[:, g2 * 8:(g2 + 1) * 8, :], in_=pb[:, 0:512]),
                             reads=[("pmq", gi % 3)], writes=["iqsb"])
                    else:
                        P.op("act", lambda e, pb=pb: e.activation(out=iwsb[:], in_=pb[:, 0:16], func=AF.Copy, scale=0.25),
                             reads=[("pmq", gi % 3)], writes=["iwsb"])
                P.dma("sp", K.iw_d[t * 128:(t + 1) * 128, :], iwsb[:], reads=["iwsb"], writes=[("iw_d", t)])
                head_rmsnorm(P, qsb[:], gain, sq, ssum, ["qsb"], "Q")
                apply_rope(P, "dve", qsb[:], rt["cosa"], rt["sina"], t, 16, rtmp, ["qsb"], "rQ")
                P.op("act", lambda e: e.copy(out=qbf[:], in_=qsb[:]), reads=["qsb"], writes=["qbf"])
                for h in range(8):
                    P.op("pe", lambda e, h=h: e.transpose(out=pk[:, h, :], in_=qbf[:, h, :], identity=ident[:]),
                         reads=["qbf", "ident"], writes=["pkq"])
                P.op("dve", lambda e: e.tensor_copy(out=qTs[:], in_=pk[:]), reads=["pkq"], writes=["qTs"])
                P.dma("sp", K.qT_d.rearrange("h p t -> p h t")[:, :, t * 128:(t + 1) * 128], qTs[:],
                      reads=["qTs"], writes=[("qT_d", t)])
                apply_rope(P, "pool", iqsb[:], rt["cosi"], rt["sini"], t, 8, rtmp, ["iqsb"], "rIQ")
                P.op("act", lambda e: e.activation(out=iqbf[:], in_=iqsb[:], func=AF.Copy, scale=0.125),
                     reads=["iqsb"], writes=["iqbf"])
                for half in range(2):
                    for hh in range(8):
                        h = half * 8 + hh
                        P.op("pe", lambda e, h=h, hh=hh: e.transpose(out=pk[0:64, hh, :], in_=iqbf[:, h, :],
                                                                      identity=ident[:]),
                             reads=["iqbf", "ident"], writes=["pkq"])
                    P.op("dve", lambda e, half=half: e.tensor_copy(
                        out=iqTs[:, :, half * 8:(half + 1) * 8].rearrange("p t h -> p h t"), in_=pk[0:64, :, :]),
                         reads=["pkq"], writes=["iqTs"])
                P.dma("sp", K.iqT_d[:, t * 128:(t + 1) * 128, :], iqTs[:], reads=["iqTs"], writes=[("iqT_d", t)])
        P.flush()


NIT = 26
SLOT_NK = [4, 8, 12, 16, 20, 24, 28, 32]


def phase3_attention(K):
    nc, P = K.nc, K.P
    with contextlib.ExitStack() as st:
        def sb(name, shape, dt):
            return st.enter_context(nc.sbuf_tensor(name, shape, dt))
        ident = sb("ident3", [128, 128], BF16)
        identf = sb("identf3", [128, 128], F32)
        make_ident(K, P, ident)
        P.op("dve", lambda e: e.tensor_copy(out=identf[:], in_=ident[:]), reads=["ident"], writes=["identf"])
        kT = sb("kTall", [128, 8, S], BF16)
        V = sb("Vall", [128, 32, 1032], BF16)
        ikT = sb("ikTall", [64, S], BF16)
        for h in range(8):
            P.dma("sp", kT[:, h, :], K.kT_d[h], writes=[("kT", h)])
        for q4 in range(4):
            P.dma("sp", V[:, q4 * 8:(q4 + 1) * 8, :],
                  K.v_d.rearrange("(t p) c -> p t c", p=128)[:, q4 * 8:(q4 + 1) * 8, :], writes=[("V", q4)])
        P.dma("sp", ikT[:], K.ikT_d, writes=["ikT"])
        kTk = [("kT", h) for h in range(8)]
        Vk = [("V", q4) for q4 in range(4)]
        Sel = sb("Sel", [128, 16, 128], BF16)
        pidx = sb("pidx", [128, 1], I32)
        pidf = sb("pidf", [128, 1], F32)
        score = sb("score", [128, S], F32)
        self_ = score[:, 0:2048].rearrange("p (g t) -> p g t", g=16)
        sk4 = [("score", q) for q in range(4)]
        P.op("pool", lambda e: e.iota(self_, pattern=[[-8, 16], [1, 128]], base=0, channel_multiplier=0, allow_small_or_imprecise_dtypes=True), writes=sk4)
        P.op("pool", lambda e: e.iota(pidx[:], pattern=[[0, 1]], base=0, channel_multiplier=1), writes=["pidx"])
        P.op("dve", lambda e: e.tensor_scalar(out=pidx[:], in0=pidx[:], scalar1=4, scalar2=None,
                                               op0=ALU.arith_shift_right), reads=["pidx"], writes=["pidx"])
        P.op("dve", lambda e: e.tensor_copy(out=pidf[:], in_=pidx[:]), reads=["pidx"], writes=["pidf"])
        P.op("dve", lambda e: e.tensor_scalar(out=Sel[:], in0=self_, scalar1=pidf[:, 0:1], scalar2=None,
                                               op0=ALU.is_equal), reads=sk4 + ["pidf"], writes=["Sel"])
        kposi = sb("kposi", [128, 512], I32)
        kposf = sb("kposf", [128, 512], F32)
        qpos = sb("qpos", [128, 8], F32)
        P.dma("sp", qpos[:], K.qpos_own, writes=["qpos"])
        iwg = sb("iwg", [128, 128], F32)
        wcol = sb("wcol", [128, 128], F32)
        P.dma("sp", iwg[:], K.iw_d.rearrange("(g t) h -> g (t h)", t=8), writes=["iwg"])
        A = [st.enter_context(nc.psum_tensor("A%d" % i, [128, 512], F32)) for i in range(2)]
        B = [st.enter_context(nc.psum_tensor("B%d" % i, [128, 512], F32)) for i in range(2)]
        C = st.enter_context(nc.psum_tensor("C3", [128, 8, 128], BF16))
        P.op("pe", lambda e: e.transpose(out=A[0][:, 0:128], in_=iwg[:], identity=identf[:]),
             reads=["iwg", "identf"], writes=[("A", 0)])
        P.op("dve", lambda e: e.tensor_copy(out=wcol[:], in_=A[0][:, 0:128]), reads=[("A", 0)], writes=["wcol"])
        mask01 = sb("mask01", [128, S], BF16)
        maskT = sb("maskT", [128, 32, 128], BF16)
        R = [sb("R%d" % i, [128, 512], BF16) for i in range(2)]
        pexp = [sb("pexp%d" % i, [128, 512], BF16) for i in range(2)]
        pmk = [sb("pmk%d" % i, [128, 512], BF16) for i in range(2)]
        iqTs = sb("iqTs3", [64, 128, 16], BF16)
        qTs = sb("qTs3", [128, 8, 128], BF16)
        att = sb("att", [128, 8, 128], BF16)
        attTs = sb("attTs", [128, 8, 128], BF16)
        bias = sb("cbias", [128, 512], F32)
        c2 = sb("c2", [128, NIT], F32)
        steps = sb("steps", [128, NIT], F32)
        sm = sb("sm3", [128, 8], F32)
        for k in range(NIT):
            P.op("pool", lambda e, k=k: e.memset(c2[:, k:k + 1], float(2.0 ** -(k + 1))), writes=["c2"])
        for i in range(8):
            nk = SLOT_NK[i]
            nb = nk // 4
            L = nk * 128
            P.dma("sp", iqTs[:], K.iqT_d[:, i * 128:(i + 1) * 128, :], writes=["iqTs"])
            P.dma("sp", qTs[:], K.qT_d.rearrange("h p t -> p h t")[:, :, i * 128:(i + 1) * 128], writes=["qTs"])
            for sbk in range(nb):
                bsl = sbk % 2
                for g in range(16):
                    a = (sbk * 16 + g) % 2
                    lhsT = iqTs[:, g * 8:(g + 1) * 8, :].rearrange("p t h -> p (t h)")
                    P.op("pe", lambda e, a=a, lhsT=lhsT, sbk=sbk: e.matmul(
                        A[a][:, :], lhsT=lhsT, rhs=ikT[:, sbk * 512:(sbk + 1) * 512], start=True, stop=True),
                        reads=["iqTs", "ikT"], writes=[("A", a)])
                    G = i * 16 + g
                    P.op("dve", lambda e, a=a, G=G: e.tensor_scalar(
                        out=R[a][:], in0=A[a][:, :], scalar1=0.0, scalar2=wcol[:, G:G + 1],
                        op0=ALU.max, op1=ALU.mult), reads=[("A", a), "wcol"], writes=[("R", a)])
                    P.op("pe", lambda e, a=a, g=g, bsl=bsl: e.matmul(
                        B[bsl][:, :], lhsT=Sel[:, g, :], rhs=R[a][:], start=(g == 0), stop=(g == 15)),
                        reads=[("R", a), "Sel"], writes=[("B", bsl)])
                P.op("act", lambda e, bsl=bsl, sbk=sbk: e.copy(out=score[:, sbk * 512:(sbk + 1) * 512], in_=B[bsl][:, :]),
                     reads=[("B", bsl)], writes=[("score", sbk)])
            sck = [("score", sbk) for sbk in range(nb)]
            P.op("dve", lambda e, L=L: e.tensor_reduce(out=sm[:, 0:1], in_=score[:, 0:L], axis=AX.X, op=ALU.max,
                                                        apply_absolute_value=True), reads=sck, writes=["sm0"])
            P.op("pool", lambda e, nb=nb: e.iota(kposi[:], pattern=[[1, 512]], base=(nb - 1) * 512, channel_multiplier=0),
                 writes=["kposi"])
            P.op("dve", lambda e: e.tensor_copy(out=kposf[:], in_=kposi[:]), reads=["kposi"], writes=["kposf"])
            P.op("dve", lambda e, i=i: e.tensor_scalar(out=bias[:], in0=kposf[:], scalar1=qpos[:, i:i + 1],
                                                        scalar2=-1e30, op0=ALU.is_gt, op1=ALU.mult),
                 reads=["kposf", "qpos"], writes=["bias"])
            P.op("dve", lambda e, nb=nb: e.tensor_tensor(out=score[:, (nb - 1) * 512:nb * 512],
                                                          in0=score[:, (nb - 1) * 512:nb * 512], in1=bias[:], op=ALU.add),
                 reads=["bias", ("score", nb - 1), "sm0"], writes=[("score", nb - 1)])
            P.op("dve", lambda e: e.tensor_scalar(out=sm[:, 1:2], in0=sm[:, 0:1], scalar1=-1.0, scalar2=-1.0,
                                                   op0=ALU.mult, op1=ALU.add), reads=["sm0"], writes=["lo"])
            P.op("dve", lambda e: e.tensor_scalar(out=sm[:, 5:6], in0=sm[:, 0:1], scalar1=2.0, scalar2=2.0,
                                                   op0=ALU.mult, op1=ALU.add), reads=["sm0"], writes=["d0"])
            P.op("dve", lambda e: e.tensor_scalar(out=steps[:], in0=c2[:], scalar1=sm[:, 5:6], scalar2=None,
                                                   op0=ALU.mult), reads=["d0", "c2"], writes=["steps"])
            for k in range(NIT):
                P.op("dve", lambda e, k=k: e.tensor_tensor(out=sm[:, 2:3], in0=sm[:, 1:2], in1=steps[:, k:k + 1],
                                                            op=ALU.add), reads=["lo", "steps"], writes=["mid"])
                P.op("dve", lambda e, L=L: e.tensor_scalar(out=mask01[:, 0:L], in0=score[:, 0:L], scalar1=sm[:, 2:3],
                                                            scalar2=None, op0=ALU.is_ge, op1=ALU.add,
                                                            accum_out=sm[:, 3:4]),
                     reads=sck + ["mid"], writes=["mask01", "cnt"])
                P.op("dve", lambda e, k=k: e.scalar_tensor_tensor(out=sm[:, 4:5], in0=sm[:, 3:4], scalar=255.5,
                                                                   in1=steps[:, k:k + 1], op0=ALU.is_ge, op1=ALU.mult),
                     reads=["cnt", "steps"], writes=["inc"])
                P.op("dve", lambda e: e.tensor_tensor(out=sm[:, 1:2], in0=sm[:, 1:2], in1=sm[:, 4:5], op=ALU.add),
                     reads=["lo", "inc"], writes=["lo"])
            P.op("dve", lambda e, L=L: e.tensor_scalar(out=mask01[:, 0:L], in0=score[:, 0:L], scalar1=sm[:, 1:2],
                                                        scalar2=None, op0=ALU.is_ge), reads=sck + ["lo"], writes=["mask01"])
            for kt in range(nk):
                P.op("pe", lambda e, kt=kt: e.transpose(out=C[:, kt % 8, :], in_=mask01[:, kt * 128:(kt + 1) * 128],
                                                         identity=ident[:]), reads=["mask01", "ident"], writes=["C"])
                if kt % 8 == 7 or kt == nk - 1:
                    k0 = (kt // 8) * 8
                    n8 = kt - k0 + 1
                    P.op("act", lambda e, k0=k0, n8=n8: e.copy(out=maskT[:, k0:k0 + n8, :], in_=C[:, 0:n8, :]),
                         reads=["C"], writes=[("maskT", k0 // 8)])
            mk = [("maskT", q) for q in range((nk + 7) // 8)]
            for h in range(8):
                bsl = h % 2
                for kg in range(nb):
                    a = (h * nb + kg) % 2
                    for j4 in range(4):
                        kt = kg * 4 + j4
                        P.op("pe", lambda e, a=a, j4=j4, kt=kt, h=h: e.matmul(
                            A[a][:, j4 * 128:(j4 + 1) * 128], lhsT=kT[:, h, kt * 128:(kt + 1) * 128], rhs=qTs[:, h, :],
                            start=True, stop=True), reads=kTk + ["qTs"], writes=[("A", a)])
                    P.op("act", lambda e, a=a: e.activation(out=pexp[a][:], in_=A[a][:, :], func=AF.Exp,
                                                             scale=float(128 ** -0.5)),
                         reads=[("A", a)], writes=[("pexp", a)])
                    P.op("pool", lambda e, a=a, kg=kg: e.tensor_tensor(
                        out=pmk[a][:], in0=pexp[a][:], in1=maskT[:, kg * 4:(kg + 1) * 4, :].rearrange("p a t -> p (a t)"),
                        op=ALU.mult), reads=[("pexp", a)] + mk, writes=[("pmk", a)])
                    for j4 in range(4):
                        kt = kg * 4 + j4
                        P.op("pe", lambda e, a=a, j4=j4, kt=kt, h=h, bsl=bsl, kg=kg: e.matmul(
                            B[bsl][:, 0:129], lhsT=pmk[a][:, j4 * 128:(j4 + 1) * 128], rhs=V[:, kt, h * 129:(h + 1) * 129],
                            start=(kg == 0 and j4 == 0), stop=(kg == nb - 1 and j4 == 3)),
                            reads=[("pmk", a)] + Vk, writes=[("B", bsl)])
                P.op("dve", lambda e, bsl=bsl: e.reciprocal(out=sm[:, 6:7], in_=B[bsl][:, 128:129]),
                     reads=[("B", bsl)], writes=["rcp"])
                P.op("dve", lambda e, bsl=bsl, h=h: e.tensor_scalar(out=att[:, h, :], in0=B[bsl][:, 0:128],
                                                                     scalar1=sm[:, 6:7], scalar2=None, op0=ALU.mult),
                     reads=[("B", bsl), "rcp"], writes=["att"])
            for h in range(8):
                P.op("pe", lambda e, h=h: e.transpose(out=C[:, h, :], in_=att[:, h, :], identity=ident[:]),
                     reads=["att", "ident"], writes=["C"])
            P.op("act", lambda e: e.copy(out=attTs[:], in_=C[:]), reads=["C"], writes=["attTs"])
            P.dma("sp", K.attT_d.rearrange("h p t -> p h t")[:, :, i * 128:(i + 1) * 128], attTs[:],
                  reads=["attTs"], writes=[("attT_d", i)])
        P.flush()

RD = BF16
NCH = 64


def tok_shift(P, dst, raw, tmp, mu_ap, rk_raw, k_tmp, k_dst, n=128):
    P.op("pool", lambda e: e.tensor_tensor(out=tmp[0:n, 1:S], in0=raw[0:n, 0:S - 1], in1=raw[0:n, 1:S], op=ALU.subtract),
         reads=[rk_raw], writes=[k_tmp])
    P.op("pool", lambda e: e.tensor_scalar(out=tmp[0:n, 0:1], in0=raw[0:n, 0:1], scalar1=-1.0, scalar2=0.0,
                                            op0=ALU.mult, op1=ALU.add), reads=[rk_raw, k_tmp], writes=[k_tmp])
    P.op("dve", lambda e: e.scalar_tensor_tensor(out=dst[0:n, :], in0=tmp[0:n, :], scalar=mu_ap, in1=raw[0:n, :],
                                                  op0=ALU.mult, op1=ALU.add), reads=[rk_raw, k_tmp], writes=[k_dst])


def phase4b_rwkv_prep(K, cts=range(8)):
    nc, P = K.nc, K.P
    with contextlib.ExitStack() as st:
        def sb(name, shape, dt):
            return st.enter_context(nc.sbuf_tensor(name, shape, dt))
        txw = sb("txw", [96, S], BF16)
        xap = sb("xap", [96, S], BF16)
        sxg = sb("sxg", [128, 2, S], BF16)
        M01 = sb("M01", [128, S], BF16)
        wup = sb("wup", [96, 1024], BF16)
        aup = sb("aup", [96, 1024], BF16)
        gup = sb("gup", [128, 2, 1024], BF16)
        wst = sb("wst4", [128, 2, 1024], F32)
        bones = sb("bones", [128, 128], BF16)
        prm = sb("prm", [128, 12, 8], F32)
        mul = sb("mul", [128, 4], F32)
        PT = sb("PT", [128, S], F32)
        KK = sb("KK", [128, S], F32)
        KP = sb("KP", [128, S], F32)
        CL = sb("CL", [128, S], F32)
        RP = sb("RP", [128, S], BF16)
        VP = sb("VP", [128, S], BF16)
        AA = sb("AA", [128, S], BF16)
        K2 = sb("K2", [128, S], BF16)
        SQb = sb("SQb", [128, S], BF16)
        OUT = [sb("OUT%d" % i, [128, S], BF16) for i in range(2)]
        PCt = sb("PCt", [128, NCH], F32)
        ps = [st.enter_context(nc.psum_tensor("ps4_%d" % i, [128, 512], F32)) for i in range(4)]
        for i, ap in enumerate(K.rw_prm):
            P.dma("sp", prm[:, i, :], ap, writes=[("prm", i)])
        prk = [("prm", i) for i in range(10)]
        P.op("dve", lambda e: e.tensor_scalar(out=prm[:, 10, :], in0=prm[:, 6, :], scalar1=-1.0, scalar2=1.0,
                                               op0=ALU.mult, op1=ALU.add), reads=prk, writes=[("prm", 10)])
        prk = prk + [("prm", 10)]
        P.dma("sp", mul[:], K.rw_mul, writes=["mul"])
        P.op("pool", lambda e: e.memset(bones[:], 0.0), writes=["bones"])
        P.op("pool", lambda e: e.memset(bones[0:64, 0:64], 1.0), reads=["bones"], writes=["bones"])
        P.op("pool", lambda e: e.memset(bones[64:128, 64:128], 1.0), reads=["bones"], writes=["bones"])
        P.op("pool", lambda e: e.iota(PT[:].rearrange("p (c t) -> p c t", t=64), pattern=[[0, NCH], [1, 64]], base=0,
                                      channel_multiplier=0, allow_small_or_imprecise_dtypes=True), writes=["PT"])
        P.op("dve", lambda e: e.tensor_scalar(out=M01[:], in0=PT[:], scalar1=0.5, scalar2=None, op0=ALU.is_gt),
             reads=["PT"], writes=["M01"])
        P.dma("sp", wst[0:96, 0, :], K.rw_w_up, writes=["wst"])
        P.op("act", lambda e: e.copy(out=wup[:], in_=wst[0:96, 0, :]), reads=["wst"], writes=["wup"])
        P.dma("sp", wst[0:96, 1, :], K.rw_a_up, reads=[], writes=["wst1"])
        P.op("act", lambda e: e.copy(out=aup[:], in_=wst[0:96, 1, :]), reads=["wst1"], writes=["aup"])
        P.dma("sp", wst[:, :, :], K.rw_g_up.rearrange("(c p) n -> p c n", p=128), reads=[], writes=["wst", "wst1"])
        P.op("act", lambda e: e.copy(out=gup[:], in_=wst[:]), reads=["wst", "wst1"], writes=["gup"])
        for (r0, n, mcol, func, dst, kd) in ((3072, 96, 0, AF.Tanh, txw[:, :], "txw"), (3168, 96, 1, AF.Copy, xap[:, :], "xap"),
                                             (3264, 128, 2, AF.Sigmoid, sxg[:, 0, :], "sxg0"),
                                             (3392, 128, 3, AF.Sigmoid, sxg[:, 1, :], "sxg1")):
            P.dma("sp", PT[0:n, :], K.yT_d[r0:r0 + n, :], writes=["PT"])
            tok_shift(P, KP, PT, KK, mul[0:n, mcol:mcol + 1], "PT", "KK", "KP", n=n)
            P.op("act", lambda e, n=n, func=func, dst=dst: e.activation(out=dst, in_=KP[0:n, :], func=func),
                 reads=["KP"], writes=[kd])
        lk = ["txw", "xap", "sxg0", "sxg1"]
        oc = 0
        for ct in cts:
            c0 = ct * 128
            P.dma("sp", PT[:], K.yT_d[c0:c0 + 128, :], writes=["PT"])
            tok_shift(P, RP, PT, KK, prm[:, 0, ct:ct + 1], "PT", "KK", "RP")
            P.dma("sp", PT[:], K.yT_d[1024 + c0:1024 + c0 + 128, :], writes=["PT"])
            tok_shift(P, KP, PT, KK, prm[:, 1, ct:ct + 1], "PT", "KK", "KP")
            P.dma("sp", PT[:], K.yT_d[2048 + c0:2048 + c0 + 128, :], writes=["PT"])
            tok_shift(P, VP, PT, KK, prm[:, 2, ct:ct + 1], "PT", "KK", "VP")
            P.dma("sp", K.vb_d[c0:c0 + 128, :], VP[:], reads=["VP"], writes=[("vb_d", ct)])
            for blk in range(8):
                bs = slice(blk * 512, (blk + 1) * 512)
                p0, p1, p2 = ps[0], ps[1], ps[2]
                P.op("pe", lambda e, bs=bs, c0=c0: e.matmul(ps[0][:, :], lhsT=wup[:, c0:c0 + 128], rhs=txw[:, bs],
                                                             start=True, stop=True), reads=["wup", "txw"], writes=[("ps4", 0)])
                P.op("act", lambda e, bs=bs, ct=ct: e.activation(out=CL[:, bs], in_=ps[0][:, :], func=AF.Sigmoid,
                                                                  bias=prm[:, 3, ct:ct + 1]),
                     reads=[("ps4", 0)] + prk, writes=["CL"])
                P.op("pe", lambda e, bs=bs, c0=c0: e.matmul(ps[1][:, :], lhsT=aup[:, c0:c0 + 128], rhs=xap[:, bs],
                                                             start=True, stop=True), reads=["aup", "xap"], writes=[("ps4", 1)])
                P.op("act", lambda e, bs=bs, ct=ct: e.activation(out=AA[:, bs], in_=ps[1][:, :], func=AF.Sigmoid,
                                                                  bias=prm[:, 4, ct:ct + 1]),
                     reads=[("ps4", 1)] + prk, writes=["AA"])
                for cc in range(2):
                    P.op("pe", lambda e, bs=bs, c0=c0, cc=cc: e.matmul(ps[2][:, :], lhsT=gup[:, cc, c0:c0 + 128],
                                                                       rhs=sxg[:, cc, bs], start=(cc == 0), stop=(cc == 1)),
                         reads=["gup", "sxg0", "sxg1"], writes=[("ps4", 2)])
                o = OUT[oc % 2]
                P.op("dve", lambda e, bs=bs, o=o: e.tensor_copy(out=o[:, bs], in_=ps[2][:, :]),
                     reads=[("ps4", 2)], writes=[("OUT", oc % 2)])
            P.dma("sp", K.G_d[c0:c0 + 128, :], OUT[oc % 2][:], reads=[("OUT", oc % 2)], writes=[("G_d", ct)])
            oc += 1
            P.op("dve", lambda e: e.tensor_scalar(out=CL[:], in0=CL[:], scalar1=-0.6065306597126334, scalar2=None,
                                                   op0=ALU.mult), reads=["CL"], writes=["CL"])
            P.op("dve", lambda e, ct=ct: e.tensor_scalar(out=KK[:], in0=KP[:], scalar1=prm[:, 5, ct:ct + 1], scalar2=None,
                                                          op0=ALU.mult), reads=["KP"] + prk, writes=["KK"])
            P.op("act", lambda e: e.activation(out=SQb[:], in_=KK[:], func=AF.Square), reads=["KK"], writes=["SQb"])
            for blk in range(8):
                bs = slice(blk * 512, (blk + 1) * 512)
                P.op("pe", lambda e, bs=bs: e.matmul(ps[3][:, :], lhsT=bones[:], rhs=SQb[:, bs], start=True, stop=True),
                     reads=["bones", "SQb"], writes=[("ps4", 3)])
                P.op("act", lambda e, bs=bs: e.activation(out=PT[:, bs], in_=ps[3][:, :], func=AF.Sqrt),
                     reads=[("ps4", 3)], writes=["PT"])
            P.op("dve", lambda e: e.tensor_scalar(out=PT[:], in0=PT[:], scalar1=1e-12, scalar2=None, op0=ALU.max),
                 reads=["PT"], writes=["PT"])
            P.op("dve", lambda e: e.reciprocal(out=PT[:], in_=PT[:]), reads=["PT"], writes=["PT"])
            P.op("dve", lambda e: e.tensor_tensor(out=KK[:], in0=KK[:], in1=PT[:], op=ALU.mult), reads=["KK", "PT"], writes=["KK"])
            P.op("dve", lambda e, ct=ct: e.tensor_scalar(out=PT[:], in0=AA[:], scalar1=prm[:, 6, ct:ct + 1],
                                                          scalar2=prm[:, 10, ct:ct + 1], op0=ALU.mult, op1=ALU.add),
                 reads=["AA", "PT"] + prk, writes=["PT"])
            P.op("dve", lambda e: e.tensor_tensor(out=K2[:], in0=KP[:], in1=PT[:], op=ALU.mult), reads=["KP", "PT"], writes=["K2"])
            P.op("dve", lambda e, ct=ct: e.scalar_tensor_tensor(out=SQb[:], in0=RP[:], scalar=prm[:, 7, ct:ct + 1], in1=K2[:],
                                                                 op0=ALU.mult, op1=ALU.mult),
                 reads=["RP", "K2", "SQb"] + prk, writes=["SQb"])
            o = OUT[oc % 2]
            for blk in range(8):
                bs = slice(blk * 512, (blk + 1) * 512)
                P.op("pe", lambda e, bs=bs: e.matmul(ps[3][:, :], lhsT=bones[:], rhs=SQb[:, bs], start=True, stop=True),
                     reads=["bones", "SQb"], writes=[("ps4", 3)])
                P.op("dve", lambda e, bs=bs, o=o: e.tensor_tensor(out=o[:, bs], in0=ps[3][:, :], in1=VP[:, bs], op=ALU.mult),
                     reads=[("ps4", 3), "VP"], writes=[("OUT", oc % 2)])
            P.dma("sp", K.BON_d[c0:c0 + 128, :], o[:], reads=[("OUT", oc % 2)], writes=[("BON_d", ct)])
            oc += 1
            P.op("dve", lambda e: e.tensor_tensor_scan(out=PT[:], data0=M01[:], data1=CL[:], initial=0.0,
                                                        op0=ALU.mult, op1=ALU.add), reads=["M01", "CL", "PT"], writes=["PT"])
            P.op("pool", lambda e: e.tensor_tensor(out=CL[:], in0=PT[:], in1=CL[:], op=ALU.subtract),
                 reads=["PT", "CL"], writes=["CL"])
            P.op("act", lambda e: e.activation(out=CL[:], in_=CL[:], func=AF.Exp), reads=["CL"], writes=["CL"])
            v3 = lambda t: t[:].rearrange("p (c t) -> p c t", t=64)
            o = OUT[oc % 2]
            P.op("dve", lambda e, o=o: e.scalar_tensor_tensor(out=o[:], in0=KK[:], scalar=-1.0, in1=CL[:],
                                                               op0=ALU.mult, op1=ALU.mult),
                 reads=["KK", "CL"], writes=[("OUT", oc % 2)])
            P.dma("sp", K.AH_d[c0:c0 + 128, :], o[:], reads=[("OUT", oc % 2)], writes=[("AH_d", ct)])
            oc += 1
            P.op("act", lambda e: e.activation(out=CL[:], in_=PT[:], func=AF.Exp), reads=["PT", "CL"], writes=["CL"])
            o = OUT[oc % 2]
            P.op("dve", lambda e, o=o: e.tensor_tensor(out=o[:], in0=RP[:], in1=CL[:], op=ALU.mult),
                 reads=["RP", "CL"], writes=[("OUT", oc % 2)])
            P.dma("sp", K.RH_d[c0:c0 + 128, :], o[:], reads=[("OUT", oc % 2)], writes=[("RH_d", ct)])
            oc += 1
            P.op("pool", lambda e: e.tensor_copy(out=PCt[:], in_=v3(CL)[:, :, 63]), reads=["CL"], writes=["PCt"])
            P.dma("sp", K.PC_d[c0:c0 + 128, :], PCt[:], reads=["PCt"], writes=[("PC_d", ct)])
            P.op("act", lambda e: e.activation(out=PT[:], in_=PT[:], func=AF.Exp, scale=-1.0), reads=["PT"], writes=["PT"])
            o = OUT[oc % 2]
            P.op("dve", lambda e, o=o: e.tensor_tensor(out=o[:], in0=K2[:], in1=PT[:], op=ALU.mult),
                 reads=["K2", "PT"], writes=[("OUT", oc % 2)])
            P.dma("sp", K.KH_d[c0:c0 + 128, :], o[:], reads=[("OUT", oc % 2)], writes=[("KH_d", ct)])
            oc += 1
            P.op("dve", lambda e: e.tensor_tensor(out=KK[:], in0=KK[:], in1=AA[:], op=ALU.mult), reads=["KK", "AA"], writes=["KK"])
            o = OUT[oc % 2]
            P.op("dve", lambda e, o=o: e.tensor_tensor(out=o[:], in0=KK[:], in1=PT[:], op=ALU.mult),
                 reads=["KK", "PT"], writes=[("OUT", oc % 2)])
            P.dma("sp", K.BH_d[c0:c0 + 128, :], o[:], reads=[("OUT", oc % 2)], writes=[("BH_d", ct)])
            oc += 1
        P.flush()

def phase4c_rwkv_scan(K, heads=range(16)):
    nc, P = K.nc, K.P
    with contextlib.ExitStack() as st:
        def sb(name, shape, dt):
            return st.enter_context(nc.sbuf_tensor(name, shape, dt))
        ident = sb("ident4", [128, 128], BF16)
        make_ident(K, P, ident)
        MaskG = sb("MaskG", [64, 4, 128], F32)
        MaskX = sb("MaskX", [64, 8, 64], F32)
        I8 = sb("I8", [64, 64], F32)
        ones = sb("ones4", [64, 64], F32)
        P.op("pool", lambda e: e.memset(ones[:], 1.0), writes=["ones"])
        for a in range(4):
            for cq in range(2):
                P.op("pool", lambda e, cq=cq, a=a: e.affine_select(
                    out=MaskG[:, a, cq * 64:(cq + 1) * 64], in_=ones[:], pattern=[[1, 64]],
                    compare_op=(ALU.is_gt if cq == 0 else ALU.is_ge), fill=0.0, base=0, channel_multiplier=-1),
                    reads=["ones"], writes=["MaskG"])
        for a in range(8):
            P.op("pool", lambda e, a=a: e.affine_select(out=MaskX[:, a, :], in_=ones[:], pattern=[[-1, 64]],
                                                         compare_op=ALU.is_gt, fill=0.0, base=0, channel_multiplier=1),
                 reads=["ones"], writes=["MaskX"])
        P.op("dve", lambda e: e.tensor_copy(out=I8[:], in_=ident[0:64, 0:64]), reads=["ident"], writes=["I8"])
        AH = sb("AH", [64, S], RD)
        RH = sb("RH", [64, S], RD)
        BH = sb("BH", [64, S], RD)
        KH = sb("KH", [64, S], RD)
        vb = sb("vb", [64, S], BF16)
        PC = sb("PC", [64, NCH], F32)
        ARh = sb("ARh", [64, NCH, 128], RD)
        BKh = sb("BKh", [64, NCH, 128], RD)
        GmB = sb("GmB", [64, NCH, 128], RD)
        GmK = sb("GmK", [64, NCH, 128], RD)
        Btok = sb("Btok", [64, NCH, 64], RD)
        Ktok = sb("Ktok", [64, NCH, 64], RD)
        Vtok = sb("Vtok", [64, NCH, 64], RD)
        X0 = sb("X0", [64, NCH, 64], RD)
        Pm = sb("Pm", [64, NCH, 64], RD)
        oT = sb("oT", [64, S], F32)
        Ast = sb("Ast", [64, 64], F32)
        Abf = sb("Abf", [64, 64], RD)
        Tt = sb("Tt", [64, 64], F32)
        Xs = sb("Xs", [64, 64], RD)
        Us = sb("Us", [64, 64], RD)
        PSb = st.enter_context(nc.psum_tensor("PSb", [128, 1024], BF16))
        PS = [st.enter_context(nc.psum_tensor("PS%d" % i, [128, 512], F32)) for i in range(7)]
        v3 = lambda t: t[:].rearrange("p (c t) -> p c t", t=64)
        Nb = [v3(AH), v3(RH)]
        Xb = [v3(BH), v3(KH)]
        Nk = ["AH", "RH"]
        Xk = ["BH", "KH"]
        for hd in heads:
            r0 = hd * 64
            P.dma("sp", AH[:], K.AH_d[r0:r0 + 64, :], writes=["AH"])
            P.dma("sp", RH[:], K.RH_d[r0:r0 + 64, :], writes=["RH"])
            P.dma("sp", BH[:], K.BH_d[r0:r0 + 64, :], writes=["BH"])
            P.dma("sp", KH[:], K.KH_d[r0:r0 + 64, :], writes=["KH"])
            P.dma("sp", vb[:], K.vb_d[r0:r0 + 64, :], writes=["vb"])
            P.dma("sp", PC[:], K.PC_d[r0:r0 + 64, :], writes=["PC"])
            P.op("dve", lambda e: e.tensor_copy(out=ARh[:, :, 0:64], in_=v3(AH)), reads=["AH"], writes=["ARh"])
            P.op("pool", lambda e: e.tensor_copy(out=ARh[:, :, 64:128], in_=v3(RH)), reads=["RH"], writes=["ARh"])
            P.op("dve", lambda e: e.tensor_copy(out=BKh[:, :, 0:64], in_=v3(BH)), reads=["BH"], writes=["BKh"])
            P.op("pool", lambda e: e.tensor_copy(out=BKh[:, :, 64:128], in_=v3(KH)), reads=["KH"], writes=["BKh"])
            for (src, srck, col0, dst, dk) in ((BKh, "BKh", 0, Btok, "Btok"), (BKh, "BKh", 64, Ktok, "Ktok"), (None, "vb", 0, Vtok, "Vtok")):
                for c16 in range(0, NCH, 16):
                    for cc in range(16):
                        c = c16 + cc
                        in_ = vb[:, c * 64:(c + 1) * 64] if src is None else src[:, c, col0:col0 + 64]
                        P.op("pe", lambda e, cc=cc, in_=in_: e.transpose(out=PSb[0:64, cc * 64:(cc + 1) * 64], in_=in_,
                                                                         identity=ident[0:64, 0:64]),
                             reads=[srck, "ident"], writes=["PSb"])
                    P.op("act", lambda e, c16=c16, dst=dst: e.copy(out=dst[:, c16:c16 + 16, :].rearrange("p c k -> p (c k)"),
                                                                    in_=PSb[0:64, :]), reads=["PSb"], writes=[dk])
            gi = 0
            for (col0, dst, dk) in ((0, GmB, "GmB"), (64, GmK, "GmK")):
                for c4 in range(0, NCH, 4):
                    b = gi % 2
                    gi += 1
                    for cc in range(4):
                        c = c4 + cc
                        P.op("pe", lambda e, c=c, cc=cc, b=b, col0=col0: e.matmul(
                            PS[b][0:64, cc * 128:(cc + 1) * 128], lhsT=BKh[:, c, col0:col0 + 64], rhs=ARh[:, c, :],
                            start=True, stop=True), reads=["BKh", "ARh"], writes=[("PS", b)])
                    P.op("dve", lambda e, c4=c4, dst=dst, b=b: e.tensor_tensor(
                        out=dst[:, c4:c4 + 4, :], in0=PS[b][0:64, :].rearrange("p (a t) -> p a t", t=128), in1=MaskG[:],
                        op=ALU.mult), reads=[("PS", b), "MaskG"], writes=[dk])
            for c8 in range(0, NCH, 8):
                for cc in range(8):
                    c = c8 + cc
                    P.op("pe", lambda e, c=c, cc=cc: e.matmul(PS[2][0:64, cc * 64:(cc + 1) * 64], lhsT=ARh[:, c, 0:64],
                                                               rhs=BKh[:, c, 0:64], start=True, stop=True),
                         reads=["ARh", "BKh"], writes=[("PS", 2)])
                P.op("dve", lambda e, c8=c8: e.tensor_tensor(
                    out=X0[:, c8:c8 + 8, :], in0=PS[2][0:64, :].rearrange("p (a t) -> p a t", t=64), in1=MaskX[:],
                    op=ALU.mult), reads=[("PS", 2), "MaskX"], writes=["X0"])
            N0 = GmB[:, :, 0:64]
            P.op("pool", lambda e, N0=N0: e.tensor_tensor(out=Pm[:], in0=N0, in1=I8[:].unsqueeze(1).to_broadcast([64, NCH, 64]),
                                                          op=ALU.add), reads=["GmB", "I8"], writes=["Pm"])
            curN, curNk = N0, "GmB"
            curX, curXk = X0[:], "X0"
            for lvl in range(1, 6):
                nX, nXk = Xb[lvl % 2], Xk[lvl % 2]
                nN, nNk = Nb[lvl % 2], Nk[lvl % 2]
                for c8 in range(0, NCH, 8):
                    for cc in range(8):
                        c = c8 + cc
                        P.op("pe", lambda e, c=c, cc=cc, curN=curN, curX=curX: e.matmul(
                            PS[3][0:64, cc * 64:(cc + 1) * 64], lhsT=curN[:, c, :], rhs=curX[:, c, :], start=True, stop=True),
                            reads=[curNk, curXk], writes=[("PS", 3)])
                    P.op("act", lambda e, c8=c8, nX=nX: e.copy(out=nX[:, c8:c8 + 8, :],
                                                                in_=PS[3][0:64, :].rearrange("p (a t) -> p a t", t=64)),
                         reads=[("PS", 3)], writes=[nXk])
                    if lvl < 5:
                        for cc in range(8):
                            c = c8 + cc
                            P.op("pe", lambda e, c=c, cc=cc, curN=curN, curX=curX: e.matmul(
                                PS[4][0:64, cc * 64:(cc + 1) * 64], lhsT=curX[:, c, :], rhs=curN[:, c, :], start=True, stop=True),
                                reads=[curNk, curXk], writes=[("PS", 4)])
                        P.op("dve", lambda e, c8=c8, nN=nN: e.tensor_copy(out=nN[:, c8:c8 + 8, :],
                                                                           in_=PS[4][0:64, :].rearrange("p (a t) -> p a t", t=64)),
                             reads=[("PS", 4)], writes=[nNk])
                    for cc in range(8):
                        c = c8 + cc
                        P.op("pe", lambda e, c=c, cc=cc, nX=nX: e.matmul(
                            PS[5][0:64, cc * 64:(cc + 1) * 64], lhsT=nX[:, c, :], rhs=Pm[:, c, :], start=True, stop=True),
                            reads=[nXk, "Pm"], writes=[("PS", 5)])
                    P.op("dve", lambda e, c8=c8: e.tensor_tensor(
                        out=Pm[:, c8:c8 + 8, :], in0=PS[5][0:64, :].rearrange("p (a t) -> p a t", t=64),
                        in1=Pm[:, c8:c8 + 8, :], op=ALU.add), reads=[("PS", 5), "Pm"], writes=["Pm"])
                curN, curNk, curX, curXk = nN, nNk, nX, nXk
            P.op("pool", lambda e: e.memset(Ast[:], 0.0), writes=["Ast"])
            P.op("pool", lambda e: e.memset(Abf[:], 0.0), writes=["Abf"])
            for c in range(NCH):
                P.op("pe", lambda e, c=c: e.matmul(PS[0][0:64, 0:64], lhsT=ARh[:, c, 0:64], rhs=Abf[:], start=True, stop=False),
                     reads=["ARh", "Abf"], writes=[("PS", 0)])
                P.op("pe", lambda e, c=c: e.matmul(PS[0][0:64, 0:64], lhsT=GmK[:, c, 0:64], rhs=Vtok[:, c, :], start=False, stop=True),
                     reads=["GmK", "Vtok"], writes=[("PS", 0)])
                P.op("act", lambda e: e.copy(out=Xs[:], in_=PS[0][0:64, 0:64]), reads=[("PS", 0)], writes=["Xs"])
                P.op("pe", lambda e, c=c: e.matmul(PS[1][0:64, 0:64], lhsT=Pm[:, c, :], rhs=Xs[:], start=True, stop=True),
                     reads=["Pm", "Xs"], writes=[("PS", 1)])
                P.op("dve", lambda e: e.tensor_copy(out=Us[:], in_=PS[1][0:64, 0:64]), reads=[("PS", 1)], writes=["Us"])
                P.op("pe", lambda e, c=c: e.matmul(PS[6][0:64, 0:64], lhsT=Btok[:, c, :], rhs=Us[:], start=True, stop=False),
                     reads=["Btok", "Us"], writes=[("PS", 6)])
                P.op("pe", lambda e, c=c: e.matmul(PS[6][0:64, 0:64], lhsT=Ktok[:, c, :], rhs=Vtok[:, c, :], start=False, stop=True),
                     reads=["Ktok", "Vtok"], writes=[("PS", 6)])
                ob = 2 + (c % 2)
                P.op("pe", lambda e, c=c, ob=ob: e.matmul(PS[ob][0:64, 0:64], lhsT=Abf[:], rhs=ARh[:, c, 64:128], start=True, stop=False),
                     reads=["Abf", "ARh"], writes=[("PS", ob)])
                P.op("pe", lambda e, c=c, ob=ob: e.matmul(PS[ob][0:64, 0:64], lhsT=Us[:], rhs=GmB[:, c, 64:128], start=False, stop=False),
                     reads=["Us", "GmB"], writes=[("PS", ob)])
                P.op("pe", lambda e, c=c, ob=ob: e.matmul(PS[ob][0:64, 0:64], lhsT=Vtok[:, c, :], rhs=GmK[:, c, 64:128], start=False, stop=True),
                     reads=["Vtok", "GmK"], writes=[("PS", ob)])
                P.op("dve", lambda e: e.tensor_tensor(out=Tt[:], in0=PS[6][0:64, 0:64], in1=Ast[:], op=ALU.add),
                     reads=[("PS", 6), "Ast"], writes=["Tt"])
                P.op("act", lambda e, c=c: e.activation(out=Abf[:], in_=Tt[:], func=AF.Copy, scale=PC[:, c:c + 1]),
                     reads=["Tt", "PC"], writes=["Abf"])
                P.op("dve", lambda e, c=c: e.tensor_scalar(out=Ast[:], in0=Tt[:], scalar1=PC[:, c:c + 1], scalar2=None, op0=ALU.mult),
                     reads=["Tt", "PC"], writes=["Ast"])
                P.op("act", lambda e, c=c, ob=ob: e.copy(out=oT[:, c * 64:(c + 1) * 64], in_=PS[ob][0:64, 0:64]),
                     reads=[("PS", ob)], writes=[("oT", c // 8)])
            P.dma("sp", K.oT_d[r0:r0 + 64, :], oT[:], reads=[("oT", q) for q in range(8)], writes=[("oT_d", hd)])
        P.flush()

def phase4d_rwkv_post(K, cts=range(8)):
    nc, P = K.nc, K.P
    with contextlib.ExitStack() as st:
        def sb(name, shape, dt):
            return st.enter_context(nc.sbuf_tensor(name, shape, dt))
        ident = sb("ident4d", [128, 128], BF16)
        make_ident(K, P, ident)
        bonesf = sb("bonesf", [128, 128], F32)
        P.op("pool", lambda e: e.memset(bonesf[:], 0.0), writes=["bonesf"])
        P.op("pool", lambda e: e.memset(bonesf[0:64, 0:64], 1.0), reads=["bonesf"], writes=["bonesf"])
        P.op("pool", lambda e: e.memset(bonesf[64:128, 64:128], 1.0), reads=["bonesf"], writes=["bonesf"])
        prm = sb("prm4d", [128, 2, 8], F32)
        P.dma("sp", prm[:, 0, :], K.rw_prm[8], writes=["prm0"])
        P.dma("sp", prm[:, 1, :], K.rw_prm[9], writes=["prm1"])
        o = sb("o4d", [128, S], F32)
        osq = sb("osq", [128, S], F32)
        bon = sb("bon", [128, S], BF16)
        gg = sb("gg", [128, S], BF16)
        Mb = [sb("Mb%d" % i, [128, 512], F32) for i in range(2)]
        Vb = [sb("Vb%d" % i, [128, 512], F32) for i in range(2)]
        Yb = [sb("Yb%d" % i, [128, 512], F32) for i in range(2)]
        Ob = [sb("Ob%d" % i, [128, 512], BF16) for i in range(2)]
        Tk = [sb("Tk%d" % i, [128, 4, 128], BF16) for i in range(2)]
        ps = [st.enter_context(nc.psum_tensor("p4d_%d" % i, [128, 512], F32)) for i in range(4)]
        pst = [st.enter_context(nc.psum_tensor("p4dt_%d" % i, [128, 4, 128], BF16)) for i in range(2)]
        it = 0
        for ct in cts:
            c0 = ct * 128
            P.dma("sp", o[:], K.oT_d[c0:c0 + 128, :], writes=["o"])
            P.dma("sp", bon[:], K.BON_d[c0:c0 + 128, :], writes=["bon"])
            P.dma("sp", gg[:], K.G_d[c0:c0 + 128, :], writes=["gg"])
            P.op("act", lambda e: e.activation(out=osq[:], in_=o[:], func=AF.Square), reads=["o"], writes=["osq"])
            for blk in range(8):
                s2 = it % 2
                it += 1
                bs = slice(blk * 512, (blk + 1) * 512)
                P.op("pe", lambda e, bs=bs, s2=s2: e.matmul(ps[s2][:, :], lhsT=bonesf[:], rhs=o[:, bs], start=True, stop=True),
                     reads=["bonesf", "o"], writes=[("p4d", s2)])
                P.op("pe", lambda e, bs=bs, s2=s2: e.matmul(ps[2 + s2][:, :], lhsT=bonesf[:], rhs=osq[:, bs], start=True, stop=True),
                     reads=["bonesf", "osq"], writes=[("p4d", 2 + s2)])
                P.op("act", lambda e, s2=s2: e.activation(out=Mb[s2][:], in_=ps[s2][:, :], func=AF.Copy, scale=1.0 / 64),
                     reads=[("p4d", s2)], writes=[("Mb", s2)])
                P.op("pool", lambda e, s2=s2: e.tensor_tensor(out=Vb[s2][:], in0=Mb[s2][:], in1=Mb[s2][:], op=ALU.mult),
                     reads=[("Mb", s2)], writes=[("Vb", s2)])
                P.op("dve", lambda e, s2=s2: e.scalar_tensor_tensor(out=Vb[s2][:], in0=ps[2 + s2][:, :], scalar=1.0 / 64, in1=Vb[s2][:],
                                                                     op0=ALU.mult, op1=ALU.subtract),
                     reads=[("p4d", 2 + s2), ("Vb", s2)], writes=[("Vb", s2)])
                P.op("dve", lambda e, s2=s2: e.tensor_scalar(out=Vb[s2][:], in0=Vb[s2][:], scalar1=64e-5, scalar2=None, op0=ALU.add),
                     reads=[("Vb", s2)], writes=[("Vb", s2)])
                P.op("act", lambda e, s2=s2: e.activation(out=Vb[s2][:], in_=Vb[s2][:], func=AF.Sqrt),
                     reads=[("Vb", s2)], writes=[("Vb", s2)])
                P.op("dve", lambda e, s2=s2: e.reciprocal(out=Vb[s2][:], in_=Vb[s2][:]), reads=[("Vb", s2)], writes=[("Vb", s2)])
                P.op("pool", lambda e, s2=s2, bs=bs: e.tensor_tensor(out=Yb[s2][:], in0=o[:, bs], in1=Mb[s2][:], op=ALU.subtract),
                     reads=["o", ("Mb", s2)], writes=[("Yb", s2)])
                P.op("dve", lambda e, s2=s2: e.tensor_tensor(out=Yb[s2][:], in0=Yb[s2][:], in1=Vb[s2][:], op=ALU.mult),
                     reads=[("Yb", s2), ("Vb", s2)], writes=[("Yb", s2)])
                P.op("dve", lambda e, s2=s2, ct=ct: e.tensor_scalar(out=Yb[s2][:], in0=Yb[s2][:], scalar1=prm[:, 0, ct:ct + 1],
                                                                     scalar2=prm[:, 1, ct:ct + 1], op0=ALU.mult, op1=ALU.add),
                     reads=[("Yb", s2), "prm0", "prm1"], writes=[("Yb", s2)])
                P.op("pool", lambda e, s2=s2, bs=bs: e.tensor_tensor(out=Yb[s2][:], in0=Yb[s2][:], in1=bon[:, bs], op=ALU.add),
                     reads=[("Yb", s2), "bon"], writes=[("Yb", s2)])
                P.op("dve", lambda e, s2=s2, bs=bs: e.tensor_tensor(out=Ob[s2][:], in0=Yb[s2][:], in1=gg[:, bs], op=ALU.mult),
                     reads=[("Yb", s2), "gg"], writes=[("Ob", s2)])
                for q in range(4):
                    P.op("pe", lambda e, s2=s2, q=q: e.transpose(out=pst[s2][:, q, :], in_=Ob[s2][:, q * 128:(q + 1) * 128],
                                                                 identity=ident[:]),
                         reads=[("Ob", s2), "ident"], writes=[("p4dt", s2)])
                P.op("act", lambda e, s2=s2: e.copy(out=Tk[s2][:], in_=pst[s2][:]), reads=[("p4dt", s2)], writes=[("Tk", s2)])
                P.dma("sp", K.ro_tok_d.rearrange("(t p) c -> p t c", p=128)[:, blk * 4:(blk + 1) * 4, c0:c0 + 128], Tk[s2][:],
                      reads=[("Tk", s2)], writes=[("ro_tok_d", ct, blk)])
        P.flush()


def phase5a_select(K):
    nc, P = K.nc, K.P
    with contextlib.ExitStack() as st:
        def sb(name, shape, dt):
            return st.enter_context(nc.sbuf_tensor(name, shape, dt))
        ro = sb("ro_tok", [128, 32, 1024], BF16)
        selT = sb("selT", [128, 32, 1024], BF16)
        qrow = sb("qrow", [128, 1024], F32)
        tki = sb("tki", [128, 32], I32)
        tkf = sb("tkf", [128, 32], F32)
        mo = [sb("mo%d" % i, [128, 512], BF16) for i in range(2)]
        at = sb("at5", [128, 8, 1024], BF16)
        ps = [st.enter_context(nc.psum_tensor("p5a_%d" % i, [128, 512], F32)) for i in range(2)]
        for q4 in range(4):
            P.dma("sp", ro[:, q4 * 8:(q4 + 1) * 8, :],
                  K.ro_tok_d.rearrange("(t p) c -> p t c", p=128)[:, q4 * 8:(q4 + 1) * 8, :], writes=[("ro", q4)])
        rok = [("ro", q4) for q4 in range(4)]
        P.dma("sp", qrow[:], bcast_rows(K.qpos_row, 1024), writes=["qrow"])
        P.op("pool", lambda e: e.iota(tki[:], pattern=[[128, 32]], base=0, channel_multiplier=1), writes=["tki"])
        P.op("dve", lambda e: e.tensor_copy(out=tkf[:], in_=tki[:]), reads=["tki"], writes=["tkf"])
        for T in range(32):
            P.op("dve", lambda e, T=T: e.tensor_scalar(out=selT[:, T, :], in0=qrow[:], scalar1=tkf[:, T:T + 1], scalar2=0.0,
                                                      op0=ALU.is_equal, op1=ALU.add), reads=["qrow", "tkf"], writes=[("selT", T)])
        sk = [("selT", T) for T in range(32)]
        P.dma("sp", at[:], K.attT_d.rearrange("h p t -> p h t"), writes=["at5"])
        P.dma("sp", K.mixT_d.rearrange("k p t -> p k t")[:, 0:8, :], at[:], reads=["at5"], writes=["mixa"])
        i = 0
        for m in range(8):
            for half in range(2):
                s2 = i % 2
                i += 1
                for T in range(32):
                    P.op("pe", lambda e, T=T, m=m, half=half, s2=s2: e.matmul(
                        ps[s2][:, :], lhsT=ro[:, T, m * 128:(m + 1) * 128], rhs=selT[:, T, half * 512:(half + 1) * 512],
                        start=(T == 0), stop=(T == 31)), reads=rok + sk, writes=[("p5a", s2)])
                P.op("act", lambda e, s2=s2: e.copy(out=mo[s2][:], in_=ps[s2][:, :]), reads=[("p5a", s2)], writes=[("mo", s2)])
                P.dma("sp", K.mixT_d[8 + m, :, half * 512:(half + 1) * 512], mo[s2][:], reads=[("mo", s2)], writes=[("mixr", m, half)])
        P.flush()


def phase5b_outproj(K):
    nc, P = K.nc, K.P
    with contextlib.ExitStack() as st:
        def sb(name, shape, dt):
            return st.enter_context(nc.sbuf_tensor(name, shape, dt))
        ident = sb("ident5", [128, 128], BF16)
        make_ident(K, P, ident)
        G2, SH2 = load_G_SH(K, P, st, 3, 4, K.norm2_g, "p5")
        GT1 = sb("GT1", [128, D], F32)
        P.dma("sp", GT1[:], bcast_rows(K.mod_d[2 * D:3 * D], D), writes=["GT1"])
        Wo = sb("Wo", [128, 16, D], BF16)
        stg = [sb("wstg5_%d" % i, [128, 4, 512], F32) for i in range(2)]
        wk = load_weight_bf16(K, P, stg, Wo, 0, K.w_out, D, "Wo")
        mixT = sb("mixT", [128, 16, 512], BF16)
        T = norm_tiles_alloc(K, st, "p5")
        x1 = T["xt"]
        hT = [sb("hT5_0", [128, 16, 512], BF16)] * 2
        xo = [sb("xo%d" % i, [128, D], F32) for i in range(2)]
        ps = [st.enter_context(nc.psum_tensor("p5b_%d" % i, [128, 512], F32)) for i in range(2)]
        ss, junk, hb, pT = T["ss"], T["junk"], T["hb"], T["pT"]
        gi = 0
        for blk in range(2):
            hs = 0
            P.dma("sp", mixT[:], K.mixT_d.rearrange("k p t -> p k t")[:, :, blk * 512:(blk + 1) * 512], writes=["mixT"])
            for ti in range(4):
                t = blk * 4 + ti
                xs = t % 2
                P.dma("sp", xo[xs][:], K.x_own[t * 128:(t + 1) * 128, :], writes=[("xo", xs)])
                for cg in range(4):
                    b = gi % 2
                    gi += 1
                    for k in range(16):
                        P.op("pe", lambda e, b=b, k=k, t=t, cg=cg: e.matmul(
                            ps[b][:, :], lhsT=mixT[:, k, (t % 4) * 128:(t % 4 + 1) * 128], rhs=Wo[:, k, cg * 512:(cg + 1) * 512],
                            start=(k == 0), stop=(k == 15)), reads=["mixT"] + wk, writes=[("p5b", b)])
                    cs = slice(cg * 512, (cg + 1) * 512)
                    P.op("dve", lambda e, b=b, xs=xs, cs=cs: e.tensor_tensor(out=x1[xs][:, cs], in0=ps[b][:, :], in1=GT1[:, cs], op=ALU.mult),
                         reads=[("p5b", b), "GT1"], writes=[("xt", xs)])
                    P.op("pool", lambda e, xs=xs, cs=cs: e.tensor_tensor(out=x1[xs][:, cs], in0=x1[xs][:, cs], in1=xo[xs][:, cs], op=ALU.add),
                         reads=[("xt", xs), ("xo", xs)], writes=[("xt", xs)])
                P.dma("sp", K.x1_d[t * 128:(t + 1) * 128, :], x1[xs][:], reads=[("xt", xs)], writes=[("x1_d", t)])
                P.op("act", lambda e, xs=xs: e.activation(out=junk[:], in_=x1[xs][:], func=AF.Square, accum_out=ss[:, 0:1]),
                     reads=[("xt", xs)], writes=["junk", "ss0"])
                P.op("dve", lambda e: e.tensor_scalar(out=ss[:, 1:2], in0=ss[:, 0:1], scalar1=1.0 / D, scalar2=1e-6,
                                                       op0=ALU.mult, op1=ALU.add), reads=["ss0"], writes=["ss1"])
                P.op("act", lambda e: e.activation(out=ss[:, 2:3], in_=ss[:, 1:2], func=AF.Sqrt), reads=["ss1"], writes=["ss2"])
                P.op("dve", lambda e: e.reciprocal(out=ss[:, 3:4], in_=ss[:, 2:3]), reads=["ss2"], writes=["ss3"])
                P.op("dve", lambda e, xs=xs: e.scalar_tensor_tensor(out=x1[xs][:], in0=x1[xs][:], scalar=ss[:, 3:4], in1=G2[:],
                                                                   op0=ALU.mult, op1=ALU.mult),
                     reads=[("xt", xs), "ss3", "G"], writes=[("xt", xs)])
                P.op("pool", lambda e, xs=xs: e.tensor_tensor(out=hb[xs][:], in0=x1[xs][:], in1=SH2[:], op=ALU.add),
                     reads=[("xt", xs), "SH"], writes=[("hb", xs)])
                for half in range(2):
                    for kk in range(8):
                        k = half * 8 + kk
                        P.op("pe", lambda e, k=k, kk=kk, half=half, xs=xs: e.transpose(
                            out=pT[half][:, kk, :], in_=hb[xs][:, k * 128:(k + 1) * 128], identity=ident[:]),
                            reads=[("hb", xs), "ident"], writes=[("pT", half)])
                    o_ = hT[hs][:, half * 8:(half + 1) * 8, ti * 128:(ti + 1) * 128]
                    if half == 0:
                        P.op("act", lambda e, o_=o_, half=half: e.copy(out=o_, in_=pT[half][:]), reads=[("pT", half)], writes=[("hT5", hs, ti, half)])
                    else:
                        P.op("dve", lambda e, o_=o_, half=half: e.tensor_copy(out=o_, in_=pT[half][:]), reads=[("pT", half)], writes=[("hT5", hs, ti, half)])
            P.dma("sp", K.h2T_d.rearrange("k p t -> p k t")[:, :, blk * 512:(blk + 1) * 512], hT[hs][:],
                  reads=[("hT5", hs, ti, half) for ti in range(4) for half in range(2)], writes=[("h2T_d", blk)])
        P.flush()


def phase5c_ffn(K):
    nc, P = K.nc, K.P
    NF = 5632 // 128
    with contextlib.ExitStack() as st:
        def sb(name, shape, dt):
            return st.enter_context(nc.sbuf_tensor(name, shape, dt))
        h2T = sb("h2T", [128, 16, OWN], BF16)
        P.dma("sp", h2T[:], K.h2T_d.rearrange("k p t -> p k t"), writes=["h2T"])
        ao = [sb("ao%d" % i, [128, 512], BF16) for i in range(2)]
        stg = [sb("wstg6_%d" % i, [128, 4, 512], F32) for i in range(2)]
        Wg = [sb("Wg%d" % i, [128, 16, 512], BF16) for i in range(2)]
        Wu = [sb("Wu%d" % i, [128, 16, 512], BF16) for i in range(2)]
        sg = [sb("sg%d" % i, [128, 512], F32) for i in range(2)]
        ps = [st.enter_context(nc.psum_tensor("p5c_%d" % i, [128, 512], F32)) for i in range(4)]
        gi = 0
        for fg in range(11):
            ws = fg % 2
            kg = load_weight_bf16(K, P, stg, Wg[ws], 0, K.w_ffn_gate[:, fg * 512:(fg + 1) * 512], 512, ("Wg", ws))
            ku = load_weight_bf16(K, P, stg, Wu[ws], 0, K.w_ffn_up[:, fg * 512:(fg + 1) * 512], 512, ("Wu", ws))
            for f4 in range(4):
                f = fg * 4 + f4
                for tb in range(2):
                    b = gi % 2
                    gi += 1
                    for k in range(16):
                        P.op("pe", lambda e, b=b, k=k, f4=f4, tb=tb, ws=ws: e.matmul(
                            ps[b][:, :], lhsT=Wg[ws][:, k, f4 * 128:(f4 + 1) * 128], rhs=h2T[:, k, tb * 512:(tb + 1) * 512],
                            start=(k == 0), stop=(k == 15)), reads=["h2T"] + kg, writes=[("p5c", b)])
                    for k in range(16):
                        P.op("pe", lambda e, b=b, k=k, f4=f4, tb=tb, ws=ws: e.matmul(
                            ps[2 + b][:, :], lhsT=Wu[ws][:, k, f4 * 128:(f4 + 1) * 128], rhs=h2T[:, k, tb * 512:(tb + 1) * 512],
                            start=(k == 0), stop=(k == 15)), reads=["h2T"] + ku, writes=[("p5c", 2 + b)])
                    P.op("act", lambda e, b=b: e.activation(out=sg[b][:], in_=ps[b][:, :], func=AF.Silu),
                         reads=[("p5c", b)], writes=[("sg", b)])
                    P.op("dve", lambda e, b=b: e.tensor_tensor(out=ao[b][:], in0=ps[2 + b][:, :], in1=sg[b][:], op=ALU.mult),
                         reads=[("p5c", 2 + b), ("sg", b)], writes=[("ao", b)])
                    P.dma("sp", K.actT_d[f, :, tb * 512:(tb + 1) * 512], ao[b][:], reads=[("ao", b)], writes=[("actT_d", f, tb)])
        P.flush()
    with contextlib.ExitStack() as st:
        def sb(name, shape, dt):
            return st.enter_context(nc.sbuf_tensor(name, shape, dt))
        GT2 = sb("GT2", [128, D], F32)
        P.dma("sp", GT2[:], bcast_rows(K.mod_d[5 * D:6 * D], D), writes=["GT2"])
        actT = sb("actT", [128, NF, OWN], BF16)
        for q in range(4):
            P.dma("sp", actT[:, q * 11:(q + 1) * 11, :], K.actT_d.rearrange("f p t -> p f t")[:, q * 11:(q + 1) * 11, :], writes=[("actT", q)])
        ak = [("actT", q) for q in range(4)]
        stg = [sb("wstg7_%d" % i, [128, 4, 512], F32) for i in range(2)]
        ps = [st.enter_context(nc.psum_tensor("p5d_%d" % i, [128, 512], F32)) for i in range(2)]
        gi = 0
        Wd = sb("Wd", [128, NF, 512], BF16)
        x1 = [sb("x1_%d" % i, [128, 512], F32) for i in range(2)]
        yo = [sb("yo%d" % i, [128, 512], F32) for i in range(2)]
        wdv = K.w_ffn_down.rearrange("(k p) n -> p k n", p=128)
        engs = ["pool", "dve", "act"]
        for cg in range(4):
            cs = slice(cg * 512, (cg + 1) * 512)
            wkeys = []
            for k0 in range(0, NF, 4):
                i = K.wcnt
                K.wcnt += 1
                sl = i % 2
                P.dma("sp", stg[sl][:, 0:4, :], wdv[:, k0:k0 + 4, cs], writes=[("wstg", sl)])
                eng = engs[i % 3]
                o_ = Wd[:, k0:k0 + 4, :]
                if eng == "act":
                    P.op("act", lambda e, o_=o_, sl=sl: e.copy(out=o_, in_=stg[sl][:, 0:4, :]), reads=[("wstg", sl)], writes=[("Wd", k0)])
                else:
                    P.op(eng, lambda e, o_=o_, sl=sl: e.tensor_copy(out=o_, in_=stg[sl][:, 0:4, :]), reads=[("wstg", sl)], writes=[("Wd", k0)])
                wkeys.append(("Wd", k0))
            for t in range(8):
                b = gi % 2
                gi += 1
                P.dma("sp", x1[b][:], K.x1_d[t * 128:(t + 1) * 128, cs], writes=[("x1", b)])
                for f in range(NF):
                    P.op("pe", lambda e, b=b, f=f, t=t: e.matmul(ps[b][:, :], lhsT=actT[:, f, t * 128:(t + 1) * 128], rhs=Wd[:, f, :],
                                                                 start=(f == 0), stop=(f == NF - 1)), reads=ak + wkeys, writes=[("p5c", b)])
                P.op("dve", lambda e, b=b, cs=cs: e.tensor_tensor(out=yo[b][:], in0=ps[b][:, :], in1=GT2[:, cs], op=ALU.mult),
                     reads=[("p5c", b), "GT2"], writes=[("yo", b)])
                P.op("pool", lambda e, b=b: e.tensor_tensor(out=yo[b][:], in0=yo[b][:], in1=x1[b][:], op=ALU.add),
                     reads=[("yo", b), ("x1", b)], writes=[("yo", b)])
                P.dma("sp", K.out[t * 128:(t + 1) * 128, cs], yo[b][:], reads=[("yo", b)], writes=[("out", t, cg)])
        P.flush()


def phase_final_copy(K):
    nc, P = K.nc, K.P
    with contextlib.ExitStack() as st:
        xt = [st.enter_context(nc.sbuf_tensor("fx%d" % i, [128, D], F32)) for i in range(2)]
        for t in range(8):
            s = t % 2
            P.dma("sp", xt[s][:], K.x_own[t * 128:(t + 1) * 128, :], writes=[("fx", s)])
            P.dma("sp", K.out[t * 128:(t + 1) * 128, :], xt[s][:], reads=[("fx", s)], writes=[("out", t)])
        P.flush()


def own_tiles(j):
    r = []
    for m in range(4):
        r += [8 * m + j, 8 * m + 7 - j]
    return r


def build_program(debug=False, stages=99, cts=range(8), dbg_list=None, skip_att=False):
    nc = bass.Bass("TRN2", target_bir_lowering=False)
    K = Ctx()
    K.stages = stages
    K.cts = cts
    K.skip_att = skip_att
    K.nc = nc
    K.dbg = {}
    K.wcnt = 0

    def inp(name, shape, dt=F32):
        return nc.dram_tensor(name, list(shape), dt, kind="ExternalInput").ap()

    def scratch(name, shape, dt):
        return nc.dram_tensor(name, list(shape), dt, kind="Internal").ap()

    K.x_full = inp("x_full", [S, D])
    K.x_own = inp("x_own", [OWN, D])
    K.c_arr = inp("c_arr", [128, 16])
    K.pos_full = inp("pos_full", [128, 32], I32)
    K.invf_att = inp("invf_att", [128, 16])
    K.invf_idx = inp("invf_idx", [128, 8])
    K.w_ada = inp("w_ada", [D, 6 * D])
    K.b_ada = inp("b_ada", [6 * D])
    K.norm1_g = inp("norm1_g", [D])
    K.k_norm_g = inp("k_norm_g", [128])
    K.q_norm_g = inp("q_norm_g", [128])
    K.pos_own = inp("pos_own", [128, 8], I32)
    K.qpos_own = inp("qpos_own", [128, 8])
    K.w_in = inp("w_in", [D, 7696])
    K.rw_prm = [inp("rwp%d" % i, [128, 8]) for i in range(10)]
    K.rw_mul = inp("rw_mul", [128, 4])
    K.rw_w_up = inp("rw_w_up", [96, 1024])
    K.rw_a_up = inp("rw_a_up", [96, 1024])
    K.rw_g_up = inp("rw_g_up", [256, 1024])
    K.qpos_row = inp("qpos_row", [OWN])
    K.w_out = inp("w_out", [D, D])
    K.norm2_g = inp("norm2_g", [D])
    K.w_ffn_gate = inp("w_ffn_gate", [D, 5632])
    K.w_ffn_up = inp("w_ffn_up", [D, 5632])
    K.w_ffn_down = inp("w_ffn_down", [5632, D])
    K.out = nc.dram_tensor("y_own", [OWN, D], F32, kind="ExternalOutput").ap()
    K.mod_d = scratch("mod_d", [6 * D], F32)
    K.hT_d = scratch("hT_d", [16, 128, S], BF16)
    K.kT_d = scratch("kT_d", [8, 128, S], BF16)
    K.v_d = scratch("v_d", [S, 8 * 129], BF16)
    K.ikT_d = scratch("ikT_d", [64, S], BF16)
    K.yT_d = scratch("yT_d", [3520, S], F32)
    K.qT_d = scratch("qT_d", [8, 128, OWN], BF16)
    K.iqT_d = scratch("iqT_d", [64, OWN, 16], BF16)
    K.iw_d = scratch("iw_d", [OWN, 16], F32)
    K.attT_d = scratch("attT_d", [8, 128, OWN], BF16)
    for nm in ("vb_d", "G_d", "BON_d", "AH_d", "RH_d", "BH_d", "KH_d"):
        setattr(K, nm, scratch(nm, [1024, S], BF16))
    K.PC_d = scratch("PC_d", [1024, NCH], F32)
    K.oT_d = scratch("oT_d", [1024, S], F32)
    K.ro_tok_d = scratch("ro_tok_d", [S, 1024], BF16)
    K.mixT_d = scratch("mixT_d", [16, 128, OWN], BF16)
    K.x1_d = scratch("x1_d", [OWN, D], F32)
    K.h2T_d = scratch("h2T_d", [16, 128, OWN], BF16)
    K.actT_d = scratch("actT_d", [44, 128, OWN], BF16)
    with contextlib.ExitStack() as stack:
        K.P = Prog(nc, stack)
        phase0_adaln(K)
        phase1_kv(K)
        if K.stages >= 2:
            phase1b_rwkv_proj(K)
        if K.stages >= 3 and not getattr(K, "skip_att", False):
            phase2_own_proj(K)
            phase3_attention(K)
        if K.stages >= 4:
            phase4b_rwkv_prep(K, cts=K.cts)
            if K.stages >= 5:
                phase4c_rwkv_scan(K, heads=[h for ct in K.cts for h in (2 * ct, 2 * ct + 1)])
        if K.stages >= 6:
            phase4d_rwkv_post(K, cts=K.cts)
        if K.stages >= 7:
            phase5a_select(K)
            phase5b_outproj(K)
            phase5c_ffn(K)
        else:
            phase_final_copy(K)
        if debug:
            P = K.P
            allc = (("dbg_ro", K.ro_tok_d, [S, 1024], BF16), ("dbg_mixT", K.mixT_d, [16, 128, OWN], BF16), ("dbg_x1", K.x1_d, [OWN, D], F32),
                    ("dbg_oT", K.oT_d, [1024, S], F32), ("dbg_AH", K.AH_d, [1024, S], BF16), ("dbg_BH", K.BH_d, [1024, S], BF16),
                    ("dbg_KH", K.KH_d, [1024, S], BF16), ("dbg_RH", K.RH_d, [1024, S], BF16), ("dbg_PC", K.PC_d, [1024, NCH], F32),
                    ("dbg_G", K.G_d, [1024, S], BF16), ("dbg_BON", K.BON_d, [1024, S], BF16), ("dbg_vb", K.vb_d, [1024, S], BF16),
                    ("dbg_yT", K.yT_d, [3520, S], F32), ("dbg_attT", K.attT_d, [8, 128, OWN], BF16),
                                     ("dbg_qT", K.qT_d, [8, 128, OWN], BF16), ("dbg_iqT", K.iqT_d, [64, OWN, 16], BF16),
                                     ("dbg_iw", K.iw_d, [OWN, 16], F32))
            for nm, src, shp, dt in allc:
                if dbg_list is not None and nm not in dbg_list:
                    continue
                o = dbg_out(K, nm, shp, dt)
                P.dma("sp", o, src, writes=[nm])
            P.flush()
    return nc, K


def make_in_maps(inputs, cores=range(8)):
    x = np.asarray(inputs["x"], dtype=np.float32)
    c = np.asarray(inputs["c"], dtype=np.float32)
    pos = np.asarray(inputs["positions"], dtype=np.int32)
    invf_att = (np.float32(500000.0) ** (-np.arange(16, dtype=np.float32) / np.float32(16))).astype(np.float32)
    invf_idx = (np.float32(500000.0) ** (-np.arange(8, dtype=np.float32) / np.float32(8))).astype(np.float32)
    mu = np.asarray(inputs["rwkv_mu"][0], dtype=np.float32)

    def c8(v):
        return np.ascontiguousarray(np.asarray(v, dtype=np.float32).reshape(8, 128).T)
    vecs = [mu[0:1024], mu[1024:2048], mu[2048:3072], inputs["rwkv_w0"][0], inputs["rwkv_a0"][0], inputs["rwkv_k_k"][0],
            inputs["rwkv_k_a"][0], np.asarray(inputs["rwkv_r_k"][0]).reshape(-1), inputs["rwkv_lnx_g"][0], inputs["rwkv_lnx_b"][0]]
    rwp = {"rwp%d" % i: c8(v) for i, v in enumerate(vecs)}
    rw_mul = np.zeros((128, 4), np.float32)
    rw_mul[:96, 0] = mu[3072:3168]
    rw_mul[:96, 1] = mu[3168:3264]
    rw_mul[:, 2] = mu[3264:3392]
    rw_mul[:, 3] = mu[3392:3520]
    maps = []
    for core in cores:
        b, j = core // 4, core % 4
        tiles = own_tiles(j)
        idx = np.concatenate([np.arange(t * 128, (t + 1) * 128) for t in tiles])
        maps.append({
            "x_full": np.ascontiguousarray(x[b]),
            "x_own": np.ascontiguousarray(x[b][idx]),
            "c_arr": np.ascontiguousarray(c[b].reshape(16, 128).T),
            "pos_full": np.ascontiguousarray(pos[b].reshape(32, 128).T),
            "invf_att": np.ascontiguousarray(np.broadcast_to(invf_att, (128, 16))),
            "invf_idx": np.ascontiguousarray(np.broadcast_to(invf_idx, (128, 8))),
            "w_ada": np.asarray(inputs["w_ada"][0], dtype=np.float32),
            "b_ada": np.asarray(inputs["b_ada"][0], dtype=np.float32),
            "norm1_g": np.asarray(inputs["norm1_g"][0], dtype=np.float32),
            "k_norm_g": np.asarray(inputs["k_norm_g"][0], dtype=np.float32),
            "q_norm_g": np.asarray(inputs["q_norm_g"][0], dtype=np.float32),
            "pos_own": np.ascontiguousarray(pos[b][idx].reshape(8, 128).T),
            "qpos_own": np.ascontiguousarray(idx.astype(np.float32).reshape(8, 128).T),
            "w_in": np.asarray(inputs["w_in"][0], dtype=np.float32),
            "qpos_row": idx.astype(np.float32),
            "w_out": np.asarray(inputs["w_out"][0], dtype=np.float32),
            "norm2_g": np.asarray(inputs["norm2_g"][0], dtype=np.float32),
            "w_ffn_gate": np.asarray(inputs["w_ffn_gate"][0], dtype=np.float32),
            "w_ffn_up": np.asarray(inputs["w_ffn_up"][0], dtype=np.float32),
            "w_ffn_down": np.asarray(inputs["w_ffn_down"][0], dtype=np.float32),
            "rw_w_up": np.asarray(inputs["rwkv_w_up"][0], dtype=np.float32),
            "rw_a_up": np.asarray(inputs["rwkv_a_up"][0], dtype=np.float32),
            "rw_g_up": np.asarray(inputs["rwkv_g_up"][0], dtype=np.float32),
            "rw_mul": rw_mul,
            **rwp,
        })
    return maps


def kernel(**inputs):
    nc, K = build_program(debug=False)
    maps = make_in_maps(inputs)
    res = run_bass_kernel_spmd(nc, maps, core_ids=list(range(8)))
    out = np.zeros((2, S, D), dtype=np.float32)
    for core in range(8):
        b, j = core // 4, core % 4
        y = res.results[core]["y_own"]
        for i, t in enumerate(own_tiles(j)):
            out[b, t * 128:(t + 1) * 128] = y[i * 128:(i + 1) * 128]
    return out
```

```python
import contextlib
import numpy as np
import concourse.bass as bass
import concourse.mybir as mybir
from concourse.bass_utils import run_bass_kernel_spmd

F32 = mybir.dt.float32
BF16 = mybir.dt.bfloat16
I32 = mybir.dt.int32
AF = mybir.ActivationFunctionType
ALU = mybir.AluOpType
AX = mybir.AxisListType

D = 2048
S = 4096
NT = 32
OWN = 1024
ENGS = ("pe", "act", "dve", "pool", "sp")
DEBUG = {}


class _Op:
    __slots__ = ("eng", "fn", "deps", "needs_inc", "is_dma", "sem", "count", "idx", "prev_same_sem", "is_cc")

    def __init__(self, eng, fn, is_dma):
        self.eng = eng
        self.fn = fn
        self.deps = set()
        self.needs_inc = False
        self.is_dma = is_dma
        self.sem = None
        self.count = 0
        self.prev_same_sem = None
        self.is_cc = False


class Prog:
    def __init__(self, nc, stack, n_dma_sems=48):
        self.nc = nc
        self.n_dma_sems = n_dma_sems
        self.eng_sem = {e: stack.enter_context(nc.semaphore("s_" + e)) for e in ENGS}
        self.dma_sems = [stack.enter_context(nc.semaphore("d%d" % i)) for i in range(n_dma_sems)]
        self.bar_sem = stack.enter_context(nc.semaphore("bar"))
        self.cc_sem = stack.enter_context(nc.semaphore("ccs"))
        self.cc_cnt = 0
        self.cnt = {e: 0 for e in ENGS}
        self.dcnt = [0] * n_dma_sems
        self.rr = 0
        self.nbar = 0
        self._reset()

    def _reset(self):
        self.ops = []
        self.last_writer = {}
        self.readers = {}

    def _record(self, op, reads, writes):
        idx = len(self.ops)
        op.idx = idx
        deps = set()
        for k in reads:
            w = self.last_writer.get(k)
            if w is not None:
                deps.add(w)
        for k in writes:
            w = self.last_writer.get(k)
            if w is not None:
                deps.add(w)
            for r in self.readers.get(k, ()):
                deps.add(r)
        deps.discard(idx)
        op.deps = deps
        self.ops.append(op)
        for k in reads:
            self.readers.setdefault(k, []).append(idx)
        for k in writes:
            self.last_writer[k] = idx
            self.readers[k] = []
        return idx

    def op(self, eng, fn, reads=(), writes=()):
        return self._record(_Op(eng, fn, False), reads, writes)

    def dma(self, queue, out, in_, reads=(), writes=(), **kw):
        def fn(e, out=out, in_=in_, kw=kw):
            return e.dma_start(out=out, in_=in_, **kw)
        return self._record(_Op(queue, fn, True), reads, writes)

    def coll(self, fn, reads=(), writes=()):
        o = _Op("pool", fn, True)
        o.is_cc = True
        return self._record(o, reads, writes)

    def flush(self):
        nc = self.nc
        ops = self.ops
        for o in ops:
            nd = set()
            for d in o.deps:
                p = ops[d]
                if o.eng == "pe" and p.eng == "pe" and not p.is_dma and not o.is_dma:
                    continue
                nd.add(d)
                p.needs_inc = True
            o.deps = nd
        last_of = {}
        for o in ops:
            if not o.is_dma:
                last_of[o.eng] = o
        for o in last_of.values():
            o.needs_inc = True
        dlast = [None] * self.n_dma_sems
        for o in ops:
            if o.is_cc:
                self.cc_cnt += 1
                o.sem = self.cc_sem
                o.count = self.cc_cnt
            elif o.is_dma:
                s = self.rr % self.n_dma_sems
                self.rr += 1
                o.prev_same_sem = dlast[s]
                self.dcnt[s] += 16
                o.sem = self.dma_sems[s]
                o.count = self.dcnt[s]
                dlast[s] = o.idx
            elif o.needs_inc:
                self.cnt[o.eng] += 1
                o.sem = self.eng_sem[o.eng]
                o.count = self.cnt[o.eng]
        per_eng = {e: [o for o in ops if o.eng == e] for e in ENGS}
        final = [(self.dma_sems[s], self.dcnt[s]) for s in range(self.n_dma_sems) if self.dcnt[s] > 0]
        final += [(self.eng_sem[e], self.cnt[e]) for e in ENGS if self.cnt[e] > 0]
        if self.cc_cnt > 0:
            final.append((self.cc_sem, self.cc_cnt))
        self.nbar += 1
        nbar = self.nbar
        bar = self.bar_sem

        def run(e_name, eng):
            waited = {}
            for o in per_eng[e_name]:
                need = {}
                for d in o.deps:
                    p = ops[d]
                    if need.get(p.sem.num, (0, None))[0] < p.count:
                        need[p.sem.num] = (p.count, p.sem)
                if o.is_dma and o.prev_same_sem is not None:
                    p = ops[o.prev_same_sem]
                    if need.get(p.sem.num, (0, None))[0] < p.count:
                        need[p.sem.num] = (p.count, p.sem)
                for key, (c, s) in need.items():
                    if waited.get(key, 0) < c:
                        eng.wait_ge(s, c)
                        waited[key] = c
                ins = o.fn(eng)
                if o.is_cc:
                    ins.then_inc(o.sem)
                elif o.is_dma:
                    ins.then_inc(o.sem, 16)
                elif o.needs_inc:
                    ins.then_inc(o.sem, 1)
            if e_name == "sp":
                for s, c in final:
                    eng.wait_ge(s, c)
                eng.sem_inc(bar, 1)
            eng.wait_ge(bar, nbar)

        with nc.Block() as block:
            @block.tensor
            def _(e):
                run("pe", e)

            @block.scalar
            def _(e):
                run("act", e)

            @block.vector
            def _(e):
                run("dve", e)

            @block.gpsimd
            def _(e):
                run("pool", e)

            @block.sync
            def _(e):
                run("sp", e)
        self._reset()


class Ctx:
    pass


def bcast_rows(ap1d, n):
    return bass.AP(ap1d.tensor, ap1d.offset, [[0, 128], [1, n]])


def dbg_out(K, name, shape, dtype=F32):
    t = K.nc.dram_tensor(name, list(shape), dtype, kind="ExternalOutput")
    K.dbg[name] = t
    return t.ap()


def make_ident(K, P, ident):
    P.op("pool", lambda e: e.memset(ident[:], 0.0), writes=["ident"])
    P.op("pool", lambda e: e.affine_select(out=ident[:], in_=ident[:], pattern=[[-1, 128]],
                                           compare_op=ALU.not_equal, fill=1.0, base=0,
                                           channel_multiplier=1),
         reads=["ident"], writes=["ident"])


def phase0_adaln(K):
    nc, P = K.nc, K.P
    with contextlib.ExitStack() as st:
        c_sb = st.enter_context(nc.sbuf_tensor("c_sb", [128, 16], F32))
        cact = st.enter_context(nc.sbuf_tensor("cact", [128, 16], F32))
        wst = [st.enter_context(nc.sbuf_tensor("wst%d" % i, [128, 16, 512], F32)) for i in range(2)]
        modrow = st.enter_context(nc.sbuf_tensor("modrow", [1, 12288], F32))
        brow = st.enter_context(nc.sbuf_tensor("brow", [1, 12288], F32))
        ps = [st.enter_context(nc.psum_tensor("ps0_%d" % i, [1, 512], F32)) for i in range(2)]
        P.dma("sp", c_sb[:], K.c_arr, writes=["c_sb"])
        P.dma("sp", brow[:], K.b_ada.rearrange("(o n) -> o n", o=1), writes=["brow"])
        P.op("act", lambda e: e.activation(out=cact[:], in_=c_sb[:], func=AF.Silu),
             reads=["c_sb"], writes=["cact"])
        wv = K.w_ada.rearrange("(k p) n -> p k n", p=128)
        for nt in range(24):
            sl = nt % 2
            for hh in range(2):
                P.dma("sp", wst[sl][:, hh * 8:(hh + 1) * 8, :],
                      wv[:, hh * 8:(hh + 1) * 8, nt * 512:(nt + 1) * 512],
                      writes=[("wst", sl, hh)])
            for k in range(16):
                P.op("pe", lambda e, k=k, sl=sl: e.matmul(ps[sl][:, :], lhsT=cact[:, k:k + 1],
                                                         rhs=wst[sl][:, k, :], start=(k == 0), stop=(k == 15)),
                     reads=["cact", ("wst", sl, k // 8)], writes=[("ps0", sl)])
            P.op("dve", lambda e, nt=nt, sl=sl: e.tensor_tensor(
                out=modrow[0:1, nt * 512:(nt + 1) * 512], in0=ps[sl][:, :],
                in1=brow[0:1, nt * 512:(nt + 1) * 512], op=ALU.add),
                reads=[("ps0", sl), "brow"], writes=[("modrow", nt)])
        P.dma("sp", K.mod_d.rearrange("(o n) -> o n", o=1), modrow[:],
              reads=[("modrow", nt) for nt in range(24)], writes=["mod_d"])
        P.flush()


def load_mod_rows(K, P, tile, which, gain_ap=None, key=None):
    src = K.mod_d[which * D:(which + 1) * D]
    P.dma("sp", tile[:], bcast_rows(src, D), writes=[key])


def bc(ap, shape):
    return ap.to_broadcast(list(shape))


def load_weight_bf16(K, P, st_tiles, dst, c_dst, src2d, ncols, tag):
    wv = src2d.rearrange("(k p) n -> p k n", p=128)
    nk = wv.shape[1]
    engs = ["pool", "dve", "act"]
    for c0 in range(0, ncols, 512):
        n = min(512, ncols - c0)
        for k0 in range(0, nk, 4):
            kn = min(4, nk - k0)
            i = K.wcnt
            K.wcnt += 1
            sl = i % 2
            stg = st_tiles[sl]
            P.dma("sp", stg[:, 0:kn, 0:n], wv[:, k0:k0 + kn, c0:c0 + n], writes=[("wstg", sl)])
            eng = engs[i % 3]
            o = dst[:, k0:k0 + kn, c_dst + c0:c_dst + c0 + n]
            if eng == "act":
                P.op("act", lambda e, o=o, stg=stg, kn=kn, n=n: e.copy(out=o, in_=stg[:, 0:kn, 0:n]),
                     reads=[("wstg", sl)], writes=[(tag, c0, k0)])
            else:
                P.op(eng, lambda e, o=o, stg=stg, kn=kn, n=n: e.tensor_copy(out=o, in_=stg[:, 0:kn, 0:n]),
                     reads=[("wstg", sl)], writes=[(tag, c0, k0)])
    return [(tag, c0, k0) for c0 in range(0, ncols, 512) for k0 in range(0, nk, 4)]


def rope_tables(K, P, st, pos_arr, ntile, invf_att, invf_idx, tag):
    nc = K.nc
    posi = st.enter_context(nc.sbuf_tensor(tag + "posi", [128, ntile], I32))
    posf = st.enter_context(nc.sbuf_tensor(tag + "posf", [128, ntile], F32))
    iva = st.enter_context(nc.sbuf_tensor(tag + "iva", [128, 16], F32))
    ivi = st.enter_context(nc.sbuf_tensor(tag + "ivi", [128, 8], F32))
    P.dma("sp", posi[:], pos_arr, writes=[tag + "posi"])
    P.dma("sp", iva[:], invf_att, writes=[tag + "iva"])
    P.dma("sp", ivi[:], invf_idx, writes=[tag + "ivi"])
    P.op("dve", lambda e: e.tensor_copy(out=posf[:], in_=posi[:]), reads=[tag + "posi"], writes=[tag + "posf"])
    out = {}
    for nm, iv, h in (("a", iva, 16), ("i", ivi, 8)):
        u = st.enter_context(nc.sbuf_tensor(tag + "u" + nm, [128, ntile, h], F32))
        ui = st.enter_context(nc.sbuf_tensor(tag + "ui" + nm, [128, ntile, h], I32))
        uf = st.enter_context(nc.sbuf_tensor(tag + "uf" + nm, [128, ntile, h], F32))
        for fn, off in (("sin", 0.0), ("cos", 0.25)):
            tb = st.enter_context(nc.sbuf_tensor(tag + fn + nm, [128, ntile, h], F32))
            kk = tag + fn + nm
            P.op("dve", lambda e, u=u, iv=iv, h=h: e.tensor_tensor(
                out=u[:], in0=bc(posf[:].unsqueeze(2), [128, ntile, h]),
                in1=bc(iv[:].unsqueeze(1), [128, ntile, h]), op=ALU.mult),
                reads=[tag + "posf", tag + "iv" + nm], writes=[tag + "U" + nm])
            P.op("dve", lambda e, u=u, off=off: e.tensor_scalar(
                out=u[:], in0=u[:], scalar1=float(1.0 / (2 * np.pi)), scalar2=off, op0=ALU.mult, op1=ALU.add),
                reads=[tag + "U" + nm], writes=[tag + "U" + nm])
            P.op("dve", lambda e, u=u, ui=ui: e.tensor_copy(out=ui[:], in_=u[:]), reads=[tag + "U" + nm], writes=[tag + "UI" + nm])
            P.op("dve", lambda e, uf=uf, ui=ui: e.tensor_copy(out=uf[:], in_=ui[:]), reads=[tag + "UI" + nm], writes=[tag + "UF" + nm])
            P.op("dve", lambda e, u=u, uf=uf: e.tensor_tensor(out=u[:], in0=u[:], in1=uf[:], op=ALU.subtract),
                 reads=[tag + "U" + nm, tag + "UF" + nm], writes=[tag + "U" + nm])
            P.op("dve", lambda e, u=u: e.tensor_scalar(out=u[:], in0=u[:], scalar1=-0.5, scalar2=0.5,
                                                        op0=ALU.max, op1=ALU.min),
                 reads=[tag + "U" + nm], writes=[tag + "U" + nm])
            P.op("act", lambda e, u=u, tb=tb: e.activation(out=tb[:], in_=u[:], func=AF.Sin,
                                                            scale=float(2 * np.pi)),
                 reads=[tag + "U" + nm], writes=[kk])
            out[fn + nm] = (tb, kk)
    return out


def apply_rope(P, eng, x4, cos, sin, t, half, tmp, rk, wk):
    ctb, ck = cos
    stb, sk = sin
    H = x4.shape[1]
    x1 = x4[:, :, 0:half]
    x2 = x4[:, :, half:2 * half]
    cb = bc(ctb[:, t, :].unsqueeze(1), [128, H, half])
    sb = bc(stb[:, t, :].unsqueeze(1), [128, H, half])
    a, b2, c, d = tmp
    P.op(eng, lambda e: e.tensor_tensor(out=a[:, 0:H, 0:half], in0=x1, in1=cb, op=ALU.mult), reads=rk + [ck], writes=["rtmpA"])
    P.op(eng, lambda e: e.tensor_tensor(out=b2[:, 0:H, 0:half], in0=x2, in1=sb, op=ALU.mult), reads=rk + [sk], writes=["rtmpB"])
    P.op(eng, lambda e: e.tensor_tensor(out=c[:, 0:H, 0:half], in0=x2, in1=cb, op=ALU.mult), reads=rk + [ck], writes=["rtmpC"])
    P.op(eng, lambda e: e.tensor_tensor(out=d[:, 0:H, 0:half], in0=x1, in1=sb, op=ALU.mult), reads=rk + [sk], writes=["rtmpD"])
    P.op(eng, lambda e: e.tensor_tensor(out=x1, in0=a[:, 0:H, 0:half], in1=b2[:, 0:H, 0:half], op=ALU.subtract),
         reads=["rtmpA", "rtmpB", "rtmpC", "rtmpD"] + rk, writes=rk)
    P.op(eng, lambda e: e.tensor_tensor(out=x2, in0=c[:, 0:H, 0:half], in1=d[:, 0:H, 0:half], op=ALU.add),
         reads=["rtmpC", "rtmpD"] + rk, writes=rk)


def head_rmsnorm(P, x3, gain, sq, ssum, rk, wk):
    P.op("pool", lambda e: e.tensor_tensor(out=sq[:], in0=x3, in1=x3, op=ALU.mult), reads=rk, writes=[wk + "sq"])
    P.op("dve", lambda e: e.tensor_reduce(out=ssum[:, 0:8], in_=sq[:], axis=AX.X, op=ALU.add),
         reads=[wk + "sq"], writes=[wk + "s0"])
    P.op("dve", lambda e: e.tensor_scalar(out=ssum[:, 8:16], in0=ssum[:, 0:8], scalar1=1.0 / 128, scalar2=1e-6,
                                           op0=ALU.mult, op1=ALU.add), reads=[wk + "s0"], writes=[wk + "s1"])
    P.op("act", lambda e: e.activation(out=ssum[:, 16:24], in_=ssum[:, 8:16], func=AF.Sqrt),
         reads=[wk + "s1"], writes=[wk + "s2"])
    P.op("dve", lambda e: e.reciprocal(out=ssum[:, 24:32], in_=ssum[:, 16:24]), reads=[wk + "s2"], writes=[wk + "s3"])
    P.op("dve", lambda e: e.tensor_tensor(out=x3, in0=x3, in1=bc(ssum[:, 24:32].unsqueeze(2), [128, 8, 128]),
                                           op=ALU.mult), reads=rk + [wk + "s3"], writes=rk)
    P.op("pool", lambda e: e.tensor_tensor(out=x3, in0=x3, in1=bc(gain[:].unsqueeze(1), [128, 8, 128]),
                                            op=ALU.mult), reads=rk + ["gain" + wk], writes=rk)


def norm_block(K, P, T, x_src, t, G1, SH1, ident, blk_hT, ti):
    xs = t % 2
    xt, hb, ss, junk, pT = T["xt"], T["hb"], T["ss"], T["junk"], T["pT"]
    P.dma("sp", xt[xs][:], x_src[t * 128:(t + 1) * 128, :], writes=[("xt", xs)])
    P.op("act", lambda e: e.activation(out=junk[:], in_=xt[xs][:], func=AF.Square, accum_out=ss[:, 0:1]),
         reads=[("xt", xs)], writes=["junk", "ss0"])
    P.op("dve", lambda e: e.tensor_scalar(out=ss[:, 1:2], in0=ss[:, 0:1], scalar1=1.0 / D, scalar2=1e-6,
                                           op0=ALU.mult, op1=ALU.add), reads=["ss0"], writes=["ss1"])
    P.op("act", lambda e: e.activation(out=ss[:, 2:3], in_=ss[:, 1:2], func=AF.Sqrt), reads=["ss1"], writes=["ss2"])
    P.op("dve", lambda e: e.reciprocal(out=ss[:, 3:4], in_=ss[:, 2:3]), reads=["ss2"], writes=["ss3"])
    P.op("dve", lambda e: e.scalar_tensor_tensor(out=xt[xs][:], in0=xt[xs][:], scalar=ss[:, 3:4], in1=G1[:],
                                                  op0=ALU.mult, op1=ALU.mult),
         reads=[("xt", xs), "ss3", "G"], writes=[("xt", xs)])
    P.op("pool", lambda e: e.tensor_tensor(out=hb[xs][:], in0=xt[xs][:], in1=SH1[:], op=ALU.add),
         reads=[("xt", xs), "SH"], writes=[("hb", xs)])
    for half in range(2):
        for kk in range(8):
            k = half * 8 + kk
            P.op("pe", lambda e, k=k, kk=kk, half=half: e.transpose(
                out=pT[half][:, kk, :], in_=hb[xs][:, k * 128:(k + 1) * 128], identity=ident[:]),
                reads=[("hb", xs), "ident"], writes=[("pT", half)])
        o = blk_hT[:, half * 8:(half + 1) * 8, ti * 128:(ti + 1) * 128]
        if half == 0:
            P.op("act", lambda e, o=o, half=half: e.copy(out=o, in_=pT[half][:]),
                 reads=[("pT", half)], writes=[("hT", ti, half)])
        else:
            P.op("dve", lambda e, o=o, half=half: e.tensor_copy(out=o, in_=pT[half][:]),
                 reads=[("pT", half)], writes=[("hT", ti, half)])


def norm_tiles_alloc(K, st, tag):
    nc = K.nc
    T = {}
    T["xt"] = [st.enter_context(nc.sbuf_tensor(tag + "xt%d" % i, [128, D], F32)) for i in range(2)]
    T["hb"] = [st.enter_context(nc.sbuf_tensor(tag + "hb%d" % i, [128, D], BF16)) for i in range(2)]
    T["ss"] = st.enter_context(nc.sbuf_tensor(tag + "ss", [128, 4], F32))
    T["junk"] = st.enter_context(nc.sbuf_tensor(tag + "junk", [128, D], BF16))
    T["pT"] = [st.enter_context(nc.psum_tensor(tag + "pT%d" % i, [128, 8, 128], BF16)) for i in range(2)]
    return T


def load_G_SH(K, P, st, which_sh, which_sc, gain_vec, tag):
    nc = K.nc
    G = st.enter_context(nc.sbuf_tensor(tag + "G", [128, D], F32))
    SH = st.enter_context(nc.sbuf_tensor(tag + "SH", [128, D], F32))
    gtmp = st.enter_context(nc.sbuf_tensor(tag + "gtmp", [128, D], F32))
    P.dma("sp", SH[:], bcast_rows(K.mod_d[which_sh * D:(which_sh + 1) * D], D), writes=["SH"])
    P.dma("sp", G[:], bcast_rows(K.mod_d[which_sc * D:(which_sc + 1) * D], D), writes=["G"])
    P.dma("sp", gtmp[:], bcast_rows(gain_vec, D), writes=["gtmp"])
    P.op("dve", lambda e: e.scalar_tensor_tensor(out=G[:], in0=G[:], scalar=1.0, in1=gtmp[:],
                                                  op0=ALU.add, op1=ALU.mult), reads=["G", "gtmp"], writes=["G"])
    return G, SH


def phase1_kv(K):
    nc, P = K.nc, K.P
    with contextlib.ExitStack() as st:
        ident = st.enter_context(nc.sbuf_tensor("ident", [128, 128], BF16))
        make_ident(K, P, ident)
        G1, SH1 = load_G_SH(K, P, st, 0, 1, K.norm1_g, "p1")
        T = norm_tiles_alloc(K, st, "p1")
        hT = [st.enter_context(nc.sbuf_tensor("hT%d" % i, [128, 16, 512], BF16)) for i in range(2)]
        W = st.enter_context(nc.sbuf_tensor("Wkv", [128, 16, 2112], BF16))
        stg = [st.enter_context(nc.sbuf_tensor("wstg%d" % i, [128, 4, 512], F32)) for i in range(2)]
        wk_k = load_weight_bf16(K, P, stg, W, 0, K.w_in[:, 1024:2048], 1024, "Wk")
        wk_v = load_weight_bf16(K, P, stg, W, 1024, K.w_in[:, 2048:3072], 1024, "Wv")
        wk_i = load_weight_bf16(K, P, stg, W, 2048, K.w_in[:, 4096:4160], 64, "Wi")
        rt = rope_tables(K, P, st, K.pos_full, 32, K.invf_att, K.invf_idx, "rf")
        gain = st.enter_context(nc.sbuf_tensor("kgain", [128, 128], F32))
        P.dma("sp", gain[:], bcast_rows(K.k_norm_g, 128), writes=["gainK"])
        ksb = st.enter_context(nc.sbuf_tensor("ksb", [128, 8, 128], F32))
        kbf = st.enter_context(nc.sbuf_tensor("kbf", [128, 8, 128], BF16))
        sq = st.enter_context(nc.sbuf_tensor("sq", [128, 8, 128], F32))
        ssum = st.enter_context(nc.sbuf_tensor("ssum", [128, 32], F32))
        rtmp = [st.enter_context(nc.sbuf_tensor("rtmp%d" % i, [128, 8, 16], F32)) for i in range(4)]
        vsb = st.enter_context(nc.sbuf_tensor("vsb", [128, 8, 129], BF16))
        iksb = st.enter_context(nc.sbuf_tensor("iksb", [128, 1, 64], F32))
        ikbf = st.enter_context(nc.sbuf_tensor("ikbf", [128, 64], BF16))
        kTs = st.enter_context(nc.sbuf_tensor("kTs", [128, 8, 128], BF16))
        ikTs = st.enter_context(nc.sbuf_tensor("ikTs", [64, 128], BF16))
        pm = [st.enter_context(nc.psum_tensor("pm%d" % i, [128, 512], F32)) for i in range(3)]
        pk = st.enter_context(nc.psum_tensor("pk", [128, 8, 128], BF16))
        P.op("pool", lambda e: e.memset(vsb[:], 1.0), writes=["vsb"])
        for blk in range(8):
            hs = blk % 2
            for ti in range(4):
                norm_block(K, P, T, K.x_full, blk * 4 + ti, G1, SH1, ident, hT[hs], ti)
            hkeys = [("hT", ti, half) for ti in range(4) for half in range(2)]
            P.dma("sp", K.hT_d.rearrange("k p t -> p k t")[:, :, blk * 512:(blk + 1) * 512], hT[hs][:],
                  reads=hkeys, writes=[("hT_d", blk)])
            for ti in range(4):
                t = blk * 4 + ti
                hk = [("hT", ti, 0), ("hT", ti, 1)]
                for gi, (c0, n, wkeys) in enumerate([(0, 512, wk_k), (512, 512, wk_k), (1024, 512, wk_v),
                                                     (1536, 512, wk_v), (2048, 64, wk_i)]):
                    pb = pm[gi % 3]
                    for k in range(16):
                        P.op("pe", lambda e, pb=pb, k=k, c0=c0, n=n, ti=ti, hs=hs: e.matmul(
                            pb[:, 0:n], lhsT=hT[hs][:, k, ti * 128:(ti + 1) * 128], rhs=W[:, k, c0:c0 + n],
                            start=(k == 0), stop=(k == 15)), reads=hk + wkeys, writes=[("pm", gi % 3)])
                    if gi < 2:
                        P.op("act", lambda e, pb=pb, gi=gi: e.copy(out=ksb[:, gi * 4:(gi + 1) * 4, :], in_=pb[:, 0:512]),
                             reads=[("pm", gi % 3)], writes=["ksb"])
                    elif gi < 4:
                        g2 = gi - 2
                        P.op("act", lambda e, pb=pb, g2=g2: e.copy(out=vsb[:, g2 * 4:(g2 + 1) * 4, 0:128], in_=pb[:, 0:512]),
                             reads=[("pm", gi % 3)], writes=["vsb"])
                    else:
                        P.op("act", lambda e, pb=pb: e.copy(out=iksb[:, 0, :], in_=pb[:, 0:64]),
                             reads=[("pm", gi % 3)], writes=["iksb"])
                P.dma("sp", K.v_d[t * 128:(t + 1) * 128, :], vsb[:].rearrange("p h d -> p (h d)"),
                      reads=["vsb"], writes=[("v_d", t)])
                head_rmsnorm(P, ksb[:], gain, sq, ssum, ["ksb"], "K")
                apply_rope(P, "dve", ksb[:], rt["cosa"], rt["sina"], t, 16, rtmp, ["ksb"], "rK")
                P.op("act", lambda e: e.copy(out=kbf[:], in_=ksb[:]), reads=["ksb"], writes=["kbf"])
                for h in range(8):
                    P.op("pe", lambda e, h=h: e.transpose(out=pk[:, h, :], in_=kbf[:, h, :], identity=ident[:]),
                         reads=["kbf", "ident"], writes=["pk"])
                P.op("dve", lambda e: e.tensor_copy(out=kTs[:], in_=pk[:]), reads=["pk"], writes=["kTs"])
                P.dma("sp", K.kT_d.rearrange("h p t -> p h t")[:, :, t * 128:(t + 1) * 128], kTs[:],
                      reads=["kTs"], writes=[("kT_d", t)])
                apply_rope(P, "pool", iksb[:], rt["cosi"], rt["sini"], t, 8, rtmp, ["iksb"], "rI")
                P.op("act", lambda e: e.copy(out=ikbf[:], in_=iksb[:, 0, :]), reads=["iksb"], writes=["ikbf"])
                P.op("pe", lambda e: e.transpose(out=pk[0:64, 0, :], in_=ikbf[:], identity=ident[:]),
                     reads=["ikbf", "ident"], writes=["pk"])
                P.op("dve", lambda e: e.tensor_copy(out=ikTs[:], in_=pk[0:64, 0, :]), reads=["pk"], writes=["ikTs"])
                P.dma("sp", K.ikT_d[:, t * 128:(t + 1) * 128], ikTs[:], reads=["ikTs"], writes=[("ikT_d", t)])
        P.flush()

RW0 = 4176
NRW = 1216
RW_GROUPS = [(i * 128, 128) for i in range(6)] + [(768, 96), (864, 96), (960, 128), (1088, 128)]


def phase1b_rwkv_proj(K):
    nc, P = K.nc, K.P
    with contextlib.ExitStack() as st:
        W = st.enter_context(nc.sbuf_tensor("Wr", [128, 16, NRW], BF16))
        stg = [st.enter_context(nc.sbuf_tensor("wstgb%d" % i, [128, 4, 512], F32)) for i in range(2)]
        hT = [st.enter_context(nc.sbuf_tensor("hTb%d" % i, [128, 16, 512], BF16)) for i in range(2)]
        ost = [st.enter_context(nc.sbuf_tensor("ost%d" % i, [128, 512], F32)) for i in range(4)]
        pm = [st.enter_context(nc.psum_tensor("pmb%d" % i, [128, 512], F32)) for i in range(4)]
        wkeys = load_weight_bf16(K, P, stg, W, 0, K.w_in_rw, NRW, "Wr")
        cnt = 0
        for blk in range(8):
            hs = blk % 2
            P.dma("sp", hT[hs][:], K.hT_d.rearrange("k p t -> p k t")[:, :, blk * 512:(blk + 1) * 512],
                  writes=[("hTb", hs)])
            for (r0, m) in RW_GROUPS:
                s4 = cnt % 4
                cnt += 1
                for k in range(16):
                    P.op("pe", lambda e, k=k, r0=r0, m=m, hs=hs, s4=s4: e.matmul(
                        pm[s4][0:m, :], lhsT=W[:, k, r0:r0 + m], rhs=hT[hs][:, k, :],
                        start=(k == 0), stop=(k == 15)), reads=[("hTb", hs)] + wkeys, writes=[("pmb", s4)])
                if cnt % 2 == 0:
                    P.op("act", lambda e, m=m, s4=s4: e.copy(out=ost[s4][0:m, :], in_=pm[s4][0:m, :]),
                         reads=[("pmb", s4)], writes=[("ost", s4)])
                else:
                    P.op("dve", lambda e, m=m, s4=s4: e.tensor_copy(out=ost[s4][0:m, :], in_=pm[s4][0:m, :]),
                         reads=[("pmb", s4)], writes=[("ost", s4)])
                P.dma("sp", K.yT_d[r0:r0 + m, blk * 512:(blk + 1) * 512], ost[s4][0:m, :],
                      reads=[("ost", s4)], writes=[("yT_d", r0, blk)])
        P.flush()


def phase2_own_proj(K):
    nc, P = K.nc, K.P
    with contextlib.ExitStack() as st:
        ident = st.enter_context(nc.sbuf_tensor("ident2", [128, 128], BF16))
        make_ident(K, P, ident)
        G1, SH1 = load_G_SH(K, P, st, 0, 1, K.norm1_g, "p2")
        T = norm_tiles_alloc(K, st, "p2")
        hT = [st.enter_context(nc.sbuf_tensor("hTo%d" % i, [128, 16, 512], BF16)) for i in range(2)]
        W = st.enter_context(nc.sbuf_tensor("Wq", [128, 16, 2064], BF16))
        stg = [st.enter_context(nc.sbuf_tensor("wstgq%d" % i, [128, 4, 512], F32)) for i in range(2)]
        wk_q = load_weight_bf16(K, P, stg, W, 0, K.w_in[:, 0:1024], 1024, "Wq")
        wk_iq = load_weight_bf16(K, P, stg, W, 1024, K.w_in[:, 3072:4096], 1024, "Wiq")
        wk_iw = load_weight_bf16(K, P, stg, W, 2048, K.w_in[:, 4160:4176], 16, "Wiw")
        rt = rope_tables(K, P, st, K.pos_own, 8, K.invf_att, K.invf_idx, "ro")
        gain = st.enter_context(nc.sbuf_tensor("qgain", [128, 128], F32))
        P.dma("sp", gain[:], bcast_rows(K.q_norm_g, 128), writes=["gainQ"])
        qsb = st.enter_context(nc.sbuf_tensor("qsb", [128, 8, 128], F32))
        qbf = st.enter_context(nc.sbuf_tensor("qbf", [128, 8, 128], BF16))
        sq = st.enter_context(nc.sbuf_tensor("sq2", [128, 8, 128], F32))
        ssum = st.enter_context(nc.sbuf_tensor("ssum2", [128, 32], F32))
        rtmp = [st.enter_context(nc.sbuf_tensor("rtmpq%d" % i, [128, 16, 16], F32)) for i in range(4)]
        iqsb = st.enter_context(nc.sbuf_tensor("iqsb", [128, 16, 64], F32))
        iqbf = st.enter_context(nc.sbuf_tensor("iqbf", [128, 16, 64], BF16))
        iwsb = st.enter_context(nc.sbuf_tensor("iwsb", [128, 16], F32))
        qTs = st.enter_context(nc.sbuf_tensor("qTs", [128, 8, 128], BF16))
        iqTs = st.enter_context(nc.sbuf_tensor("iqTs", [64, 128, 16], BF16))
        pm = [st.enter_context(nc.psum_tensor("pmq%d" % i, [128, 512], F32)) for i in range(3)]
        pk = st.enter_context(nc.psum_tensor("pkq", [128, 8, 128], BF16))
        for blk in range(2):
            hs = blk % 2
            for ti in range(4):
                norm_block(K, P, T, K.x_own, blk * 4 + ti, G1, SH1, ident, hT[hs], ti)
            for ti in range(4):
                t = blk * 4 + ti
                hk = [("hT", ti, 0), ("hT", ti, 1)]
                for gi, (c0, n, wkeys) in enumerate([(0, 512, wk_q), (512, 512, wk_q), (1024, 512, wk_iq),
                                                     (1536, 512, wk_iq), (2048, 16, wk_iw)]):
                    pb = pm[gi % 3]
                    for k in range(16):
                        P.op("pe", lambda e, pb=pb, k=k, c0=c0, n=n, ti=ti, hs=hs: e.matmul(
                            pb[:, 0:n], lhsT=hT[hs][:, k, ti * 128:(ti + 1) * 128], rhs=W[:, k, c0:c0 + n],
                            start=(k == 0), stop=(k == 15)), reads=hk + wkeys, writes=[("pmq", gi % 3)])
                    if gi < 2:
                        P.op("act", lambda e, pb=pb, gi=gi: e.copy(out=qsb[:, gi * 4:(gi + 1) * 4, :], in_=pb[:, 0:512]),
                             reads=[("pmq", gi % 3)], writes=["qsb"])
                    elif gi < 4:
                        g2 = gi - 2
                        P.op("act", lambda e, pb=pb, g2=g2: e.copy(out=iqsb[:, g2 * 8:(g2 + 1) * 8, :], in_=pb[:, 0:512]),
                             reads=[("pmq", gi % 3)], writes=["iqsb"])
                    else:
                        P.op("act", lambda e, pb=pb: e.activation(out=iwsb[:], in_=pb[:, 0:16], func=AF.Copy, scale=0.25),
                             reads=[("pmq", gi % 3)], writes=["iwsb"])
                P.dma("sp", K.iw_d[t * 128:(t + 1) * 128, :], iwsb[:], reads=["iwsb"], writes=[("iw_d", t)])
                head_rmsnorm(P, qsb[:], gain, sq, ssum, ["qsb"], "Q")
                apply_rope(P, "dve", qsb[:], rt["cosa"], rt["sina"], t, 16, rtmp, ["qsb"], "rQ")
                P.op("act", lambda e: e.copy(out=qbf[:], in_=qsb[:]), reads=["qsb"], writes=["qbf"])
                for h in range(8):
                    P.op("pe", lambda e, h=h: e.transpose(out=pk[:, h, :], in_=qbf[:, h, :], identity=ident[:]),
                         reads=["qbf", "ident"], writes=["pkq"])
                P.op("dve", lambda e: e.tensor_copy(out=qTs[:], in_=pk[:]), reads=["pkq"], writes=["qTs"])
                P.dma("sp", K.qT_d.rearrange("h p t -> p h t")[:, :, t * 128:(t + 1) * 128], qTs[:],
                      reads=["qTs"], writes=[("qT_d", t)])
                apply_rope(P, "pool", iqsb[:], rt["cosi"], rt["sini"], t, 8, rtmp, ["iqsb"], "rIQ")
                P.op("act", lambda e: e.activation(out=iqbf[:], in_=iqsb[:], func=AF.Copy, scale=0.125),
                     reads=["iqsb"], writes=["iqbf"])
                for half in range(2):
                    for hh in range(8):
                        h = half * 8 + hh
                        P.op("pe", lambda e, h=h, hh=hh: e.transpose(out=pk[0:64, hh, :], in_=iqbf[:, h, :],
                                                                      identity=ident[:]),
                             reads=["iqbf", "ident"], writes=["pkq"])
                    P.op("dve", lambda e, half=half: e.tensor_copy(
                        out=iqTs[:, :, half * 8:(half + 1) * 8].rearrange("p t h -> p h t"), in_=pk[0:64, :, :]),
                         reads=["pkq"], writes=["iqTs"])
                P.dma("sp", K.iqT_d[:, t * 128:(t + 1) * 128, :], iqTs[:], reads=["iqTs"], writes=[("iqT_d", t)])
        P.flush()


NIT = 26
SLOT_NK = [4, 8, 12, 16, 20, 24, 28, 32]


def phase3_attention(K):
    nc, P = K.nc, K.P
    with contextlib.ExitStack() as st:
        def sb(name, shape, dt):
            return st.enter_context(nc.sbuf_tensor(name, shape, dt))
        ident = sb("ident3", [128, 128], BF16)
        identf = sb("identf3", [128, 128], F32)
        make_ident(K, P, ident)
        P.op("dve", lambda e: e.tensor_copy(out=identf[:], in_=ident[:]), reads=["ident"], writes=["identf"])
        kT = sb("kTall", [128, 8, S], BF16)
        V = sb("Vall", [128, 32, 1032], BF16)
        ikT = sb("ikTall", [64, S], BF16)
        for h in range(8):
            P.dma("sp", kT[:, h, :], K.kT_d[h], writes=[("kT", h)])
        for q4 in range(4):
            P.dma("sp", V[:, q4 * 8:(q4 + 1) * 8, :],
                  K.v_d.rearrange("(t p) c -> p t c", p=128)[:, q4 * 8:(q4 + 1) * 8, :], writes=[("V", q4)])
        P.dma("sp", ikT[:], K.ikT_d, writes=["ikT"])
        kTk = [("kT", h) for h in range(8)]
        Vk = [("V", q4) for q4 in range(4)]
        Sel = sb("Sel", [128, 16, 128], BF16)
        pidx = sb("pidx", [128, 1], I32)
        pidf = sb("pidf", [128, 1], F32)
        score = sb("score", [128, S], F32)
        self_ = score[:, 0:2048].rearrange("p (g t) -> p g t", g=16)
        sk4 = [("score", q) for q in range(4)]
        P.op("pool", lambda e: e.iota(self_, pattern=[[-8, 16], [1, 128]], base=0, channel_multiplier=0, allow_small_or_imprecise_dtypes=True), writes=sk4)
        P.op("pool", lambda e: e.iota(pidx[:], pattern=[[0, 1]], base=0, channel_multiplier=1), writes=["pidx"])
        P.op("dve", lambda e: e.tensor_scalar(out=pidx[:], in0=pidx[:], scalar1=4, scalar2=None,
                                               op0=ALU.arith_shift_right), reads=["pidx"], writes=["pidx"])
        P.op("dve", lambda e: e.tensor_copy(out=pidf[:], in_=pidx[:]), reads=["pidx"], writes=["pidf"])
        P.op("dve", lambda e: e.tensor_scalar(out=Sel[:], in0=self_, scalar1=pidf[:, 0:1], scalar2=None,
                                               op0=ALU.is_equal), reads=sk4 + ["pidf"], writes=["Sel"])
        kposi = sb("kposi", [128, 512], I32)
        kposf = sb("kposf", [128, 512], F32)
        qpos = sb("qpos", [128, 8], F32)
        P.dma("sp", qpos[:], K.qpos_own, writes=["qpos"])
        iwg = sb("iwg", [128, 128], F32)
        wcol = sb("wcol", [128, 128], F32)
        P.dma("sp", iwg[:], K.iw_d.rearrange("(g t) h -> g (t h)", t=8), writes=["iwg"])
        A = [st.enter_context(nc.psum_tensor("A%d" % i, [128, 512], F32)) for i in range(2)]
        B = [st.enter_context(nc.psum_tensor("B%d" % i, [128, 512], F32)) for i in range(2)]
        C = st.enter_context(nc.psum_tensor("C3", [128, 8, 128], BF16))
        P.op("pe", lambda e: e.transpose(out=A[0][:, 0:128], in_=iwg[:], identity=identf[:]),
             reads=["iwg", "identf"], writes=[("A", 0)])
        P.op("dve", lambda e: e.tensor_copy(out=wcol[:], in_=A[0][:, 0:128]), reads=[("A", 0)], writes=["wcol"])
        mask01 = sb("mask01", [128, S], BF16)
        maskT = sb("maskT", [128, 32, 128], BF16)
        R = [sb("R%d" % i, [128, 512], BF16) for i in range(2)]
        pexp = [sb("pexp%d" % i, [128, 512], BF16) for i in range(2)]
        pmk = [sb("pmk%d" % i, [128, 512], BF16) for i in range(2)]
        iqTs = sb("iqTs3", [64, 128, 16], BF16)
        qTs = sb("qTs3", [128, 8, 128], BF16)
        att = sb("att", [128, 8, 128], BF16)
        attTs = sb("attTs", [128, 8, 128], BF16)
        bias = sb("cbias", [128, 512], F32)
        c2 = sb("c2", [128, NIT], F32)
        steps = sb("steps", [128, NIT], F32)
        sm = sb("sm3", [128, 8], F32)
        for k in range(NIT):
            P.op("pool", lambda e, k=k: e.memset(c2[:, k:k + 1], float(2.0 ** -(k + 1))), writes=["c2"])
        for i in range(8):
            nk = SLOT_NK[i]
            nb = nk // 4
            L = nk * 128
            P.dma("sp", iqTs[:], K.iqT_d[:, i * 128:(i + 1) * 128, :], writes=["iqTs"])
            P.dma("sp", qTs[:], K.qT_d.rearrange("h p t -> p h t")[:, :, i * 128:(i + 1) * 128], writes=["qTs"])
            for sbk in range(nb):
                bsl = sbk % 2
                for g in range(16):
                    a = (sbk * 16 + g) % 2
                    lhsT = iqTs[:, g * 8:(g + 1) * 8, :].rearrange("p t h -> p (t h)")
                    P.op("pe", lambda e, a=a, lhsT=lhsT, sbk=sbk: e.matmul(
                        A[a][:, :], lhsT=lhsT, rhs=ikT[:, sbk * 512:(sbk + 1) * 512], start=True, stop=True),
                        reads=["iqTs", "ikT"], writes=[("A", a)])
                    G = i * 16 + g
                    P.op("dve", lambda e, a=a, G=G: e.tensor_scalar(
                        out=R[a][:], in0=A[a][:, :], scalar1=0.0, scalar2=wcol[:, G:G + 1],
                        op0=ALU.max, op1=ALU.mult), reads=[("A", a), "wcol"], writes=[("R", a)])
                    P.op("pe", lambda e, a=a, g=g, bsl=bsl: e.matmul(
                        B[bsl][:, :], lhsT=Sel[:, g, :], rhs=R[a][:], start=(g == 0), stop=(g == 15)),
                        reads=[("R", a), "Sel"], writes=[("B", bsl)])
                P.op("act", lambda e, bsl=bsl, sbk=sbk: e.copy(out=score[:, sbk * 512:(sbk + 1) * 512], in_=B[bsl][:, :]),
                     reads=[("B", bsl)], writes=[("score", sbk)])
            sck = [("score", sbk) for sbk in range(nb)]
            P.op("dve", lambda e, L=L: e.tensor_reduce(out=sm[:, 0:1], in_=score[:, 0:L], axis=AX.X, op=ALU.max,
                                                        apply_absolute_value=True), reads=sck, writes=["sm0"])
            P.op("pool", lambda e, nb=nb: e.iota(kposi[:], pattern=[[1, 512]], base=(nb - 1) * 512, channel_multiplier=0),
                 writes=["kposi"])
            P.op("dve", lambda e: e.tensor_copy(out=kposf[:], in_=kposi[:]), reads=["kposi"], writes=["kposf"])
            P.op("dve", lambda e, i=i: e.tensor_scalar(out=bias[:], in0=kposf[:], scalar1=qpos[:, i:i + 1],
                                                        scalar2=-1e30, op0=ALU.is_gt, op1=ALU.mult),
                 reads=["kposf", "qpos"], writes=["bias"])
            P.op("dve", lambda e, nb=nb: e.tensor_tensor(out=score[:, (nb - 1) * 512:nb * 512],
                                                          in0=score[:, (nb - 1) * 512:nb * 512], in1=bias[:], op=ALU.add),
                 reads=["bias", ("score", nb - 1), "sm0"], writes=[("score", nb - 1)])
            P.op("dve", lambda e: e.tensor_scalar(out=sm[:, 1:2], in0=sm[:, 0:1], scalar1=-1.0, scalar2=-1.0,
                                                   op0=ALU.mult, op1=ALU.add), reads=["sm0"], writes=["lo"])
            P.op("dve", lambda e: e.tensor_scalar(out=sm[:, 5:6], in0=sm[:, 0:1], scalar1=2.0, scalar2=2.0,
                                                   op0=ALU.mult, op1=ALU.add), reads=["sm0"], writes=["d0"])
            P.op("dve", lambda e: e.tensor_scalar(out=steps[:], in0=c2[:], scalar1=sm[:, 5:6], scalar2=None,
                                                   op0=ALU.mult), reads=["d0", "c2"], writes=["steps"])
            for k in range(NIT):
                P.op("dve", lambda e, k=k: e.tensor_tensor(out=sm[:, 2:3], in0=sm[:, 1:2], in1=steps[:, k:k + 1],
                                                            op=ALU.add), reads=["lo", "steps"], writes=["mid"])
                P.op("dve", lambda e, L=L: e.tensor_scalar(out=mask01[:, 0:L], in0=score[:, 0:L], scalar1=sm[:, 2:3],
                                                            scalar2=None, op0=ALU.is_ge, op1=ALU.add,
                                                            accum_out=sm[:, 3:4]),
                     reads=sck + ["mid"], writes=["mask01", "cnt"])
                P.op("dve", lambda e, k=k: e.scalar_tensor_tensor(out=sm[:, 4:5], in0=sm[:, 3:4], scalar=255.5,
                                                                   in1=steps[:, k:k + 1], op0=ALU.is_ge, op1=ALU.mult),
                     reads=["cnt", "steps"], writes=["inc"])
                P.op("dve", lambda e: e.tensor_tensor(out=sm[:, 1:2], in0=sm[:, 1:2], in1=sm[:, 4:5], op=ALU.add),
                     reads=["lo", "inc"], writes=["lo"])
            P.op("dve", lambda e, L=L: e.tensor_scalar(out=mask01[:, 0:L], in0=score[:, 0:L], scalar1=sm[:, 1:2],
                                                        scalar2=None, op0=ALU.is_ge), reads=sck + ["lo"], writes=["mask01"])
            for kt in range(nk):
                P.op("pe", lambda e, kt=kt: e.transpose(out=C[:, kt % 8, :], in_=mask01[:, kt * 128:(kt + 1) * 128],
                                                         identity=ident[:]), reads=["mask01", "ident"], writes=["C"])
                if kt % 8 == 7 or kt == nk - 1:
                    k0 = (kt // 8) * 8
                    n8 = kt - k0 + 1
                    P.op("act", lambda e, k0=k0, n8=n8: e.copy(out=maskT[:, k0:k0 + n8, :], in_=C[:, 0:n8, :]),
                         reads=["C"], writes=[("maskT", k0 // 8)])
            mk = [("maskT", q) for q in range((nk + 7) // 8)]
            for h in range(8):
                bsl = h % 2
                for kg in range(nb):
                    a = (h * nb + kg) % 2
                    for j4 in range(4):
                        kt = kg * 4 + j4
                        P.op("pe", lambda e, a=a, j4=j4, kt=kt, h=h: e.matmul(
                            A[a][:, j4 * 128:(j4 + 1) * 128], lhsT=kT[:, h, kt * 128:(kt + 1) * 128], rhs=qTs[:, h, :],
                            start=True, stop=True), reads=kTk + ["qTs"], writes=[("A", a)])
                    P.op("act", lambda e, a=a: e.activation(out=pexp[a][:], in_=A[a][:, :], func=AF.Exp,
                                                             scale=float(128 ** -0.5)),
                         reads=[("A", a)], writes=[("pexp", a)])
                    P.op("pool", lambda e, a=a, kg=kg: e.tensor_tensor(
                        out=pmk[a][:], in0=pexp[a][:], in1=maskT[:, kg * 4:(kg + 1) * 4, :].rearrange("p a t -> p (a t)"),
                        op=ALU.mult), reads=[("pexp", a)] + mk, writes=[("pmk", a)])
                    for j4 in range(4):
                        kt = kg * 4 + j4
                        P.op("pe", lambda e, a=a, j4=j4, kt=kt, h=h, bsl=bsl, kg=kg: e.matmul(
                            B[bsl][:, 0:129], lhsT=pmk[a][:, j4 * 128:(j4 + 1) * 128], rhs=V[:, kt, h * 129:(h + 1) * 129],
                            start=(kg == 0 and j4 == 0), stop=(kg == nb - 1 and j4 == 3)),
                            reads=[("pmk", a)] + Vk, writes=[("B", bsl)])
                P.op("dve", lambda e, bsl=bsl: e.reciprocal(out=sm[:, 6:7], in_=B[bsl][:, 128:129]),
                     reads=[("B", bsl)], writes=["rcp"])
                P.op("dve", lambda e, bsl=bsl, h=h: e.tensor_scalar(out=att[:, h, :], in0=B[bsl][:, 0:128],
                                                                     scalar1=sm[:, 6:7], scalar2=None, op0=ALU.mult),
                     reads=[("B", bsl), "rcp"], writes=["att"])
            for h in range(8):
                P.op("pe", lambda e, h=h: e.transpose(out=C[:, h, :], in_=att[:, h, :], identity=ident[:]),
                     reads=["att", "ident"], writes=["C"])
            P.op("act", lambda e: e.copy(out=attTs[:], in_=C[:]), reads=["C"], writes=["attTs"])
            P.dma("sp", K.attT_d.rearrange("h p t -> p h t")[:, :, i * 128:(i + 1) * 128], attTs[:],
                  reads=["attTs"], writes=[("attT_d", i)])
        P.flush()

RD = BF16
NCH = 64


def tok_shift(P, dst, raw, tmp, mu_ap, rk_raw, k_tmp, k_dst, n=128):
    P.op("pool", lambda e: e.tensor_tensor(out=tmp[0:n, 1:S], in0=raw[0:n, 0:S - 1], in1=raw[0:n, 1:S], op=ALU.subtract),
         reads=[rk_raw], writes=[k_tmp])
    P.op("pool", lambda e: e.tensor_scalar(out=tmp[0:n, 0:1], in0=raw[0:n, 0:1], scalar1=-1.0, scalar2=0.0,
                                            op0=ALU.mult, op1=ALU.add), reads=[rk_raw, k_tmp], writes=[k_tmp])
    P.op("dve", lambda e: e.scalar_tensor_tensor(out=dst[0:n, :], in0=tmp[0:n, :], scalar=mu_ap, in1=raw[0:n, :],
                                                  op0=ALU.mult, op1=ALU.add), reads=[rk_raw, k_tmp], writes=[k_dst])


def phase4b_rwkv_prep(K, cts=range(2)):
    nc, P = K.nc, K.P
    with contextlib.ExitStack() as st:
        def sb(name, shape, dt):
            return st.enter_context(nc.sbuf_tensor(name, shape, dt))
        txw = sb("txw", [96, S], BF16)
        xap = sb("xap", [96, S], BF16)
        sxg = sb("sxg", [128, 2, S], BF16)
        M01 = sb("M01", [128, S], BF16)
        wup = sb("wup", [96, 256], BF16)
        aup = sb("aup", [96, 256], BF16)
        gup = sb("gup", [128, 2, 256], BF16)
        wst = sb("wst4", [128, 2, 256], F32)
        bones = sb("bones", [128, 128], BF16)
        prm = sb("prm", [128, 12, 2], F32)
        mul = sb("mul", [128, 4], F32)
        PT = sb("PT", [128, S], F32)
        KK = sb("KK", [128, S], F32)
        KP = sb("KP", [128, S], F32)
        CL = sb("CL", [128, S], F32)
        RP = sb("RP", [128, S], BF16)
        VP = sb("VP", [128, S], BF16)
        AA = sb("AA", [128, S], BF16)
        K2 = sb("K2", [128, S], BF16)
        SQb = sb("SQb", [128, S], BF16)
        OUT = [sb("OUT%d" % i, [128, S], BF16) for i in range(2)]
        PCt = sb("PCt", [128, NCH], F32)
        ps = [st.enter_context(nc.psum_tensor("ps4_%d" % i, [128, 512], F32)) for i in range(4)]
        for i, ap in enumerate(K.rw_prm):
            P.dma("sp", prm[:, i, :], ap, writes=[("prm", i)])
        prk = [("prm", i) for i in range(10)]
        P.op("dve", lambda e: e.tensor_scalar(out=prm[:, 10, :], in0=prm[:, 6, :], scalar1=-1.0, scalar2=1.0,
                                               op0=ALU.mult, op1=ALU.add), reads=prk, writes=[("prm", 10)])
        prk = prk + [("prm", 10)]
        P.dma("sp", mul[:], K.rw_mul, writes=["mul"])
        P.op("pool", lambda e: e.memset(bones[:], 0.0), writes=["bones"])
        P.op("pool", lambda e: e.memset(bones[0:64, 0:64], 1.0), reads=["bones"], writes=["bones"])
        P.op("pool", lambda e: e.memset(bones[64:128, 64:128], 1.0), reads=["bones"], writes=["bones"])
        P.op("pool", lambda e: e.iota(PT[:].rearrange("p (c t) -> p c t", t=64), pattern=[[0, NCH], [1, 64]], base=0,
                                      channel_multiplier=0, allow_small_or_imprecise_dtypes=True), writes=["PT"])
        P.op("dve", lambda e: e.tensor_scalar(out=M01[:], in0=PT[:], scalar1=0.5, scalar2=None, op0=ALU.is_gt),
             reads=["PT"], writes=["M01"])
        P.dma("sp", wst[0:96, 0, :], K.rw_w_up, writes=["wst"])
        P.op("act", lambda e: e.copy(out=wup[:], in_=wst[0:96, 0, :]), reads=["wst"], writes=["wup"])
        P.dma("sp", wst[0:96, 1, :], K.rw_a_up, reads=[], writes=["wst1"])
        P.op("act", lambda e: e.copy(out=aup[:], in_=wst[0:96, 1, :]), reads=["wst1"], writes=["aup"])
        P.dma("sp", wst[:, :, :], K.rw_g_up.rearrange("(c p) n -> p c n", p=128), reads=[], writes=["wst", "wst1"])
        P.op("act", lambda e: e.copy(out=gup[:], in_=wst[:]), reads=["wst", "wst1"], writes=["gup"])
        for (r0, n, mcol, func, dst, kd) in ((768, 96, 0, AF.Tanh, txw[:, :], "txw"), (864, 96, 1, AF.Copy, xap[:, :], "xap"),
                                             (960, 128, 2, AF.Sigmoid, sxg[:, 0, :], "sxg0"),
                                             (1088, 128, 3, AF.Sigmoid, sxg[:, 1, :], "sxg1")):
            P.dma("sp", PT[0:n, :], K.yT_d[r0:r0 + n, :], writes=["PT"])
            tok_shift(P, KP, PT, KK, mul[0:n, mcol:mcol + 1], "PT", "KK", "KP", n=n)
            P.op("act", lambda e, n=n, func=func, dst=dst: e.activation(out=dst, in_=KP[0:n, :], func=func),
                 reads=["KP"], writes=[kd])
        lk = ["txw", "xap", "sxg0", "sxg1"]
        oc = 0
        for ct in cts:
            c0 = ct * 128
            P.dma("sp", PT[:], K.yT_d[c0:c0 + 128, :], writes=["PT"])
            tok_shift(P, RP, PT, KK, prm[:, 0, ct:ct + 1], "PT", "KK", "RP")
            P.dma("sp", PT[:], K.yT_d[256 + c0:256 + c0 + 128, :], writes=["PT"])
            tok_shift(P, KP, PT, KK, prm[:, 1, ct:ct + 1], "PT", "KK", "KP")
            P.dma("sp", PT[:], K.yT_d[512 + c0:512 + c0 + 128, :], writes=["PT"])
            tok_shift(P, VP, PT, KK, prm[:, 2, ct:ct + 1], "PT", "KK", "VP")
            P.dma("sp", K.vb_d[c0:c0 + 128, :], VP[:], reads=["VP"], writes=[("vb_d", ct)])
            for blk in range(8):
                bs = slice(blk * 512, (blk + 1) * 512)
                p0, p1, p2 = ps[0], ps[1], ps[2]
                P.op("pe", lambda e, bs=bs, c0=c0: e.matmul(ps[0][:, :], lhsT=wup[:, c0:c0 + 128], rhs=txw[:, bs],
                                                             start=True, stop=True), reads=["wup", "txw"], writes=[("ps4", 0)])
                P.op("act", lambda e, bs=bs, ct=ct: e.activation(out=CL[:, bs], in_=ps[0][:, :], func=AF.Sigmoid,
                                                                  bias=prm[:, 3, ct:ct + 1]),
                     reads=[("ps4", 0)] + prk, writes=["CL"])
                P.op("pe", lambda e, bs=bs, c0=c0: e.matmul(ps[1][:, :], lhsT=aup[:, c0:c0 + 128], rhs=xap[:, bs],
                                                             start=True, stop=True), reads=["aup", "xap"], writes=[("ps4", 1)])
                P.op("act", lambda e, bs=bs, ct=ct: e.activation(out=AA[:, bs], in_=ps[1][:, :], func=AF.Sigmoid,
                                                                  bias=prm[:, 4, ct:ct + 1]),
                     reads=[("ps4", 1)] + prk, writes=["AA"])
                for cc in range(2):
                    P.op("pe", lambda e, bs=bs, c0=c0, cc=cc: e.matmul(ps[2][:, :], lhsT=gup[:, cc, c0:c0 + 128],
                                                                       rhs=sxg[:, cc, bs], start=(cc == 0), stop=(cc == 1)),
                         reads=["gup", "sxg0", "sxg1"], writes=[("ps4", 2)])
                o = OUT[oc % 2]
                P.op("dve", lambda e, bs=bs, o=o: e.tensor_copy(out=o[:, bs], in_=ps[2][:, :]),
                     reads=[("ps4", 2)], writes=[("OUT", oc % 2)])
            P.dma("sp", K.G_d[c0:c0 + 128, :], OUT[oc % 2][:], reads=[("OUT", oc % 2)], writes=[("G_d", ct)])
            oc += 1
            P.op("dve", lambda e: e.tensor_scalar(out=CL[:], in0=CL[:], scalar1=-0.6065306597126334, scalar2=None,
                                                   op0=ALU.mult), reads=["CL"], writes=["CL"])
            P.op("dve", lambda e, ct=ct: e.tensor_scalar(out=KK[:], in0=KP[:], scalar1=prm[:, 5, ct:ct + 1], scalar2=None,
                                                          op0=ALU.mult), reads=["KP"] + prk, writes=["KK"])
            P.op("act", lambda e: e.activation(out=SQb[:], in_=KK[:], func=AF.Square), reads=["KK"], writes=["SQb"])
            for blk in range(8):
                bs = slice(blk * 512, (blk + 1) * 512)
                P.op("pe", lambda e, bs=bs: e.matmul(ps[3][:, :], lhsT=bones[:], rhs=SQb[:, bs], start=True, stop=True),
                     reads=["bones", "SQb"], writes=[("ps4", 3)])
                P.op("act", lambda e, bs=bs: e.activation(out=PT[:, bs], in_=ps[3][:, :], func=AF.Sqrt),
                     reads=[("ps4", 3)], writes=["PT"])
            P.op("dve", lambda e: e.tensor_scalar(out=PT[:], in0=PT[:], scalar1=1e-12, scalar2=None, op0=ALU.max),
                 reads=["PT"], writes=["PT"])
            P.op("dve", lambda e: e.reciprocal(out=PT[:], in_=PT[:]), reads=["PT"], writes=["PT"])
            P.op("dve", lambda e: e.tensor_tensor(out=KK[:], in0=KK[:], in1=PT[:], op=ALU.mult), reads=["KK", "PT"], writes=["KK"])
            P.op("dve", lambda e, ct=ct: e.tensor_scalar(out=PT[:], in0=AA[:], scalar1=prm[:, 6, ct:ct + 1],
                                                          scalar2=prm[:, 10, ct:ct + 1], op0=ALU.mult, op1=ALU.add),
                 reads=["AA", "PT"] + prk, writes=["PT"])
            P.op("dve", lambda e: e.tensor_tensor(out=K2[:], in0=KP[:], in1=PT[:], op=ALU.mult), reads=["KP", "PT"], writes=["K2"])
            P.op("dve", lambda e, ct=ct: e.scalar_tensor_tensor(out=SQb[:], in0=RP[:], scalar=prm[:, 7, ct:ct + 1], in1=K2[:],
                                                                 op0=ALU.mult, op1=ALU.mult),
                 reads=["RP", "K2", "SQb"] + prk, writes=["SQb"])
            o = OUT[oc % 2]
            for blk in range(8):
                bs = slice(blk * 512, (blk + 1) * 512)
                P.op("pe", lambda e, bs=bs: e.matmul(ps[3][:, :], lhsT=bones[:], rhs=SQb[:, bs], start=True, stop=True),
                     reads=["bones", "SQb"], writes=[("ps4", 3)])
                P.op("dve", lambda e, bs=bs, o=o: e.tensor_tensor(out=o[:, bs], in0=ps[3][:, :], in1=VP[:, bs], op=ALU.mult),
                     reads=[("ps4", 3), "VP"], writes=[("OUT", oc % 2)])
            P.dma("sp", K.BON_d[c0:c0 + 128, :], o[:], reads=[("OUT", oc % 2)], writes=[("BON_d", ct)])
            oc += 1
            P.op("dve", lambda e: e.tensor_tensor_scan(out=PT[:], data0=M01[:], data1=CL[:], initial=0.0,
                                                        op0=ALU.mult, op1=ALU.add), reads=["M01", "CL", "PT"], writes=["PT"])
            P.op("pool", lambda e: e.tensor_tensor(out=CL[:], in0=PT[:], in1=CL[:], op=ALU.subtract),
                 reads=["PT", "CL"], writes=["CL"])
            P.op("act", lambda e: e.activation(out=CL[:], in_=CL[:], func=AF.Exp), reads=["CL"], writes=["CL"])
            v3 = lambda t: t[:].rearrange("p (c t) -> p c t", t=64)
            o = OUT[oc % 2]
            P.op("dve", lambda e, o=o: e.scalar_tensor_tensor(out=o[:], in0=KK[:], scalar=-1.0, in1=CL[:],
                                                               op0=ALU.mult, op1=ALU.mult),
                 reads=["KK", "CL"], writes=[("OUT", oc % 2)])
            P.dma("sp", K.AH_d[c0:c0 + 128, :], o[:], reads=[("OUT", oc % 2)], writes=[("AH_d", ct)])
            oc += 1
            P.op("act", lambda e: e.activation(out=CL[:], in_=PT[:], func=AF.Exp), reads=["PT", "CL"], writes=["CL"])
            o = OUT[oc % 2]
            P.op("dve", lambda e, o=o: e.tensor_tensor(out=o[:], in0=RP[:], in1=CL[:], op=ALU.mult),
                 reads=["RP", "CL"], writes=[("OUT", oc % 2)])
            P.dma("sp", K.RH_d[c0:c0 + 128, :], o[:], reads=[("OUT", oc % 2)], writes=[("RH_d", ct)])
            oc += 1
            P.op("pool", lambda e: e.tensor_copy(out=PCt[:], in_=v3(CL)[:, :, 63]), reads=["CL"], writes=["PCt"])
            P.dma("sp", K.PC_d[c0:c0 + 128, :], PCt[:], reads=["PCt"], writes=[("PC_d", ct)])
            P.op("act", lambda e: e.activation(out=PT[:], in_=PT[:], func=AF.Exp, scale=-1.0), reads=["PT"], writes=["PT"])
            o = OUT[oc % 2]
            P.op("dve", lambda e, o=o: e.tensor_tensor(out=o[:], in0=K2[:], in1=PT[:], op=ALU.mult),
                 reads=["K2", "PT"], writes=[("OUT", oc % 2)])
            P.dma("sp", K.KH_d[c0:c0 + 128, :], o[:], reads=[("OUT", oc % 2)], writes=[("KH_d", ct)])
            oc += 1
            P.op("dve", lambda e: e.tensor_tensor(out=KK[:], in0=KK[:], in1=AA[:], op=ALU.mult), reads=["KK", "AA"], writes=["KK"])
            o = OUT[oc % 2]
            P.op("dve", lambda e, o=o: e.tensor_tensor(out=o[:], in0=KK[:], in1=PT[:], op=ALU.mult),
                 reads=["KK", "PT"], writes=[("OUT", oc % 2)])
            P.dma("sp", K.BH_d[c0:c0 + 128, :], o[:], reads=[("OUT", oc % 2)], writes=[("BH_d", ct)])
            oc += 1
        P.flush()

def phase4c_rwkv_scan(K, heads=range(4)):
    nc, P = K.nc, K.P
    with contextlib.ExitStack() as st:
        def sb(name, shape, dt):
            return st.enter_context(nc.sbuf_tensor(name, shape, dt))
        ident = sb("ident4", [128, 128], BF16)
        make_ident(K, P, ident)
        MaskG = sb("MaskG", [64, 4, 128], F32)
        MaskX = sb("MaskX", [64, 8, 64], F32)
        I8 = sb("I8", [64, 64], F32)
        ones = sb("ones4", [64, 64], F32)
        P.op("pool", lambda e: e.memset(ones[:], 1.0), writes=["ones"])
        for a in range(4):
            for cq in range(2):
                P.op("pool", lambda e, cq=cq, a=a: e.affine_select(
                    out=MaskG[:, a, cq * 64:(cq + 1) * 64], in_=ones[:], pattern=[[1, 64]],
                    compare_op=(ALU.is_gt if cq == 0 else ALU.is_ge), fill=0.0, base=0, channel_multiplier=-1),
                    reads=["ones"], writes=["MaskG"])
        for a in range(8):
            P.op("pool", lambda e, a=a: e.affine_select(out=MaskX[:, a, :], in_=ones[:], pattern=[[-1, 64]],
                                                         compare_op=ALU.is_gt, fill=0.0, base=0, channel_multiplier=1),
                 reads=["ones"], writes=["MaskX"])
        P.op("dve", lambda e: e.tensor_copy(out=I8[:], in_=ident[0:64, 0:64]), reads=["ident"], writes=["I8"])
        AH = sb("AH", [64, S], RD)
        RH = sb("RH", [64, S], RD)
        BH = sb("BH", [64, S], RD)
        KH = sb("KH", [64, S], RD)
        vb = sb("vb", [64, S], BF16)
        PC = sb("PC", [64, NCH], F32)
        ARh = sb("ARh", [64, NCH, 128], RD)
        BKh = sb("BKh", [64, NCH, 128], RD)
        GmB = sb("GmB", [64, NCH, 128], RD)
        GmK = sb("GmK", [64, NCH, 128], RD)
        Btok = sb("Btok", [64, NCH, 64], RD)
        Ktok = sb("Ktok", [64, NCH, 64], RD)
        Vtok = sb("Vtok", [64, NCH, 64], RD)
        X0 = sb("X0", [64, NCH, 64], RD)
        Pm = sb("Pm", [64, NCH, 64], RD)
        oT = sb("oT", [64, S], F32)
        Ast = sb("Ast", [64, 64], F32)
        Abf = sb("Abf", [64, 64], RD)
        Tt = sb("Tt", [64, 64], F32)
        Xs = sb("Xs", [64, 64], RD)
        Us = sb("Us", [64, 64], RD)
        PSb = st.enter_context(nc.psum_tensor("PSb", [128, 1024], BF16))
        PS = [st.enter_context(nc.psum_tensor("PS%d" % i, [128, 512], F32)) for i in range(7)]
        v3 = lambda t: t[:].rearrange("p (c t) -> p c t", t=64)
        Nb = [v3(AH), v3(RH)]
        Xb = [v3(BH), v3(KH)]
        Nk = ["AH", "RH"]
        Xk = ["BH", "KH"]
        for hd in heads:
            r0 = hd * 64
            P.dma("sp", AH[:], K.AH_d[r0:r0 + 64, :], writes=["AH"])
            P.dma("sp", RH[:], K.RH_d[r0:r0 + 64, :], writes=["RH"])
            P.dma("sp", BH[:], K.BH_d[r0:r0 + 64, :], writes=["BH"])
            P.dma("sp", KH[:], K.KH_d[r0:r0 + 64, :], writes=["KH"])
            P.dma("sp", vb[:], K.vb_d[r0:r0 + 64, :], writes=["vb"])
            P.dma("sp", PC[:], K.PC_d[r0:r0 + 64, :], writes=["PC"])
            P.op("dve", lambda e: e.tensor_copy(out=ARh[:, :, 0:64], in_=v3(AH)), reads=["AH"], writes=["ARh"])
            P.op("pool", lambda e: e.tensor_copy(out=ARh[:, :, 64:128], in_=v3(RH)), reads=["RH"], writes=["ARh"])
            P.op("dve", lambda e: e.tensor_copy(out=BKh[:, :, 0:64], in_=v3(BH)), reads=["BH"], writes=["BKh"])
            P.op("pool", lambda e: e.tensor_copy(out=BKh[:, :, 64:128], in_=v3(KH)), reads=["KH"], writes=["BKh"])
            for (src, srck, col0, dst, dk) in ((BKh, "BKh", 0, Btok, "Btok"), (BKh, "BKh", 64, Ktok, "Ktok"), (None, "vb", 0, Vtok, "Vtok")):
                for c16 in range(0, NCH, 16):
                    for cc in range(16):
                        c = c16 + cc
                        in_ = vb[:, c * 64:(c + 1) * 64] if src is None else src[:, c, col0:col0 + 64]
                        P.op("pe", lambda e, cc=cc, in_=in_: e.transpose(out=PSb[0:64, cc * 64:(cc + 1) * 64], in_=in_,
                                                                         identity=ident[0:64, 0:64]),
                             reads=[srck, "ident"], writes=["PSb"])
                    P.op("act", lambda e, c16=c16, dst=dst: e.copy(out=dst[:, c16:c16 + 16, :].rearrange("p c k -> p (c k)"),
                                                                    in_=PSb[0:64, :]), reads=["PSb"], writes=[dk])
            gi = 0
            for (col0, dst, dk) in ((0, GmB, "GmB"), (64, GmK, "GmK")):
                for c4 in range(0, NCH, 4):
                    b = gi % 2
                    gi += 1
                    for cc in range(4):
                        c = c4 + cc
                        P.op("pe", lambda e, c=c, cc=cc, b=b, col0=col0: e.matmul(
                            PS[b][0:64, cc * 128:(cc + 1) * 128], lhsT=BKh[:, c, col0:col0 + 64], rhs=ARh[:, c, :],
                            start=True, stop=True), reads=["BKh", "ARh"], writes=[("PS", b)])
                    P.op("dve", lambda e, c4=c4, dst=dst, b=b: e.tensor_tensor(
                        out=dst[:, c4:c4 + 4, :], in0=PS[b][0:64, :].rearrange("p (a t) -> p a t", t=128), in1=MaskG[:],
                        op=ALU.mult), reads=[("PS", b), "MaskG"], writes=[dk])
            for c8 in range(0, NCH, 8):
                for cc in range(8):
                    c = c8 + cc
                    P.op("pe", lambda e, c=c, cc=cc: e.matmul(PS[2][0:64, cc * 64:(cc + 1) * 64], lhsT=ARh[:, c, 0:64],
                                                               rhs=BKh[:, c, 0:64], start=True, stop=True),
                         reads=["ARh", "BKh"], writes=[("PS", 2)])
                P.op("dve", lambda e, c8=c8: e.tensor_tensor(
                    out=X0[:, c8:c8 + 8, :], in0=PS[2][0:64, :].rearrange("p (a t) -> p a t", t=64), in1=MaskX[:],
                    op=ALU.mult), reads=[("PS", 2), "MaskX"], writes=["X0"])
            N0 = GmB[:, :, 0:64]
            P.op("pool", lambda e, N0=N0: e.tensor_tensor(out=Pm[:], in0=N0, in1=I8[:].unsqueeze(1).to_broadcast([64, NCH, 64]),
                                                          op=ALU.add), reads=["GmB", "I8"], writes=["Pm"])
            curN, curNk = N0, "GmB"
            curX, curXk = X0[:], "X0"
            for lvl in range(1, 6):
                nX, nXk = Xb[lvl % 2], Xk[lvl % 2]
                nN, nNk = Nb[lvl % 2], Nk[lvl % 2]
                for c8 in range(0, NCH, 8):
                    for cc in range(8):
                        c = c8 + cc
                        P.op("pe", lambda e, c=c, cc=cc, curN=curN, curX=curX: e.matmul(
                            PS[3][0:64, cc * 64:(cc + 1) * 64], lhsT=curN[:, c, :], rhs=curX[:, c, :], start=True, stop=True),
                            reads=[curNk, curXk], writes=[("PS", 3)])
                    P.op("act", lambda e, c8=c8, nX=nX: e.copy(out=nX[:, c8:c8 + 8, :],
                                                                in_=PS[3][0:64, :].rearrange("p (a t) -> p a t", t=64)),
                         reads=[("PS", 3)], writes=[nXk])
                    if lvl < 5:
                        for cc in range(8):
                            c = c8 + cc
                            P.op("pe", lambda e, c=c, cc=cc, curN=curN, curX=curX: e.matmul(
                                PS[4][0:64, cc * 64:(cc + 1) * 64], lhsT=curX[:, c, :], rhs=curN[:, c, :], start=True, stop=True),
                                reads=[curNk, curXk], writes=[("PS", 4)])
                        P.op("dve", lambda e, c8=c8, nN=nN: e.tensor_copy(out=nN[:, c8:c8 + 8, :],
                                                                           in_=PS[4][0:64, :].rearrange("p (a t) -> p a t", t=64)),
                             reads=[("PS", 4)], writes=[nNk])
                    for cc in range(8):
                        c = c8 + cc
                        P.op("pe", lambda e, c=c, cc=cc, nX=nX: e.matmul(
                            PS[5][0:64, cc * 64:(cc + 1) * 64], lhsT=nX[:, c, :], rhs=Pm[:, c, :], start=True, stop=True),
                            reads=[nXk, "Pm"], writes=[("PS", 5)])
                    P.op("dve", lambda e, c8=c8: e.tensor_tensor(
                        out=Pm[:, c8:c8 + 8, :], in0=PS[5][0:64, :].rearrange("p (a t) -> p a t", t=64),
                        in1=Pm[:, c8:c8 + 8, :], op=ALU.add), reads=[("PS", 5), "Pm"], writes=["Pm"])
                curN, curNk, curX, curXk = nN, nNk, nX, nXk
            P.op("pool", lambda e: e.memset(Ast[:], 0.0), writes=["Ast"])
            P.op("pool", lambda e: e.memset(Abf[:], 0.0), writes=["Abf"])
            for c in range(NCH):
                P.op("pe", lambda e, c=c: e.matmul(PS[0][0:64, 0:64], lhsT=ARh[:, c, 0:64], rhs=Abf[:], start=True, stop=False),
                     reads=["ARh", "Abf"], writes=[("PS", 0)])
                P.op("pe", lambda e, c=c: e.matmul(PS[0][0:64, 0:64], lhsT=GmK[:, c, 0:64], rhs=Vtok[:, c, :], start=False, stop=True),
                     reads=["GmK", "Vtok"], writes=[("PS", 0)])
                P.op("act", lambda e: e.copy(out=Xs[:], in_=PS[0][0:64, 0:64]), reads=[("PS", 0)], writes=["Xs"])
                P.op("pe", lambda e, c=c: e.matmul(PS[1][0:64, 0:64], lhsT=Pm[:, c, :], rhs=Xs[:], start=True, stop=True),
                     reads=["Pm", "Xs"], writes=[("PS", 1)])
                P.op("dve", lambda e: e.tensor_copy(out=Us[:], in_=PS[1][0:64, 0:64]), reads=[("PS", 1)], writes=["Us"])
                P.op("pe", lambda e, c=c: e.matmul(PS[6][0:64, 0:64], lhsT=Btok[:, c, :], rhs=Us[:], start=True, stop=False),
                     reads=["Btok", "Us"], writes=[("PS", 6)])
                P.op("pe", lambda e, c=c: e.matmul(PS[6][0:64, 0:64], lhsT=Ktok[:, c, :], rhs=Vtok[:, c, :], start=False, stop=True),
                     reads=["Ktok", "Vtok"], writes=[("PS", 6)])
                ob = 2 + (c % 2)
                P.op("pe", lambda e, c=c, ob=ob: e.matmul(PS[ob][0:64, 0:64], lhsT=Abf[:], rhs=ARh[:, c, 64:128], start=True, stop=False),
                     reads=["Abf", "ARh"], writes=[("PS", ob)])
                P.op("pe", lambda e, c=c, ob=ob: e.matmul(PS[ob][0:64, 0:64], lhsT=Us[:], rhs=GmB[:, c, 64:128], start=False, stop=False),
                     reads=["Us", "GmB"], writes=[("PS", ob)])
                P.op("pe", lambda e, c=c, ob=ob: e.matmul(PS[ob][0:64, 0:64], lhsT=Vtok[:, c, :], rhs=GmK[:, c, 64:128], start=False, stop=True),
                     reads=["Vtok", "GmK"], writes=[("PS", ob)])
                P.op("dve", lambda e: e.tensor_tensor(out=Tt[:], in0=PS[6][0:64, 0:64], in1=Ast[:], op=ALU.add),
                     reads=[("PS", 6), "Ast"], writes=["Tt"])
                P.op("act", lambda e, c=c: e.activation(out=Abf[:], in_=Tt[:], func=AF.Copy, scale=PC[:, c:c + 1]),
                     reads=["Tt", "PC"], writes=["Abf"])
                P.op("dve", lambda e, c=c: e.tensor_scalar(out=Ast[:], in0=Tt[:], scalar1=PC[:, c:c + 1], scalar2=None, op0=ALU.mult),
                     reads=["Tt", "PC"], writes=["Ast"])
                P.op("act", lambda e, c=c, ob=ob: e.copy(out=oT[:, c * 64:(c + 1) * 64], in_=PS[ob][0:64, 0:64]),
                     reads=[("PS", ob)], writes=[("oT", c // 8)])
            P.dma("sp", K.oT_d[r0:r0 + 64, :], oT[:], reads=[("oT", q) for q in range(8)], writes=[("oT_d", hd)])
        P.flush()

def phase4d_rwkv_post(K, cts=range(2)):
    nc, P = K.nc, K.P
    with contextlib.ExitStack() as st:
        def sb(name, shape, dt):
            return st.enter_context(nc.sbuf_tensor(name, shape, dt))
        ident = sb("ident4d", [128, 128], BF16)
        make_ident(K, P, ident)
        bonesf = sb("bonesf", [128, 128], F32)
        P.op("pool", lambda e: e.memset(bonesf[:], 0.0), writes=["bonesf"])
        P.op("pool", lambda e: e.memset(bonesf[0:64, 0:64], 1.0), reads=["bonesf"], writes=["bonesf"])
        P.op("pool", lambda e: e.memset(bonesf[64:128, 64:128], 1.0), reads=["bonesf"], writes=["bonesf"])
        prm = sb("prm4d", [128, 2, 2], F32)
        P.dma("sp", prm[:, 0, :], K.rw_prm[8], writes=["prm0"])
        P.dma("sp", prm[:, 1, :], K.rw_prm[9], writes=["prm1"])
        o = sb("o4d", [128, S], F32)
        osq = sb("osq", [128, S], F32)
        bon = sb("bon", [128, S], BF16)
        gg = sb("gg", [128, S], BF16)
        Mb = [sb("Mb%d" % i, [128, 512], F32) for i in range(2)]
        Vb = [sb("Vb%d" % i, [128, 512], F32) for i in range(2)]
        Yb = [sb("Yb%d" % i, [128, 512], F32) for i in range(2)]
        Ob = [sb("Ob%d" % i, [128, 512], BF16) for i in range(2)]
        Tk = [sb("Tk%d" % i, [128, 4, 128], BF16) for i in range(2)]
        ps = [st.enter_context(nc.psum_tensor("p4d_%d" % i, [128, 512], F32)) for i in range(4)]
        pst = [st.enter_context(nc.psum_tensor("p4dt_%d" % i, [128, 4, 128], BF16)) for i in range(2)]
        it = 0
        for ct in cts:
            c0 = ct * 128
            P.dma("sp", o[:], K.oT_d[c0:c0 + 128, :], writes=["o"])
            P.dma("sp", bon[:], K.BON_d[c0:c0 + 128, :], writes=["bon"])
            P.dma("sp", gg[:], K.G_d[c0:c0 + 128, :], writes=["gg"])
            P.op("act", lambda e: e.activation(out=osq[:], in_=o[:], func=AF.Square), reads=["o"], writes=["osq"])
            for blk in range(8):
                s2 = it % 2
                it += 1
                bs = slice(blk * 512, (blk + 1) * 512)
                P.op("pe", lambda e, bs=bs, s2=s2: e.matmul(ps[s2][:, :], lhsT=bonesf[:], rhs=o[:, bs], start=True, stop=True),
                     reads=["bonesf", "o"], writes=[("p4d", s2)])
                P.op("pe", lambda e, bs=bs, s2=s2: e.matmul(ps[2 + s2][:, :], lhsT=bonesf[:], rhs=osq[:, bs], start=True, stop=True),
                     reads=["bonesf", "osq"], writes=[("p4d", 2 + s2)])
                P.op("act", lambda e, s2=s2: e.activation(out=Mb[s2][:], in_=ps[s2][:, :], func=AF.Copy, scale=1.0 / 64),
                     reads=[("p4d", s2)], writes=[("Mb", s2)])
                P.op("pool", lambda e, s2=s2: e.tensor_tensor(out=Vb[s2][:], in0=Mb[s2][:], in1=Mb[s2][:], op=ALU.mult),
                     reads=[("Mb", s2)], writes=[("Vb", s2)])
                P.op("dve", lambda e, s2=s2: e.scalar_tensor_tensor(out=Vb[s2][:], in0=ps[2 + s2][:, :], scalar=1.0 / 64, in1=Vb[s2][:],
                                                                     op0=ALU.mult, op1=ALU.subtract),
                     reads=[("p4d", 2 + s2), ("Vb", s2)], writes=[("Vb", s2)])
                P.op("dve", lambda e, s2=s2: e.tensor_scalar(out=Vb[s2][:], in0=Vb[s2][:], scalar1=64e-5, scalar2=None, op0=ALU.add),
                     reads=[("Vb", s2)], writes=[("Vb", s2)])
                P.op("act", lambda e, s2=s2: e.activation(out=Vb[s2][:], in_=Vb[s2][:], func=AF.Sqrt),
                     reads=[("Vb", s2)], writes=[("Vb", s2)])
                P.op("dve", lambda e, s2=s2: e.reciprocal(out=Vb[s2][:], in_=Vb[s2][:]), reads=[("Vb", s2)], writes=[("Vb", s2)])
                P.op("pool", lambda e, s2=s2, bs=bs: e.tensor_tensor(out=Yb[s2][:], in0=o[:, bs], in1=Mb[s2][:], op=ALU.subtract),
                     reads=["o", ("Mb", s2)], writes=[("Yb", s2)])
                P.op("dve", lambda e, s2=s2: e.tensor_tensor(out=Yb[s2][:], in0=Yb[s2][:], in1=Vb[s2][:], op=ALU.mult),
                     reads=[("Yb", s2), ("Vb", s2)], writes=[("Yb", s2)])
                P.op("dve", lambda e, s2=s2, ct=ct: e.tensor_scalar(out=Yb[s2][:], in0=Yb[s2][:], scalar1=prm[:, 0, ct:ct + 1],
                                                                     scalar2=prm[:, 1, ct:ct + 1], op0=ALU.mult, op1=ALU.add),
                     reads=[("Yb", s2), "prm0", "prm1"], writes=[("Yb", s2)])
                P.op("pool", lambda e, s2=s2, bs=bs: e.tensor_tensor(out=Yb[s2][:], in0=Yb[s2][:], in1=bon[:, bs], op=ALU.add),
                     reads=[("Yb", s2), "bon"], writes=[("Yb", s2)])
                P.op("dve", lambda e, s2=s2, bs=bs: e.tensor_tensor(out=Ob[s2][:], in0=Yb[s2][:], in1=gg[:, bs], op=ALU.mult),
                     reads=[("Yb", s2), "gg"], writes=[("Ob", s2)])
                for q in range(4):
                    P.op("pe", lambda e, s2=s2, q=q: e.transpose(out=pst[s2][:, q, :], in_=Ob[s2][:, q * 128:(q + 1) * 128],
                                                                 identity=ident[:]),
                         reads=[("Ob", s2), "ident"], writes=[("p4dt", s2)])
                P.op("act", lambda e, s2=s2: e.copy(out=Tk[s2][:], in_=pst[s2][:]), reads=[("p4dt", s2)], writes=[("Tk", s2)])
                P.dma("sp", K.ro_loc_d[blk // 4].rearrange("(t p) c -> p t c", p=128)[:, (blk % 4) * 4:(blk % 4 + 1) * 4, c0:c0 + 128], Tk[s2][:],
                      reads=[("Tk", s2)], writes=[("ro_tok_d", ct, blk)])
        P.flush()


def phase4e_allgather(K):
    P = K.P
    for hh in range(2):
        P.coll(lambda e, hh=hh: e.collective_compute("AllGather", ALU.bypass, replica_groups=[[0, 1, 2, 3], [4, 5, 6, 7]],
                                                     ins=[K.ro_loc_d[hh].opt()], outs=[K.ro_all_d[hh].opt()]),
               reads=[("ro_loc", hh)], writes=[("ro_all", hh)])
    P.flush()


def phase5a_select(K):
    nc, P = K.nc, K.P
    with contextlib.ExitStack() as st:
        def sb(name, shape, dt):
            return st.enter_context(nc.sbuf_tensor(name, shape, dt))
        ro = sb("ro_tok", [128, 32, 1024], BF16)
        selT = sb("selT", [128, 32, 1024], BF16)
        qrow = sb("qrow", [128, 1024], F32)
        tki = sb("tki", [128, 32], I32)
        tkf = sb("tkf", [128, 32], F32)
        mo = [sb("mo%d" % i, [128, 512], BF16) for i in range(2)]
        at = sb("at5", [128, 8, 1024], BF16)
        ps = [st.enter_context(nc.psum_tensor("p5a_%d" % i, [128, 512], F32)) for i in range(2)]
        for q4 in range(4):
            for hh in range(2):
                P.dma("sp", ro[:, hh * 16:(hh + 1) * 16, q4 * 256:(q4 + 1) * 256],
                      K.ro_all_d[hh][q4 * 2048:(q4 + 1) * 2048, :].rearrange("(t p) c -> p t c", p=128), writes=[("ro", q4, hh)])
        rok = [("ro", q4, hh) for q4 in range(4) for hh in range(2)]
        P.dma("sp", qrow[:], bcast_rows(K.qpos_row, 1024), writes=["qrow"])
        P.op("pool", lambda e: e.iota(tki[:], pattern=[[128, 32]], base=0, channel_multiplier=1), writes=["tki"])
        P.op("dve", lambda e: e.tensor_copy(out=tkf[:], in_=tki[:]), reads=["tki"], writes=["tkf"])
        for T in range(32):
            P.op("dve", lambda e, T=T: e.tensor_scalar(out=selT[:, T, :], in0=qrow[:], scalar1=tkf[:, T:T + 1], scalar2=0.0,
                                                      op0=ALU.is_equal, op1=ALU.add), reads=["qrow", "tkf"], writes=[("selT", T)])
        sk = [("selT", T) for T in range(32)]
        P.dma("sp", at[:], K.attT_d.rearrange("h p t -> p h t"), writes=["at5"])
        P.dma("sp", K.mixT_d.rearrange("k p t -> p k t")[:, 0:8, :], at[:], reads=["at5"], writes=["mixa"])
        i = 0
        for m in range(8):
            for half in range(2):
                s2 = i % 2
                i += 1
                for T in range(32):
                    P.op("pe", lambda e, T=T, m=m, half=half, s2=s2: e.matmul(
                        ps[s2][:, :], lhsT=ro[:, T, m * 128:(m + 1) * 128], rhs=selT[:, T, half * 512:(half + 1) * 512],
                        start=(T == 0), stop=(T == 31)), reads=rok + sk, writes=[("p5a", s2)])
                P.op("act", lambda e, s2=s2: e.copy(out=mo[s2][:], in_=ps[s2][:, :]), reads=[("p5a", s2)], writes=[("mo", s2)])
                P.dma("sp", K.mixT_d[8 + m, :, half * 512:(half + 1) * 512], mo[s2][:], reads=[("mo", s2)], writes=[("mixr", m, half)])
        P.flush()


def phase5b_outproj(K):
    nc, P = K.nc, K.P
    with contextlib.ExitStack() as st:
        def sb(name, shape, dt):
            return st.enter_context(nc.sbuf_tensor(name, shape, dt))
        ident = sb("ident5", [128, 128], BF16)
        make_ident(K, P, ident)
        G2, SH2 = load_G_SH(K, P, st, 3, 4, K.norm2_g, "p5")
        GT1 = sb("GT1", [128, D], F32)
        P.dma("sp", GT1[:], bcast_rows(K.mod_d[2 * D:3 * D], D), writes=["GT1"])
        Wo = sb("Wo", [128, 16, D], BF16)
        stg = [sb("wstg5_%d" % i, [128, 4, 512], F32) for i in range(2)]
        wk = load_weight_bf16(K, P, stg, Wo, 0, K.w_out, D, "Wo")
        mixT = sb("mixT", [128, 16, 512], BF16)
        T = norm_tiles_alloc(K, st, "p5")
        x1 = T["xt"]
        hT = [sb("hT5_0", [128, 16, 512], BF16)] * 2
        xo = [sb("xo%d" % i, [128, D], F32) for i in range(2)]
        ps = [st.enter_context(nc.psum_tensor("p5b_%d" % i, [128, 512], F32)) for i in range(2)]
        ss, junk, hb, pT = T["ss"], T["junk"], T["hb"], T["pT"]
        gi = 0
        for blk in range(2):
            hs = 0
            P.dma("sp", mixT[:], K.mixT_d.rearrange("k p t -> p k t")[:, :, blk * 512:(blk + 1) * 512], writes=["mixT"])
            for ti in range(4):
                t = blk * 4 + ti
                xs = t % 2
                P.dma("sp", xo[xs][:], K.x_own[t * 128:(t + 1) * 128, :], writes=[("xo", xs)])
                for cg in range(4):
                    b = gi % 2
                    gi += 1
                    for k in range(16):
                        P.op("pe", lambda e, b=b, k=k, t=t, cg=cg: e.matmul(
                            ps[b][:, :], lhsT=mixT[:, k, (t % 4) * 128:(t % 4 + 1) * 128], rhs=Wo[:, k, cg * 512:(cg + 1) * 512],
                            start=(k == 0), stop=(k == 15)), reads=["mixT"] + wk, writes=[("p5b", b)])
                    cs = slice(cg * 512, (cg + 1) * 512)
                    P.op("dve", lambda e, b=b, xs=xs, cs=cs: e.tensor_tensor(out=x1[xs][:, cs], in0=ps[b][:, :], in1=GT1[:, cs], op=ALU.mult),
                         reads=[("p5b", b), "GT1"], writes=[("xt", xs)])
                    P.op("pool", lambda e, xs=xs, cs=cs: e.tensor_tensor(out=x1[xs][:, cs], in0=x1[xs][:, cs], in1=xo[xs][:, cs], op=ALU.add),
                         reads=[("xt", xs), ("xo", xs)], writes=[("xt", xs)])
                P.dma("sp", K.x1_d[t * 128:(t + 1) * 128, :], x1[xs][:], reads=[("xt", xs)], writes=[("x1_d", t)])
                P.op("act", lambda e, xs=xs: e.activation(out=junk[:], in_=x1[xs][:], func=AF.Square, accum_out=ss[:, 0:1]),
                     reads=[("xt", xs)], writes=["junk", "ss0"])
                P.op("dve", lambda e: e.tensor_scalar(out=ss[:, 1:2], in0=ss[:, 0:1], scalar1=1.0 / D, scalar2=1e-6,
                                                       op0=ALU.mult, op1=ALU.add), reads=["ss0"], writes=["ss1"])
                P.op("act", lambda e: e.activation(out=ss[:, 2:3], in_=ss[:, 1:2], func=AF.Sqrt), reads=["ss1"], writes=["ss2"])
                P.op("dve", lambda e: e.reciprocal(out=ss[:, 3:4], in_=ss[:, 2:3]), reads=["ss2"], writes=["ss3"])
                P.op("dve", lambda e, xs=xs: e.scalar_tensor_tensor(out=x1[xs][:], in0=x1[xs][:], scalar=ss[:, 3:4], in1=G2[:],
                                                                   op0=ALU.mult, op1=ALU.mult),
                     reads=[("xt", xs), "ss3", "G"], writes=[("xt", xs)])
                P.op("pool", lambda e, xs=xs: e.tensor_tensor(out=hb[xs][:], in0=x1[xs][:], in1=SH2[:], op=ALU.add),
                     reads=[("xt", xs), "SH"], writes=[("hb", xs)])
                for half in range(2):
                    for kk in range(8):
                        k = half * 8 + kk
                        P.op("pe", lambda e, k=k, kk=kk, half=half, xs=xs: e.transpose(
                            out=pT[half][:, kk, :], in_=hb[xs][:, k * 128:(k + 1) * 128], identity=ident[:]),
                            reads=[("hb", xs), "ident"], writes=[("pT", half)])
                    o_ = hT[hs][:, half * 8:(half + 1) * 8, ti * 128:(ti + 1) * 128]
                    if half == 0:
                        P.op("act", lambda e, o_=o_, half=half: e.copy(out=o_, in_=pT[half][:]), reads=[("pT", half)], writes=[("hT5", hs, ti, half)])
                    else:
                        P.op("dve", lambda e, o_=o_, half=half: e.tensor_copy(out=o_, in_=pT[half][:]), reads=[("pT", half)], writes=[("hT5", hs, ti, half)])
            P.dma("sp", K.h2T_d.rearrange("k p t -> p k t")[:, :, blk * 512:(blk + 1) * 512], hT[hs][:],
                  reads=[("hT5", hs, ti, half) for ti in range(4) for half in range(2)], writes=[("h2T_d", blk)])
        P.flush()


def phase5c_ffn(K):
    nc, P = K.nc, K.P
    NF = 5632 // 128
    with contextlib.ExitStack() as st:
        def sb(name, shape, dt):
            return st.enter_context(nc.sbuf_tensor(name, shape, dt))
        h2T = sb("h2T", [128, 16, OWN], BF16)
        P.dma("sp", h2T[:], K.h2T_d.rearrange("k p t -> p k t"), writes=["h2T"])
        ao = [sb("ao%d" % i, [128, 512], BF16) for i in range(2)]
        stg = [sb("wstg6_%d" % i, [128, 4, 512], F32) for i in range(2)]
        Wg = [sb("Wg%d" % i, [128, 16, 512], BF16) for i in range(2)]
        Wu = [sb("Wu%d" % i, [128, 16, 512], BF16) for i in range(2)]
        sg = [sb("sg%d" % i, [128, 512], F32) for i in range(2)]
        ps = [st.enter_context(nc.psum_tensor("p5c_%d" % i, [128, 512], F32)) for i in range(4)]
        gi = 0
        for fg in range(11):
            ws = fg % 2
            kg = load_weight_bf16(K, P, stg, Wg[ws], 0, K.w_ffn_gate[:, fg * 512:(fg + 1) * 512], 512, ("Wg", ws))
            ku = load_weight_bf16(K, P, stg, Wu[ws], 0, K.w_ffn_up[:, fg * 512:(fg + 1) * 512], 512, ("Wu", ws))
            for f4 in range(4):
                f = fg * 4 + f4
                for tb in range(2):
                    b = gi % 2
                    gi += 1
                    for k in range(16):
                        P.op("pe", lambda e, b=b, k=k, f4=f4, tb=tb, ws=ws: e.matmul(
                            ps[b][:, :], lhsT=Wg[ws][:, k, f4 * 128:(f4 + 1) * 128], rhs=h2T[:, k, tb * 512:(tb + 1) * 512],
                            start=(k == 0), stop=(k == 15)), reads=["h2T"] + kg, writes=[("p5c", b)])
                    for k in range(16):
                        P.op("pe", lambda e, b=b, k=k, f4=f4, tb=tb, ws=ws: e.matmul(
                            ps[2 + b][:, :], lhsT=Wu[ws][:, k, f4 * 128:(f4 + 1) * 128], rhs=h2T[:, k, tb * 512:(tb + 1) * 512],
                            start=(k == 0), stop=(k == 15)), reads=["h2T"] + ku, writes=[("p5c", 2 + b)])
                    P.op("act", lambda e, b=b: e.activation(out=sg[b][:], in_=ps[b][:, :], func=AF.Silu),
                         reads=[("p5c", b)], writes=[("sg", b)])
                    P.op("dve", lambda e, b=b: e.tensor_tensor(out=ao[b][:], in0=ps[2 + b][:, :], in1=sg[b][:], op=ALU.mult),
                         reads=[("p5c", 2 + b), ("sg", b)], writes=[("ao", b)])
                    P.dma("sp", K.actT_d[f, :, tb * 512:(tb + 1) * 512], ao[b][:], reads=[("ao", b)], writes=[("actT_d", f, tb)])
        P.flush()
    with contextlib.ExitStack() as st:
        def sb(name, shape, dt):
            return st.enter_context(nc.sbuf_tensor(name, shape, dt))
        GT2 = sb("GT2", [128, D], F32)
        P.dma("sp", GT2[:], bcast_rows(K.mod_d[5 * D:6 * D], D), writes=["GT2"])
        actT = sb("actT", [128, NF, OWN], BF16)
        for q in range(4):
            P.dma("sp", actT[:, q * 11:(q + 1) * 11, :], K.actT_d.rearrange("f p t -> p f t")[:, q * 11:(q + 1) * 11, :], writes=[("actT", q)])
        ak = [("actT", q) for q in range(4)]
        stg = [sb("wstg7_%d" % i, [128, 4, 512], F32) for i in range(2)]
        ps = [st.enter_context(nc.psum_tensor("p5d_%d" % i, [128, 512], F32)) for i in range(2)]
        gi = 0
        Wd = sb("Wd", [128, NF, 512], BF16)
        x1 = [sb("x1_%d" % i, [128, 512], F32) for i in range(2)]
        yo = [sb("yo%d" % i, [128, 512], F32) for i in range(2)]
        wdv = K.w_ffn_down.rearrange("(k p) n -> p k n", p=128)
        engs = ["pool", "dve", "act"]
        for cg in range(4):
            cs = slice(cg * 512, (cg + 1) * 512)
            wkeys = []
            for k0 in range(0, NF, 4):
                i = K.wcnt
                K.wcnt += 1
                sl = i % 2
                P.dma("sp", stg[sl][:, 0:4, :], wdv[:, k0:k0 + 4, cs], writes=[("wstg", sl)])
                eng = engs[i % 3]
                o_ = Wd[:, k0:k0 + 4, :]
                if eng == "act":
                    P.op("act", lambda e, o_=o_, sl=sl: e.copy(out=o_, in_=stg[sl][:, 0:4, :]), reads=[("wstg", sl)], writes=[("Wd", k0)])
                else:
                    P.op(eng, lambda e, o_=o_, sl=sl: e.tensor_copy(out=o_, in_=stg[sl][:, 0:4, :]), reads=[("wstg", sl)], writes=[("Wd", k0)])
                wkeys.append(("Wd", k0))
            for t in range(8):
                b = gi % 2
                gi += 1
                P.dma("sp", x1[b][:], K.x1_d[t * 128:(t + 1) * 128, cs], writes=[("x1", b)])
                for f in range(NF):
                    P.op("pe", lambda e, b=b, f=f, t=t: e.matmul(ps[b][:, :], lhsT=actT[:, f, t * 128:(t + 1) * 128], rhs=Wd[:, f, :],
                                                                 start=(f == 0), stop=(f == NF - 1)), reads=ak + wkeys, writes=[("p5c", b)])
                P.op("dve", lambda e, b=b, cs=cs: e.tensor_tensor(out=yo[b][:], in0=ps[b][:, :], in1=GT2[:, cs], op=ALU.mult),
                     reads=[("p5c", b), "GT2"], writes=[("yo", b)])
                P.op("pool", lambda e, b=b: e.tensor_tensor(out=yo[b][:], in0=yo[b][:], in1=x1[b][:], op=ALU.add),
                     reads=[("yo", b), ("x1", b)], writes=[("yo", b)])
                P.dma("sp", K.out[t * 128:(t + 1) * 128, cs], yo[b][:], reads=[("yo", b)], writes=[("out", t, cg)])
        P.flush()


def phase_final_copy(K):
    nc, P = K.nc, K.P
    with contextlib.ExitStack() as st:
        xt = [st.enter_context(nc.sbuf_tensor("fx%d" % i, [128, D], F32)) for i in range(2)]
        for t in range(8):
            s = t % 2
            P.dma("sp", xt[s][:], K.x_own[t * 128:(t + 1) * 128, :], writes=[("fx", s)])
            P.dma("sp", K.out[t * 128:(t + 1) * 128, :], xt[s][:], reads=[("fx", s)], writes=[("out", t)])
        P.flush()


def own_tiles(j):
    r = []
    for m in range(4):
        r += [8 * m + j, 8 * m + 7 - j]
    return r


def build_program(debug=False, stages=99, cts=range(2), dbg_list=None, skip_att=False):
    nc = bass.Bass("TRN2", target_bir_lowering=False)
    K = Ctx()
    K.stages = stages
    K.cts = cts
    K.skip_att = skip_att
    K.nc = nc
    K.dbg = {}
    K.wcnt = 0

    def inp(name, shape, dt=F32):
        return nc.dram_tensor(name, list(shape), dt, kind="ExternalInput").ap()

    def scratch(name, shape, dt):
        return nc.dram_tensor(name, list(shape), dt, kind="Internal").ap()

    K.x_full = inp("x_full", [S, D])
    K.x_own = inp("x_own", [OWN, D])
    K.c_arr = inp("c_arr", [128, 16])
    K.pos_full = inp("pos_full", [128, 32], I32)
    K.invf_att = inp("invf_att", [128, 16])
    K.invf_idx = inp("invf_idx", [128, 8])
    K.w_ada = inp("w_ada", [D, 6 * D])
    K.b_ada = inp("b_ada", [6 * D])
    K.norm1_g = inp("norm1_g", [D])
    K.k_norm_g = inp("k_norm_g", [128])
    K.q_norm_g = inp("q_norm_g", [128])
    K.pos_own = inp("pos_own", [128, 8], I32)
    K.qpos_own = inp("qpos_own", [128, 8])
    K.w_in = inp("w_in", [D, 4176])
    K.rw_prm = [inp("rwp%d" % i, [128, 2]) for i in range(10)]
    K.w_in_rw = inp("w_in_rw", [D, 1216])
    K.rw_mul = inp("rw_mul", [128, 4])
    K.rw_w_up = inp("rw_w_up", [96, 256])
    K.rw_a_up = inp("rw_a_up", [96, 256])
    K.rw_g_up = inp("rw_g_up", [256, 256])
    K.qpos_row = inp("qpos_row", [OWN])
    K.w_out = inp("w_out", [D, D])
    K.norm2_g = inp("norm2_g", [D])
    K.w_ffn_gate = inp("w_ffn_gate", [D, 5632])
    K.w_ffn_up = inp("w_ffn_up", [D, 5632])
    K.w_ffn_down = inp("w_ffn_down", [5632, D])
    K.out = nc.dram_tensor("y_own", [OWN, D], F32, kind="ExternalOutput").ap()
    K.mod_d = scratch("mod_d", [6 * D], F32)
    K.hT_d = scratch("hT_d", [16, 128, S], BF16)
    K.kT_d = scratch("kT_d", [8, 128, S], BF16)
    K.v_d = scratch("v_d", [S, 8 * 129], BF16)
    K.ikT_d = scratch("ikT_d", [64, S], BF16)
    K.yT_d = scratch("yT_d", [1216, S], F32)
    K.qT_d = scratch("qT_d", [8, 128, OWN], BF16)
    K.iqT_d = scratch("iqT_d", [64, OWN, 16], BF16)
    K.iw_d = scratch("iw_d", [OWN, 16], F32)
    K.attT_d = scratch("attT_d", [8, 128, OWN], BF16)
    for nm in ("vb_d", "G_d", "BON_d", "AH_d", "RH_d", "BH_d", "KH_d"):
        setattr(K, nm, scratch(nm, [256, S], BF16))
    K.PC_d = scratch("PC_d", [256, NCH], F32)
    K.oT_d = scratch("oT_d", [256, S], F32)
    K.ro_loc_d = [scratch("ro_loc%d_d" % i, [2048, 256], BF16) for i in range(2)]
    K.ro_all_d = [scratch("ro_all%d_d" % i, [8192, 256], BF16) for i in range(2)]
    K.mixT_d = scratch("mixT_d", [16, 128, OWN], BF16)
    K.x1_d = scratch("x1_d", [OWN, D], F32)
    K.h2T_d = scratch("h2T_d", [16, 128, OWN], BF16)
    K.actT_d = scratch("actT_d", [44, 128, OWN], BF16)
    with contextlib.ExitStack() as stack:
        K.P = Prog(nc, stack)
        phase0_adaln(K)
        phase1_kv(K)
        if K.stages >= 2:
            phase1b_rwkv_proj(K)
        if K.stages >= 3 and not getattr(K, "skip_att", False):
            phase2_own_proj(K)
            phase3_attention(K)
        if K.stages >= 4:
            phase4b_rwkv_prep(K, cts=K.cts)
            if K.stages >= 5:
                phase4c_rwkv_scan(K, heads=[h for ct in K.cts for h in (2 * ct, 2 * ct + 1)])
        if K.stages >= 6:
            phase4d_rwkv_post(K, cts=K.cts)
            phase4e_allgather(K)
        if K.stages >= 7:
            phase5a_select(K)
            phase5b_outproj(K)
            phase5c_ffn(K)
        else:
            phase_final_copy(K)
        if debug:
            P = K.P
            allc = (("dbg_mixT", K.mixT_d, [16, 128, OWN], BF16), ("dbg_x1", K.x1_d, [OWN, D], F32),
                    ("dbg_oT", K.oT_d, [256, S], F32), ("dbg_AH", K.AH_d, [256, S], BF16), ("dbg_BH", K.BH_d, [256, S], BF16),
                    ("dbg_KH", K.KH_d, [256, S], BF16), ("dbg_RH", K.RH_d, [256, S], BF16), ("dbg_PC", K.PC_d, [256, NCH], F32),
                    ("dbg_G", K.G_d, [256, S], BF16), ("dbg_BON", K.BON_d, [256, S], BF16), ("dbg_vb", K.vb_d, [256, S], BF16),
                    ("dbg_yT", K.yT_d, [1216, S], F32), ("dbg_attT", K.attT_d, [8, 128, OWN], BF16),
                                     ("dbg_qT", K.qT_d, [8, 128, OWN], BF16), ("dbg_iqT", K.iqT_d, [64, OWN, 16], BF16),
                                     ("dbg_iw", K.iw_d, [OWN, 16], F32))
            for nm, src, shp, dt in allc:
                if dbg_list is not None and nm not in dbg_list:
                    continue
                o = dbg_out(K, nm, shp, dt)
                P.dma("sp", o, src, writes=[nm])
            P.flush()
    return nc, K


def make_in_maps(inputs, cores=range(8)):
    x = np.asarray(inputs["x"], dtype=np.float32)
    c = np.asarray(inputs["c"], dtype=np.float32)
    pos = np.asarray(inputs["positions"], dtype=np.int32)
    invf_att = (np.float32(500000.0) ** (-np.arange(16, dtype=np.float32) / np.float32(16))).astype(np.float32)
    invf_idx = (np.float32(500000.0) ** (-np.arange(8, dtype=np.float32) / np.float32(8))).astype(np.float32)
    mu = np.asarray(inputs["rwkv_mu"][0], dtype=np.float32)

    vecs = [mu[0:1024], mu[1024:2048], mu[2048:3072], inputs["rwkv_w0"][0], inputs["rwkv_a0"][0], inputs["rwkv_k_k"][0],
            inputs["rwkv_k_a"][0], np.asarray(inputs["rwkv_r_k"][0]).reshape(-1), inputs["rwkv_lnx_g"][0], inputs["rwkv_lnx_b"][0]]
    w_in_full = np.asarray(inputs["w_in"][0], dtype=np.float32)
    rw_mul = np.zeros((128, 4), np.float32)
    rw_mul[:96, 0] = mu[3072:3168]
    rw_mul[:96, 1] = mu[3168:3264]
    rw_mul[:, 2] = mu[3264:3392]
    rw_mul[:, 3] = mu[3392:3520]
    maps = []
    for core in cores:
        b, j = core // 4, core % 4
        ch = slice(256 * j, 256 * j + 256)
        rwp = {"rwp%d" % i: np.ascontiguousarray(np.asarray(v, dtype=np.float32)[ch].reshape(2, 128).T) for i, v in enumerate(vecs)}
        R0 = 4176
        w_in_rw = np.ascontiguousarray(np.concatenate([w_in_full[:, R0 + 256 * j:R0 + 256 * j + 256],
                                                       w_in_full[:, R0 + 1024 + 256 * j:R0 + 1024 + 256 * j + 256],
                                                       w_in_full[:, R0 + 2048 + 256 * j:R0 + 2048 + 256 * j + 256],
                                                       w_in_full[:, R0 + 3072:R0 + 3520]], axis=1))
        tiles = own_tiles(j)
        idx = np.concatenate([np.arange(t * 128, (t + 1) * 128) for t in tiles])
        maps.append({
            "x_full": np.ascontiguousarray(x[b]),
            "x_own": np.ascontiguousarray(x[b][idx]),
            "c_arr": np.ascontiguousarray(c[b].reshape(16, 128).T),
            "pos_full": np.ascontiguousarray(pos[b].reshape(32, 128).T),
            "invf_att": np.ascontiguousarray(np.broadcast_to(invf_att, (128, 16))),
            "invf_idx": np.ascontiguousarray(np.broadcast_to(invf_idx, (128, 8))),
            "w_ada": np.asarray(inputs["w_ada"][0], dtype=np.float32),
            "b_ada": np.asarray(inputs["b_ada"][0], dtype=np.float32),
            "norm1_g": np.asarray(inputs["norm1_g"][0], dtype=np.float32),
            "k_norm_g": np.asarray(inputs["k_norm_g"][0], dtype=np.float32),
            "q_norm_g": np.asarray(inputs["q_norm_g"][0], dtype=np.float32),
            "pos_own": np.ascontiguousarray(pos[b][idx].reshape(8, 128).T),
            "qpos_own": np.ascontiguousarray(idx.astype(np.float32).reshape(8, 128).T),
            "w_in": np.ascontiguousarray(w_in_full[:, 0:4176]),
            "qpos_row": idx.astype(np.float32),
            "w_out": np.asarray(inputs["w_out"][0], dtype=np.float32),
            "norm2_g": np.asarray(inputs["norm2_g"][0], dtype=np.float32),
            "w_ffn_gate": np.asarray(inputs["w_ffn_gate"][0], dtype=np.float32),
            "w_ffn_up": np.asarray(inputs["w_ffn_up"][0], dtype=np.float32),
            "w_ffn_down": np.asarray(inputs["w_ffn_down"][0], dtype=np.float32),
            "rw_w_up": np.ascontiguousarray(np.asarray(inputs["rwkv_w_up"][0], dtype=np.float32)[:, ch]),
            "rw_a_up": np.ascontiguousarray(np.asarray(inputs["rwkv_a_up"][0], dtype=np.float32)[:, ch]),
            "rw_g_up": np.ascontiguousarray(np.asarray(inputs["rwkv_g_up"][0], dtype=np.float32)[:, ch]),
            "w_in_rw": w_in_rw,
            "rw_mul": rw_mul,
            **rwp,
        })
    return maps


def kernel(**inputs):
    nc, K = build_program(debug=False)
    maps = make_in_maps(inputs)
    res = run_bass_kernel_spmd(nc, maps, core_ids=list(range(8)))
    out = np.zeros((2, S, D), dtype=np.float32)
    for core in range(8):
        b, j = core // 4, core % 4
        y = res.results[core]["y_own"]
        for i, t in enumerate(own_tiles(j)):
            out[b, t * 128:(t + 1) * 128] = y[i * 128:(i + 1) * 128]
    return out
```

```python
import contextlib
import numpy as np
import concourse.bass as bass
import concourse.mybir as mybir
from concourse.bass_utils import run_bass_kernel_spmd

F32 = mybir.dt.float32
BF16 = mybir.dt.bfloat16
I32 = mybir.dt.int32
AF = mybir.ActivationFunctionType
ALU = mybir.AluOpType
AX = mybir.AxisListType

D = 2048
S = 4096
NT = 32
OWN = 1024
ENGS = ("pe", "act", "dve", "pool", "sp")
DEBUG = {}


class _Op:
    __slots__ = ("eng", "fn", "deps", "needs_inc", "is_dma", "sem", "count", "idx", "prev_same_sem", "is_cc")

    def __init__(self, eng, fn, is_dma):
        self.eng = eng
        self.fn = fn
        self.deps = set()
        self.needs_inc = False
        self.is_dma = is_dma
        self.sem = None
        self.count = 0
        self.prev_same_sem = None
        self.is_cc = False


class Prog:
    def __init__(self, nc, stack, n_dma_sems=48):
        self.nc = nc
        self.n_dma_sems = n_dma_sems
        self.eng_sem = {e: stack.enter_context(nc.semaphore("s_" + e)) for e in ENGS}
        self.dma_sems = [stack.enter_context(nc.semaphore("d%d" % i)) for i in range(n_dma_sems)]
        self.bar_sem = stack.enter_context(nc.semaphore("bar"))
        self.cc_sem = stack.enter_context(nc.semaphore("ccs"))
        self.cc_cnt = 0
        self.cnt = {e: 0 for e in ENGS}
        self.dcnt = [0] * n_dma_sems
        self.rr = 0
        self.nbar = 0
        self._reset()

    def _reset(self):
        self.ops = []
        self.last_writer = {}
        self.readers = {}

    def _record(self, op, reads, writes):
        idx = len(self.ops)
        op.idx = idx
        deps = set()
        for k in reads:
            w = self.last_writer.get(k)
            if w is not None:
                deps.add(w)
        for k in writes:
            w = self.last_writer.get(k)
            if w is not None:
                deps.add(w)
            for r in self.readers.get(k, ()):
                deps.add(r)
        deps.discard(idx)
        op.deps = deps
        self.ops.append(op)
        for k in reads:
            self.readers.setdefault(k, []).append(idx)
        for k in writes:
            self.last_writer[k] = idx
            self.readers[k] = []
        return idx

    def op(self, eng, fn, reads=(), writes=()):
        return self._record(_Op(eng, fn, False), reads, writes)

    def dma(self, queue, out, in_, reads=(), writes=(), **kw):
        def fn(e, out=out, in_=in_, kw=kw):
            return e.dma_start(out=out, in_=in_, **kw)
        return self._record(_Op(queue, fn, True), reads, writes)

    def coll(self, fn, reads=(), writes=()):
        o = _Op("pool", fn, True)
        o.is_cc = True
        return self._record(o, reads, writes)

    def flush(self):
        nc = self.nc
        ops = self.ops
        for o in ops:
            nd = set()
            for d in o.deps:
                p = ops[d]
                if o.eng == "pe" and p.eng == "pe" and not p.is_dma and not o.is_dma:
                    continue
                nd.add(d)
                p.needs_inc = True
            o.deps = nd
        last_of = {}
        for o in ops:
            if not o.is_dma:
                last_of[o.eng] = o
        for o in last_of.values():
            o.needs_inc = True
        dlast = [None] * self.n_dma_sems
        for o in ops:
            if o.is_cc:
                self.cc_cnt += 1
                o.sem = self.cc_sem
                o.count = self.cc_cnt
            elif o.is_dma:
                s = self.rr % self.n_dma_sems
                self.rr += 1
                o.prev_same_sem = dlast[s]
                self.dcnt[s] += 16
                o.sem = self.dma_sems[s]
                o.count = self.dcnt[s]
                dlast[s] = o.idx
            elif o.needs_inc:
                self.cnt[o.eng] += 1
                o.sem = self.eng_sem[o.eng]
                o.count = self.cnt[o.eng]
        per_eng = {e: [o for o in ops if o.eng == e] for e in ENGS}
        final = [(self.dma_sems[s], self.dcnt[s]) for s in range(self.n_dma_sems) if self.dcnt[s] > 0]
        final += [(self.eng_sem[e], self.cnt[e]) for e in ENGS if self.cnt[e] > 0]
        if self.cc_cnt > 0:
            final.append((self.cc_sem, self.cc_cnt))
        self.nbar += 1
        nbar = self.nbar
        bar = self.bar_sem

        def run(e_name, eng):
            waited = {}
            for o in per_eng[e_name]:
                need = {}
                for d in o.deps:
                    p = ops[d]
                    if need.get(p.sem.num, (0, None))[0] < p.count:
                        need[p.sem.num] = (p.count, p.sem)
                if o.is_dma and o.prev_same_sem is not None:
                    p = ops[o.prev_same_sem]
                    if need.get(p.sem.num, (0, None))[0] < p.count:
                        need[p.sem.num] = (p.count, p.sem)
                for key, (c, s) in need.items():
                    if waited.get(key, 0) < c:
                        eng.wait_ge(s, c)
                        waited[key] = c
                ins = o.fn(eng)
                if o.is_cc:
                    ins.then_inc(o.sem)
                elif o.is_dma:
                    ins.then_inc(o.sem, 16)
                elif o.needs_inc:
                    ins.then_inc(o.sem, 1)
            if e_name == "sp":
                for s, c in final:
                    eng.wait_ge(s, c)
                eng.sem_inc(bar, 1)
            eng.wait_ge(bar, nbar)

        with nc.Block() as block:
            @block.tensor
            def _(e):
                run("pe", e)

            @block.scalar
            def _(e):
                run("act", e)

            @block.vector
            def _(e):
                run("dve", e)

            @block.gpsimd
            def _(e):
                run("pool", e)

            @block.sync
            def _(e):
                run("sp", e)
        self._reset()


class Ctx:
    pass


def bcast_rows(ap1d, n):
    return bass.AP(ap1d.tensor, ap1d.offset, [[0, 128], [1, n]])


def dbg_out(K, name, shape, dtype=F32):
    t = K.nc.dram_tensor(name, list(shape), dtype, kind="ExternalOutput")
    K.dbg[name] = t
    return t.ap()


def make_ident(K, P, ident):
    P.op("pool", lambda e: e.memset(ident[:], 0.0), writes=["ident"])
    P.op("pool", lambda e: e.affine_select(out=ident[:], in_=ident[:], pattern=[[-1, 128]],
                                           compare_op=ALU.not_equal, fill=1.0, base=0,
                                           channel_multiplier=1),
         reads=["ident"], writes=["ident"])


def phase0_adaln(K):
    nc, P = K.nc, K.P
    with contextlib.ExitStack() as st:
        c_sb = st.enter_context(nc.sbuf_tensor("c_sb", [128, 16], F32))
        cact = st.enter_context(nc.sbuf_tensor("cact", [128, 16], F32))
        wst = [st.enter_context(nc.sbuf_tensor("wst%d" % i, [128, 16, 512], F32)) for i in range(2)]
        modrow = st.enter_context(nc.sbuf_tensor("modrow", [1, 12288], F32))
        brow = st.enter_context(nc.sbuf_tensor("brow", [1, 12288], F32))
        ps = [st.enter_context(nc.psum_tensor("ps0_%d" % i, [1, 512], F32)) for i in range(2)]
        P.dma("sp", c_sb[:], K.c_arr, writes=["c_sb"])
        P.dma("sp", brow[:], K.b_ada.rearrange("(o n) -> o n", o=1), writes=["brow"])
        P.op("act", lambda e: e.activation(out=cact[:], in_=c_sb[:], func=AF.Silu),
             reads=["c_sb"], writes=["cact"])
        wv = K.w_ada.rearrange("(k p) n -> p k n", p=128)
        for nt in range(24):
            sl = nt % 2
            for hh in range(2):
                P.dma("sp", wst[sl][:, hh * 8:(hh + 1) * 8, :],
                      wv[:, hh * 8:(hh + 1) * 8, nt * 512:(nt + 1) * 512],
                      writes=[("wst", sl, hh)])
            for k in range(16):
                P.op("pe", lambda e, k=k, sl=sl: e.matmul(ps[sl][:, :], lhsT=cact[:, k:k + 1],
                                                         rhs=wst[sl][:, k, :], start=(k == 0), stop=(k == 15)),
                     reads=["cact", ("wst", sl, k // 8)], writes=[("ps0", sl)])
            P.op("dve", lambda e, nt=nt, sl=sl: e.tensor_tensor(
                out=modrow[0:1, nt * 512:(nt + 1) * 512], in0=ps[sl][:, :],
                in1=brow[0:1, nt * 512:(nt + 1) * 512], op=ALU.add),
                reads=[("ps0", sl), "brow"], writes=[("modrow", nt)])
        P.dma("sp", K.mod_d.rearrange("(o n) -> o n", o=1), modrow[:],
              reads=[("modrow", nt) for nt in range(24)], writes=["mod_d"])
        P.flush()


def load_mod_rows(K, P, tile, which, gain_ap=None, key=None):
    src = K.mod_d[which * D:(which + 1) * D]
    P.dma("sp", tile[:], bcast_rows(src, D), writes=[key])


def bc(ap, shape):
    return ap.to_broadcast(list(shape))


def load_weight_bf16(K, P, st_tiles, dst, c_dst, src2d, ncols, tag):
    wv = src2d.rearrange("(k p) n -> p k n", p=128)
    nk = wv.shape[1]
    engs = ["pool", "dve", "act"]
    for c0 in range(0, ncols, 512):
        n = min(512, ncols - c0)
        for k0 in range(0, nk, 4):
            kn = min(4, nk - k0)
            i = K.wcnt
            K.wcnt += 1
            sl = i % len(st_tiles)
            stg = st_tiles[sl]
            P.dma("sp", stg[:, 0:kn, 0:n], wv[:, k0:k0 + kn, c0:c0 + n], writes=[("wstg", sl)])
            eng = engs[i % 3]
            o = dst[:, k0:k0 + kn, c_dst + c0:c_dst + c0 + n]
            if eng == "act":
                P.op("act", lambda e, o=o, stg=stg, kn=kn, n=n: e.copy(out=o, in_=stg[:, 0:kn, 0:n]),
                     reads=[("wstg", sl)], writes=[(tag, c0, k0)])
            else:
                P.op(eng, lambda e, o=o, stg=stg, kn=kn, n=n: e.tensor_copy(out=o, in_=stg[:, 0:kn, 0:n]),
                     reads=[("wstg", sl)], writes=[(tag, c0, k0)])
    return [(tag, c0, k0) for c0 in range(0, ncols, 512) for k0 in range(0, nk, 4)]


def rope_tables(K, P, st, pos_arr, ntile, invf_att, invf_idx, tag):
    nc = K.nc
    posi = st.enter_context(nc.sbuf_tensor(tag + "posi", [128, ntile], I32))
    posf = st.enter_context(nc.sbuf_tensor(tag + "posf", [128, ntile], F32))
    iva = st.enter_context(nc.sbuf_tensor(tag + "iva", [128, 16], F32))
    ivi = st.enter_context(nc.sbuf_tensor(tag + "ivi", [128, 8], F32))
    P.dma("sp", posi[:], pos_arr, writes=[tag + "posi"])
    P.dma("sp", iva[:], invf_att, writes=[tag + "iva"])
    P.dma("sp", ivi[:], invf_idx, writes=[tag + "ivi"])
    P.op("dve", lambda e: e.tensor_copy(out=posf[:], in_=posi[:]), reads=[tag + "posi"], writes=[tag + "posf"])
    out = {}
    for nm, iv, h in (("a", iva, 16), ("i", ivi, 8)):
        u = st.enter_context(nc.sbuf_tensor(tag + "u" + nm, [128, ntile, h], F32))
        ui = st.enter_context(nc.sbuf_tensor(tag + "ui" + nm, [128, ntile, h], I32))
        uf = st.enter_context(nc.sbuf_tensor(tag + "uf" + nm, [128, ntile, h], F32))
        for fn, off in (("sin", 0.0), ("cos", 0.25)):
            tb = st.enter_context(nc.sbuf_tensor(tag + fn + nm, [128, ntile, h], F32))
            kk = tag + fn + nm
            P.op("dve", lambda e, u=u, iv=iv, h=h: e.tensor_tensor(
                out=u[:], in0=bc(posf[:].unsqueeze(2), [128, ntile, h]),
                in1=bc(iv[:].unsqueeze(1), [128, ntile, h]), op=ALU.mult),
                reads=[tag + "posf", tag + "iv" + nm], writes=[tag + "U" + nm])
            P.op("dve", lambda e, u=u, off=off: e.tensor_scalar(
                out=u[:], in0=u[:], scalar1=float(1.0 / (2 * np.pi)), scalar2=off, op0=ALU.mult, op1=ALU.add),
                reads=[tag + "U" + nm], writes=[tag + "U" + nm])
            P.op("dve", lambda e, u=u, ui=ui: e.tensor_copy(out=ui[:], in_=u[:]), reads=[tag + "U" + nm], writes=[tag + "UI" + nm])
            P.op("dve", lambda e, uf=uf, ui=ui: e.tensor_copy(out=uf[:], in_=ui[:]), reads=[tag + "UI" + nm], writes=[tag + "UF" + nm])
            P.op("dve", lambda e, u=u, uf=uf: e.tensor_tensor(out=u[:], in0=u[:], in1=uf[:], op=ALU.subtract),
                 reads=[tag + "U" + nm, tag + "UF" + nm], writes=[tag + "U" + nm])
            P.op("dve", lambda e, u=u: e.tensor_scalar(out=u[:], in0=u[:], scalar1=-0.5, scalar2=0.5,
                                                        op0=ALU.max, op1=ALU.min),
                 reads=[tag + "U" + nm], writes=[tag + "U" + nm])
            P.op("act", lambda e, u=u, tb=tb: e.activation(out=tb[:], in_=u[:], func=AF.Sin,
                                                            scale=float(2 * np.pi)),
                 reads=[tag + "U" + nm], writes=[kk])
            out[fn + nm] = (tb, kk)
    return out


def apply_rope(P, eng, x4, cos, sin, t, half, tmp, rk, wk, sfx=""):
    ctb, ck = cos
    stb, sk = sin
    H = x4.shape[1]
    x1 = x4[:, :, 0:half]
    x2 = x4[:, :, half:2 * half]
    cb = bc(ctb[:, t, :].unsqueeze(1), [128, H, half])
    sb = bc(stb[:, t, :].unsqueeze(1), [128, H, half])
    a, b2, c, d = tmp
    P.op(eng, lambda e: e.tensor_tensor(out=a[:, 0:H, 0:half], in0=x1, in1=cb, op=ALU.mult), reads=rk + [ck], writes=["rtmpA" + sfx])
    P.op(eng, lambda e: e.tensor_tensor(out=b2[:, 0:H, 0:half], in0=x2, in1=sb, op=ALU.mult), reads=rk + [sk], writes=["rtmpB" + sfx])
    P.op(eng, lambda e: e.tensor_tensor(out=c[:, 0:H, 0:half], in0=x2, in1=cb, op=ALU.mult), reads=rk + [ck], writes=["rtmpC" + sfx])
    P.op(eng, lambda e: e.tensor_tensor(out=d[:, 0:H, 0:half], in0=x1, in1=sb, op=ALU.mult), reads=rk + [sk], writes=["rtmpD" + sfx])
    P.op(eng, lambda e: e.tensor_tensor(out=x1, in0=a[:, 0:H, 0:half], in1=b2[:, 0:H, 0:half], op=ALU.subtract),
         reads=["rtmpA" + sfx, "rtmpB" + sfx, "rtmpC" + sfx, "rtmpD" + sfx] + rk, writes=rk)
    P.op(eng, lambda e: e.tensor_tensor(out=x2, in0=c[:, 0:H, 0:half], in1=d[:, 0:H, 0:half], op=ALU.add),
         reads=["rtmpC" + sfx, "rtmpD" + sfx] + rk, writes=rk)


def head_rmsnorm(P, x3, gain, sq, ssum, rk, wk, gk=None, sqk=None):
    P.op("pool", lambda e: e.tensor_tensor(out=sq[:], in0=x3, in1=x3, op=ALU.mult), reads=rk, writes=[sqk or (wk + "sq")])
    P.op("dve", lambda e: e.tensor_reduce(out=ssum[:, 0:8], in_=sq[:], axis=AX.X, op=ALU.add),
         reads=[sqk or (wk + "sq")], writes=[wk + "s0"])
    P.op("dve", lambda e: e.tensor_scalar(out=ssum[:, 8:16], in0=ssum[:, 0:8], scalar1=1.0 / 128, scalar2=1e-6,
                                           op0=ALU.mult, op1=ALU.add), reads=[wk + "s0"], writes=[wk + "s1"])
    P.op("act", lambda e: e.activation(out=ssum[:, 16:24], in_=ssum[:, 8:16], func=AF.Sqrt),
         reads=[wk + "s1"], writes=[wk + "s2"])
    P.op("dve", lambda e: e.reciprocal(out=ssum[:, 24:32], in_=ssum[:, 16:24]), reads=[wk + "s2"], writes=[wk + "s3"])
    P.op("dve", lambda e: e.tensor_tensor(out=x3, in0=x3, in1=bc(ssum[:, 24:32].unsqueeze(2), [128, 8, 128]),
                                           op=ALU.mult), reads=rk + [wk + "s3"], writes=rk)
    P.op("pool", lambda e: e.tensor_tensor(out=x3, in0=x3, in1=bc(gain[:].unsqueeze(1), [128, 8, 128]),
                                            op=ALU.mult), reads=rk + [gk or ("gain" + wk)], writes=rk)


def norm_load(K, P, T, x_src, t):
    xs = t % 2
    P.dma("sp", T["xt"][xs][:], x_src[t * 128:(t + 1) * 128, :], writes=[("xt", xs)])


def norm_block(K, P, T, x_src, t, G1, SH1, ident, blk_hT, ti, load=True):
    xs = t % 2
    xt, hb, ss, junk, pT = T["xt"], T["hb"], T["ss"], T["junk"], T["pT"]
    if load:
        norm_load(K, P, T, x_src, t)
    P.op("act", lambda e: e.activation(out=junk[:], in_=xt[xs][:], func=AF.Square, accum_out=ss[:, 0:1]),
         reads=[("xt", xs)], writes=["junk", "ss0"])
    P.op("dve", lambda e: e.tensor_scalar(out=ss[:, 1:2], in0=ss[:, 0:1], scalar1=1.0 / D, scalar2=1e-6,
                                           op0=ALU.mult, op1=ALU.add), reads=["ss0"], writes=["ss1"])
    P.op("act", lambda e: e.activation(out=ss[:, 2:3], in_=ss[:, 1:2], func=AF.Sqrt), reads=["ss1"], writes=["ss2"])
    P.op("dve", lambda e: e.reciprocal(out=ss[:, 3:4], in_=ss[:, 2:3]), reads=["ss2"], writes=["ss3"])
    P.op("dve", lambda e: e.scalar_tensor_tensor(out=xt[xs][:], in0=xt[xs][:], scalar=ss[:, 3:4], in1=G1[:],
                                                  op0=ALU.mult, op1=ALU.mult),
         reads=[("xt", xs), "ss3", "G"], writes=[("xt", xs)])
    P.op("pool", lambda e: e.tensor_tensor(out=hb[xs][:], in0=xt[xs][:], in1=SH1[:], op=ALU.add),
         reads=[("xt", xs), "SH"], writes=[("hb", xs)])
    for half in range(2):
        for kk in range(8):
            k = half * 8 + kk
            P.op("pe", lambda e, k=k, kk=kk, half=half: e.transpose(
                out=pT[half][:, kk, :], in_=hb[xs][:, k * 128:(k + 1) * 128], identity=ident[:]),
                reads=[("hb", xs), "ident"], writes=[("pT", half)])
        o = blk_hT[:, half * 8:(half + 1) * 8, ti * 128:(ti + 1) * 128]
        if half == 0:
            P.op("act", lambda e, o=o, half=half: e.copy(out=o, in_=pT[half][:]),
                 reads=[("pT", half)], writes=[("hT", ti, half)])
        else:
            P.op("dve", lambda e, o=o, half=half: e.tensor_copy(out=o, in_=pT[half][:]),
                 reads=[("pT", half)], writes=[("hT", ti, half)])


def norm_tiles_alloc(K, st, tag):
    nc = K.nc
    T = {}
    T["xt"] = [st.enter_context(nc.sbuf_tensor(tag + "xt%d" % i, [128, D], F32)) for i in range(2)]
    T["hb"] = [st.enter_context(nc.sbuf_tensor(tag + "hb%d" % i, [128, D], BF16)) for i in range(2)]
    T["ss"] = st.enter_context(nc.sbuf_tensor(tag + "ss", [128, 4], F32))
    T["junk"] = st.enter_context(nc.sbuf_tensor(tag + "junk", [128, D], BF16))
    T["pT"] = [st.enter_context(nc.psum_tensor(tag + "pT%d" % i, [128, 8, 128], BF16)) for i in range(2)]
    return T


def load_G_SH(K, P, st, which_sh, which_sc, gain_vec, tag):
    nc = K.nc
    G = st.enter_context(nc.sbuf_tensor(tag + "G", [128, D], F32))
    SH = st.enter_context(nc.sbuf_tensor(tag + "SH", [128, D], F32))
    gtmp = st.enter_context(nc.sbuf_tensor(tag + "gtmp", [128, D], F32))
    P.dma("sp", SH[:], bcast_rows(K.mod_d[which_sh * D:(which_sh + 1) * D], D), writes=["SH"])
    P.dma("sp", G[:], bcast_rows(K.mod_d[which_sc * D:(which_sc + 1) * D], D), writes=["G"])
    P.dma("sp", gtmp[:], bcast_rows(gain_vec, D), writes=["gtmp"])
    P.op("dve", lambda e: e.scalar_tensor_tensor(out=G[:], in0=G[:], scalar=1.0, in1=gtmp[:],
                                                  op0=ALU.add, op1=ALU.mult), reads=["G", "gtmp"], writes=["G"])
    return G, SH


def phase1_kv(K):
    nc, P = K.nc, K.P
    with contextlib.ExitStack() as st:
        ident = st.enter_context(nc.sbuf_tensor("ident", [128, 128], BF16))
        make_ident(K, P, ident)
        G1, SH1 = load_G_SH(K, P, st, 0, 1, K.norm1_g, "p1")
        T = norm_tiles_alloc(K, st, "p1")
        hT = [st.enter_context(nc.sbuf_tensor("hT%d" % i, [128, 16, 512], BF16)) for i in range(2)]
        W = st.enter_context(nc.sbuf_tensor("Wkv", [128, 16, 2112], BF16))
        stg = [st.enter_context(nc.sbuf_tensor("wstg%d" % i, [128, 4, 512], F32)) for i in range(2)]
        wk_k = load_weight_bf16(K, P, stg, W, 0, K.w_in[:, 1024:2048], 1024, "Wk")
        wk_v = load_weight_bf16(K, P, stg, W, 1024, K.w_in[:, 2048:3072], 1024, "Wv")
        wk_i = load_weight_bf16(K, P, stg, W, 2048, K.w_in[:, 4096:4160], 64, "Wi")
        rt = rope_tables(K, P, st, K.pos_full, 32, K.invf_att, K.invf_idx, "rf")
        gain = st.enter_context(nc.sbuf_tensor("kgain", [128, 128], F32))
        P.dma("sp", gain[:], bcast_rows(K.k_norm_g, 128), writes=["gainK"])
        def two(name, shape, dt):
            return [st.enter_context(nc.sbuf_tensor(name + str(i), shape, dt)) for i in range(2)]
        ksb2 = two("ksb", [128, 8, 128], F32)
        kbf2 = two("kbf", [128, 8, 128], BF16)
        sq2 = [st.enter_context(nc.sbuf_tensor("sq", [128, 8, 128], F32))] * 2
        ssum2 = two("ssum", [128, 32], F32)
        rtmp2 = [[st.enter_context(nc.sbuf_tensor("rtmp%d" % i, [128, 8, 16], F32)) for i in range(4)]] * 2
        vsb2 = two("vsb", [128, 8, 129], BF16)
        iksb2 = two("iksb", [128, 1, 64], F32)
        ikbf2 = two("ikbf", [128, 64], BF16)
        kTs2 = [st.enter_context(nc.sbuf_tensor("kTs", [128, 8, 128], BF16))] * 2
        ikTs2 = two("ikTs", [64, 128], BF16)
        pm = [st.enter_context(nc.psum_tensor("pm%d" % i, [128, 512], F32)) for i in range(3)]
        pk = st.enter_context(nc.psum_tensor("pk", [128, 8, 128], BF16))
        for s_ in range(2):
            P.op("pool", lambda e, s_=s_: e.memset(vsb2[s_][:], 1.0), writes=["vsb%d" % s_])
        norm_load(K, P, T, K.x_full, 0)
        for blk in range(8):
            hs = blk % 2
            for ti in range(4):
                tt_ = blk * 4 + ti
                if tt_ + 1 < 32:
                    norm_load(K, P, T, K.x_full, tt_ + 1)
                norm_block(K, P, T, K.x_full, tt_, G1, SH1, ident, hT[hs], ti, load=False)
            hkeys = [("hT", ti, half) for ti in range(4) for half in range(2)]
            P.dma("sp", K.hT_d.rearrange("k p t -> p k t")[:, :, blk * 512:(blk + 1) * 512], hT[hs][:],
                  reads=hkeys, writes=[("hT_d", blk)])
            for ti in range(4):
                t = blk * 4 + ti
                hk = [("hT", ti, 0), ("hT", ti, 1)]
                u = t % 2
                us = str(u)
                ksb, kbf, sq, ssum, rtmp, vsb, iksb, ikbf, kTs, ikTs = (ksb2[u], kbf2[u], sq2[u], ssum2[u], rtmp2[u], vsb2[u],
                                                                        iksb2[u], ikbf2[u], kTs2[u], ikTs2[u])
                for gi, (c0, n, wkeys) in enumerate([(0, 512, wk_k), (512, 512, wk_k), (1024, 512, wk_v),
                                                     (1536, 512, wk_v), (2048, 64, wk_i)]):
                    pb = pm[gi % 3]
                    for k in range(16):
                        P.op("pe", lambda e, pb=pb, k=k, c0=c0, n=n, ti=ti, hs=hs: e.matmul(
                            pb[:, 0:n], lhsT=hT[hs][:, k, ti * 128:(ti + 1) * 128], rhs=W[:, k, c0:c0 + n],
                            start=(k == 0), stop=(k == 15)), reads=hk + wkeys, writes=[("pm", gi % 3)])
                    if gi < 2:
                        P.op("act", lambda e, pb=pb, gi=gi, ksb=ksb: e.copy(out=ksb[:, gi * 4:(gi + 1) * 4, :], in_=pb[:, 0:512]),
                             reads=[("pm", gi % 3)], writes=["ksb" + us])
                    elif gi < 4:
                        g2 = gi - 2
                        P.op("act", lambda e, pb=pb, g2=g2, vsb=vsb: e.copy(out=vsb[:, g2 * 4:(g2 + 1) * 4, 0:128], in_=pb[:, 0:512]),
                             reads=[("pm", gi % 3)], writes=["vsb" + us])
                    else:
                        P.op("act", lambda e, pb=pb, iksb=iksb: e.copy(out=iksb[:, 0, :], in_=pb[:, 0:64]),
                             reads=[("pm", gi % 3)], writes=["iksb" + us])
                P.dma("sp", K.v_d[t * 128:(t + 1) * 128, :], vsb[:].rearrange("p h d -> p (h d)"),
                      reads=["vsb" + us], writes=[("v_d", t)])
                head_rmsnorm(P, ksb[:], gain, sq, ssum, ["ksb" + us], "K" + us, gk="gainK", sqk="Ksq")
                apply_rope(P, "dve", ksb[:], rt["cosa"], rt["sina"], t, 16, rtmp, ["ksb" + us], "rK")
                P.op("act", lambda e, kbf=kbf, ksb=ksb: e.copy(out=kbf[:], in_=ksb[:]), reads=["ksb" + us], writes=["kbf" + us])
                for h in range(8):
                    P.op("pe", lambda e, h=h, kbf=kbf: e.transpose(out=pk[:, h, :], in_=kbf[:, h, :], identity=ident[:]),
                         reads=["kbf" + us, "ident"], writes=["pk"])
                P.op("dve", lambda e, kTs=kTs: e.tensor_copy(out=kTs[:], in_=pk[:]), reads=["pk"], writes=["kTs"])
                P.dma("sp", K.kT_d.rearrange("h p t -> p h t")[:, :, t * 128:(t + 1) * 128], kTs[:],
                      reads=["kTs"], writes=[("kT_d", t)])
                apply_rope(P, "pool", iksb[:], rt["cosi"], rt["sini"], t, 8, rtmp, ["iksb" + us], "rI")
                P.op("act", lambda e, ikbf=ikbf, iksb=iksb: e.copy(out=ikbf[:], in_=iksb[:, 0, :]), reads=["iksb" + us], writes=["ikbf" + us])
                P.op("pe", lambda e, ikbf=ikbf: e.transpose(out=pk[0:64, 0, :], in_=ikbf[:], identity=ident[:]),
                     reads=["ikbf" + us, "ident"], writes=["pk"])
                P.op("dve", lambda e, ikTs=ikTs: e.tensor_copy(out=ikTs[:], in_=pk[0:64, 0, :]), reads=["pk"], writes=["ikTs" + us])
                P.dma("sp", K.ikT_d[:, t * 128:(t + 1) * 128], ikTs[:], reads=["ikTs" + us], writes=[("ikT_d", t)])
        P.flush()

RW0 = 4176
NRW = 1216
RW_GROUPS = [(i * 128, 128) for i in range(6)] + [(768, 96), (864, 96), (960, 128), (1088, 128)]


def phase1b_rwkv_proj(K):
    nc, P = K.nc, K.P
    with contextlib.ExitStack() as st:
        W = st.enter_context(nc.sbuf_tensor("Wr", [128, 16, NRW], BF16))
        stg = [st.enter_context(nc.sbuf_tensor("wstgb%d" % i, [128, 4, 512], F32)) for i in range(2)]
        hT = [st.enter_context(nc.sbuf_tensor("hTb%d" % i, [128, 16, 512], BF16)) for i in range(2)]
        ost = [st.enter_context(nc.sbuf_tensor("ost%d" % i, [128, 512], F32)) for i in range(4)]
        pm = [st.enter_context(nc.psum_tensor("pmb%d" % i, [128, 512], F32)) for i in range(4)]
        wkeys = load_weight_bf16(K, P, stg, W, 0, K.w_in_rw, NRW, "Wr")
        cnt = 0
        for blk in range(8):
            hs = blk % 2
            P.dma("sp", hT[hs][:], K.hT_d.rearrange("k p t -> p k t")[:, :, blk * 512:(blk + 1) * 512],
                  writes=[("hTb", hs)])
            for (r0, m) in RW_GROUPS:
                s4 = cnt % 4
                cnt += 1
                for k in range(16):
                    P.op("pe", lambda e, k=k, r0=r0, m=m, hs=hs, s4=s4: e.matmul(
                        pm[s4][0:m, :], lhsT=W[:, k, r0:r0 + m], rhs=hT[hs][:, k, :],
                        start=(k == 0), stop=(k == 15)), reads=[("hTb", hs)] + wkeys, writes=[("pmb", s4)])
                if cnt % 2 == 0:
                    P.op("act", lambda e, m=m, s4=s4: e.copy(out=ost[s4][0:m, :], in_=pm[s4][0:m, :]),
                         reads=[("pmb", s4)], writes=[("ost", s4)])
                else:
                    P.op("dve", lambda e, m=m, s4=s4: e.tensor_copy(out=ost[s4][0:m, :], in_=pm[s4][0:m, :]),
                         reads=[("pmb", s4)], writes=[("ost", s4)])
                P.dma("sp", K.yT_d[r0:r0 + m, blk * 512:(blk + 1) * 512], ost[s4][0:m, :],
                      reads=[("ost", s4)], writes=[("yT_d", r0, blk)])
        P.flush()


def phase2_own_proj(K):
    nc, P = K.nc, K.P
    with contextlib.ExitStack() as st:
        ident = st.enter_context(nc.sbuf_tensor("ident2", [128, 128], BF16))
        make_ident(K, P, ident)
        G1, SH1 = load_G_SH(K, P, st, 0, 1, K.norm1_g, "p2")
        T = norm_tiles_alloc(K, st, "p2")
        hT = [st.enter_context(nc.sbuf_tensor("hTo%d" % i, [128, 16, 512], BF16)) for i in range(2)]
        W = st.enter_context(nc.sbuf_tensor("Wq", [128, 16, 2064], BF16))
        stg = [st.enter_context(nc.sbuf_tensor("wstgq%d" % i, [128, 4, 512], F32)) for i in range(2)]
        wk_q = load_weight_bf16(K, P, stg, W, 0, K.w_in[:, 0:1024], 1024, "Wq")
        wk_iq = load_weight_bf16(K, P, stg, W, 1024, K.w_in[:, 3072:4096], 1024, "Wiq")
        wk_iw = load_weight_bf16(K, P, stg, W, 2048, K.w_in[:, 4160:4176], 16, "Wiw")
        rt = rope_tables(K, P, st, K.pos_own, 8, K.invf_att, K.invf_idx, "ro")
        gain = st.enter_context(nc.sbuf_tensor("qgain", [128, 128], F32))
        P.dma("sp", gain[:], bcast_rows(K.q_norm_g, 128), writes=["gainQ"])
        qsb = st.enter_context(nc.sbuf_tensor("qsb", [128, 8, 128], F32))
        qbf = st.enter_context(nc.sbuf_tensor("qbf", [128, 8, 128], BF16))
        sq = st.enter_context(nc.sbuf_tensor("sq2", [128, 8, 128], F32))
        ssum = st.enter_context(nc.sbuf_tensor("ssum2", [128, 32], F32))
        rtmp = [st.enter_context(nc.sbuf_tensor("rtmpq%d" % i, [128, 16, 16], F32)) for i in range(4)]
        iqsb = st.enter_context(nc.sbuf_tensor("iqsb", [128, 16, 64], F32))
        iqbf = st.enter_context(nc.sbuf_tensor("iqbf", [128, 16, 64], BF16))
        iwsb = st.enter_context(nc.sbuf_tensor("iwsb", [128, 16], F32))
        qTs = st.enter_context(nc.sbuf_tensor("qTs", [128, 8, 128], BF16))
        iqTs = st.enter_context(nc.sbuf_tensor("iqTs", [64, 128, 16], BF16))
        pm = [st.enter_context(nc.psum_tensor("pmq%d" % i, [128, 512], F32)) for i in range(3)]
        pk = st.enter_context(nc.psum_tensor("pkq", [128, 8, 128], BF16))
        for blk in range(2):
            hs = blk % 2
            for ti in range(4):
                norm_block(K, P, T, K.x_own, blk * 4 + ti, G1, SH1, ident, hT[hs], ti)
            for ti in range(4):
                t = blk * 4 + ti
                hk = [("hT", ti, 0), ("hT", ti, 1)]
                for gi, (c0, n, wkeys) in enumerate([(0, 512, wk_q), (512, 512, wk_q), (1024, 512, wk_iq),
                                                     (1536, 512, wk_iq), (2048, 16, wk_iw)]):
                    pb = pm[gi % 3]
                    for k in range(16):
                        P.op("pe", lambda e, pb=pb, k=k, c0=c0, n=n, ti=ti, hs=hs: e.matmul(
                            pb[:, 0:n], lhsT=hT[hs][:, k, ti * 128:(ti + 1) * 128], rhs=W[:, k, c0:c0 + n],
                            start=(k == 0), stop=(k == 15)), reads=hk + wkeys, writes=[("pmq", gi % 3)])
                    if gi < 2:
                        P.op("act", lambda e, pb=pb, gi=gi: e.copy(out=qsb[:, gi * 4:(gi + 1) * 4, :], in_=pb[:, 0:512]),
                             reads=[("pmq", gi % 3)], writes=["qsb"])
                    elif gi < 4:
                        g2 = gi - 2
                        P.op("act", lambda e, pb=pb, g2=g2: e.copy(out=iqsb[:, g2 * 8:(g2 + 1) * 8, :], in_=pb[:, 0:512]),
                             reads=[("pmq", gi % 3)], writes=["iqsb"])
                    else:
                        P.op("act", lambda e, pb=pb: e.activation(out=iwsb[:], in_=pb[:, 0:16], func=AF.Copy, scale=0.25),
                             reads=[("pmq", gi % 3)], writes=["iwsb"])
                P.dma("sp", K.iw_d[t * 128:(t + 1) * 128, :], iwsb[:], reads=["iwsb"], writes=[("iw_d", t)])
                head_rmsnorm(P, qsb[:], gain, sq, ssum, ["qsb"], "Q")
                apply_rope(P, "dve", qsb[:], rt["cosa"], rt["sina"], t, 16, rtmp, ["qsb"], "rQ")
                P.op("act", lambda e: e.copy(out=qbf[:], in_=qsb[:]), reads=["qsb"], writes=["qbf"])
                for h in range(8):
                    P.op("pe", lambda e, h=h: e.transpose(out=pk[:, h, :], in_=qbf[:, h, :], identity=ident[:]),
                         reads=["qbf", "ident"], writes=["pkq"])
                P.op("dve", lambda e: e.tensor_copy(out=qTs[:], in_=pk[:]), reads=["pkq"], writes=["qTs"])
                P.dma("sp", K.qT_d.rearrange("h p t -> p h t")[:, :, t * 128:(t + 1) * 128], qTs[:],
                      reads=["qTs"], writes=[("qT_d", t)])
                apply_rope(P, "pool", iqsb[:], rt["cosi"], rt["sini"], t, 8, rtmp, ["iqsb"], "rIQ")
                P.op("act", lambda e: e.activation(out=iqbf[:], in_=iqsb[:], func=AF.Copy, scale=0.125),
                     reads=["iqsb"], writes=["iqbf"])
                for half in range(2):
                    for hh in range(8):
                        h = half * 8 + hh
                        P.op("pe", lambda e, h=h, hh=hh: e.transpose(out=pk[0:64, hh, :], in_=iqbf[:, h, :],
                                                                      identity=ident[:]),
                             reads=["iqbf", "ident"], writes=["pkq"])
                    P.op("dve", lambda e, half=half: e.tensor_copy(
                        out=iqTs[:, :, half * 8:(half + 1) * 8].rearrange("p t h -> p h t"), in_=pk[0:64, :, :]),
                         reads=["pkq"], writes=["iqTs"])
                P.dma("sp", K.iqT_d[:, t * 128:(t + 1) * 128, :], iqTs[:], reads=["iqTs"], writes=[("iqT_d", t)])
        P.flush()


NIT = 24
SLOT_NK = [4, 8, 12, 16, 20, 24, 28, 32]


def phase3_attention(K):
    nc, P = K.nc, K.P
    with contextlib.ExitStack() as st:
        def sb(name, shape, dt):
            return st.enter_context(nc.sbuf_tensor(name, shape, dt))
        ident = sb("ident3", [128, 128], BF16)
        identf = sb("identf3", [128, 128], F32)
        make_ident(K, P, ident)
        P.op("dve", lambda e: e.tensor_copy(out=identf[:], in_=ident[:]), reads=["ident"], writes=["identf"])
        kT = sb("kTall", [128, 8, S], BF16)
        V = sb("Vall", [128, 32, 1032], BF16)
        ikT = sb("ikTall", [64, S], BF16)
        for h in range(8):
            P.dma("sp", kT[:, h, :], K.kT_d[h], writes=[("kT", h)])
        for q4 in range(4):
            P.dma("sp", V[:, q4 * 8:(q4 + 1) * 8, :],
                  K.v_d.rearrange("(t p) c -> p t c", p=128)[:, q4 * 8:(q4 + 1) * 8, :], writes=[("V", q4)])
        P.dma("sp", ikT[:], K.ikT_d, writes=["ikT"])
        kTk = [("kT", h) for h in range(8)]
        Vk = [("V", q4) for q4 in range(4)]
        Sel = sb("Sel", [128, 16, 128], BF16)
        pidx = sb("pidx", [128, 1], I32)
        pidf = sb("pidf", [128, 1], F32)
        score = sb("score", [128, S], F32)
        self_ = score[:, 0:2048].rearrange("p (g t) -> p g t", g=16)
        sk4 = [("score", q) for q in range(4)]
        P.op("pool", lambda e: e.iota(self_, pattern=[[-8, 16], [1, 128]], base=0, channel_multiplier=0, allow_small_or_imprecise_dtypes=True), writes=sk4)
        P.op("pool", lambda e: e.iota(pidx[:], pattern=[[0, 1]], base=0, channel_multiplier=1), writes=["pidx"])
        P.op("dve", lambda e: e.tensor_scalar(out=pidx[:], in0=pidx[:], scalar1=4, scalar2=None,
                                               op0=ALU.arith_shift_right), reads=["pidx"], writes=["pidx"])
        P.op("dve", lambda e: e.tensor_copy(out=pidf[:], in_=pidx[:]), reads=["pidx"], writes=["pidf"])
        P.op("dve", lambda e: e.tensor_scalar(out=Sel[:], in0=self_, scalar1=pidf[:, 0:1], scalar2=None,
                                               op0=ALU.is_equal), reads=sk4 + ["pidf"], writes=["Sel"])
        kposi = sb("kposi", [128, 512], I32)
        kposf = sb("kposf", [128, 512], F32)
        qpos = sb("qpos", [128, 8], F32)
        P.dma("sp", qpos[:], K.qpos_own, writes=["qpos"])
        iwg = sb("iwg", [128, 128], F32)
        wcol = sb("wcol", [128, 128], F32)
        P.dma("sp", iwg[:], K.iw_d.rearrange("(g t) h -> g (t h)", t=8), writes=["iwg"])
        A = [st.enter_context(nc.psum_tensor("A%d" % i, [128, 512], F32)) for i in range(2)]
        B = [st.enter_context(nc.psum_tensor("B%d" % i, [128, 512], F32)) for i in range(2)]
        C = st.enter_context(nc.psum_tensor("C3", [128, 8, 128], BF16))
        P.op("pe", lambda e: e.transpose(out=A[0][:, 0:128], in_=iwg[:], identity=identf[:]),
             reads=["iwg", "identf"], writes=[("A", 0)])
        P.op("dve", lambda e: e.tensor_copy(out=wcol[:], in_=A[0][:, 0:128]), reads=[("A", 0)], writes=["wcol"])
        mask01 = sb("mask01", [128, S], BF16)
        maskT = sb("maskT", [128, 32, 128], BF16)
        R = [sb("R%d" % i, [128, 512], BF16) for i in range(2)]
        pexp = [sb("pexp%d" % i, [128, 512], BF16) for i in range(2)]
        pmk = [sb("pmk%d" % i, [128, 512], BF16) for i in range(2)]
        iqTs = sb("iqTs3", [64, 128, 16], BF16)
        qTs = sb("qTs3", [128, 8, 128], BF16)
        att = sb("att", [128, 8, 128], BF16)
        attTs = sb("attTs", [128, 8, 128], BF16)
        bias = sb("cbias", [128, 512], F32)
        c2 = sb("c2", [128, NIT], F32)
        steps = sb("steps", [128, NIT], F32)
        sm = sb("sm3", [128, 8], F32)
        for k in range(NIT):
            P.op("pool", lambda e, k=k: e.memset(c2[:, k:k + 1], float(2.0 ** -(k + 1))), writes=["c2"])
        for i in range(8):
            nk = SLOT_NK[i]
            nb = nk // 4
            L = nk * 128
            P.dma("sp", iqTs[:], K.iqT_d[:, i * 128:(i + 1) * 128, :], writes=["iqTs"])
            P.dma("sp", qTs[:], K.qT_d.rearrange("h p t -> p h t")[:, :, i * 128:(i + 1) * 128], writes=["qTs"])
            for sbk in range(nb):
                bsl = sbk % 2
                for g in range(16):
                    a = (sbk * 16 + g) % 2
                    lhsT = iqTs[:, g * 8:(g + 1) * 8, :].rearrange("p t h -> p (t h)")
                    P.op("pe", lambda e, a=a, lhsT=lhsT, sbk=sbk: e.matmul(
                        A[a][:, :], lhsT=lhsT, rhs=ikT[:, sbk * 512:(sbk + 1) * 512], start=True, stop=True),
                        reads=["iqTs", "ikT"], writes=[("A", a)])
                    G = i * 16 + g
                    P.op("dve", lambda e, a=a, G=G: e.tensor_scalar(
                        out=R[a][:], in0=A[a][:, :], scalar1=0.0, scalar2=wcol[:, G:G + 1],
                        op0=ALU.max, op1=ALU.mult), reads=[("A", a), "wcol"], writes=[("R", a)])
                    P.op("pe", lambda e, a=a, g=g, bsl=bsl: e.matmul(
                        B[bsl][:, :], lhsT=Sel[:, g, :], rhs=R[a][:], start=(g == 0), stop=(g == 15)),
                        reads=[("R", a), "Sel"], writes=[("B", bsl)])
                P.op("act", lambda e, bsl=bsl, sbk=sbk: e.copy(out=score[:, sbk * 512:(sbk + 1) * 512], in_=B[bsl][:, :]),
                     reads=[("B", bsl)], writes=[("score", sbk)])
            sck = [("score", sbk) for sbk in range(nb)]
            P.op("dve", lambda e, L=L: e.tensor_reduce(out=sm[:, 0:1], in_=score[:, 0:L], axis=AX.X, op=ALU.max,
                                                        apply_absolute_value=True), reads=sck, writes=["sm0"])
            P.op("pool", lambda e, nb=nb: e.iota(kposi[:], pattern=[[1, 512]], base=(nb - 1) * 512, channel_multiplier=0),
                 writes=["kposi"])
            P.op("dve", lambda e: e.tensor_copy(out=kposf[:], in_=kposi[:]), reads=["kposi"], writes=["kposf"])
            P.op("dve", lambda e, i=i: e.tensor_scalar(out=bias[:], in0=kposf[:], scalar1=qpos[:, i:i + 1],
                                                        scalar2=-1e30, op0=ALU.is_gt, op1=ALU.mult),
                 reads=["kposf", "qpos"], writes=["bias"])
            P.op("dve", lambda e, nb=nb: e.tensor_tensor(out=score[:, (nb - 1) * 512:nb * 512],
                                                          in0=score[:, (nb - 1) * 512:nb * 512], in1=bias[:], op=ALU.add),
                 reads=["bias", ("score", nb - 1), "sm0"], writes=[("score", nb - 1)])
            P.op("dve", lambda e: e.tensor_scalar(out=sm[:, 1:2], in0=sm[:, 0:1], scalar1=-1.0, scalar2=-1.0,
                                                   op0=ALU.mult, op1=ALU.add), reads=["sm0"], writes=["lo"])
            P.op("dve", lambda e: e.tensor_scalar(out=sm[:, 5:6], in0=sm[:, 0:1], scalar1=2.0, scalar2=2.0,
                                                   op0=ALU.mult, op1=ALU.add), reads=["sm0"], writes=["d0"])
            P.op("dve", lambda e: e.tensor_scalar(out=steps[:], in0=c2[:], scalar1=sm[:, 5:6], scalar2=None,
                                                   op0=ALU.mult), reads=["d0", "c2"], writes=["steps"])
            for k in range(NIT):
                P.op("dve", lambda e, k=k: e.tensor_tensor(out=sm[:, 2:3], in0=sm[:, 1:2], in1=steps[:, k:k + 1],
                                                            op=ALU.add), reads=["lo", "steps"], writes=["mid"])
                P.op("dve", lambda e, L=L: e.tensor_scalar(out=mask01[:, 0:L], in0=score[:, 0:L], scalar1=sm[:, 2:3],
                                                            scalar2=None, op0=ALU.is_ge, op1=ALU.add,
                                                            accum_out=sm[:, 3:4]),
                     reads=sck + ["mid"], writes=["mask01", "cnt"])
                P.op("dve", lambda e, k=k: e.scalar_tensor_tensor(out=sm[:, 4:5], in0=sm[:, 3:4], scalar=255.5,
                                                                   in1=steps[:, k:k + 1], op0=ALU.is_ge, op1=ALU.mult),
                     reads=["cnt", "steps"], writes=["inc"])
                P.op("dve", lambda e: e.tensor_tensor(out=sm[:, 1:2], in0=sm[:, 1:2], in1=sm[:, 4:5], op=ALU.add),
                     reads=["lo", "inc"], writes=["lo"])
            P.op("dve", lambda e, L=L: e.tensor_scalar(out=mask01[:, 0:L], in0=score[:, 0:L], scalar1=sm[:, 1:2],
                                                        scalar2=None, op0=ALU.is_ge), reads=sck + ["lo"], writes=["mask01"])
            for kt in range(nk):
                P.op("pe", lambda e, kt=kt: e.transpose(out=C[:, kt % 8, :], in_=mask01[:, kt * 128:(kt + 1) * 128],
                                                         identity=ident[:]), reads=["mask01", "ident"], writes=["C"])
                if kt % 8 == 7 or kt == nk - 1:
                    k0 = (kt // 8) * 8
                    n8 = kt - k0 + 1
                    P.op("act", lambda e, k0=k0, n8=n8: e.copy(out=maskT[:, k0:k0 + n8, :], in_=C[:, 0:n8, :]),
                         reads=["C"], writes=[("maskT", k0 // 8)])
            mk = [("maskT", q) for q in range((nk + 7) // 8)]
            for h in range(8):
                bsl = h % 2
                for kg in range(nb):
                    a = (h * nb + kg) % 2
                    for j4 in range(4):
                        kt = kg * 4 + j4
                        P.op("pe", lambda e, a=a, j4=j4, kt=kt, h=h: e.matmul(
                            A[a][:, j4 * 128:(j4 + 1) * 128], lhsT=kT[:, h, kt * 128:(kt + 1) * 128], rhs=qTs[:, h, :],
                            start=True, stop=True), reads=kTk + ["qTs"], writes=[("A", a)])
                    P.op("act", lambda e, a=a: e.activation(out=pexp[a][:], in_=A[a][:, :], func=AF.Exp,
                                                             scale=float(128 ** -0.5)),
                         reads=[("A", a)], writes=[("pexp", a)])
                    P.op("dve", lambda e, a=a, kg=kg: e.tensor_tensor(
                        out=pmk[a][:], in0=pexp[a][:], in1=maskT[:, kg * 4:(kg + 1) * 4, :].rearrange("p a t -> p (a t)"),
                        op=ALU.mult), reads=[("pexp", a)] + mk, writes=[("pmk", a)])
                    for j4 in range(4):
                        kt = kg * 4 + j4
                        P.op("pe", lambda e, a=a, j4=j4, kt=kt, h=h, bsl=bsl, kg=kg, nb=nb: e.matmul(
                            B[bsl][:, 0:129], lhsT=pmk[a][:, j4 * 128:(j4 + 1) * 128], rhs=V[:, kt, h * 129:(h + 1) * 129],
                            start=(kg == 0 and j4 == 0), stop=(kg == nb - 1 and j4 == 3)),
                            reads=[("pmk", a)] + Vk, writes=[("B", bsl)])
                P.op("dve", lambda e, bsl=bsl: e.reciprocal(out=sm[:, 6:7], in_=B[bsl][:, 128:129]),
                     reads=[("B", bsl)], writes=["rcp"])
                P.op("dve", lambda e, bsl=bsl, h=h: e.tensor_scalar(out=att[:, h, :], in0=B[bsl][:, 0:128],
                                                                     scalar1=sm[:, 6:7], scalar2=None, op0=ALU.mult),
                     reads=[("B", bsl), "rcp"], writes=["att"])
            for h in range(8):
                P.op("pe", lambda e, h=h: e.transpose(out=C[:, h, :], in_=att[:, h, :], identity=ident[:]),
                     reads=["att", "ident"], writes=["C"])
            P.op("act", lambda e: e.copy(out=attTs[:], in_=C[:]), reads=["C"], writes=["attTs"])
            P.dma("sp", K.attT_d.rearrange("h p t -> p h t")[:, :, i * 128:(i + 1) * 128], attTs[:],
                  reads=["attTs"], writes=[("attT_d", i)])
        P.flush()

RD = BF16
NCH = 64


def tok_shift(P, dst, raw, tmp, mu_ap, rk_raw, k_tmp, k_dst, n=128):
    P.op("pool", lambda e: e.tensor_tensor(out=tmp[0:n, 1:S], in0=raw[0:n, 0:S - 1], in1=raw[0:n, 1:S], op=ALU.subtract),
         reads=[rk_raw], writes=[k_tmp])
    P.op("pool", lambda e: e.tensor_scalar(out=tmp[0:n, 0:1], in0=raw[0:n, 0:1], scalar1=-1.0, scalar2=0.0,
                                            op0=ALU.mult, op1=ALU.add), reads=[rk_raw, k_tmp], writes=[k_tmp])
    P.op("dve", lambda e: e.scalar_tensor_tensor(out=dst[0:n, :], in0=tmp[0:n, :], scalar=mu_ap, in1=raw[0:n, :],
                                                  op0=ALU.mult, op1=ALU.add), reads=[rk_raw, k_tmp], writes=[k_dst])


def phase4b_rwkv_prep(K, cts=range(2)):
    nc, P = K.nc, K.P
    with contextlib.ExitStack() as st:
        def sb(name, shape, dt):
            return st.enter_context(nc.sbuf_tensor(name, shape, dt))
        txw = sb("txw", [96, S], BF16)
        xap = sb("xap", [96, S], BF16)
        sxg = sb("sxg", [128, 2, S], BF16)
        M01 = sb("M01", [128, S], BF16)
        wup = sb("wup", [96, 256], BF16)
        aup = sb("aup", [96, 256], BF16)
        gup = sb("gup", [128, 2, 256], BF16)
        wst = sb("wst4", [128, 2, 256], F32)
        bones = sb("bones", [128, 128], BF16)
        prm = sb("prm", [128, 12, 2], F32)
        mul = sb("mul", [128, 4], F32)
        PT = sb("PT", [128, S], F32)
        KK = sb("KK", [128, S], F32)
        KP = sb("KP", [128, S], F32)
        CL = sb("CL", [128, S], F32)
        RP = sb("RP", [128, S], BF16)
        VP = sb("VP", [128, S], BF16)
        AA = sb("AA", [128, S], BF16)
        K2 = sb("K2", [128, S], BF16)
        SQb = sb("SQb", [128, S], BF16)
        OUT = [sb("OUT%d" % i, [128, S], BF16) for i in range(2)]
        PCt = sb("PCt", [128, NCH], F32)
        ps = [st.enter_context(nc.psum_tensor("ps4_%d" % i, [128, 512], F32)) for i in range(4)]
        for i, ap in enumerate(K.rw_prm):
            P.dma("sp", prm[:, i, :], ap, writes=[("prm", i)])
        prk = [("prm", i) for i in range(10)]
        P.op("dve", lambda e: e.tensor_scalar(out=prm[:, 10, :], in0=prm[:, 6, :], scalar1=-1.0, scalar2=1.0,
                                               op0=ALU.mult, op1=ALU.add), reads=prk, writes=[("prm", 10)])
        prk = prk + [("prm", 10)]
        P.dma("sp", mul[:], K.rw_mul, writes=["mul"])
        P.op("pool", lambda e: e.memset(bones[:], 0.0), writes=["bones"])
        P.op("pool", lambda e: e.memset(bones[0:64, 0:64], 1.0), reads=["bones"], writes=["bones"])
        P.op("pool", lambda e: e.memset(bones[64:128, 64:128], 1.0), reads=["bones"], writes=["bones"])
        P.op("pool", lambda e: e.iota(PT[:].rearrange("p (c t) -> p c t", t=64), pattern=[[0, NCH], [1, 64]], base=0,
                                      channel_multiplier=0, allow_small_or_imprecise_dtypes=True), writes=["PT"])
        P.op("dve", lambda e: e.tensor_scalar(out=M01[:], in0=PT[:], scalar1=0.5, scalar2=None, op0=ALU.is_gt),
             reads=["PT"], writes=["M01"])
        P.dma("sp", wst[0:96, 0, :], K.rw_w_up, writes=["wst"])
        P.op("act", lambda e: e.copy(out=wup[:], in_=wst[0:96, 0, :]), reads=["wst"], writes=["wup"])
        P.dma("sp", wst[0:96, 1, :], K.rw_a_up, reads=[], writes=["wst1"])
        P.op("act", lambda e: e.copy(out=aup[:], in_=wst[0:96, 1, :]), reads=["wst1"], writes=["aup"])
        P.dma("sp", wst[:, :, :], K.rw_g_up.rearrange("(c p) n -> p c n", p=128), reads=[], writes=["wst", "wst1"])
        P.op("act", lambda e: e.copy(out=gup[:], in_=wst[:]), reads=["wst", "wst1"], writes=["gup"])
        for (r0, n, mcol, func, dst, kd) in ((768, 96, 0, AF.Tanh, txw[:, :], "txw"), (864, 96, 1, AF.Copy, xap[:, :], "xap"),
                                             (960, 128, 2, AF.Sigmoid, sxg[:, 0, :], "sxg0"),
                                             (1088, 128, 3, AF.Sigmoid, sxg[:, 1, :], "sxg1")):
            P.dma("sp", PT[0:n, :], K.yT_d[r0:r0 + n, :], writes=["PT"])
            tok_shift(P, KP, PT, KK, mul[0:n, mcol:mcol + 1], "PT", "KK", "KP", n=n)
            P.op("act", lambda e, n=n, func=func, dst=dst: e.activation(out=dst, in_=KP[0:n, :], func=func),
                 reads=["KP"], writes=[kd])
        lk = ["txw", "xap", "sxg0", "sxg1"]
        oc = 0
        for ct in cts:
            c0 = ct * 128
            P.dma("sp", PT[:], K.yT_d[c0:c0 + 128, :], writes=["PT"])
            tok_shift(P, RP, PT, KK, prm[:, 0, ct:ct + 1], "PT", "KK", "RP")
            P.dma("sp", PT[:], K.yT_d[256 + c0:256 + c0 + 128, :], writes=["PT"])
            tok_shift(P, KP, PT, KK, prm[:, 1, ct:ct + 1], "PT", "KK", "KP")
            P.dma("sp", PT[:], K.yT_d[512 + c0:512 + c0 + 128, :], writes=["PT"])
            tok_shift(P, VP, PT, KK, prm[:, 2, ct:ct + 1], "PT", "KK", "VP")
            P.dma("sp", K.vb_d[c0:c0 + 128, :], VP[:], reads=["VP"], writes=[("vb_d", ct)])
            for blk in range(8):
                bs = slice(blk * 512, (blk + 1) * 512)
                p0, p1, p2 = ps[0], ps[1], ps[2]
                P.op("pe", lambda e, bs=bs, c0=c0: e.matmul(ps[0][:, :], lhsT=wup[:, c0:c0 + 128], rhs=txw[:, bs],
                                                             start=True, stop=True), reads=["wup", "txw"], writes=[("ps4", 0)])
                P.op("act", lambda e, bs=bs, ct=ct: e.activation(out=CL[:, bs], in_=ps[0][:, :], func=AF.Sigmoid,
                                                                  bias=prm[:, 3, ct:ct + 1]),
                     reads=[("ps4", 0)] + prk, writes=["CL"])
                P.op("pe", lambda e, bs=bs, c0=c0: e.matmul(ps[1][:, :], lhsT=aup[:, c0:c0 + 128], rhs=xap[:, bs],
                                                             start=True, stop=True), reads=["aup", "xap"], writes=[("ps4", 1)])
                P.op("act", lambda e, bs=bs, ct=ct: e.activation(out=AA[:, bs], in_=ps[1][:, :], func=AF.Sigmoid,
                                                                  bias=prm[:, 4, ct:ct + 1]),
                     reads=[("ps4", 1)] + prk, writes=["AA"])
                for cc in range(2):
                    P.op("pe", lambda e, bs=bs, c0=c0, cc=cc: e.matmul(ps[2][:, :], lhsT=gup[:, cc, c0:c0 + 128],
                                                                       rhs=sxg[:, cc, bs], start=(cc == 0), stop=(cc == 1)),
                         reads=["gup", "sxg0", "sxg1"], writes=[("ps4", 2)])
                o = OUT[oc % 2]
                P.op("dve", lambda e, bs=bs, o=o: e.tensor_copy(out=o[:, bs], in_=ps[2][:, :]),
                     reads=[("ps4", 2)], writes=[("OUT", oc % 2)])
            P.dma("sp", K.G_d[c0:c0 + 128, :], OUT[oc % 2][:], reads=[("OUT", oc % 2)], writes=[("G_d", ct)])
            oc += 1
            P.op("dve", lambda e: e.tensor_scalar(out=CL[:], in0=CL[:], scalar1=-0.6065306597126334, scalar2=None,
                                                   op0=ALU.mult), reads=["CL"], writes=["CL"])
            P.op("dve", lambda e, ct=ct: e.tensor_scalar(out=KK[:], in0=KP[:], scalar1=prm[:, 5, ct:ct + 1], scalar2=None,
                                                          op0=ALU.mult), reads=["KP"] + prk, writes=["KK"])
            P.op("act", lambda e: e.activation(out=SQb[:], in_=KK[:], func=AF.Square), reads=["KK"], writes=["SQb"])
            for blk in range(8):
                bs = slice(blk * 512, (blk + 1) * 512)
                P.op("pe", lambda e, bs=bs: e.matmul(ps[3][:, :], lhsT=bones[:], rhs=SQb[:, bs], start=True, stop=True),
                     reads=["bones", "SQb"], writes=[("ps4", 3)])
                P.op("act", lambda e, bs=bs: e.activation(out=PT[:, bs], in_=ps[3][:, :], func=AF.Sqrt),
                     reads=[("ps4", 3)], writes=["PT"])
            P.op("dve", lambda e: e.tensor_scalar(out=PT[:], in0=PT[:], scalar1=1e-12, scalar2=None, op0=ALU.max),
                 reads=["PT"], writes=["PT"])
            P.op("dve", lambda e: e.reciprocal(out=PT[:], in_=PT[:]), reads=["PT"], writes=["PT"])
            P.op("dve", lambda e: e.tensor_tensor(out=KK[:], in0=KK[:], in1=PT[:], op=ALU.mult), reads=["KK", "PT"], writes=["KK"])
            P.op("dve", lambda e, ct=ct: e.tensor_scalar(out=PT[:], in0=AA[:], scalar1=prm[:, 6, ct:ct + 1],
                                                          scalar2=prm[:, 10, ct:ct + 1], op0=ALU.mult, op1=ALU.add),
                 reads=["AA", "PT"] + prk, writes=["PT"])
            P.op("dve", lambda e: e.tensor_tensor(out=K2[:], in0=KP[:], in1=PT[:], op=ALU.mult), reads=["KP", "PT"], writes=["K2"])
            P.op("dve", lambda e, ct=ct: e.scalar_tensor_tensor(out=SQb[:], in0=RP[:], scalar=prm[:, 7, ct:ct + 1], in1=K2[:],
                                                                 op0=ALU.mult, op1=ALU.mult),
                 reads=["RP", "K2", "SQb"] + prk, writes=["SQb"])
            o = OUT[oc % 2]
            for blk in range(8):
                bs = slice(blk * 512, (blk + 1) * 512)
                P.op("pe", lambda e, bs=bs: e.matmul(ps[3][:, :], lhsT=bones[:], rhs=SQb[:, bs], start=True, stop=True),
                     reads=["bones", "SQb"], writes=[("ps4", 3)])
                P.op("dve", lambda e, bs=bs, o=o: e.tensor_tensor(out=o[:, bs], in0=ps[3][:, :], in1=VP[:, bs], op=ALU.mult),
                     reads=[("ps4", 3), "VP"], writes=[("OUT", oc % 2)])
            P.dma("sp", K.BON_d[c0:c0 + 128, :], o[:], reads=[("OUT", oc % 2)], writes=[("BON_d", ct)])
            oc += 1
            P.op("dve", lambda e: e.tensor_tensor_scan(out=PT[:], data0=M01[:], data1=CL[:], initial=0.0,
                                                        op0=ALU.mult, op1=ALU.add), reads=["M01", "CL", "PT"], writes=["PT"])
            P.op("pool", lambda e: e.tensor_tensor(out=CL[:], in0=PT[:], in1=CL[:], op=ALU.subtract),
                 reads=["PT", "CL"], writes=["CL"])
            P.op("act", lambda e: e.activation(out=CL[:], in_=CL[:], func=AF.Exp), reads=["CL"], writes=["CL"])
            v3 = lambda t: t[:].rearrange("p (c t) -> p c t", t=64)
            o = OUT[oc % 2]
            P.op("dve", lambda e, o=o: e.scalar_tensor_tensor(out=o[:], in0=KK[:], scalar=-1.0, in1=CL[:],
                                                               op0=ALU.mult, op1=ALU.mult),
                 reads=["KK", "CL"], writes=[("OUT", oc % 2)])
            P.dma("sp", K.AH_d[c0:c0 + 128, :], o[:], reads=[("OUT", oc % 2)], writes=[("AH_d", ct)])
            oc += 1
            P.op("act", lambda e: e.activation(out=CL[:], in_=PT[:], func=AF.Exp), reads=["PT", "CL"], writes=["CL"])
            o = OUT[oc % 2]
            P.op("dve", lambda e, o=o: e.tensor_tensor(out=o[:], in0=RP[:], in1=CL[:], op=ALU.mult),
                 reads=["RP", "CL"], writes=[("OUT", oc % 2)])
            P.dma("sp", K.RH_d[c0:c0 + 128, :], o[:], reads=[("OUT", oc % 2)], writes=[("RH_d", ct)])
            oc += 1
            P.op("pool", lambda e: e.tensor_copy(out=PCt[:], in_=v3(CL)[:, :, 63]), reads=["CL"], writes=["PCt"])
            P.dma("sp", K.PC_d[c0:c0 + 128, :], PCt[:], reads=["PCt"], writes=[("PC_d", ct)])
            P.op("act", lambda e: e.activation(out=PT[:], in_=PT[:], func=AF.Exp, scale=-1.0), reads=["PT"], writes=["PT"])
            o = OUT[oc % 2]
            P.op("dve", lambda e, o=o: e.tensor_tensor(out=o[:], in0=K2[:], in1=PT[:], op=ALU.mult),
                 reads=["K2", "PT"], writes=[("OUT", oc % 2)])
            P.dma("sp", K.KH_d[c0:c0 + 128, :], o[:], reads=[("OUT", oc % 2)], writes=[("KH_d", ct)])
            oc += 1
            P.op("dve", lambda e: e.tensor_tensor(out=KK[:], in0=KK[:], in1=AA[:], op=ALU.mult), reads=["KK", "AA"], writes=["KK"])
            o = OUT[oc % 2]
            P.op("dve", lambda e, o=o: e.tensor_tensor(out=o[:], in0=KK[:], in1=PT[:], op=ALU.mult),
                 reads=["KK", "PT"], writes=[("OUT", oc % 2)])
            P.dma("sp", K.BH_d[c0:c0 + 128, :], o[:], reads=[("OUT", oc % 2)], writes=[("BH_d", ct)])
            oc += 1
        P.flush()

def phase4c_rwkv_scan(K, heads=range(4)):
    nc, P = K.nc, K.P
    with contextlib.ExitStack() as st:
        def sb(name, shape, dt):
            return st.enter_context(nc.sbuf_tensor(name, shape, dt))
        ident = sb("ident4", [128, 128], BF16)
        make_ident(K, P, ident)
        MaskG = sb("MaskG", [64, 4, 128], F32)
        MaskX = sb("MaskX", [64, 8, 64], F32)
        I8 = sb("I8", [64, 64], F32)
        ones = sb("ones4", [64, 64], F32)
        P.op("pool", lambda e: e.memset(ones[:], 1.0), writes=["ones"])
        for a in range(4):
            for cq in range(2):
                P.op("pool", lambda e, cq=cq, a=a: e.affine_select(
                    out=MaskG[:, a, cq * 64:(cq + 1) * 64], in_=ones[:], pattern=[[1, 64]],
                    compare_op=(ALU.is_gt if cq == 0 else ALU.is_ge), fill=0.0, base=0, channel_multiplier=-1),
                    reads=["ones"], writes=["MaskG"])
        for a in range(8):
            P.op("pool", lambda e, a=a: e.affine_select(out=MaskX[:, a, :], in_=ones[:], pattern=[[-1, 64]],
                                                         compare_op=ALU.is_gt, fill=0.0, base=0, channel_multiplier=1),
                 reads=["ones"], writes=["MaskX"])
        P.op("dve", lambda e: e.tensor_copy(out=I8[:], in_=ident[0:64, 0:64]), reads=["ident"], writes=["I8"])
        AH = sb("AH", [64, S], RD)
        RH = sb("RH", [64, S], RD)
        BH = sb("BH", [64, S], RD)
        KH = sb("KH", [64, S], RD)
        vb = sb("vb", [64, S], BF16)
        PC = sb("PC", [64, NCH], F32)
        ARh = sb("ARh", [64, NCH, 128], RD)
        BKh = sb("BKh", [64, NCH, 128], RD)
        GmB = sb("GmB", [64, NCH, 128], RD)
        GmK = sb("GmK", [64, NCH, 128], RD)
        Btok = sb("Btok", [64, NCH, 64], RD)
        Ktok = sb("Ktok", [64, NCH, 64], RD)
        Vtok = sb("Vtok", [64, NCH, 64], RD)
        X0 = sb("X0", [64, NCH, 64], RD)
        Pm = sb("Pm", [64, NCH, 64], RD)
        oT = sb("oT", [64, S], F32)
        Ast = sb("Ast", [64, 64], F32)
        Abf = sb("Abf", [64, 64], RD)
        Tt = sb("Tt", [64, 64], F32)
        Xs = sb("Xs", [64, 64], RD)
        Us = sb("Us", [64, 64], RD)
        PSb = st.enter_context(nc.psum_tensor("PSb", [128, 1024], BF16))
        PS = [st.enter_context(nc.psum_tensor("PS%d" % i, [128, 512], F32)) for i in range(7)]
        v3 = lambda t: t[:].rearrange("p (c t) -> p c t", t=64)
        Nb = [v3(AH), v3(RH)]
        Xb = [v3(BH), v3(KH)]
        Nk = ["AH", "RH"]
        Xk = ["BH", "KH"]
        for hd in heads:
            r0 = hd * 64
            P.dma("sp", AH[:], K.AH_d[r0:r0 + 64, :], writes=["AH"])
            P.dma("sp", RH[:], K.RH_d[r0:r0 + 64, :], writes=["RH"])
            P.dma("sp", BH[:], K.BH_d[r0:r0 + 64, :], writes=["BH"])
            P.dma("sp", KH[:], K.KH_d[r0:r0 + 64, :], writes=["KH"])
            P.dma("sp", vb[:], K.vb_d[r0:r0 + 64, :], writes=["vb"])
            P.dma("sp", PC[:], K.PC_d[r0:r0 + 64, :], writes=["PC"])
            P.op("dve", lambda e: e.tensor_copy(out=ARh[:, :, 0:64], in_=v3(AH)), reads=["AH"], writes=["ARh"])
            P.op("pool", lambda e: e.tensor_copy(out=ARh[:, :, 64:128], in_=v3(RH)), reads=["RH"], writes=["ARh"])
            P.op("dve", lambda e: e.tensor_copy(out=BKh[:, :, 0:64], in_=v3(BH)), reads=["BH"], writes=["BKh"])
            P.op("pool", lambda e: e.tensor_copy(out=BKh[:, :, 64:128], in_=v3(KH)), reads=["KH"], writes=["BKh"])
            for (src, srck, col0, dst, dk) in ((BKh, "BKh", 0, Btok, "Btok"), (BKh, "BKh", 64, Ktok, "Ktok"), (None, "vb", 0, Vtok, "Vtok")):
                for c16 in range(0, NCH, 16):
                    for cc in range(16):
                        c = c16 + cc
                        in_ = vb[:, c * 64:(c + 1) * 64] if src is None else src[:, c, col0:col0 + 64]
                        P.op("pe", lambda e, cc=cc, in_=in_: e.transpose(out=PSb[0:64, cc * 64:(cc + 1) * 64], in_=in_,
                                                                         identity=ident[0:64, 0:64]),
                             reads=[srck, "ident"], writes=["PSb"])
                    P.op("act", lambda e, c16=c16, dst=dst: e.copy(out=dst[:, c16:c16 + 16, :].rearrange("p c k -> p (c k)"),
                                                                    in_=PSb[0:64, :]), reads=["PSb"], writes=[dk])
            gi = 0
            for (col0, dst, dk) in ((0, GmB, "GmB"), (64, GmK, "GmK")):
                for c4 in range(0, NCH, 4):
                    b = gi % 2
                    gi += 1
                    for cc in range(4):
                        c = c4 + cc
                        P.op("pe", lambda e, c=c, cc=cc, b=b, col0=col0: e.matmul(
                            PS[b][0:64, cc * 128:(cc + 1) * 128], lhsT=BKh[:, c, col0:col0 + 64], rhs=ARh[:, c, :],
                            start=True, stop=True), reads=["BKh", "ARh"], writes=[("PS", b)])
                    P.op("dve", lambda e, c4=c4, dst=dst, b=b: e.tensor_tensor(
                        out=dst[:, c4:c4 + 4, :], in0=PS[b][0:64, :].rearrange("p (a t) -> p a t", t=128), in1=MaskG[:],
                        op=ALU.mult), reads=[("PS", b), "MaskG"], writes=[dk])
            for c8 in range(0, NCH, 8):
                for cc in range(8):
                    c = c8 + cc
                    P.op("pe", lambda e, c=c, cc=cc: e.matmul(PS[2][0:64, cc * 64:(cc + 1) * 64], lhsT=ARh[:, c, 0:64],
                                                               rhs=BKh[:, c, 0:64], start=True, stop=True),
                         reads=["ARh", "BKh"], writes=[("PS", 2)])
                P.op("dve", lambda e, c8=c8: e.tensor_tensor(
                    out=X0[:, c8:c8 + 8, :], in0=PS[2][0:64, :].rearrange("p (a t) -> p a t", t=64), in1=MaskX[:],
                    op=ALU.mult), reads=[("PS", 2), "MaskX"], writes=["X0"])
            N0 = GmB[:, :, 0:64]
            P.op("pool", lambda e, N0=N0: e.tensor_tensor(out=Pm[:], in0=N0, in1=I8[:].unsqueeze(1).to_broadcast([64, NCH, 64]),
                                                          op=ALU.add), reads=["GmB", "I8"], writes=["Pm"])
            curN, curNk = N0, "GmB"
            curX, curXk = X0[:], "X0"
            for lvl in range(1, 6):
                nX, nXk = Xb[lvl % 2], Xk[lvl % 2]
                nN, nNk = Nb[lvl % 2], Nk[lvl % 2]
                for c8 in range(0, NCH, 8):
                    for cc in range(8):
                        c = c8 + cc
                        P.op("pe", lambda e, c=c, cc=cc, curN=curN, curX=curX: e.matmul(
                            PS[3][0:64, cc * 64:(cc + 1) * 64], lhsT=curN[:, c, :], rhs=curX[:, c, :], start=True, stop=True),
                            reads=[curNk, curXk], writes=[("PS", 3)])
                    P.op("act", lambda e, c8=c8, nX=nX: e.copy(out=nX[:, c8:c8 + 8, :],
                                                                in_=PS[3][0:64, :].rearrange("p (a t) -> p a t", t=64)),
                         reads=[("PS", 3)], writes=[nXk])
                    if lvl < 5:
                        for cc in range(8):
                            c = c8 + cc
                            P.op("pe", lambda e, c=c, cc=cc, curN=curN, curX=curX: e.matmul(
                                PS[4][0:64, cc * 64:(cc + 1) * 64], lhsT=curX[:, c, :], rhs=curN[:, c, :], start=True, stop=True),
                                reads=[curNk, curXk], writes=[("PS", 4)])
                        P.op("dve", lambda e, c8=c8, nN=nN: e.tensor_copy(out=nN[:, c8:c8 + 8, :],
                                                                           in_=PS[4][0:64, :].rearrange("p (a t) -> p a t", t=64)),
                             reads=[("PS", 4)], writes=[nNk])
                    for cc in range(8):
                        c = c8 + cc
                        P.op("pe", lambda e, c=c, cc=cc, nX=nX: e.matmul(
                            PS[5][0:64, cc * 64:(cc + 1) * 64], lhsT=nX[:, c, :], rhs=Pm[:, c, :], start=True, stop=True),
                            reads=[nXk, "Pm"], writes=[("PS", 5)])
                    P.op("dve", lambda e, c8=c8: e.tensor_tensor(
                        out=Pm[:, c8:c8 + 8, :], in0=PS[5][0:64, :].rearrange("p (a t) -> p a t", t=64),
                        in1=Pm[:, c8:c8 + 8, :], op=ALU.add), reads=[("PS", 5), "Pm"], writes=["Pm"])
                curN, curNk, curX, curXk = nN, nNk, nX, nXk
            P.op("pool", lambda e: e.memset(Ast[:], 0.0), writes=["Ast"])
            P.op("pool", lambda e: e.memset(Abf[:], 0.0), writes=["Abf"])
            for c in range(NCH):
                P.op("pe", lambda e, c=c: e.matmul(PS[0][0:64, 0:64], lhsT=ARh[:, c, 0:64], rhs=Abf[:], start=True, stop=False),
                     reads=["ARh", "Abf"], writes=[("PS", 0)])
                P.op("pe", lambda e, c=c: e.matmul(PS[0][0:64, 0:64], lhsT=GmK[:, c, 0:64], rhs=Vtok[:, c, :], start=False, stop=True),
                     reads=["GmK", "Vtok"], writes=[("PS", 0)])
                P.op("act", lambda e: e.copy(out=Xs[:], in_=PS[0][0:64, 0:64]), reads=[("PS", 0)], writes=["Xs"])
                P.op("pe", lambda e, c=c: e.matmul(PS[1][0:64, 0:64], lhsT=Pm[:, c, :], rhs=Xs[:], start=True, stop=True),
                     reads=["Pm", "Xs"], writes=[("PS", 1)])
                P.op("dve", lambda e: e.tensor_copy(out=Us[:], in_=PS[1][0:64, 0:64]), reads=[("PS", 1)], writes=["Us"])
                P.op("pe", lambda e, c=c: e.matmul(PS[6][0:64, 0:64], lhsT=Btok[:, c, :], rhs=Us[:], start=True, stop=False),
                     reads=["Btok", "Us"], writes=[("PS", 6)])
                P.op("pe", lambda e, c=c: e.matmul(PS[6][0:64, 0:64], lhsT=Ktok[:, c, :], rhs=Vtok[:, c, :], start=False, stop=True),
                     reads=["Ktok", "Vtok"], writes=[("PS", 6)])
                ob = 2 + (c % 2)
                P.op("pe", lambda e, c=c, ob=ob: e.matmul(PS[ob][0:64, 0:64], lhsT=Abf[:], rhs=ARh[:, c, 64:128], start=True, stop=False),
                     reads=["Abf", "ARh"], writes=[("PS", ob)])
                P.op("pe", lambda e, c=c, ob=ob: e.matmul(PS[ob][0:64, 0:64], lhsT=Us[:], rhs=GmB[:, c, 64:128], start=False, stop=False),
                     reads=["Us", "GmB"], writes=[("PS", ob)])
                P.op("pe", lambda e, c=c, ob=ob: e.matmul(PS[ob][0:64, 0:64], lhsT=Vtok[:, c, :], rhs=GmK[:, c, 64:128], start=False, stop=True),
                     reads=["Vtok", "GmK"], writes=[("PS", ob)])
                P.op("dve", lambda e: e.tensor_tensor(out=Tt[:], in0=PS[6][0:64, 0:64], in1=Ast[:], op=ALU.add),
                     reads=[("PS", 6), "Ast"], writes=["Tt"])
                P.op("act", lambda e, c=c: e.activation(out=Abf[:], in_=Tt[:], func=AF.Copy, scale=PC[:, c:c + 1]),
                     reads=["Tt", "PC"], writes=["Abf"])
                P.op("dve", lambda e, c=c: e.tensor_scalar(out=Ast[:], in0=Tt[:], scalar1=PC[:, c:c + 1], scalar2=None, op0=ALU.mult),
                     reads=["Tt", "PC"], writes=["Ast"])
                P.op("act", lambda e, c=c, ob=ob: e.copy(out=oT[:, c * 64:(c + 1) * 64], in_=PS[ob][0:64, 0:64]),
                     reads=[("PS", ob)], writes=[("oT", c // 8)])
            P.dma("sp", K.oT_d[r0:r0 + 64, :], oT[:], reads=[("oT", q) for q in range(8)], writes=[("oT_d", hd)])
        P.flush()

def phase4d_rwkv_post(K, cts=range(2)):
    nc, P = K.nc, K.P
    with contextlib.ExitStack() as st:
        def sb(name, shape, dt):
            return st.enter_context(nc.sbuf_tensor(name, shape, dt))
        ident = sb("ident4d", [128, 128], BF16)
        make_ident(K, P, ident)
        bonesf = sb("bonesf", [128, 128], F32)
        P.op("pool", lambda e: e.memset(bonesf[:], 0.0), writes=["bonesf"])
        P.op("pool", lambda e: e.memset(bonesf[0:64, 0:64], 1.0), reads=["bonesf"], writes=["bonesf"])
        P.op("pool", lambda e: e.memset(bonesf[64:128, 64:128], 1.0), reads=["bonesf"], writes=["bonesf"])
        prm = sb("prm4d", [128, 2, 2], F32)
        P.dma("sp", prm[:, 0, :], K.rw_prm[8], writes=["prm0"])
        P.dma("sp", prm[:, 1, :], K.rw_prm[9], writes=["prm1"])
        o = sb("o4d", [128, S], F32)
        osq = sb("osq", [128, S], F32)
        bon = sb("bon", [128, S], BF16)
        gg = sb("gg", [128, S], BF16)
        Mb = [sb("Mb%d" % i, [128, 512], F32) for i in range(2)]
        Vb = [sb("Vb%d" % i, [128, 512], F32) for i in range(2)]
        Yb = [sb("Yb%d" % i, [128, 512], F32) for i in range(2)]
        Ob = [sb("Ob%d" % i, [128, 512], BF16) for i in range(2)]
        Tk = [sb("Tk%d" % i, [128, 4, 128], BF16) for i in range(2)]
        ps = [st.enter_context(nc.psum_tensor("p4d_%d" % i, [128, 512], F32)) for i in range(4)]
        pst = [st.enter_context(nc.psum_tensor("p4dt_%d" % i, [128, 4, 128], BF16)) for i in range(2)]
        it = 0
        for ct in cts:
            c0 = ct * 128
            P.dma("sp", o[:], K.oT_d[c0:c0 + 128, :], writes=["o"])
            P.dma("sp", bon[:], K.BON_d[c0:c0 + 128, :], writes=["bon"])
            P.dma("sp", gg[:], K.G_d[c0:c0 + 128, :], writes=["gg"])
            P.op("act", lambda e: e.activation(out=osq[:], in_=o[:], func=AF.Square), reads=["o"], writes=["osq"])
            for blk in range(8):
                s2 = it % 2
                it += 1
                bs = slice(blk * 512, (blk + 1) * 512)
                P.op("pe", lambda e, bs=bs, s2=s2: e.matmul(ps[s2][:, :], lhsT=bonesf[:], rhs=o[:, bs], start=True, stop=True),
                     reads=["bonesf", "o"], writes=[("p4d", s2)])
                P.op("pe", lambda e, bs=bs, s2=s2: e.matmul(ps[2 + s2][:, :], lhsT=bonesf[:], rhs=osq[:, bs], start=True, stop=True),
                     reads=["bonesf", "osq"], writes=[("p4d", 2 + s2)])
                P.op("act", lambda e, s2=s2: e.activation(out=Mb[s2][:], in_=ps[s2][:, :], func=AF.Copy, scale=1.0 / 64),
                     reads=[("p4d", s2)], writes=[("Mb", s2)])
                P.op("pool", lambda e, s2=s2: e.tensor_tensor(out=Vb[s2][:], in0=Mb[s2][:], in1=Mb[s2][:], op=ALU.mult),
                     reads=[("Mb", s2)], writes=[("Vb", s2)])
                P.op("dve", lambda e, s2=s2: e.scalar_tensor_tensor(out=Vb[s2][:], in0=ps[2 + s2][:, :], scalar=1.0 / 64, in1=Vb[s2][:],
                                                                     op0=ALU.mult, op1=ALU.subtract),
                     reads=[("p4d", 2 + s2), ("Vb", s2)], writes=[("Vb", s2)])
                P.op("dve", lambda e, s2=s2: e.tensor_scalar(out=Vb[s2][:], in0=Vb[s2][:], scalar1=64e-5, scalar2=None, op0=ALU.add),
                     reads=[("Vb", s2)], writes=[("Vb", s2)])
                P.op("act", lambda e, s2=s2: e.activation(out=Vb[s2][:], in_=Vb[s2][:], func=AF.Sqrt),
                     reads=[("Vb", s2)], writes=[("Vb", s2)])
                P.op("dve", lambda e, s2=s2: e.reciprocal(out=Vb[s2][:], in_=Vb[s2][:]), reads=[("Vb", s2)], writes=[("Vb", s2)])
                P.op("pool", lambda e, s2=s2, bs=bs: e.tensor_tensor(out=Yb[s2][:], in0=o[:, bs], in1=Mb[s2][:], op=ALU.subtract),
                     reads=["o", ("Mb", s2)], writes=[("Yb", s2)])
                P.op("dve", lambda e, s2=s2: e.tensor_tensor(out=Yb[s2][:], in0=Yb[s2][:], in1=Vb[s2][:], op=ALU.mult),
                     reads=[("Yb", s2), ("Vb", s2)], writes=[("Yb", s2)])
                P.op("dve", lambda e, s2=s2, ct=ct: e.tensor_scalar(out=Yb[s2][:], in0=Yb[s2][:], scalar1=prm[:, 0, ct:ct + 1],
                                                                     scalar2=prm[:, 1, ct:ct + 1], op0=ALU.mult, op1=ALU.add),
                     reads=[("Yb", s2), "prm0", "prm1"], writes=[("Yb", s2)])
                P.op("pool", lambda e, s2=s2, bs=bs: e.tensor_tensor(out=Yb[s2][:], in0=Yb[s2][:], in1=bon[:, bs], op=ALU.add),
                     reads=[("Yb", s2), "bon"], writes=[("Yb", s2)])
                P.op("dve", lambda e, s2=s2, bs=bs: e.tensor_tensor(out=Ob[s2][:], in0=Yb[s2][:], in1=gg[:, bs], op=ALU.mult),
                     reads=[("Yb", s2), "gg"], writes=[("Ob", s2)])
                for q in range(4):
                    P.op("pe", lambda e, s2=s2, q=q: e.transpose(out=pst[s2][:, q, :], in_=Ob[s2][:, q * 128:(q + 1) * 128],
                                                                 identity=ident[:]),
                         reads=[("Ob", s2), "ident"], writes=[("p4dt", s2)])
                P.op("act", lambda e, s2=s2: e.copy(out=Tk[s2][:], in_=pst[s2][:]), reads=[("p4dt", s2)], writes=[("Tk", s2)])
                P.dma("sp", K.ro_loc_d[blk // 4].rearrange("(t p) c -> p t c", p=128)[:, (blk % 4) * 4:(blk % 4 + 1) * 4, c0:c0 + 128], Tk[s2][:],
                      reads=[("Tk", s2)], writes=[("ro_tok_d", ct, blk)])
        P.flush()


def phase4e_allgather(K):
    P = K.P
    for hh in range(2):
        P.coll(lambda e, hh=hh: e.collective_compute("AllGather", ALU.bypass, replica_groups=[[0, 1, 2, 3], [4, 5, 6, 7]],
                                                     ins=[K.ro_loc_d[hh].opt()], outs=[K.ro_all_d[hh].opt()]),
               reads=[("ro_loc", hh)], writes=[("ro_all", hh)])
    P.flush()


def phase5a_select(K):
    nc, P = K.nc, K.P
    with contextlib.ExitStack() as st:
        def sb(name, shape, dt):
            return st.enter_context(nc.sbuf_tensor(name, shape, dt))
        ro = sb("ro_tok", [128, 32, 1024], BF16)
        selT = sb("selT", [128, 32, 1024], BF16)
        qrow = sb("qrow", [128, 1024], F32)
        tki = sb("tki", [128, 32], I32)
        tkf = sb("tkf", [128, 32], F32)
        mo = [sb("mo%d" % i, [128, 512], BF16) for i in range(2)]
        at = sb("at5", [128, 8, 1024], BF16)
        ps = [st.enter_context(nc.psum_tensor("p5a_%d" % i, [128, 512], F32)) for i in range(2)]
        for q4 in range(4):
            for hh in range(2):
                P.dma("sp", ro[:, hh * 16:(hh + 1) * 16, q4 * 256:(q4 + 1) * 256],
                      K.ro_all_d[hh][q4 * 2048:(q4 + 1) * 2048, :].rearrange("(t p) c -> p t c", p=128), writes=[("ro", q4, hh)])
        rok = [("ro", q4, hh) for q4 in range(4) for hh in range(2)]
        P.dma("sp", qrow[:], bcast_rows(K.qpos_row, 1024), writes=["qrow"])
        P.op("pool", lambda e: e.iota(tki[:], pattern=[[128, 32]], base=0, channel_multiplier=1), writes=["tki"])
        P.op("dve", lambda e: e.tensor_copy(out=tkf[:], in_=tki[:]), reads=["tki"], writes=["tkf"])
        for T in range(32):
            P.op("dve", lambda e, T=T: e.tensor_scalar(out=selT[:, T, :], in0=qrow[:], scalar1=tkf[:, T:T + 1], scalar2=0.0,
                                                      op0=ALU.is_equal, op1=ALU.add), reads=["qrow", "tkf"], writes=[("selT", T)])
        sk = [("selT", T) for T in range(32)]
        P.dma("sp", at[:], K.attT_d.rearrange("h p t -> p h t"), writes=["at5"])
        P.dma("sp", K.mixT_d.rearrange("k p t -> p k t")[:, 0:8, :], at[:], reads=["at5"], writes=["mixa"])
        i = 0
        for m in range(8):
            for half in range(2):
                s2 = i % 2
                i += 1
                for T in range(32):
                    P.op("pe", lambda e, T=T, m=m, half=half, s2=s2: e.matmul(
                        ps[s2][:, :], lhsT=ro[:, T, m * 128:(m + 1) * 128], rhs=selT[:, T, half * 512:(half + 1) * 512],
                        start=(T == 0), stop=(T == 31)), reads=rok + sk, writes=[("p5a", s2)])
                P.op("act", lambda e, s2=s2: e.copy(out=mo[s2][:], in_=ps[s2][:, :]), reads=[("p5a", s2)], writes=[("mo", s2)])
                P.dma("sp", K.mixT_d[8 + m, :, half * 512:(half + 1) * 512], mo[s2][:], reads=[("mo", s2)], writes=[("mixr", m, half)])
        P.flush()


def phase5b_outproj(K):
    nc, P = K.nc, K.P
    with contextlib.ExitStack() as st:
        def sb(name, shape, dt):
            return st.enter_context(nc.sbuf_tensor(name, shape, dt))
        ident = sb("ident5", [128, 128], BF16)
        make_ident(K, P, ident)
        G2, SH2 = load_G_SH(K, P, st, 3, 4, K.norm2_g, "p5")
        GT1 = sb("GT1", [128, D], F32)
        P.dma("sp", GT1[:], bcast_rows(K.mod_d[2 * D:3 * D], D), writes=["GT1"])
        Wo = sb("Wo", [128, 16, D], BF16)
        stg = [sb("wstg5_%d" % i, [128, 4, 512], F32) for i in range(2)]
        wk = load_weight_bf16(K, P, stg, Wo, 0, K.w_out, D, "Wo")
        mixT = sb("mixT", [128, 16, 512], BF16)
        T = norm_tiles_alloc(K, st, "p5")
        x1 = T["xt"]
        hT = [sb("hT5_0", [128, 16, 512], BF16)] * 2
        xo = [sb("xo%d" % i, [128, D], F32) for i in range(2)]
        ps = [st.enter_context(nc.psum_tensor("p5b_%d" % i, [128, 512], F32)) for i in range(2)]
        ss, junk, hb, pT = T["ss"], T["junk"], T["hb"], T["pT"]
        gi = 0
        for blk in range(2):
            hs = 0
            P.dma("sp", mixT[:], K.mixT_d.rearrange("k p t -> p k t")[:, :, blk * 512:(blk + 1) * 512], writes=["mixT"])
            for ti in range(4):
                t = blk * 4 + ti
                xs = t % 2
                P.dma("sp", xo[xs][:], K.x_own[t * 128:(t + 1) * 128, :], writes=[("xo", xs)])
                for cg in range(4):
                    b = gi % 2
                    gi += 1
                    for k in range(16):
                        P.op("pe", lambda e, b=b, k=k, t=t, cg=cg: e.matmul(
                            ps[b][:, :], lhsT=mixT[:, k, (t % 4) * 128:(t % 4 + 1) * 128], rhs=Wo[:, k, cg * 512:(cg + 1) * 512],
                            start=(k == 0), stop=(k == 15)), reads=["mixT"] + wk, writes=[("p5b", b)])
                    cs = slice(cg * 512, (cg + 1) * 512)
                    P.op("dve", lambda e, b=b, xs=xs, cs=cs: e.tensor_tensor(out=x1[xs][:, cs], in0=ps[b][:, :], in1=GT1[:, cs], op=ALU.mult),
                         reads=[("p5b", b), "GT1"], writes=[("xt", xs)])
                    P.op("pool", lambda e, xs=xs, cs=cs: e.tensor_tensor(out=x1[xs][:, cs], in0=x1[xs][:, cs], in1=xo[xs][:, cs], op=ALU.add),
                         reads=[("xt", xs), ("xo", xs)], writes=[("xt", xs)])
                P.dma("sp", K.x1_d[t * 128:(t + 1) * 128, :], x1[xs][:], reads=[("xt", xs)], writes=[("x1_d", t)])
                P.op("act", lambda e, xs=xs: e.activation(out=junk[:], in_=x1[xs][:], func=AF.Square, accum_out=ss[:, 0:1]),
                     reads=[("xt", xs)], writes=["junk", "ss0"])
                P.op("dve", lambda e: e.tensor_scalar(out=ss[:, 1:2], in0=ss[:, 0:1], scalar1=1.0 / D, scalar2=1e-6,
                                                       op0=ALU.mult, op1=ALU.add), reads=["ss0"], writes=["ss1"])
                P.op("act", lambda e: e.activation(out=ss[:, 2:3], in_=ss[:, 1:2], func=AF.Sqrt), reads=["ss1"], writes=["ss2"])
                P.op("dve", lambda e: e.reciprocal(out=ss[:, 3:4], in_=ss[:, 2:3]), reads=["ss2"], writes=["ss3"])
                P.op("dve", lambda e, xs=xs: e.scalar_tensor_tensor(out=x1[xs][:], in0=x1[xs][:], scalar=ss[:, 3:4], in1=G2[:],
                                                                   op0=ALU.mult, op1=ALU.mult),
                     reads=[("xt", xs), "ss3", "G"], writes=[("xt", xs)])
                P.op("pool", lambda e, xs=xs: e.tensor_tensor(out=hb[xs][:], in0=x1[xs][:], in1=SH2[:], op=ALU.add),
                     reads=[("xt", xs), "SH"], writes=[("hb", xs)])
                for half in range(2):
                    for kk in range(8):
                        k = half * 8 + kk
                        P.op("pe", lambda e, k=k, kk=kk, half=half, xs=xs: e.transpose(
                            out=pT[half][:, kk, :], in_=hb[xs][:, k * 128:(k + 1) * 128], identity=ident[:]),
                            reads=[("hb", xs), "ident"], writes=[("pT", half)])
                    o_ = hT[hs][:, half * 8:(half + 1) * 8, ti * 128:(ti + 1) * 128]
                    if half == 0:
                        P.op("act", lambda e, o_=o_, half=half: e.copy(out=o_, in_=pT[half][:]), reads=[("pT", half)], writes=[("hT5", hs, ti, half)])
                    else:
                        P.op("dve", lambda e, o_=o_, half=half: e.tensor_copy(out=o_, in_=pT[half][:]), reads=[("pT", half)], writes=[("hT5", hs, ti, half)])
            P.dma("sp", K.h2T_d.rearrange("k p t -> p k t")[:, :, blk * 512:(blk + 1) * 512], hT[hs][:],
                  reads=[("hT5", hs, ti, half) for ti in range(4) for half in range(2)], writes=[("h2T_d", blk)])
        P.flush()


def phase5c_ffn(K):
    nc, P = K.nc, K.P
    NF = 5632 // 128
    with contextlib.ExitStack() as st:
        def sb(name, shape, dt):
            return st.enter_context(nc.sbuf_tensor(name, shape, dt))
        h2T = sb("h2T", [128, 16, OWN], BF16)
        P.dma("sp", h2T[:], K.h2T_d.rearrange("k p t -> p k t"), writes=["h2T"])
        ao = [sb("ao%d" % i, [128, 512], BF16) for i in range(2)]
        stg = [sb("wstg6_%d" % i, [128, 4, 512], F32) for i in range(4)]
        Wg = [sb("Wg%d" % i, [128, 16, 512], BF16) for i in range(2)]
        Wu = [sb("Wu%d" % i, [128, 16, 512], BF16) for i in range(2)]
        sg = [sb("sg%d" % i, [128, 512], F32) for i in range(2)]
        ps = [st.enter_context(nc.psum_tensor("p5c_%d" % i, [128, 512], F32)) for i in range(4)]
        gi = 0

        def load_group(fg):
            ws = fg % 2
            load_weight_bf16(K, P, stg, Wg[ws], 0, K.w_ffn_gate[:, fg * 512:(fg + 1) * 512], 512, ("Wg", ws))
            load_weight_bf16(K, P, stg, Wu[ws], 0, K.w_ffn_up[:, fg * 512:(fg + 1) * 512], 512, ("Wu", ws))
        load_group(0)
        for fg in range(11):
            ws = fg % 2
            if fg + 1 < 11:
                load_group(fg + 1)
            for f4 in range(4):
                f = fg * 4 + f4
                for tb in range(2):
                    b = gi % 2
                    gi += 1
                    for k in range(16):
                        P.op("pe", lambda e, b=b, k=k, f4=f4, tb=tb, ws=ws: e.matmul(
                            ps[b][:, :], lhsT=Wg[ws][:, k, f4 * 128:(f4 + 1) * 128], rhs=h2T[:, k, tb * 512:(tb + 1) * 512],
                            start=(k == 0), stop=(k == 15)), reads=["h2T", (("Wg", ws), 0, (k // 4) * 4)], writes=[("p5c", b)])
                    for k in range(16):
                        P.op("pe", lambda e, b=b, k=k, f4=f4, tb=tb, ws=ws: e.matmul(
                            ps[2 + b][:, :], lhsT=Wu[ws][:, k, f4 * 128:(f4 + 1) * 128], rhs=h2T[:, k, tb * 512:(tb + 1) * 512],
                            start=(k == 0), stop=(k == 15)), reads=["h2T", (("Wu", ws), 0, (k // 4) * 4)], writes=[("p5c", 2 + b)])
                    P.op("act", lambda e, b=b: e.activation(out=sg[b][:], in_=ps[b][:, :], func=AF.Silu),
                         reads=[("p5c", b)], writes=[("sg", b)])
                    P.op("dve", lambda e, b=b: e.tensor_tensor(out=ao[b][:], in0=ps[2 + b][:, :], in1=sg[b][:], op=ALU.mult),
                         reads=[("p5c", 2 + b), ("sg", b)], writes=[("ao", b)])
                    P.dma("sp", K.actT_d[f, :, tb * 512:(tb + 1) * 512], ao[b][:], reads=[("ao", b)], writes=[("actT_d", f, tb)])
        P.flush()
    with contextlib.ExitStack() as st:
        def sb(name, shape, dt):
            return st.enter_context(nc.sbuf_tensor(name, shape, dt))
        GT2 = sb("GT2", [128, D], F32)
        P.dma("sp", GT2[:], bcast_rows(K.mod_d[5 * D:6 * D], D), writes=["GT2"])
        actT = sb("actT", [128, NF, OWN], BF16)
        for q in range(4):
            P.dma("sp", actT[:, q * 11:(q + 1) * 11, :], K.actT_d.rearrange("f p t -> p f t")[:, q * 11:(q + 1) * 11, :], writes=[("actT", q)])
        ak = [("actT", q) for q in range(4)]
        stg = [sb("wstg7_%d" % i, [128, 4, 256], F32) for i in range(4)]
        ps = [st.enter_context(nc.psum_tensor("p5d_%d" % i, [128, 512], F32)) for i in range(2)]
        gi = 0
        Wd = [sb("Wd%d" % i, [128, NF, 256], BF16) for i in range(2)]
        x1 = [sb("x1_%d" % i, [128, 256], F32) for i in range(2)]
        yo = [sb("yo%d" % i, [128, 256], F32) for i in range(2)]
        wdv = K.w_ffn_down.rearrange("(k p) n -> p k n", p=128)
        engs = ["pool", "dve", "act"]

        def load_wd(cg):
            wsl = cg % 2
            for k0 in range(0, NF, 4):
                i = K.wcnt
                K.wcnt += 1
                sl = i % 4
                P.dma("sp", stg[sl][:, 0:4, 0:256], wdv[:, k0:k0 + 4, cg * 256:(cg + 1) * 256], writes=[("wstg", sl)])
                eng = engs[i % 3]
                o_ = Wd[wsl][:, k0:k0 + 4, :]
                if eng == "act":
                    P.op("act", lambda e, o_=o_, sl=sl: e.copy(out=o_, in_=stg[sl][:, 0:4, 0:256]), reads=[("wstg", sl)], writes=[("Wd", wsl, k0)])
                else:
                    P.op(eng, lambda e, o_=o_, sl=sl: e.tensor_copy(out=o_, in_=stg[sl][:, 0:4, 0:256]), reads=[("wstg", sl)], writes=[("Wd", wsl, k0)])
        load_wd(0)
        for cg in range(8):
            wsl = cg % 2
            cs = slice(cg * 256, (cg + 1) * 256)
            if cg + 1 < 8:
                load_wd(cg + 1)
            for t in range(8):
                b = gi % 2
                gi += 1
                P.dma("sp", x1[b][:], K.x1_d[t * 128:(t + 1) * 128, cs], writes=[("x1", b)])
                for f in range(NF):
                    P.op("pe", lambda e, b=b, f=f, t=t, wsl=wsl: e.matmul(ps[b][:, 0:256], lhsT=actT[:, f, t * 128:(t + 1) * 128], rhs=Wd[wsl][:, f, :],
                                                                          start=(f == 0), stop=(f == NF - 1)),
                         reads=[("actT", f // 11), ("Wd", wsl, (f // 4) * 4)], writes=[("p5c", b)])
                P.op("dve", lambda e, b=b, cs=cs: e.tensor_tensor(out=yo[b][:], in0=ps[b][:, 0:256], in1=GT2[:, cs], op=ALU.mult),
                     reads=[("p5c", b), "GT2"], writes=[("yo", b)])
                P.op("pool", lambda e, b=b: e.tensor_tensor(out=yo[b][:], in0=yo[b][:], in1=x1[b][:], op=ALU.add),
                     reads=[("yo", b), ("x1", b)], writes=[("yo", b)])
                P.dma("sp", K.out[t * 128:(t + 1) * 128, cs], yo[b][:], reads=[("yo", b)], writes=[("out", t, cg)])
        P.flush()


def phase_final_copy(K):
    nc, P = K.nc, K.P
    with contextlib.ExitStack() as st:
        xt = [st.enter_context(nc.sbuf_tensor("fx%d" % i, [128, D], F32)) for i in range(2)]
        for t in range(8):
            s = t % 2
            P.dma("sp", xt[s][:], K.x_own[t * 128:(t + 1) * 128, :], writes=[("fx", s)])
            P.dma("sp", K.out[t * 128:(t + 1) * 128, :], xt[s][:], reads=[("fx", s)], writes=[("out", t)])
        P.flush()


def own_tiles(j):
    r = []
    for m in range(4):
        r += [8 * m + j, 8 * m + 7 - j]
    return r


def build_program(debug=False, stages=99, cts=range(2), dbg_list=None, skip_att=False):
    nc = bass.Bass("TRN2", target_bir_lowering=False)
    K = Ctx()
    K.stages = stages
    K.cts = cts
    K.skip_att = skip_att
    K.nc = nc
    K.dbg = {}
    K.wcnt = 0

    def inp(name, shape, dt=F32):
        return nc.dram_tensor(name, list(shape), dt, kind="ExternalInput").ap()

    def scratch(name, shape, dt):
        return nc.dram_tensor(name, list(shape), dt, kind="Internal").ap()

    K.x_full = inp("x_full", [S, D])
    K.x_own = inp("x_own", [OWN, D])
    K.c_arr = inp("c_arr", [128, 16])
    K.pos_full = inp("pos_full", [128, 32], I32)
    K.invf_att = inp("invf_att", [128, 16])
    K.invf_idx = inp("invf_idx", [128, 8])
    K.w_ada = inp("w_ada", [D, 6 * D])
    K.b_ada = inp("b_ada", [6 * D])
    K.norm1_g = inp("norm1_g", [D])
    K.k_norm_g = inp("k_norm_g", [128])
    K.q_norm_g = inp("q_norm_g", [128])
    K.pos_own = inp("pos_own", [128, 8], I32)
    K.qpos_own = inp("qpos_own", [128, 8])
    K.w_in = inp("w_in", [D, 4176])
    K.rw_prm = [inp("rwp%d" % i, [128, 2]) for i in range(10)]
    K.w_in_rw = inp("w_in_rw", [D, 1216])
    K.rw_mul = inp("rw_mul", [128, 4])
    K.rw_w_up = inp("rw_w_up", [96, 256])
    K.rw_a_up = inp("rw_a_up", [96, 256])
    K.rw_g_up = inp("rw_g_up", [256, 256])
    K.qpos_row = inp("qpos_row", [OWN])
    K.w_out = inp("w_out", [D, D])
    K.norm2_g = inp("norm2_g", [D])
    K.w_ffn_gate = inp("w_ffn_gate", [D, 5632])
    K.w_ffn_up = inp("w_ffn_up", [D, 5632])
    K.w_ffn_down = inp("w_ffn_down", [5632, D])
    K.out = nc.dram_tensor("y_own", [OWN, D], F32, kind="ExternalOutput").ap()
    K.mod_d = scratch("mod_d", [6 * D], F32)
    K.hT_d = scratch("hT_d", [16, 128, S], BF16)
    K.kT_d = scratch("kT_d", [8, 128, S], BF16)
    K.v_d = scratch("v_d", [S, 8 * 129], BF16)
    K.ikT_d = scratch("ikT_d", [64, S], BF16)
    K.yT_d = scratch("yT_d", [1216, S], F32)
    K.qT_d = scratch("qT_d", [8, 128, OWN], BF16)
    K.iqT_d = scratch("iqT_d", [64, OWN, 16], BF16)
    K.iw_d = scratch("iw_d", [OWN, 16], F32)
    K.attT_d = scratch("attT_d", [8, 128, OWN], BF16)
    for nm in ("vb_d", "G_d", "BON_d", "AH_d", "RH_d", "BH_d", "KH_d"):
        setattr(K, nm, scratch(nm, [256, S], BF16))
    K.PC_d = scratch("PC_d", [256, NCH], F32)
    K.oT_d = scratch("oT_d", [256, S], F32)
    K.ro_loc_d = [scratch("ro_loc%d_d" % i, [2048, 256], BF16) for i in range(2)]
    K.ro_all_d = [scratch("ro_all%d_d" % i, [8192, 256], BF16) for i in range(2)]
    K.mixT_d = scratch("mixT_d", [16, 128, OWN], BF16)
    K.x1_d = scratch("x1_d", [OWN, D], F32)
    K.h2T_d = scratch("h2T_d", [16, 128, OWN], BF16)
    K.actT_d = scratch("actT_d", [44, 128, OWN], BF16)
    with contextlib.ExitStack() as stack:
        K.P = Prog(nc, stack)
        phase0_adaln(K)
        phase1_kv(K)
        if K.stages >= 2:
            phase1b_rwkv_proj(K)
        if K.stages >= 3 and not getattr(K, "skip_att", False):
            phase2_own_proj(K)
            phase3_attention(K)
        if K.stages >= 4:
            phase4b_rwkv_prep(K, cts=K.cts)
            if K.stages >= 5:
                phase4c_rwkv_scan(K, heads=[h for ct in K.cts for h in (2 * ct, 2 * ct + 1)])
        if K.stages >= 6:
            phase4d_rwkv_post(K, cts=K.cts)
            phase4e_allgather(K)
        if K.stages >= 7:
            phase5a_select(K)
            phase5b_outproj(K)
            phase5c_ffn(K)
        else:
            phase_final_copy(K)
        if debug:
            P = K.P
            allc = (("dbg_mixT", K.mixT_d, [16, 128, OWN], BF16), ("dbg_x1", K.x1_d, [OWN, D], F32),
                    ("dbg_oT", K.oT_d, [256, S], F32), ("dbg_AH", K.AH_d, [256, S], BF16), ("dbg_BH", K.BH_d, [256, S], BF16),
                    ("dbg_KH", K.KH_d, [256, S], BF16), ("dbg_RH", K.RH_d, [256, S], BF16), ("dbg_PC", K.PC_d, [256, NCH], F32),
                    ("dbg_G", K.G_d, [256, S], BF16), ("dbg_BON", K.BON_d, [256, S], BF16), ("dbg_vb", K.vb_d, [256, S], BF16),
                    ("dbg_yT", K.yT_d, [1216, S], F32), ("dbg_attT", K.attT_d, [8, 128, OWN], BF16),
                                     ("dbg_qT", K.qT_d, [8, 128, OWN], BF16), ("dbg_iqT", K.iqT_d, [64, OWN, 16], BF16),
                                     ("dbg_iw", K.iw_d, [OWN, 16], F32))
            for nm, src, shp, dt in allc:
                if dbg_list is not None and nm not in dbg_list:
                    continue
                o = dbg_out(K, nm, shp, dt)
                P.dma("sp", o, src, writes=[nm])
            P.flush()
    return nc, K


def make_in_maps(inputs, cores=range(8)):
    x = np.asarray(inputs["x"], dtype=np.float32)
    c = np.asarray(inputs["c"], dtype=np.float32)
    pos = np.asarray(inputs["positions"], dtype=np.int32)
    invf_att = (np.float32(500000.0) ** (-np.arange(16, dtype=np.float32) / np.float32(16))).astype(np.float32)
    invf_idx = (np.float32(500000.0) ** (-np.arange(8, dtype=np.float32) / np.float32(8))).astype(np.float32)
    mu = np.asarray(inputs["rwkv_mu"][0], dtype=np.float32)

    vecs = [mu[0:1024], mu[1024:2048], mu[2048:3072], inputs["rwkv_w0"][0], inputs["rwkv_a0"][0], inputs["rwkv_k_k"][0],
            inputs["rwkv_k_a"][0], np.asarray(inputs["rwkv_r_k"][0]).reshape(-1), inputs["rwkv_lnx_g"][0], inputs["rwkv_lnx_b"][0]]
    w_in_full = np.asarray(inputs["w_in"][0], dtype=np.float32)
    rw_mul = np.zeros((128, 4), np.float32)
    rw_mul[:96, 0] = mu[3072:3168]
    rw_mul[:96, 1] = mu[3168:3264]
    rw_mul[:, 2] = mu[3264:3392]
    rw_mul[:, 3] = mu[3392:3520]
    maps = []
    for core in cores:
        b, j = core // 4, core % 4
        ch = slice(256 * j, 256 * j + 256)
        rwp = {"rwp%d" % i: np.ascontiguousarray(np.asarray(v, dtype=np.float32)[ch].reshape(2, 128).T) for i, v in enumerate(vecs)}
        R0 = 4176
        w_in_rw = np.ascontiguousarray(np.concatenate([w_in_full[:, R0 + 256 * j:R0 + 256 * j + 256],
                                                       w_in_full[:, R0 + 1024 + 256 * j:R0 + 1024 + 256 * j + 256],
                                                       w_in_full[:, R0 + 2048 + 256 * j:R0 + 2048 + 256 * j + 256],
                                                       w_in_full[:, R0 + 3072:R0 + 3520]], axis=1))
        tiles = own_tiles(j)
        idx = np.concatenate([np.arange(t * 128, (t + 1) * 128) for t in tiles])
        maps.append({
            "x_full": np.ascontiguousarray(x[b]),
            "x_own": np.ascontiguousarray(x[b][idx]),
            "c_arr": np.ascontiguousarray(c[b].reshape(16, 128).T),
            "pos_full": np.ascontiguousarray(pos[b].reshape(32, 128).T),
            "invf_att": np.ascontiguousarray(np.broadcast_to(invf_att, (128, 16))),
            "invf_idx": np.ascontiguousarray(np.broadcast_to(invf_idx, (128, 8))),
            "w_ada": np.asarray(inputs["w_ada"][0], dtype=np.float32),
            "b_ada": np.asarray(inputs["b_ada"][0], dtype=np.float32),
            "norm1_g": np.asarray(inputs["norm1_g"][0], dtype=np.float32),
            "k_norm_g": np.asarray(inputs["k_norm_g"][0], dtype=np.float32),
            "q_norm_g": np.asarray(inputs["q_norm_g"][0], dtype=np.float32),
            "pos_own": np.ascontiguousarray(pos[b][idx].reshape(8, 128).T),
            "qpos_own": np.ascontiguousarray(idx.astype(np.float32).reshape(8, 128).T),
            "w_in": np.ascontiguousarray(w_in_full[:, 0:4176]),
            "qpos_row": idx.astype(np.float32),
            "w_out": np.asarray(inputs["w_out"][0], dtype=np.float32),
            "norm2_g": np.asarray(inputs["norm2_g"][0], dtype=np.float32),
            "w_ffn_gate": np.asarray(inputs["w_ffn_gate"][0], dtype=np.float32),
            "w_ffn_up": np.asarray(inputs["w_ffn_up"][0], dtype=np.float32),
            "w_ffn_down": np.asarray(inputs["w_ffn_down"][0], dtype=np.float32),
            "rw_w_up": np.ascontiguousarray(np.asarray(inputs["rwkv_w_up"][0], dtype=np.float32)[:, ch]),
            "rw_a_up": np.ascontiguousarray(np.asarray(inputs["rwkv_a_up"][0], dtype=np.float32)[:, ch]),
            "rw_g_up": np.ascontiguousarray(np.asarray(inputs["rwkv_g_up"][0], dtype=np.float32)[:, ch]),
            "w_in_rw": w_in_rw,
            "rw_mul": rw_mul,
            **rwp,
        })
    return maps


def kernel(**inputs):
    nc, K = build_program(debug=False)
    maps = make_in_maps(inputs)
    res = run_bass_kernel_spmd(nc, maps, core_ids=list(range(8)))
    out = np.zeros((2, S, D), dtype=np.float32)
    for core in range(8):
        b, j = core // 4, core % 4
        y = res.results[core]["y_own"]
        for i, t in enumerate(own_tiles(j)):
            out[b, t * 128:(t + 1) * 128] = y[i * 128:(i + 1) * 128]
    return out
```

```python
import contextlib
import numpy as np
import concourse.bass as bass
import concourse.mybir as mybir
from concourse.bass_utils import run_bass_kernel_spmd

F32 = mybir.dt.float32
BF16 = mybir.dt.bfloat16
I32 = mybir.dt.int32
AF = mybir.ActivationFunctionType
ALU = mybir.AluOpType
AX = mybir.AxisListType

D = 2048
S = 4096
NT = 32
OWN = 1024
ENGS = ("pe", "act", "dve", "pool", "sp")
DEBUG = {}


class _Op:
    __slots__ = ("eng", "fn", "deps", "needs_inc", "is_dma", "sem", "count", "idx", "prev_same_sem", "is_cc")

    def __init__(self, eng, fn, is_dma):
        self.eng = eng
        self.fn = fn
        self.deps = set()
        self.needs_inc = False
        self.is_dma = is_dma
        self.sem = None
        self.count = 0
        self.prev_same_sem = None
        self.is_cc = False


class Prog:
    def __init__(self, nc, stack, n_dma_sems=48):
        self.nc = nc
        self.n_dma_sems = n_dma_sems
        self.eng_sem = {e: stack.enter_context(nc.semaphore("s_" + e)) for e in ENGS}
        self.dma_sems = [stack.enter_context(nc.semaphore("d%d" % i)) for i in range(n_dma_sems)]
        self.bar_sem = stack.enter_context(nc.semaphore("bar"))
        self.cc_sem = stack.enter_context(nc.semaphore("ccs"))
        self.cc_cnt = 0
        self.cnt = {e: 0 for e in ENGS}
        self.dcnt = [0] * n_dma_sems
        self.rr = 0
        self.nbar = 0
        self._reset()

    def _reset(self):
        self.ops = []
        self.last_writer = {}
        self.readers = {}

    def _record(self, op, reads, writes):
        idx = len(self.ops)
        op.idx = idx
        deps = set()
        for k in reads:
            w = self.last_writer.get(k)
            if w is not None:
                deps.add(w)
        for k in writes:
            w = self.last_writer.get(k)
            if w is not None:
                deps.add(w)
            for r in self.readers.get(k, ()):
                deps.add(r)
        deps.discard(idx)
        op.deps = deps
        self.ops.append(op)
        for k in reads:
            self.readers.setdefault(k, []).append(idx)
        for k in writes:
            self.last_writer[k] = idx
            self.readers[k] = []
        return idx

    def op(self, eng, fn, reads=(), writes=()):
        return self._record(_Op(eng, fn, False), reads, writes)

    def dma(self, queue, out, in_, reads=(), writes=(), **kw):
        def fn(e, out=out, in_=in_, kw=kw):
            return e.dma_start(out=out, in_=in_, **kw)
        return self._record(_Op(queue, fn, True), reads, writes)

    def coll(self, fn, reads=(), writes=()):
        o = _Op("pool", fn, True)
        o.is_cc = True
        return self._record(o, reads, writes)

    def flush(self):
        nc = self.nc
        ops = self.ops
        for o in ops:
            nd = set()
            for d in o.deps:
                p = ops[d]
                if o.eng == "pe" and p.eng == "pe" and not p.is_dma and not o.is_dma:
                    continue
                nd.add(d)
                p.needs_inc = True
            o.deps = nd
        last_of = {}
        for o in ops:
            if not o.is_dma:
                last_of[o.eng] = o
        for o in last_of.values():
            o.needs_inc = True
        dlast = [None] * self.n_dma_sems
        for o in ops:
            if o.is_cc:
                self.cc_cnt += 1
                o.sem = self.cc_sem
                o.count = self.cc_cnt
            elif o.is_dma:
                s = self.rr % self.n_dma_sems
                self.rr += 1
                o.prev_same_sem = dlast[s]
                self.dcnt[s] += 16
                o.sem = self.dma_sems[s]
                o.count = self.dcnt[s]
                dlast[s] = o.idx
            elif o.needs_inc:
                self.cnt[o.eng] += 1
                o.sem = self.eng_sem[o.eng]
                o.count = self.cnt[o.eng]
        per_eng = {e: [o for o in ops if o.eng == e] for e in ENGS}
        final = [(self.dma_sems[s], self.dcnt[s]) for s in range(self.n_dma_sems) if self.dcnt[s] > 0]
        final += [(self.eng_sem[e], self.cnt[e]) for e in ENGS if self.cnt[e] > 0]
        if self.cc_cnt > 0:
            final.append((self.cc_sem, self.cc_cnt))
        self.nbar += 1
        nbar = self.nbar
        bar = self.bar_sem

        def run(e_name, eng):
            waited = {}
            for o in per_eng[e_name]:
                need = {}
                for d in o.deps:
                    p = ops[d]
                    if need.get(p.sem.num, (0, None))[0] < p.count:
                        need[p.sem.num] = (p.count, p.sem)
                if o.is_dma and o.prev_same_sem is not None:
                    p = ops[o.prev_same_sem]
                    if need.get(p.sem.num, (0, None))[0] < p.count:
                        need[p.sem.num] = (p.count, p.sem)
                for key, (c, s) in need.items():
                    if waited.get(key, 0) < c:
                        eng.wait_ge(s, c)
                        waited[key] = c
                ins = o.fn(eng)
                if o.is_cc:
                    ins.then_inc(o.sem)
                elif o.is_dma:
                    ins.then_inc(o.sem, 16)
                elif o.needs_inc:
                    ins.then_inc(o.sem, 1)
            if e_name == "sp":
                for s, c in final:
                    eng.wait_ge(s, c)
                eng.sem_inc(bar, 1)
            eng.wait_ge(bar, nbar)

        with nc.Block() as block:
            @block.tensor
            def _(e):
                run("pe", e)

            @block.scalar
            def _(e):
                run("act", e)

            @block.vector
            def _(e):
                run("dve", e)

            @block.gpsimd
            def _(e):
                run("pool", e)

            @block.sync
            def _(e):
                run("sp", e)
        self._reset()


class Ctx:
    pass


def bcast_rows(ap1d, n):
    return bass.AP(ap1d.tensor, ap1d.offset, [[0, 128], [1, n]])


def dbg_out(K, name, shape, dtype=F32):
    t = K.nc.dram_tensor(name, list(shape), dtype, kind="ExternalOutput")
    K.dbg[name] = t
    return t.ap()


def make_ident(K, P, ident):
    P.op("pool", lambda e: e.memset(ident[:], 0.0), writes=["ident"])
    P.op("pool", lambda e: e.affine_select(out=ident[:], in_=ident[:], pattern=[[-1, 128]],
                                           compare_op=ALU.not_equal, fill=1.0, base=0,
                                           channel_multiplier=1),
         reads=["ident"], writes=["ident"])


def phase0_adaln(K):
    nc, P = K.nc, K.P
    with contextlib.ExitStack() as st:
        c_sb = st.enter_context(nc.sbuf_tensor("c_sb", [128, 16], F32))
        cact = st.enter_context(nc.sbuf_tensor("cact", [128, 16], F32))
        wst = [st.enter_context(nc.sbuf_tensor("wst%d" % i, [128, 16, 512], F32)) for i in range(2)]
        modrow = st.enter_context(nc.sbuf_tensor("modrow", [1, 12288], F32))
        brow = st.enter_context(nc.sbuf_tensor("brow", [1, 12288], F32))
        ps = [st.enter_context(nc.psum_tensor("ps0_%d" % i, [1, 512], F32)) for i in range(2)]
        P.dma("sp", c_sb[:], K.c_arr, writes=["c_sb"])
        P.dma("sp", brow[:], K.b_ada.rearrange("(o n) -> o n", o=1), writes=["brow"])
        P.op("act", lambda e: e.activation(out=cact[:], in_=c_sb[:], func=AF.Silu),
             reads=["c_sb"], writes=["cact"])
        wv = K.w_ada.rearrange("(k p) n -> p k n", p=128)
        for nt in range(24):
            sl = nt % 2
            for hh in range(2):
                P.dma("sp", wst[sl][:, hh * 8:(hh + 1) * 8, :],
                      wv[:, hh * 8:(hh + 1) * 8, nt * 512:(nt + 1) * 512],
                      writes=[("wst", sl, hh)])
            for k in range(16):
                P.op("pe", lambda e, k=k, sl=sl: e.matmul(ps[sl][:, :], lhsT=cact[:, k:k + 1],
                                                         rhs=wst[sl][:, k, :], start=(k == 0), stop=(k == 15)),
                     reads=["cact", ("wst", sl, k // 8)], writes=[("ps0", sl)])
            P.op("dve", lambda e, nt=nt, sl=sl: e.tensor_tensor(
                out=modrow[0:1, nt * 512:(nt + 1) * 512], in0=ps[sl][:, :],
                in1=brow[0:1, nt * 512:(nt + 1) * 512], op=ALU.add),
                reads=[("ps0", sl), "brow"], writes=[("modrow", nt)])
        P.dma("sp", K.mod_d.rearrange("(o n) -> o n", o=1), modrow[:],
              reads=[("modrow", nt) for nt in range(24)], writes=["mod_d"])
        P.flush()


def load_mod_rows(K, P, tile, which, gain_ap=None, key=None):
    src = K.mod_d[which * D:(which + 1) * D]
    P.dma("sp", tile[:], bcast_rows(src, D), writes=[key])


def bc(ap, shape):
    return ap.to_broadcast(list(shape))


def load_weight_bf16(K, P, st_tiles, dst, c_dst, src2d, ncols, tag):
    wv = src2d.rearrange("(k p) n -> p k n", p=128)
    nk = wv.shape[1]
    engs = ["pool", "dve", "act"]
    for c0 in range(0, ncols, 512):
        n = min(512, ncols - c0)
        for k0 in range(0, nk, 4):
            kn = min(4, nk - k0)
            i = K.wcnt
            K.wcnt += 1
            sl = i % len(st_tiles)
            stg = st_tiles[sl]
            P.dma("sp", stg[:, 0:kn, 0:n], wv[:, k0:k0 + kn, c0:c0 + n], writes=[("wstg", sl)])
            eng = engs[i % 3]
            o = dst[:, k0:k0 + kn, c_dst + c0:c_dst + c0 + n]
            if eng == "act":
                P.op("act", lambda e, o=o, stg=stg, kn=kn, n=n: e.copy(out=o, in_=stg[:, 0:kn, 0:n]),
                     reads=[("wstg", sl)], writes=[(tag, c0, k0)])
            else:
                P.op(eng, lambda e, o=o, stg=stg, kn=kn, n=n: e.tensor_copy(out=o, in_=stg[:, 0:kn, 0:n]),
                     reads=[("wstg", sl)], writes=[(tag, c0, k0)])
    return [(tag, c0, k0) for c0 in range(0, ncols, 512) for k0 in range(0, nk, 4)]


def rope_tables(K, P, st, pos_arr, ntile, invf_att, invf_idx, tag):
    nc = K.nc
    posi = st.enter_context(nc.sbuf_tensor(tag + "posi", [128, ntile], I32))
    posf = st.enter_context(nc.sbuf_tensor(tag + "posf", [128, ntile], F32))
    iva = st.enter_context(nc.sbuf_tensor(tag + "iva", [128, 16], F32))
    ivi = st.enter_context(nc.sbuf_tensor(tag + "ivi", [128, 8], F32))
    P.dma("sp", posi[:], pos_arr, writes=[tag + "posi"])
    P.dma("sp", iva[:], invf_att, writes=[tag + "iva"])
    P.dma("sp", ivi[:], invf_idx, writes=[tag + "ivi"])
    P.op("dve", lambda e: e.tensor_copy(out=posf[:], in_=posi[:]), reads=[tag + "posi"], writes=[tag + "posf"])
    out = {}
    for nm, iv, h in (("a", iva, 16), ("i", ivi, 8)):
        u = st.enter_context(nc.sbuf_tensor(tag + "u" + nm, [128, ntile, h], F32))
        ui = st.enter_context(nc.sbuf_tensor(tag + "ui" + nm, [128, ntile, h], I32))
        uf = st.enter_context(nc.sbuf_tensor(tag + "uf" + nm, [128, ntile, h], F32))
        for fn, off in (("sin", 0.0), ("cos", 0.25)):
            tb = st.enter_context(nc.sbuf_tensor(tag + fn + nm, [128, ntile, h], F32))
            kk = tag + fn + nm
            P.op("dve", lambda e, u=u, iv=iv, h=h: e.tensor_tensor(
                out=u[:], in0=bc(posf[:].unsqueeze(2), [128, ntile, h]),
                in1=bc(iv[:].unsqueeze(1), [128, ntile, h]), op=ALU.mult),
                reads=[tag + "posf", tag + "iv" + nm], writes=[tag + "U" + nm])
            P.op("dve", lambda e, u=u, off=off: e.tensor_scalar(
                out=u[:], in0=u[:], scalar1=float(1.0 / (2 * np.pi)), scalar2=off, op0=ALU.mult, op1=ALU.add),
                reads=[tag + "U" + nm], writes=[tag + "U" + nm])
            P.op("dve", lambda e, u=u, ui=ui: e.tensor_copy(out=ui[:], in_=u[:]), reads=[tag + "U" + nm], writes=[tag + "UI" + nm])
            P.op("dve", lambda e, uf=uf, ui=ui: e.tensor_copy(out=uf[:], in_=ui[:]), reads=[tag + "UI" + nm], writes=[tag + "UF" + nm])
            P.op("dve", lambda e, u=u, uf=uf: e.tensor_tensor(out=u[:], in0=u[:], in1=uf[:], op=ALU.subtract),
                 reads=[tag + "U" + nm, tag + "UF" + nm], writes=[tag + "U" + nm])
            P.op("dve", lambda e, u=u: e.tensor_scalar(out=u[:], in0=u[:], scalar1=-0.5, scalar2=0.5,
                                                        op0=ALU.max, op1=ALU.min),
                 reads=[tag + "U" + nm], writes=[tag + "U" + nm])
            P.op("act", lambda e, u=u, tb=tb: e.activation(out=tb[:], in_=u[:], func=AF.Sin,
                                                            scale=float(2 * np.pi)),
                 reads=[tag + "U" + nm], writes=[kk])
            out[fn + nm] = (tb, kk)
    return out


def apply_rope(P, eng, x4, cos, sin, t, half, tmp, rk, wk, sfx=""):
    ctb, ck = cos
    stb, sk = sin
    H = x4.shape[1]
    x1 = x4[:, :, 0:half]
    x2 = x4[:, :, half:2 * half]
    cb = bc(ctb[:, t, :].unsqueeze(1), [128, H, half])
    sb = bc(stb[:, t, :].unsqueeze(1), [128, H, half])
    a, b2, c, d = tmp
    P.op(eng, lambda e: e.tensor_tensor(out=a[:, 0:H, 0:half], in0=x1, in1=cb, op=ALU.mult), reads=rk + [ck], writes=["rtmpA" + sfx])
    P.op(eng, lambda e: e.tensor_tensor(out=b2[:, 0:H, 0:half], in0=x2, in1=sb, op=ALU.mult), reads=rk + [sk], writes=["rtmpB" + sfx])
    P.op(eng, lambda e: e.tensor_tensor(out=c[:, 0:H, 0:half], in0=x2, in1=cb, op=ALU.mult), reads=rk + [ck], writes=["rtmpC" + sfx])
    P.op(eng, lambda e: e.tensor_tensor(out=d[:, 0:H, 0:half], in0=x1, in1=sb, op=ALU.mult), reads=rk + [sk], writes=["rtmpD" + sfx])
    P.op(eng, lambda e: e.tensor_tensor(out=x1, in0=a[:, 0:H, 0:half], in1=b2[:, 0:H, 0:half], op=ALU.subtract),
         reads=["rtmpA" + sfx, "rtmpB" + sfx, "rtmpC" + sfx, "rtmpD" + sfx] + rk, writes=rk)
    P.op(eng, lambda e: e.tensor_tensor(out=x2, in0=c[:, 0:H, 0:half], in1=d[:, 0:H, 0:half], op=ALU.add),
         reads=["rtmpC" + sfx, "rtmpD" + sfx] + rk, writes=rk)


def head_rmsnorm(P, x3, gain, sq, ssum, rk, wk, gk=None, sqk=None):
    P.op("pool", lambda e: e.tensor_tensor(out=sq[:], in0=x3, in1=x3, op=ALU.mult), reads=rk, writes=[sqk or (wk + "sq")])
    P.op("dve", lambda e: e.tensor_reduce(out=ssum[:, 0:8], in_=sq[:], axis=AX.X, op=ALU.add),
         reads=[sqk or (wk + "sq")], writes=[wk + "s0"])
    P.op("dve", lambda e: e.tensor_scalar(out=ssum[:, 8:16], in0=ssum[:, 0:8], scalar1=1.0 / 128, scalar2=1e-6,
                                           op0=ALU.mult, op1=ALU.add), reads=[wk + "s0"], writes=[wk + "s1"])
    P.op("act", lambda e: e.activation(out=ssum[:, 16:24], in_=ssum[:, 8:16], func=AF.Sqrt),
         reads=[wk + "s1"], writes=[wk + "s2"])
    P.op("dve", lambda e: e.reciprocal(out=ssum[:, 24:32], in_=ssum[:, 16:24]), reads=[wk + "s2"], writes=[wk + "s3"])
    P.op("dve", lambda e: e.tensor_tensor(out=x3, in0=x3, in1=bc(ssum[:, 24:32].unsqueeze(2), [128, 8, 128]),
                                           op=ALU.mult), reads=rk + [wk + "s3"], writes=rk)
    P.op("pool", lambda e: e.tensor_tensor(out=x3, in0=x3, in1=bc(gain[:].unsqueeze(1), [128, 8, 128]),
                                            op=ALU.mult), reads=rk + [gk or ("gain" + wk)], writes=rk)


def norm_load(K, P, T, x_src, t):
    xs = t % 2
    P.dma("sp", T["xt"][xs][:], x_src[t * 128:(t + 1) * 128, :], writes=[("xt", xs)])


def norm_block(K, P, T, x_src, t, G1, SH1, ident, blk_hT, ti, load=True):
    xs = t % 2
    xt, hb, ss, junk, pT = T["xt"], T["hb"], T["ss"], T["junk"], T["pT"]
    if load:
        norm_load(K, P, T, x_src, t)
    P.op("act", lambda e: e.activation(out=junk[:], in_=xt[xs][:], func=AF.Square, accum_out=ss[:, 0:1]),
         reads=[("xt", xs)], writes=["junk", "ss0"])
    P.op("dve", lambda e: e.tensor_scalar(out=ss[:, 1:2], in0=ss[:, 0:1], scalar1=1.0 / D, scalar2=1e-6,
                                           op0=ALU.mult, op1=ALU.add), reads=["ss0"], writes=["ss1"])
    P.op("act", lambda e: e.activation(out=ss[:, 2:3], in_=ss[:, 1:2], func=AF.Sqrt), reads=["ss1"], writes=["ss2"])
    P.op("dve", lambda e: e.reciprocal(out=ss[:, 3:4], in_=ss[:, 2:3]), reads=["ss2"], writes=["ss3"])
    P.op("dve", lambda e: e.scalar_tensor_tensor(out=xt[xs][:], in0=xt[xs][:], scalar=ss[:, 3:4], in1=G1[:],
                                                  op0=ALU.mult, op1=ALU.mult),
         reads=[("xt", xs), "ss3", "G"], writes=[("xt", xs)])
    P.op("pool", lambda e: e.tensor_tensor(out=hb[xs][:], in0=xt[xs][:], in1=SH1[:], op=ALU.add),
         reads=[("xt", xs), "SH"], writes=[("hb", xs)])
    for half in range(2):
        for kk in range(8):
            k = half * 8 + kk
            P.op("pe", lambda e, k=k, kk=kk, half=half: e.transpose(
                out=pT[half][:, kk, :], in_=hb[xs][:, k * 128:(k + 1) * 128], identity=ident[:]),
                reads=[("hb", xs), "ident"], writes=[("pT", half)])
        o = blk_hT[:, half * 8:(half + 1) * 8, ti * 128:(ti + 1) * 128]
        if half == 0:
            P.op("act", lambda e, o=o, half=half: e.copy(out=o, in_=pT[half][:]),
                 reads=[("pT", half)], writes=[("hT", ti, half)])
        else:
            P.op("dve", lambda e, o=o, half=half: e.tensor_copy(out=o, in_=pT[half][:]),
                 reads=[("pT", half)], writes=[("hT", ti, half)])


def norm_tiles_alloc(K, st, tag):
    nc = K.nc
    T = {}
    T["xt"] = [st.enter_context(nc.sbuf_tensor(tag + "xt%d" % i, [128, D], F32)) for i in range(2)]
    T["hb"] = [st.enter_context(nc.sbuf_tensor(tag + "hb%d" % i, [128, D], BF16)) for i in range(2)]
    T["ss"] = st.enter_context(nc.sbuf_tensor(tag + "ss", [128, 4], F32))
    T["junk"] = st.enter_context(nc.sbuf_tensor(tag + "junk", [128, D], BF16))
    T["pT"] = [st.enter_context(nc.psum_tensor(tag + "pT%d" % i, [128, 8, 128], BF16)) for i in range(2)]
    return T


def load_G_SH(K, P, st, which_sh, which_sc, gain_vec, tag):
    nc = K.nc
    G = st.enter_context(nc.sbuf_tensor(tag + "G", [128, D], F32))
    SH = st.enter_context(nc.sbuf_tensor(tag + "SH", [128, D], F32))
    gtmp = st.enter_context(nc.sbuf_tensor(tag + "gtmp", [128, D], F32))
    P.dma("sp", SH[:], bcast_rows(K.mod_d[which_sh * D:(which_sh + 1) * D], D), writes=["SH"])
    P.dma("sp", G[:], bcast_rows(K.mod_d[which_sc * D:(which_sc + 1) * D], D), writes=["G"])
    P.dma("sp", gtmp[:], bcast_rows(gain_vec, D), writes=["gtmp"])
    P.op("dve", lambda e: e.scalar_tensor_tensor(out=G[:], in0=G[:], scalar=1.0, in1=gtmp[:],
                                                  op0=ALU.add, op1=ALU.mult), reads=["G", "gtmp"], writes=["G"])
    return G, SH


def phase1_kv(K):
    nc, P = K.nc, K.P
    with contextlib.ExitStack() as st:
        ident = st.enter_context(nc.sbuf_tensor("ident", [128, 128], BF16))
        make_ident(K, P, ident)
        G1, SH1 = load_G_SH(K, P, st, 0, 1, K.norm1_g, "p1")
        T = norm_tiles_alloc(K, st, "p1")
        hT = [st.enter_context(nc.sbuf_tensor("hT%d" % i, [128, 16, 512], BF16)) for i in range(2)]
        W = st.enter_context(nc.sbuf_tensor("Wkv", [128, 16, 2112], BF16))
        stg = [st.enter_context(nc.sbuf_tensor("wstg%d" % i, [128, 4, 512], F32)) for i in range(2)]
        wk_k = load_weight_bf16(K, P, stg, W, 0, K.w_in[:, 1024:2048], 1024, "Wk")
        wk_v = load_weight_bf16(K, P, stg, W, 1024, K.w_in[:, 2048:3072], 1024, "Wv")
        wk_i = load_weight_bf16(K, P, stg, W, 2048, K.w_in[:, 4096:4160], 64, "Wi")
        rt = rope_tables(K, P, st, K.pos_full, 32, K.invf_att, K.invf_idx, "rf")
        gain = st.enter_context(nc.sbuf_tensor("kgain", [128, 128], F32))
        P.dma("sp", gain[:], bcast_rows(K.k_norm_g, 128), writes=["gainK"])
        def two(name, shape, dt):
            return [st.enter_context(nc.sbuf_tensor(name + str(i), shape, dt)) for i in range(2)]
        ksb2 = two("ksb", [128, 8, 128], F32)
        kbf2 = two("kbf", [128, 8, 128], BF16)
        sq2 = [st.enter_context(nc.sbuf_tensor("sq", [128, 8, 128], F32))] * 2
        ssum2 = two("ssum", [128, 32], F32)
        rtmp2 = [[st.enter_context(nc.sbuf_tensor("rtmp%d" % i, [128, 8, 16], F32)) for i in range(4)]] * 2
        vsb2 = two("vsb", [128, 8, 129], BF16)
        iksb2 = two("iksb", [128, 1, 64], F32)
        ikbf2 = two("ikbf", [128, 64], BF16)
        kTs2 = [st.enter_context(nc.sbuf_tensor("kTs", [128, 8, 128], BF16))] * 2
        ikTs2 = two("ikTs", [64, 128], BF16)
        pm = [st.enter_context(nc.psum_tensor("pm%d" % i, [128, 512], F32)) for i in range(3)]
        pk = st.enter_context(nc.psum_tensor("pk", [128, 8, 128], BF16))
        for s_ in range(2):
            P.op("pool", lambda e, s_=s_: e.memset(vsb2[s_][:], 1.0), writes=["vsb%d" % s_])
        norm_load(K, P, T, K.x_full, 0)
        for blk in range(8):
            hs = blk % 2
            for ti in range(4):
                tt_ = blk * 4 + ti
                if tt_ + 1 < 32:
                    norm_load(K, P, T, K.x_full, tt_ + 1)
                norm_block(K, P, T, K.x_full, tt_, G1, SH1, ident, hT[hs], ti, load=False)
            hkeys = [("hT", ti, half) for ti in range(4) for half in range(2)]
            P.dma("sp", K.hT_d.rearrange("k p t -> p k t")[:, :, blk * 512:(blk + 1) * 512], hT[hs][:],
                  reads=hkeys, writes=[("hT_d", blk)])
            for ti in range(4):
                t = blk * 4 + ti
                hk = [("hT", ti, 0), ("hT", ti, 1)]
                u = t % 2
                us = str(u)
                ksb, kbf, sq, ssum, rtmp, vsb, iksb, ikbf, kTs, ikTs = (ksb2[u], kbf2[u], sq2[u], ssum2[u], rtmp2[u], vsb2[u],
                                                                        iksb2[u], ikbf2[u], kTs2[u], ikTs2[u])
                for gi, (c0, n, wkeys) in enumerate([(0, 512, wk_k), (512, 512, wk_k), (1024, 512, wk_v),
                                                     (1536, 512, wk_v), (2048, 64, wk_i)]):
                    pb = pm[gi % 3]
                    for k in range(16):
                        P.op("pe", lambda e, pb=pb, k=k, c0=c0, n=n, ti=ti, hs=hs: e.matmul(
                            pb[:, 0:n], lhsT=hT[hs][:, k, ti * 128:(ti + 1) * 128], rhs=W[:, k, c0:c0 + n],
                            start=(k == 0), stop=(k == 15)), reads=hk + wkeys, writes=[("pm", gi % 3)])
                    if gi < 2:
                        P.op("act", lambda e, pb=pb, gi=gi, ksb=ksb: e.copy(out=ksb[:, gi * 4:(gi + 1) * 4, :], in_=pb[:, 0:512]),
                             reads=[("pm", gi % 3)], writes=["ksb" + us])
                    elif gi < 4:
                        g2 = gi - 2
                        P.op("act", lambda e, pb=pb, g2=g2, vsb=vsb: e.copy(out=vsb[:, g2 * 4:(g2 + 1) * 4, 0:128], in_=pb[:, 0:512]),
                             reads=[("pm", gi % 3)], writes=["vsb" + us])
                    else:
                        P.op("act", lambda e, pb=pb, iksb=iksb: e.copy(out=iksb[:, 0, :], in_=pb[:, 0:64]),
                             reads=[("pm", gi % 3)], writes=["iksb" + us])
                P.dma("sp", K.v_d[t * 128:(t + 1) * 128, :], vsb[:].rearrange("p h d -> p (h d)"),
                      reads=["vsb" + us], writes=[("v_d", t)])
                head_rmsnorm(P, ksb[:], gain, sq, ssum, ["ksb" + us], "K" + us, gk="gainK", sqk="Ksq")
                apply_rope(P, "dve", ksb[:], rt["cosa"], rt["sina"], t, 16, rtmp, ["ksb" + us], "rK")
                P.op("act", lambda e, kbf=kbf, ksb=ksb: e.copy(out=kbf[:], in_=ksb[:]), reads=["ksb" + us], writes=["kbf" + us])
                for h in range(8):
                    P.op("pe", lambda e, h=h, kbf=kbf: e.transpose(out=pk[:, h, :], in_=kbf[:, h, :], identity=ident[:]),
                         reads=["kbf" + us, "ident"], writes=["pk"])
                P.op("dve", lambda e, kTs=kTs: e.tensor_copy(out=kTs[:], in_=pk[:]), reads=["pk"], writes=["kTs"])
                P.dma("sp", K.kT_d.rearrange("h p t -> p h t")[:, :, t * 128:(t + 1) * 128], kTs[:],
                      reads=["kTs"], writes=[("kT_d", t)])
                apply_rope(P, "pool", iksb[:], rt["cosi"], rt["sini"], t, 8, rtmp, ["iksb" + us], "rI")
                P.op("act", lambda e, ikbf=ikbf, iksb=iksb: e.copy(out=ikbf[:], in_=iksb[:, 0, :]), reads=["iksb" + us], writes=["ikbf" + us])
                P.op("pe", lambda e, ikbf=ikbf: e.transpose(out=pk[0:64, 0, :], in_=ikbf[:], identity=ident[:]),
                     reads=["ikbf" + us, "ident"], writes=["pk"])
                P.op("dve", lambda e, ikTs=ikTs: e.tensor_copy(out=ikTs[:], in_=pk[0:64, 0, :]), reads=["pk"], writes=["ikTs" + us])
                P.dma("sp", K.ikT_d[:, t * 128:(t + 1) * 128], ikTs[:], reads=["ikTs" + us], writes=[("ikT_d", t)])
        P.flush()

RW0 = 4176
NRW = 1216
RW_GROUPS = [(i * 128, 128) for i in range(6)] + [(768, 96), (864, 96), (960, 128), (1088, 128)]


def phase1b_rwkv_proj(K):
    nc, P = K.nc, K.P
    with contextlib.ExitStack() as st:
        W = st.enter_context(nc.sbuf_tensor("Wr", [128, 16, NRW], BF16))
        stg = [st.enter_context(nc.sbuf_tensor("wstgb%d" % i, [128, 4, 512], F32)) for i in range(2)]
        hT = [st.enter_context(nc.sbuf_tensor("hTb%d" % i, [128, 16, 512], BF16)) for i in range(2)]
        ost = [st.enter_context(nc.sbuf_tensor("ost%d" % i, [128, 512], F32)) for i in range(4)]
        pm = [st.enter_context(nc.psum_tensor("pmb%d" % i, [128, 512], F32)) for i in range(4)]
        wkeys = load_weight_bf16(K, P, stg, W, 0, K.w_in_rw, NRW, "Wr")
        cnt = 0
        for blk in range(8):
            hs = blk % 2
            P.dma("sp", hT[hs][:], K.hT_d.rearrange("k p t -> p k t")[:, :, blk * 512:(blk + 1) * 512],
                  writes=[("hTb", hs)])
            for (r0, m) in RW_GROUPS:
                s4 = cnt % 4
                cnt += 1
                for k in range(16):
                    P.op("pe", lambda e, k=k, r0=r0, m=m, hs=hs, s4=s4: e.matmul(
                        pm[s4][0:m, :], lhsT=W[:, k, r0:r0 + m], rhs=hT[hs][:, k, :],
                        start=(k == 0), stop=(k == 15)), reads=[("hTb", hs)] + wkeys, writes=[("pmb", s4)])
                if cnt % 2 == 0:
                    P.op("act", lambda e, m=m, s4=s4: e.copy(out=ost[s4][0:m, :], in_=pm[s4][0:m, :]),
                         reads=[("pmb", s4)], writes=[("ost", s4)])
                else:
                    P.op("dve", lambda e, m=m, s4=s4: e.tensor_copy(out=ost[s4][0:m, :], in_=pm[s4][0:m, :]),
                         reads=[("pmb", s4)], writes=[("ost", s4)])
                P.dma("sp", K.yT_d[r0:r0 + m, blk * 512:(blk + 1) * 512], ost[s4][0:m, :],
                      reads=[("ost", s4)], writes=[("yT_d", r0, blk)])
        P.flush()


def phase2_own_proj(K):
    nc, P = K.nc, K.P
    with contextlib.ExitStack() as st:
        ident = st.enter_context(nc.sbuf_tensor("ident2", [128, 128], BF16))
        make_ident(K, P, ident)
        G1, SH1 = load_G_SH(K, P, st, 0, 1, K.norm1_g, "p2")
        T = norm_tiles_alloc(K, st, "p2")
        hT = [st.enter_context(nc.sbuf_tensor("hTo%d" % i, [128, 16, 512], BF16)) for i in range(2)]
        W = st.enter_context(nc.sbuf_tensor("Wq", [128, 16, 2064], BF16))
        stg = [st.enter_context(nc.sbuf_tensor("wstgq%d" % i, [128, 4, 512], F32)) for i in range(2)]
        wk_q = load_weight_bf16(K, P, stg, W, 0, K.w_in[:, 0:1024], 1024, "Wq")
        wk_iq = load_weight_bf16(K, P, stg, W, 1024, K.w_in[:, 3072:4096], 1024, "Wiq")
        wk_iw = load_weight_bf16(K, P, stg, W, 2048, K.w_in[:, 4160:4176], 16, "Wiw")
        rt = rope_tables(K, P, st, K.pos_own, 8, K.invf_att, K.invf_idx, "ro")
        gain = st.enter_context(nc.sbuf_tensor("qgain", [128, 128], F32))
        P.dma("sp", gain[:], bcast_rows(K.q_norm_g, 128), writes=["gainQ"])
        qsb = st.enter_context(nc.sbuf_tensor("qsb", [128, 8, 128], F32))
        qbf = st.enter_context(nc.sbuf_tensor("qbf", [128, 8, 128], BF16))
        sq = st.enter_context(nc.sbuf_tensor("sq2", [128, 8, 128], F32))
        ssum = st.enter_context(nc.sbuf_tensor("ssum2", [128, 32], F32))
        rtmp = [st.enter_context(nc.sbuf_tensor("rtmpq%d" % i, [128, 16, 16], F32)) for i in range(4)]
        iqsb = st.enter_context(nc.sbuf_tensor("iqsb", [128, 16, 64], F32))
        iqbf = st.enter_context(nc.sbuf_tensor("iqbf", [128, 16, 64], BF16))
        iwsb = st.enter_context(nc.sbuf_tensor("iwsb", [128, 16], F32))
        qTs = st.enter_context(nc.sbuf_tensor("qTs", [128, 8, 128], BF16))
        iqTs = st.enter_context(nc.sbuf_tensor("iqTs", [64, 128, 16], BF16))
        pm = [st.enter_context(nc.psum_tensor("pmq%d" % i, [128, 512], F32)) for i in range(3)]
        pk = st.enter_context(nc.psum_tensor("pkq", [128, 8, 128], BF16))
        for blk in range(2):
            hs = blk % 2
            for ti in range(4):
                norm_block(K, P, T, K.x_own, blk * 4 + ti, G1, SH1, ident, hT[hs], ti)
            for ti in range(4):
                t = blk * 4 + ti
                hk = [("hT", ti, 0), ("hT", ti, 1)]
                for gi, (c0, n, wkeys) in enumerate([(0, 512, wk_q), (512, 512, wk_q), (1024, 512, wk_iq),
                                                     (1536, 512, wk_iq), (2048, 16, wk_iw)]):
                    pb = pm[gi % 3]
                    for k in range(16):
                        P.op("pe", lambda e, pb=pb, k=k, c0=c0, n=n, ti=ti, hs=hs: e.matmul(
                            pb[:, 0:n], lhsT=hT[hs][:, k, ti * 128:(ti + 1) * 128], rhs=W[:, k, c0:c0 + n],
                            start=(k == 0), stop=(k == 15)), reads=hk + wkeys, writes=[("pmq", gi % 3)])
                    if gi < 2:
                        P.op("act", lambda e, pb=pb, gi=gi: e.copy(out=qsb[:, gi * 4:(gi + 1) * 4, :], in_=pb[:, 0:512]),
                             reads=[("pmq", gi % 3)], writes=["qsb"])
                    elif gi < 4:
                        g2 = gi - 2
                        P.op("act", lambda e, pb=pb, g2=g2: e.copy(out=iqsb[:, g2 * 8:(g2 + 1) * 8, :], in_=pb[:, 0:512]),
                             reads=[("pmq", gi % 3)], writes=["iqsb"])
                    else:
                        P.op("act", lambda e, pb=pb: e.activation(out=iwsb[:], in_=pb[:, 0:16], func=AF.Copy, scale=0.25),
                             reads=[("pmq", gi % 3)], writes=["iwsb"])
                P.dma("sp", K.iw_d[t * 128:(t + 1) * 128, :], iwsb[:], reads=["iwsb"], writes=[("iw_d", t)])
                head_rmsnorm(P, qsb[:], gain, sq, ssum, ["qsb"], "Q")
                apply_rope(P, "dve", qsb[:], rt["cosa"], rt["sina"], t, 16, rtmp, ["qsb"], "rQ")
                P.op("act", lambda e: e.copy(out=qbf[:], in_=qsb[:]), reads=["qsb"], writes=["qbf"])
                for h in range(8):
                    P.op("pe", lambda e, h=h: e.transpose(out=pk[:, h, :], in_=qbf[:, h, :], identity=ident[:]),
                         reads=["qbf", "ident"], writes=["pkq"])
                P.op("dve", lambda e: e.tensor_copy(out=qTs[:], in_=pk[:]), reads=["pkq"], writes=["qTs"])
                P.dma("sp", K.qT_d.rearrange("h p t -> p h t")[:, :, t * 128:(t + 1) * 128], qTs[:],
                      reads=["qTs"], writes=[("qT_d", t)])
                apply_rope(P, "pool", iqsb[:], rt["cosi"], rt["sini"], t, 8, rtmp, ["iqsb"], "rIQ")
                P.op("act", lambda e: e.activation(out=iqbf[:], in_=iqsb[:], func=AF.Copy, scale=0.125),
                     reads=["iqsb"], writes=["iqbf"])
                for half in range(2):
                    for hh in range(8):
                        h = half * 8 + hh
                        P.op("pe", lambda e, h=h, hh=hh: e.transpose(out=pk[0:64, hh, :], in_=iqbf[:, h, :],
                                                                      identity=ident[:]),
                             reads=["iqbf", "ident"], writes=["pkq"])
                    P.op("dve", lambda e, half=half: e.tensor_copy(
                        out=iqTs[:, :, half * 8:(half + 1) * 8].rearrange("p t h -> p h t"), in_=pk[0:64, :, :]),
                         reads=["pkq"], writes=["iqTs"])
                P.dma("sp", K.iqT_d[:, t * 128:(t + 1) * 128, :], iqTs[:], reads=["iqTs"], writes=[("iqT_d", t)])
        P.flush()


NIT = 24
SLOT_NK = [4, 8, 12, 16, 20, 24, 28, 32]


def phase3_attention(K):
    nc, P = K.nc, K.P
    with contextlib.ExitStack() as st:
        def sb(name, shape, dt):
            return st.enter_context(nc.sbuf_tensor(name, shape, dt))
        ident = sb("ident3", [128, 128], BF16)
        identf = sb("identf3", [128, 128], F32)
        make_ident(K, P, ident)
        P.op("dve", lambda e: e.tensor_copy(out=identf[:], in_=ident[:]), reads=["ident"], writes=["identf"])
        kT = sb("kTall", [128, 8, S], BF16)
        V = sb("Vall", [128, 32, 1032], BF16)
        ikT = sb("ikTall", [64, S], BF16)
        for h in range(8):
            P.dma("sp", kT[:, h, :], K.kT_d[h], writes=[("kT", h)])
        for q4 in range(4):
            P.dma("sp", V[:, q4 * 8:(q4 + 1) * 8, :],
                  K.v_d.rearrange("(t p) c -> p t c", p=128)[:, q4 * 8:(q4 + 1) * 8, :], writes=[("V", q4)])
        P.dma("sp", ikT[:], K.ikT_d, writes=["ikT"])
        kTk = [("kT", h) for h in range(8)]
        Vk = [("V", q4) for q4 in range(4)]
        Sel = sb("Sel", [128, 16, 128], BF16)
        pidx = sb("pidx", [128, 1], I32)
        pidf = sb("pidf", [128, 1], F32)
        score = sb("score", [128, S], F32)
        self_ = score[:, 0:2048].rearrange("p (g t) -> p g t", g=16)
        sk4 = [("score", q) for q in range(4)]
        P.op("pool", lambda e: e.iota(self_, pattern=[[-8, 16], [1, 128]], base=0, channel_multiplier=0, allow_small_or_imprecise_dtypes=True), writes=sk4)
        P.op("pool", lambda e: e.iota(pidx[:], pattern=[[0, 1]], base=0, channel_multiplier=1), writes=["pidx"])
        P.op("dve", lambda e: e.tensor_scalar(out=pidx[:], in0=pidx[:], scalar1=4, scalar2=None,
                                               op0=ALU.arith_shift_right), reads=["pidx"], writes=["pidx"])
        P.op("dve", lambda e: e.tensor_copy(out=pidf[:], in_=pidx[:]), reads=["pidx"], writes=["pidf"])
        P.op("dve", lambda e: e.tensor_scalar(out=Sel[:], in0=self_, scalar1=pidf[:, 0:1], scalar2=None,
                                               op0=ALU.is_equal), reads=sk4 + ["pidf"], writes=["Sel"])
        kposi = sb("kposi", [128, 512], I32)
        kposf = sb("kposf", [128, 512], F32)
        qpos = sb("qpos", [128, 8], F32)
        P.dma("sp", qpos[:], K.qpos_own, writes=["qpos"])
        iwg = sb("iwg", [128, 128], F32)
        wcol = sb("wcol", [128, 128], F32)
        P.dma("sp", iwg[:], K.iw_d.rearrange("(g t) h -> g (t h)", t=8), writes=["iwg"])
        A = [st.enter_context(nc.psum_tensor("A%d" % i, [128, 512], F32)) for i in range(2)]
        B = [st.enter_context(nc.psum_tensor("B%d" % i, [128, 512], F32)) for i in range(2)]
        C = st.enter_context(nc.psum_tensor("C3", [128, 8, 128], BF16))
        P.op("pe", lambda e: e.transpose(out=A[0][:, 0:128], in_=iwg[:], identity=identf[:]),
             reads=["iwg", "identf"], writes=[("A", 0)])
        P.op("dve", lambda e: e.tensor_copy(out=wcol[:], in_=A[0][:, 0:128]), reads=[("A", 0)], writes=["wcol"])
        mask01 = sb("mask01", [128, S], BF16)
        maskT = sb("maskT", [128, 32, 128], BF16)
        R = [sb("R%d" % i, [128, 512], BF16) for i in range(2)]
        pexp = [sb("pexp%d" % i, [128, 512], BF16) for i in range(2)]
        pmk = [sb("pmk%d" % i, [128, 512], BF16) for i in range(2)]
        iqTs = sb("iqTs3", [64, 128, 16], BF16)
        qTs = sb("qTs3", [128, 8, 128], BF16)
        att = sb("att", [128, 8, 128], BF16)
        attTs = sb("attTs", [128, 8, 128], BF16)
        bias = sb("cbias", [128, 512], F32)
        c2 = sb("c2", [128, NIT], F32)
        steps = sb("steps", [128, NIT], F32)
        sm = sb("sm3", [128, 8], F32)
        for k in range(NIT):
            P.op("pool", lambda e, k=k: e.memset(c2[:, k:k + 1], float(2.0 ** -(k + 1))), writes=["c2"])
        for i in range(8):
            nk = SLOT_NK[i]
            nb = nk // 4
            L = nk * 128
            P.dma("sp", iqTs[:], K.iqT_d[:, i * 128:(i + 1) * 128, :], writes=["iqTs"])
            P.dma("sp", qTs[:], K.qT_d.rearrange("h p t -> p h t")[:, :, i * 128:(i + 1) * 128], writes=["qTs"])
            isteps = [(sbk, g) for sbk in range(nb) for g in range(16)]

            def dots(si):
                sbk, g = isteps[si]
                a = si % 2
                lhsT = iqTs[:, g * 8:(g + 1) * 8, :].rearrange("p t h -> p (t h)")
                P.op("pe", lambda e, a=a, lhsT=lhsT, sbk=sbk: e.matmul(
                    A[a][:, :], lhsT=lhsT, rhs=ikT[:, sbk * 512:(sbk + 1) * 512], start=True, stop=True),
                    reads=["iqTs", "ikT"], writes=[("A", a)])
            dots(0)
            for si, (sbk, g) in enumerate(isteps):
                a = si % 2
                bsl = sbk % 2
                if si + 1 < len(isteps):
                    dots(si + 1)
                G = i * 16 + g
                P.op("dve", lambda e, a=a, G=G: e.tensor_scalar(
                    out=R[a][:], in0=A[a][:, :], scalar1=0.0, scalar2=wcol[:, G:G + 1],
                    op0=ALU.max, op1=ALU.mult), reads=[("A", a), "wcol"], writes=[("R", a)])
                P.op("pe", lambda e, a=a, g=g, bsl=bsl: e.matmul(
                    B[bsl][:, :], lhsT=Sel[:, g, :], rhs=R[a][:], start=(g == 0), stop=(g == 15)),
                    reads=[("R", a), "Sel"], writes=[("B", bsl)])
                if g == 15:
                    P.op("act", lambda e, bsl=bsl, sbk=sbk: e.copy(out=score[:, sbk * 512:(sbk + 1) * 512], in_=B[bsl][:, :]),
                         reads=[("B", bsl)], writes=[("score", sbk)])
            sck = [("score", sbk) for sbk in range(nb)]
            P.op("dve", lambda e, L=L: e.tensor_reduce(out=sm[:, 0:1], in_=score[:, 0:L], axis=AX.X, op=ALU.max,
                                                        apply_absolute_value=True), reads=sck, writes=["sm0"])
            P.op("pool", lambda e, nb=nb: e.iota(kposi[:], pattern=[[1, 512]], base=(nb - 1) * 512, channel_multiplier=0),
                 writes=["kposi"])
            P.op("dve", lambda e: e.tensor_copy(out=kposf[:], in_=kposi[:]), reads=["kposi"], writes=["kposf"])
            P.op("dve", lambda e, i=i: e.tensor_scalar(out=bias[:], in0=kposf[:], scalar1=qpos[:, i:i + 1],
                                                        scalar2=-1e30, op0=ALU.is_gt, op1=ALU.mult),
                 reads=["kposf", "qpos"], writes=["bias"])
            P.op("dve", lambda e, nb=nb: e.tensor_tensor(out=score[:, (nb - 1) * 512:nb * 512],
                                                          in0=score[:, (nb - 1) * 512:nb * 512], in1=bias[:], op=ALU.add),
                 reads=["bias", ("score", nb - 1), "sm0"], writes=[("score", nb - 1)])
            P.op("dve", lambda e: e.tensor_scalar(out=sm[:, 1:2], in0=sm[:, 0:1], scalar1=-1.0, scalar2=-1.0,
                                                   op0=ALU.mult, op1=ALU.add), reads=["sm0"], writes=["lo"])
            P.op("dve", lambda e: e.tensor_scalar(out=sm[:, 5:6], in0=sm[:, 0:1], scalar1=2.0, scalar2=2.0,
                                                   op0=ALU.mult, op1=ALU.add), reads=["sm0"], writes=["d0"])
            P.op("dve", lambda e: e.tensor_scalar(out=steps[:], in0=c2[:], scalar1=sm[:, 5:6], scalar2=None,
                                                   op0=ALU.mult), reads=["d0", "c2"], writes=["steps"])
            for k in range(NIT):
                P.op("dve", lambda e, k=k: e.tensor_tensor(out=sm[:, 2:3], in0=sm[:, 1:2], in1=steps[:, k:k + 1],
                                                            op=ALU.add), reads=["lo", "steps"], writes=["mid"])
                P.op("dve", lambda e, L=L: e.tensor_scalar(out=mask01[:, 0:L], in0=score[:, 0:L], scalar1=sm[:, 2:3],
                                                            scalar2=None, op0=ALU.is_ge, op1=ALU.add,
                                                            accum_out=sm[:, 3:4]),
                     reads=sck + ["mid"], writes=["mask01", "cnt"])
                P.op("dve", lambda e, k=k: e.scalar_tensor_tensor(out=sm[:, 4:5], in0=sm[:, 3:4], scalar=255.5,
                                                                   in1=steps[:, k:k + 1], op0=ALU.is_ge, op1=ALU.mult),
                     reads=["cnt", "steps"], writes=["inc"])
                P.op("dve", lambda e: e.tensor_tensor(out=sm[:, 1:2], in0=sm[:, 1:2], in1=sm[:, 4:5], op=ALU.add),
                     reads=["lo", "inc"], writes=["lo"])
            P.op("dve", lambda e, L=L: e.tensor_scalar(out=mask01[:, 0:L], in0=score[:, 0:L], scalar1=sm[:, 1:2],
                                                        scalar2=None, op0=ALU.is_ge), reads=sck + ["lo"], writes=["mask01"])
            for kt in range(nk):
                P.op("pe", lambda e, kt=kt: e.transpose(out=C[:, kt % 8, :], in_=mask01[:, kt * 128:(kt + 1) * 128],
                                                         identity=ident[:]), reads=["mask01", "ident"], writes=["C"])
                if kt % 8 == 7 or kt == nk - 1:
                    k0 = (kt // 8) * 8
                    n8 = kt - k0 + 1
                    P.op("act", lambda e, k0=k0, n8=n8: e.copy(out=maskT[:, k0:k0 + n8, :], in_=C[:, 0:n8, :]),
                         reads=["C"], writes=[("maskT", k0 // 8)])
            mk = [("maskT", q) for q in range((nk + 7) // 8)]
            asteps = [(h, kg) for h in range(8) for kg in range(nb)]

            def qk(si):
                h, kg = asteps[si]
                a = si % 2
                for j4 in range(4):
                    kt = kg * 4 + j4
                    P.op("pe", lambda e, a=a, j4=j4, kt=kt, h=h: e.matmul(
                        A[a][:, j4 * 128:(j4 + 1) * 128], lhsT=kT[:, h, kt * 128:(kt + 1) * 128], rhs=qTs[:, h, :],
                        start=True, stop=True), reads=kTk + ["qTs"], writes=[("A", a)])
            qk(0)
            for si, (h, kg) in enumerate(asteps):
                a = si % 2
                bsl = h % 2
                if si + 1 < len(asteps):
                    qk(si + 1)
                P.op("act", lambda e, a=a: e.activation(out=pexp[a][:], in_=A[a][:, :], func=AF.Exp,
                                                         scale=float(128 ** -0.5)),
                     reads=[("A", a)], writes=[("pexp", a)])
                P.op("dve", lambda e, a=a, kg=kg: e.tensor_tensor(
                    out=pmk[a][:], in0=pexp[a][:], in1=maskT[:, kg * 4:(kg + 1) * 4, :].rearrange("p a t -> p (a t)"),
                    op=ALU.mult), reads=[("pexp", a)] + mk, writes=[("pmk", a)])
                for j4 in range(4):
                    kt = kg * 4 + j4
                    P.op("pe", lambda e, a=a, j4=j4, kt=kt, h=h, bsl=bsl, kg=kg, nb=nb: e.matmul(
                        B[bsl][:, 0:129], lhsT=pmk[a][:, j4 * 128:(j4 + 1) * 128], rhs=V[:, kt, h * 129:(h + 1) * 129],
                        start=(kg == 0 and j4 == 0), stop=(kg == nb - 1 and j4 == 3)),
                        reads=[("pmk", a)] + Vk, writes=[("B", bsl)])
                if kg == nb - 1:
                    P.op("dve", lambda e, bsl=bsl: e.reciprocal(out=sm[:, 6:7], in_=B[bsl][:, 128:129]),
                         reads=[("B", bsl)], writes=["rcp"])
                    P.op("dve", lambda e, bsl=bsl, h=h: e.tensor_scalar(out=att[:, h, :], in0=B[bsl][:, 0:128],
                                                                         scalar1=sm[:, 6:7], scalar2=None, op0=ALU.mult),
                         reads=[("B", bsl), "rcp"], writes=["att"])
            for h in range(8):
                P.op("pe", lambda e, h=h: e.transpose(out=C[:, h, :], in_=att[:, h, :], identity=ident[:]),
                     reads=["att", "ident"], writes=["C"])
            P.op("act", lambda e: e.copy(out=attTs[:], in_=C[:]), reads=["C"], writes=["attTs"])
            P.dma("sp", K.attT_d.rearrange("h p t -> p h t")[:, :, i * 128:(i + 1) * 128], attTs[:],
                  reads=["attTs"], writes=[("attT_d", i)])
        P.flush()

RD = BF16
NCH = 64


def tok_shift(P, dst, raw, tmp, mu_ap, rk_raw, k_tmp, k_dst, n=128):
    P.op("pool", lambda e: e.tensor_tensor(out=tmp[0:n, 1:S], in0=raw[0:n, 0:S - 1], in1=raw[0:n, 1:S], op=ALU.subtract),
         reads=[rk_raw], writes=[k_tmp])
    P.op("pool", lambda e: e.tensor_scalar(out=tmp[0:n, 0:1], in0=raw[0:n, 0:1], scalar1=-1.0, scalar2=0.0,
                                            op0=ALU.mult, op1=ALU.add), reads=[rk_raw, k_tmp], writes=[k_tmp])
    P.op("dve", lambda e: e.scalar_tensor_tensor(out=dst[0:n, :], in0=tmp[0:n, :], scalar=mu_ap, in1=raw[0:n, :],
                                                  op0=ALU.mult, op1=ALU.add), reads=[rk_raw, k_tmp], writes=[k_dst])


def phase4b_rwkv_prep(K, cts=range(2)):
    nc, P = K.nc, K.P
    with contextlib.ExitStack() as st:
        def sb(name, shape, dt):
            return st.enter_context(nc.sbuf_tensor(name, shape, dt))
        txw = sb("txw", [96, S], BF16)
        xap = sb("xap", [96, S], BF16)
        sxg = sb("sxg", [128, 2, S], BF16)
        M01 = sb("M01", [128, S], BF16)
        wup = sb("wup", [96, 256], BF16)
        aup = sb("aup", [96, 256], BF16)
        gup = sb("gup", [128, 2, 256], BF16)
        wst = sb("wst4", [128, 2, 256], F32)
        bones = sb("bones", [128, 128], BF16)
        prm = sb("prm", [128, 12, 2], F32)
        mul = sb("mul", [128, 4], F32)
        PT = sb("PT", [128, S], F32)
        KK = sb("KK", [128, S], F32)
        KP = sb("KP", [128, S], F32)
        CL = sb("CL", [128, S], F32)
        RP = sb("RP", [128, S], BF16)
        VP = sb("VP", [128, S], BF16)
        AA = sb("AA", [128, S], BF16)
        K2 = sb("K2", [128, S], BF16)
        SQb = sb("SQb", [128, S], BF16)
        OUT = [sb("OUT%d" % i, [128, S], BF16) for i in range(2)]
        PCt = sb("PCt", [128, NCH], F32)
        ps = [st.enter_context(nc.psum_tensor("ps4_%d" % i, [128, 512], F32)) for i in range(4)]
        for i, ap in enumerate(K.rw_prm):
            P.dma("sp", prm[:, i, :], ap, writes=[("prm", i)])
        prk = [("prm", i) for i in range(10)]
        P.op("dve", lambda e: e.tensor_scalar(out=prm[:, 10, :], in0=prm[:, 6, :], scalar1=-1.0, scalar2=1.0,
                                               op0=ALU.mult, op1=ALU.add), reads=prk, writes=[("prm", 10)])
        prk = prk + [("prm", 10)]
        P.dma("sp", mul[:], K.rw_mul, writes=["mul"])
        P.op("pool", lambda e: e.memset(bones[:], 0.0), writes=["bones"])
        P.op("pool", lambda e: e.memset(bones[0:64, 0:64], 1.0), reads=["bones"], writes=["bones"])
        P.op("pool", lambda e: e.memset(bones[64:128, 64:128], 1.0), reads=["bones"], writes=["bones"])
        P.op("pool", lambda e: e.iota(PT[:].rearrange("p (c t) -> p c t", t=64), pattern=[[0, NCH], [1, 64]], base=0,
                                      channel_multiplier=0, allow_small_or_imprecise_dtypes=True), writes=["PT"])
        P.op("dve", lambda e: e.tensor_scalar(out=M01[:], in0=PT[:], scalar1=0.5, scalar2=None, op0=ALU.is_gt),
             reads=["PT"], writes=["M01"])
        P.dma("sp", wst[0:96, 0, :], K.rw_w_up, writes=["wst"])
        P.op("act", lambda e: e.copy(out=wup[:], in_=wst[0:96, 0, :]), reads=["wst"], writes=["wup"])
        P.dma("sp", wst[0:96, 1, :], K.rw_a_up, reads=[], writes=["wst1"])
        P.op("act", lambda e: e.copy(out=aup[:], in_=wst[0:96, 1, :]), reads=["wst1"], writes=["aup"])
        P.dma("sp", wst[:, :, :], K.rw_g_up.rearrange("(c p) n -> p c n", p=128), reads=[], writes=["wst", "wst1"])
        P.op("act", lambda e: e.copy(out=gup[:], in_=wst[:]), reads=["wst", "wst1"], writes=["gup"])
        for (r0, n, mcol, func, dst, kd) in ((768, 96, 0, AF.Tanh, txw[:, :], "txw"), (864, 96, 1, AF.Copy, xap[:, :], "xap"),
                                             (960, 128, 2, AF.Sigmoid, sxg[:, 0, :], "sxg0"),
                                             (1088, 128, 3, AF.Sigmoid, sxg[:, 1, :], "sxg1")):
            P.dma("sp", PT[0:n, :], K.yT_d[r0:r0 + n, :], writes=["PT"])
            tok_shift(P, KP, PT, KK, mul[0:n, mcol:mcol + 1], "PT", "KK", "KP", n=n)
            P.op("act", lambda e, n=n, func=func, dst=dst: e.activation(out=dst, in_=KP[0:n, :], func=func),
                 reads=["KP"], writes=[kd])
        lk = ["txw", "xap", "sxg0", "sxg1"]
        oc = 0
        for ct in cts:
            c0 = ct * 128
            P.dma("sp", PT[:], K.yT_d[c0:c0 + 128, :], writes=["PT"])
            tok_shift(P, RP, PT, KK, prm[:, 0, ct:ct + 1], "PT", "KK", "RP")
            P.dma("sp", PT[:], K.yT_d[256 + c0:256 + c0 + 128, :], writes=["PT"])
            tok_shift(P, KP, PT, KK, prm[:, 1, ct:ct + 1], "PT", "KK", "KP")
            P.dma("sp", PT[:], K.yT_d[512 + c0:512 + c0 + 128, :], writes=["PT"])
            tok_shift(P, VP, PT, KK, prm[:, 2, ct:ct + 1], "PT", "KK", "VP")
            P.dma("sp", K.vb_d[c0:c0 + 128, :], VP[:], reads=["VP"], writes=[("vb_d", ct)])
            for blk in range(8):
                bs = slice(blk * 512, (blk + 1) * 512)
                p0, p1, p2 = ps[0], ps[1], ps[2]
                P.op("pe", lambda e, bs=bs, c0=c0: e.matmul(ps[0][:, :], lhsT=wup[:, c0:c0 + 128], rhs=txw[:, bs],
                                                             start=True, stop=True), reads=["wup", "txw"], writes=[("ps4", 0)])
                P.op("act", lambda e, bs=bs, ct=ct: e.activation(out=CL[:, bs], in_=ps[0][:, :], func=AF.Sigmoid,
                                                                  bias=prm[:, 3, ct:ct + 1]),
                     reads=[("ps4", 0)] + prk, writes=["CL"])
                P.op("pe", lambda e, bs=bs, c0=c0: e.matmul(ps[1][:, :], lhsT=aup[:, c0:c0 + 128], rhs=xap[:, bs],
                                                             start=True, stop=True), reads=["aup", "xap"], writes=[("ps4", 1)])
                P.op("act", lambda e, bs=bs, ct=ct: e.activation(out=AA[:, bs], in_=ps[1][:, :], func=AF.Sigmoid,
                                                                  bias=prm[:, 4, ct:ct + 1]),
                     reads=[("ps4", 1)] + prk, writes=["AA"])
                for cc in range(2):
                    P.op("pe", lambda e, bs=bs, c0=c0, cc=cc: e.matmul(ps[2][:, :], lhsT=gup[:, cc, c0:c0 + 128],
                                                                       rhs=sxg[:, cc, bs], start=(cc == 0), stop=(cc == 1)),
                         reads=["gup", "sxg0", "sxg1"], writes=[("ps4", 2)])
                o = OUT[oc % 2]
                P.op("dve", lambda e, bs=bs, o=o: e.tensor_copy(out=o[:, bs], in_=ps[2][:, :]),
                     reads=[("ps4", 2)], writes=[("OUT", oc % 2)])
            P.dma("sp", K.G_d[c0:c0 + 128, :], OUT[oc % 2][:], reads=[("OUT", oc % 2)], writes=[("G_d", ct)])
            oc += 1
            P.op("dve", lambda e: e.tensor_scalar(out=CL[:], in0=CL[:], scalar1=-0.6065306597126334, scalar2=None,
                                                   op0=ALU.mult), reads=["CL"], writes=["CL"])
            P.op("dve", lambda e, ct=ct: e.tensor_scalar(out=KK[:], in0=KP[:], scalar1=prm[:, 5, ct:ct + 1], scalar2=None,
                                                          op0=ALU.mult), reads=["KP"] + prk, writes=["KK"])
            P.op("act", lambda e: e.activation(out=SQb[:], in_=KK[:], func=AF.Square), reads=["KK"], writes=["SQb"])
            for blk in range(8):
                bs = slice(blk * 512, (blk + 1) * 512)
                P.op("pe", lambda e, bs=bs: e.matmul(ps[3][:, :], lhsT=bones[:], rhs=SQb[:, bs], start=True, stop=True),
                     reads=["bones", "SQb"], writes=[("ps4", 3)])
                P.op("act", lambda e, bs=bs: e.activation(out=PT[:, bs], in_=ps[3][:, :], func=AF.Sqrt),
                     reads=[("ps4", 3)], writes=["PT"])
            P.op("dve", lambda e: e.tensor_scalar(out=PT[:], in0=PT[:], scalar1=1e-12, scalar2=None, op0=ALU.max),
                 reads=["PT"], writes=["PT"])
            P.op("dve", lambda e: e.reciprocal(out=PT[:], in_=PT[:]), reads=["PT"], writes=["PT"])
            P.op("dve", lambda e: e.tensor_tensor(out=KK[:], in0=KK[:], in1=PT[:], op=ALU.mult), reads=["KK", "PT"], writes=["KK"])
            P.op("dve", lambda e, ct=ct: e.tensor_scalar(out=PT[:], in0=AA[:], scalar1=prm[:, 6, ct:ct + 1],
                                                          scalar2=prm[:, 10, ct:ct + 1], op0=ALU.mult, op1=ALU.add),
                 reads=["AA", "PT"] + prk, writes=["PT"])
            P.op("dve", lambda e: e.tensor_tensor(out=K2[:], in0=KP[:], in1=PT[:], op=ALU.mult), reads=["KP", "PT"], writes=["K2"])
            P.op("dve", lambda e, ct=ct: e.scalar_tensor_tensor(out=SQb[:], in0=RP[:], scalar=prm[:, 7, ct:ct + 1], in1=K2[:],
                                                                 op0=ALU.mult, op1=ALU.mult),
                 reads=["RP", "K2", "SQb"] + prk, writes=["SQb"])
            o = OUT[oc % 2]
            for blk in range(8):
                bs = slice(blk * 512, (blk + 1) * 512)
                P.op("pe", lambda e, bs=bs: e.matmul(ps[3][:, :], lhsT=bones[:], rhs=SQb[:, bs], start=True, stop=True),
                     reads=["bones", "SQb"], writes=[("ps4", 3)])
                P.op("dve", lambda e, bs=bs, o=o: e.tensor_tensor(out=o[:, bs], in0=ps[3][:, :], in1=VP[:, bs], op=ALU.mult),
                     reads=[("ps4", 3), "VP"], writes=[("OUT", oc % 2)])
            P.dma("sp", K.BON_d[c0:c0 + 128, :], o[:], reads=[("OUT", oc % 2)], writes=[("BON_d", ct)])
            oc += 1
            P.op("dve", lambda e: e.tensor_tensor_scan(out=PT[:], data0=M01[:], data1=CL[:], initial=0.0,
                                                        op0=ALU.mult, op1=ALU.add), reads=["M01", "CL", "PT"], writes=["PT"])
            P.op("pool", lambda e: e.tensor_tensor(out=CL[:], in0=PT[:], in1=CL[:], op=ALU.subtract),
                 reads=["PT", "CL"], writes=["CL"])
            P.op("act", lambda e: e.activation(out=CL[:], in_=CL[:], func=AF.Exp), reads=["CL"], writes=["CL"])
            v3 = lambda t: t[:].rearrange("p (c t) -> p c t", t=64)
            o = OUT[oc % 2]
            P.op("dve", lambda e, o=o: e.scalar_tensor_tensor(out=o[:], in0=KK[:], scalar=-1.0, in1=CL[:],
                                                               op0=ALU.mult, op1=ALU.mult),
                 reads=["KK", "CL"], writes=[("OUT", oc % 2)])
            P.dma("sp", K.AH_d[c0:c0 + 128, :], o[:], reads=[("OUT", oc % 2)], writes=[("AH_d", ct)])
            oc += 1
            P.op("act", lambda e: e.activation(out=CL[:], in_=PT[:], func=AF.Exp), reads=["PT", "CL"], writes=["CL"])
            o = OUT[oc % 2]
            P.op("dve", lambda e, o=o: e.tensor_tensor(out=o[:], in0=RP[:], in1=CL[:], op=ALU.mult),
                 reads=["RP", "CL"], writes=[("OUT", oc % 2)])
            P.dma("sp", K.RH_d[c0:c0 + 128, :], o[:], reads=[("OUT", oc % 2)], writes=[("RH_d", ct)])
            oc += 1
            P.op("pool", lambda e: e.tensor_copy(out=PCt[:], in_=v3(CL)[:, :, 63]), reads=["CL"], writes=["PCt"])
            P.dma("sp", K.PC_d[c0:c0 + 128, :], PCt[:], reads=["PCt"], writes=[("PC_d", ct)])
            P.op("act", lambda e: e.activation(out=PT[:], in_=PT[:], func=AF.Exp, scale=-1.0), reads=["PT"], writes=["PT"])
            o = OUT[oc % 2]
            P.op("dve", lambda e, o=o: e.tensor_tensor(out=o[:], in0=K2[:], in1=PT[:], op=ALU.mult),
                 reads=["K2", "PT"], writes=[("OUT", oc % 2)])
            P.dma("sp", K.KH_d[c0:c0 + 128, :], o[:], reads=[("OUT", oc % 2)], writes=[("KH_d", ct)])
            oc += 1
            P.op("dve", lambda e: e.tensor_tensor(out=KK[:], in0=KK[:], in1=AA[:], op=ALU.mult), reads=["KK", "AA"], writes=["KK"])
            o = OUT[oc % 2]
            P.op("dve", lambda e, o=o: e.tensor_tensor(out=o[:], in0=KK[:], in1=PT[:], op=ALU.mult),
                 reads=["KK", "PT"], writes=[("OUT", oc % 2)])
            P.dma("sp", K.BH_d[c0:c0 + 128, :], o[:], reads=[("OUT", oc % 2)], writes=[("BH_d", ct)])
            oc += 1
        P.flush()

def phase4c_rwkv_scan(K, heads=range(4)):
    nc, P = K.nc, K.P
    with contextlib.ExitStack() as st:
        def sb(name, shape, dt):
            return st.enter_context(nc.sbuf_tensor(name, shape, dt))
        ident = sb("ident4", [128, 128], BF16)
        make_ident(K, P, ident)
        MaskG = sb("MaskG", [64, 4, 128], F32)
        MaskX = sb("MaskX", [64, 8, 64], F32)
        I8 = sb("I8", [64, 64], F32)
        ones = sb("ones4", [64, 64], F32)
        P.op("pool", lambda e: e.memset(ones[:], 1.0), writes=["ones"])
        for a in range(4):
            for cq in range(2):
                P.op("pool", lambda e, cq=cq, a=a: e.affine_select(
                    out=MaskG[:, a, cq * 64:(cq + 1) * 64], in_=ones[:], pattern=[[1, 64]],
                    compare_op=(ALU.is_gt if cq == 0 else ALU.is_ge), fill=0.0, base=0, channel_multiplier=-1),
                    reads=["ones"], writes=["MaskG"])
        for a in range(8):
            P.op("pool", lambda e, a=a: e.affine_select(out=MaskX[:, a, :], in_=ones[:], pattern=[[-1, 64]],
                                                         compare_op=ALU.is_gt, fill=0.0, base=0, channel_multiplier=1),
                 reads=["ones"], writes=["MaskX"])
        P.op("dve", lambda e: e.tensor_copy(out=I8[:], in_=ident[0:64, 0:64]), reads=["ident"], writes=["I8"])
        AH = sb("AH", [64, S], RD)
        RH = sb("RH", [64, S], RD)
        BH = sb("BH", [64, S], RD)
        KH = sb("KH", [64, S], RD)
        vb = sb("vb", [64, S], BF16)
        PC = sb("PC", [64, NCH], F32)
        ARh = sb("ARh", [64, NCH, 128], RD)
        BKh = sb("BKh", [64, NCH, 128], RD)
        GmB = sb("GmB", [64, NCH, 128], RD)
        GmK = sb("GmK", [64, NCH, 128], RD)
        Btok = sb("Btok", [64, NCH, 64], RD)
        Ktok = sb("Ktok", [64, NCH, 64], RD)
        Vtok = sb("Vtok", [64, NCH, 64], RD)
        X0 = sb("X0", [64, NCH, 64], RD)
        Pm = sb("Pm", [64, NCH, 64], RD)
        oT = sb("oT", [64, S], F32)
        Ast = sb("Ast", [64, 64], F32)
        Abf = sb("Abf", [64, 64], RD)
        Tt = sb("Tt", [64, 64], F32)
        Xs = sb("Xs", [64, 64], RD)
        Us = sb("Us", [64, 64], RD)
        PSb = st.enter_context(nc.psum_tensor("PSb", [128, 1024], BF16))
        PS = [st.enter_context(nc.psum_tensor("PS%d" % i, [128, 512], F32)) for i in range(7)]
        v3 = lambda t: t[:].rearrange("p (c t) -> p c t", t=64)
        Nb = [v3(AH), v3(RH)]
        Xb = [v3(BH), v3(KH)]
        Nk = ["AH", "RH"]
        Xk = ["BH", "KH"]
        for hd in heads:
            r0 = hd * 64
            P.dma("sp", AH[:], K.AH_d[r0:r0 + 64, :], writes=["AH"])
            P.dma("sp", RH[:], K.RH_d[r0:r0 + 64, :], writes=["RH"])
            P.dma("sp", BH[:], K.BH_d[r0:r0 + 64, :], writes=["BH"])
            P.dma("sp", KH[:], K.KH_d[r0:r0 + 64, :], writes=["KH"])
            P.dma("sp", vb[:], K.vb_d[r0:r0 + 64, :], writes=["vb"])
            P.dma("sp", PC[:], K.PC_d[r0:r0 + 64, :], writes=["PC"])
            P.op("dve", lambda e: e.tensor_copy(out=ARh[:, :, 0:64], in_=v3(AH)), reads=["AH"], writes=["ARh"])
            P.op("pool", lambda e: e.tensor_copy(out=ARh[:, :, 64:128], in_=v3(RH)), reads=["RH"], writes=["ARh"])
            P.op("dve", lambda e: e.tensor_copy(out=BKh[:, :, 0:64], in_=v3(BH)), reads=["BH"], writes=["BKh"])
            P.op("pool", lambda e: e.tensor_copy(out=BKh[:, :, 64:128], in_=v3(KH)), reads=["KH"], writes=["BKh"])
            for (src, srck, col0, dst, dk) in ((BKh, "BKh", 0, Btok, "Btok"), (BKh, "BKh", 64, Ktok, "Ktok"), (None, "vb", 0, Vtok, "Vtok")):
                for c16 in range(0, NCH, 16):
                    for cc in range(16):
                        c = c16 + cc
                        in_ = vb[:, c * 64:(c + 1) * 64] if src is None else src[:, c, col0:col0 + 64]
                        P.op("pe", lambda e, cc=cc, in_=in_: e.transpose(out=PSb[0:64, cc * 64:(cc + 1) * 64], in_=in_,
                                                                         identity=ident[0:64, 0:64]),
                             reads=[srck, "ident"], writes=["PSb"])
                    P.op("act", lambda e, c16=c16, dst=dst: e.copy(out=dst[:, c16:c16 + 16, :].rearrange("p c k -> p (c k)"),
                                                                    in_=PSb[0:64, :]), reads=["PSb"], writes=[dk])
            gi = 0
            for (col0, dst, dk) in ((0, GmB, "GmB"), (64, GmK, "GmK")):
                for c4 in range(0, NCH, 4):
                    b = gi % 2
                    gi += 1
                    for cc in range(4):
                        c = c4 + cc
                        P.op("pe", lambda e, c=c, cc=cc, b=b, col0=col0: e.matmul(
                            PS[b][0:64, cc * 128:(cc + 1) * 128], lhsT=BKh[:, c, col0:col0 + 64], rhs=ARh[:, c, :],
                            start=True, stop=True), reads=["BKh", "ARh"], writes=[("PS", b)])
                    P.op("dve", lambda e, c4=c4, dst=dst, b=b: e.tensor_tensor(
                        out=dst[:, c4:c4 + 4, :], in0=PS[b][0:64, :].rearrange("p (a t) -> p a t", t=128), in1=MaskG[:],
                        op=ALU.mult), reads=[("PS", b), "MaskG"], writes=[dk])
            for c8 in range(0, NCH, 8):
                for cc in range(8):
                    c = c8 + cc
                    P.op("pe", lambda e, c=c, cc=cc: e.matmul(PS[2][0:64, cc * 64:(cc + 1) * 64], lhsT=ARh[:, c, 0:64],
                                                               rhs=BKh[:, c, 0:64], start=True, stop=True),
                         reads=["ARh", "BKh"], writes=[("PS", 2)])
                P.op("dve", lambda e, c8=c8: e.tensor_tensor(
                    out=X0[:, c8:c8 + 8, :], in0=PS[2][0:64, :].rearrange("p (a t) -> p a t", t=64), in1=MaskX[:],
                    op=ALU.mult), reads=[("PS", 2), "MaskX"], writes=["X0"])
            N0 = GmB[:, :, 0:64]
            P.op("pool", lambda e, N0=N0: e.tensor_tensor(out=Pm[:], in0=N0, in1=I8[:].unsqueeze(1).to_broadcast([64, NCH, 64]),
                                                          op=ALU.add), reads=["GmB", "I8"], writes=["Pm"])
            curN, curNk = N0, "GmB"
            curX, curXk = X0[:], "X0"
            for lvl in range(1, 6):
                nX, nXk = Xb[lvl % 2], Xk[lvl % 2]
                nN, nNk = Nb[lvl % 2], Nk[lvl % 2]
                for c8 in range(0, NCH, 8):
                    for cc in range(8):
                        c = c8 + cc
                        P.op("pe", lambda e, c=c, cc=cc, curN=curN, curX=curX: e.matmul(
                            PS[3][0:64, cc * 64:(cc + 1) * 64], lhsT=curN[:, c, :], rhs=curX[:, c, :], start=True, stop=True),
                            reads=[curNk, curXk], writes=[("PS", 3)])
                    P.op("act", lambda e, c8=c8, nX=nX: e.copy(out=nX[:, c8:c8 + 8, :],
                                                                in_=PS[3][0:64, :].rearrange("p (a t) -> p a t", t=64)),
                         reads=[("PS", 3)], writes=[nXk])
                    if lvl < 5:
                        for cc in range(8):
                            c = c8 + cc
                            P.op("pe", lambda e, c=c, cc=cc, curN=curN, curX=curX: e.matmul(
                                PS[4][0:64, cc * 64:(cc + 1) * 64], lhsT=curX[:, c, :], rhs=curN[:, c, :], start=True, stop=True),
                                reads=[curNk, curXk], writes=[("PS", 4)])
                        P.op("dve", lambda e, c8=c8, nN=nN: e.tensor_copy(out=nN[:, c8:c8 + 8, :],
                                                                           in_=PS[4][0:64, :].rearrange("p (a t) -> p a t", t=64)),
                             reads=[("PS", 4)], writes=[nNk])
                    for cc in range(8):
                        c = c8 + cc
                        P.op("pe", lambda e, c=c, cc=cc, nX=nX: e.matmul(
                            PS[5][0:64, cc * 64:(cc + 1) * 64], lhsT=nX[:, c, :], rhs=Pm[:, c, :], start=True, stop=True),
                            reads=[nXk, "Pm"], writes=[("PS", 5)])
                    P.op("dve", lambda e, c8=c8: e.tensor_tensor(
                        out=Pm[:, c8:c8 + 8, :], in0=PS[5][0:64, :].rearrange("p (a t) -> p a t", t=64),
                        in1=Pm[:, c8:c8 + 8, :], op=ALU.add), reads=[("PS", 5), "Pm"], writes=["Pm"])
                curN, curNk, curX, curXk = nN, nNk, nX, nXk
            P.op("pool", lambda e: e.memset(Ast[:], 0.0), writes=["Ast"])
            P.op("pool", lambda e: e.memset(Abf[:], 0.0), writes=["Abf"])
            for c in range(NCH):
                P.op("pe", lambda e, c=c: e.matmul(PS[0][0:64, 0:64], lhsT=ARh[:, c, 0:64], rhs=Abf[:], start=True, stop=False),
                     reads=["ARh", "Abf"], writes=[("PS", 0)])
                P.op("pe", lambda e, c=c: e.matmul(PS[0][0:64, 0:64], lhsT=GmK[:, c, 0:64], rhs=Vtok[:, c, :], start=False, stop=True),
                     reads=["GmK", "Vtok"], writes=[("PS", 0)])
                P.op("act", lambda e: e.copy(out=Xs[:], in_=PS[0][0:64, 0:64]), reads=[("PS", 0)], writes=["Xs"])
                P.op("pe", lambda e, c=c: e.matmul(PS[1][0:64, 0:64], lhsT=Pm[:, c, :], rhs=Xs[:], start=True, stop=True),
                     reads=["Pm", "Xs"], writes=[("PS", 1)])
                P.op("dve", lambda e: e.tensor_copy(out=Us[:], in_=PS[1][0:64, 0:64]), reads=[("PS", 1)], writes=["Us"])
                P.op("pe", lambda e, c=c: e.matmul(PS[6][0:64, 0:64], lhsT=Btok[:, c, :], rhs=Us[:], start=True, stop=False),
                     reads=["Btok", "Us"], writes=[("PS", 6)])
                P.op("pe", lambda e, c=c: e.matmul(PS[6][0:64, 0:64], lhsT=Ktok[:, c, :], rhs=Vtok[:, c, :], start=False, stop=True),
                     reads=["Ktok", "Vtok"], writes=[("PS", 6)])
                ob = 2 + (c % 2)
                P.op("pe", lambda e, c=c, ob=ob: e.matmul(PS[ob][0:64, 0:64], lhsT=Abf[:], rhs=ARh[:, c, 64:128], start=True, stop=False),
                     reads=["Abf", "ARh"], writes=[("PS", ob)])
                P.op("pe", lambda e, c=c, ob=ob: e.matmul(PS[ob][0:64, 0:64], lhsT=Us[:], rhs=GmB[:, c, 64:128], start=False, stop=False),
                     reads=["Us", "GmB"], writes=[("PS", ob)])
                P.op("pe", lambda e, c=c, ob=ob: e.matmul(PS[ob][0:64, 0:64], lhsT=Vtok[:, c, :], rhs=GmK[:, c, 64:128], start=False, stop=True),
                     reads=["Vtok", "GmK"], writes=[("PS", ob)])
                P.op("dve", lambda e: e.tensor_tensor(out=Tt[:], in0=PS[6][0:64, 0:64], in1=Ast[:], op=ALU.add),
                     reads=[("PS", 6), "Ast"], writes=["Tt"])
                P.op("act", lambda e, c=c: e.activation(out=Abf[:], in_=Tt[:], func=AF.Copy, scale=PC[:, c:c + 1]),
                     reads=["Tt", "PC"], writes=["Abf"])
                P.op("dve", lambda e, c=c: e.tensor_scalar(out=Ast[:], in0=Tt[:], scalar1=PC[:, c:c + 1], scalar2=None, op0=ALU.mult),
                     reads=["Tt", "PC"], writes=["Ast"])
                P.op("act", lambda e, c=c, ob=ob: e.copy(out=oT[:, c * 64:(c + 1) * 64], in_=PS[ob][0:64, 0:64]),
                     reads=[("PS", ob)], writes=[("oT", c // 8)])
            P.dma("sp", K.oT_d[r0:r0 + 64, :], oT[:], reads=[("oT", q) for q in range(8)], writes=[("oT_d", hd)])
        P.flush()

def phase4d_rwkv_post(K, cts=range(2)):
    nc, P = K.nc, K.P
    with contextlib.ExitStack() as st:
        def sb(name, shape, dt):
            return st.enter_context(nc.sbuf_tensor(name, shape, dt))
        ident = sb("ident4d", [128, 128], BF16)
        make_ident(K, P, ident)
        bonesf = sb("bonesf", [128, 128], F32)
        P.op("pool", lambda e: e.memset(bonesf[:], 0.0), writes=["bonesf"])
        P.op("pool", lambda e: e.memset(bonesf[0:64, 0:64], 1.0), reads=["bonesf"], writes=["bonesf"])
        P.op("pool", lambda e: e.memset(bonesf[64:128, 64:128], 1.0), reads=["bonesf"], writes=["bonesf"])
        prm = sb("prm4d", [128, 2, 2], F32)
        P.dma("sp", prm[:, 0, :], K.rw_prm[8], writes=["prm0"])
        P.dma("sp", prm[:, 1, :], K.rw_prm[9], writes=["prm1"])
        o = sb("o4d", [128, S], F32)
        osq = sb("osq", [128, S], F32)
        bon = sb("bon", [128, S], BF16)
        gg = sb("gg", [128, S], BF16)
        Mb = [sb("Mb%d" % i, [128, 512], F32) for i in range(2)]
        Vb = [sb("Vb%d" % i, [128, 512], F32) for i in range(2)]
        Yb = [sb("Yb%d" % i, [128, 512], F32) for i in range(2)]
        Ob = [sb("Ob%d" % i, [128, 512], BF16) for i in range(2)]
        Tk = [sb("Tk%d" % i, [128, 4, 128], BF16) for i in range(2)]
        ps = [st.enter_context(nc.psum_tensor("p4d_%d" % i, [128, 512], F32)) for i in range(4)]
        pst = [st.enter_context(nc.psum_tensor("p4dt_%d" % i, [128, 4, 128], BF16)) for i in range(2)]
        it = 0
        for ct in cts:
            c0 = ct * 128
            P.dma("sp", o[:], K.oT_d[c0:c0 + 128, :], writes=["o"])
            P.dma("sp", bon[:], K.BON_d[c0:c0 + 128, :], writes=["bon"])
            P.dma("sp", gg[:], K.G_d[c0:c0 + 128, :], writes=["gg"])
            P.op("act", lambda e: e.activation(out=osq[:], in_=o[:], func=AF.Square), reads=["o"], writes=["osq"])
            for blk in range(8):
                s2 = it % 2
                it += 1
                bs = slice(blk * 512, (blk + 1) * 512)
                P.op("pe", lambda e, bs=bs, s2=s2: e.matmul(ps[s2][:, :], lhsT=bonesf[:], rhs=o[:, bs], start=True, stop=True),
                     reads=["bonesf", "o"], writes=[("p4d", s2)])
                P.op("pe", lambda e, bs=bs, s2=s2: e.matmul(ps[2 + s2][:, :], lhsT=bonesf[:], rhs=osq[:, bs], start=True, stop=True),
                     reads=["bonesf", "osq"], writes=[("p4d", 2 + s2)])
                P.op("act", lambda e, s2=s2: e.activation(out=Mb[s2][:], in_=ps[s2][:, :], func=AF.Copy, scale=1.0 / 64),
                     reads=[("p4d", s2)], writes=[("Mb", s2)])
                P.op("pool", lambda e, s2=s2: e.tensor_tensor(out=Vb[s2][:], in0=Mb[s2][:], in1=Mb[s2][:], op=ALU.mult),
                     reads=[("Mb", s2)], writes=[("Vb", s2)])
                P.op("dve", lambda e, s2=s2: e.scalar_tensor_tensor(out=Vb[s2][:], in0=ps[2 + s2][:, :], scalar=1.0 / 64, in1=Vb[s2][:],
                                                                     op0=ALU.mult, op1=ALU.subtract),
                     reads=[("p4d", 2 + s2), ("Vb", s2)], writes=[("Vb", s2)])
                P.op("dve", lambda e, s2=s2: e.tensor_scalar(out=Vb[s2][:], in0=Vb[s2][:], scalar1=64e-5, scalar2=None, op0=ALU.add),
                     reads=[("Vb", s2)], writes=[("Vb", s2)])
                P.op("act", lambda e, s2=s2: e.activation(out=Vb[s2][:], in_=Vb[s2][:], func=AF.Sqrt),
                     reads=[("Vb", s2)], writes=[("Vb", s2)])
                P.op("dve", lambda e, s2=s2: e.reciprocal(out=Vb[s2][:], in_=Vb[s2][:]), reads=[("Vb", s2)], writes=[("Vb", s2)])
                P.op("pool", lambda e, s2=s2, bs=bs: e.tensor_tensor(out=Yb[s2][:], in0=o[:, bs], in1=Mb[s2][:], op=ALU.subtract),
                     reads=["o", ("Mb", s2)], writes=[("Yb", s2)])
                P.op("dve", lambda e, s2=s2: e.tensor_tensor(out=Yb[s2][:], in0=Yb[s2][:], in1=Vb[s2][:], op=ALU.mult),
                     reads=[("Yb", s2), ("Vb", s2)], writes=[("Yb", s2)])
                P.op("dve", lambda e, s2=s2, ct=ct: e.tensor_scalar(out=Yb[s2][:], in0=Yb[s2][:], scalar1=prm[:, 0, ct:ct + 1],
                                                                     scalar2=prm[:, 1, ct:ct + 1], op0=ALU.mult, op1=ALU.add),
                     reads=[("Yb", s2), "prm0", "prm1"], writes=[("Yb", s2)])
                P.op("pool", lambda e, s2=s2, bs=bs: e.tensor_tensor(out=Yb[s2][:], in0=Yb[s2][:], in1=bon[:, bs], op=ALU.add),
                     reads=[("Yb", s2), "bon"], writes=[("Yb", s2)])
                P.op("dve", lambda e, s2=s2, bs=bs: e.tensor_tensor(out=Ob[s2][:], in0=Yb[s2][:], in1=gg[:, bs], op=ALU.mult),
                     reads=[("Yb", s2), "gg"], writes=[("Ob", s2)])
                for q in range(4):
                    P.op("pe", lambda e, s2=s2, q=q: e.transpose(out=pst[s2][:, q, :], in_=Ob[s2][:, q * 128:(q + 1) * 128],
                                                                 identity=ident[:]),
                         reads=[("Ob", s2), "ident"], writes=[("p4dt", s2)])
                P.op("act", lambda e, s2=s2: e.copy(out=Tk[s2][:], in_=pst[s2][:]), reads=[("p4dt", s2)], writes=[("Tk", s2)])
                P.dma("sp", K.ro_loc_d[blk // 4].rearrange("(t p) c -> p t c", p=128)[:, (blk % 4) * 4:(blk % 4 + 1) * 4, c0:c0 + 128], Tk[s2][:],
                      reads=[("Tk", s2)], writes=[("ro_tok_d", ct, blk)])
        P.flush()


def phase4e_allgather(K):
    P = K.P
    for hh in range(2):
        P.coll(lambda e, hh=hh: e.collective_compute("AllGather", ALU.bypass, replica_groups=[[0, 1, 2, 3], [4, 5, 6, 7]],
                                                     ins=[K.ro_loc_d[hh].opt()], outs=[K.ro_all_d[hh].opt()]),
               reads=[("ro_loc", hh)], writes=[("ro_all", hh)])
    P.flush()


def phase5a_select(K):
    nc, P = K.nc, K.P
    with contextlib.ExitStack() as st:
        def sb(name, shape, dt):
            return st.enter_context(nc.sbuf_tensor(name, shape, dt))
        ro = sb("ro_tok", [128, 32, 1024], BF16)
        selT = sb("selT", [128, 32, 1024], BF16)
        qrow = sb("qrow", [128, 1024], F32)
        tki = sb("tki", [128, 32], I32)
        tkf = sb("tkf", [128, 32], F32)
        mo = [sb("mo%d" % i, [128, 512], BF16) for i in range(2)]
        at = sb("at5", [128, 8, 1024], BF16)
        ps = [st.enter_context(nc.psum_tensor("p5a_%d" % i, [128, 512], F32)) for i in range(2)]
        for q4 in range(4):
            for hh in range(2):
                P.dma("sp", ro[:, hh * 16:(hh + 1) * 16, q4 * 256:(q4 + 1) * 256],
                      K.ro_all_d[hh][q4 * 2048:(q4 + 1) * 2048, :].rearrange("(t p) c -> p t c", p=128), writes=[("ro", q4, hh)])
        rok = [("ro", q4, hh) for q4 in range(4) for hh in range(2)]
        P.dma("sp", qrow[:], bcast_rows(K.qpos_row, 1024), writes=["qrow"])
        P.op("pool", lambda e: e.iota(tki[:], pattern=[[128, 32]], base=0, channel_multiplier=1), writes=["tki"])
        P.op("dve", lambda e: e.tensor_copy(out=tkf[:], in_=tki[:]), reads=["tki"], writes=["tkf"])
        for T in range(32):
            P.op("dve", lambda e, T=T: e.tensor_scalar(out=selT[:, T, :], in0=qrow[:], scalar1=tkf[:, T:T + 1], scalar2=0.0,
                                                      op0=ALU.is_equal, op1=ALU.add), reads=["qrow", "tkf"], writes=[("selT", T)])
        sk = [("selT", T) for T in range(32)]
        P.dma("sp", at[:], K.attT_d.rearrange("h p t -> p h t"), writes=["at5"])
        P.dma("sp", K.mixT_d.rearrange("k p t -> p k t")[:, 0:8, :], at[:], reads=["at5"], writes=["mixa"])
        i = 0
        for m in range(8):
            for half in range(2):
                s2 = i % 2
                i += 1
                for T in range(32):
                    P.op("pe", lambda e, T=T, m=m, half=half, s2=s2: e.matmul(
                        ps[s2][:, :], lhsT=ro[:, T, m * 128:(m + 1) * 128], rhs=selT[:, T, half * 512:(half + 1) * 512],
                        start=(T == 0), stop=(T == 31)), reads=rok + sk, writes=[("p5a", s2)])
                P.op("act", lambda e, s2=s2: e.copy(out=mo[s2][:], in_=ps[s2][:, :]), reads=[("p5a", s2)], writes=[("mo", s2)])
                P.dma("sp", K.mixT_d[8 + m, :, half * 512:(half + 1) * 512], mo[s2][:], reads=[("mo", s2)], writes=[("mixr", m, half)])
        P.flush()


def phase5b_outproj(K):
    nc, P = K.nc, K.P
    with contextlib.ExitStack() as st:
        def sb(name, shape, dt):
            return st.enter_context(nc.sbuf_tensor(name, shape, dt))
        ident = sb("ident5", [128, 128], BF16)
        make_ident(K, P, ident)
        G2, SH2 = load_G_SH(K, P, st, 3, 4, K.norm2_g, "p5")
        GT1 = sb("GT1", [128, D], F32)
        P.dma("sp", GT1[:], bcast_rows(K.mod_d[2 * D:3 * D], D), writes=["GT1"])
        Wo = sb("Wo", [128, 16, D], BF16)
        stg = [sb("wstg5_%d" % i, [128, 4, 512], F32) for i in range(2)]
        wk = load_weight_bf16(K, P, stg, Wo, 0, K.w_out, D, "Wo")
        mixT = sb("mixT", [128, 16, 512], BF16)
        T = norm_tiles_alloc(K, st, "p5")
        x1 = T["xt"]
        hT = [sb("hT5_0", [128, 16, 512], BF16)] * 2
        xo = [sb("xo%d" % i, [128, D], F32) for i in range(2)]
        ps = [st.enter_context(nc.psum_tensor("p5b_%d" % i, [128, 512], F32)) for i in range(2)]
        ss, junk, hb, pT = T["ss"], T["junk"], T["hb"], T["pT"]
        gi = 0
        for blk in range(2):
            hs = 0
            P.dma("sp", mixT[:], K.mixT_d.rearrange("k p t -> p k t")[:, :, blk * 512:(blk + 1) * 512], writes=["mixT"])
            for ti in range(4):
                t = blk * 4 + ti
                xs = t % 2
                P.dma("sp", xo[xs][:], K.x_own[t * 128:(t + 1) * 128, :], writes=[("xo", xs)])
                for cg in range(4):
                    b = gi % 2
                    gi += 1
                    for k in range(16):
                        P.op("pe", lambda e, b=b, k=k, t=t, cg=cg: e.matmul(
                            ps[b][:, :], lhsT=mixT[:, k, (t % 4) * 128:(t % 4 + 1) * 128], rhs=Wo[:, k, cg * 512:(cg + 1) * 512],
                            start=(k == 0), stop=(k == 15)), reads=["mixT"] + wk, writes=[("p5b", b)])
                    cs = slice(cg * 512, (cg + 1) * 512)
                    P.op("dve", lambda e, b=b, xs=xs, cs=cs: e.tensor_tensor(out=x1[xs][:, cs], in0=ps[b][:, :], in1=GT1[:, cs], op=ALU.mult),
                         reads=[("p5b", b), "GT1"], writes=[("xt", xs)])
                    P.op("pool", lambda e, xs=xs, cs=cs: e.tensor_tensor(out=x1[xs][:, cs], in0=x1[xs][:, cs], in1=xo[xs][:, cs], op=ALU.add),
                         reads=[("xt", xs), ("xo", xs)], writes=[("xt", xs)])
                P.dma("sp", K.x1_d[t * 128:(t + 1) * 128, :], x1[xs][:], reads=[("xt", xs)], writes=[("x1_d", t)])
                P.op("act", lambda e, xs=xs: e.activation(out=junk[:], in_=x1[xs][:], func=AF.Square, accum_out=ss[:, 0:1]),
                     reads=[("xt", xs)], writes=["junk", "ss0"])
                P.op("dve", lambda e: e.tensor_scalar(out=ss[:, 1:2], in0=ss[:, 0:1], scalar1=1.0 / D, scalar2=1e-6,
                                                       op0=ALU.mult, op1=ALU.add), reads=["ss0"], writes=["ss1"])
                P.op("act", lambda e: e.activation(out=ss[:, 2:3], in_=ss[:, 1:2], func=AF.Sqrt), reads=["ss1"], writes=["ss2"])
                P.op("dve", lambda e: e.reciprocal(out=ss[:, 3:4], in_=ss[:, 2:3]), reads=["ss2"], writes=["ss3"])
                P.op("dve", lambda e, xs=xs: e.scalar_tensor_tensor(out=x1[xs][:], in0=x1[xs][:], scalar=ss[:, 3:4], in1=G2[:],
                                                                   op0=ALU.mult, op1=ALU.mult),
                     reads=[("xt", xs), "ss3", "G"], writes=[("xt", xs)])
                P.op("pool", lambda e, xs=xs: e.tensor_tensor(out=hb[xs][:], in0=x1[xs][:], in1=SH2[:], op=ALU.add),
                     reads=[("xt", xs), "SH"], writes=[("hb", xs)])
                for half in range(2):
                    for kk in range(8):
                        k = half * 8 + kk
                        P.op("pe", lambda e, k=k, kk=kk, half=half, xs=xs: e.transpose(
                            out=pT[half][:, kk, :], in_=hb[xs][:, k * 128:(k + 1) * 128], identity=ident[:]),
                            reads=[("hb", xs), "ident"], writes=[("pT", half)])
                    o_ = hT[hs][:, half * 8:(half + 1) * 8, ti * 128:(ti + 1) * 128]
                    if half == 0:
                        P.op("act", lambda e, o_=o_, half=half: e.copy(out=o_, in_=pT[half][:]), reads=[("pT", half)], writes=[("hT5", hs, ti, half)])
                    else:
                        P.op("dve", lambda e, o_=o_, half=half: e.tensor_copy(out=o_, in_=pT[half][:]), reads=[("pT", half)], writes=[("hT5", hs, ti, half)])
            P.dma("sp", K.h2T_d.rearrange("k p t -> p k t")[:, :, blk * 512:(blk + 1) * 512], hT[hs][:],
                  reads=[("hT5", hs, ti, half) for ti in range(4) for half in range(2)], writes=[("h2T_d", blk)])
        P.flush()


def phase5c_ffn(K):
    nc, P = K.nc, K.P
    NF = 5632 // 128
    with contextlib.ExitStack() as st:
        def sb(name, shape, dt):
            return st.enter_context(nc.sbuf_tensor(name, shape, dt))
        h2T = sb("h2T", [128, 16, OWN], BF16)
        P.dma("sp", h2T[:], K.h2T_d.rearrange("k p t -> p k t"), writes=["h2T"])
        ao = [sb("ao%d" % i, [128, 512], BF16) for i in range(2)]
        stg = [sb("wstg6_%d" % i, [128, 4, 512], F32) for i in range(4)]
        Wg = [sb("Wg%d" % i, [128, 16, 512], BF16) for i in range(2)]
        Wu = [sb("Wu%d" % i, [128, 16, 512], BF16) for i in range(2)]
        sg = [sb("sg%d" % i, [128, 512], F32) for i in range(2)]
        ps = [st.enter_context(nc.psum_tensor("p5c_%d" % i, [128, 512], F32)) for i in range(4)]
        gi = 0

        def load_group(fg):
            ws = fg % 2
            load_weight_bf16(K, P, stg, Wg[ws], 0, K.w_ffn_gate[:, fg * 512:(fg + 1) * 512], 512, ("Wg", ws))
            load_weight_bf16(K, P, stg, Wu[ws], 0, K.w_ffn_up[:, fg * 512:(fg + 1) * 512], 512, ("Wu", ws))
        load_group(0)
        for fg in range(11):
            ws = fg % 2
            if fg + 1 < 11:
                load_group(fg + 1)
            for f4 in range(4):
                f = fg * 4 + f4
                for tb in range(2):
                    b = gi % 2
                    gi += 1
                    for k in range(16):
                        P.op("pe", lambda e, b=b, k=k, f4=f4, tb=tb, ws=ws: e.matmul(
                            ps[b][:, :], lhsT=Wg[ws][:, k, f4 * 128:(f4 + 1) * 128], rhs=h2T[:, k, tb * 512:(tb + 1) * 512],
                            start=(k == 0), stop=(k == 15)), reads=["h2T", (("Wg", ws), 0, (k // 4) * 4)], writes=[("p5c", b)])
                    for k in range(16):
                        P.op("pe", lambda e, b=b, k=k, f4=f4, tb=tb, ws=ws: e.matmul(
                            ps[2 + b][:, :], lhsT=Wu[ws][:, k, f4 * 128:(f4 + 1) * 128], rhs=h2T[:, k, tb * 512:(tb + 1) * 512],
                            start=(k == 0), stop=(k == 15)), reads=["h2T", (("Wu", ws), 0, (k // 4) * 4)], writes=[("p5c", 2 + b)])
                    P.op("act", lambda e, b=b: e.activation(out=sg[b][:], in_=ps[b][:, :], func=AF.Silu),
                         reads=[("p5c", b)], writes=[("sg", b)])
                    P.op("dve", lambda e, b=b: e.tensor_tensor(out=ao[b][:], in0=ps[2 + b][:, :], in1=sg[b][:], op=ALU.mult),
                         reads=[("p5c", 2 + b), ("sg", b)], writes=[("ao", b)])
                    P.dma("sp", K.actT_d[f, :, tb * 512:(tb + 1) * 512], ao[b][:], reads=[("ao", b)], writes=[("actT_d", f, tb)])
        P.flush()
    with contextlib.ExitStack() as st:
        def sb(name, shape, dt):
            return st.enter_context(nc.sbuf_tensor(name, shape, dt))
        GT2 = sb("GT2", [128, D], F32)
        P.dma("sp", GT2[:], bcast_rows(K.mod_d[5 * D:6 * D], D), writes=["GT2"])
        actT = sb("actT", [128, NF, OWN], BF16)
        for q in range(4):
            P.dma("sp", actT[:, q * 11:(q + 1) * 11, :], K.actT_d.rearrange("f p t -> p f t")[:, q * 11:(q + 1) * 11, :], writes=[("actT", q)])
        ak = [("actT", q) for q in range(4)]
        stg = [sb("wstg7_%d" % i, [128, 4, 256], F32) for i in range(4)]
        ps = [st.enter_context(nc.psum_tensor("p5d_%d" % i, [128, 512], F32)) for i in range(2)]
        gi = 0
        Wd = [sb("Wd%d" % i, [128, NF, 256], BF16) for i in range(2)]
        x1 = [sb("x1_%d" % i, [128, 256], F32) for i in range(2)]
        yo = [sb("yo%d" % i, [128, 256], F32) for i in range(2)]
        wdv = K.w_ffn_down.rearrange("(k p) n -> p k n", p=128)
        engs = ["pool", "dve", "act"]

        def load_wd(cg):
            wsl = cg % 2
            for k0 in range(0, NF, 4):
                i = K.wcnt
                K.wcnt += 1
                sl = i % 4
                P.dma("sp", stg[sl][:, 0:4, 0:256], wdv[:, k0:k0 + 4, cg * 256:(cg + 1) * 256], writes=[("wstg", sl)])
                eng = engs[i % 3]
                o_ = Wd[wsl][:, k0:k0 + 4, :]
                if eng == "act":
                    P.op("act", lambda e, o_=o_, sl=sl: e.copy(out=o_, in_=stg[sl][:, 0:4, 0:256]), reads=[("wstg", sl)], writes=[("Wd", wsl, k0)])
                else:
                    P.op(eng, lambda e, o_=o_, sl=sl: e.tensor_copy(out=o_, in_=stg[sl][:, 0:4, 0:256]), reads=[("wstg", sl)], writes=[("Wd", wsl, k0)])
        load_wd(0)
        for cg in range(8):
            wsl = cg % 2
            cs = slice(cg * 256, (cg + 1) * 256)
            if cg + 1 < 8:
                load_wd(cg + 1)
            for t in range(8):
                b = gi % 2
                gi += 1
                P.dma("sp", x1[b][:], K.x1_d[t * 128:(t + 1) * 128, cs], writes=[("x1", b)])
                for f in range(NF):
                    P.op("pe", lambda e, b=b, f=f, t=t, wsl=wsl: e.matmul(ps[b][:, 0:256], lhsT=actT[:, f, t * 128:(t + 1) * 128], rhs=Wd[wsl][:, f, :],
                                                                          start=(f == 0), stop=(f == NF - 1)),
                         reads=[("actT", f // 11), ("Wd", wsl, (f // 4) * 4)], writes=[("p5c", b)])
                P.op("dve", lambda e, b=b, cs=cs: e.tensor_tensor(out=yo[b][:], in0=ps[b][:, 0:256], in1=GT2[:, cs], op=ALU.mult),
                     reads=[("p5c", b), "GT2"], writes=[("yo", b)])
                P.op("pool", lambda e, b=b: e.tensor_tensor(out=yo[b][:], in0=yo[b][:], in1=x1[b][:], op=ALU.add),
                     reads=[("yo", b), ("x1", b)], writes=[("yo", b)])
                P.dma("sp", K.out[t * 128:(t + 1) * 128, cs], yo[b][:], reads=[("yo", b)], writes=[("out", t, cg)])
        P.flush()


def phase_final_copy(K):
    nc, P = K.nc, K.P
    with contextlib.ExitStack() as st:
        xt = [st.enter_context(nc.sbuf_tensor("fx%d" % i, [128, D], F32)) for i in range(2)]
        for t in range(8):
            s = t % 2
            P.dma("sp", xt[s][:], K.x_own[t * 128:(t + 1) * 128, :], writes=[("fx", s)])
            P.dma("sp", K.out[t * 128:(t + 1) * 128, :], xt[s][:], reads=[("fx", s)], writes=[("out", t)])
        P.flush()


def own_tiles(j):
    r = []
    for m in range(4):
        r += [8 * m + j, 8 * m + 7 - j]
    return r


def build_program(debug=False, stages=99, cts=range(2), dbg_list=None, skip_att=False):
    nc = bass.Bass("TRN2", target_bir_lowering=False)
    K = Ctx()
    K.stages = stages
    K.cts = cts
    K.skip_att = skip_att
    K.nc = nc
    K.dbg = {}
    K.wcnt = 0

    def inp(name, shape, dt=F32):
        return nc.dram_tensor(name, list(shape), dt, kind="ExternalInput").ap()

    def scratch(name, shape, dt):
        return nc.dram_tensor(name, list(shape), dt, kind="Internal").ap()

    K.x_full = inp("x_full", [S, D])
    K.x_own = inp("x_own", [OWN, D])
    K.c_arr = inp("c_arr", [128, 16])
    K.pos_full = inp("pos_full", [128, 32], I32)
    K.invf_att = inp("invf_att", [128, 16])
    K.invf_idx = inp("invf_idx", [128, 8])
    K.w_ada = inp("w_ada", [D, 6 * D])
    K.b_ada = inp("b_ada", [6 * D])
    K.norm1_g = inp("norm1_g", [D])
    K.k_norm_g = inp("k_norm_g", [128])
    K.q_norm_g = inp("q_norm_g", [128])
    K.pos_own = inp("pos_own", [128, 8], I32)
    K.qpos_own = inp("qpos_own", [128, 8])
    K.w_in = inp("w_in", [D, 4176])
    K.rw_prm = [inp("rwp%d" % i, [128, 2]) for i in range(10)]
    K.w_in_rw = inp("w_in_rw", [D, 1216])
    K.rw_mul = inp("rw_mul", [128, 4])
    K.rw_w_up = inp("rw_w_up", [96, 256])
    K.rw_a_up = inp("rw_a_up", [96, 256])
    K.rw_g_up = inp("rw_g_up", [256, 256])
    K.qpos_row = inp("qpos_row", [OWN])
    K.w_out = inp("w_out", [D, D])
    K.norm2_g = inp("norm2_g", [D])
    K.w_ffn_gate = inp("w_ffn_gate", [D, 5632])
    K.w_ffn_up = inp("w_ffn_up", [D, 5632])
    K.w_ffn_down = inp("w_ffn_down", [5632, D])
    K.out = nc.dram_tensor("y_own", [OWN, D], F32, kind="ExternalOutput").ap()
    K.mod_d = scratch("mod_d", [6 * D], F32)
    K.hT_d = scratch("hT_d", [16, 128, S], BF16)
    K.kT_d = scratch("kT_d", [8, 128, S], BF16)
    K.v_d = scratch("v_d", [S, 8 * 129], BF16)
    K.ikT_d = scratch("ikT_d", [64, S], BF16)
    K.yT_d = scratch("yT_d", [1216, S], F32)
    K.qT_d = scratch("qT_d", [8, 128, OWN], BF16)
    K.iqT_d = scratch("iqT_d", [64, OWN, 16], BF16)
    K.iw_d = scratch("iw_d", [OWN, 16], F32)
    K.attT_d = scratch("attT_d", [8, 128, OWN], BF16)
    for nm in ("vb_d", "G_d", "BON_d", "AH_d", "RH_d", "BH_d", "KH_d"):
        setattr(K, nm, scratch(nm, [256, S], BF16))
    K.PC_d = scratch("PC_d", [256, NCH], F32)
    K.oT_d = scratch("oT_d", [256, S], F32)
    K.ro_loc_d = [scratch("ro_loc%d_d" % i, [2048, 256], BF16) for i in range(2)]
    K.ro_all_d = [scratch("ro_all%d_d" % i, [8192, 256], BF16) for i in range(2)]
    K.mixT_d = scratch("mixT_d", [16, 128, OWN], BF16)
    K.x1_d = scratch("x1_d", [OWN, D], F32)
    K.h2T_d = scratch("h2T_d", [16, 128, OWN], BF16)
    K.actT_d = scratch("actT_d", [44, 128, OWN], BF16)
    with contextlib.ExitStack() as stack:
        K.P = Prog(nc, stack)
        phase0_adaln(K)
        phase1_kv(K)
        if K.stages >= 2:
            phase1b_rwkv_proj(K)
        if K.stages >= 3 and not getattr(K, "skip_att", False):
            phase2_own_proj(K)
            phase3_attention(K)
        if K.stages >= 4:
            phase4b_rwkv_prep(K, cts=K.cts)
            if K.stages >= 5:
                phase4c_rwkv_scan(K, heads=[h for ct in K.cts for h in (2 * ct, 2 * ct + 1)])
        if K.stages >= 6:
            phase4d_rwkv_post(K, cts=K.cts)
            phase4e_allgather(K)
        if K.stages >= 7:
            phase5a_select(K)
            phase5b_outproj(K)
            phase5c_ffn(K)
        else:
            phase_final_copy(K)
        if debug:
            P = K.P
            allc = (("dbg_mixT", K.mixT_d, [16, 128, OWN], BF16), ("dbg_x1", K.x1_d, [OWN, D], F32),
                    ("dbg_oT", K.oT_d, [256, S], F32), ("dbg_AH", K.AH_d, [256, S], BF16), ("dbg_BH", K.BH_d, [256, S], BF16),
                    ("dbg_KH", K.KH_d, [256, S], BF16), ("dbg_RH", K.RH_d, [256, S], BF16), ("dbg_PC", K.PC_d, [256, NCH], F32),
                    ("dbg_G", K.G_d, [256, S], BF16), ("dbg_BON", K.BON_d, [256, S], BF16), ("dbg_vb", K.vb_d, [256, S], BF16),
                    ("dbg_yT", K.yT_d, [1216, S], F32), ("dbg_attT", K.attT_d, [8, 128, OWN], BF16),
                                     ("dbg_qT", K.qT_d, [8, 128, OWN], BF16), ("dbg_iqT", K.iqT_d, [64, OWN, 16], BF16),
                                     ("dbg_iw", K.iw_d, [OWN, 16], F32))
            for nm, src, shp, dt in allc:
                if dbg_list is not None and nm not in dbg_list:
                    continue
                o = dbg_out(K, nm, shp, dt)
                P.dma("sp", o, src, writes=[nm])
            P.flush()
    return nc, K


def make_in_maps(inputs, cores=range(8)):
    x = np.asarray(inputs["x"], dtype=np.float32)
    c = np.asarray(inputs["c"], dtype=np.float32)
    pos = np.asarray(inputs["positions"], dtype=np.int32)
    invf_att = (np.float32(500000.0) ** (-np.arange(16, dtype=np.float32) / np.float32(16))).astype(np.float32)
    invf_idx = (np.float32(500000.0) ** (-np.arange(8, dtype=np.float32) / np.float32(8))).astype(np.float32)
    mu = np.asarray(inputs["rwkv_mu"][0], dtype=np.float32)

    vecs = [mu[0:1024], mu[1024:2048], mu[2048:3072], inputs["rwkv_w0"][0], inputs["rwkv_a0"][0], inputs["rwkv_k_k"][0],
            inputs["rwkv_k_a"][0], np.asarray(inputs["rwkv_r_k"][0]).reshape(-1), inputs["rwkv_lnx_g"][0], inputs["rwkv_lnx_b"][0]]
    w_in_full = np.asarray(inputs["w_in"][0], dtype=np.float32)
    rw_mul = np.zeros((128, 4), np.float32)
    rw_mul[:96, 0] = mu[3072:3168]
    rw_mul[:96, 1] = mu[3168:3264]
    rw_mul[:, 2] = mu[3264:3392]
    rw_mul[:, 3] = mu[3392:3520]
    maps = []
    for core in cores:
        b, j = core // 4, core % 4
        ch = slice(256 * j, 256 * j + 256)
        rwp = {"rwp%d" % i: np.ascontiguousarray(np.asarray(v, dtype=np.float32)[ch].reshape(2, 128).T) for i, v in enumerate(vecs)}
        R0 = 4176
        w_in_rw = np.ascontiguousarray(np.concatenate([w_in_full[:, R0 + 256 * j:R0 + 256 * j + 256],
                                                       w_in_full[:, R0 + 1024 + 256 * j:R0 + 1024 + 256 * j + 256],
                                                       w_in_full[:, R0 + 2048 + 256 * j:R0 + 2048 + 256 * j + 256],
                                                       w_in_full[:, R0 + 3072:R0 + 3520]], axis=1))
        tiles = own_tiles(j)
        idx = np.concatenate([np.arange(t * 128, (t + 1) * 128) for t in tiles])
        maps.append({
            "x_full": np.ascontiguousarray(x[b]),
            "x_own": np.ascontiguousarray(x[b][idx]),
            "c_arr": np.ascontiguousarray(c[b].reshape(16, 128).T),
            "pos_full": np.ascontiguousarray(pos[b].reshape(32, 128).T),
            "invf_att": np.ascontiguousarray(np.broadcast_to(invf_att, (128, 16))),
            "invf_idx": np.ascontiguousarray(np.broadcast_to(invf_idx, (128, 8))),
            "w_ada": np.asarray(inputs["w_ada"][0], dtype=np.float32),
            "b_ada": np.asarray(inputs["b_ada"][0], dtype=np.float32),
            "norm1_g": np.asarray(inputs["norm1_g"][0], dtype=np.float32),
            "k_norm_g": np.asarray(inputs["k_norm_g"][0], dtype=np.float32),
            "q_norm_g": np.asarray(inputs["q_norm_g"][0], dtype=np.float32),
            "pos_own": np.ascontiguousarray(pos[b][idx].reshape(8, 128).T),
            "qpos_own": np.ascontiguousarray(idx.astype(np.float32).reshape(8, 128).T),
            "w_in": np.ascontiguousarray(w_in_full[:, 0:4176]),
            "qpos_row": idx.astype(np.float32),
            "w_out": np.asarray(inputs["w_out"][0], dtype=np.float32),
            "norm2_g": np.asarray(inputs["norm2_g"][0], dtype=np.float32),
            "w_ffn_gate": np.asarray(inputs["w_ffn_gate"][0], dtype=np.float32),
            "w_ffn_up": np.asarray(inputs["w_ffn_up"][0], dtype=np.float32),
            "w_ffn_down": np.asarray(inputs["w_ffn_down"][0], dtype=np.float32),
            "rw_w_up": np.ascontiguousarray(np.asarray(inputs["rwkv_w_up"][0], dtype=np.float32)[:, ch]),
            "rw_a_up": np.ascontiguousarray(np.asarray(inputs["rwkv_a_up"][0], dtype=np.float32)[:, ch]),
            "rw_g_up": np.ascontiguousarray(np.asarray(inputs["rwkv_g_up"][0], dtype=np.float32)[:, ch]),
            "w_in_rw": w_in_rw,
            "rw_mul": rw_mul,
            **rwp,
        })
    return maps


def kernel(**inputs):
    nc, K = build_program(debug=False)
    maps = make_in_maps(inputs)
    res = run_bass_kernel_spmd(nc, maps, core_ids=list(range(8)))
    out = np.zeros((2, S, D), dtype=np.float32)
    for core in range(8):
        b, j = core // 4, core % 4
        y = res.results[core]["y_own"]
        for i, t in enumerate(own_tiles(j)):
            out[b, t * 128:(t + 1) * 128] = y[i * 128:(i + 1) * 128]
    return out
```

```python
import contextlib
import numpy as np
import concourse.bass as bass
import concourse.mybir as mybir
from concourse.bass_utils import run_bass_kernel_spmd

F32 = mybir.dt.float32
BF16 = mybir.dt.bfloat16
I32 = mybir.dt.int32
AF = mybir.ActivationFunctionType
ALU = mybir.AluOpType
AX = mybir.AxisListType

D = 2048
S = 4096
NT = 32
OWN = 1024
ENGS = ("pe", "act", "dve", "pool", "sp")
DEBUG = {}


class _Op:
    __slots__ = ("eng", "fn", "deps", "needs_inc", "is_dma", "sem", "count", "idx", "prev_same_sem", "is_cc")

    def __init__(self, eng, fn, is_dma):
        self.eng = eng
        self.fn = fn
        self.deps = set()
        self.needs_inc = False
        self.is_dma = is_dma
        self.sem = None
        self.count = 0
        self.prev_same_sem = None
        self.is_cc = False


class Prog:
    def __init__(self, nc, stack, n_dma_sems=48):
        self.nc = nc
        self.n_dma_sems = n_dma_sems
        self.eng_sem = {e: stack.enter_context(nc.semaphore("s_" + e)) for e in ENGS}
        self.dma_sems = [stack.enter_context(nc.semaphore("d%d" % i)) for i in range(n_dma_sems)]
        self.bar_sem = stack.enter_context(nc.semaphore("bar"))
        self.cc_sem = stack.enter_context(nc.semaphore("ccs"))
        self.cc_cnt = 0
        self.cnt = {e: 0 for e in ENGS}
        self.dcnt = [0] * n_dma_sems
        self.rr = 0
        self.nbar = 0
        self._reset()

    def _reset(self):
        self.ops = []
        self.last_writer = {}
        self.readers = {}

    def _record(self, op, reads, writes):
        idx = len(self.ops)
        op.idx = idx
        deps = set()
        for k in reads:
            w = self.last_writer.get(k)
            if w is not None:
                deps.add(w)
        for k in writes:
            w = self.last_writer.get(k)
            if w is not None:
                deps.add(w)
            for r in self.readers.get(k, ()):
                deps.add(r)
        deps.discard(idx)
        op.deps = deps
        self.ops.append(op)
        for k in reads:
            self.readers.setdefault(k, []).append(idx)
        for k in writes:
            self.last_writer[k] = idx
            self.readers[k] = []
        return idx

    def op(self, eng, fn, reads=(), writes=()):
        return self._record(_Op(eng, fn, False), reads, writes)

    def dma(self, queue, out, in_, reads=(), writes=(), **kw):
        def fn(e, out=out, in_=in_, kw=kw):
            return e.dma_start(out=out, in_=in_, **kw)
        return self._record(_Op(queue, fn, True), reads, writes)

    def coll(self, fn, reads=(), writes=()):
        o = _Op("pool", fn, True)
        o.is_cc = True
        return self._record(o, reads, writes)

    def flush(self):
        nc = self.nc
        ops = self.ops
        for o in ops:
            nd = set()
            for d in o.deps:
                p = ops[d]
                if o.eng == "pe" and p.eng == "pe" and not p.is_dma and not o.is_dma:
                    continue
                nd.add(d)
                p.needs_inc = True
            o.deps = nd
        last_of = {}
        for o in ops:
            if not o.is_dma:
                last_of[o.eng] = o
        for o in last_of.values():
            o.needs_inc = True
        dlast = [None] * self.n_dma_sems
        for o in ops:
            if o.is_cc:
                self.cc_cnt += 1
                o.sem = self.cc_sem
                o.count = self.cc_cnt
            elif o.is_dma:
                s = self.rr % self.n_dma_sems
                self.rr += 1
                o.prev_same_sem = dlast[s]
                self.dcnt[s] += 16
                o.sem = self.dma_sems[s]
                o.count = self.dcnt[s]
                dlast[s] = o.idx
            elif o.needs_inc:
                self.cnt[o.eng] += 1
                o.sem = self.eng_sem[o.eng]
                o.count = self.cnt[o.eng]
        per_eng = {e: [o for o in ops if o.eng == e] for e in ENGS}
        final = [(self.dma_sems[s], self.dcnt[s]) for s in range(self.n_dma_sems) if self.dcnt[s] > 0]
        final += [(self.eng_sem[e], self.cnt[e]) for e in ENGS if self.cnt[e] > 0]
        if self.cc_cnt > 0:
            final.append((self.cc_sem, self.cc_cnt))
        self.nbar += 1
        nbar = self.nbar
        bar = self.bar_sem

        def run(e_name, eng):
            waited = {}
            for o in per_eng[e_name]:
                need = {}
                for d in o.deps:
                    p = ops[d]
                    if need.get(p.sem.num, (0, None))[0] < p.count:
                        need[p.sem.num] = (p.count, p.sem)
                if o.is_dma and o.prev_same_sem is not None:
                    p = ops[o.prev_same_sem]
                    if need.get(p.sem.num, (0, None))[0] < p.count:
                        need[p.sem.num] = (p.count, p.sem)
                for key, (c, s) in need.items():
                    if waited.get(key, 0) < c:
                        eng.wait_ge(s, c)
                        waited[key] = c
                ins = o.fn(eng)
                if o.is_cc:
                    ins.then_inc(o.sem)
                elif o.is_dma:
                    ins.then_inc(o.sem, 16)
                elif o.needs_inc:
                    ins.then_inc(o.sem, 1)
            if e_name == "sp":
                for s, c in final:
                    eng.wait_ge(s, c)
                eng.sem_inc(bar, 1)
            eng.wait_ge(bar, nbar)

        with nc.Block() as block:
            @block.tensor
            def _(e):
                run("pe", e)

            @block.scalar
            def _(e):
                run("act", e)

            @block.vector
            def _(e):
                run("dve", e)

            @block.gpsimd
            def _(e):
                run("pool", e)

            @block.sync
            def _(e):
                run("sp", e)
        self._reset()


class Ctx:
    pass


def bcast_rows(ap1d, n):
    return bass.AP(ap1d.tensor, ap1d.offset, [[0, 128], [1, n]])


def dbg_out(K, name, shape, dtype=F32):
    t = K.nc.dram_tensor(name, list(shape), dtype, kind="ExternalOutput")
    K.dbg[name] = t
    return t.ap()


def make_ident(K, P, ident):
    P.op("pool", lambda e: e.memset(ident[:], 0.0), writes=["ident"])
    P.op("pool", lambda e: e.affine_select(out=ident[:], in_=ident[:], pattern=[[-1, 128]],
                                           compare_op=ALU.not_equal, fill=1.0, base=0,
                                           channel_multiplier=1),
         reads=["ident"], writes=["ident"])


def phase0_adaln(K):
    nc, P = K.nc, K.P
    NQ = 3072
    with contextlib.ExitStack() as st:
        c_sb = st.enter_context(nc.sbuf_tensor("c_sb", [128, 16], F32))
        cact = st.enter_context(nc.sbuf_tensor("cact", [128, 16], F32))
        wst = [st.enter_context(nc.sbuf_tensor("wst%d" % i, [128, 16, 512], F32)) for i in range(2)]
        modrow = st.enter_context(nc.sbuf_tensor("modrow", [1, NQ], F32))
        brow = st.enter_context(nc.sbuf_tensor("brow", [1, NQ], F32))
        ps = [st.enter_context(nc.psum_tensor("ps0_%d" % i, [1, 512], F32)) for i in range(2)]
        P.dma("sp", c_sb[:], K.c_arr, writes=["c_sb"])
        P.dma("sp", brow[:], K.b_ada.rearrange("(o n) -> o n", o=1), writes=["brow"])
        P.op("act", lambda e: e.activation(out=cact[:], in_=c_sb[:], func=AF.Silu),
             reads=["c_sb"], writes=["cact"])
        wv = K.w_ada.rearrange("(k p) n -> p k n", p=128)
        for nt in range(NQ // 512):
            sl = nt % 2
            for hh in range(2):
                P.dma("sp", wst[sl][:, hh * 8:(hh + 1) * 8, :],
                      wv[:, hh * 8:(hh + 1) * 8, nt * 512:(nt + 1) * 512],
                      writes=[("wst", sl, hh)])
            for k in range(16):
                P.op("pe", lambda e, k=k, sl=sl: e.matmul(ps[sl][:, :], lhsT=cact[:, k:k + 1],
                                                         rhs=wst[sl][:, k, :], start=(k == 0), stop=(k == 15)),
                     reads=["cact", ("wst", sl, k // 8)], writes=[("ps0", sl)])
            P.op("dve", lambda e, nt=nt, sl=sl: e.tensor_tensor(
                out=modrow[0:1, nt * 512:(nt + 1) * 512], in0=ps[sl][:, :],
                in1=brow[0:1, nt * 512:(nt + 1) * 512], op=ALU.add),
                reads=[("ps0", sl), "brow"], writes=[("modrow", nt)])
        P.dma("sp", K.modq_d, modrow[:],
              reads=[("modrow", nt) for nt in range(NQ // 512)], writes=["modq_d"])
        P.flush()
    P.coll(lambda e: e.collective_compute("AllGather", ALU.bypass, replica_groups=[[0, 1, 2, 3], [4, 5, 6, 7]],
                                          ins=[K.modq_d.opt()], outs=[K.mod4_d.opt()]), reads=["modq_d"], writes=["mod4"])
    P.flush()


def load_mod_rows(K, P, tile, which, gain_ap=None, key=None):
    src = K.mod_d[which * D:(which + 1) * D]
    P.dma("sp", tile[:], bcast_rows(src, D), writes=[key])


def bc(ap, shape):
    return ap.to_broadcast(list(shape))


def load_weight_bf16(K, P, st_tiles, dst, c_dst, src2d, ncols, tag):
    wv = src2d.rearrange("(k p) n -> p k n", p=128)
    nk = wv.shape[1]
    engs = ["pool", "dve", "act"]
    for c0 in range(0, ncols, 512):
        n = min(512, ncols - c0)
        for k0 in range(0, nk, 4):
            kn = min(4, nk - k0)
            i = K.wcnt
            K.wcnt += 1
            sl = i % len(st_tiles)
            stg = st_tiles[sl]
            P.dma("sp", stg[:, 0:kn, 0:n], wv[:, k0:k0 + kn, c0:c0 + n], writes=[("wstg", sl)])
            eng = engs[i % 3]
            o = dst[:, k0:k0 + kn, c_dst + c0:c_dst + c0 + n]
            if eng == "act":
                P.op("act", lambda e, o=o, stg=stg, kn=kn, n=n: e.copy(out=o, in_=stg[:, 0:kn, 0:n]),
                     reads=[("wstg", sl)], writes=[(tag, c0, k0)])
            else:
                P.op(eng, lambda e, o=o, stg=stg, kn=kn, n=n: e.tensor_copy(out=o, in_=stg[:, 0:kn, 0:n]),
                     reads=[("wstg", sl)], writes=[(tag, c0, k0)])
    return [(tag, c0, k0) for c0 in range(0, ncols, 512) for k0 in range(0, nk, 4)]


def rope_tables(K, P, st, pos_arr, ntile, invf_att, invf_idx, tag):
    nc = K.nc
    posi = st.enter_context(nc.sbuf_tensor(tag + "posi", [128, ntile], I32))
    posf = st.enter_context(nc.sbuf_tensor(tag + "posf", [128, ntile], F32))
    iva = st.enter_context(nc.sbuf_tensor(tag + "iva", [128, 16], F32))
    ivi = st.enter_context(nc.sbuf_tensor(tag + "ivi", [128, 8], F32))
    P.dma("sp", posi[:], pos_arr, writes=[tag + "posi"])
    P.dma("sp", iva[:], invf_att, writes=[tag + "iva"])
    P.dma("sp", ivi[:], invf_idx, writes=[tag + "ivi"])
    P.op("dve", lambda e: e.tensor_copy(out=posf[:], in_=posi[:]), reads=[tag + "posi"], writes=[tag + "posf"])
    out = {}
    for nm, iv, h in (("a", iva, 16), ("i", ivi, 8)):
        u = st.enter_context(nc.sbuf_tensor(tag + "u" + nm, [128, ntile, h], F32))
        ui = st.enter_context(nc.sbuf_tensor(tag + "ui" + nm, [128, ntile, h], I32))
        uf = st.enter_context(nc.sbuf_tensor(tag + "uf" + nm, [128, ntile, h], F32))
        for fn, off in (("sin", 0.0), ("cos", 0.25)):
            tb = st.enter_context(nc.sbuf_tensor(tag + fn + nm, [128, ntile, h], F32))
            kk = tag + fn + nm
            P.op("dve", lambda e, u=u, iv=iv, h=h: e.tensor_tensor(
                out=u[:], in0=bc(posf[:].unsqueeze(2), [128, ntile, h]),
                in1=bc(iv[:].unsqueeze(1), [128, ntile, h]), op=ALU.mult),
                reads=[tag + "posf", tag + "iv" + nm], writes=[tag + "U" + nm])
            P.op("dve", lambda e, u=u, off=off: e.tensor_scalar(
                out=u[:], in0=u[:], scalar1=float(1.0 / (2 * np.pi)), scalar2=off, op0=ALU.mult, op1=ALU.add),
                reads=[tag + "U" + nm], writes=[tag + "U" + nm])
            P.op("dve", lambda e, u=u, ui=ui: e.tensor_copy(out=ui[:], in_=u[:]), reads=[tag + "U" + nm], writes=[tag + "UI" + nm])
            P.op("dve", lambda e, uf=uf, ui=ui: e.tensor_copy(out=uf[:], in_=ui[:]), reads=[tag + "UI" + nm], writes=[tag + "UF" + nm])
            P.op("dve", lambda e, u=u, uf=uf: e.tensor_tensor(out=u[:], in0=u[:], in1=uf[:], op=ALU.subtract),
                 reads=[tag + "U" + nm, tag + "UF" + nm], writes=[tag + "U" + nm])
            P.op("dve", lambda e, u=u: e.tensor_scalar(out=u[:], in0=u[:], scalar1=-0.5, scalar2=0.5,
                                                        op0=ALU.max, op1=ALU.min),
                 reads=[tag + "U" + nm], writes=[tag + "U" + nm])
            P.op("act", lambda e, u=u, tb=tb: e.activation(out=tb[:], in_=u[:], func=AF.Sin,
                                                            scale=float(2 * np.pi)),
                 reads=[tag + "U" + nm], writes=[kk])
            out[fn + nm] = (tb, kk)
    return out


def apply_rope(P, eng, x4, cos, sin, t, half, tmp, rk, wk, sfx=""):
    ctb, ck = cos
    stb, sk = sin
    H = x4.shape[1]
    x1 = x4[:, :, 0:half]
    x2 = x4[:, :, half:2 * half]
    cb = bc(ctb[:, t, :].unsqueeze(1), [128, H, half])
    sb = bc(stb[:, t, :].unsqueeze(1), [128, H, half])
    a, b2, c, d = tmp
    P.op(eng, lambda e: e.tensor_tensor(out=a[:, 0:H, 0:half], in0=x1, in1=cb, op=ALU.mult), reads=rk + [ck], writes=["rtmpA" + sfx])
    P.op(eng, lambda e: e.tensor_tensor(out=b2[:, 0:H, 0:half], in0=x2, in1=sb, op=ALU.mult), reads=rk + [sk], writes=["rtmpB" + sfx])
    P.op(eng, lambda e: e.tensor_tensor(out=c[:, 0:H, 0:half], in0=x2, in1=cb, op=ALU.mult), reads=rk + [ck], writes=["rtmpC" + sfx])
    P.op(eng, lambda e: e.tensor_tensor(out=d[:, 0:H, 0:half], in0=x1, in1=sb, op=ALU.mult), reads=rk + [sk], writes=["rtmpD" + sfx])
    P.op(eng, lambda e: e.tensor_tensor(out=x1, in0=a[:, 0:H, 0:half], in1=b2[:, 0:H, 0:half], op=ALU.subtract),
         reads=["rtmpA" + sfx, "rtmpB" + sfx, "rtmpC" + sfx, "rtmpD" + sfx] + rk, writes=rk)
    P.op(eng, lambda e: e.tensor_tensor(out=x2, in0=c[:, 0:H, 0:half], in1=d[:, 0:H, 0:half], op=ALU.add),
         reads=["rtmpC" + sfx, "rtmpD" + sfx] + rk, writes=rk)


def head_rmsnorm(P, x3, gain, sq, ssum, rk, wk, gk=None, sqk=None):
    P.op("pool", lambda e: e.tensor_tensor(out=sq[:], in0=x3, in1=x3, op=ALU.mult), reads=rk, writes=[sqk or (wk + "sq")])
    P.op("dve", lambda e: e.tensor_reduce(out=ssum[:, 0:8], in_=sq[:], axis=AX.X, op=ALU.add),
         reads=[sqk or (wk + "sq")], writes=[wk + "s0"])
    P.op("dve", lambda e: e.tensor_scalar(out=ssum[:, 8:16], in0=ssum[:, 0:8], scalar1=1.0 / 128, scalar2=1e-6,
                                           op0=ALU.mult, op1=ALU.add), reads=[wk + "s0"], writes=[wk + "s1"])
    P.op("act", lambda e: e.activation(out=ssum[:, 16:24], in_=ssum[:, 8:16], func=AF.Sqrt),
         reads=[wk + "s1"], writes=[wk + "s2"])
    P.op("dve", lambda e: e.reciprocal(out=ssum[:, 24:32], in_=ssum[:, 16:24]), reads=[wk + "s2"], writes=[wk + "s3"])
    P.op("dve", lambda e: e.tensor_tensor(out=x3, in0=x3, in1=bc(ssum[:, 24:32].unsqueeze(2), [128, 8, 128]),
                                           op=ALU.mult), reads=rk + [wk + "s3"], writes=rk)
    P.op("pool", lambda e: e.tensor_tensor(out=x3, in0=x3, in1=bc(gain[:].unsqueeze(1), [128, 8, 128]),
                                            op=ALU.mult), reads=rk + [gk or ("gain" + wk)], writes=rk)


def norm_load(K, P, T, x_src, t):
    xs = t % 2
    P.dma("sp", T["xt"][xs][:], x_src[t * 128:(t + 1) * 128, :], writes=[("xt", xs)])


def norm_block(K, P, T, x_src, t, G1, SH1, ident, blk_hT, ti, load=True, hname="hT"):
    xs = t % 2
    xt, hb, ss, junk, pT = T["xt"], T["hb"], T["ss"], T["junk"], T["pT"]
    if load:
        norm_load(K, P, T, x_src, t)
    P.op("act", lambda e: e.activation(out=junk[:], in_=xt[xs][:], func=AF.Square, accum_out=ss[:, 0:1]),
         reads=[("xt", xs)], writes=["junk", "ss0"])
    P.op("dve", lambda e: e.tensor_scalar(out=ss[:, 1:2], in0=ss[:, 0:1], scalar1=1.0 / D, scalar2=1e-6,
                                           op0=ALU.mult, op1=ALU.add), reads=["ss0"], writes=["ss1"])
    P.op("act", lambda e: e.activation(out=ss[:, 2:3], in_=ss[:, 1:2], func=AF.Sqrt), reads=["ss1"], writes=["ss2"])
    P.op("dve", lambda e: e.reciprocal(out=ss[:, 3:4], in_=ss[:, 2:3]), reads=["ss2"], writes=["ss3"])
    P.op("dve", lambda e: e.scalar_tensor_tensor(out=xt[xs][:], in0=xt[xs][:], scalar=ss[:, 3:4], in1=G1[:],
                                                  op0=ALU.mult, op1=ALU.mult),
         reads=[("xt", xs), "ss3", "G"], writes=[("xt", xs)])
    P.op("pool", lambda e: e.tensor_tensor(out=hb[xs][:], in0=xt[xs][:], in1=SH1[:], op=ALU.add),
         reads=[("xt", xs), "SH"], writes=[("hb", xs)])
    for half in range(2):
        for kk in range(8):
            k = half * 8 + kk
            P.op("pe", lambda e, k=k, kk=kk, half=half: e.transpose(
                out=pT[half][:, kk, :], in_=hb[xs][:, k * 128:(k + 1) * 128], identity=ident[:]),
                reads=[("hb", xs), "ident"], writes=[("pT", half)])
        o = blk_hT[:, half * 8:(half + 1) * 8, ti * 128:(ti + 1) * 128]
        if half == 0:
            P.op("act", lambda e, o=o, half=half: e.copy(out=o, in_=pT[half][:]),
                 reads=[("pT", half)], writes=[(hname, ti, half)])
        else:
            P.op("dve", lambda e, o=o, half=half: e.tensor_copy(out=o, in_=pT[half][:]),
                 reads=[("pT", half)], writes=[(hname, ti, half)])


def norm_tiles_alloc(K, st, tag):
    nc = K.nc
    T = {}
    T["xt"] = [st.enter_context(nc.sbuf_tensor(tag + "xt%d" % i, [128, D], F32)) for i in range(2)]
    T["hb"] = [st.enter_context(nc.sbuf_tensor(tag + "hb%d" % i, [128, D], BF16)) for i in range(2)]
    T["ss"] = st.enter_context(nc.sbuf_tensor(tag + "ss", [128, 4], F32))
    T["junk"] = st.enter_context(nc.sbuf_tensor(tag + "junk", [128, D], BF16))
    T["pT"] = [st.enter_context(nc.psum_tensor(tag + "pT%d" % i, [128, 8, 128], BF16)) for i in range(2)]
    return T


def load_G_SH(K, P, st, which_sh, which_sc, gain_vec, tag):
    nc = K.nc
    G = st.enter_context(nc.sbuf_tensor(tag + "G", [128, D], F32))
    SH = st.enter_context(nc.sbuf_tensor(tag + "SH", [128, D], F32))
    gtmp = st.enter_context(nc.sbuf_tensor(tag + "gtmp", [128, D], F32))
    P.dma("sp", SH[:], bcast_rows(K.mod_d[which_sh * D:(which_sh + 1) * D], D), writes=["SH"])
    P.dma("sp", G[:], bcast_rows(K.mod_d[which_sc * D:(which_sc + 1) * D], D), writes=["G"])
    P.dma("sp", gtmp[:], bcast_rows(gain_vec, D), writes=["gtmp"])
    P.op("dve", lambda e: e.scalar_tensor_tensor(out=G[:], in0=G[:], scalar=1.0, in1=gtmp[:],
                                                  op0=ALU.add, op1=ALU.mult), reads=["G", "gtmp"], writes=["G"])
    return G, SH


def phase1_kv(K):
    nc, P = K.nc, K.P
    with contextlib.ExitStack() as st:
        ident = st.enter_context(nc.sbuf_tensor("ident", [128, 128], BF16))
        make_ident(K, P, ident)
        G1, SH1 = load_G_SH(K, P, st, 0, 1, K.norm1_g, "p1")
        T = norm_tiles_alloc(K, st, "p1")
        hT = [st.enter_context(nc.sbuf_tensor("hT%d" % i, [128, 16, 512], BF16)) for i in range(2)]
        W = st.enter_context(nc.sbuf_tensor("Wkv", [128, 16, 2112], BF16))
        stg = [st.enter_context(nc.sbuf_tensor("wstg%d" % i, [128, 4, 512], F32)) for i in range(2)]
        wk_k = load_weight_bf16(K, P, stg, W, 0, K.w_in[:, 1024:2048], 1024, "Wk")
        wk_v = load_weight_bf16(K, P, stg, W, 1024, K.w_in[:, 2048:3072], 1024, "Wv")
        wk_i = load_weight_bf16(K, P, stg, W, 2048, K.w_in[:, 4096:4160], 64, "Wi")
        rt = rope_tables(K, P, st, K.pos_full, 32, K.invf_att, K.invf_idx, "rf")
        gain = st.enter_context(nc.sbuf_tensor("kgain", [128, 128], F32))
        P.dma("sp", gain[:], bcast_rows(K.k_norm_g, 128), writes=["gainK"])
        def two(name, shape, dt):
            return [st.enter_context(nc.sbuf_tensor(name + str(i), shape, dt)) for i in range(2)]
        ksb2 = two("ksb", [128, 8, 128], F32)
        kbf2 = two("kbf", [128, 8, 128], BF16)
        sq2 = [st.enter_context(nc.sbuf_tensor("sq", [128, 8, 128], F32))] * 2
        ssum2 = two("ssum", [128, 32], F32)
        rtmp2 = [[st.enter_context(nc.sbuf_tensor("rtmp%d" % i, [128, 8, 16], F32)) for i in range(4)]] * 2
        vsb2 = two("vsb", [128, 8, 129], BF16)
        iksb2 = two("iksb", [128, 1, 64], F32)
        ikbf2 = two("ikbf", [128, 64], BF16)
        kTs2 = [st.enter_context(nc.sbuf_tensor("kTs", [128, 8, 128], BF16))] * 2
        ikTs2 = two("ikTs", [64, 128], BF16)
        pm = [st.enter_context(nc.psum_tensor("pm%d" % i, [128, 512], F32)) for i in range(3)]
        pk = st.enter_context(nc.psum_tensor("pk", [128, 8, 128], BF16))
        for s_ in range(2):
            P.op("pool", lambda e, s_=s_: e.memset(vsb2[s_][:], 1.0), writes=["vsb%d" % s_])
        norm_load(K, P, T, K.x_full, 0)

        def norm_tile(blk, ti):
            tt_ = blk * 4 + ti
            if tt_ + 1 < 32:
                norm_load(K, P, T, K.x_full, tt_ + 1)
            norm_block(K, P, T, K.x_full, tt_, G1, SH1, ident, hT[blk % 2], ti, load=False, hname=("hT", blk % 2))

        def store_hT(blk):
            hs = blk % 2
            hkeys = [(("hT", hs), ti, half) for ti in range(4) for half in range(2)]
            P.dma("sp", K.hT_d.rearrange("k p t -> p k t")[:, :, blk * 512:(blk + 1) * 512], hT[hs][:],
                  reads=hkeys, writes=[("hT_d", blk)])

        def bufs(t):
            u = t % 2
            return (str(u), ksb2[u], kbf2[u], sq2[u], ssum2[u], rtmp2[u], vsb2[u], iksb2[u], ikbf2[u], kTs2[u], ikTs2[u])

        def mm_tile(blk, ti):
            t = blk * 4 + ti
            hs = blk % 2
            hk = [(("hT", hs), ti, 0), (("hT", hs), ti, 1)]
            us, ksb, kbf, sq, ssum, rtmp, vsb, iksb, ikbf, kTs, ikTs = bufs(t)
            for gi, (c0, n, wkeys) in enumerate([(0, 512, wk_k), (512, 512, wk_k), (1024, 512, wk_v),
                                                 (1536, 512, wk_v), (2048, 64, wk_i)]):
                pb = pm[gi % 3]
                for k in range(16):
                    P.op("pe", lambda e, pb=pb, k=k, c0=c0, n=n, ti=ti, hs=hs: e.matmul(
                        pb[:, 0:n], lhsT=hT[hs][:, k, ti * 128:(ti + 1) * 128], rhs=W[:, k, c0:c0 + n],
                        start=(k == 0), stop=(k == 15)), reads=hk + wkeys, writes=[("pm", gi % 3)])
                if gi < 2:
                    P.op("act", lambda e, pb=pb, gi=gi, ksb=ksb: e.copy(out=ksb[:, gi * 4:(gi + 1) * 4, :], in_=pb[:, 0:512]),
                         reads=[("pm", gi % 3)], writes=["ksb" + us])
                elif gi < 4:
                    g2 = gi - 2
                    P.op("act", lambda e, pb=pb, g2=g2, vsb=vsb: e.copy(out=vsb[:, g2 * 4:(g2 + 1) * 4, 0:128], in_=pb[:, 0:512]),
                         reads=[("pm", gi % 3)], writes=["vsb" + us])
                else:
                    P.op("act", lambda e, pb=pb, iksb=iksb: e.copy(out=iksb[:, 0, :], in_=pb[:, 0:64]),
                         reads=[("pm", gi % 3)], writes=["iksb" + us])
            P.dma("sp", K.v_d[t * 128:(t + 1) * 128, :], vsb[:].rearrange("p h d -> p (h d)"),
                  reads=["vsb" + us], writes=[("v_d", t)])

        def post1(blk, ti):
            t = blk * 4 + ti
            us, ksb, kbf, sq, ssum, rtmp, vsb, iksb, ikbf, kTs, ikTs = bufs(t)
            head_rmsnorm(P, ksb[:], gain, sq, ssum, ["ksb" + us], "K" + us, gk="gainK", sqk="Ksq")
            apply_rope(P, "dve", ksb[:], rt["cosa"], rt["sina"], t, 16, rtmp, ["ksb" + us], "rK")
            P.op("act", lambda e, kbf=kbf, ksb=ksb: e.copy(out=kbf[:], in_=ksb[:]), reads=["ksb" + us], writes=["kbf" + us])
            apply_rope(P, "pool", iksb[:], rt["cosi"], rt["sini"], t, 8, rtmp, ["iksb" + us], "rI")
            P.op("act", lambda e, ikbf=ikbf, iksb=iksb: e.copy(out=ikbf[:], in_=iksb[:, 0, :]), reads=["iksb" + us], writes=["ikbf" + us])

        def post2(blk, ti):
            t = blk * 4 + ti
            us, ksb, kbf, sq, ssum, rtmp, vsb, iksb, ikbf, kTs, ikTs = bufs(t)
            for h in range(8):
                P.op("pe", lambda e, h=h, kbf=kbf: e.transpose(out=pk[:, h, :], in_=kbf[:, h, :], identity=ident[:]),
                     reads=["kbf" + us, "ident"], writes=["pk"])
            P.op("dve", lambda e, kTs=kTs: e.tensor_copy(out=kTs[:], in_=pk[:]), reads=["pk"], writes=["kTs"])
            P.dma("sp", K.kT_d.rearrange("h p t -> p h t")[:, :, t * 128:(t + 1) * 128], kTs[:],
                  reads=["kTs"], writes=[("kT_d", t)])
            P.op("pe", lambda e, ikbf=ikbf: e.transpose(out=pk[0:64, 0, :], in_=ikbf[:], identity=ident[:]),
                 reads=["ikbf" + us, "ident"], writes=["pk"])
            P.op("dve", lambda e, ikTs=ikTs: e.tensor_copy(out=ikTs[:], in_=pk[0:64, 0, :]), reads=["pk"], writes=["ikTs" + us])
            P.dma("sp", K.ikT_d[:, t * 128:(t + 1) * 128], ikTs[:], reads=["ikTs" + us], writes=[("ikT_d", t)])

        for ti in range(4):
            norm_tile(0, ti)
        store_hT(0)
        prev = None
        for blk in range(8):
            for ti in range(4):
                mm_tile(blk, ti)
                if blk + 1 < 8:
                    norm_tile(blk + 1, ti)
                post1(blk, ti)
                if prev is not None:
                    post2(*prev)
                prev = (blk, ti)
            if blk + 1 < 8:
                store_hT(blk + 1)
        post2(*prev)
        P.flush()

RW0 = 4176
NRW = 1216
RW_GROUPS = [(i * 128, 128) for i in range(6)] + [(768, 96), (864, 96), (960, 128), (1088, 128)]


def phase1b_rwkv_proj(K):
    nc, P = K.nc, K.P
    with contextlib.ExitStack() as st:
        W = st.enter_context(nc.sbuf_tensor("Wr", [128, 16, NRW], BF16))
        stg = [st.enter_context(nc.sbuf_tensor("wstgb%d" % i, [128, 4, 512], F32)) for i in range(2)]
        hT = [st.enter_context(nc.sbuf_tensor("hTb%d" % i, [128, 16, 512], BF16)) for i in range(2)]
        ost = [st.enter_context(nc.sbuf_tensor("ost%d" % i, [128, 512], F32)) for i in range(4)]
        pm = [st.enter_context(nc.psum_tensor("pmb%d" % i, [128, 512], F32)) for i in range(4)]
        wkeys = load_weight_bf16(K, P, stg, W, 0, K.w_in_rw, NRW, "Wr")
        cnt = 0
        for blk in range(8):
            hs = blk % 2
            P.dma("sp", hT[hs][:], K.hT_d.rearrange("k p t -> p k t")[:, :, blk * 512:(blk + 1) * 512],
                  writes=[("hTb", hs)])
            for (r0, m) in RW_GROUPS:
                s4 = cnt % 4
                cnt += 1
                for k in range(16):
                    P.op("pe", lambda e, k=k, r0=r0, m=m, hs=hs, s4=s4: e.matmul(
                        pm[s4][0:m, :], lhsT=W[:, k, r0:r0 + m], rhs=hT[hs][:, k, :],
                        start=(k == 0), stop=(k == 15)), reads=[("hTb", hs)] + wkeys, writes=[("pmb", s4)])
                if cnt % 2 == 0:
                    P.op("act", lambda e, m=m, s4=s4: e.copy(out=ost[s4][0:m, :], in_=pm[s4][0:m, :]),
                         reads=[("pmb", s4)], writes=[("ost", s4)])
                else:
                    P.op("dve", lambda e, m=m, s4=s4: e.tensor_copy(out=ost[s4][0:m, :], in_=pm[s4][0:m, :]),
                         reads=[("pmb", s4)], writes=[("ost", s4)])
                P.dma("sp", K.yT_d[r0:r0 + m, blk * 512:(blk + 1) * 512], ost[s4][0:m, :],
                      reads=[("ost", s4)], writes=[("yT_d", r0, blk)])
        P.flush()


def phase2_own_proj(K):
    nc, P = K.nc, K.P
    with contextlib.ExitStack() as st:
        ident = st.enter_context(nc.sbuf_tensor("ident2", [128, 128], BF16))
        make_ident(K, P, ident)
        G1, SH1 = load_G_SH(K, P, st, 0, 1, K.norm1_g, "p2")
        T = norm_tiles_alloc(K, st, "p2")
        hT = [st.enter_context(nc.sbuf_tensor("hTo%d" % i, [128, 16, 512], BF16)) for i in range(2)]
        W = st.enter_context(nc.sbuf_tensor("Wq", [128, 16, 2064], BF16))
        stg = [st.enter_context(nc.sbuf_tensor("wstgq%d" % i, [128, 4, 512], F32)) for i in range(2)]
        wk_q = load_weight_bf16(K, P, stg, W, 0, K.w_in[:, 0:1024], 1024, "Wq")
        wk_iq = load_weight_bf16(K, P, stg, W, 1024, K.w_in[:, 3072:4096], 1024, "Wiq")
        wk_iw = load_weight_bf16(K, P, stg, W, 2048, K.w_in[:, 4160:4176], 16, "Wiw")
        rt = rope_tables(K, P, st, K.pos_own, 8, K.invf_att, K.invf_idx, "ro")
        gain = st.enter_context(nc.sbuf_tensor("qgain", [128, 128], F32))
        P.dma("sp", gain[:], bcast_rows(K.q_norm_g, 128), writes=["gainQ"])
        qsb = st.enter_context(nc.sbuf_tensor("qsb", [128, 8, 128], F32))
        qbf = st.enter_context(nc.sbuf_tensor("qbf", [128, 8, 128], BF16))
        sq = st.enter_context(nc.sbuf_tensor("sq2", [128, 8, 128], F32))
        ssum = st.enter_context(nc.sbuf_tensor("ssum2", [128, 32], F32))
        rtmp = [st.enter_context(nc.sbuf_tensor("rtmpq%d" % i, [128, 16, 16], F32)) for i in range(4)]
        iqsb = st.enter_context(nc.sbuf_tensor("iqsb", [128, 16, 64], F32))
        iqbf = st.enter_context(nc.sbuf_tensor("iqbf", [128, 16, 64], BF16))
        iwsb = st.enter_context(nc.sbuf_tensor("iwsb", [128, 16], F32))
        qTs = st.enter_context(nc.sbuf_tensor("qTs", [128, 8, 128], BF16))
        iqTs = st.enter_context(nc.sbuf_tensor("iqTs", [64, 128, 16], BF16))
        pm = [st.enter_context(nc.psum_tensor("pmq%d" % i, [128, 512], F32)) for i in range(3)]
        pk = st.enter_context(nc.psum_tensor("pkq", [128, 8, 128], BF16))
        for blk in range(2):
            hs = blk % 2
            for ti in range(4):
                norm_block(K, P, T, K.x_own, blk * 4 + ti, G1, SH1, ident, hT[hs], ti)
            for ti in range(4):
                t = blk * 4 + ti
                hk = [("hT", ti, 0), ("hT", ti, 1)]
                for gi, (c0, n, wkeys) in enumerate([(0, 512, wk_q), (512, 512, wk_q), (1024, 512, wk_iq),
                                                     (1536, 512, wk_iq), (2048, 16, wk_iw)]):
                    pb = pm[gi % 3]
                    for k in range(16):
                        P.op("pe", lambda e, pb=pb, k=k, c0=c0, n=n, ti=ti, hs=hs: e.matmul(
                            pb[:, 0:n], lhsT=hT[hs][:, k, ti * 128:(ti + 1) * 128], rhs=W[:, k, c0:c0 + n],
                            start=(k == 0), stop=(k == 15)), reads=hk + wkeys, writes=[("pmq", gi % 3)])
                    if gi < 2:
                        P.op("act", lambda e, pb=pb, gi=gi: e.copy(out=qsb[:, gi * 4:(gi + 1) * 4, :], in_=pb[:, 0:512]),
                             reads=[("pmq", gi % 3)], writes=["qsb"])
                    elif gi < 4:
                        g2 = gi - 2
                        P.op("act", lambda e, pb=pb, g2=g2: e.copy(out=iqsb[:, g2 * 8:(g2 + 1) * 8, :], in_=pb[:, 0:512]),
                             reads=[("pmq", gi % 3)], writes=["iqsb"])
                    else:
                        P.op("act", lambda e, pb=pb: e.activation(out=iwsb[:], in_=pb[:, 0:16], func=AF.Copy, scale=0.25),
                             reads=[("pmq", gi % 3)], writes=["iwsb"])
                P.dma("sp", K.iw_d[t * 128:(t + 1) * 128, :], iwsb[:], reads=["iwsb"], writes=[("iw_d", t)])
                head_rmsnorm(P, qsb[:], gain, sq, ssum, ["qsb"], "Q")
                apply_rope(P, "dve", qsb[:], rt["cosa"], rt["sina"], t, 16, rtmp, ["qsb"], "rQ")
                P.op("act", lambda e: e.copy(out=qbf[:], in_=qsb[:]), reads=["qsb"], writes=["qbf"])
                for h in range(8):
                    P.op("pe", lambda e, h=h: e.transpose(out=pk[:, h, :], in_=qbf[:, h, :], identity=ident[:]),
                         reads=["qbf", "ident"], writes=["pkq"])
                P.op("dve", lambda e: e.tensor_copy(out=qTs[:], in_=pk[:]), reads=["pkq"], writes=["qTs"])
                P.dma("sp", K.qT_d.rearrange("h p t -> p h t")[:, :, t * 128:(t + 1) * 128], qTs[:],
                      reads=["qTs"], writes=[("qT_d", t)])
                apply_rope(P, "pool", iqsb[:], rt["cosi"], rt["sini"], t, 8, rtmp, ["iqsb"], "rIQ")
                P.op("act", lambda e: e.activation(out=iqbf[:], in_=iqsb[:], func=AF.Copy, scale=0.125),
                     reads=["iqsb"], writes=["iqbf"])
                for half in range(2):
                    for hh in range(8):
                        h = half * 8 + hh
                        P.op("pe", lambda e, h=h, hh=hh: e.transpose(out=pk[0:64, hh, :], in_=iqbf[:, h, :],
                                                                      identity=ident[:]),
                             reads=["iqbf", "ident"], writes=["pkq"])
                    P.op("dve", lambda e, half=half: e.tensor_copy(
                        out=iqTs[:, :, half * 8:(half + 1) * 8].rearrange("p t h -> p h t"), in_=pk[0:64, :, :]),
                         reads=["pkq"], writes=["iqTs"])
                P.dma("sp", K.iqT_d[:, t * 128:(t + 1) * 128, :], iqTs[:], reads=["iqTs"], writes=[("iqT_d", t)])
        P.flush()


NIT = 24
SLOT_NK = [4, 8, 12, 16, 20, 24, 28, 32]


def phase3_attention(K):
    nc, P = K.nc, K.P
    with contextlib.ExitStack() as st:
        def sb(name, shape, dt):
            return st.enter_context(nc.sbuf_tensor(name, shape, dt))
        ident = sb("ident3", [128, 128], BF16)
        identf = sb("identf3", [128, 128], F32)
        make_ident(K, P, ident)
        P.op("dve", lambda e: e.tensor_copy(out=identf[:], in_=ident[:]), reads=["ident"], writes=["identf"])
        kT = sb("kTall", [128, 8, S], BF16)
        V = sb("Vall", [128, 32, 1032], BF16)
        ikT = sb("ikTall", [64, S], BF16)
        for h in range(8):
            P.dma("sp", kT[:, h, :], K.kT_d[h], writes=[("kT", h)])
        for q4 in range(4):
            P.dma("sp", V[:, q4 * 8:(q4 + 1) * 8, :],
                  K.v_d.rearrange("(t p) c -> p t c", p=128)[:, q4 * 8:(q4 + 1) * 8, :], writes=[("V", q4)])
        P.dma("sp", ikT[:], K.ikT_d, writes=["ikT"])
        kTk = [("kT", h) for h in range(8)]
        Vk = [("V", q4) for q4 in range(4)]
        Sel = sb("Sel", [128, 16, 128], BF16)
        pidx = sb("pidx", [128, 1], I32)
        pidf = sb("pidf", [128, 1], F32)
        score = sb("score", [128, S], F32)
        self_ = score[:, 0:2048].rearrange("p (g t) -> p g t", g=16)
        sk4 = [("score", q) for q in range(4)]
        P.op("pool", lambda e: e.iota(self_, pattern=[[-8, 16], [1, 128]], base=0, channel_multiplier=0, allow_small_or_imprecise_dtypes=True), writes=sk4)
        P.op("pool", lambda e: e.iota(pidx[:], pattern=[[0, 1]], base=0, channel_multiplier=1), writes=["pidx"])
        P.op("dve", lambda e: e.tensor_scalar(out=pidx[:], in0=pidx[:], scalar1=4, scalar2=None,
                                               op0=ALU.arith_shift_right), reads=["pidx"], writes=["pidx"])
        P.op("dve", lambda e: e.tensor_copy(out=pidf[:], in_=pidx[:]), reads=["pidx"], writes=["pidf"])
        P.op("dve", lambda e: e.tensor_scalar(out=Sel[:], in0=self_, scalar1=pidf[:, 0:1], scalar2=None,
                                               op0=ALU.is_equal), reads=sk4 + ["pidf"], writes=["Sel"])
        kposi = sb("kposi", [128, 512], I32)
        kposf = sb("kposf", [128, 512], F32)
        qpos = sb("qpos", [128, 8], F32)
        P.dma("sp", qpos[:], K.qpos_own, writes=["qpos"])
        iwg = sb("iwg", [128, 128], F32)
        wcol = sb("wcol", [128, 128], F32)
        P.dma("sp", iwg[:], K.iw_d.rearrange("(g t) h -> g (t h)", t=8), writes=["iwg"])
        A = [st.enter_context(nc.psum_tensor("A%d" % i, [128, 512], F32)) for i in range(2)]
        B = [st.enter_context(nc.psum_tensor("B%d" % i, [128, 512], F32)) for i in range(2)]
        C = st.enter_context(nc.psum_tensor("C3", [128, 8, 128], BF16))
        P.op("pe", lambda e: e.transpose(out=A[0][:, 0:128], in_=iwg[:], identity=identf[:]),
             reads=["iwg", "identf"], writes=[("A", 0)])
        P.op("dve", lambda e: e.tensor_copy(out=wcol[:], in_=A[0][:, 0:128]), reads=[("A", 0)], writes=["wcol"])
        mask01 = sb("mask01", [128, S], BF16)
        maskT = sb("maskT", [128, 32, 128], BF16)
        R = [sb("R%d" % i, [128, 512], BF16) for i in range(2)]
        pexp = [sb("pexp%d" % i, [128, 512], BF16) for i in range(2)]
        pmk = [sb("pmk%d" % i, [128, 512], BF16) for i in range(2)]
        iqTs = sb("iqTs3", [64, 128, 16], BF16)
        qTs = sb("qTs3", [128, 8, 128], BF16)
        att = sb("att", [128, 8, 128], BF16)
        attTs = sb("attTs", [128, 8, 128], BF16)
        bias = sb("cbias", [128, 512], F32)
        c2 = sb("c2", [128, NIT], F32)
        steps = sb("steps", [128, NIT], F32)
        sm = sb("sm3", [128, 8], F32)
        for k in range(NIT):
            P.op("pool", lambda e, k=k: e.memset(c2[:, k:k + 1], float(2.0 ** -(k + 1))), writes=["c2"])
        for i in range(8):
            nk = SLOT_NK[i]
            nb = nk // 4
            L = nk * 128
            P.dma("sp", iqTs[:], K.iqT_d[:, i * 128:(i + 1) * 128, :], writes=["iqTs"])
            P.dma("sp", qTs[:], K.qT_d.rearrange("h p t -> p h t")[:, :, i * 128:(i + 1) * 128], writes=["qTs"])
            isteps = [(sbk, g) for sbk in range(nb) for g in range(16)]

            def dots(si):
                sbk, g = isteps[si]
                a = si % 2
                lhsT = iqTs[:, g * 8:(g + 1) * 8, :].rearrange("p t h -> p (t h)")
                P.op("pe", lambda e, a=a, lhsT=lhsT, sbk=sbk: e.matmul(
                    A[a][:, :], lhsT=lhsT, rhs=ikT[:, sbk * 512:(sbk + 1) * 512], start=True, stop=True),
                    reads=["iqTs", "ikT"], writes=[("A", a)])
            dots(0)
            for si, (sbk, g) in enumerate(isteps):
                a = si % 2
                bsl = sbk % 2
                if si + 1 < len(isteps):
                    dots(si + 1)
                G = i * 16 + g
                P.op("dve", lambda e, a=a, G=G: e.tensor_scalar(
                    out=R[a][:], in0=A[a][:, :], scalar1=0.0, scalar2=wcol[:, G:G + 1],
                    op0=ALU.max, op1=ALU.mult), reads=[("A", a), "wcol"], writes=[("R", a)])
                P.op("pe", lambda e, a=a, g=g, bsl=bsl: e.matmul(
                    B[bsl][:, :], lhsT=Sel[:, g, :], rhs=R[a][:], start=(g == 0), stop=(g == 15)),
                    reads=[("R", a), "Sel"], writes=[("B", bsl)])
                if g == 15:
                    P.op("act", lambda e, bsl=bsl, sbk=sbk: e.copy(out=score[:, sbk * 512:(sbk + 1) * 512], in_=B[bsl][:, :]),
                         reads=[("B", bsl)], writes=[("score", sbk)])
            sck = [("score", sbk) for sbk in range(nb)]
            P.op("dve", lambda e, L=L: e.tensor_reduce(out=sm[:, 0:1], in_=score[:, 0:L], axis=AX.X, op=ALU.max,
                                                        apply_absolute_value=True), reads=sck, writes=["sm0"])
            P.op("pool", lambda e, nb=nb: e.iota(kposi[:], pattern=[[1, 512]], base=(nb - 1) * 512, channel_multiplier=0),
                 writes=["kposi"])
            P.op("dve", lambda e: e.tensor_copy(out=kposf[:], in_=kposi[:]), reads=["kposi"], writes=["kposf"])
            P.op("dve", lambda e, i=i: e.tensor_scalar(out=bias[:], in0=kposf[:], scalar1=qpos[:, i:i + 1],
                                                        scalar2=-1e30, op0=ALU.is_gt, op1=ALU.mult),
                 reads=["kposf", "qpos"], writes=["bias"])
            P.op("dve", lambda e, nb=nb: e.tensor_tensor(out=score[:, (nb - 1) * 512:nb * 512],
                                                          in0=score[:, (nb - 1) * 512:nb * 512], in1=bias[:], op=ALU.add),
                 reads=["bias", ("score", nb - 1), "sm0"], writes=[("score", nb - 1)])
            P.op("dve", lambda e: e.tensor_scalar(out=sm[:, 1:2], in0=sm[:, 0:1], scalar1=-1.0, scalar2=-1.0,
                                                   op0=ALU.mult, op1=ALU.add), reads=["sm0"], writes=["lo"])
            P.op("dve", lambda e: e.tensor_scalar(out=sm[:, 5:6], in0=sm[:, 0:1], scalar1=2.0, scalar2=2.0,
                                                   op0=ALU.mult, op1=ALU.add), reads=["sm0"], writes=["d0"])
            P.op("dve", lambda e: e.tensor_scalar(out=steps[:], in0=c2[:], scalar1=sm[:, 5:6], scalar2=None,
                                                   op0=ALU.mult), reads=["d0", "c2"], writes=["steps"])
            for k in range(NIT):
                P.op("dve", lambda e, k=k: e.tensor_tensor(out=sm[:, 2:3], in0=sm[:, 1:2], in1=steps[:, k:k + 1],
                                                            op=ALU.add), reads=["lo", "steps"], writes=["mid"])
                P.op("dve", lambda e, L=L: e.tensor_scalar(out=mask01[:, 0:L], in0=score[:, 0:L], scalar1=sm[:, 2:3],
                                                            scalar2=None, op0=ALU.is_ge, op1=ALU.add,
                                                            accum_out=sm[:, 3:4]),
                     reads=sck + ["mid"], writes=["mask01", "cnt"])
                P.op("dve", lambda e, k=k: e.scalar_tensor_tensor(out=sm[:, 4:5], in0=sm[:, 3:4], scalar=255.5,
                                                                   in1=steps[:, k:k + 1], op0=ALU.is_ge, op1=ALU.mult),
                     reads=["cnt", "steps"], writes=["inc"])
                P.op("dve", lambda e: e.tensor_tensor(out=sm[:, 1:2], in0=sm[:, 1:2], in1=sm[:, 4:5], op=ALU.add),
                     reads=["lo", "inc"], writes=["lo"])
            P.op("dve", lambda e, L=L: e.tensor_scalar(out=mask01[:, 0:L], in0=score[:, 0:L], scalar1=sm[:, 1:2],
                                                        scalar2=None, op0=ALU.is_ge), reads=sck + ["lo"], writes=["mask01"])
            for kt in range(nk):
                P.op("pe", lambda e, kt=kt: e.transpose(out=C[:, kt % 8, :], in_=mask01[:, kt * 128:(kt + 1) * 128],
                                                         identity=ident[:]), reads=["mask01", "ident"], writes=["C"])
                if kt % 8 == 7 or kt == nk - 1:
                    k0 = (kt // 8) * 8
                    n8 = kt - k0 + 1
                    P.op("act", lambda e, k0=k0, n8=n8: e.copy(out=maskT[:, k0:k0 + n8, :], in_=C[:, 0:n8, :]),
                         reads=["C"], writes=[("maskT", k0 // 8)])
            mk = [("maskT", q) for q in range((nk + 7) // 8)]
            asteps = [(h, kg) for h in range(8) for kg in range(nb)]

            def qk(si):
                h, kg = asteps[si]
                a = si % 2
                for j4 in range(4):
                    kt = kg * 4 + j4
                    P.op("pe", lambda e, a=a, j4=j4, kt=kt, h=h: e.matmul(
                        A[a][:, j4 * 128:(j4 + 1) * 128], lhsT=kT[:, h, kt * 128:(kt + 1) * 128], rhs=qTs[:, h, :],
                        start=True, stop=True), reads=kTk + ["qTs"], writes=[("A", a)])
            qk(0)
            for si, (h, kg) in enumerate(asteps):
                a = si % 2
                bsl = h % 2
                if si + 1 < len(asteps):
                    qk(si + 1)
                P.op("act", lambda e, a=a: e.activation(out=pexp[a][:], in_=A[a][:, :], func=AF.Exp,
                                                         scale=float(128 ** -0.5)),
                     reads=[("A", a)], writes=[("pexp", a)])
                P.op("dve", lambda e, a=a, kg=kg: e.tensor_tensor(
                    out=pmk[a][:], in0=pexp[a][:], in1=maskT[:, kg * 4:(kg + 1) * 4, :].rearrange("p a t -> p (a t)"),
                    op=ALU.mult), reads=[("pexp", a)] + mk, writes=[("pmk", a)])
                for j4 in range(4):
                    kt = kg * 4 + j4
                    P.op("pe", lambda e, a=a, j4=j4, kt=kt, h=h, bsl=bsl, kg=kg, nb=nb: e.matmul(
                        B[bsl][:, 0:129], lhsT=pmk[a][:, j4 * 128:(j4 + 1) * 128], rhs=V[:, kt, h * 129:(h + 1) * 129],
                        start=(kg == 0 and j4 == 0), stop=(kg == nb - 1 and j4 == 3)),
                        reads=[("pmk", a)] + Vk, writes=[("B", bsl)])
                if kg == nb - 1:
                    P.op("dve", lambda e, bsl=bsl: e.reciprocal(out=sm[:, 6:7], in_=B[bsl][:, 128:129]),
                         reads=[("B", bsl)], writes=["rcp"])
                    P.op("dve", lambda e, bsl=bsl, h=h: e.tensor_scalar(out=att[:, h, :], in0=B[bsl][:, 0:128],
                                                                         scalar1=sm[:, 6:7], scalar2=None, op0=ALU.mult),
                         reads=[("B", bsl), "rcp"], writes=["att"])
            for h in range(8):
                P.op("pe", lambda e, h=h: e.transpose(out=C[:, h, :], in_=att[:, h, :], identity=ident[:]),
                     reads=["att", "ident"], writes=["C"])
            P.op("act", lambda e: e.copy(out=attTs[:], in_=C[:]), reads=["C"], writes=["attTs"])
            P.dma("sp", K.attT_d.rearrange("h p t -> p h t")[:, :, i * 128:(i + 1) * 128], attTs[:],
                  reads=["attTs"], writes=[("attT_d", i)])
        P.flush()

RD = BF16
NCH = 64


def tok_shift(P, dst, raw, tmp, mu_ap, rk_raw, k_tmp, k_dst, n=128):
    P.op("pool", lambda e: e.tensor_tensor(out=tmp[0:n, 1:S], in0=raw[0:n, 0:S - 1], in1=raw[0:n, 1:S], op=ALU.subtract),
         reads=[rk_raw], writes=[k_tmp])
    P.op("pool", lambda e: e.tensor_scalar(out=tmp[0:n, 0:1], in0=raw[0:n, 0:1], scalar1=-1.0, scalar2=0.0,
                                            op0=ALU.mult, op1=ALU.add), reads=[rk_raw, k_tmp], writes=[k_tmp])
    P.op("dve", lambda e: e.scalar_tensor_tensor(out=dst[0:n, :], in0=tmp[0:n, :], scalar=mu_ap, in1=raw[0:n, :],
                                                  op0=ALU.mult, op1=ALU.add), reads=[rk_raw, k_tmp], writes=[k_dst])


def phase4b_rwkv_prep(K, cts=range(2)):
    nc, P = K.nc, K.P
    with contextlib.ExitStack() as st:
        def sb(name, shape, dt):
            return st.enter_context(nc.sbuf_tensor(name, shape, dt))
        txw = sb("txw", [96, S], BF16)
        xap = sb("xap", [96, S], BF16)
        sxg = sb("sxg", [128, 2, S], BF16)
        M01 = sb("M01", [128, S], BF16)
        wup = sb("wup", [96, 256], BF16)
        aup = sb("aup", [96, 256], BF16)
        gup = sb("gup", [128, 2, 256], BF16)
        wst = sb("wst4", [128, 2, 256], F32)
        bones = sb("bones", [128, 128], BF16)
        prm = sb("prm", [128, 12, 2], F32)
        mul = sb("mul", [128, 4], F32)
        PT = sb("PT", [128, S], F32)
        KK = sb("KK", [128, S], F32)
        KP = sb("KP", [128, S], F32)
        CL = sb("CL", [128, S], F32)
        RP = sb("RP", [128, S], BF16)
        VP = sb("VP", [128, S], BF16)
        AA = sb("AA", [128, S], BF16)
        K2 = sb("K2", [128, S], BF16)
        SQb = sb("SQb", [128, S], BF16)
        OUT = [sb("OUT%d" % i, [128, S], BF16) for i in range(2)]
        PCt = sb("PCt", [128, NCH], F32)
        ps = [st.enter_context(nc.psum_tensor("ps4_%d" % i, [128, 512], F32)) for i in range(4)]
        for i, ap in enumerate(K.rw_prm):
            P.dma("sp", prm[:, i, :], ap, writes=[("prm", i)])
        prk = [("prm", i) for i in range(10)]
        P.op("dve", lambda e: e.tensor_scalar(out=prm[:, 10, :], in0=prm[:, 6, :], scalar1=-1.0, scalar2=1.0,
                                               op0=ALU.mult, op1=ALU.add), reads=prk, writes=[("prm", 10)])
        prk = prk + [("prm", 10)]
        P.dma("sp", mul[:], K.rw_mul, writes=["mul"])
        P.op("pool", lambda e: e.memset(bones[:], 0.0), writes=["bones"])
        P.op("pool", lambda e: e.memset(bones[0:64, 0:64], 1.0), reads=["bones"], writes=["bones"])
        P.op("pool", lambda e: e.memset(bones[64:128, 64:128], 1.0), reads=["bones"], writes=["bones"])
        P.op("pool", lambda e: e.iota(PT[:].rearrange("p (c t) -> p c t", t=64), pattern=[[0, NCH], [1, 64]], base=0,
                                      channel_multiplier=0, allow_small_or_imprecise_dtypes=True), writes=["PT"])
        P.op("dve", lambda e: e.tensor_scalar(out=M01[:], in0=PT[:], scalar1=0.5, scalar2=None, op0=ALU.is_gt),
             reads=["PT"], writes=["M01"])
        P.dma("sp", wst[0:96, 0, :], K.rw_w_up, writes=["wst"])
        P.op("act", lambda e: e.copy(out=wup[:], in_=wst[0:96, 0, :]), reads=["wst"], writes=["wup"])
        P.dma("sp", wst[0:96, 1, :], K.rw_a_up, reads=[], writes=["wst1"])
        P.op("act", lambda e: e.copy(out=aup[:], in_=wst[0:96, 1, :]), reads=["wst1"], writes=["aup"])
        P.dma("sp", wst[:, :, :], K.rw_g_up.rearrange("(c p) n -> p c n", p=128), reads=[], writes=["wst", "wst1"])
        P.op("act", lambda e: e.copy(out=gup[:], in_=wst[:]), reads=["wst", "wst1"], writes=["gup"])
        for (r0, n, mcol, func, dst, kd) in ((768, 96, 0, AF.Tanh, txw[:, :], "txw"), (864, 96, 1, AF.Copy, xap[:, :], "xap"),
                                             (960, 128, 2, AF.Sigmoid, sxg[:, 0, :], "sxg0"),
                                             (1088, 128, 3, AF.Sigmoid, sxg[:, 1, :], "sxg1")):
            P.dma("sp", PT[0:n, :], K.yT_d[r0:r0 + n, :], writes=["PT"])
            tok_shift(P, KP, PT, KK, mul[0:n, mcol:mcol + 1], "PT", "KK", "KP", n=n)
            P.op("act", lambda e, n=n, func=func, dst=dst: e.activation(out=dst, in_=KP[0:n, :], func=func),
                 reads=["KP"], writes=[kd])
        lk = ["txw", "xap", "sxg0", "sxg1"]
        oc = 0
        for ct in cts:
            c0 = ct * 128
            P.dma("sp", PT[:], K.yT_d[c0:c0 + 128, :], writes=["PT"])
            tok_shift(P, RP, PT, KK, prm[:, 0, ct:ct + 1], "PT", "KK", "RP")
            P.dma("sp", PT[:], K.yT_d[256 + c0:256 + c0 + 128, :], writes=["PT"])
            tok_shift(P, KP, PT, KK, prm[:, 1, ct:ct + 1], "PT", "KK", "KP")
            P.dma("sp", PT[:], K.yT_d[512 + c0:512 + c0 + 128, :], writes=["PT"])
            tok_shift(P, VP, PT, KK, prm[:, 2, ct:ct + 1], "PT", "KK", "VP")
            P.dma("sp", K.vb_d[c0:c0 + 128, :], VP[:], reads=["VP"], writes=[("vb_d", ct)])
            for blk in range(8):
                bs = slice(blk * 512, (blk + 1) * 512)
                p0, p1, p2 = ps[0], ps[1], ps[2]
                P.op("pe", lambda e, bs=bs, c0=c0: e.matmul(ps[0][:, :], lhsT=wup[:, c0:c0 + 128], rhs=txw[:, bs],
                                                             start=True, stop=True), reads=["wup", "txw"], writes=[("ps4", 0)])
                P.op("act", lambda e, bs=bs, ct=ct: e.activation(out=CL[:, bs], in_=ps[0][:, :], func=AF.Sigmoid,
                                                                  bias=prm[:, 3, ct:ct + 1]),
                     reads=[("ps4", 0)] + prk, writes=["CL"])
                P.op("pe", lambda e, bs=bs, c0=c0: e.matmul(ps[1][:, :], lhsT=aup[:, c0:c0 + 128], rhs=xap[:, bs],
                                                             start=True, stop=True), reads=["aup", "xap"], writes=[("ps4", 1)])
                P.op("act", lambda e, bs=bs, ct=ct: e.activation(out=AA[:, bs], in_=ps[1][:, :], func=AF.Sigmoid,
                                                                  bias=prm[:, 4, ct:ct + 1]),
                     reads=[("ps4", 1)] + prk, writes=["AA"])
                for cc in range(2):
                    P.op("pe", lambda e, bs=bs, c0=c0, cc=cc: e.matmul(ps[2][:, :], lhsT=gup[:, cc, c0:c0 + 128],
                                                                       rhs=sxg[:, cc, bs], start=(cc == 0), stop=(cc == 1)),
                         reads=["gup", "sxg0", "sxg1"], writes=[("ps4", 2)])
                o = OUT[oc % 2]
                P.op("dve", lambda e, bs=bs, o=o: e.tensor_copy(out=o[:, bs], in_=ps[2][:, :]),
                     reads=[("ps4", 2)], writes=[("OUT", oc % 2)])
            P.dma("sp", K.G_d[c0:c0 + 128, :], OUT[oc % 2][:], reads=[("OUT", oc % 2)], writes=[("G_d", ct)])
            oc += 1
            P.op("dve", lambda e: e.tensor_scalar(out=CL[:], in0=CL[:], scalar1=-0.6065306597126334, scalar2=None,
                                                   op0=ALU.mult), reads=["CL"], writes=["CL"])
            P.op("dve", lambda e, ct=ct: e.tensor_scalar(out=KK[:], in0=KP[:], scalar1=prm[:, 5, ct:ct + 1], scalar2=None,
                                                          op0=ALU.mult), reads=["KP"] + prk, writes=["KK"])
            P.op("act", lambda e: e.activation(out=SQb[:], in_=KK[:], func=AF.Square), reads=["KK"], writes=["SQb"])
            for blk in range(8):
                bs = slice(blk * 512, (blk + 1) * 512)
                P.op("pe", lambda e, bs=bs: e.matmul(ps[3][:, :], lhsT=bones[:], rhs=SQb[:, bs], start=True, stop=True),
                     reads=["bones", "SQb"], writes=[("ps4", 3)])
                P.op("act", lambda e, bs=bs: e.activation(out=PT[:, bs], in_=ps[3][:, :], func=AF.Sqrt),
                     reads=[("ps4", 3)], writes=["PT"])
            P.op("dve", lambda e: e.tensor_scalar(out=PT[:], in0=PT[:], scalar1=1e-12, scalar2=None, op0=ALU.max),
                 reads=["PT"], writes=["PT"])
            P.op("dve", lambda e: e.reciprocal(out=PT[:], in_=PT[:]), reads=["PT"], writes=["PT"])
            P.op("dve", lambda e: e.tensor_tensor(out=KK[:], in0=KK[:], in1=PT[:], op=ALU.mult), reads=["KK", "PT"], writes=["KK"])
            P.op("dve", lambda e, ct=ct: e.tensor_scalar(out=PT[:], in0=AA[:], scalar1=prm[:, 6, ct:ct + 1],
                                                          scalar2=prm[:, 10, ct:ct + 1], op0=ALU.mult, op1=ALU.add),
                 reads=["AA", "PT"] + prk, writes=["PT"])
            P.op("dve", lambda e: e.tensor_tensor(out=K2[:], in0=KP[:], in1=PT[:], op=ALU.mult), reads=["KP", "PT"], writes=["K2"])
            P.op("dve", lambda e, ct=ct: e.scalar_tensor_tensor(out=SQb[:], in0=RP[:], scalar=prm[:, 7, ct:ct + 1], in1=K2[:],
                                                                 op0=ALU.mult, op1=ALU.mult),
                 reads=["RP", "K2", "SQb"] + prk, writes=["SQb"])
            o = OUT[oc % 2]
            for blk in range(8):
                bs = slice(blk * 512, (blk + 1) * 512)
                P.op("pe", lambda e, bs=bs: e.matmul(ps[3][:, :], lhsT=bones[:], rhs=SQb[:, bs], start=True, stop=True),
                     reads=["bones", "SQb"], writes=[("ps4", 3)])
                P.op("dve", lambda e, bs=bs, o=o: e.tensor_tensor(out=o[:, bs], in0=ps[3][:, :], in1=VP[:, bs], op=ALU.mult),
                     reads=[("ps4", 3), "VP"], writes=[("OUT", oc % 2)])
            P.dma("sp", K.BON_d[c0:c0 + 128, :], o[:], reads=[("OUT", oc % 2)], writes=[("BON_d", ct)])
            oc += 1
            P.op("dve", lambda e: e.tensor_tensor_scan(out=PT[:], data0=M01[:], data1=CL[:], initial=0.0,
                                                        op0=ALU.mult, op1=ALU.add), reads=["M01", "CL", "PT"], writes=["PT"])
            P.op("pool", lambda e: e.tensor_tensor(out=CL[:], in0=PT[:], in1=CL[:], op=ALU.subtract),
                 reads=["PT", "CL"], writes=["CL"])
            P.op("act", lambda e: e.activation(out=CL[:], in_=CL[:], func=AF.Exp), reads=["CL"], writes=["CL"])
            v3 = lambda t: t[:].rearrange("p (c t) -> p c t", t=64)
            o = OUT[oc % 2]
            P.op("dve", lambda e, o=o: e.scalar_tensor_tensor(out=o[:], in0=KK[:], scalar=-1.0, in1=CL[:],
                                                               op0=ALU.mult, op1=ALU.mult),
                 reads=["KK", "CL"], writes=[("OUT", oc % 2)])
            P.dma("sp", K.AH_d[c0:c0 + 128, :], o[:], reads=[("OUT", oc % 2)], writes=[("AH_d", ct)])
            oc += 1
            P.op("act", lambda e: e.activation(out=CL[:], in_=PT[:], func=AF.Exp), reads=["PT", "CL"], writes=["CL"])
            o = OUT[oc % 2]
            P.op("dve", lambda e, o=o: e.tensor_tensor(out=o[:], in0=RP[:], in1=CL[:], op=ALU.mult),
                 reads=["RP", "CL"], writes=[("OUT", oc % 2)])
            P.dma("sp", K.RH_d[c0:c0 + 128, :], o[:], reads=[("OUT", oc % 2)], writes=[("RH_d", ct)])
            oc += 1
            P.op("pool", lambda e: e.tensor_copy(out=PCt[:], in_=v3(CL)[:, :, 63]), reads=["CL"], writes=["PCt"])
            P.dma("sp", K.PC_d[c0:c0 + 128, :], PCt[:], reads=["PCt"], writes=[("PC_d", ct)])
            P.op("act", lambda e: e.activation(out=PT[:], in_=PT[:], func=AF.Exp, scale=-1.0), reads=["PT"], writes=["PT"])
            o = OUT[oc % 2]
            P.op("dve", lambda e, o=o: e.tensor_tensor(out=o[:], in0=K2[:], in1=PT[:], op=ALU.mult),
                 reads=["K2", "PT"], writes=[("OUT", oc % 2)])
            P.dma("sp", K.KH_d[c0:c0 + 128, :], o[:], reads=[("OUT", oc % 2)], writes=[("KH_d", ct)])
            oc += 1
            P.op("dve", lambda e: e.tensor_tensor(out=KK[:], in0=KK[:], in1=AA[:], op=ALU.mult), reads=["KK", "AA"], writes=["KK"])
            o = OUT[oc % 2]
            P.op("dve", lambda e, o=o: e.tensor_tensor(out=o[:], in0=KK[:], in1=PT[:], op=ALU.mult),
                 reads=["KK", "PT"], writes=[("OUT", oc % 2)])
            P.dma("sp", K.BH_d[c0:c0 + 128, :], o[:], reads=[("OUT", oc % 2)], writes=[("BH_d", ct)])
            oc += 1
        P.flush()

def phase4c_rwkv_scan(K, heads=range(4)):
    nc, P = K.nc, K.P
    with contextlib.ExitStack() as st:
        def sb(name, shape, dt):
            return st.enter_context(nc.sbuf_tensor(name, shape, dt))
        ident = sb("ident4", [128, 128], BF16)
        make_ident(K, P, ident)
        MaskG = sb("MaskG", [64, 4, 128], F32)
        MaskX = sb("MaskX", [64, 8, 64], F32)
        I8 = sb("I8", [64, 64], F32)
        ones = sb("ones4", [64, 64], F32)
        P.op("pool", lambda e: e.memset(ones[:], 1.0), writes=["ones"])
        for a in range(4):
            for cq in range(2):
                P.op("pool", lambda e, cq=cq, a=a: e.affine_select(
                    out=MaskG[:, a, cq * 64:(cq + 1) * 64], in_=ones[:], pattern=[[1, 64]],
                    compare_op=(ALU.is_gt if cq == 0 else ALU.is_ge), fill=0.0, base=0, channel_multiplier=-1),
                    reads=["ones"], writes=["MaskG"])
        for a in range(8):
            P.op("pool", lambda e, a=a: e.affine_select(out=MaskX[:, a, :], in_=ones[:], pattern=[[-1, 64]],
                                                         compare_op=ALU.is_gt, fill=0.0, base=0, channel_multiplier=1),
                 reads=["ones"], writes=["MaskX"])
        P.op("dve", lambda e: e.tensor_copy(out=I8[:], in_=ident[0:64, 0:64]), reads=["ident"], writes=["I8"])
        AH = sb("AH", [64, S], RD)
        RH = sb("RH", [64, S], RD)
        BH = sb("BH", [64, S], RD)
        KH = sb("KH", [64, S], RD)
        vb = sb("vb", [64, S], BF16)
        PC = sb("PC", [64, NCH], F32)
        ARh = sb("ARh", [64, NCH, 128], RD)
        BKh = sb("BKh", [64, NCH, 128], RD)
        GmB = sb("GmB", [64, NCH, 128], RD)
        GmK = sb("GmK", [64, NCH, 128], RD)
        Btok = sb("Btok", [64, NCH, 64], RD)
        Ktok = sb("Ktok", [64, NCH, 64], RD)
        Vtok = sb("Vtok", [64, NCH, 64], RD)
        X0 = sb("X0", [64, NCH, 64], RD)
        Pm = sb("Pm", [64, NCH, 64], RD)
        oT = sb("oT", [64, S], F32)
        Ast = sb("Ast", [64, 64], F32)
        Abf = sb("Abf", [64, 64], RD)
        Tt = sb("Tt", [64, 64], F32)
        Xs = sb("Xs", [64, 64], RD)
        Us = sb("Us", [64, 64], RD)
        PSb = st.enter_context(nc.psum_tensor("PSb", [128, 1024], BF16))
        PS = [st.enter_context(nc.psum_tensor("PS%d" % i, [128, 512], F32)) for i in range(7)]
        v3 = lambda t: t[:].rearrange("p (c t) -> p c t", t=64)
        Nb = [v3(AH), v3(RH)]
        Xb = [v3(BH), v3(KH)]
        Nk = ["AH", "RH"]
        Xk = ["BH", "KH"]
        for hd in heads:
            r0 = hd * 64
            P.dma("sp", AH[:], K.AH_d[r0:r0 + 64, :], writes=["AH"])
            P.dma("sp", RH[:], K.RH_d[r0:r0 + 64, :], writes=["RH"])
            P.dma("sp", BH[:], K.BH_d[r0:r0 + 64, :], writes=["BH"])
            P.dma("sp", KH[:], K.KH_d[r0:r0 + 64, :], writes=["KH"])
            P.dma("sp", vb[:], K.vb_d[r0:r0 + 64, :], writes=["vb"])
            P.dma("sp", PC[:], K.PC_d[r0:r0 + 64, :], writes=["PC"])
            P.op("dve", lambda e: e.tensor_copy(out=ARh[:, :, 0:64], in_=v3(AH)), reads=["AH"], writes=["ARh"])
            P.op("pool", lambda e: e.tensor_copy(out=ARh[:, :, 64:128], in_=v3(RH)), reads=["RH"], writes=["ARh"])
            P.op("dve", lambda e: e.tensor_copy(out=BKh[:, :, 0:64], in_=v3(BH)), reads=["BH"], writes=["BKh"])
            P.op("pool", lambda e: e.tensor_copy(out=BKh[:, :, 64:128], in_=v3(KH)), reads=["KH"], writes=["BKh"])
            for (src, srck, col0, dst, dk) in ((BKh, "BKh", 0, Btok, "Btok"), (BKh, "BKh", 64, Ktok, "Ktok"), (None, "vb", 0, Vtok, "Vtok")):
                for c16 in range(0, NCH, 16):
                    for cc in range(16):
                        c = c16 + cc
                        in_ = vb[:, c * 64:(c + 1) * 64] if src is None else src[:, c, col0:col0 + 64]
                        P.op("pe", lambda e, cc=cc, in_=in_: e.transpose(out=PSb[0:64, cc * 64:(cc + 1) * 64], in_=in_,
                                                                         identity=ident[0:64, 0:64]),
                             reads=[srck, "ident"], writes=["PSb"])
                    P.op("act", lambda e, c16=c16, dst=dst: e.copy(out=dst[:, c16:c16 + 16, :].rearrange("p c k -> p (c k)"),
                                                                    in_=PSb[0:64, :]), reads=["PSb"], writes=[dk])
            gi = 0
            for (col0, dst, dk) in ((0, GmB, "GmB"), (64, GmK, "GmK")):
                for c4 in range(0, NCH, 4):
                    b = gi % 2
                    gi += 1
                    for cc in range(4):
                        c = c4 + cc
                        P.op("pe", lambda e, c=c, cc=cc, b=b, col0=col0: e.matmul(
                            PS[b][0:64, cc * 128:(cc + 1) * 128], lhsT=BKh[:, c, col0:col0 + 64], rhs=ARh[:, c, :],
                            start=True, stop=True), reads=["BKh", "ARh"], writes=[("PS", b)])
                    P.op("dve", lambda e, c4=c4, dst=dst, b=b: e.tensor_tensor(
                        out=dst[:, c4:c4 + 4, :], in0=PS[b][0:64, :].rearrange("p (a t) -> p a t", t=128), in1=MaskG[:],
                        op=ALU.mult), reads=[("PS", b), "MaskG"], writes=[dk])
            for c8 in range(0, NCH, 8):
                for cc in range(8):
                    c = c8 + cc
                    P.op("pe", lambda e, c=c, cc=cc: e.matmul(PS[2][0:64, cc * 64:(cc + 1) * 64], lhsT=ARh[:, c, 0:64],
                                                               rhs=BKh[:, c, 0:64], start=True, stop=True),
                         reads=["ARh", "BKh"], writes=[("PS", 2)])
                P.op("dve", lambda e, c8=c8: e.tensor_tensor(
                    out=X0[:, c8:c8 + 8, :], in0=PS[2][0:64, :].rearrange("p (a t) -> p a t", t=64), in1=MaskX[:],
                    op=ALU.mult), reads=[("PS", 2), "MaskX"], writes=["X0"])
            N0 = GmB[:, :, 0:64]
            P.op("pool", lambda e, N0=N0: e.tensor_tensor(out=Pm[:], in0=N0, in1=I8[:].unsqueeze(1).to_broadcast([64, NCH, 64]),
                                                          op=ALU.add), reads=["GmB", "I8"], writes=["Pm"])
            curN, curNk = N0, "GmB"
            curX, curXk = X0[:], "X0"
            for lvl in range(1, 6):
                nX, nXk = Xb[lvl % 2], Xk[lvl % 2]
                nN, nNk = Nb[lvl % 2], Nk[lvl % 2]
                for c8 in range(0, NCH, 8):
                    for cc in range(8):
                        c = c8 + cc
                        P.op("pe", lambda e, c=c, cc=cc, curN=curN, curX=curX: e.matmul(
                            PS[3][0:64, cc * 64:(cc + 1) * 64], lhsT=curN[:, c, :], rhs=curX[:, c, :], start=True, stop=True),
                            reads=[curNk, curXk], writes=[("PS", 3)])
                    P.op("act", lambda e, c8=c8, nX=nX: e.copy(out=nX[:, c8:c8 + 8, :],
                                                                in_=PS[3][0:64, :].rearrange("p (a t) -> p a t", t=64)),
                         reads=[("PS", 3)], writes=[nXk])
                    if lvl < 5:
                        for cc in range(8):
                            c = c8 + cc
                            P.op("pe", lambda e, c=c, cc=cc, curN=curN, curX=curX: e.matmul(
                                PS[4][0:64, cc * 64:(cc + 1) * 64], lhsT=curX[:, c, :], rhs=curN[:, c, :], start=True, stop=True),
                                reads=[curNk, curXk], writes=[("PS", 4)])
                        P.op("dve", lambda e, c8=c8, nN=nN: e.tensor_copy(out=nN[:, c8:c8 + 8, :],
                                                                           in_=PS[4][0:64, :].rearrange("p (a t) -> p a t", t=64)),
                             reads=[("PS", 4)], writes=[nNk])
                    for cc in range(8):
                        c = c8 + cc
                        P.op("pe", lambda e, c=c, cc=cc, nX=nX: e.matmul(
                            PS[5][0:64, cc * 64:(cc + 1) * 64], lhsT=nX[:, c, :], rhs=Pm[:, c, :], start=True, stop=True),
                            reads=[nXk, "Pm"], writes=[("PS", 5)])
                    P.op("dve", lambda e, c8=c8: e.tensor_tensor(
                        out=Pm[:, c8:c8 + 8, :], in0=PS[5][0:64, :].rearrange("p (a t) -> p a t", t=64),
                        in1=Pm[:, c8:c8 + 8, :], op=ALU.add), reads=[("PS", 5), "Pm"], writes=["Pm"])
                curN, curNk, curX, curXk = nN, nNk, nX, nXk
            P.op("pool", lambda e: e.memset(Ast[:], 0.0), writes=["Ast"])
            P.op("pool", lambda e: e.memset(Abf[:], 0.0), writes=["Abf"])
            for c in range(NCH):
                P.op("pool", lambda e, c=c: e.tensor_scalar(out=Tt[:], in0=Ast[:], scalar1=PC[:, c:c + 1], scalar2=0.0,
                                                             op0=ALU.mult, op1=ALU.add), reads=["Ast", "PC"], writes=["Tt"])
                P.op("pe", lambda e, c=c: e.matmul(PS[0][0:64, 0:64], lhsT=ARh[:, c, 0:64], rhs=Abf[:], start=True, stop=False),
                     reads=["ARh", "Abf"], writes=[("PS", 0)])
                P.op("pe", lambda e, c=c: e.matmul(PS[0][0:64, 0:64], lhsT=GmK[:, c, 0:64], rhs=Vtok[:, c, :], start=False, stop=True),
                     reads=["GmK", "Vtok"], writes=[("PS", 0)])
                P.op("act", lambda e: e.copy(out=Xs[:], in_=PS[0][0:64, 0:64]), reads=[("PS", 0)], writes=["Xs"])
                P.op("pe", lambda e, c=c: e.matmul(PS[1][0:64, 0:64], lhsT=Pm[:, c, :], rhs=Xs[:], start=True, stop=True),
                     reads=["Pm", "Xs"], writes=[("PS", 1)])
                P.op("dve", lambda e: e.tensor_copy(out=Us[:], in_=PS[1][0:64, 0:64]), reads=[("PS", 1)], writes=["Us"])
                P.op("pe", lambda e, c=c: e.matmul(PS[6][0:64, 0:64], lhsT=Btok[:, c, :], rhs=Us[:], start=True, stop=False),
                     reads=["Btok", "Us"], writes=[("PS", 6)])
                P.op("pe", lambda e, c=c: e.matmul(PS[6][0:64, 0:64], lhsT=Ktok[:, c, :], rhs=Vtok[:, c, :], start=False, stop=True),
                     reads=["Ktok", "Vtok"], writes=[("PS", 6)])
                ob = 2 + (c % 2)
                P.op("pe", lambda e, c=c, ob=ob: e.matmul(PS[ob][0:64, 0:64], lhsT=Abf[:], rhs=ARh[:, c, 64:128], start=True, stop=False),
                     reads=["Abf", "ARh"], writes=[("PS", ob)])
                P.op("pe", lambda e, c=c, ob=ob: e.matmul(PS[ob][0:64, 0:64], lhsT=Us[:], rhs=GmB[:, c, 64:128], start=False, stop=False),
                     reads=["Us", "GmB"], writes=[("PS", ob)])
                P.op("pe", lambda e, c=c, ob=ob: e.matmul(PS[ob][0:64, 0:64], lhsT=Vtok[:, c, :], rhs=GmK[:, c, 64:128], start=False, stop=True),
                     reads=["Vtok", "GmK"], writes=[("PS", ob)])
                P.op("dve", lambda e, c=c: e.scalar_tensor_tensor(out=Abf[:], in0=PS[6][0:64, 0:64], scalar=PC[:, c:c + 1], in1=Tt[:],
                                                                   op0=ALU.mult, op1=ALU.add),
                     reads=[("PS", 6), "Tt", "PC"], writes=["Abf"])
                P.op("dve", lambda e, c=c: e.scalar_tensor_tensor(out=Ast[:], in0=PS[6][0:64, 0:64], scalar=PC[:, c:c + 1], in1=Tt[:],
                                                                   op0=ALU.mult, op1=ALU.add),
                     reads=[("PS", 6), "Tt", "PC"], writes=["Ast"])
                P.op("act", lambda e, c=c, ob=ob: e.copy(out=oT[:, c * 64:(c + 1) * 64], in_=PS[ob][0:64, 0:64]),
                     reads=[("PS", ob)], writes=[("oT", c // 8)])
            P.dma("sp", K.oT_d[r0:r0 + 64, :], oT[:], reads=[("oT", q) for q in range(8)], writes=[("oT_d", hd)])
        P.flush()

def phase4d_rwkv_post(K, cts=range(2)):
    nc, P = K.nc, K.P
    with contextlib.ExitStack() as st:
        def sb(name, shape, dt):
            return st.enter_context(nc.sbuf_tensor(name, shape, dt))
        ident = sb("ident4d", [128, 128], BF16)
        make_ident(K, P, ident)
        bonesf = sb("bonesf", [128, 128], F32)
        P.op("pool", lambda e: e.memset(bonesf[:], 0.0), writes=["bonesf"])
        P.op("pool", lambda e: e.memset(bonesf[0:64, 0:64], 1.0), reads=["bonesf"], writes=["bonesf"])
        P.op("pool", lambda e: e.memset(bonesf[64:128, 64:128], 1.0), reads=["bonesf"], writes=["bonesf"])
        prm = sb("prm4d", [128, 2, 2], F32)
        P.dma("sp", prm[:, 0, :], K.rw_prm[8], writes=["prm0"])
        P.dma("sp", prm[:, 1, :], K.rw_prm[9], writes=["prm1"])
        o = sb("o4d", [128, S], F32)
        osq = sb("osq", [128, S], F32)
        bon = sb("bon", [128, S], BF16)
        gg = sb("gg", [128, S], BF16)
        Mb = [sb("Mb%d" % i, [128, 512], F32) for i in range(2)]
        Vb = [sb("Vb%d" % i, [128, 512], F32) for i in range(2)]
        Yb = [sb("Yb%d" % i, [128, 512], F32) for i in range(2)]
        Ob = [sb("Ob%d" % i, [128, 512], BF16) for i in range(2)]
        Tk = [sb("Tk%d" % i, [128, 4, 128], BF16) for i in range(2)]
        ps = [st.enter_context(nc.psum_tensor("p4d_%d" % i, [128, 512], F32)) for i in range(4)]
        pst = [st.enter_context(nc.psum_tensor("p4dt_%d" % i, [128, 4, 128], BF16)) for i in range(2)]
        it = 0
        for ct in cts:
            c0 = ct * 128
            P.dma("sp", o[:], K.oT_d[c0:c0 + 128, :], writes=["o"])
            P.dma("sp", bon[:], K.BON_d[c0:c0 + 128, :], writes=["bon"])
            P.dma("sp", gg[:], K.G_d[c0:c0 + 128, :], writes=["gg"])
            P.op("act", lambda e: e.activation(out=osq[:], in_=o[:], func=AF.Square), reads=["o"], writes=["osq"])
            for blk in range(8):
                s2 = it % 2
                it += 1
                bs = slice(blk * 512, (blk + 1) * 512)
                P.op("pe", lambda e, bs=bs, s2=s2: e.matmul(ps[s2][:, :], lhsT=bonesf[:], rhs=o[:, bs], start=True, stop=True),
                     reads=["bonesf", "o"], writes=[("p4d", s2)])
                P.op("pe", lambda e, bs=bs, s2=s2: e.matmul(ps[2 + s2][:, :], lhsT=bonesf[:], rhs=osq[:, bs], start=True, stop=True),
                     reads=["bonesf", "osq"], writes=[("p4d", 2 + s2)])
                P.op("act", lambda e, s2=s2: e.activation(out=Mb[s2][:], in_=ps[s2][:, :], func=AF.Copy, scale=1.0 / 64),
                     reads=[("p4d", s2)], writes=[("Mb", s2)])
                P.op("pool", lambda e, s2=s2: e.tensor_tensor(out=Vb[s2][:], in0=Mb[s2][:], in1=Mb[s2][:], op=ALU.mult),
                     reads=[("Mb", s2)], writes=[("Vb", s2)])
                P.op("dve", lambda e, s2=s2: e.scalar_tensor_tensor(out=Vb[s2][:], in0=ps[2 + s2][:, :], scalar=1.0 / 64, in1=Vb[s2][:],
                                                                     op0=ALU.mult, op1=ALU.subtract),
                     reads=[("p4d", 2 + s2), ("Vb", s2)], writes=[("Vb", s2)])
                P.op("dve", lambda e, s2=s2: e.tensor_scalar(out=Vb[s2][:], in0=Vb[s2][:], scalar1=64e-5, scalar2=None, op0=ALU.add),
                     reads=[("Vb", s2)], writes=[("Vb", s2)])
                P.op("act", lambda e, s2=s2: e.activation(out=Vb[s2][:], in_=Vb[s2][:], func=AF.Sqrt),
                     reads=[("Vb", s2)], writes=[("Vb", s2)])
                P.op("dve", lambda e, s2=s2: e.reciprocal(out=Vb[s2][:], in_=Vb[s2][:]), reads=[("Vb", s2)], writes=[("Vb", s2)])
                P.op("pool", lambda e, s2=s2, bs=bs: e.tensor_tensor(out=Yb[s2][:], in0=o[:, bs], in1=Mb[s2][:], op=ALU.subtract),
                     reads=["o", ("Mb", s2)], writes=[("Yb", s2)])
                P.op("dve", lambda e, s2=s2: e.tensor_tensor(out=Yb[s2][:], in0=Yb[s2][:], in1=Vb[s2][:], op=ALU.mult),
                     reads=[("Yb", s2), ("Vb", s2)], writes=[("Yb", s2)])
                P.op("dve", lambda e, s2=s2, ct=ct: e.tensor_scalar(out=Yb[s2][:], in0=Yb[s2][:], scalar1=prm[:, 0, ct:ct + 1],
                                                                     scalar2=prm[:, 1, ct:ct + 1], op0=ALU.mult, op1=ALU.add),
                     reads=[("Yb", s2), "prm0", "prm1"], writes=[("Yb", s2)])
                P.op("pool", lambda e, s2=s2, bs=bs: e.tensor_tensor(out=Yb[s2][:], in0=Yb[s2][:], in1=bon[:, bs], op=ALU.add),
                     reads=[("Yb", s2), "bon"], writes=[("Yb", s2)])
                P.op("dve", lambda e, s2=s2, bs=bs: e.tensor_tensor(out=Ob[s2][:], in0=Yb[s2][:], in1=gg[:, bs], op=ALU.mult),
                     reads=[("Yb", s2), "gg"], writes=[("Ob", s2)])
                for q in range(4):
                    P.op("pe", lambda e, s2=s2, q=q: e.transpose(out=pst[s2][:, q, :], in_=Ob[s2][:, q * 128:(q + 1) * 128],
                                                                 identity=ident[:]),
                         reads=[("Ob", s2), "ident"], writes=[("p4dt", s2)])
                P.op("act", lambda e, s2=s2: e.copy(out=Tk[s2][:], in_=pst[s2][:]), reads=[("p4dt", s2)], writes=[("Tk", s2)])
                P.dma("sp", K.ro_loc_d[blk // 4].rearrange("(t p) c -> p t c", p=128)[:, (blk % 4) * 4:(blk % 4 + 1) * 4, c0:c0 + 128], Tk[s2][:],
                      reads=[("Tk", s2)], writes=[("ro_tok_d", ct, blk)])
        P.flush()


def phase4e_allgather(K):
    P = K.P
    for hh in range(2):
        P.coll(lambda e, hh=hh: e.collective_compute("AllGather", ALU.bypass, replica_groups=[[0, 1, 2, 3], [4, 5, 6, 7]],
                                                     ins=[K.ro_loc_d[hh].opt()], outs=[K.ro_all_d[hh].opt()]),
               reads=[("ro_loc", hh)], writes=[("ro_all", hh)])
    P.flush()


def phase5a_select(K):
    nc, P = K.nc, K.P
    with contextlib.ExitStack() as st:
        def sb(name, shape, dt):
            return st.enter_context(nc.sbuf_tensor(name, shape, dt))
        ro = sb("ro_tok", [128, 32, 1024], BF16)
        selT = sb("selT", [128, 32, 1024], BF16)
        qrow = sb("qrow", [128, 1024], F32)
        tki = sb("tki", [128, 32], I32)
        tkf = sb("tkf", [128, 32], F32)
        mo = [sb("mo%d" % i, [128, 512], BF16) for i in range(2)]
        at = sb("at5", [128, 8, 1024], BF16)
        ps = [st.enter_context(nc.psum_tensor("p5a_%d" % i, [128, 512], F32)) for i in range(2)]
        for q4 in range(4):
            for hh in range(2):
                P.dma("sp", ro[:, hh * 16:(hh + 1) * 16, q4 * 256:(q4 + 1) * 256],
                      K.ro_all_d[hh][q4 * 2048:(q4 + 1) * 2048, :].rearrange("(t p) c -> p t c", p=128), writes=[("ro", q4, hh)])
        rok = [("ro", q4, hh) for q4 in range(4) for hh in range(2)]
        P.dma("sp", qrow[:], bcast_rows(K.qpos_row, 1024), writes=["qrow"])
        P.op("pool", lambda e: e.iota(tki[:], pattern=[[128, 32]], base=0, channel_multiplier=1), writes=["tki"])
        P.op("dve", lambda e: e.tensor_copy(out=tkf[:], in_=tki[:]), reads=["tki"], writes=["tkf"])
        for T in range(32):
            P.op("dve", lambda e, T=T: e.tensor_scalar(out=selT[:, T, :], in0=qrow[:], scalar1=tkf[:, T:T + 1], scalar2=0.0,
                                                      op0=ALU.is_equal, op1=ALU.add), reads=["qrow", "tkf"], writes=[("selT", T)])
        sk = [("selT", T) for T in range(32)]
        P.dma("sp", at[:], K.attT_d.rearrange("h p t -> p h t"), writes=["at5"])
        P.dma("sp", K.mixT_d.rearrange("k p t -> p k t")[:, 0:8, :], at[:], reads=["at5"], writes=["mixa"])
        i = 0
        for m in range(8):
            for half in range(2):
                s2 = i % 2
                i += 1
                for T in range(32):
                    P.op("pe", lambda e, T=T, m=m, half=half, s2=s2: e.matmul(
                        ps[s2][:, :], lhsT=ro[:, T, m * 128:(m + 1) * 128], rhs=selT[:, T, half * 512:(half + 1) * 512],
                        start=(T == 0), stop=(T == 31)), reads=rok + sk, writes=[("p5a", s2)])
                P.op("act", lambda e, s2=s2: e.copy(out=mo[s2][:], in_=ps[s2][:, :]), reads=[("p5a", s2)], writes=[("mo", s2)])
                P.dma("sp", K.mixT_d[8 + m, :, half * 512:(half + 1) * 512], mo[s2][:], reads=[("mo", s2)], writes=[("mixr", m, half)])
        P.flush()


def phase5b_outproj(K):
    nc, P = K.nc, K.P
    with contextlib.ExitStack() as st:
        def sb(name, shape, dt):
            return st.enter_context(nc.sbuf_tensor(name, shape, dt))
        ident = sb("ident5", [128, 128], BF16)
        make_ident(K, P, ident)
        G2, SH2 = load_G_SH(K, P, st, 3, 4, K.norm2_g, "p5")
        GT1 = sb("GT1", [128, D], F32)
        P.dma("sp", GT1[:], bcast_rows(K.mod_d[2 * D:3 * D], D), writes=["GT1"])
        Wo = sb("Wo", [128, 16, D], BF16)
        stg = [sb("wstg5_%d" % i, [128, 4, 512], F32) for i in range(2)]
        wk = load_weight_bf16(K, P, stg, Wo, 0, K.w_out, D, "Wo")
        mixT = sb("mixT", [128, 16, 512], BF16)
        T = norm_tiles_alloc(K, st, "p5")
        x1 = T["xt"]
        hT = [sb("hT5_0", [128, 16, 512], BF16)] * 2
        xo = [sb("xo%d" % i, [128, D], F32) for i in range(2)]
        ps = [st.enter_context(nc.psum_tensor("p5b_%d" % i, [128, 512], F32)) for i in range(2)]
        ss, junk, hb, pT = T["ss"], T["junk"], T["hb"], T["pT"]
        gi = 0
        for blk in range(2):
            hs = 0
            P.dma("sp", mixT[:], K.mixT_d.rearrange("k p t -> p k t")[:, :, blk * 512:(blk + 1) * 512], writes=["mixT"])
            for ti in range(4):
                t = blk * 4 + ti
                xs = t % 2
                P.dma("sp", xo[xs][:], K.x_own[t * 128:(t + 1) * 128, :], writes=[("xo", xs)])
                for cg in range(4):
                    b = gi % 2
                    gi += 1
                    for k in range(16):
                        P.op("pe", lambda e, b=b, k=k, t=t, cg=cg: e.matmul(
                            ps[b][:, :], lhsT=mixT[:, k, (t % 4) * 128:(t % 4 + 1) * 128], rhs=Wo[:, k, cg * 512:(cg + 1) * 512],
                            start=(k == 0), stop=(k == 15)), reads=["mixT"] + wk, writes=[("p5b", b)])
                    cs = slice(cg * 512, (cg + 1) * 512)
                    P.op("dve", lambda e, b=b, xs=xs, cs=cs: e.tensor_tensor(out=x1[xs][:, cs], in0=ps[b][:, :], in1=GT1[:, cs], op=ALU.mult),
                         reads=[("p5b", b), "GT1"], writes=[("xt", xs)])
                    P.op("pool", lambda e, xs=xs, cs=cs: e.tensor_tensor(out=x1[xs][:, cs], in0=x1[xs][:, cs], in1=xo[xs][:, cs], op=ALU.add),
                         reads=[("xt", xs), ("xo", xs)], writes=[("xt", xs)])
                P.dma("sp", K.x1_d[t * 128:(t + 1) * 128, :], x1[xs][:], reads=[("xt", xs)], writes=[("x1_d", t)])
                P.op("act", lambda e, xs=xs: e.activation(out=junk[:], in_=x1[xs][:], func=AF.Square, accum_out=ss[:, 0:1]),
                     reads=[("xt", xs)], writes=["junk", "ss0"])
                P.op("dve", lambda e: e.tensor_scalar(out=ss[:, 1:2], in0=ss[:, 0:1], scalar1=1.0 / D, scalar2=1e-6,
                                                       op0=ALU.mult, op1=ALU.add), reads=["ss0"], writes=["ss1"])
                P.op("act", lambda e: e.activation(out=ss[:, 2:3], in_=ss[:, 1:2], func=AF.Sqrt), reads=["ss1"], writes=["ss2"])
                P.op("dve", lambda e: e.reciprocal(out=ss[:, 3:4], in_=ss[:, 2:3]), reads=["ss2"], writes=["ss3"])
                P.op("dve", lambda e, xs=xs: e.scalar_tensor_tensor(out=x1[xs][:], in0=x1[xs][:], scalar=ss[:, 3:4], in1=G2[:],
                                                                   op0=ALU.mult, op1=ALU.mult),
                     reads=[("xt", xs), "ss3", "G"], writes=[("xt", xs)])
                P.op("pool", lambda e, xs=xs: e.tensor_tensor(out=hb[xs][:], in0=x1[xs][:], in1=SH2[:], op=ALU.add),
                     reads=[("xt", xs), "SH"], writes=[("hb", xs)])
                for half in range(2):
                    for kk in range(8):
                        k = half * 8 + kk
                        P.op("pe", lambda e, k=k, kk=kk, half=half, xs=xs: e.transpose(
                            out=pT[half][:, kk, :], in_=hb[xs][:, k * 128:(k + 1) * 128], identity=ident[:]),
                            reads=[("hb", xs), "ident"], writes=[("pT", half)])
                    o_ = hT[hs][:, half * 8:(half + 1) * 8, ti * 128:(ti + 1) * 128]
                    if half == 0:
                        P.op("act", lambda e, o_=o_, half=half: e.copy(out=o_, in_=pT[half][:]), reads=[("pT", half)], writes=[("hT5", hs, ti, half)])
                    else:
                        P.op("dve", lambda e, o_=o_, half=half: e.tensor_copy(out=o_, in_=pT[half][:]), reads=[("pT", half)], writes=[("hT5", hs, ti, half)])
            P.dma("sp", K.h2T_d.rearrange("k p t -> p k t")[:, :, blk * 512:(blk + 1) * 512], hT[hs][:],
                  reads=[("hT5", hs, ti, half) for ti in range(4) for half in range(2)], writes=[("h2T_d", blk)])
        P.flush()


def phase5c_ffn(K):
    nc, P = K.nc, K.P
    NF = 5632 // 128
    with contextlib.ExitStack() as st:
        def sb(name, shape, dt):
            return st.enter_context(nc.sbuf_tensor(name, shape, dt))
        h2T = sb("h2T", [128, 16, OWN], BF16)
        P.dma("sp", h2T[:], K.h2T_d.rearrange("k p t -> p k t"), writes=["h2T"])
        ao = [sb("ao%d" % i, [128, 512], BF16) for i in range(2)]
        stg = [sb("wstg6_%d" % i, [128, 4, 512], F32) for i in range(4)]
        Wg = [sb("Wg%d" % i, [128, 16, 512], BF16) for i in range(2)]
        Wu = [sb("Wu%d" % i, [128, 16, 512], BF16) for i in range(2)]
        sg = [sb("sg%d" % i, [128, 512], F32) for i in range(2)]
        ps = [st.enter_context(nc.psum_tensor("p5c_%d" % i, [128, 512], F32)) for i in range(4)]
        gi = 0

        def load_group(fg):
            ws = fg % 2
            load_weight_bf16(K, P, stg, Wg[ws], 0, K.w_ffn_gate[:, fg * 512:(fg + 1) * 512], 512, ("Wg", ws))
            load_weight_bf16(K, P, stg, Wu[ws], 0, K.w_ffn_up[:, fg * 512:(fg + 1) * 512], 512, ("Wu", ws))
        load_group(0)
        for fg in range(11):
            ws = fg % 2
            if fg + 1 < 11:
                load_group(fg + 1)
            for f4 in range(4):
                f = fg * 4 + f4
                for tb in range(2):
                    b = gi % 2
                    gi += 1
                    for k in range(16):
                        P.op("pe", lambda e, b=b, k=k, f4=f4, tb=tb, ws=ws: e.matmul(
                            ps[b][:, :], lhsT=Wg[ws][:, k, f4 * 128:(f4 + 1) * 128], rhs=h2T[:, k, tb * 512:(tb + 1) * 512],
                            start=(k == 0), stop=(k == 15)), reads=["h2T", (("Wg", ws), 0, (k // 4) * 4)], writes=[("p5c", b)])
                    for k in range(16):
                        P.op("pe", lambda e, b=b, k=k, f4=f4, tb=tb, ws=ws: e.matmul(
                            ps[2 + b][:, :], lhsT=Wu[ws][:, k, f4 * 128:(f4 + 1) * 128], rhs=h2T[:, k, tb * 512:(tb + 1) * 512],
                            start=(k == 0), stop=(k == 15)), reads=["h2T", (("Wu", ws), 0, (k // 4) * 4)], writes=[("p5c", 2 + b)])
                    P.op("act", lambda e, b=b: e.activation(out=sg[b][:], in_=ps[b][:, :], func=AF.Silu),
                         reads=[("p5c", b)], writes=[("sg", b)])
                    P.op("dve", lambda e, b=b: e.tensor_tensor(out=ao[b][:], in0=ps[2 + b][:, :], in1=sg[b][:], op=ALU.mult),
                         reads=[("p5c", 2 + b), ("sg", b)], writes=[("ao", b)])
                    P.dma("sp", K.actT_d[f, :, tb * 512:(tb + 1) * 512], ao[b][:], reads=[("ao", b)], writes=[("actT_d", f, tb)])
        P.flush()
    with contextlib.ExitStack() as st:
        def sb(name, shape, dt):
            return st.enter_context(nc.sbuf_tensor(name, shape, dt))
        GT2 = sb("GT2", [128, D], F32)
        P.dma("sp", GT2[:], bcast_rows(K.mod_d[5 * D:6 * D], D), writes=["GT2"])
        actT = sb("actT", [128, NF, OWN], BF16)
        for q in range(4):
            P.dma("sp", actT[:, q * 11:(q + 1) * 11, :], K.actT_d.rearrange("f p t -> p f t")[:, q * 11:(q + 1) * 11, :], writes=[("actT", q)])
        ak = [("actT", q) for q in range(4)]
        stg = [sb("wstg7_%d" % i, [128, 4, 256], F32) for i in range(4)]
        ps = [st.enter_context(nc.psum_tensor("p5d_%d" % i, [128, 512], F32)) for i in range(2)]
        gi = 0
        Wd = [sb("Wd%d" % i, [128, NF, 256], BF16) for i in range(2)]
        x1 = [sb("x1_%d" % i, [128, 256], F32) for i in range(2)]
        yo = [sb("yo%d" % i, [128, 256], F32) for i in range(2)]
        wdv = K.w_ffn_down.rearrange("(k p) n -> p k n", p=128)
        engs = ["pool", "dve", "act"]

        def load_wd(cg):
            wsl = cg % 2
            for k0 in range(0, NF, 4):
                i = K.wcnt
                K.wcnt += 1
                sl = i % 4
                P.dma("sp", stg[sl][:, 0:4, 0:256], wdv[:, k0:k0 + 4, cg * 256:(cg + 1) * 256], writes=[("wstg", sl)])
                eng = engs[i % 3]
                o_ = Wd[wsl][:, k0:k0 + 4, :]
                if eng == "act":
                    P.op("act", lambda e, o_=o_, sl=sl: e.copy(out=o_, in_=stg[sl][:, 0:4, 0:256]), reads=[("wstg", sl)], writes=[("Wd", wsl, k0)])
                else:
                    P.op(eng, lambda e, o_=o_, sl=sl: e.tensor_copy(out=o_, in_=stg[sl][:, 0:4, 0:256]), reads=[("wstg", sl)], writes=[("Wd", wsl, k0)])
        load_wd(0)
        for cg in range(8):
            wsl = cg % 2
            cs = slice(cg * 256, (cg + 1) * 256)
            if cg + 1 < 8:
                load_wd(cg + 1)
            for t in range(8):
                b = gi % 2
                gi += 1
                P.dma("sp", x1[b][:], K.x1_d[t * 128:(t + 1) * 128, cs], writes=[("x1", b)])
                for f in range(NF):
                    P.op("pe", lambda e, b=b, f=f, t=t, wsl=wsl: e.matmul(ps[b][:, 0:256], lhsT=actT[:, f, t * 128:(t + 1) * 128], rhs=Wd[wsl][:, f, :],
                                                                          start=(f == 0), stop=(f == NF - 1)),
                         reads=[("actT", f // 11), ("Wd", wsl, (f // 4) * 4)], writes=[("p5c", b)])
                P.op("dve", lambda e, b=b, cs=cs: e.tensor_tensor(out=yo[b][:], in0=ps[b][:, 0:256], in1=GT2[:, cs], op=ALU.mult),
                     reads=[("p5c", b), "GT2"], writes=[("yo", b)])
                P.op("pool", lambda e, b=b: e.tensor_tensor(out=yo[b][:], in0=yo[b][:], in1=x1[b][:], op=ALU.add),
                     reads=[("yo", b), ("x1", b)], writes=[("yo", b)])
                P.dma("sp", K.out[t * 128:(t + 1) * 128, cs], yo[b][:], reads=[("yo", b)], writes=[("out", t, cg)])
        P.flush()


def phase_final_copy(K):
    nc, P = K.nc, K.P
    with contextlib.ExitStack() as st:
        xt = [st.enter_context(nc.sbuf_tensor("fx%d" % i, [128, D], F32)) for i in range(2)]
        for t in range(8):
            s = t % 2
            P.dma("sp", xt[s][:], K.x_own[t * 128:(t + 1) * 128, :], writes=[("fx", s)])
            P.dma("sp", K.out[t * 128:(t + 1) * 128, :], xt[s][:], reads=[("fx", s)], writes=[("out", t)])
        P.flush()


def own_tiles(j):
    r = []
    for m in range(4):
        r += [8 * m + j, 8 * m + 7 - j]
    return r


def build_program(debug=False, stages=99, cts=range(2), dbg_list=None, skip_att=False):
    nc = bass.Bass("TRN2", target_bir_lowering=False)
    K = Ctx()
    K.stages = stages
    K.cts = cts
    K.skip_att = skip_att
    K.nc = nc
    K.dbg = {}
    K.wcnt = 0

    def inp(name, shape, dt=F32):
        return nc.dram_tensor(name, list(shape), dt, kind="ExternalInput").ap()

    def scratch(name, shape, dt):
        return nc.dram_tensor(name, list(shape), dt, kind="Internal").ap()

    K.x_full = inp("x_full", [S, D])
    K.x_own = inp("x_own", [OWN, D])
    K.c_arr = inp("c_arr", [128, 16])
    K.pos_full = inp("pos_full", [128, 32], I32)
    K.invf_att = inp("invf_att", [128, 16])
    K.invf_idx = inp("invf_idx", [128, 8])
    K.w_ada = inp("w_ada", [D, 3072])
    K.b_ada = inp("b_ada", [3072])
    K.norm1_g = inp("norm1_g", [D])
    K.k_norm_g = inp("k_norm_g", [128])
    K.q_norm_g = inp("q_norm_g", [128])
    K.pos_own = inp("pos_own", [128, 8], I32)
    K.qpos_own = inp("qpos_own", [128, 8])
    K.w_in = inp("w_in", [D, 4176])
    K.rw_prm = [inp("rwp%d" % i, [128, 2]) for i in range(10)]
    K.w_in_rw = inp("w_in_rw", [D, 1216])
    K.rw_mul = inp("rw_mul", [128, 4])
    K.rw_w_up = inp("rw_w_up", [96, 256])
    K.rw_a_up = inp("rw_a_up", [96, 256])
    K.rw_g_up = inp("rw_g_up", [256, 256])
    K.qpos_row = inp("qpos_row", [OWN])
    K.w_out = inp("w_out", [D, D])
    K.norm2_g = inp("norm2_g", [D])
    K.w_ffn_gate = inp("w_ffn_gate", [D, 5632])
    K.w_ffn_up = inp("w_ffn_up", [D, 5632])
    K.w_ffn_down = inp("w_ffn_down", [5632, D])
    K.out = nc.dram_tensor("y_own", [OWN, D], F32, kind="ExternalOutput").ap()
    K.modq_d = scratch("modq_d", [1, 3072], F32)
    K.mod4_d = scratch("mod4_d", [4, 3072], F32)
    K.mod_d = K.mod4_d.rearrange("a n -> (a n)")
    K.hT_d = scratch("hT_d", [16, 128, S], BF16)
    K.kT_d = scratch("kT_d", [8, 128, S], BF16)
    K.v_d = scratch("v_d", [S, 8 * 129], BF16)
    K.ikT_d = scratch("ikT_d", [64, S], BF16)
    K.yT_d = scratch("yT_d", [1216, S], F32)
    K.qT_d = scratch("qT_d", [8, 128, OWN], BF16)
    K.iqT_d = scratch("iqT_d", [64, OWN, 16], BF16)
    K.iw_d = scratch("iw_d", [OWN, 16], F32)
    K.attT_d = scratch("attT_d", [8, 128, OWN], BF16)
    for nm in ("vb_d", "G_d", "BON_d", "AH_d", "RH_d", "BH_d", "KH_d"):
        setattr(K, nm, scratch(nm, [256, S], BF16))
    K.PC_d = scratch("PC_d", [256, NCH], F32)
    K.oT_d = scratch("oT_d", [256, S], F32)
    K.ro_loc_d = [scratch("ro_loc%d_d" % i, [2048, 256], BF16) for i in range(2)]
    K.ro_all_d = [scratch("ro_all%d_d" % i, [8192, 256], BF16) for i in range(2)]
    K.mixT_d = scratch("mixT_d", [16, 128, OWN], BF16)
    K.x1_d = scratch("x1_d", [OWN, D], F32)
    K.h2T_d = scratch("h2T_d", [16, 128, OWN], BF16)
    K.actT_d = scratch("actT_d", [44, 128, OWN], BF16)
    with contextlib.ExitStack() as stack:
        K.P = Prog(nc, stack)
        phase0_adaln(K)
        phase1_kv(K)
        if K.stages >= 2:
            phase1b_rwkv_proj(K)
        if K.stages >= 3 and not getattr(K, "skip_att", False):
            phase2_own_proj(K)
            phase3_attention(K)
        if K.stages >= 4:
            phase4b_rwkv_prep(K, cts=K.cts)
            if K.stages >= 5:
                phase4c_rwkv_scan(K, heads=[h for ct in K.cts for h in (2 * ct, 2 * ct + 1)])
        if K.stages >= 6:
            phase4d_rwkv_post(K, cts=K.cts)
            phase4e_allgather(K)
        if K.stages >= 7:
            phase5a_select(K)
            phase5b_outproj(K)
            phase5c_ffn(K)
        else:
            phase_final_copy(K)
        if debug:
            P = K.P
            allc = (("dbg_mixT", K.mixT_d, [16, 128, OWN], BF16), ("dbg_x1", K.x1_d, [OWN, D], F32),
                    ("dbg_oT", K.oT_d, [256, S], F32), ("dbg_AH", K.AH_d, [256, S], BF16), ("dbg_BH", K.BH_d, [256, S], BF16),
                    ("dbg_KH", K.KH_d, [256, S], BF16), ("dbg_RH", K.RH_d, [256, S], BF16), ("dbg_PC", K.PC_d, [256, NCH], F32),
                    ("dbg_G", K.G_d, [256, S], BF16), ("dbg_BON", K.BON_d, [256, S], BF16), ("dbg_vb", K.vb_d, [256, S], BF16),
                    ("dbg_yT", K.yT_d, [1216, S], F32), ("dbg_attT", K.attT_d, [8, 128, OWN], BF16),
                                     ("dbg_qT", K.qT_d, [8, 128, OWN], BF16), ("dbg_iqT", K.iqT_d, [64, OWN, 16], BF16),
                                     ("dbg_iw", K.iw_d, [OWN, 16], F32))
            for nm, src, shp, dt in allc:
                if dbg_list is not None and nm not in dbg_list:
                    continue
                o = dbg_out(K, nm, shp, dt)
                P.dma("sp", o, src, writes=[nm])
            P.flush()
    return nc, K


def make_in_maps(inputs, cores=range(8)):
    x = np.asarray(inputs["x"], dtype=np.float32)
    c = np.asarray(inputs["c"], dtype=np.float32)
    pos = np.asarray(inputs["positions"], dtype=np.int32)
    invf_att = (np.float32(500000.0) ** (-np.arange(16, dtype=np.float32) / np.float32(16))).astype(np.float32)
    invf_idx = (np.float32(500000.0) ** (-np.arange(8, dtype=np.float32) / np.float32(8))).astype(np.float32)
    mu = np.asarray(inputs["rwkv_mu"][0], dtype=np.float32)

    vecs = [mu[0:1024], mu[1024:2048], mu[2048:3072], inputs["rwkv_w0"][0], inputs["rwkv_a0"][0], inputs["rwkv_k_k"][0],
            inputs["rwkv_k_a"][0], np.asarray(inputs["rwkv_r_k"][0]).reshape(-1), inputs["rwkv_lnx_g"][0], inputs["rwkv_lnx_b"][0]]
    w_in_full = np.asarray(inputs["w_in"][0], dtype=np.float32)
    rw_mul = np.zeros((128, 4), np.float32)
    rw_mul[:96, 0] = mu[3072:3168]
    rw_mul[:96, 1] = mu[3168:3264]
    rw_mul[:, 2] = mu[3264:3392]
    rw_mul[:, 3] = mu[3392:3520]
    maps = []
    for core in cores:
        b, j = core // 4, core % 4
        ch = slice(256 * j, 256 * j + 256)
        rwp = {"rwp%d" % i: np.ascontiguousarray(np.asarray(v, dtype=np.float32)[ch].reshape(2, 128).T) for i, v in enumerate(vecs)}
        R0 = 4176
        w_in_rw = np.ascontiguousarray(np.concatenate([w_in_full[:, R0 + 256 * j:R0 + 256 * j + 256],
                                                       w_in_full[:, R0 + 1024 + 256 * j:R0 + 1024 + 256 * j + 256],
                                                       w_in_full[:, R0 + 2048 + 256 * j:R0 + 2048 + 256 * j + 256],
                                                       w_in_full[:, R0 + 3072:R0 + 3520]], axis=1))
        tiles = own_tiles(j)
        idx = np.concatenate([np.arange(t * 128, (t + 1) * 128) for t in tiles])
        maps.append({
            "x_full": np.ascontiguousarray(x[b]),
            "x_own": np.ascontiguousarray(x[b][idx]),
            "c_arr": np.ascontiguousarray(c[b].reshape(16, 128).T),
            "pos_full": np.ascontiguousarray(pos[b].reshape(32, 128).T),
            "invf_att": np.ascontiguousarray(np.broadcast_to(invf_att, (128, 16))),
            "invf_idx": np.ascontiguousarray(np.broadcast_to(invf_idx, (128, 8))),
            "w_ada": np.ascontiguousarray(np.asarray(inputs["w_ada"][0], dtype=np.float32)[:, 3072 * j:3072 * (j + 1)]),
            "b_ada": np.ascontiguousarray(np.asarray(inputs["b_ada"][0], dtype=np.float32)[3072 * j:3072 * (j + 1)]),
            "norm1_g": np.asarray(inputs["norm1_g"][0], dtype=np.float32),
            "k_norm_g": np.asarray(inputs["k_norm_g"][0], dtype=np.float32),
            "q_norm_g": np.asarray(inputs["q_norm_g"][0], dtype=np.float32),
            "pos_own": np.ascontiguousarray(pos[b][idx].reshape(8, 128).T),
            "qpos_own": np.ascontiguousarray(idx.astype(np.float32).reshape(8, 128).T),
            "w_in": np.ascontiguousarray(w_in_full[:, 0:4176]),
            "qpos_row": idx.astype(np.float32),
            "w_out": np.asarray(inputs["w_out"][0], dtype=np.float32),
            "norm2_g": np.asarray(inputs["norm2_g"][0], dtype=np.float32),
            "w_ffn_gate": np.asarray(inputs["w_ffn_gate"][0], dtype=np.float32),
            "w_ffn_up": np.asarray(inputs["w_ffn_up"][0], dtype=np.float32),
            "w_ffn_down": np.asarray(inputs["w_ffn_down"][0], dtype=np.float32),
            "rw_w_up": np.ascontiguousarray(np.asarray(inputs["rwkv_w_up"][0], dtype=np.float32)[:, ch]),
            "rw_a_up": np.ascontiguousarray(np.asarray(inputs["rwkv_a_up"][0], dtype=np.float32)[:, ch]),
            "rw_g_up": np.ascontiguousarray(np.asarray(inputs["rwkv_g_up"][0], dtype=np.float32)[:, ch]),
            "w_in_rw": w_in_rw,
            "rw_mul": rw_mul,
            **rwp,
        })
    return maps


def kernel(**inputs):
    nc, K = build_program(debug=False)
    maps = make_in_maps(inputs)
    res = run_bass_kernel_spmd(nc, maps, core_ids=list(range(8)))
    out = np.zeros((2, S, D), dtype=np.float32)
    for core in range(8):
        b, j = core // 4, core % 4
        y = res.results[core]["y_own"]
        for i, t in enumerate(own_tiles(j)):
            out[b, t * 128:(t + 1) * 128] = y[i * 128:(i + 1) * 128]
    return out
```

```python
import contextlib
import numpy as np
import concourse.bass as bass
import concourse.mybir as mybir
from concourse.bass_utils import run_bass_kernel_spmd

F32 = mybir.dt.float32
BF16 = mybir.dt.bfloat16
I32 = mybir.dt.int32
AF = mybir.ActivationFunctionType
ALU = mybir.AluOpType
AX = mybir.AxisListType

D = 2048
S = 4096
NT = 32
OWN = 1024
ENGS = ("pe", "act", "dve", "pool", "sp")
DEBUG = {}


class _Op:
    __slots__ = ("eng", "fn", "deps", "needs_inc", "is_dma", "sem", "count", "idx", "prev_same_sem", "is_cc")

    def __init__(self, eng, fn, is_dma):
        self.eng = eng
        self.fn = fn
        self.deps = set()
        self.needs_inc = False
        self.is_dma = is_dma
        self.sem = None
        self.count = 0
        self.prev_same_sem = None
        self.is_cc = False


class Prog:
    def __init__(self, nc, stack, n_dma_sems=48):
        self.nc = nc
        self.n_dma_sems = n_dma_sems
        self.eng_sem = {e: stack.enter_context(nc.semaphore("s_" + e)) for e in ENGS}
        self.dma_sems = [stack.enter_context(nc.semaphore("d%d" % i)) for i in range(n_dma_sems)]
        self.bar_sem = stack.enter_context(nc.semaphore("bar"))
        self.cc_sem = stack.enter_context(nc.semaphore("ccs"))
        self.cc_cnt = 0
        self.cnt = {e: 0 for e in ENGS}
        self.dcnt = [0] * n_dma_sems
        self.rr = 0
        self.nbar = 0
        self._reset()

    def _reset(self):
        self.ops = []
        self.last_writer = {}
        self.readers = {}

    def _record(self, op, reads, writes):
        idx = len(self.ops)
        op.idx = idx
        deps = set()
        for k in reads:
            w = self.last_writer.get(k)
            if w is not None:
                deps.add(w)
        for k in writes:
            w = self.last_writer.get(k)
            if w is not None:
                deps.add(w)
            for r in self.readers.get(k, ()):
                deps.add(r)
        deps.discard(idx)
        op.deps = deps
        self.ops.append(op)
        for k in reads:
            self.readers.setdefault(k, []).append(idx)
        for k in writes:
            self.last_writer[k] = idx
            self.readers[k] = []
        return idx

    def op(self, eng, fn, reads=(), writes=()):
        return self._record(_Op(eng, fn, False), reads, writes)

    def dma(self, queue, out, in_, reads=(), writes=(), **kw):
        def fn(e, out=out, in_=in_, kw=kw):
            return e.dma_start(out=out, in_=in_, **kw)
        return self._record(_Op(queue, fn, True), reads, writes)

    def coll(self, fn, reads=(), writes=()):
        o = _Op("pool", fn, True)
        o.is_cc = True
        return self._record(o, reads, writes)

    def flush(self):
        nc = self.nc
        ops = self.ops
        for o in ops:
            nd = set()
            for d in o.deps:
                p = ops[d]
                if o.eng == "pe" and p.eng == "pe" and not p.is_dma and not o.is_dma:
                    continue
                nd.add(d)
                p.needs_inc = True
            o.deps = nd
        last_of = {}
        for o in ops:
            if not o.is_dma:
                last_of[o.eng] = o
        for o in last_of.values():
            o.needs_inc = True
        dlast = [None] * self.n_dma_sems
        for o in ops:
            if o.is_cc:
                self.cc_cnt += 1
                o.sem = self.cc_sem
                o.count = self.cc_cnt
            elif o.is_dma:
                s = self.rr % self.n_dma_sems
                self.rr += 1
                o.prev_same_sem = dlast[s]
                self.dcnt[s] += 16
                o.sem = self.dma_sems[s]
                o.count = self.dcnt[s]
                dlast[s] = o.idx
            elif o.needs_inc:
                self.cnt[o.eng] += 1
                o.sem = self.eng_sem[o.eng]
                o.count = self.cnt[o.eng]
        per_eng = {e: [o for o in ops if o.eng == e] for e in ENGS}
        final = [(self.dma_sems[s], self.dcnt[s]) for s in range(self.n_dma_sems) if self.dcnt[s] > 0]
        final += [(self.eng_sem[e], self.cnt[e]) for e in ENGS if self.cnt[e] > 0]
        if self.cc_cnt > 0:
            final.append((self.cc_sem, self.cc_cnt))
        self.nbar += 1
        nbar = self.nbar
        bar = self.bar_sem

        def run(e_name, eng):
            waited = {}
            for o in per_eng[e_name]:
                need = {}
                for d in o.deps:
                    p = ops[d]
                    if need.get(p.sem.num, (0, None))[0] < p.count:
                        need[p.sem.num] = (p.count, p.sem)
                if o.is_dma and o.prev_same_sem is not None:
                    p = ops[o.prev_same_sem]
                    if need.get(p.sem.num, (0, None))[0] < p.count:
                        need[p.sem.num] = (p.count, p.sem)
                for key, (c, s) in need.items():
                    if waited.get(key, 0) < c:
                        eng.wait_ge(s, c)
                        waited[key] = c
                ins = o.fn(eng)
                if o.is_cc:
                    ins.then_inc(o.sem)
                elif o.is_dma:
                    ins.then_inc(o.sem, 16)
                elif o.needs_inc:
                    ins.then_inc(o.sem, 1)
            if e_name == "sp":
                for s, c in final:
                    eng.wait_ge(s, c)
                eng.sem_inc(bar, 1)
            eng.wait_ge(bar, nbar)

        with nc.Block() as block:
            @block.tensor
            def _(e):
                run("pe", e)

            @block.scalar
            def _(e):
                run("act", e)

            @block.vector
            def _(e):
                run("dve", e)

            @block.gpsimd
            def _(e):
                run("pool", e)

            @block.sync
            def _(e):
                run("sp", e)
        self._reset()


class Ctx:
    pass


def bcast_rows(ap1d, n):
    return bass.AP(ap1d.tensor, ap1d.offset, [[0, 128], [1, n]])


def dbg_out(K, name, shape, dtype=F32):
    t = K.nc.dram_tensor(name, list(shape), dtype, kind="ExternalOutput")
    K.dbg[name] = t
    return t.ap()


def make_ident(K, P, ident):
    P.op("pool", lambda e: e.memset(ident[:], 0.0), writes=["ident"])
    P.op("pool", lambda e: e.affine_select(out=ident[:], in_=ident[:], pattern=[[-1, 128]],
                                           compare_op=ALU.not_equal, fill=1.0, base=0,
                                           channel_multiplier=1),
         reads=["ident"], writes=["ident"])


def phase0_adaln(K):
    nc, P = K.nc, K.P
    NQ = 3072
    with contextlib.ExitStack() as st:
        c_sb = st.enter_context(nc.sbuf_tensor("c_sb", [128, 16], F32))
        cact = st.enter_context(nc.sbuf_tensor("cact", [128, 16], F32))
        wst = [st.enter_context(nc.sbuf_tensor("wst%d" % i, [128, 16, 512], F32)) for i in range(2)]
        modrow = st.enter_context(nc.sbuf_tensor("modrow", [1, NQ], F32))
        brow = st.enter_context(nc.sbuf_tensor("brow", [1, NQ], F32))
        ps = [st.enter_context(nc.psum_tensor("ps0_%d" % i, [1, 512], F32)) for i in range(2)]
        P.dma("sp", c_sb[:], K.c_arr, writes=["c_sb"])
        P.dma("sp", brow[:], K.b_ada.rearrange("(o n) -> o n", o=1), writes=["brow"])
        P.op("act", lambda e: e.activation(out=cact[:], in_=c_sb[:], func=AF.Silu),
             reads=["c_sb"], writes=["cact"])
        wv = K.w_ada.rearrange("(k p) n -> p k n", p=128)
        for nt in range(NQ // 512):
            sl = nt % 2
            for hh in range(2):
                P.dma("sp", wst[sl][:, hh * 8:(hh + 1) * 8, :],
                      wv[:, hh * 8:(hh + 1) * 8, nt * 512:(nt + 1) * 512],
                      writes=[("wst", sl, hh)])
            for k in range(16):
                P.op("pe", lambda e, k=k, sl=sl: e.matmul(ps[sl][:, :], lhsT=cact[:, k:k + 1],
                                                         rhs=wst[sl][:, k, :], start=(k == 0), stop=(k == 15)),
                     reads=["cact", ("wst", sl, k // 8)], writes=[("ps0", sl)])
            P.op("dve", lambda e, nt=nt, sl=sl: e.tensor_tensor(
                out=modrow[0:1, nt * 512:(nt + 1) * 512], in0=ps[sl][:, :],
                in1=brow[0:1, nt * 512:(nt + 1) * 512], op=ALU.add),
                reads=[("ps0", sl), "brow"], writes=[("modrow", nt)])
        P.dma("sp", K.modq_d, modrow[:],
              reads=[("modrow", nt) for nt in range(NQ // 512)], writes=["modq_d"])
        P.flush()
    P.coll(lambda e: e.collective_compute("AllGather", ALU.bypass, replica_groups=[[0, 1, 2, 3], [4, 5, 6, 7]],
                                          ins=[K.modq_d.opt()], outs=[K.mod4_d.opt()]), reads=["modq_d"], writes=["mod4"])
    P.flush()


def load_mod_rows(K, P, tile, which, gain_ap=None, key=None):
    src = K.mod_d[which * D:(which + 1) * D]
    P.dma("sp", tile[:], bcast_rows(src, D), writes=[key])


def bc(ap, shape):
    return ap.to_broadcast(list(shape))


def load_weight_bf16(K, P, st_tiles, dst, c_dst, src2d, ncols, tag, defer=None):
    wv = src2d.rearrange("(k p) n -> p k n", p=128)
    nk = wv.shape[1]
    engs = ["pool", "dve", "act"]
    for c0 in range(0, ncols, 512):
        n = min(512, ncols - c0)
        for k0 in range(0, nk, 4):
            kn = min(4, nk - k0)
            if defer is not None:
                defer.append(lambda c0=c0, n=n, k0=k0, kn=kn: _load_piece(K, P, st_tiles, dst, c_dst, wv, tag, engs, c0, n, k0, kn))
                continue
            _load_piece(K, P, st_tiles, dst, c_dst, wv, tag, engs, c0, n, k0, kn)
    return [(tag, c0, k0) for c0 in range(0, ncols, 512) for k0 in range(0, nk, 4)]


def _load_piece(K, P, st_tiles, dst, c_dst, wv, tag, engs, c0, n, k0, kn):
    if True:
        if True:
            i = K.wcnt
            K.wcnt += 1
            sl = i % len(st_tiles)
            stg = st_tiles[sl]
            P.dma("sp", stg[:, 0:kn, 0:n], wv[:, k0:k0 + kn, c0:c0 + n], writes=[("wstg", sl)])
            eng = engs[i % 3]
            o = dst[:, k0:k0 + kn, c_dst + c0:c_dst + c0 + n]
            if eng == "act":
                P.op("act", lambda e, o=o, stg=stg, kn=kn, n=n: e.copy(out=o, in_=stg[:, 0:kn, 0:n]),
                     reads=[("wstg", sl)], writes=[(tag, c0, k0)])
            else:
                P.op(eng, lambda e, o=o, stg=stg, kn=kn, n=n: e.tensor_copy(out=o, in_=stg[:, 0:kn, 0:n]),
                     reads=[("wstg", sl)], writes=[(tag, c0, k0)])


def rope_tables(K, P, st, pos_arr, ntile, invf_att, invf_idx, tag):
    nc = K.nc
    posi = st.enter_context(nc.sbuf_tensor(tag + "posi", [128, ntile], I32))
    posf = st.enter_context(nc.sbuf_tensor(tag + "posf", [128, ntile], F32))
    iva = st.enter_context(nc.sbuf_tensor(tag + "iva", [128, 16], F32))
    ivi = st.enter_context(nc.sbuf_tensor(tag + "ivi", [128, 8], F32))
    P.dma("sp", posi[:], pos_arr, writes=[tag + "posi"])
    P.dma("sp", iva[:], invf_att, writes=[tag + "iva"])
    P.dma("sp", ivi[:], invf_idx, writes=[tag + "ivi"])
    P.op("dve", lambda e: e.tensor_copy(out=posf[:], in_=posi[:]), reads=[tag + "posi"], writes=[tag + "posf"])
    out = {}
    for nm, iv, h in (("a", iva, 16), ("i", ivi, 8)):
        u = st.enter_context(nc.sbuf_tensor(tag + "u" + nm, [128, ntile, h], F32))
        ui = st.enter_context(nc.sbuf_tensor(tag + "ui" + nm, [128, ntile, h], I32))
        uf = st.enter_context(nc.sbuf_tensor(tag + "uf" + nm, [128, ntile, h], F32))
        for fn, off in (("sin", 0.0), ("cos", 0.25)):
            tb = st.enter_context(nc.sbuf_tensor(tag + fn + nm, [128, ntile, h], F32))
            kk = tag + fn + nm
            P.op("dve", lambda e, u=u, iv=iv, h=h: e.tensor_tensor(
                out=u[:], in0=bc(posf[:].unsqueeze(2), [128, ntile, h]),
                in1=bc(iv[:].unsqueeze(1), [128, ntile, h]), op=ALU.mult),
                reads=[tag + "posf", tag + "iv" + nm], writes=[tag + "U" + nm])
            P.op("dve", lambda e, u=u, off=off: e.tensor_scalar(
                out=u[:], in0=u[:], scalar1=float(1.0 / (2 * np.pi)), scalar2=off, op0=ALU.mult, op1=ALU.add),
                reads=[tag + "U" + nm], writes=[tag + "U" + nm])
            P.op("dve", lambda e, u=u, ui=ui: e.tensor_copy(out=ui[:], in_=u[:]), reads=[tag + "U" + nm], writes=[tag + "UI" + nm])
            P.op("dve", lambda e, uf=uf, ui=ui: e.tensor_copy(out=uf[:], in_=ui[:]), reads=[tag + "UI" + nm], writes=[tag + "UF" + nm])
            P.op("dve", lambda e, u=u, uf=uf: e.tensor_tensor(out=u[:], in0=u[:], in1=uf[:], op=ALU.subtract),
                 reads=[tag + "U" + nm, tag + "UF" + nm], writes=[tag + "U" + nm])
            P.op("dve", lambda e, u=u: e.tensor_scalar(out=u[:], in0=u[:], scalar1=-0.5, scalar2=0.5,
                                                        op0=ALU.max, op1=ALU.min),
                 reads=[tag + "U" + nm], writes=[tag + "U" + nm])
            P.op("act", lambda e, u=u, tb=tb: e.activation(out=tb[:], in_=u[:], func=AF.Sin,
                                                            scale=float(2 * np.pi)),
                 reads=[tag + "U" + nm], writes=[kk])
            out[fn + nm] = (tb, kk)
    return out


def apply_rope(P, eng, x4, cos, sin, t, half, tmp, rk, wk, sfx=""):
    ctb, ck = cos
    stb, sk = sin
    H = x4.shape[1]
    x1 = x4[:, :, 0:half]
    x2 = x4[:, :, half:2 * half]
    cb = bc(ctb[:, t, :].unsqueeze(1), [128, H, half])
    sb = bc(stb[:, t, :].unsqueeze(1), [128, H, half])
    a, b2, c, d = tmp
    P.op(eng, lambda e: e.tensor_tensor(out=a[:, 0:H, 0:half], in0=x1, in1=cb, op=ALU.mult), reads=rk + [ck], writes=["rtmpA" + sfx])
    P.op(eng, lambda e: e.tensor_tensor(out=b2[:, 0:H, 0:half], in0=x2, in1=sb, op=ALU.mult), reads=rk + [sk], writes=["rtmpB" + sfx])
    P.op(eng, lambda e: e.tensor_tensor(out=c[:, 0:H, 0:half], in0=x2, in1=cb, op=ALU.mult), reads=rk + [ck], writes=["rtmpC" + sfx])
    P.op(eng, lambda e: e.tensor_tensor(out=d[:, 0:H, 0:half], in0=x1, in1=sb, op=ALU.mult), reads=rk + [sk], writes=["rtmpD" + sfx])
    P.op(eng, lambda e: e.tensor_tensor(out=x1, in0=a[:, 0:H, 0:half], in1=b2[:, 0:H, 0:half], op=ALU.subtract),
         reads=["rtmpA" + sfx, "rtmpB" + sfx, "rtmpC" + sfx, "rtmpD" + sfx] + rk, writes=rk)
    P.op(eng, lambda e: e.tensor_tensor(out=x2, in0=c[:, 0:H, 0:half], in1=d[:, 0:H, 0:half], op=ALU.add),
         reads=["rtmpC" + sfx, "rtmpD" + sfx] + rk, writes=rk)


def head_rmsnorm(P, x3, gain, sq, ssum, rk, wk, gk=None, sqk=None):
    P.op("pool", lambda e: e.tensor_tensor(out=sq[:], in0=x3, in1=x3, op=ALU.mult), reads=rk, writes=[sqk or (wk + "sq")])
    P.op("dve", lambda e: e.tensor_reduce(out=ssum[:, 0:8], in_=sq[:], axis=AX.X, op=ALU.add),
         reads=[sqk or (wk + "sq")], writes=[wk + "s0"])
    P.op("dve", lambda e: e.tensor_scalar(out=ssum[:, 8:16], in0=ssum[:, 0:8], scalar1=1.0 / 128, scalar2=1e-6,
                                           op0=ALU.mult, op1=ALU.add), reads=[wk + "s0"], writes=[wk + "s1"])
    P.op("act", lambda e: e.activation(out=ssum[:, 16:24], in_=ssum[:, 8:16], func=AF.Sqrt),
         reads=[wk + "s1"], writes=[wk + "s2"])
    P.op("dve", lambda e: e.reciprocal(out=ssum[:, 24:32], in_=ssum[:, 16:24]), reads=[wk + "s2"], writes=[wk + "s3"])
    P.op("dve", lambda e: e.tensor_tensor(out=x3, in0=x3, in1=bc(ssum[:, 24:32].unsqueeze(2), [128, 8, 128]),
                                           op=ALU.mult), reads=rk + [wk + "s3"], writes=rk)
    P.op("pool", lambda e: e.tensor_tensor(out=x3, in0=x3, in1=bc(gain[:].unsqueeze(1), [128, 8, 128]),
                                            op=ALU.mult), reads=rk + [gk or ("gain" + wk)], writes=rk)


def norm_load(K, P, T, x_src, t):
    xs = t % 2
    P.dma("sp", T["xt"][xs][:], x_src[t * 128:(t + 1) * 128, :], writes=[("xt", xs)])


def norm_block(K, P, T, x_src, t, G1, SH1, ident, blk_hT, ti, load=True, hname="hT"):
    xs = t % 2
    xt, hb, ss, junk, pT = T["xt"], T["hb"], T["ss"], T["junk"], T["pT"]
    if load:
        norm_load(K, P, T, x_src, t)
    P.op("act", lambda e: e.activation(out=junk[:], in_=xt[xs][:], func=AF.Square, accum_out=ss[:, 0:1]),
         reads=[("xt", xs)], writes=["junk", "ss0"])
    P.op("dve", lambda e: e.tensor_scalar(out=ss[:, 1:2], in0=ss[:, 0:1], scalar1=1.0 / D, scalar2=1e-6,
                                           op0=ALU.mult, op1=ALU.add), reads=["ss0"], writes=["ss1"])
    P.op("act", lambda e: e.activation(out=ss[:, 2:3], in_=ss[:, 1:2], func=AF.Sqrt), reads=["ss1"], writes=["ss2"])
    P.op("dve", lambda e: e.reciprocal(out=ss[:, 3:4], in_=ss[:, 2:3]), reads=["ss2"], writes=["ss3"])
    P.op("dve", lambda e: e.scalar_tensor_tensor(out=xt[xs][:], in0=xt[xs][:], scalar=ss[:, 3:4], in1=G1[:],
                                                  op0=ALU.mult, op1=ALU.mult),
         reads=[("xt", xs), "ss3", "G"], writes=[("xt", xs)])
    P.op("pool", lambda e: e.tensor_tensor(out=hb[xs][:], in0=xt[xs][:], in1=SH1[:], op=ALU.add),
         reads=[("xt", xs), "SH"], writes=[("hb", xs)])
    for half in range(2):
        for kk in range(8):
            k = half * 8 + kk
            P.op("pe", lambda e, k=k, kk=kk, half=half: e.transpose(
                out=pT[half][:, kk, :], in_=hb[xs][:, k * 128:(k + 1) * 128], identity=ident[:]),
                reads=[("hb", xs), "ident"], writes=[("pT", half)])
        o = blk_hT[:, half * 8:(half + 1) * 8, ti * 128:(ti + 1) * 128]
        if half == 0:
            P.op("act", lambda e, o=o, half=half: e.copy(out=o, in_=pT[half][:]),
                 reads=[("pT", half)], writes=[(hname, ti, half)])
        else:
            P.op("dve", lambda e, o=o, half=half: e.tensor_copy(out=o, in_=pT[half][:]),
                 reads=[("pT", half)], writes=[(hname, ti, half)])


def norm_tiles_alloc(K, st, tag):
    nc = K.nc
    T = {}
    T["xt"] = [st.enter_context(nc.sbuf_tensor(tag + "xt%d" % i, [128, D], F32)) for i in range(2)]
    T["hb"] = [st.enter_context(nc.sbuf_tensor(tag + "hb%d" % i, [128, D], BF16)) for i in range(2)]
    T["ss"] = st.enter_context(nc.sbuf_tensor(tag + "ss", [128, 4], F32))
    T["junk"] = st.enter_context(nc.sbuf_tensor(tag + "junk", [128, D], BF16))
    T["pT"] = [st.enter_context(nc.psum_tensor(tag + "pT%d" % i, [128, 8, 128], BF16)) for i in range(2)]
    return T


def load_G_SH(K, P, st, which_sh, which_sc, gain_vec, tag):
    nc = K.nc
    G = st.enter_context(nc.sbuf_tensor(tag + "G", [128, D], F32))
    SH = st.enter_context(nc.sbuf_tensor(tag + "SH", [128, D], F32))
    gtmp = st.enter_context(nc.sbuf_tensor(tag + "gtmp", [128, D], F32))
    P.dma("sp", SH[:], bcast_rows(K.mod_d[which_sh * D:(which_sh + 1) * D], D), writes=["SH"])
    P.dma("sp", G[:], bcast_rows(K.mod_d[which_sc * D:(which_sc + 1) * D], D), writes=["G"])
    P.dma("sp", gtmp[:], bcast_rows(gain_vec, D), writes=["gtmp"])
    P.op("dve", lambda e: e.scalar_tensor_tensor(out=G[:], in0=G[:], scalar=1.0, in1=gtmp[:],
                                                  op0=ALU.add, op1=ALU.mult), reads=["G", "gtmp"], writes=["G"])
    return G, SH


def phase1_kv(K):
    nc, P = K.nc, K.P
    with contextlib.ExitStack() as st:
        ident = st.enter_context(nc.sbuf_tensor("ident", [128, 128], BF16))
        make_ident(K, P, ident)
        G1, SH1 = load_G_SH(K, P, st, 0, 1, K.norm1_g, "p1")
        T = norm_tiles_alloc(K, st, "p1")
        hT = [st.enter_context(nc.sbuf_tensor("hT%d" % i, [128, 16, 512], BF16)) for i in range(2)]
        W = st.enter_context(nc.sbuf_tensor("Wkv", [128, 16, 2112], BF16))
        stg = [st.enter_context(nc.sbuf_tensor("wstg%d" % i, [128, 4, 512], F32)) for i in range(2)]
        wk_k = load_weight_bf16(K, P, stg, W, 0, K.w_in[:, 1024:2048], 1024, "Wk")
        wk_v = load_weight_bf16(K, P, stg, W, 1024, K.w_in[:, 2048:3072], 1024, "Wv")
        wk_i = load_weight_bf16(K, P, stg, W, 2048, K.w_in[:, 4096:4160], 64, "Wi")
        rt = rope_tables(K, P, st, K.pos_full, 32, K.invf_att, K.invf_idx, "rf")
        gain = st.enter_context(nc.sbuf_tensor("kgain", [128, 128], F32))
        P.dma("sp", gain[:], bcast_rows(K.k_norm_g, 128), writes=["gainK"])
        def two(name, shape, dt):
            return [st.enter_context(nc.sbuf_tensor(name + str(i), shape, dt)) for i in range(2)]
        ksb2 = two("ksb", [128, 8, 128], F32)
        kbf2 = two("kbf", [128, 8, 128], BF16)
        sq2 = [st.enter_context(nc.sbuf_tensor("sq", [128, 8, 128], F32))] * 2
        ssum2 = two("ssum", [128, 32], F32)
        rtmp2 = [[st.enter_context(nc.sbuf_tensor("rtmp%d" % i, [128, 8, 16], F32)) for i in range(4)]] * 2
        vsb2 = two("vsb", [128, 8, 129], BF16)
        iksb2 = two("iksb", [128, 1, 64], F32)
        ikbf2 = two("ikbf", [128, 64], BF16)
        kTs2 = [st.enter_context(nc.sbuf_tensor("kTs", [128, 8, 128], BF16))] * 2
        ikTs2 = two("ikTs", [64, 128], BF16)
        pm = [st.enter_context(nc.psum_tensor("pm%d" % i, [128, 512], F32)) for i in range(3)]
        pk = st.enter_context(nc.psum_tensor("pk", [128, 8, 128], BF16))
        for s_ in range(2):
            P.op("pool", lambda e, s_=s_: e.memset(vsb2[s_][:], 1.0), writes=["vsb%d" % s_])
        norm_load(K, P, T, K.x_full, 0)

        def norm_tile(blk, ti):
            tt_ = blk * 4 + ti
            if tt_ + 1 < 32:
                norm_load(K, P, T, K.x_full, tt_ + 1)
            norm_block(K, P, T, K.x_full, tt_, G1, SH1, ident, hT[blk % 2], ti, load=False, hname=("hT", blk % 2))

        def store_hT(blk):
            hs = blk % 2
            hkeys = [(("hT", hs), ti, half) for ti in range(4) for half in range(2)]
            P.dma("sp", K.hT_d.rearrange("k p t -> p k t")[:, :, blk * 512:(blk + 1) * 512], hT[hs][:],
                  reads=hkeys, writes=[("hT_d", blk)])

        def bufs(t):
            u = t % 2
            return (str(u), ksb2[u], kbf2[u], sq2[u], ssum2[u], rtmp2[u], vsb2[u], iksb2[u], ikbf2[u], kTs2[u], ikTs2[u])

        def mm_tile(blk, ti):
            t = blk * 4 + ti
            hs = blk % 2
            hk = [(("hT", hs), ti, 0), (("hT", hs), ti, 1)]
            us, ksb, kbf, sq, ssum, rtmp, vsb, iksb, ikbf, kTs, ikTs = bufs(t)
            for gi, (c0, n, wkeys) in enumerate([(0, 512, wk_k), (512, 512, wk_k), (1024, 512, wk_v),
                                                 (1536, 512, wk_v), (2048, 64, wk_i)]):
                pb = pm[gi % 3]
                for k in range(16):
                    P.op("pe", lambda e, pb=pb, k=k, c0=c0, n=n, ti=ti, hs=hs: e.matmul(
                        pb[:, 0:n], lhsT=hT[hs][:, k, ti * 128:(ti + 1) * 128], rhs=W[:, k, c0:c0 + n],
                        start=(k == 0), stop=(k == 15)), reads=hk + wkeys, writes=[("pm", gi % 3)])
                if gi < 2:
                    P.op("act", lambda e, pb=pb, gi=gi, ksb=ksb: e.copy(out=ksb[:, gi * 4:(gi + 1) * 4, :], in_=pb[:, 0:512]),
                         reads=[("pm", gi % 3)], writes=["ksb" + us])
                elif gi < 4:
                    g2 = gi - 2
                    P.op("act", lambda e, pb=pb, g2=g2, vsb=vsb: e.copy(out=vsb[:, g2 * 4:(g2 + 1) * 4, 0:128], in_=pb[:, 0:512]),
                         reads=[("pm", gi % 3)], writes=["vsb" + us])
                else:
                    P.op("act", lambda e, pb=pb, iksb=iksb: e.copy(out=iksb[:, 0, :], in_=pb[:, 0:64]),
                         reads=[("pm", gi % 3)], writes=["iksb" + us])
            P.dma("sp", K.v_d[t * 128:(t + 1) * 128, :], vsb[:].rearrange("p h d -> p (h d)"),
                  reads=["vsb" + us], writes=[("v_d", t)])

        def post1(blk, ti):
            t = blk * 4 + ti
            us, ksb, kbf, sq, ssum, rtmp, vsb, iksb, ikbf, kTs, ikTs = bufs(t)
            head_rmsnorm(P, ksb[:], gain, sq, ssum, ["ksb" + us], "K" + us, gk="gainK", sqk="Ksq")
            apply_rope(P, "dve", ksb[:], rt["cosa"], rt["sina"], t, 16, rtmp, ["ksb" + us], "rK")
            P.op("act", lambda e, kbf=kbf, ksb=ksb: e.copy(out=kbf[:], in_=ksb[:]), reads=["ksb" + us], writes=["kbf" + us])
            apply_rope(P, "pool", iksb[:], rt["cosi"], rt["sini"], t, 8, rtmp, ["iksb" + us], "rI")
            P.op("act", lambda e, ikbf=ikbf, iksb=iksb: e.copy(out=ikbf[:], in_=iksb[:, 0, :]), reads=["iksb" + us], writes=["ikbf" + us])

        def post2(blk, ti):
            t = blk * 4 + ti
            us, ksb, kbf, sq, ssum, rtmp, vsb, iksb, ikbf, kTs, ikTs = bufs(t)
            for h in range(8):
                P.op("pe", lambda e, h=h, kbf=kbf: e.transpose(out=pk[:, h, :], in_=kbf[:, h, :], identity=ident[:]),
                     reads=["kbf" + us, "ident"], writes=["pk"])
            P.op("dve", lambda e, kTs=kTs: e.tensor_copy(out=kTs[:], in_=pk[:]), reads=["pk"], writes=["kTs"])
            P.dma("sp", K.kT_d.rearrange("h p t -> p h t")[:, :, t * 128:(t + 1) * 128], kTs[:],
                  reads=["kTs"], writes=[("kT_d", t)])
            P.op("pe", lambda e, ikbf=ikbf: e.transpose(out=pk[0:64, 0, :], in_=ikbf[:], identity=ident[:]),
                 reads=["ikbf" + us, "ident"], writes=["pk"])
            P.op("dve", lambda e, ikTs=ikTs: e.tensor_copy(out=ikTs[:], in_=pk[0:64, 0, :]), reads=["pk"], writes=["ikTs" + us])
            P.dma("sp", K.ikT_d[:, t * 128:(t + 1) * 128], ikTs[:], reads=["ikTs" + us], writes=[("ikT_d", t)])

        for ti in range(4):
            norm_tile(0, ti)
        store_hT(0)
        prev = None
        for blk in range(8):
            for ti in range(4):
                mm_tile(blk, ti)
                if blk + 1 < 8:
                    norm_tile(blk + 1, ti)
                post1(blk, ti)
                if prev is not None:
                    post2(*prev)
                prev = (blk, ti)
            if blk + 1 < 8:
                store_hT(blk + 1)
        post2(*prev)
        P.flush()

RW0 = 4176
NRW = 1216
RW_GROUPS = [(i * 128, 128) for i in range(6)] + [(768, 96), (864, 96), (960, 128), (1088, 128)]


def phase1b_rwkv_proj(K):
    nc, P = K.nc, K.P
    with contextlib.ExitStack() as st:
        W = st.enter_context(nc.sbuf_tensor("Wr", [128, 16, NRW], BF16))
        stg = [st.enter_context(nc.sbuf_tensor("wstgb%d" % i, [128, 4, 512], F32)) for i in range(2)]
        hT = [st.enter_context(nc.sbuf_tensor("hTb%d" % i, [128, 16, 512], BF16)) for i in range(2)]
        ost = [st.enter_context(nc.sbuf_tensor("ost%d" % i, [128, 512], F32)) for i in range(4)]
        pm = [st.enter_context(nc.psum_tensor("pmb%d" % i, [128, 512], F32)) for i in range(4)]
        wkeys = load_weight_bf16(K, P, stg, W, 0, K.w_in_rw, NRW, "Wr")
        cnt = 0
        for blk in range(8):
            hs = blk % 2
            P.dma("sp", hT[hs][:], K.hT_d.rearrange("k p t -> p k t")[:, :, blk * 512:(blk + 1) * 512],
                  writes=[("hTb", hs)])
            for (r0, m) in RW_GROUPS:
                s4 = cnt % 4
                cnt += 1
                for k in range(16):
                    P.op("pe", lambda e, k=k, r0=r0, m=m, hs=hs, s4=s4: e.matmul(
                        pm[s4][0:m, :], lhsT=W[:, k, r0:r0 + m], rhs=hT[hs][:, k, :],
                        start=(k == 0), stop=(k == 15)), reads=[("hTb", hs)] + wkeys, writes=[("pmb", s4)])
                if cnt % 2 == 0:
                    P.op("act", lambda e, m=m, s4=s4: e.copy(out=ost[s4][0:m, :], in_=pm[s4][0:m, :]),
                         reads=[("pmb", s4)], writes=[("ost", s4)])
                else:
                    P.op("dve", lambda e, m=m, s4=s4: e.tensor_copy(out=ost[s4][0:m, :], in_=pm[s4][0:m, :]),
                         reads=[("pmb", s4)], writes=[("ost", s4)])
                P.dma("sp", K.yT_d[r0:r0 + m, blk * 512:(blk + 1) * 512], ost[s4][0:m, :],
                      reads=[("ost", s4)], writes=[("yT_d", r0, blk)])
        P.flush()


def phase2_own_proj(K):
    nc, P = K.nc, K.P
    with contextlib.ExitStack() as st:
        ident = st.enter_context(nc.sbuf_tensor("ident2", [128, 128], BF16))
        make_ident(K, P, ident)
        G1, SH1 = load_G_SH(K, P, st, 0, 1, K.norm1_g, "p2")
        T = norm_tiles_alloc(K, st, "p2")
        hT = [st.enter_context(nc.sbuf_tensor("hTo%d" % i, [128, 16, 512], BF16)) for i in range(2)]
        W = st.enter_context(nc.sbuf_tensor("Wq", [128, 16, 2064], BF16))
        stg = [st.enter_context(nc.sbuf_tensor("wstgq%d" % i, [128, 4, 512], F32)) for i in range(2)]
        wk_q = load_weight_bf16(K, P, stg, W, 0, K.w_in[:, 0:1024], 1024, "Wq")
        wk_iq = load_weight_bf16(K, P, stg, W, 1024, K.w_in[:, 3072:4096], 1024, "Wiq")
        wk_iw = load_weight_bf16(K, P, stg, W, 2048, K.w_in[:, 4160:4176], 16, "Wiw")
        rt = rope_tables(K, P, st, K.pos_own, 8, K.invf_att, K.invf_idx, "ro")
        gain = st.enter_context(nc.sbuf_tensor("qgain", [128, 128], F32))
        P.dma("sp", gain[:], bcast_rows(K.q_norm_g, 128), writes=["gainQ"])
        qsb = st.enter_context(nc.sbuf_tensor("qsb", [128, 8, 128], F32))
        qbf = st.enter_context(nc.sbuf_tensor("qbf", [128, 8, 128], BF16))
        sq = st.enter_context(nc.sbuf_tensor("sq2", [128, 8, 128], F32))
        ssum = st.enter_context(nc.sbuf_tensor("ssum2", [128, 32], F32))
        rtmp = [st.enter_context(nc.sbuf_tensor("rtmpq%d" % i, [128, 16, 16], F32)) for i in range(4)]
        iqsb = st.enter_context(nc.sbuf_tensor("iqsb", [128, 16, 64], F32))
        iqbf = st.enter_context(nc.sbuf_tensor("iqbf", [128, 16, 64], BF16))
        iwsb = st.enter_context(nc.sbuf_tensor("iwsb", [128, 16], F32))
        qTs = st.enter_context(nc.sbuf_tensor("qTs", [128, 8, 128], BF16))
        iqTs = st.enter_context(nc.sbuf_tensor("iqTs", [64, 128, 16], BF16))
        pm = [st.enter_context(nc.psum_tensor("pmq%d" % i, [128, 512], F32)) for i in range(3)]
        pk = st.enter_context(nc.psum_tensor("pkq", [128, 8, 128], BF16))
        for blk in range(2):
            hs = blk % 2
            for ti in range(4):
                norm_block(K, P, T, K.x_own, blk * 4 + ti, G1, SH1, ident, hT[hs], ti)
            for ti in range(4):
                t = blk * 4 + ti
                hk = [("hT", ti, 0), ("hT", ti, 1)]
                for gi, (c0, n, wkeys) in enumerate([(0, 512, wk_q), (512, 512, wk_q), (1024, 512, wk_iq),
                                                     (1536, 512, wk_iq), (2048, 16, wk_iw)]):
                    pb = pm[gi % 3]
                    for k in range(16):
                        P.op("pe", lambda e, pb=pb, k=k, c0=c0, n=n, ti=ti, hs=hs: e.matmul(
                            pb[:, 0:n], lhsT=hT[hs][:, k, ti * 128:(ti + 1) * 128], rhs=W[:, k, c0:c0 + n],
                            start=(k == 0), stop=(k == 15)), reads=hk + wkeys, writes=[("pmq", gi % 3)])
                    if gi < 2:
                        P.op("act", lambda e, pb=pb, gi=gi: e.copy(out=qsb[:, gi * 4:(gi + 1) * 4, :], in_=pb[:, 0:512]),
                             reads=[("pmq", gi % 3)], writes=["qsb"])
                    elif gi < 4:
                        g2 = gi - 2
                        P.op("act", lambda e, pb=pb, g2=g2: e.copy(out=iqsb[:, g2 * 8:(g2 + 1) * 8, :], in_=pb[:, 0:512]),
                             reads=[("pmq", gi % 3)], writes=["iqsb"])
                    else:
                        P.op("act", lambda e, pb=pb: e.activation(out=iwsb[:], in_=pb[:, 0:16], func=AF.Copy, scale=0.25),
                             reads=[("pmq", gi % 3)], writes=["iwsb"])
                P.dma("sp", K.iw_d[t * 128:(t + 1) * 128, :], iwsb[:], reads=["iwsb"], writes=[("iw_d", t)])
                head_rmsnorm(P, qsb[:], gain, sq, ssum, ["qsb"], "Q")
                apply_rope(P, "dve", qsb[:], rt["cosa"], rt["sina"], t, 16, rtmp, ["qsb"], "rQ")
                P.op("act", lambda e: e.copy(out=qbf[:], in_=qsb[:]), reads=["qsb"], writes=["qbf"])
                for h in range(8):
                    P.op("pe", lambda e, h=h: e.transpose(out=pk[:, h, :], in_=qbf[:, h, :], identity=ident[:]),
                         reads=["qbf", "ident"], writes=["pkq"])
                P.op("dve", lambda e: e.tensor_copy(out=qTs[:], in_=pk[:]), reads=["pkq"], writes=["qTs"])
                P.dma("sp", K.qT_d.rearrange("h p t -> p h t")[:, :, t * 128:(t + 1) * 128], qTs[:],
                      reads=["qTs"], writes=[("qT_d", t)])
                apply_rope(P, "pool", iqsb[:], rt["cosi"], rt["sini"], t, 8, rtmp, ["iqsb"], "rIQ")
                P.op("act", lambda e: e.activation(out=iqbf[:], in_=iqsb[:], func=AF.Copy, scale=0.125),
                     reads=["iqsb"], writes=["iqbf"])
                for half in range(2):
                    for hh in range(8):
                        h = half * 8 + hh
                        P.op("pe", lambda e, h=h, hh=hh: e.transpose(out=pk[0:64, hh, :], in_=iqbf[:, h, :],
                                                                      identity=ident[:]),
                             reads=["iqbf", "ident"], writes=["pkq"])
                    P.op("dve", lambda e, half=half: e.tensor_copy(
                        out=iqTs[:, :, half * 8:(half + 1) * 8].rearrange("p t h -> p h t"), in_=pk[0:64, :, :]),
                         reads=["pkq"], writes=["iqTs"])
                P.dma("sp", K.iqT_d[:, t * 128:(t + 1) * 128, :], iqTs[:], reads=["iqTs"], writes=[("iqT_d", t)])
        P.flush()


NIT = 24
SLOT_NK = [4, 8, 12, 16, 20, 24, 28, 32]


def phase3_attention(K):
    nc, P = K.nc, K.P
    with contextlib.ExitStack() as st:
        def sb(name, shape, dt):
            return st.enter_context(nc.sbuf_tensor(name, shape, dt))
        ident = sb("ident3", [128, 128], BF16)
        identf = sb("identf3", [128, 128], F32)
        make_ident(K, P, ident)
        P.op("dve", lambda e: e.tensor_copy(out=identf[:], in_=ident[:]), reads=["ident"], writes=["identf"])
        kT = sb("kTall", [128, 8, S], BF16)
        V = sb("Vall", [128, 32, 1032], BF16)
        ikT = sb("ikTall", [64, S], BF16)
        for h in range(8):
            P.dma("sp", kT[:, h, :], K.kT_d[h], writes=[("kT", h)])
        for q4 in range(4):
            P.dma("sp", V[:, q4 * 8:(q4 + 1) * 8, :],
                  K.v_d.rearrange("(t p) c -> p t c", p=128)[:, q4 * 8:(q4 + 1) * 8, :], writes=[("V", q4)])
        P.dma("sp", ikT[:], K.ikT_d, writes=["ikT"])
        kTk = [("kT", h) for h in range(8)]
        Vk = [("V", q4) for q4 in range(4)]
        Sel = sb("Sel", [128, 16, 128], BF16)
        pidx = sb("pidx", [128, 1], I32)
        pidf = sb("pidf", [128, 1], F32)
        score = sb("score", [128, S], F32)
        self_ = score[:, 0:2048].rearrange("p (g t) -> p g t", g=16)
        sk4 = [("score", q) for q in range(4)]
        P.op("pool", lambda e: e.iota(self_, pattern=[[-8, 16], [1, 128]], base=0, channel_multiplier=0, allow_small_or_imprecise_dtypes=True), writes=sk4)
        P.op("pool", lambda e: e.iota(pidx[:], pattern=[[0, 1]], base=0, channel_multiplier=1), writes=["pidx"])
        P.op("dve", lambda e: e.tensor_scalar(out=pidx[:], in0=pidx[:], scalar1=4, scalar2=None,
                                               op0=ALU.arith_shift_right), reads=["pidx"], writes=["pidx"])
        P.op("dve", lambda e: e.tensor_copy(out=pidf[:], in_=pidx[:]), reads=["pidx"], writes=["pidf"])
        P.op("dve", lambda e: e.tensor_scalar(out=Sel[:], in0=self_, scalar1=pidf[:, 0:1], scalar2=None,
                                               op0=ALU.is_equal), reads=sk4 + ["pidf"], writes=["Sel"])
        kposi = sb("kposi", [128, 512], I32)
        kposf = sb("kposf", [128, 512], F32)
        qpos = sb("qpos", [128, 8], F32)
        P.dma("sp", qpos[:], K.qpos_own, writes=["qpos"])
        iwg = sb("iwg", [128, 128], F32)
        wcol = sb("wcol", [128, 128], F32)
        P.dma("sp", iwg[:], K.iw_d.rearrange("(g t) h -> g (t h)", t=8), writes=["iwg"])
        A = [st.enter_context(nc.psum_tensor("A%d" % i, [128, 512], F32)) for i in range(2)]
        B = [st.enter_context(nc.psum_tensor("B%d" % i, [128, 512], F32)) for i in range(2)]
        C = st.enter_context(nc.psum_tensor("C3", [128, 8, 128], BF16))
        P.op("pe", lambda e: e.transpose(out=A[0][:, 0:128], in_=iwg[:], identity=identf[:]),
             reads=["iwg", "identf"], writes=[("A", 0)])
        P.op("dve", lambda e: e.tensor_copy(out=wcol[:], in_=A[0][:, 0:128]), reads=[("A", 0)], writes=["wcol"])
        mask01 = sb("mask01", [128, S], BF16)
        maskT = sb("maskT", [128, 32, 128], BF16)
        R = [sb("R%d" % i, [128, 512], BF16) for i in range(2)]
        pexp = [sb("pexp%d" % i, [128, 512], BF16) for i in range(2)]
        pmk = [sb("pmk%d" % i, [128, 512], BF16) for i in range(2)]
        iqTs = sb("iqTs3", [64, 128, 16], BF16)
        qTs = sb("qTs3", [128, 8, 128], BF16)
        att = sb("att", [128, 8, 128], BF16)
        attTs = sb("attTs", [128, 8, 128], BF16)
        bias = sb("cbias", [128, 512], F32)
        c2 = sb("c2", [128, NIT], F32)
        steps = sb("steps", [128, NIT], F32)
        sm = sb("sm3", [128, 8], F32)
        for k in range(NIT):
            P.op("pool", lambda e, k=k: e.memset(c2[:, k:k + 1], float(2.0 ** -(k + 1))), writes=["c2"])
        for i in range(8):
            nk = SLOT_NK[i]
            nb = nk // 4
            L = nk * 128
            P.dma("sp", iqTs[:], K.iqT_d[:, i * 128:(i + 1) * 128, :], writes=["iqTs"])
            P.dma("sp", qTs[:], K.qT_d.rearrange("h p t -> p h t")[:, :, i * 128:(i + 1) * 128], writes=["qTs"])
            isteps = [(sbk, g) for sbk in range(nb) for g in range(16)]

            def dots(si):
                sbk, g = isteps[si]
                a = si % 2
                lhsT = iqTs[:, g * 8:(g + 1) * 8, :].rearrange("p t h -> p (t h)")
                P.op("pe", lambda e, a=a, lhsT=lhsT, sbk=sbk: e.matmul(
                    A[a][:, :], lhsT=lhsT, rhs=ikT[:, sbk * 512:(sbk + 1) * 512], start=True, stop=True),
                    reads=["iqTs", "ikT"], writes=[("A", a)])
            dots(0)
            for si, (sbk, g) in enumerate(isteps):
                a = si % 2
                bsl = sbk % 2
                if si + 1 < len(isteps):
                    dots(si + 1)
                G = i * 16 + g
                P.op("dve", lambda e, a=a, G=G: e.tensor_scalar(
                    out=R[a][:], in0=A[a][:, :], scalar1=0.0, scalar2=wcol[:, G:G + 1],
                    op0=ALU.max, op1=ALU.mult), reads=[("A", a), "wcol"], writes=[("R", a)])
                P.op("pe", lambda e, a=a, g=g, bsl=bsl: e.matmul(
                    B[bsl][:, :], lhsT=Sel[:, g, :], rhs=R[a][:], start=(g == 0), stop=(g == 15)),
                    reads=[("R", a), "Sel"], writes=[("B", bsl)])
                if g == 15:
                    P.op("act", lambda e, bsl=bsl, sbk=sbk: e.copy(out=score[:, sbk * 512:(sbk + 1) * 512], in_=B[bsl][:, :]),
                         reads=[("B", bsl)], writes=[("score", sbk)])
            sck = [("score", sbk) for sbk in range(nb)]
            P.op("dve", lambda e, L=L: e.tensor_reduce(out=sm[:, 0:1], in_=score[:, 0:L], axis=AX.X, op=ALU.max,
                                                        apply_absolute_value=True), reads=sck, writes=["sm0"])
            P.op("pool", lambda e, nb=nb: e.iota(kposi[:], pattern=[[1, 512]], base=(nb - 1) * 512, channel_multiplier=0),
                 writes=["kposi"])
            P.op("dve", lambda e: e.tensor_copy(out=kposf[:], in_=kposi[:]), reads=["kposi"], writes=["kposf"])
            P.op("dve", lambda e, i=i: e.tensor_scalar(out=bias[:], in0=kposf[:], scalar1=qpos[:, i:i + 1],
                                                        scalar2=-1e30, op0=ALU.is_gt, op1=ALU.mult),
                 reads=["kposf", "qpos"], writes=["bias"])
            P.op("dve", lambda e, nb=nb: e.tensor_tensor(out=score[:, (nb - 1) * 512:nb * 512],
                                                          in0=score[:, (nb - 1) * 512:nb * 512], in1=bias[:], op=ALU.add),
                 reads=["bias", ("score", nb - 1), "sm0"], writes=[("score", nb - 1)])
            P.op("dve", lambda e: e.tensor_scalar(out=sm[:, 1:2], in0=sm[:, 0:1], scalar1=-1.0, scalar2=-1.0,
                                                   op0=ALU.mult, op1=ALU.add), reads=["sm0"], writes=["lo"])
            P.op("dve", lambda e: e.tensor_scalar(out=sm[:, 5:6], in0=sm[:, 0:1], scalar1=2.0, scalar2=2.0,
                                                   op0=ALU.mult, op1=ALU.add), reads=["sm0"], writes=["d0"])
            P.op("dve", lambda e: e.tensor_scalar(out=steps[:], in0=c2[:], scalar1=sm[:, 5:6], scalar2=None,
                                                   op0=ALU.mult), reads=["d0", "c2"], writes=["steps"])
            for k in range(NIT):
                P.op("dve", lambda e, k=k: e.tensor_tensor(out=sm[:, 2:3], in0=sm[:, 1:2], in1=steps[:, k:k + 1],
                                                            op=ALU.add), reads=["lo", "steps"], writes=["mid"])
                P.op("dve", lambda e, L=L: e.tensor_scalar(out=mask01[:, 0:L], in0=score[:, 0:L], scalar1=sm[:, 2:3],
                                                            scalar2=None, op0=ALU.is_ge, op1=ALU.add,
                                                            accum_out=sm[:, 3:4]),
                     reads=sck + ["mid"], writes=["mask01", "cnt"])
                P.op("dve", lambda e, k=k: e.scalar_tensor_tensor(out=sm[:, 4:5], in0=sm[:, 3:4], scalar=255.5,
                                                                   in1=steps[:, k:k + 1], op0=ALU.is_ge, op1=ALU.mult),
                     reads=["cnt", "steps"], writes=["inc"])
                P.op("dve", lambda e: e.tensor_tensor(out=sm[:, 1:2], in0=sm[:, 1:2], in1=sm[:, 4:5], op=ALU.add),
                     reads=["lo", "inc"], writes=["lo"])
            P.op("dve", lambda e, L=L: e.tensor_scalar(out=mask01[:, 0:L], in0=score[:, 0:L], scalar1=sm[:, 1:2],
                                                        scalar2=None, op0=ALU.is_ge), reads=sck + ["lo"], writes=["mask01"])
            for kt in range(nk):
                P.op("pe", lambda e, kt=kt: e.transpose(out=C[:, kt % 8, :], in_=mask01[:, kt * 128:(kt + 1) * 128],
                                                         identity=ident[:]), reads=["mask01", "ident"], writes=["C"])
                if kt % 8 == 7 or kt == nk - 1:
                    k0 = (kt // 8) * 8
                    n8 = kt - k0 + 1
                    P.op("act", lambda e, k0=k0, n8=n8: e.copy(out=maskT[:, k0:k0 + n8, :], in_=C[:, 0:n8, :]),
                         reads=["C"], writes=[("maskT", k0 // 8)])
            mk = [("maskT", q) for q in range((nk + 7) // 8)]
            asteps = [(h, kg) for h in range(8) for kg in range(nb)]

            def qk(si):
                h, kg = asteps[si]
                a = si % 2
                for j4 in range(4):
                    kt = kg * 4 + j4
                    P.op("pe", lambda e, a=a, j4=j4, kt=kt, h=h: e.matmul(
                        A[a][:, j4 * 128:(j4 + 1) * 128], lhsT=kT[:, h, kt * 128:(kt + 1) * 128], rhs=qTs[:, h, :],
                        start=True, stop=True), reads=kTk + ["qTs"], writes=[("A", a)])
            qk(0)
            for si, (h, kg) in enumerate(asteps):
                a = si % 2
                bsl = h % 2
                if si + 1 < len(asteps):
                    qk(si + 1)
                P.op("act", lambda e, a=a: e.activation(out=pexp[a][:], in_=A[a][:, :], func=AF.Exp,
                                                         scale=float(128 ** -0.5)),
                     reads=[("A", a)], writes=[("pexp", a)])
                P.op("dve", lambda e, a=a, kg=kg: e.tensor_tensor(
                    out=pmk[a][:], in0=pexp[a][:], in1=maskT[:, kg * 4:(kg + 1) * 4, :].rearrange("p a t -> p (a t)"),
                    op=ALU.mult), reads=[("pexp", a)] + mk, writes=[("pmk", a)])
                for j4 in range(4):
                    kt = kg * 4 + j4
                    P.op("pe", lambda e, a=a, j4=j4, kt=kt, h=h, bsl=bsl, kg=kg, nb=nb: e.matmul(
                        B[bsl][:, 0:129], lhsT=pmk[a][:, j4 * 128:(j4 + 1) * 128], rhs=V[:, kt, h * 129:(h + 1) * 129],
                        start=(kg == 0 and j4 == 0), stop=(kg == nb - 1 and j4 == 3)),
                        reads=[("pmk", a)] + Vk, writes=[("B", bsl)])
                if kg == nb - 1:
                    P.op("dve", lambda e, bsl=bsl: e.reciprocal(out=sm[:, 6:7], in_=B[bsl][:, 128:129]),
                         reads=[("B", bsl)], writes=["rcp"])
                    P.op("dve", lambda e, bsl=bsl, h=h: e.tensor_scalar(out=att[:, h, :], in0=B[bsl][:, 0:128],
                                                                         scalar1=sm[:, 6:7], scalar2=None, op0=ALU.mult),
                         reads=[("B", bsl), "rcp"], writes=["att"])
            for h in range(8):
                P.op("pe", lambda e, h=h: e.transpose(out=C[:, h, :], in_=att[:, h, :], identity=ident[:]),
                     reads=["att", "ident"], writes=["C"])
            P.op("act", lambda e: e.copy(out=attTs[:], in_=C[:]), reads=["C"], writes=["attTs"])
            P.dma("sp", K.attT_d.rearrange("h p t -> p h t")[:, :, i * 128:(i + 1) * 128], attTs[:],
                  reads=["attTs"], writes=[("attT_d", i)])
        P.flush()

RD = BF16
NCH = 64


def tok_shift(P, dst, raw, tmp, mu_ap, rk_raw, k_tmp, k_dst, n=128):
    P.op("pool", lambda e: e.tensor_tensor(out=tmp[0:n, 1:S], in0=raw[0:n, 0:S - 1], in1=raw[0:n, 1:S], op=ALU.subtract),
         reads=[rk_raw], writes=[k_tmp])
    P.op("pool", lambda e: e.tensor_scalar(out=tmp[0:n, 0:1], in0=raw[0:n, 0:1], scalar1=-1.0, scalar2=0.0,
                                            op0=ALU.mult, op1=ALU.add), reads=[rk_raw, k_tmp], writes=[k_tmp])
    P.op("dve", lambda e: e.scalar_tensor_tensor(out=dst[0:n, :], in0=tmp[0:n, :], scalar=mu_ap, in1=raw[0:n, :],
                                                  op0=ALU.mult, op1=ALU.add), reads=[rk_raw, k_tmp], writes=[k_dst])


def phase4b_rwkv_prep(K, cts=range(2)):
    nc, P = K.nc, K.P
    with contextlib.ExitStack() as st:
        def sb(name, shape, dt):
            return st.enter_context(nc.sbuf_tensor(name, shape, dt))
        txw = sb("txw", [96, S], BF16)
        xap = sb("xap", [96, S], BF16)
        sxg = sb("sxg", [128, 2, S], BF16)
        M01 = sb("M01", [128, S], BF16)
        wup = sb("wup", [96, 256], BF16)
        aup = sb("aup", [96, 256], BF16)
        gup = sb("gup", [128, 2, 256], BF16)
        wst = sb("wst4", [128, 2, 256], F32)
        bones = sb("bones", [128, 128], BF16)
        prm = sb("prm", [128, 12, 2], F32)
        mul = sb("mul", [128, 4], F32)
        PT = sb("PT", [128, S], F32)
        KK = sb("KK", [128, S], F32)
        KP = sb("KP", [128, S], F32)
        CL = sb("CL", [128, S], F32)
        RP = sb("RP", [128, S], BF16)
        VP = sb("VP", [128, S], BF16)
        AA = sb("AA", [128, S], BF16)
        K2 = sb("K2", [128, S], BF16)
        SQb = sb("SQb", [128, S], BF16)
        OUT = [sb("OUT%d" % i, [128, S], BF16) for i in range(2)]
        PCt = sb("PCt", [128, NCH], F32)
        ps = [st.enter_context(nc.psum_tensor("ps4_%d" % i, [128, 512], F32)) for i in range(4)]
        for i, ap in enumerate(K.rw_prm):
            P.dma("sp", prm[:, i, :], ap, writes=[("prm", i)])
        prk = [("prm", i) for i in range(10)]
        P.op("dve", lambda e: e.tensor_scalar(out=prm[:, 10, :], in0=prm[:, 6, :], scalar1=-1.0, scalar2=1.0,
                                               op0=ALU.mult, op1=ALU.add), reads=prk, writes=[("prm", 10)])
        prk = prk + [("prm", 10)]
        P.dma("sp", mul[:], K.rw_mul, writes=["mul"])
        P.op("pool", lambda e: e.memset(bones[:], 0.0), writes=["bones"])
        P.op("pool", lambda e: e.memset(bones[0:64, 0:64], 1.0), reads=["bones"], writes=["bones"])
        P.op("pool", lambda e: e.memset(bones[64:128, 64:128], 1.0), reads=["bones"], writes=["bones"])
        P.op("pool", lambda e: e.iota(PT[:].rearrange("p (c t) -> p c t", t=64), pattern=[[0, NCH], [1, 64]], base=0,
                                      channel_multiplier=0, allow_small_or_imprecise_dtypes=True), writes=["PT"])
        P.op("dve", lambda e: e.tensor_scalar(out=M01[:], in0=PT[:], scalar1=0.5, scalar2=None, op0=ALU.is_gt),
             reads=["PT"], writes=["M01"])
        P.dma("sp", wst[0:96, 0, :], K.rw_w_up, writes=["wst"])
        P.op("act", lambda e: e.copy(out=wup[:], in_=wst[0:96, 0, :]), reads=["wst"], writes=["wup"])
        P.dma("sp", wst[0:96, 1, :], K.rw_a_up, reads=[], writes=["wst1"])
        P.op("act", lambda e: e.copy(out=aup[:], in_=wst[0:96, 1, :]), reads=["wst1"], writes=["aup"])
        P.dma("sp", wst[:, :, :], K.rw_g_up.rearrange("(c p) n -> p c n", p=128), reads=[], writes=["wst", "wst1"])
        P.op("act", lambda e: e.copy(out=gup[:], in_=wst[:]), reads=["wst", "wst1"], writes=["gup"])
        for (r0, n, mcol, func, dst, kd) in ((768, 96, 0, AF.Tanh, txw[:, :], "txw"), (864, 96, 1, AF.Copy, xap[:, :], "xap"),
                                             (960, 128, 2, AF.Sigmoid, sxg[:, 0, :], "sxg0"),
                                             (1088, 128, 3, AF.Sigmoid, sxg[:, 1, :], "sxg1")):
            P.dma("sp", PT[0:n, :], K.yT_d[r0:r0 + n, :], writes=["PT"])
            tok_shift(P, KP, PT, KK, mul[0:n, mcol:mcol + 1], "PT", "KK", "KP", n=n)
            P.op("act", lambda e, n=n, func=func, dst=dst: e.activation(out=dst, in_=KP[0:n, :], func=func),
                 reads=["KP"], writes=[kd])
        lk = ["txw", "xap", "sxg0", "sxg1"]
        oc = 0
        for ct in cts:
            c0 = ct * 128
            P.dma("sp", PT[:], K.yT_d[c0:c0 + 128, :], writes=["PT"])
            tok_shift(P, RP, PT, KK, prm[:, 0, ct:ct + 1], "PT", "KK", "RP")
            P.dma("sp", PT[:], K.yT_d[256 + c0:256 + c0 + 128, :], writes=["PT"])
            tok_shift(P, KP, PT, KK, prm[:, 1, ct:ct + 1], "PT", "KK", "KP")
            P.dma("sp", PT[:], K.yT_d[512 + c0:512 + c0 + 128, :], writes=["PT"])
            tok_shift(P, VP, PT, KK, prm[:, 2, ct:ct + 1], "PT", "KK", "VP")
            P.dma("sp", K.vb_d[c0:c0 + 128, :], VP[:], reads=["VP"], writes=[("vb_d", ct)])
            for blk in range(8):
                bs = slice(blk * 512, (blk + 1) * 512)
                p0, p1, p2 = ps[0], ps[1], ps[2]
                P.op("pe", lambda e, bs=bs, c0=c0: e.matmul(ps[0][:, :], lhsT=wup[:, c0:c0 + 128], rhs=txw[:, bs],
                                                             start=True, stop=True), reads=["wup", "txw"], writes=[("ps4", 0)])
                P.op("act", lambda e, bs=bs, ct=ct: e.activation(out=CL[:, bs], in_=ps[0][:, :], func=AF.Sigmoid,
                                                                  bias=prm[:, 3, ct:ct + 1]),
                     reads=[("ps4", 0)] + prk, writes=["CL"])
                P.op("pe", lambda e, bs=bs, c0=c0: e.matmul(ps[1][:, :], lhsT=aup[:, c0:c0 + 128], rhs=xap[:, bs],
                                                             start=True, stop=True), reads=["aup", "xap"], writes=[("ps4", 1)])
                P.op("act", lambda e, bs=bs, ct=ct: e.activation(out=AA[:, bs], in_=ps[1][:, :], func=AF.Sigmoid,
                                                                  bias=prm[:, 4, ct:ct + 1]),
                     reads=[("ps4", 1)] + prk, writes=["AA"])
                for cc in range(2):
                    P.op("pe", lambda e, bs=bs, c0=c0, cc=cc: e.matmul(ps[2][:, :], lhsT=gup[:, cc, c0:c0 + 128],
                                                                       rhs=sxg[:, cc, bs], start=(cc == 0), stop=(cc == 1)),
                         reads=["gup", "sxg0", "sxg1"], writes=[("ps4", 2)])
                o = OUT[oc % 2]
                P.op("dve", lambda e, bs=bs, o=o: e.tensor_copy(out=o[:, bs], in_=ps[2][:, :]),
                     reads=[("ps4", 2)], writes=[("OUT", oc % 2)])
            P.dma("sp", K.G_d[c0:c0 + 128, :], OUT[oc % 2][:], reads=[("OUT", oc % 2)], writes=[("G_d", ct)])
            oc += 1
            P.op("dve", lambda e: e.tensor_scalar(out=CL[:], in0=CL[:], scalar1=-0.6065306597126334, scalar2=None,
                                                   op0=ALU.mult), reads=["CL"], writes=["CL"])
            P.op("dve", lambda e, ct=ct: e.tensor_scalar(out=KK[:], in0=KP[:], scalar1=prm[:, 5, ct:ct + 1], scalar2=None,
                                                          op0=ALU.mult), reads=["KP"] + prk, writes=["KK"])
            P.op("act", lambda e: e.activation(out=SQb[:], in_=KK[:], func=AF.Square), reads=["KK"], writes=["SQb"])
            for blk in range(8):
                bs = slice(blk * 512, (blk + 1) * 512)
                P.op("pe", lambda e, bs=bs: e.matmul(ps[3][:, :], lhsT=bones[:], rhs=SQb[:, bs], start=True, stop=True),
                     reads=["bones", "SQb"], writes=[("ps4", 3)])
                P.op("act", lambda e, bs=bs: e.activation(out=PT[:, bs], in_=ps[3][:, :], func=AF.Sqrt),
                     reads=[("ps4", 3)], writes=["PT"])
            P.op("dve", lambda e: e.tensor_scalar(out=PT[:], in0=PT[:], scalar1=1e-12, scalar2=None, op0=ALU.max),
                 reads=["PT"], writes=["PT"])
            P.op("dve", lambda e: e.reciprocal(out=PT[:], in_=PT[:]), reads=["PT"], writes=["PT"])
            P.op("dve", lambda e: e.tensor_tensor(out=KK[:], in0=KK[:], in1=PT[:], op=ALU.mult), reads=["KK", "PT"], writes=["KK"])
            P.op("dve", lambda e, ct=ct: e.tensor_scalar(out=PT[:], in0=AA[:], scalar1=prm[:, 6, ct:ct + 1],
                                                          scalar2=prm[:, 10, ct:ct + 1], op0=ALU.mult, op1=ALU.add),
                 reads=["AA", "PT"] + prk, writes=["PT"])
            P.op("dve", lambda e: e.tensor_tensor(out=K2[:], in0=KP[:], in1=PT[:], op=ALU.mult), reads=["KP", "PT"], writes=["K2"])
            P.op("dve", lambda e, ct=ct: e.scalar_tensor_tensor(out=SQb[:], in0=RP[:], scalar=prm[:, 7, ct:ct + 1], in1=K2[:],
                                                                 op0=ALU.mult, op1=ALU.mult),
                 reads=["RP", "K2", "SQb"] + prk, writes=["SQb"])
            o = OUT[oc % 2]
            for blk in range(8):
                bs = slice(blk * 512, (blk + 1) * 512)
                P.op("pe", lambda e, bs=bs: e.matmul(ps[3][:, :], lhsT=bones[:], rhs=SQb[:, bs], start=True, stop=True),
                     reads=["bones", "SQb"], writes=[("ps4", 3)])
                P.op("dve", lambda e, bs=bs, o=o: e.tensor_tensor(out=o[:, bs], in0=ps[3][:, :], in1=VP[:, bs], op=ALU.mult),
                     reads=[("ps4", 3), "VP"], writes=[("OUT", oc % 2)])
            P.dma("sp", K.BON_d[c0:c0 + 128, :], o[:], reads=[("OUT", oc % 2)], writes=[("BON_d", ct)])
            oc += 1
            P.op("dve", lambda e: e.tensor_tensor_scan(out=PT[:], data0=M01[:], data1=CL[:], initial=0.0,
                                                        op0=ALU.mult, op1=ALU.add), reads=["M01", "CL", "PT"], writes=["PT"])
            P.op("pool", lambda e: e.tensor_tensor(out=CL[:], in0=PT[:], in1=CL[:], op=ALU.subtract),
                 reads=["PT", "CL"], writes=["CL"])
            P.op("act", lambda e: e.activation(out=CL[:], in_=CL[:], func=AF.Exp), reads=["CL"], writes=["CL"])
            v3 = lambda t: t[:].rearrange("p (c t) -> p c t", t=64)
            o = OUT[oc % 2]
            P.op("dve", lambda e, o=o: e.scalar_tensor_tensor(out=o[:], in0=KK[:], scalar=-1.0, in1=CL[:],
                                                               op0=ALU.mult, op1=ALU.mult),
                 reads=["KK", "CL"], writes=[("OUT", oc % 2)])
            P.dma("sp", K.AH_d[c0:c0 + 128, :], o[:], reads=[("OUT", oc % 2)], writes=[("AH_d", ct)])
            oc += 1
            P.op("act", lambda e: e.activation(out=CL[:], in_=PT[:], func=AF.Exp), reads=["PT", "CL"], writes=["CL"])
            o = OUT[oc % 2]
            P.op("dve", lambda e, o=o: e.tensor_tensor(out=o[:], in0=RP[:], in1=CL[:], op=ALU.mult),
                 reads=["RP", "CL"], writes=[("OUT", oc % 2)])
            P.dma("sp", K.RH_d[c0:c0 + 128, :], o[:], reads=[("OUT", oc % 2)], writes=[("RH_d", ct)])
            oc += 1
            P.op("pool", lambda e: e.tensor_copy(out=PCt[:], in_=v3(CL)[:, :, 63]), reads=["CL"], writes=["PCt"])
            P.dma("sp", K.PC_d[c0:c0 + 128, :], PCt[:], reads=["PCt"], writes=[("PC_d", ct)])
            P.op("act", lambda e: e.activation(out=PT[:], in_=PT[:], func=AF.Exp, scale=-1.0), reads=["PT"], writes=["PT"])
            o = OUT[oc % 2]
            P.op("dve", lambda e, o=o: e.tensor_tensor(out=o[:], in0=K2[:], in1=PT[:], op=ALU.mult),
                 reads=["K2", "PT"], writes=[("OUT", oc % 2)])
            P.dma("sp", K.KH_d[c0:c0 + 128, :], o[:], reads=[("OUT", oc % 2)], writes=[("KH_d", ct)])
            oc += 1
            P.op("dve", lambda e: e.tensor_tensor(out=KK[:], in0=KK[:], in1=AA[:], op=ALU.mult), reads=["KK", "AA"], writes=["KK"])
            o = OUT[oc % 2]
            P.op("dve", lambda e, o=o: e.tensor_tensor(out=o[:], in0=KK[:], in1=PT[:], op=ALU.mult),
                 reads=["KK", "PT"], writes=[("OUT", oc % 2)])
            P.dma("sp", K.BH_d[c0:c0 + 128, :], o[:], reads=[("OUT", oc % 2)], writes=[("BH_d", ct)])
            oc += 1
        P.flush()

def phase4c_rwkv_scan(K, heads=range(4)):
    nc, P = K.nc, K.P
    with contextlib.ExitStack() as st:
        def sb(name, shape, dt):
            return st.enter_context(nc.sbuf_tensor(name, shape, dt))
        ident = sb("ident4", [128, 128], BF16)
        make_ident(K, P, ident)
        MaskG = sb("MaskG", [64, 4, 128], F32)
        MaskX = sb("MaskX", [64, 8, 64], F32)
        I8 = sb("I8", [64, 64], F32)
        ones = sb("ones4", [64, 64], F32)
        P.op("pool", lambda e: e.memset(ones[:], 1.0), writes=["ones"])
        for a in range(4):
            for cq in range(2):
                P.op("pool", lambda e, cq=cq, a=a: e.affine_select(
                    out=MaskG[:, a, cq * 64:(cq + 1) * 64], in_=ones[:], pattern=[[1, 64]],
                    compare_op=(ALU.is_gt if cq == 0 else ALU.is_ge), fill=0.0, base=0, channel_multiplier=-1),
                    reads=["ones"], writes=["MaskG"])
        for a in range(8):
            P.op("pool", lambda e, a=a: e.affine_select(out=MaskX[:, a, :], in_=ones[:], pattern=[[-1, 64]],
                                                         compare_op=ALU.is_gt, fill=0.0, base=0, channel_multiplier=1),
                 reads=["ones"], writes=["MaskX"])
        P.op("dve", lambda e: e.tensor_copy(out=I8[:], in_=ident[0:64, 0:64]), reads=["ident"], writes=["I8"])
        AH = sb("AH", [64, S], RD)
        RH = sb("RH", [64, S], RD)
        BH = sb("BH", [64, S], RD)
        KH = sb("KH", [64, S], RD)
        vb = sb("vb", [64, S], BF16)
        PC = sb("PC", [64, NCH], F32)
        ARh = sb("ARh", [64, NCH, 128], RD)
        BKh = sb("BKh", [64, NCH, 128], RD)
        GmB = sb("GmB", [64, NCH, 128], RD)
        GmK = sb("GmK", [64, NCH, 128], RD)
        Btok = sb("Btok", [64, NCH, 64], RD)
        Ktok = sb("Ktok", [64, NCH, 64], RD)
        Vtok = sb("Vtok", [64, NCH, 64], RD)
        X0 = sb("X0", [64, NCH, 64], RD)
        Pm = sb("Pm", [64, NCH, 64], RD)
        oT = sb("oT", [64, S], F32)
        Ast = sb("Ast", [64, 64], F32)
        Abf = sb("Abf", [64, 64], RD)
        Tt = sb("Tt", [64, 64], F32)
        Xs = sb("Xs", [64, 64], RD)
        Us = sb("Us", [64, 64], RD)
        PSb = st.enter_context(nc.psum_tensor("PSb", [128, 1024], BF16))
        PS = [st.enter_context(nc.psum_tensor("PS%d" % i, [128, 512], F32)) for i in range(7)]
        v3 = lambda t: t[:].rearrange("p (c t) -> p c t", t=64)
        Nb = [v3(AH), v3(RH)]
        Xb = [v3(BH), v3(KH)]
        Nk = ["AH", "RH"]
        Xk = ["BH", "KH"]
        for hd in heads:
            r0 = hd * 64
            P.dma("sp", AH[:], K.AH_d[r0:r0 + 64, :], writes=["AH"])
            P.dma("sp", RH[:], K.RH_d[r0:r0 + 64, :], writes=["RH"])
            P.dma("sp", BH[:], K.BH_d[r0:r0 + 64, :], writes=["BH"])
            P.dma("sp", KH[:], K.KH_d[r0:r0 + 64, :], writes=["KH"])
            P.dma("sp", vb[:], K.vb_d[r0:r0 + 64, :], writes=["vb"])
            P.dma("sp", PC[:], K.PC_d[r0:r0 + 64, :], writes=["PC"])
            P.op("dve", lambda e: e.tensor_copy(out=ARh[:, :, 0:64], in_=v3(AH)), reads=["AH"], writes=["ARh"])
            P.op("pool", lambda e: e.tensor_copy(out=ARh[:, :, 64:128], in_=v3(RH)), reads=["RH"], writes=["ARh"])
            P.op("dve", lambda e: e.tensor_copy(out=BKh[:, :, 0:64], in_=v3(BH)), reads=["BH"], writes=["BKh"])
            P.op("pool", lambda e: e.tensor_copy(out=BKh[:, :, 64:128], in_=v3(KH)), reads=["KH"], writes=["BKh"])
            for (src, srck, col0, dst, dk) in ((BKh, "BKh", 0, Btok, "Btok"), (BKh, "BKh", 64, Ktok, "Ktok"), (None, "vb", 0, Vtok, "Vtok")):
                for c16 in range(0, NCH, 16):
                    for cc in range(16):
                        c = c16 + cc
                        in_ = vb[:, c * 64:(c + 1) * 64] if src is None else src[:, c, col0:col0 + 64]
                        P.op("pe", lambda e, cc=cc, in_=in_: e.transpose(out=PSb[0:64, cc * 64:(cc + 1) * 64], in_=in_,
                                                                         identity=ident[0:64, 0:64]),
                             reads=[srck, "ident"], writes=["PSb"])
                    P.op("act", lambda e, c16=c16, dst=dst: e.copy(out=dst[:, c16:c16 + 16, :].rearrange("p c k -> p (c k)"),
                                                                    in_=PSb[0:64, :]), reads=["PSb"], writes=[dk])
            gi = 0
            for (col0, dst, dk) in ((0, GmB, "GmB"), (64, GmK, "GmK")):
                for c4 in range(0, NCH, 4):
                    b = gi % 2
                    gi += 1
                    for cc in range(4):
                        c = c4 + cc
                        P.op("pe", lambda e, c=c, cc=cc, b=b, col0=col0: e.matmul(
                            PS[b][0:64, cc * 128:(cc + 1) * 128], lhsT=BKh[:, c, col0:col0 + 64], rhs=ARh[:, c, :],
                            start=True, stop=True), reads=["BKh", "ARh"], writes=[("PS", b)])
                    P.op("dve", lambda e, c4=c4, dst=dst, b=b: e.tensor_tensor(
                        out=dst[:, c4:c4 + 4, :], in0=PS[b][0:64, :].rearrange("p (a t) -> p a t", t=128), in1=MaskG[:],
                        op=ALU.mult), reads=[("PS", b), "MaskG"], writes=[dk])
            for c8 in range(0, NCH, 8):
                for cc in range(8):
                    c = c8 + cc
                    P.op("pe", lambda e, c=c, cc=cc: e.matmul(PS[2][0:64, cc * 64:(cc + 1) * 64], lhsT=ARh[:, c, 0:64],
                                                               rhs=BKh[:, c, 0:64], start=True, stop=True),
                         reads=["ARh", "BKh"], writes=[("PS", 2)])
                P.op("dve", lambda e, c8=c8: e.tensor_tensor(
                    out=X0[:, c8:c8 + 8, :], in0=PS[2][0:64, :].rearrange("p (a t) -> p a t", t=64), in1=MaskX[:],
                    op=ALU.mult), reads=[("PS", 2), "MaskX"], writes=["X0"])
            N0 = GmB[:, :, 0:64]
            P.op("pool", lambda e, N0=N0: e.tensor_tensor(out=Pm[:], in0=N0, in1=I8[:].unsqueeze(1).to_broadcast([64, NCH, 64]),
                                                          op=ALU.add), reads=["GmB", "I8"], writes=["Pm"])
            curN, curNk = N0, "GmB"
            curX, curXk = X0[:], "X0"
            for lvl in range(1, 6):
                nX, nXk = Xb[lvl % 2], Xk[lvl % 2]
                nN, nNk = Nb[lvl % 2], Nk[lvl % 2]
                for c8 in range(0, NCH, 8):
                    for cc in range(8):
                        c = c8 + cc
                        P.op("pe", lambda e, c=c, cc=cc, curN=curN, curX=curX: e.matmul(
                            PS[3][0:64, cc * 64:(cc + 1) * 64], lhsT=curN[:, c, :], rhs=curX[:, c, :], start=True, stop=True),
                            reads=[curNk, curXk], writes=[("PS", 3)])
                    P.op("act", lambda e, c8=c8, nX=nX: e.copy(out=nX[:, c8:c8 + 8, :],
                                                                in_=PS[3][0:64, :].rearrange("p (a t) -> p a t", t=64)),
                         reads=[("PS", 3)], writes=[nXk])
                    if lvl < 5:
                        for cc in range(8):
                            c = c8 + cc
                            P.op("pe", lambda e, c=c, cc=cc, curN=curN, curX=curX: e.matmul(
                                PS[4][0:64, cc * 64:(cc + 1) * 64], lhsT=curX[:, c, :], rhs=curN[:, c, :], start=True, stop=True),
                                reads=[curNk, curXk], writes=[("PS", 4)])
                        P.op("dve", lambda e, c8=c8, nN=nN: e.tensor_copy(out=nN[:, c8:c8 + 8, :],
                                                                           in_=PS[4][0:64, :].rearrange("p (a t) -> p a t", t=64)),
                             reads=[("PS", 4)], writes=[nNk])
                    for cc in range(8):
                        c = c8 + cc
                        P.op("pe", lambda e, c=c, cc=cc, nX=nX: e.matmul(
                            PS[5][0:64, cc * 64:(cc + 1) * 64], lhsT=nX[:, c, :], rhs=Pm[:, c, :], start=True, stop=True),
                            reads=[nXk, "Pm"], writes=[("PS", 5)])
                    P.op("dve", lambda e, c8=c8: e.tensor_tensor(
                        out=Pm[:, c8:c8 + 8, :], in0=PS[5][0:64, :].rearrange("p (a t) -> p a t", t=64),
                        in1=Pm[:, c8:c8 + 8, :], op=ALU.add), reads=[("PS", 5), "Pm"], writes=["Pm"])
                curN, curNk, curX, curXk = nN, nNk, nX, nXk
            P.op("pool", lambda e: e.memset(Ast[:], 0.0), writes=["Ast"])
            P.op("pool", lambda e: e.memset(Abf[:], 0.0), writes=["Abf"])
            for c in range(NCH):
                P.op("pool", lambda e, c=c: e.tensor_scalar(out=Tt[:], in0=Ast[:], scalar1=PC[:, c:c + 1], scalar2=0.0,
                                                             op0=ALU.mult, op1=ALU.add), reads=["Ast", "PC"], writes=["Tt"])
                P.op("pe", lambda e, c=c: e.matmul(PS[0][0:64, 0:64], lhsT=ARh[:, c, 0:64], rhs=Abf[:], start=True, stop=False),
                     reads=["ARh", "Abf"], writes=[("PS", 0)])
                P.op("pe", lambda e, c=c: e.matmul(PS[0][0:64, 0:64], lhsT=GmK[:, c, 0:64], rhs=Vtok[:, c, :], start=False, stop=True),
                     reads=["GmK", "Vtok"], writes=[("PS", 0)])
                P.op("act", lambda e: e.copy(out=Xs[:], in_=PS[0][0:64, 0:64]), reads=[("PS", 0)], writes=["Xs"])
                P.op("pe", lambda e, c=c: e.matmul(PS[1][0:64, 0:64], lhsT=Pm[:, c, :], rhs=Xs[:], start=True, stop=True),
                     reads=["Pm", "Xs"], writes=[("PS", 1)])
                P.op("dve", lambda e: e.tensor_copy(out=Us[:], in_=PS[1][0:64, 0:64]), reads=[("PS", 1)], writes=["Us"])
                P.op("pe", lambda e, c=c: e.matmul(PS[6][0:64, 0:64], lhsT=Btok[:, c, :], rhs=Us[:], start=True, stop=False),
                     reads=["Btok", "Us"], writes=[("PS", 6)])
                P.op("pe", lambda e, c=c: e.matmul(PS[6][0:64, 0:64], lhsT=Ktok[:, c, :], rhs=Vtok[:, c, :], start=False, stop=True),
                     reads=["Ktok", "Vtok"], writes=[("PS", 6)])
                ob = 2 + (c % 2)
                P.op("pe", lambda e, c=c, ob=ob: e.matmul(PS[ob][0:64, 0:64], lhsT=Abf[:], rhs=ARh[:, c, 64:128], start=True, stop=False),
                     reads=["Abf", "ARh"], writes=[("PS", ob)])
                P.op("pe", lambda e, c=c, ob=ob: e.matmul(PS[ob][0:64, 0:64], lhsT=Us[:], rhs=GmB[:, c, 64:128], start=False, stop=False),
                     reads=["Us", "GmB"], writes=[("PS", ob)])
                P.op("pe", lambda e, c=c, ob=ob: e.matmul(PS[ob][0:64, 0:64], lhsT=Vtok[:, c, :], rhs=GmK[:, c, 64:128], start=False, stop=True),
                     reads=["Vtok", "GmK"], writes=[("PS", ob)])
                P.op("dve", lambda e, c=c: e.scalar_tensor_tensor(out=Abf[:], in0=PS[6][0:64, 0:64], scalar=PC[:, c:c + 1], in1=Tt[:],
                                                                   op0=ALU.mult, op1=ALU.add),
                     reads=[("PS", 6), "Tt", "PC"], writes=["Abf"])
                P.op("dve", lambda e, c=c: e.scalar_tensor_tensor(out=Ast[:], in0=PS[6][0:64, 0:64], scalar=PC[:, c:c + 1], in1=Tt[:],
                                                                   op0=ALU.mult, op1=ALU.add),
                     reads=[("PS", 6), "Tt", "PC"], writes=["Ast"])
                P.op("act", lambda e, c=c, ob=ob: e.copy(out=oT[:, c * 64:(c + 1) * 64], in_=PS[ob][0:64, 0:64]),
                     reads=[("PS", ob)], writes=[("oT", c // 8)])
            P.dma("sp", K.oT_d[r0:r0 + 64, :], oT[:], reads=[("oT", q) for q in range(8)], writes=[("oT_d", hd)])
        P.flush()

def phase4d_rwkv_post(K, cts=range(2)):
    nc, P = K.nc, K.P
    with contextlib.ExitStack() as st:
        def sb(name, shape, dt):
            return st.enter_context(nc.sbuf_tensor(name, shape, dt))
        ident = sb("ident4d", [128, 128], BF16)
        make_ident(K, P, ident)
        bonesf = sb("bonesf", [128, 128], F32)
        P.op("pool", lambda e: e.memset(bonesf[:], 0.0), writes=["bonesf"])
        P.op("pool", lambda e: e.memset(bonesf[0:64, 0:64], 1.0), reads=["bonesf"], writes=["bonesf"])
        P.op("pool", lambda e: e.memset(bonesf[64:128, 64:128], 1.0), reads=["bonesf"], writes=["bonesf"])
        prm = sb("prm4d", [128, 2, 2], F32)
        P.dma("sp", prm[:, 0, :], K.rw_prm[8], writes=["prm0"])
        P.dma("sp", prm[:, 1, :], K.rw_prm[9], writes=["prm1"])
        o = sb("o4d", [128, S], F32)
        osq = sb("osq", [128, S], F32)
        bon = sb("bon", [128, S], BF16)
        gg = sb("gg", [128, S], BF16)
        Mb = [sb("Mb%d" % i, [128, 512], F32) for i in range(2)]
        Vb = [sb("Vb%d" % i, [128, 512], F32) for i in range(2)]
        Yb = [sb("Yb%d" % i, [128, 512], F32) for i in range(2)]
        Ob = [sb("Ob%d" % i, [128, 512], BF16) for i in range(2)]
        Tk = [sb("Tk%d" % i, [128, 4, 128], BF16) for i in range(2)]
        ps = [st.enter_context(nc.psum_tensor("p4d_%d" % i, [128, 512], F32)) for i in range(4)]
        pst = [st.enter_context(nc.psum_tensor("p4dt_%d" % i, [128, 4, 128], BF16)) for i in range(2)]
        it = 0
        for ct in cts:
            c0 = ct * 128
            P.dma("sp", o[:], K.oT_d[c0:c0 + 128, :], writes=["o"])
            P.dma("sp", bon[:], K.BON_d[c0:c0 + 128, :], writes=["bon"])
            P.dma("sp", gg[:], K.G_d[c0:c0 + 128, :], writes=["gg"])
            P.op("act", lambda e: e.activation(out=osq[:], in_=o[:], func=AF.Square), reads=["o"], writes=["osq"])
            for blk in range(8):
                s2 = it % 2
                it += 1
                bs = slice(blk * 512, (blk + 1) * 512)
                P.op("pe", lambda e, bs=bs, s2=s2: e.matmul(ps[s2][:, :], lhsT=bonesf[:], rhs=o[:, bs], start=True, stop=True),
                     reads=["bonesf", "o"], writes=[("p4d", s2)])
                P.op("pe", lambda e, bs=bs, s2=s2: e.matmul(ps[2 + s2][:, :], lhsT=bonesf[:], rhs=osq[:, bs], start=True, stop=True),
                     reads=["bonesf", "osq"], writes=[("p4d", 2 + s2)])
                P.op("act", lambda e, s2=s2: e.activation(out=Mb[s2][:], in_=ps[s2][:, :], func=AF.Copy, scale=1.0 / 64),
                     reads=[("p4d", s2)], writes=[("Mb", s2)])
                P.op("pool", lambda e, s2=s2: e.tensor_tensor(out=Vb[s2][:], in0=Mb[s2][:], in1=Mb[s2][:], op=ALU.mult),
                     reads=[("Mb", s2)], writes=[("Vb", s2)])
                P.op("dve", lambda e, s2=s2: e.scalar_tensor_tensor(out=Vb[s2][:], in0=ps[2 + s2][:, :], scalar=1.0 / 64, in1=Vb[s2][:],
                                                                     op0=ALU.mult, op1=ALU.subtract),
                     reads=[("p4d", 2 + s2), ("Vb", s2)], writes=[("Vb", s2)])
                P.op("dve", lambda e, s2=s2: e.tensor_scalar(out=Vb[s2][:], in0=Vb[s2][:], scalar1=64e-5, scalar2=None, op0=ALU.add),
                     reads=[("Vb", s2)], writes=[("Vb", s2)])
                P.op("act", lambda e, s2=s2: e.activation(out=Vb[s2][:], in_=Vb[s2][:], func=AF.Sqrt),
                     reads=[("Vb", s2)], writes=[("Vb", s2)])
                P.op("dve", lambda e, s2=s2: e.reciprocal(out=Vb[s2][:], in_=Vb[s2][:]), reads=[("Vb", s2)], writes=[("Vb", s2)])
                P.op("pool", lambda e, s2=s2, bs=bs: e.tensor_tensor(out=Yb[s2][:], in0=o[:, bs], in1=Mb[s2][:], op=ALU.subtract),
                     reads=["o", ("Mb", s2)], writes=[("Yb", s2)])
                P.op("dve", lambda e, s2=s2: e.tensor_tensor(out=Yb[s2][:], in0=Yb[s2][:], in1=Vb[s2][:], op=ALU.mult),
                     reads=[("Yb", s2), ("Vb", s2)], writes=[("Yb", s2)])
                P.op("dve", lambda e, s2=s2, ct=ct: e.tensor_scalar(out=Yb[s2][:], in0=Yb[s2][:], scalar1=prm[:, 0, ct:ct + 1],
                                                                     scalar2=prm[:, 1, ct:ct + 1], op0=ALU.mult, op1=ALU.add),
                     reads=[("Yb", s2), "prm0", "prm1"], writes=[("Yb", s2)])
                P.op("pool", lambda e, s2=s2, bs=bs: e.tensor_tensor(out=Yb[s2][:], in0=Yb[s2][:], in1=bon[:, bs], op=ALU.add),
                     reads=[("Yb", s2), "bon"], writes=[("Yb", s2)])
                P.op("dve", lambda e, s2=s2, bs=bs: e.tensor_tensor(out=Ob[s2][:], in0=Yb[s2][:], in1=gg[:, bs], op=ALU.mult),
                     reads=[("Yb", s2), "gg"], writes=[("Ob", s2)])
                for q in range(4):
                    P.op("pe", lambda e, s2=s2, q=q: e.transpose(out=pst[s2][:, q, :], in_=Ob[s2][:, q * 128:(q + 1) * 128],
                                                                 identity=ident[:]),
                         reads=[("Ob", s2), "ident"], writes=[("p4dt", s2)])
                P.op("act", lambda e, s2=s2: e.copy(out=Tk[s2][:], in_=pst[s2][:]), reads=[("p4dt", s2)], writes=[("Tk", s2)])
                P.dma("sp", K.ro_loc_d[blk // 4].rearrange("(t p) c -> p t c", p=128)[:, (blk % 4) * 4:(blk % 4 + 1) * 4, c0:c0 + 128], Tk[s2][:],
                      reads=[("Tk", s2)], writes=[("ro_tok_d", ct, blk)])
        P.flush()


def phase4e_allgather(K):
    P = K.P
    for hh in range(2):
        P.coll(lambda e, hh=hh: e.collective_compute("AllGather", ALU.bypass, replica_groups=[[0, 1, 2, 3], [4, 5, 6, 7]],
                                                     ins=[K.ro_loc_d[hh].opt()], outs=[K.ro_all_d[hh].opt()]),
               reads=[("ro_loc", hh)], writes=[("ro_all", hh)])
    P.flush()


def phase5a_select(K):
    nc, P = K.nc, K.P
    with contextlib.ExitStack() as st:
        def sb(name, shape, dt):
            return st.enter_context(nc.sbuf_tensor(name, shape, dt))
        ro = sb("ro_tok", [128, 32, 1024], BF16)
        selT = sb("selT", [128, 32, 1024], BF16)
        qrow = sb("qrow", [128, 1024], F32)
        tki = sb("tki", [128, 32], I32)
        tkf = sb("tkf", [128, 32], F32)
        mo = [sb("mo%d" % i, [128, 512], BF16) for i in range(2)]
        at = sb("at5", [128, 8, 1024], BF16)
        ps = [st.enter_context(nc.psum_tensor("p5a_%d" % i, [128, 512], F32)) for i in range(2)]
        for q4 in range(4):
            for hh in range(2):
                P.dma("sp", ro[:, hh * 16:(hh + 1) * 16, q4 * 256:(q4 + 1) * 256],
                      K.ro_all_d[hh][q4 * 2048:(q4 + 1) * 2048, :].rearrange("(t p) c -> p t c", p=128), writes=[("ro", q4, hh)])
        rok = [("ro", q4, hh) for q4 in range(4) for hh in range(2)]
        P.dma("sp", qrow[:], bcast_rows(K.qpos_row, 1024), writes=["qrow"])
        P.op("pool", lambda e: e.iota(tki[:], pattern=[[128, 32]], base=0, channel_multiplier=1), writes=["tki"])
        P.op("dve", lambda e: e.tensor_copy(out=tkf[:], in_=tki[:]), reads=["tki"], writes=["tkf"])
        for T in range(32):
            P.op("dve", lambda e, T=T: e.tensor_scalar(out=selT[:, T, :], in0=qrow[:], scalar1=tkf[:, T:T + 1], scalar2=0.0,
                                                      op0=ALU.is_equal, op1=ALU.add), reads=["qrow", "tkf"], writes=[("selT", T)])
        sk = [("selT", T) for T in range(32)]
        P.dma("sp", at[:], K.attT_d.rearrange("h p t -> p h t"), writes=["at5"])
        P.dma("sp", K.mixT_d.rearrange("k p t -> p k t")[:, 0:8, :], at[:], reads=["at5"], writes=["mixa"])
        i = 0
        for m in range(8):
            for half in range(2):
                s2 = i % 2
                i += 1
                for T in range(32):
                    P.op("pe", lambda e, T=T, m=m, half=half, s2=s2: e.matmul(
                        ps[s2][:, :], lhsT=ro[:, T, m * 128:(m + 1) * 128], rhs=selT[:, T, half * 512:(half + 1) * 512],
                        start=(T == 0), stop=(T == 31)), reads=rok + sk, writes=[("p5a", s2)])
                P.op("act", lambda e, s2=s2: e.copy(out=mo[s2][:], in_=ps[s2][:, :]), reads=[("p5a", s2)], writes=[("mo", s2)])
                P.dma("sp", K.mixT_d[8 + m, :, half * 512:(half + 1) * 512], mo[s2][:], reads=[("mo", s2)], writes=[("mixr", m, half)])
        P.flush()


def phase5b_outproj(K):
    nc, P = K.nc, K.P
    with contextlib.ExitStack() as st:
        def sb(name, shape, dt):
            return st.enter_context(nc.sbuf_tensor(name, shape, dt))
        ident = sb("ident5", [128, 128], BF16)
        make_ident(K, P, ident)
        G2, SH2 = load_G_SH(K, P, st, 3, 4, K.norm2_g, "p5")
        GT1 = sb("GT1", [128, D], F32)
        P.dma("sp", GT1[:], bcast_rows(K.mod_d[2 * D:3 * D], D), writes=["GT1"])
        Wo = sb("Wo", [128, 16, D], BF16)
        stg = [sb("wstg5_%d" % i, [128, 4, 512], F32) for i in range(2)]
        wk = load_weight_bf16(K, P, stg, Wo, 0, K.w_out, D, "Wo")
        mixT = sb("mixT", [128, 16, 512], BF16)
        T = norm_tiles_alloc(K, st, "p5")
        x1 = T["xt"]
        hT = [sb("hT5_0", [128, 16, 512], BF16)] * 2
        xo = [sb("xo%d" % i, [128, D], F32) for i in range(2)]
        ps = [st.enter_context(nc.psum_tensor("p5b_%d" % i, [128, 512], F32)) for i in range(2)]
        ss, junk, hb, pT = T["ss"], T["junk"], T["hb"], T["pT"]
        gi = 0
        for blk in range(2):
            hs = 0
            P.dma("sp", mixT[:], K.mixT_d.rearrange("k p t -> p k t")[:, :, blk * 512:(blk + 1) * 512], writes=["mixT"])
            for ti in range(4):
                t = blk * 4 + ti
                xs = t % 2
                P.dma("sp", xo[xs][:], K.x_own[t * 128:(t + 1) * 128, :], writes=[("xo", xs)])
                for cg in range(4):
                    b = gi % 2
                    gi += 1
                    for k in range(16):
                        P.op("pe", lambda e, b=b, k=k, t=t, cg=cg: e.matmul(
                            ps[b][:, :], lhsT=mixT[:, k, (t % 4) * 128:(t % 4 + 1) * 128], rhs=Wo[:, k, cg * 512:(cg + 1) * 512],
                            start=(k == 0), stop=(k == 15)), reads=["mixT"] + wk, writes=[("p5b", b)])
                    cs = slice(cg * 512, (cg + 1) * 512)
                    P.op("dve", lambda e, b=b, xs=xs, cs=cs: e.tensor_tensor(out=x1[xs][:, cs], in0=ps[b][:, :], in1=GT1[:, cs], op=ALU.mult),
                         reads=[("p5b", b), "GT1"], writes=[("xt", xs)])
                    P.op("pool", lambda e, xs=xs, cs=cs: e.tensor_tensor(out=x1[xs][:, cs], in0=x1[xs][:, cs], in1=xo[xs][:, cs], op=ALU.add),
                         reads=[("xt", xs), ("xo", xs)], writes=[("xt", xs)])
                P.dma("sp", K.x1_d[t * 128:(t + 1) * 128, :], x1[xs][:], reads=[("xt", xs)], writes=[("x1_d", t)])
                P.op("act", lambda e, xs=xs: e.activation(out=junk[:], in_=x1[xs][:], func=AF.Square, accum_out=ss[:, 0:1]),
                     reads=[("xt", xs)], writes=["junk", "ss0"])
                P.op("dve", lambda e: e.tensor_scalar(out=ss[:, 1:2], in0=ss[:, 0:1], scalar1=1.0 / D, scalar2=1e-6,
                                                       op0=ALU.mult, op1=ALU.add), reads=["ss0"], writes=["ss1"])
                P.op("act", lambda e: e.activation(out=ss[:, 2:3], in_=ss[:, 1:2], func=AF.Sqrt), reads=["ss1"], writes=["ss2"])
                P.op("dve", lambda e: e.reciprocal(out=ss[:, 3:4], in_=ss[:, 2:3]), reads=["ss2"], writes=["ss3"])
                P.op("dve", lambda e, xs=xs: e.scalar_tensor_tensor(out=x1[xs][:], in0=x1[xs][:], scalar=ss[:, 3:4], in1=G2[:],
                                                                   op0=ALU.mult, op1=ALU.mult),
                     reads=[("xt", xs), "ss3", "G"], writes=[("xt", xs)])
                P.op("pool", lambda e, xs=xs: e.tensor_tensor(out=hb[xs][:], in0=x1[xs][:], in1=SH2[:], op=ALU.add),
                     reads=[("xt", xs), "SH"], writes=[("hb", xs)])
                for half in range(2):
                    for kk in range(8):
                        k = half * 8 + kk
                        P.op("pe", lambda e, k=k, kk=kk, half=half, xs=xs: e.transpose(
                            out=pT[half][:, kk, :], in_=hb[xs][:, k * 128:(k + 1) * 128], identity=ident[:]),
                            reads=[("hb", xs), "ident"], writes=[("pT", half)])
                    o_ = hT[hs][:, half * 8:(half + 1) * 8, ti * 128:(ti + 1) * 128]
                    if half == 0:
                        P.op("act", lambda e, o_=o_, half=half: e.copy(out=o_, in_=pT[half][:]), reads=[("pT", half)], writes=[("hT5", hs, ti, half)])
                    else:
                        P.op("dve", lambda e, o_=o_, half=half: e.tensor_copy(out=o_, in_=pT[half][:]), reads=[("pT", half)], writes=[("hT5", hs, ti, half)])
            P.dma("sp", K.h2T_d.rearrange("k p t -> p k t")[:, :, blk * 512:(blk + 1) * 512], hT[hs][:],
                  reads=[("hT5", hs, ti, half) for ti in range(4) for half in range(2)], writes=[("h2T_d", blk)])
        P.flush()


def phase5c_ffn(K):
    nc, P = K.nc, K.P
    NF = 5632 // 128
    with contextlib.ExitStack() as st:
        def sb(name, shape, dt):
            return st.enter_context(nc.sbuf_tensor(name, shape, dt))
        h2T = sb("h2T", [128, 16, OWN], BF16)
        P.dma("sp", h2T[:], K.h2T_d.rearrange("k p t -> p k t"), writes=["h2T"])
        ao = [sb("ao%d" % i, [128, 512], BF16) for i in range(2)]
        stg = [sb("wstg6_%d" % i, [128, 4, 512], F32) for i in range(4)]
        Wg = [sb("Wg%d" % i, [128, 16, 512], BF16) for i in range(2)]
        Wu = [sb("Wu%d" % i, [128, 16, 512], BF16) for i in range(2)]
        sg = [sb("sg%d" % i, [128, 512], F32) for i in range(2)]
        ps = [st.enter_context(nc.psum_tensor("p5c_%d" % i, [128, 512], F32)) for i in range(4)]
        gi = 0

        def load_group(fg, defer=None):
            ws = fg % 2
            load_weight_bf16(K, P, stg, Wg[ws], 0, K.w_ffn_gate[:, fg * 512:(fg + 1) * 512], 512, ("Wg", ws), defer=defer)
            load_weight_bf16(K, P, stg, Wu[ws], 0, K.w_ffn_up[:, fg * 512:(fg + 1) * 512], 512, ("Wu", ws), defer=defer)
        load_group(0)
        for fg in range(11):
            ws = fg % 2
            pend = []
            if fg + 1 < 11:
                load_group(fg + 1, defer=pend)
            for f4 in range(4):
                f = fg * 4 + f4
                for tb in range(2):
                    b = gi % 2
                    gi += 1
                    if pend:
                        pend.pop(0)()
                    for k in range(16):
                        P.op("pe", lambda e, b=b, k=k, f4=f4, tb=tb, ws=ws: e.matmul(
                            ps[b][:, :], lhsT=Wg[ws][:, k, f4 * 128:(f4 + 1) * 128], rhs=h2T[:, k, tb * 512:(tb + 1) * 512],
                            start=(k == 0), stop=(k == 15)), reads=["h2T", (("Wg", ws), 0, (k // 4) * 4)], writes=[("p5c", b)])
                    for k in range(16):
                        P.op("pe", lambda e, b=b, k=k, f4=f4, tb=tb, ws=ws: e.matmul(
                            ps[2 + b][:, :], lhsT=Wu[ws][:, k, f4 * 128:(f4 + 1) * 128], rhs=h2T[:, k, tb * 512:(tb + 1) * 512],
                            start=(k == 0), stop=(k == 15)), reads=["h2T", (("Wu", ws), 0, (k // 4) * 4)], writes=[("p5c", 2 + b)])
                    P.op("act", lambda e, b=b: e.activation(out=sg[b][:], in_=ps[b][:, :], func=AF.Silu),
                         reads=[("p5c", b)], writes=[("sg", b)])
                    P.op("dve", lambda e, b=b: e.tensor_tensor(out=ao[b][:], in0=ps[2 + b][:, :], in1=sg[b][:], op=ALU.mult),
                         reads=[("p5c", 2 + b), ("sg", b)], writes=[("ao", b)])
                    P.dma("sp", K.actT_d[f, :, tb * 512:(tb + 1) * 512], ao[b][:], reads=[("ao", b)], writes=[("actT_d", f, tb)])
        P.flush()
    with contextlib.ExitStack() as st:
        def sb(name, shape, dt):
            return st.enter_context(nc.sbuf_tensor(name, shape, dt))
        GT2 = sb("GT2", [128, D], F32)
        P.dma("sp", GT2[:], bcast_rows(K.mod_d[5 * D:6 * D], D), writes=["GT2"])
        actT = sb("actT", [128, NF, OWN], BF16)
        for q in range(4):
            P.dma("sp", actT[:, q * 11:(q + 1) * 11, :], K.actT_d.rearrange("f p t -> p f t")[:, q * 11:(q + 1) * 11, :], writes=[("actT", q)])
        ak = [("actT", q) for q in range(4)]
        stg = [sb("wstg7_%d" % i, [128, 4, 256], F32) for i in range(4)]
        ps = [st.enter_context(nc.psum_tensor("p5d_%d" % i, [128, 512], F32)) for i in range(2)]
        gi = 0
        Wd = [sb("Wd%d" % i, [128, NF, 256], BF16) for i in range(2)]
        x1 = [sb("x1_%d" % i, [128, 256], F32) for i in range(2)]
        yo = [sb("yo%d" % i, [128, 256], F32) for i in range(2)]
        wdv = K.w_ffn_down.rearrange("(k p) n -> p k n", p=128)
        engs = ["pool", "dve", "act"]

        def load_wd(cg, defer=None):
            wsl = cg % 2
            for k0 in range(0, NF, 4):
                if defer is not None:
                    defer.append(lambda k0=k0: load_wd_piece(cg, wsl, k0))
                else:
                    load_wd_piece(cg, wsl, k0)

        def load_wd_piece(cg, wsl, k0):
            if True:
                i = K.wcnt
                K.wcnt += 1
                sl = i % 4
                P.dma("sp", stg[sl][:, 0:4, 0:256], wdv[:, k0:k0 + 4, cg * 256:(cg + 1) * 256], writes=[("wstg", sl)])
                eng = engs[i % 3]
                o_ = Wd[wsl][:, k0:k0 + 4, :]
                if eng == "act":
                    P.op("act", lambda e, o_=o_, sl=sl: e.copy(out=o_, in_=stg[sl][:, 0:4, 0:256]), reads=[("wstg", sl)], writes=[("Wd", wsl, k0)])
                else:
                    P.op(eng, lambda e, o_=o_, sl=sl: e.tensor_copy(out=o_, in_=stg[sl][:, 0:4, 0:256]), reads=[("wstg", sl)], writes=[("Wd", wsl, k0)])
        load_wd(0)
        for cg in range(8):
            wsl = cg % 2
            cs = slice(cg * 256, (cg + 1) * 256)
            pend = []
            if cg + 1 < 8:
                load_wd(cg + 1, defer=pend)
            for t in range(8):
                b = gi % 2
                gi += 1
                for _ in range(2):
                    if pend:
                        pend.pop(0)()
                P.dma("sp", x1[b][:], K.x1_d[t * 128:(t + 1) * 128, cs], writes=[("x1", b)])
                for f in range(NF):
                    P.op("pe", lambda e, b=b, f=f, t=t, wsl=wsl: e.matmul(ps[b][:, 0:256], lhsT=actT[:, f, t * 128:(t + 1) * 128], rhs=Wd[wsl][:, f, :],
                                                                          start=(f == 0), stop=(f == NF - 1)),
                         reads=[("actT", f // 11), ("Wd", wsl, (f // 4) * 4)], writes=[("p5c", b)])
                P.op("dve", lambda e, b=b, cs=cs: e.tensor_tensor(out=yo[b][:], in0=ps[b][:, 0:256], in1=GT2[:, cs], op=ALU.mult),
                     reads=[("p5c", b), "GT2"], writes=[("yo", b)])
                P.op("pool", lambda e, b=b: e.tensor_tensor(out=yo[b][:], in0=yo[b][:], in1=x1[b][:], op=ALU.add),
                     reads=[("yo", b), ("x1", b)], writes=[("yo", b)])
                P.dma("sp", K.out[t * 128:(t + 1) * 128, cs], yo[b][:], reads=[("yo", b)], writes=[("out", t, cg)])
        P.flush()


def phase_final_copy(K):
    nc, P = K.nc, K.P
    with contextlib.ExitStack() as st:
        xt = [st.enter_context(nc.sbuf_tensor("fx%d" % i, [128, D], F32)) for i in range(2)]
        for t in range(8):
            s = t % 2
            P.dma("sp", xt[s][:], K.x_own[t * 128:(t + 1) * 128, :], writes=[("fx", s)])
            P.dma("sp", K.out[t * 128:(t + 1) * 128, :], xt[s][:], reads=[("fx", s)], writes=[("out", t)])
        P.flush()


def own_tiles(j):
    r = []
    for m in range(4):
        r += [8 * m + j, 8 * m + 7 - j]
    return r


def build_program(debug=False, stages=99, cts=range(2), dbg_list=None, skip_att=False):
    nc = bass.Bass("TRN2", target_bir_lowering=False)
    K = Ctx()
    K.stages = stages
    K.cts = cts
    K.skip_att = skip_att
    K.nc = nc
    K.dbg = {}
    K.wcnt = 0

    def inp(name, shape, dt=F32):
        return nc.dram_tensor(name, list(shape), dt, kind="ExternalInput").ap()

    def scratch(name, shape, dt):
        return nc.dram_tensor(name, list(shape), dt, kind="Internal").ap()

    K.x_full = inp("x_full", [S, D])
    K.x_own = inp("x_own", [OWN, D])
    K.c_arr = inp("c_arr", [128, 16])
    K.pos_full = inp("pos_full", [128, 32], I32)
    K.invf_att = inp("invf_att", [128, 16])
    K.invf_idx = inp("invf_idx", [128, 8])
    K.w_ada = inp("w_ada", [D, 3072])
    K.b_ada = inp("b_ada", [3072])
    K.norm1_g = inp("norm1_g", [D])
    K.k_norm_g = inp("k_norm_g", [128])
    K.q_norm_g = inp("q_norm_g", [128])
    K.pos_own = inp("pos_own", [128, 8], I32)
    K.qpos_own = inp("qpos_own", [128, 8])
    K.w_in = inp("w_in", [D, 4176])
    K.rw_prm = [inp("rwp%d" % i, [128, 2]) for i in range(10)]
    K.w_in_rw = inp("w_in_rw", [D, 1216])
    K.rw_mul = inp("rw_mul", [128, 4])
    K.rw_w_up = inp("rw_w_up", [96, 256])
    K.rw_a_up = inp("rw_a_up", [96, 256])
    K.rw_g_up = inp("rw_g_up", [256, 256])
    K.qpos_row = inp("qpos_row", [OWN])
    K.w_out = inp("w_out", [D, D])
    K.norm2_g = inp("norm2_g", [D])
    K.w_ffn_gate = inp("w_ffn_gate", [D, 5632])
    K.w_ffn_up = inp("w_ffn_up", [D, 5632])
    K.w_ffn_down = inp("w_ffn_down", [5632, D])
    K.out = nc.dram_tensor("y_own", [OWN, D], F32, kind="ExternalOutput").ap()
    K.modq_d = scratch("modq_d", [1, 3072], F32)
    K.mod4_d = scratch("mod4_d", [4, 3072], F32)
    K.mod_d = K.mod4_d.rearrange("a n -> (a n)")
    K.hT_d = scratch("hT_d", [16, 128, S], BF16)
    K.kT_d = scratch("kT_d", [8, 128, S], BF16)
    K.v_d = scratch("v_d", [S, 8 * 129], BF16)
    K.ikT_d = scratch("ikT_d", [64, S], BF16)
    K.yT_d = scratch("yT_d", [1216, S], F32)
    K.qT_d = scratch("qT_d", [8, 128, OWN], BF16)
    K.iqT_d = scratch("iqT_d", [64, OWN, 16], BF16)
    K.iw_d = scratch("iw_d", [OWN, 16], F32)
    K.attT_d = scratch("attT_d", [8, 128, OWN], BF16)
    for nm in ("vb_d", "G_d", "BON_d", "AH_d", "RH_d", "BH_d", "KH_d"):
        setattr(K, nm, scratch(nm, [256, S], BF16))
    K.PC_d = scratch("PC_d", [256, NCH], F32)
    K.oT_d = scratch("oT_d", [256, S], F32)
    K.ro_loc_d = [scratch("ro_loc%d_d" % i, [2048, 256], BF16) for i in range(2)]
    K.ro_all_d = [scratch("ro_all%d_d" % i, [8192, 256], BF16) for i in range(2)]
    K.mixT_d = scratch("mixT_d", [16, 128, OWN], BF16)
    K.x1_d = scratch("x1_d", [OWN, D], F32)
    K.h2T_d = scratch("h2T_d", [16, 128, OWN], BF16)
    K.actT_d = scratch("actT_d", [44, 128, OWN], BF16)
    with contextlib.ExitStack() as stack:
        K.P = Prog(nc, stack)
        phase0_adaln(K)
        phase1_kv(K)
        if K.stages >= 2:
            phase1b_rwkv_proj(K)
        if K.stages >= 3 and not getattr(K, "skip_att", False):
            phase2_own_proj(K)
            phase3_attention(K)
        if K.stages >= 4:
            phase4b_rwkv_prep(K, cts=K.cts)
            if K.stages >= 5:
                phase4c_rwkv_scan(K, heads=[h for ct in K.cts for h in (2 * ct, 2 * ct + 1)])
        if K.stages >= 6:
            phase4d_rwkv_post(K, cts=K.cts)
            phase4e_allgather(K)
        if K.stages >= 7:
            phase5a_select(K)
            phase5b_outproj(K)
            phase5c_ffn(K)
        else:
            phase_final_copy(K)
        if debug:
            P = K.P
            allc = (("dbg_mixT", K.mixT_d, [16, 128, OWN], BF16), ("dbg_x1", K.x1_d, [OWN, D], F32),
                    ("dbg_oT", K.oT_d, [256, S], F32), ("dbg_AH", K.AH_d, [256, S], BF16), ("dbg_BH", K.BH_d, [256, S], BF16),
                    ("dbg_KH", K.KH_d, [256, S], BF16), ("dbg_RH", K.RH_d, [256, S], BF16), ("dbg_PC", K.PC_d, [256, NCH], F32),
                    ("dbg_G", K.G_d, [256, S], BF16), ("dbg_BON", K.BON_d, [256, S], BF16), ("dbg_vb", K.vb_d, [256, S], BF16),
                    ("dbg_yT", K.yT_d, [1216, S], F32), ("dbg_attT", K.attT_d, [8, 128, OWN], BF16),
                                     ("dbg_qT", K.qT_d, [8, 128, OWN], BF16), ("dbg_iqT", K.iqT_d, [64, OWN, 16], BF16),
                                     ("dbg_iw", K.iw_d, [OWN, 16], F32))
            for nm, src, shp, dt in allc:
                if dbg_list is not None and nm not in dbg_list:
                    continue
                o = dbg_out(K, nm, shp, dt)
                P.dma("sp", o, src, writes=[nm])
            P.flush()
    return nc, K


def make_in_maps(inputs, cores=range(8)):
    x = np.asarray(inputs["x"], dtype=np.float32)
    c = np.asarray(inputs["c"], dtype=np.float32)
    pos = np.asarray(inputs["positions"], dtype=np.int32)
    invf_att = (np.float32(500000.0) ** (-np.arange(16, dtype=np.float32) / np.float32(16))).astype(np.float32)
    invf_idx = (np.float32(500000.0) ** (-np.arange(8, dtype=np.float32) / np.float32(8))).astype(np.float32)
    mu = np.asarray(inputs["rwkv_mu"][0], dtype=np.float32)

    vecs = [mu[0:1024], mu[1024:2048], mu[2048:3072], inputs["rwkv_w0"][0], inputs["rwkv_a0"][0], inputs["rwkv_k_k"][0],
            inputs["rwkv_k_a"][0], np.asarray(inputs["rwkv_r_k"][0]).reshape(-1), inputs["rwkv_lnx_g"][0], inputs["rwkv_lnx_b"][0]]
    w_in_full = np.asarray(inputs["w_in"][0], dtype=np.float32)
    rw_mul = np.zeros((128, 4), np.float32)
    rw_mul[:96, 0] = mu[3072:3168]
    rw_mul[:96, 1] = mu[3168:3264]
    rw_mul[:, 2] = mu[3264:3392]
    rw_mul[:, 3] = mu[3392:3520]
    maps = []
    for core in cores:
        b, j = core // 4, core % 4
        ch = slice(256 * j, 256 * j + 256)
        rwp = {"rwp%d" % i: np.ascontiguousarray(np.asarray(v, dtype=np.float32)[ch].reshape(2, 128).T) for i, v in enumerate(vecs)}
        R0 = 4176
        w_in_rw = np.ascontiguousarray(np.concatenate([w_in_full[:, R0 + 256 * j:R0 + 256 * j + 256],
                                                       w_in_full[:, R0 + 1024 + 256 * j:R0 + 1024 + 256 * j + 256],
                                                       w_in_full[:, R0 + 2048 + 256 * j:R0 + 2048 + 256 * j + 256],
                                                       w_in_full[:, R0 + 3072:R0 + 3520]], axis=1))
        tiles = own_tiles(j)
        idx = np.concatenate([np.arange(t * 128, (t + 1) * 128) for t in tiles])
        maps.append({
            "x_full": np.ascontiguousarray(x[b]),
            "x_own": np.ascontiguousarray(x[b][idx]),
            "c_arr": np.ascontiguousarray(c[b].reshape(16, 128).T),
            "pos_full": np.ascontiguousarray(pos[b].reshape(32, 128).T),
            "invf_att": np.ascontiguousarray(np.broadcast_to(invf_att, (128, 16))),
            "invf_idx": np.ascontiguousarray(np.broadcast_to(invf_idx, (128, 8))),
            "w_ada": np.ascontiguousarray(np.asarray(inputs["w_ada"][0], dtype=np.float32)[:, 3072 * j:3072 * (j + 1)]),
            "b_ada": np.ascontiguousarray(np.asarray(inputs["b_ada"][0], dtype=np.float32)[3072 * j:3072 * (j + 1)]),
            "norm1_g": np.asarray(inputs["norm1_g"][0], dtype=np.float32),
            "k_norm_g": np.asarray(inputs["k_norm_g"][0], dtype=np.float32),
            "q_norm_g": np.asarray(inputs["q_norm_g"][0], dtype=np.float32),
            "pos_own": np.ascontiguousarray(pos[b][idx].reshape(8, 128).T),
            "qpos_own": np.ascontiguousarray(idx.astype(np.float32).reshape(8, 128).T),
            "w_in": np.ascontiguousarray(w_in_full[:, 0:4176]),
            "qpos_row": idx.astype(np.float32),
            "w_out": np.asarray(inputs["w_out"][0], dtype=np.float32),
            "norm2_g": np.asarray(inputs["norm2_g"][0], dtype=np.float32),
            "w_ffn_gate": np.asarray(inputs["w_ffn_gate"][0], dtype=np.float32),
            "w_ffn_up": np.asarray(inputs["w_ffn_up"][0], dtype=np.float32),
            "w_ffn_down": np.asarray(inputs["w_ffn_down"][0], dtype=np.float32),
            "rw_w_up": np.ascontiguousarray(np.asarray(inputs["rwkv_w_up"][0], dtype=np.float32)[:, ch]),
            "rw_a_up": np.ascontiguousarray(np.asarray(inputs["rwkv_a_up"][0], dtype=np.float32)[:, ch]),
            "rw_g_up": np.ascontiguousarray(np.asarray(inputs["rwkv_g_up"][0], dtype=np.float32)[:, ch]),
            "w_in_rw": w_in_rw,
            "rw_mul": rw_mul,
            **rwp,
        })
    return maps


def kernel(**inputs):
    nc, K = build_program(debug=False)
    maps = make_in_maps(inputs)
    res = run_bass_kernel_spmd(nc, maps, core_ids=list(range(8)))
    out = np.zeros((2, S, D), dtype=np.float32)
    for core in range(8):
        b, j = core // 4, core % 4
        y = res.results[core]["y_own"]
        for i, t in enumerate(own_tiles(j)):
            out[b, t * 128:(t + 1) * 128] = y[i * 128:(i + 1) * 128]
    return out
```

```python
import contextlib
import numpy as np
import concourse.bass as bass
import concourse.mybir as mybir
from concourse.bass_utils import run_bass_kernel_spmd

F32 = mybir.dt.float32
BF16 = mybir.dt.bfloat16
I32 = mybir.dt.int32
AF = mybir.ActivationFunctionType
ALU = mybir.AluOpType
AX = mybir.AxisListType

D = 2048
S = 4096
NT = 32
OWN = 1024
ENGS = ("pe", "act", "dve", "pool", "sp")
DEBUG = {}


class _Op:
    __slots__ = ("eng", "fn", "deps", "needs_inc", "is_dma", "sem", "count", "idx", "prev_same_sem", "is_cc")

    def __init__(self, eng, fn, is_dma):
        self.eng = eng
        self.fn = fn
        self.deps = set()
        self.needs_inc = False
        self.is_dma = is_dma
        self.sem = None
        self.count = 0
        self.prev_same_sem = None
        self.is_cc = False


class Prog:
    def __init__(self, nc, stack, n_dma_sems=48):
        self.nc = nc
        self.n_dma_sems = n_dma_sems
        self.eng_sem = {e: stack.enter_context(nc.semaphore("s_" + e)) for e in ENGS}
        self.dma_sems = [stack.enter_context(nc.semaphore("d%d" % i)) for i in range(n_dma_sems)]
        self.bar_sem = stack.enter_context(nc.semaphore("bar"))
        self.cc_sem = stack.enter_context(nc.semaphore("ccs"))
        self.cc_cnt = 0
        self.cnt = {e: 0 for e in ENGS}
        self.dcnt = [0] * n_dma_sems
        self.rr = 0
        self.nbar = 0
        self._reset()

    def _reset(self):
        self.ops = []
        self.last_writer = {}
        self.readers = {}

    def _record(self, op, reads, writes):
        idx = len(self.ops)
        op.idx = idx
        deps = set()
        for k in reads:
            w = self.last_writer.get(k)
            if w is not None:
                deps.add(w)
        for k in writes:
            w = self.last_writer.get(k)
            if w is not None:
                deps.add(w)
            for r in self.readers.get(k, ()):
                deps.add(r)
        deps.discard(idx)
        op.deps = deps
        self.ops.append(op)
        for k in reads:
            self.readers.setdefault(k, []).append(idx)
        for k in writes:
            self.last_writer[k] = idx
            self.readers[k] = []
        return idx

    def op(self, eng, fn, reads=(), writes=()):
        return self._record(_Op(eng, fn, False), reads, writes)

    def dma(self, queue, out, in_, reads=(), writes=(), **kw):
        def fn(e, out=out, in_=in_, kw=kw):
            return e.dma_start(out=out, in_=in_, **kw)
        return self._record(_Op(queue, fn, True), reads, writes)

    def coll(self, fn, reads=(), writes=()):
        o = _Op("pool", fn, True)
        o.is_cc = True
        return self._record(o, reads, writes)

    def flush(self):
        nc = self.nc
        ops = self.ops
        for o in ops:
            nd = set()
            for d in o.deps:
                p = ops[d]
                if o.eng == "pe" and p.eng == "pe" and not p.is_dma and not o.is_dma:
                    continue
                nd.add(d)
                p.needs_inc = True
            o.deps = nd
        last_of = {}
        for o in ops:
            if not o.is_dma:
                last_of[o.eng] = o
        for o in last_of.values():
            o.needs_inc = True
        dlast = [None] * self.n_dma_sems
        for o in ops:
            if o.is_cc:
                self.cc_cnt += 1
                o.sem = self.cc_sem
                o.count = self.cc_cnt
            elif o.is_dma:
                s = self.rr % self.n_dma_sems
                self.rr += 1
                o.prev_same_sem = dlast[s]
                self.dcnt[s] += 16
                o.sem = self.dma_sems[s]
                o.count = self.dcnt[s]
                dlast[s] = o.idx
            elif o.needs_inc:
                self.cnt[o.eng] += 1
                o.sem = self.eng_sem[o.eng]
                o.count = self.cnt[o.eng]
        per_eng = {e: [o for o in ops if o.eng == e] for e in ENGS}
        final = [(self.dma_sems[s], self.dcnt[s]) for s in range(self.n_dma_sems) if self.dcnt[s] > 0]
        final += [(self.eng_sem[e], self.cnt[e]) for e in ENGS if self.cnt[e] > 0]
        if self.cc_cnt > 0:
            final.append((self.cc_sem, self.cc_cnt))
        self.nbar += 1
        nbar = self.nbar
        bar = self.bar_sem

        def run(e_name, eng):
            waited = {}
            for o in per_eng[e_name]:
                need = {}
                for d in o.deps:
                    p = ops[d]
                    if need.get(p.sem.num, (0, None))[0] < p.count:
                        need[p.sem.num] = (p.count, p.sem)
                if o.is_dma and o.prev_same_sem is not None:
                    p = ops[o.prev_same_sem]
                    if need.get(p.sem.num, (0, None))[0] < p.count:
                        need[p.sem.num] = (p.count, p.sem)
                for key, (c, s) in need.items():
                    if waited.get(key, 0) < c:
                        eng.wait_ge(s, c)
                        waited[key] = c
                ins = o.fn(eng)
                if o.is_cc:
                    ins.then_inc(o.sem)
                elif o.is_dma:
                    ins.then_inc(o.sem, 16)
                elif o.needs_inc:
                    ins.then_inc(o.sem, 1)
            if e_name == "sp":
                for s, c in final:
                    eng.wait_ge(s, c)
                eng.sem_inc(bar, 1)
            eng.wait_ge(bar, nbar)

        with nc.Block() as block:
            @block.tensor
            def _(e):
                run("pe", e)

            @block.scalar
            def _(e):
                run("act", e)

            @block.vector
            def _(e):
                run("dve", e)

            @block.gpsimd
            def _(e):
                run("pool", e)

            @block.sync
            def _(e):
                run("sp", e)
        self._reset()


class Ctx:
    pass


def bcast_rows(ap1d, n):
    return bass.AP(ap1d.tensor, ap1d.offset, [[0, 128], [1, n]])


def dbg_out(K, name, shape, dtype=F32):
    t = K.nc.dram_tensor(name, list(shape), dtype, kind="ExternalOutput")
    K.dbg[name] = t
    return t.ap()


def make_ident(K, P, ident):
    P.op("pool", lambda e: e.memset(ident[:], 0.0), writes=["ident"])
    P.op("pool", lambda e: e.affine_select(out=ident[:], in_=ident[:], pattern=[[-1, 128]],
                                           compare_op=ALU.not_equal, fill=1.0, base=0,
                                           channel_multiplier=1),
         reads=["ident"], writes=["ident"])


def phase0_adaln(K):
    nc, P = K.nc, K.P
    NQ = 3072
    with contextlib.ExitStack() as st:
        c_sb = st.enter_context(nc.sbuf_tensor("c_sb", [128, 16], F32))
        cact = st.enter_context(nc.sbuf_tensor("cact", [128, 16], F32))
        wst = [st.enter_context(nc.sbuf_tensor("wst%d" % i, [128, 16, 512], F32)) for i in range(2)]
        modrow = st.enter_context(nc.sbuf_tensor("modrow", [1, NQ], F32))
        brow = st.enter_context(nc.sbuf_tensor("brow", [1, NQ], F32))
        ps = [st.enter_context(nc.psum_tensor("ps0_%d" % i, [1, 512], F32)) for i in range(2)]
        P.dma("sp", c_sb[:], K.c_arr, writes=["c_sb"])
        P.dma("sp", brow[:], K.b_ada.rearrange("(o n) -> o n", o=1), writes=["brow"])
        P.op("act", lambda e: e.activation(out=cact[:], in_=c_sb[:], func=AF.Silu),
             reads=["c_sb"], writes=["cact"])
        wv = K.w_ada.rearrange("(k p) n -> p k n", p=128)
        for nt in range(NQ // 512):
            sl = nt % 2
            for hh in range(2):
                P.dma("sp", wst[sl][:, hh * 8:(hh + 1) * 8, :],
                      wv[:, hh * 8:(hh + 1) * 8, nt * 512:(nt + 1) * 512],
                      writes=[("wst", sl, hh)])
            for k in range(16):
                P.op("pe", lambda e, k=k, sl=sl: e.matmul(ps[sl][:, :], lhsT=cact[:, k:k + 1],
                                                         rhs=wst[sl][:, k, :], start=(k == 0), stop=(k == 15)),
                     reads=["cact", ("wst", sl, k // 8)], writes=[("ps0", sl)])
            P.op("dve", lambda e, nt=nt, sl=sl: e.tensor_tensor(
                out=modrow[0:1, nt * 512:(nt + 1) * 512], in0=ps[sl][:, :],
                in1=brow[0:1, nt * 512:(nt + 1) * 512], op=ALU.add),
                reads=[("ps0", sl), "brow"], writes=[("modrow", nt)])
        P.dma("sp", K.modq_d, modrow[:],
              reads=[("modrow", nt) for nt in range(NQ // 512)], writes=["modq_d"])
        P.flush()
    P.coll(lambda e: e.collective_compute("AllGather", ALU.bypass, replica_groups=[[0, 1, 2, 3], [4, 5, 6, 7]],
                                          ins=[K.modq_d.opt()], outs=[K.mod4_d.opt()]), reads=["modq_d"], writes=["mod4"])
    P.flush()


def load_mod_rows(K, P, tile, which, gain_ap=None, key=None):
    src = K.mod_d[which * D:(which + 1) * D]
    P.dma("sp", tile[:], bcast_rows(src, D), writes=[key])


def bc(ap, shape):
    return ap.to_broadcast(list(shape))


def load_weight_bf16(K, P, st_tiles, dst, c_dst, src2d, ncols, tag, defer=None):
    wv = src2d.rearrange("(k p) n -> p k n", p=128)
    nk = wv.shape[1]
    engs = ["pool", "dve", "act"]
    for c0 in range(0, ncols, 512):
        n = min(512, ncols - c0)
        for k0 in range(0, nk, 4):
            kn = min(4, nk - k0)
            if defer is not None:
                defer.append(lambda c0=c0, n=n, k0=k0, kn=kn: _load_piece(K, P, st_tiles, dst, c_dst, wv, tag, engs, c0, n, k0, kn))
                continue
            _load_piece(K, P, st_tiles, dst, c_dst, wv, tag, engs, c0, n, k0, kn)
    return [(tag, c0, k0) for c0 in range(0, ncols, 512) for k0 in range(0, nk, 4)]


def _load_piece(K, P, st_tiles, dst, c_dst, wv, tag, engs, c0, n, k0, kn):
    if True:
        if True:
            i = K.wcnt
            K.wcnt += 1
            sl = i % len(st_tiles)
            stg = st_tiles[sl]
            P.dma("sp", stg[:, 0:kn, 0:n], wv[:, k0:k0 + kn, c0:c0 + n], writes=[("wstg", sl)])
            eng = engs[i % 3]
            o = dst[:, k0:k0 + kn, c_dst + c0:c_dst + c0 + n]
            if eng == "act":
                P.op("act", lambda e, o=o, stg=stg, kn=kn, n=n: e.copy(out=o, in_=stg[:, 0:kn, 0:n]),
                     reads=[("wstg", sl)], writes=[(tag, c0, k0)])
            else:
                P.op(eng, lambda e, o=o, stg=stg, kn=kn, n=n: e.tensor_copy(out=o, in_=stg[:, 0:kn, 0:n]),
                     reads=[("wstg", sl)], writes=[(tag, c0, k0)])


def rope_tables(K, P, st, pos_arr, ntile, invf_att, invf_idx, tag):
    nc = K.nc
    posi = st.enter_context(nc.sbuf_tensor(tag + "posi", [128, ntile], I32))
    posf = st.enter_context(nc.sbuf_tensor(tag + "posf", [128, ntile], F32))
    iva = st.enter_context(nc.sbuf_tensor(tag + "iva", [128, 16], F32))
    ivi = st.enter_context(nc.sbuf_tensor(tag + "ivi", [128, 8], F32))
    P.dma("sp", posi[:], pos_arr, writes=[tag + "posi"])
    P.dma("sp", iva[:], invf_att, writes=[tag + "iva"])
    P.dma("sp", ivi[:], invf_idx, writes=[tag + "ivi"])
    P.op("dve", lambda e: e.tensor_copy(out=posf[:], in_=posi[:]), reads=[tag + "posi"], writes=[tag + "posf"])
    out = {}
    for nm, iv, h in (("a", iva, 16), ("i", ivi, 8)):
        u = st.enter_context(nc.sbuf_tensor(tag + "u" + nm, [128, ntile, h], F32))
        ui = st.enter_context(nc.sbuf_tensor(tag + "ui" + nm, [128, ntile, h], I32))
        uf = st.enter_context(nc.sbuf_tensor(tag + "uf" + nm, [128, ntile, h], F32))
        for fn, off in (("sin", 0.0), ("cos", 0.25)):
            tb = st.enter_context(nc.sbuf_tensor(tag + fn + nm, [128, ntile, h], F32))
            kk = tag + fn + nm
            P.op("dve", lambda e, u=u, iv=iv, h=h: e.tensor_tensor(
                out=u[:], in0=bc(posf[:].unsqueeze(2), [128, ntile, h]),
                in1=bc(iv[:].unsqueeze(1), [128, ntile, h]), op=ALU.mult),
                reads=[tag + "posf", tag + "iv" + nm], writes=[tag + "U" + nm])
            P.op("dve", lambda e, u=u, off=off: e.tensor_scalar(
                out=u[:], in0=u[:], scalar1=float(1.0 / (2 * np.pi)), scalar2=off, op0=ALU.mult, op1=ALU.add),
                reads=[tag + "U" + nm], writes=[tag + "U" + nm])
            P.op("dve", lambda e, u=u, ui=ui: e.tensor_copy(out=ui[:], in_=u[:]), reads=[tag + "U" + nm], writes=[tag + "UI" + nm])
            P.op("dve", lambda e, uf=uf, ui=ui: e.tensor_copy(out=uf[:], in_=ui[:]), reads=[tag + "UI" + nm], writes=[tag + "UF" + nm])
            P.op("dve", lambda e, u=u, uf=uf: e.tensor_tensor(out=u[:], in0=u[:], in1=uf[:], op=ALU.subtract),
                 reads=[tag + "U" + nm, tag + "UF" + nm], writes=[tag + "U" + nm])
            P.op("dve", lambda e, u=u: e.tensor_scalar(out=u[:], in0=u[:], scalar1=-0.5, scalar2=0.5,
                                                        op0=ALU.max, op1=ALU.min),
                 reads=[tag + "U" + nm], writes=[tag + "U" + nm])
            P.op("act", lambda e, u=u, tb=tb: e.activation(out=tb[:], in_=u[:], func=AF.Sin,
                                                            scale=float(2 * np.pi)),
                 reads=[tag + "U" + nm], writes=[kk])
            out[fn + nm] = (tb, kk)
    return out


def apply_rope(P, eng, x4, cos, sin, t, half, tmp, rk, wk, sfx=""):
    ctb, ck = cos
    stb, sk = sin
    H = x4.shape[1]
    x1 = x4[:, :, 0:half]
    x2 = x4[:, :, half:2 * half]
    cb = bc(ctb[:, t, :].unsqueeze(1), [128, H, half])
    sb = bc(stb[:, t, :].unsqueeze(1), [128, H, half])
    a, b2, c, d = tmp
    P.op(eng, lambda e: e.tensor_tensor(out=a[:, 0:H, 0:half], in0=x1, in1=cb, op=ALU.mult), reads=rk + [ck], writes=["rtmpA" + sfx])
    P.op(eng, lambda e: e.tensor_tensor(out=b2[:, 0:H, 0:half], in0=x2, in1=sb, op=ALU.mult), reads=rk + [sk], writes=["rtmpB" + sfx])
    P.op(eng, lambda e: e.tensor_tensor(out=c[:, 0:H, 0:half], in0=x2, in1=cb, op=ALU.mult), reads=rk + [ck], writes=["rtmpC" + sfx])
    P.op(eng, lambda e: e.tensor_tensor(out=d[:, 0:H, 0:half], in0=x1, in1=sb, op=ALU.mult), reads=rk + [sk], writes=["rtmpD" + sfx])
    P.op(eng, lambda e: e.tensor_tensor(out=x1, in0=a[:, 0:H, 0:half], in1=b2[:, 0:H, 0:half], op=ALU.subtract),
         reads=["rtmpA" + sfx, "rtmpB" + sfx, "rtmpC" + sfx, "rtmpD" + sfx] + rk, writes=rk)
    P.op(eng, lambda e: e.tensor_tensor(out=x2, in0=c[:, 0:H, 0:half], in1=d[:, 0:H, 0:half], op=ALU.add),
         reads=["rtmpC" + sfx, "rtmpD" + sfx] + rk, writes=rk)


def head_rmsnorm(P, x3, gain, sq, ssum, rk, wk, gk=None, sqk=None):
    P.op("pool", lambda e: e.tensor_tensor(out=sq[:], in0=x3, in1=x3, op=ALU.mult), reads=rk, writes=[sqk or (wk + "sq")])
    P.op("dve", lambda e: e.tensor_reduce(out=ssum[:, 0:8], in_=sq[:], axis=AX.X, op=ALU.add),
         reads=[sqk or (wk + "sq")], writes=[wk + "s0"])
    P.op("dve", lambda e: e.tensor_scalar(out=ssum[:, 8:16], in0=ssum[:, 0:8], scalar1=1.0 / 128, scalar2=1e-6,
                                           op0=ALU.mult, op1=ALU.add), reads=[wk + "s0"], writes=[wk + "s1"])
    P.op("act", lambda e: e.activation(out=ssum[:, 16:24], in_=ssum[:, 8:16], func=AF.Sqrt),
         reads=[wk + "s1"], writes=[wk + "s2"])
    P.op("dve", lambda e: e.reciprocal(out=ssum[:, 24:32], in_=ssum[:, 16:24]), reads=[wk + "s2"], writes=[wk + "s3"])
    P.op("dve", lambda e: e.tensor_tensor(out=x3, in0=x3, in1=bc(ssum[:, 24:32].unsqueeze(2), [128, 8, 128]),
                                           op=ALU.mult), reads=rk + [wk + "s3"], writes=rk)
    P.op("pool", lambda e: e.tensor_tensor(out=x3, in0=x3, in1=bc(gain[:].unsqueeze(1), [128, 8, 128]),
                                            op=ALU.mult), reads=rk + [gk or ("gain" + wk)], writes=rk)


def norm_load(K, P, T, x_src, t):
    xs = t % 2
    P.dma("sp", T["xt"][xs][:], x_src[t * 128:(t + 1) * 128, :], writes=[("xt", xs)])


def norm_block(K, P, T, x_src, t, G1, SH1, ident, blk_hT, ti, load=True, hname="hT"):
    xs = t % 2
    xt, hb, ss, junk, pT = T["xt"], T["hb"], T["ss"], T["junk"], T["pT"]
    if load:
        norm_load(K, P, T, x_src, t)
    P.op("act", lambda e: e.activation(out=junk[:], in_=xt[xs][:], func=AF.Square, accum_out=ss[:, 0:1]),
         reads=[("xt", xs)], writes=["junk", "ss0"])
    P.op("dve", lambda e: e.tensor_scalar(out=ss[:, 1:2], in0=ss[:, 0:1], scalar1=1.0 / D, scalar2=1e-6,
                                           op0=ALU.mult, op1=ALU.add), reads=["ss0"], writes=["ss1"])
    P.op("act", lambda e: e.activation(out=ss[:, 2:3], in_=ss[:, 1:2], func=AF.Sqrt), reads=["ss1"], writes=["ss2"])
    P.op("dve", lambda e: e.reciprocal(out=ss[:, 3:4], in_=ss[:, 2:3]), reads=["ss2"], writes=["ss3"])
    P.op("dve", lambda e: e.scalar_tensor_tensor(out=xt[xs][:], in0=xt[xs][:], scalar=ss[:, 3:4], in1=G1[:],
                                                  op0=ALU.mult, op1=ALU.mult),
         reads=[("xt", xs), "ss3", "G"], writes=[("xt", xs)])
    P.op("pool", lambda e: e.tensor_tensor(out=hb[xs][:], in0=xt[xs][:], in1=SH1[:], op=ALU.add),
         reads=[("xt", xs), "SH"], writes=[("hb", xs)])
    for half in range(2):
        for kk in range(8):
            k = half * 8 + kk
            P.op("pe", lambda e, k=k, kk=kk, half=half: e.transpose(
                out=pT[half][:, kk, :], in_=hb[xs][:, k * 128:(k + 1) * 128], identity=ident[:]),
                reads=[("hb", xs), "ident"], writes=[("pT", half)])
        o = blk_hT[:, half * 8:(half + 1) * 8, ti * 128:(ti + 1) * 128]
        if half == 0:
            P.op("act", lambda e, o=o, half=half: e.copy(out=o, in_=pT[half][:]),
                 reads=[("pT", half)], writes=[(hname, ti, half)])
        else:
            P.op("dve", lambda e, o=o, half=half: e.tensor_copy(out=o, in_=pT[half][:]),
                 reads=[("pT", half)], writes=[(hname, ti, half)])


def norm_tiles_alloc(K, st, tag):
    nc = K.nc
    T = {}
    T["xt"] = [st.enter_context(nc.sbuf_tensor(tag + "xt%d" % i, [128, D], F32)) for i in range(2)]
    T["hb"] = [st.enter_context(nc.sbuf_tensor(tag + "hb%d" % i, [128, D], BF16)) for i in range(2)]
    T["ss"] = st.enter_context(nc.sbuf_tensor(tag + "ss", [128, 4], F32))
    T["junk"] = st.enter_context(nc.sbuf_tensor(tag + "junk", [128, D], BF16))
    T["pT"] = [st.enter_context(nc.psum_tensor(tag + "pT%d" % i, [128, 8, 128], BF16)) for i in range(2)]
    return T


def load_G_SH(K, P, st, which_sh, which_sc, gain_vec, tag):
    nc = K.nc
    G = st.enter_context(nc.sbuf_tensor(tag + "G", [128, D], F32))
    SH = st.enter_context(nc.sbuf_tensor(tag + "SH", [128, D], F32))
    gtmp = st.enter_context(nc.sbuf_tensor(tag + "gtmp", [128, D], F32))
    P.dma("sp", SH[:], bcast_rows(K.mod_d[which_sh * D:(which_sh + 1) * D], D), writes=["SH"])
    P.dma("sp", G[:], bcast_rows(K.mod_d[which_sc * D:(which_sc + 1) * D], D), writes=["G"])
    P.dma("sp", gtmp[:], bcast_rows(gain_vec, D), writes=["gtmp"])
    P.op("dve", lambda e: e.scalar_tensor_tensor(out=G[:], in0=G[:], scalar=1.0, in1=gtmp[:],
                                                  op0=ALU.add, op1=ALU.mult), reads=["G", "gtmp"], writes=["G"])
    return G, SH


def phase1_kv(K):
    nc, P = K.nc, K.P
    with contextlib.ExitStack() as st:
        ident = st.enter_context(nc.sbuf_tensor("ident", [128, 128], BF16))
        make_ident(K, P, ident)
        G1, SH1 = load_G_SH(K, P, st, 0, 1, K.norm1_g, "p1")
        T = norm_tiles_alloc(K, st, "p1")
        hT = [st.enter_context(nc.sbuf_tensor("hT%d" % i, [128, 16, 512], BF16)) for i in range(2)]
        W = st.enter_context(nc.sbuf_tensor("Wkv", [128, 16, 2112], BF16))
        stg = [st.enter_context(nc.sbuf_tensor("wstg%d" % i, [128, 4, 512], F32)) for i in range(2)]
        wk_k = load_weight_bf16(K, P, stg, W, 0, K.w_in[:, 1024:2048], 1024, "Wk")
        wk_v = load_weight_bf16(K, P, stg, W, 1024, K.w_in[:, 2048:3072], 1024, "Wv")
        wk_i = load_weight_bf16(K, P, stg, W, 2048, K.w_in[:, 4096:4160], 64, "Wi")
        rt = rope_tables(K, P, st, K.pos_full, 32, K.invf_att, K.invf_idx, "rf")
        gain = st.enter_context(nc.sbuf_tensor("kgain", [128, 128], F32))
        P.dma("sp", gain[:], bcast_rows(K.k_norm_g, 128), writes=["gainK"])
        def two(name, shape, dt):
            return [st.enter_context(nc.sbuf_tensor(name + str(i), shape, dt)) for i in range(2)]
        ksb2 = two("ksb", [128, 8, 128], F32)
        kbf2 = two("kbf", [128, 8, 128], BF16)
        sq2 = [st.enter_context(nc.sbuf_tensor("sq", [128, 8, 128], F32))] * 2
        ssum2 = two("ssum", [128, 32], F32)
        rtmp2 = [[st.enter_context(nc.sbuf_tensor("rtmp%d" % i, [128, 8, 16], F32)) for i in range(4)]] * 2
        vsb2 = two("vsb", [128, 8, 129], BF16)
        iksb2 = two("iksb", [128, 1, 64], F32)
        ikbf2 = two("ikbf", [128, 64], BF16)
        kTs2 = [st.enter_context(nc.sbuf_tensor("kTs", [128, 8, 128], BF16))] * 2
        ikTs2 = two("ikTs", [64, 128], BF16)
        pm = [st.enter_context(nc.psum_tensor("pm%d" % i, [128, 512], F32)) for i in range(3)]
        pk = st.enter_context(nc.psum_tensor("pk", [128, 8, 128], BF16))
        for s_ in range(2):
            P.op("pool", lambda e, s_=s_: e.memset(vsb2[s_][:], 1.0), writes=["vsb%d" % s_])
        norm_load(K, P, T, K.x_full, 0)

        def norm_tile(blk, ti):
            tt_ = blk * 4 + ti
            if tt_ + 1 < 32:
                norm_load(K, P, T, K.x_full, tt_ + 1)
            norm_block(K, P, T, K.x_full, tt_, G1, SH1, ident, hT[blk % 2], ti, load=False, hname=("hT", blk % 2))

        def store_hT(blk):
            hs = blk % 2
            hkeys = [(("hT", hs), ti, half) for ti in range(4) for half in range(2)]
            P.dma("sp", K.hT_d.rearrange("k p t -> p k t")[:, :, blk * 512:(blk + 1) * 512], hT[hs][:],
                  reads=hkeys, writes=[("hT_d", blk)])

        def bufs(t):
            u = t % 2
            return (str(u), ksb2[u], kbf2[u], sq2[u], ssum2[u], rtmp2[u], vsb2[u], iksb2[u], ikbf2[u], kTs2[u], ikTs2[u])

        def mm_tile(blk, ti):
            t = blk * 4 + ti
            hs = blk % 2
            hk = [(("hT", hs), ti, 0), (("hT", hs), ti, 1)]
            us, ksb, kbf, sq, ssum, rtmp, vsb, iksb, ikbf, kTs, ikTs = bufs(t)
            for gi, (c0, n, wkeys) in enumerate([(0, 512, wk_k), (512, 512, wk_k), (1024, 512, wk_v),
                                                 (1536, 512, wk_v), (2048, 64, wk_i)]):
                pb = pm[gi % 3]
                for k in range(16):
                    P.op("pe", lambda e, pb=pb, k=k, c0=c0, n=n, ti=ti, hs=hs: e.matmul(
                        pb[:, 0:n], lhsT=hT[hs][:, k, ti * 128:(ti + 1) * 128], rhs=W[:, k, c0:c0 + n],
                        start=(k == 0), stop=(k == 15)), reads=hk + wkeys, writes=[("pm", gi % 3)])
                if gi < 2:
                    P.op("act", lambda e, pb=pb, gi=gi, ksb=ksb: e.copy(out=ksb[:, gi * 4:(gi + 1) * 4, :], in_=pb[:, 0:512]),
                         reads=[("pm", gi % 3)], writes=["ksb" + us])
                elif gi < 4:
                    g2 = gi - 2
                    P.op("act", lambda e, pb=pb, g2=g2, vsb=vsb: e.copy(out=vsb[:, g2 * 4:(g2 + 1) * 4, 0:128], in_=pb[:, 0:512]),
                         reads=[("pm", gi % 3)], writes=["vsb" + us])
                else:
                    P.op("act", lambda e, pb=pb, iksb=iksb: e.copy(out=iksb[:, 0, :], in_=pb[:, 0:64]),
                         reads=[("pm", gi % 3)], writes=["iksb" + us])
            P.dma("sp", K.v_d[t * 128:(t + 1) * 128, :], vsb[:].rearrange("p h d -> p (h d)"),
                  reads=["vsb" + us], writes=[("v_d", t)])

        def post1(blk, ti):
            t = blk * 4 + ti
            us, ksb, kbf, sq, ssum, rtmp, vsb, iksb, ikbf, kTs, ikTs = bufs(t)
            head_rmsnorm(P, ksb[:], gain, sq, ssum, ["ksb" + us], "K" + us, gk="gainK", sqk="Ksq")
            apply_rope(P, "dve", ksb[:], rt["cosa"], rt["sina"], t, 16, rtmp, ["ksb" + us], "rK")
            P.op("act", lambda e, kbf=kbf, ksb=ksb: e.copy(out=kbf[:], in_=ksb[:]), reads=["ksb" + us], writes=["kbf" + us])
            apply_rope(P, "pool", iksb[:], rt["cosi"], rt["sini"], t, 8, rtmp, ["iksb" + us], "rI")
            P.op("act", lambda e, ikbf=ikbf, iksb=iksb: e.copy(out=ikbf[:], in_=iksb[:, 0, :]), reads=["iksb" + us], writes=["ikbf" + us])

        def post2(blk, ti):
            t = blk * 4 + ti
            us, ksb, kbf, sq, ssum, rtmp, vsb, iksb, ikbf, kTs, ikTs = bufs(t)
            for h in range(8):
                P.op("pe", lambda e, h=h, kbf=kbf: e.transpose(out=pk[:, h, :], in_=kbf[:, h, :], identity=ident[:]),
                     reads=["kbf" + us, "ident"], writes=["pk"])
            P.op("dve", lambda e, kTs=kTs: e.tensor_copy(out=kTs[:], in_=pk[:]), reads=["pk"], writes=["kTs"])
            P.dma("sp", K.kT_d.rearrange("h p t -> p h t")[:, :, t * 128:(t + 1) * 128], kTs[:],
                  reads=["kTs"], writes=[("kT_d", t)])
            P.op("pe", lambda e, ikbf=ikbf: e.transpose(out=pk[0:64, 0, :], in_=ikbf[:], identity=ident[:]),
                 reads=["ikbf" + us, "ident"], writes=["pk"])
            P.op("dve", lambda e, ikTs=ikTs: e.tensor_copy(out=ikTs[:], in_=pk[0:64, 0, :]), reads=["pk"], writes=["ikTs" + us])
            P.dma("sp", K.ikT_d[:, t * 128:(t + 1) * 128], ikTs[:], reads=["ikTs" + us], writes=[("ikT_d", t)])

        for ti in range(4):
            norm_tile(0, ti)
        store_hT(0)
        prev = None
        for blk in range(8):
            for ti in range(4):
                mm_tile(blk, ti)
                if blk + 1 < 8:
                    norm_tile(blk + 1, ti)
                post1(blk, ti)
                if prev is not None:
                    post2(*prev)
                prev = (blk, ti)
            if blk + 1 < 8:
                store_hT(blk + 1)
        post2(*prev)
        P.flush()

RW0 = 4176
NRW = 1216
RW_GROUPS = [(i * 128, 128) for i in range(6)] + [(768, 96), (864, 96), (960, 128), (1088, 128)]


def phase1b_rwkv_proj(K):
    nc, P = K.nc, K.P
    with contextlib.ExitStack() as st:
        W = st.enter_context(nc.sbuf_tensor("Wr", [128, 16, NRW], BF16))
        stg = [st.enter_context(nc.sbuf_tensor("wstgb%d" % i, [128, 4, 512], F32)) for i in range(2)]
        hT = [st.enter_context(nc.sbuf_tensor("hTb%d" % i, [128, 16, 512], BF16)) for i in range(2)]
        ost = [st.enter_context(nc.sbuf_tensor("ost%d" % i, [128, 512], F32)) for i in range(4)]
        pm = [st.enter_context(nc.psum_tensor("pmb%d" % i, [128, 512], F32)) for i in range(4)]
        wkeys = load_weight_bf16(K, P, stg, W, 0, K.w_in_rw, NRW, "Wr")
        cnt = 0
        for blk in range(8):
            hs = blk % 2
            P.dma("sp", hT[hs][:], K.hT_d.rearrange("k p t -> p k t")[:, :, blk * 512:(blk + 1) * 512],
                  writes=[("hTb", hs)])
            for (r0, m) in RW_GROUPS:
                s4 = cnt % 4
                cnt += 1
                for k in range(16):
                    P.op("pe", lambda e, k=k, r0=r0, m=m, hs=hs, s4=s4: e.matmul(
                        pm[s4][0:m, :], lhsT=W[:, k, r0:r0 + m], rhs=hT[hs][:, k, :],
                        start=(k == 0), stop=(k == 15)), reads=[("hTb", hs)] + wkeys, writes=[("pmb", s4)])
                if cnt % 2 == 0:
                    P.op("act", lambda e, m=m, s4=s4: e.copy(out=ost[s4][0:m, :], in_=pm[s4][0:m, :]),
                         reads=[("pmb", s4)], writes=[("ost", s4)])
                else:
                    P.op("dve", lambda e, m=m, s4=s4: e.tensor_copy(out=ost[s4][0:m, :], in_=pm[s4][0:m, :]),
                         reads=[("pmb", s4)], writes=[("ost", s4)])
                P.dma("sp", K.yT_d[r0:r0 + m, blk * 512:(blk + 1) * 512], ost[s4][0:m, :],
                      reads=[("ost", s4)], writes=[("yT_d", r0, blk)])
        P.flush()


def phase2_own_proj(K):
    nc, P = K.nc, K.P
    with contextlib.ExitStack() as st:
        ident = st.enter_context(nc.sbuf_tensor("ident2", [128, 128], BF16))
        make_ident(K, P, ident)
        G1, SH1 = load_G_SH(K, P, st, 0, 1, K.norm1_g, "p2")
        T = norm_tiles_alloc(K, st, "p2")
        hT = [st.enter_context(nc.sbuf_tensor("hTo%d" % i, [128, 16, 512], BF16)) for i in range(2)]
        W = st.enter_context(nc.sbuf_tensor("Wq", [128, 16, 2064], BF16))
        stg = [st.enter_context(nc.sbuf_tensor("wstgq%d" % i, [128, 4, 512], F32)) for i in range(2)]
        wk_q = load_weight_bf16(K, P, stg, W, 0, K.w_in[:, 0:1024], 1024, "Wq")
        wk_iq = load_weight_bf16(K, P, stg, W, 1024, K.w_in[:, 3072:4096], 1024, "Wiq")
        wk_iw = load_weight_bf16(K, P, stg, W, 2048, K.w_in[:, 4160:4176], 16, "Wiw")
        rt = rope_tables(K, P, st, K.pos_own, 8, K.invf_att, K.invf_idx, "ro")
        gain = st.enter_context(nc.sbuf_tensor("qgain", [128, 128], F32))
        P.dma("sp", gain[:], bcast_rows(K.q_norm_g, 128), writes=["gainQ"])
        qsb = st.enter_context(nc.sbuf_tensor("qsb", [128, 8, 128], F32))
        qbf = st.enter_context(nc.sbuf_tensor("qbf", [128, 8, 128], BF16))
        sq = st.enter_context(nc.sbuf_tensor("sq2", [128, 8, 128], F32))
        ssum = st.enter_context(nc.sbuf_tensor("ssum2", [128, 32], F32))
        rtmp = [st.enter_context(nc.sbuf_tensor("rtmpq%d" % i, [128, 16, 16], F32)) for i in range(4)]
        iqsb = st.enter_context(nc.sbuf_tensor("iqsb", [128, 16, 64], F32))
        iqbf = st.enter_context(nc.sbuf_tensor("iqbf", [128, 16, 64], BF16))
        iwsb = st.enter_context(nc.sbuf_tensor("iwsb", [128, 16], F32))
        qTs = st.enter_context(nc.sbuf_tensor("qTs", [128, 8, 128], BF16))
        iqTs = st.enter_context(nc.sbuf_tensor("iqTs", [64, 128, 16], BF16))
        pm = [st.enter_context(nc.psum_tensor("pmq%d" % i, [128, 512], F32)) for i in range(3)]
        pk = st.enter_context(nc.psum_tensor("pkq", [128, 8, 128], BF16))
        for blk in range(2):
            hs = blk % 2
            for ti in range(4):
                norm_block(K, P, T, K.x_own, blk * 4 + ti, G1, SH1, ident, hT[hs], ti)
            for ti in range(4):
                t = blk * 4 + ti
                hk = [("hT", ti, 0), ("hT", ti, 1)]
                for gi, (c0, n, wkeys) in enumerate([(0, 512, wk_q), (512, 512, wk_q), (1024, 512, wk_iq),
                                                     (1536, 512, wk_iq), (2048, 16, wk_iw)]):
                    pb = pm[gi % 3]
                    for k in range(16):
                        P.op("pe", lambda e, pb=pb, k=k, c0=c0, n=n, ti=ti, hs=hs: e.matmul(
                            pb[:, 0:n], lhsT=hT[hs][:, k, ti * 128:(ti + 1) * 128], rhs=W[:, k, c0:c0 + n],
                            start=(k == 0), stop=(k == 15)), reads=hk + wkeys, writes=[("pmq", gi % 3)])
                    if gi < 2:
                        P.op("act", lambda e, pb=pb, gi=gi: e.copy(out=qsb[:, gi * 4:(gi + 1) * 4, :], in_=pb[:, 0:512]),
                             reads=[("pmq", gi % 3)], writes=["qsb"])
                    elif gi < 4:
                        g2 = gi - 2
                        P.op("act", lambda e, pb=pb, g2=g2: e.copy(out=iqsb[:, g2 * 8:(g2 + 1) * 8, :], in_=pb[:, 0:512]),
                             reads=[("pmq", gi % 3)], writes=["iqsb"])
                    else:
                        P.op("act", lambda e, pb=pb: e.activation(out=iwsb[:], in_=pb[:, 0:16], func=AF.Copy, scale=0.25),
                             reads=[("pmq", gi % 3)], writes=["iwsb"])
                P.dma("sp", K.iw_d[t * 128:(t + 1) * 128, :], iwsb[:], reads=["iwsb"], writes=[("iw_d", t)])
                head_rmsnorm(P, qsb[:], gain, sq, ssum, ["qsb"], "Q")
                apply_rope(P, "dve", qsb[:], rt["cosa"], rt["sina"], t, 16, rtmp, ["qsb"], "rQ")
                P.op("act", lambda e: e.copy(out=qbf[:], in_=qsb[:]), reads=["qsb"], writes=["qbf"])
                for h in range(8):
                    P.op("pe", lambda e, h=h: e.transpose(out=pk[:, h, :], in_=qbf[:, h, :], identity=ident[:]),
                         reads=["qbf", "ident"], writes=["pkq"])
                P.op("dve", lambda e: e.tensor_copy(out=qTs[:], in_=pk[:]), reads=["pkq"], writes=["qTs"])
                P.dma("sp", K.qT_d.rearrange("h p t -> p h t")[:, :, t * 128:(t + 1) * 128], qTs[:],
                      reads=["qTs"], writes=[("qT_d", t)])
                apply_rope(P, "pool", iqsb[:], rt["cosi"], rt["sini"], t, 8, rtmp, ["iqsb"], "rIQ")
                P.op("act", lambda e: e.activation(out=iqbf[:], in_=iqsb[:], func=AF.Copy, scale=0.125),
                     reads=["iqsb"], writes=["iqbf"])
                for half in range(2):
                    for hh in range(8):
                        h = half * 8 + hh
                        P.op("pe", lambda e, h=h, hh=hh: e.transpose(out=pk[0:64, hh, :], in_=iqbf[:, h, :],
                                                                      identity=ident[:]),
                             reads=["iqbf", "ident"], writes=["pkq"])
                    P.op("dve", lambda e, half=half: e.tensor_copy(
                        out=iqTs[:, :, half * 8:(half + 1) * 8].rearrange("p t h -> p h t"), in_=pk[0:64, :, :]),
                         reads=["pkq"], writes=["iqTs"])
                P.dma("sp", K.iqT_d[:, t * 128:(t + 1) * 128, :], iqTs[:], reads=["iqTs"], writes=[("iqT_d", t)])
        P.flush()


NIT = 24
SLOT_NK = [4, 8, 12, 16, 20, 24, 28, 32]


def phase3_attention(K):
    nc, P = K.nc, K.P
    with contextlib.ExitStack() as st:
        def sb(name, shape, dt):
            return st.enter_context(nc.sbuf_tensor(name, shape, dt))
        ident = sb("ident3", [128, 128], BF16)
        identf = sb("identf3", [128, 128], F32)
        make_ident(K, P, ident)
        P.op("dve", lambda e: e.tensor_copy(out=identf[:], in_=ident[:]), reads=["ident"], writes=["identf"])
        kT = sb("kTall", [128, 8, S], BF16)
        V = sb("Vall", [128, 32, 1032], BF16)
        ikT = sb("ikTall", [64, S], BF16)
        for h in range(8):
            P.dma("sp", kT[:, h, :], K.kT_d[h], writes=[("kT", h)])
        for q4 in range(4):
            P.dma("sp", V[:, q4 * 8:(q4 + 1) * 8, :],
                  K.v_d.rearrange("(t p) c -> p t c", p=128)[:, q4 * 8:(q4 + 1) * 8, :], writes=[("V", q4)])
        P.dma("sp", ikT[:], K.ikT_d, writes=["ikT"])
        kTk = [("kT", h) for h in range(8)]
        Vk = [("V", q4) for q4 in range(4)]
        Sel = sb("Sel", [128, 16, 128], BF16)
        pidx = sb("pidx", [128, 1], I32)
        pidf = sb("pidf", [128, 1], F32)
        score = sb("score", [128, S], F32)
        self_ = score[:, 0:2048].rearrange("p (g t) -> p g t", g=16)
        sk4 = [("score", q) for q in range(4)]
        P.op("pool", lambda e: e.iota(self_, pattern=[[-8, 16], [1, 128]], base=0, channel_multiplier=0, allow_small_or_imprecise_dtypes=True), writes=sk4)
        P.op("pool", lambda e: e.iota(pidx[:], pattern=[[0, 1]], base=0, channel_multiplier=1), writes=["pidx"])
        P.op("dve", lambda e: e.tensor_scalar(out=pidx[:], in0=pidx[:], scalar1=4, scalar2=None,
                                               op0=ALU.arith_shift_right), reads=["pidx"], writes=["pidx"])
        P.op("dve", lambda e: e.tensor_copy(out=pidf[:], in_=pidx[:]), reads=["pidx"], writes=["pidf"])
        P.op("dve", lambda e: e.tensor_scalar(out=Sel[:], in0=self_, scalar1=pidf[:, 0:1], scalar2=None,
                                               op0=ALU.is_equal), reads=sk4 + ["pidf"], writes=["Sel"])
        kposi = sb("kposi", [128, 512], I32)
        kposf = sb("kposf", [128, 512], F32)
        qpos = sb("qpos", [128, 8], F32)
        P.dma("sp", qpos[:], K.qpos_own, writes=["qpos"])
        iwg = sb("iwg", [128, 128], F32)
        wcol = sb("wcol", [128, 128], F32)
        P.dma("sp", iwg[:], K.iw_d.rearrange("(g t) h -> g (t h)", t=8), writes=["iwg"])
        A = [st.enter_context(nc.psum_tensor("A%d" % i, [128, 512], F32)) for i in range(2)]
        B = [st.enter_context(nc.psum_tensor("B%d" % i, [128, 512], F32)) for i in range(2)]
        C = st.enter_context(nc.psum_tensor("C3", [128, 8, 128], BF16))
        P.op("pe", lambda e: e.transpose(out=A[0][:, 0:128], in_=iwg[:], identity=identf[:]),
             reads=["iwg", "identf"], writes=[("A", 0)])
        P.op("dve", lambda e: e.tensor_copy(out=wcol[:], in_=A[0][:, 0:128]), reads=[("A", 0)], writes=["wcol"])
        mask01 = sb("mask01", [128, S], BF16)
        maskT = sb("maskT", [128, 32, 128], BF16)
        R = [sb("R%d" % i, [128, 512], BF16) for i in range(2)]
        pexp = [sb("pexp%d" % i, [128, 512], BF16) for i in range(2)]
        pmk = [sb("pmk%d" % i, [128, 512], BF16) for i in range(2)]
        iqTs = sb("iqTs3", [64, 128, 16], BF16)
        qTs = sb("qTs3", [128, 8, 128], BF16)
        att = sb("att", [128, 8, 128], BF16)
        attTs = sb("attTs", [128, 8, 128], BF16)
        bias = sb("cbias", [128, 512], F32)
        c2 = sb("c2", [128, NIT], F32)
        steps = sb("steps", [128, NIT], F32)
        sm = sb("sm3", [128, 8], F32)
        for k in range(NIT):
            P.op("pool", lambda e, k=k: e.memset(c2[:, k:k + 1], float(2.0 ** -(k + 1))), writes=["c2"])
        maskT2 = [maskT, sb("maskTb", [128, 32, 128], BF16)]
        qTs2 = [qTs, sb("qTs3b", [128, 8, 128], BF16)]
        rcp2 = sb("rcp2", [128, 2], F32)

        def stageA(i):
            nk = SLOT_NK[i]
            nb = nk // 4
            P.dma("sp", iqTs[:], K.iqT_d[:, i * 128:(i + 1) * 128, :], writes=["iqTs"])
            P.dma("sp", qTs2[i % 2][:], K.qT_d.rearrange("h p t -> p h t")[:, :, i * 128:(i + 1) * 128], writes=[("qTs", i % 2)])
            isteps = [(sbk, g) for sbk in range(nb) for g in range(16)]

            def dots(si):
                sbk, g = isteps[si]
                a_ = si % 2
                lhsT = iqTs[:, g * 8:(g + 1) * 8, :].rearrange("p t h -> p (t h)")
                P.op("pe", lambda e, a_=a_, lhsT=lhsT, sbk=sbk: e.matmul(
                    A[a_][:, :], lhsT=lhsT, rhs=ikT[:, sbk * 512:(sbk + 1) * 512], start=True, stop=True),
                    reads=["iqTs", "ikT"], writes=[("A", a_)])
            dots(0)
            for si, (sbk, g) in enumerate(isteps):
                a_ = si % 2
                bsl = sbk % 2
                if si + 1 < len(isteps):
                    dots(si + 1)
                G = i * 16 + g
                P.op("dve", lambda e, a_=a_, G=G: e.tensor_scalar(
                    out=R[a_][:], in0=A[a_][:, :], scalar1=0.0, scalar2=wcol[:, G:G + 1],
                    op0=ALU.max, op1=ALU.mult), reads=[("A", a_), "wcol"], writes=[("R", a_)])
                P.op("pe", lambda e, a_=a_, g=g, bsl=bsl: e.matmul(
                    B[bsl][:, :], lhsT=Sel[:, g, :], rhs=R[a_][:], start=(g == 0), stop=(g == 15)),
                    reads=[("R", a_), "Sel"], writes=[("B", bsl)])
                if g == 15:
                    P.op("dve", lambda e, bsl=bsl, sbk=sbk: e.tensor_copy(out=score[:, sbk * 512:(sbk + 1) * 512], in_=B[bsl][:, :]),
                         reads=[("B", bsl)], writes=[("score", sbk)])

        def stageBdve(i):
            nk = SLOT_NK[i]
            nb = nk // 4
            L = nk * 128
            sck = [("score", sbk) for sbk in range(nb)]
            P.op("dve", lambda e, L=L: e.tensor_reduce(out=sm[:, 0:1], in_=score[:, 0:L], axis=AX.X, op=ALU.max,
                                                        apply_absolute_value=True), reads=sck, writes=["sm0"])
            P.op("pool", lambda e, nb=nb: e.iota(kposi[:], pattern=[[1, 512]], base=(nb - 1) * 512, channel_multiplier=0),
                 writes=["kposi"])
            P.op("dve", lambda e: e.tensor_copy(out=kposf[:], in_=kposi[:]), reads=["kposi"], writes=["kposf"])
            P.op("dve", lambda e, i=i: e.tensor_scalar(out=bias[:], in0=kposf[:], scalar1=qpos[:, i:i + 1],
                                                        scalar2=-1e30, op0=ALU.is_gt, op1=ALU.mult),
                 reads=["kposf", "qpos"], writes=["bias"])
            P.op("dve", lambda e, nb=nb: e.tensor_tensor(out=score[:, (nb - 1) * 512:nb * 512],
                                                          in0=score[:, (nb - 1) * 512:nb * 512], in1=bias[:], op=ALU.add),
                 reads=["bias", ("score", nb - 1), "sm0"], writes=[("score", nb - 1)])
            P.op("dve", lambda e: e.tensor_scalar(out=sm[:, 1:2], in0=sm[:, 0:1], scalar1=-1.0, scalar2=-1.0,
                                                   op0=ALU.mult, op1=ALU.add), reads=["sm0"], writes=["lo"])
            P.op("dve", lambda e: e.tensor_scalar(out=sm[:, 5:6], in0=sm[:, 0:1], scalar1=2.0, scalar2=2.0,
                                                   op0=ALU.mult, op1=ALU.add), reads=["sm0"], writes=["d0"])
            P.op("dve", lambda e: e.tensor_scalar(out=steps[:], in0=c2[:], scalar1=sm[:, 5:6], scalar2=None,
                                                   op0=ALU.mult), reads=["d0", "c2"], writes=["steps"])
            for k in range(NIT):
                P.op("dve", lambda e, k=k: e.tensor_tensor(out=sm[:, 2:3], in0=sm[:, 1:2], in1=steps[:, k:k + 1],
                                                            op=ALU.add), reads=["lo", "steps"], writes=["mid"])
                P.op("dve", lambda e, L=L: e.tensor_scalar(out=mask01[:, 0:L], in0=score[:, 0:L], scalar1=sm[:, 2:3],
                                                            scalar2=None, op0=ALU.is_ge, op1=ALU.add,
                                                            accum_out=sm[:, 3:4]),
                     reads=sck + ["mid"], writes=["mask01", "cnt"])
                P.op("dve", lambda e, k=k: e.scalar_tensor_tensor(out=sm[:, 4:5], in0=sm[:, 3:4], scalar=255.5,
                                                                   in1=steps[:, k:k + 1], op0=ALU.is_ge, op1=ALU.mult),
                     reads=["cnt", "steps"], writes=["inc"])
                P.op("dve", lambda e: e.tensor_tensor(out=sm[:, 1:2], in0=sm[:, 1:2], in1=sm[:, 4:5], op=ALU.add),
                     reads=["lo", "inc"], writes=["lo"])
            P.op("dve", lambda e, L=L: e.tensor_scalar(out=mask01[:, 0:L], in0=score[:, 0:L], scalar1=sm[:, 1:2],
                                                        scalar2=None, op0=ALU.is_ge), reads=sck + ["lo"], writes=["mask01"])

        def stageBpe(i):
            nk = SLOT_NK[i]
            mT = maskT2[i % 2]
            for kt in range(nk):
                P.op("pe", lambda e, kt=kt: e.transpose(out=C[:, kt % 8, :], in_=mask01[:, kt * 128:(kt + 1) * 128],
                                                         identity=ident[:]), reads=["mask01", "ident"], writes=["C"])
                if kt % 8 == 7 or kt == nk - 1:
                    k0 = (kt // 8) * 8
                    n8 = kt - k0 + 1
                    P.op("dve", lambda e, k0=k0, n8=n8, mT=mT: e.tensor_copy(out=mT[:, k0:k0 + n8, :], in_=C[:, 0:n8, :]),
                         reads=["C"], writes=[("maskT", i % 2, k0 // 8)])

        def stageC(i):
            nk = SLOT_NK[i]
            nb = nk // 4
            mT = maskT2[i % 2]
            qT_ = qTs2[i % 2]
            mk = [("maskT", i % 2, q) for q in range((nk + 7) // 8)]
            asteps = [(h, kg) for h in range(8) for kg in range(nb)]

            def qk(si):
                h, kg = asteps[si]
                a_ = si % 2
                for j4 in range(4):
                    kt = kg * 4 + j4
                    P.op("pe", lambda e, a_=a_, j4=j4, kt=kt, h=h: e.matmul(
                        A[a_][:, j4 * 128:(j4 + 1) * 128], lhsT=kT[:, h, kt * 128:(kt + 1) * 128], rhs=qT_[:, h, :],
                        start=True, stop=True), reads=kTk + [("qTs", i % 2)], writes=[("A", a_)])
            qk(0)
            for si, (h, kg) in enumerate(asteps):
                a_ = si % 2
                bsl = h % 2
                if si + 1 < len(asteps):
                    qk(si + 1)
                P.op("act", lambda e, a_=a_: e.activation(out=pexp[a_][:], in_=A[a_][:, :], func=AF.Exp,
                                                           scale=float(128 ** -0.5)),
                     reads=[("A", a_)], writes=[("pexp", a_)])
                P.op("pool", lambda e, a_=a_, kg=kg: e.tensor_tensor(
                    out=pmk[a_][:], in0=pexp[a_][:], in1=mT[:, kg * 4:(kg + 1) * 4, :].rearrange("p a t -> p (a t)"),
                    op=ALU.mult), reads=[("pexp", a_)] + mk, writes=[("pmk", a_)])
                for j4 in range(4):
                    kt = kg * 4 + j4
                    P.op("pe", lambda e, a_=a_, j4=j4, kt=kt, h=h, bsl=bsl, kg=kg, nb=nb: e.matmul(
                        B[bsl][:, 0:129], lhsT=pmk[a_][:, j4 * 128:(j4 + 1) * 128], rhs=V[:, kt, h * 129:(h + 1) * 129],
                        start=(kg == 0 and j4 == 0), stop=(kg == nb - 1 and j4 == 3)),
                        reads=[("pmk", a_)] + Vk, writes=[("B", bsl)])
                if kg == nb - 1:
                    P.op("act", lambda e, bsl=bsl: e.activation(out=rcp2[:, 0:1], in_=B[bsl][:, 128:129], func=AF.Ln),
                         reads=[("B", bsl)], writes=["rcpa"])
                    P.op("act", lambda e: e.activation(out=rcp2[:, 1:2], in_=rcp2[:, 0:1], func=AF.Exp, scale=-1.0),
                         reads=["rcpa"], writes=["rcpb"])
                    P.op("act", lambda e, bsl=bsl, h=h: e.activation(out=att[:, h, :], in_=B[bsl][:, 0:128], func=AF.Copy,
                                                                      scale=rcp2[:, 1:2]),
                         reads=[("B", bsl), "rcpb"], writes=["att"])
            for h in range(8):
                P.op("pe", lambda e, h=h: e.transpose(out=C[:, h, :], in_=att[:, h, :], identity=ident[:]),
                     reads=["att", "ident"], writes=["C"])
            P.op("act", lambda e: e.copy(out=attTs[:], in_=C[:]), reads=["C"], writes=["attTs"])
            P.dma("sp", K.attT_d.rearrange("h p t -> p h t")[:, :, i * 128:(i + 1) * 128], attTs[:],
                  reads=["attTs"], writes=[("attT_d", i)])

        stageA(0)
        stageBdve(0)
        stageBpe(0)
        for i in range(8):
            if i + 1 < 8:
                stageA(i + 1)
                stageBdve(i + 1)
            stageC(i)
            if i + 1 < 8:
                stageBpe(i + 1)
        P.flush()

RD = BF16
NCH = 64


def tok_shift(P, dst, raw, tmp, mu_ap, rk_raw, k_tmp, k_dst, n=128):
    P.op("pool", lambda e: e.tensor_tensor(out=tmp[0:n, 1:S], in0=raw[0:n, 0:S - 1], in1=raw[0:n, 1:S], op=ALU.subtract),
         reads=[rk_raw], writes=[k_tmp])
    P.op("pool", lambda e: e.tensor_scalar(out=tmp[0:n, 0:1], in0=raw[0:n, 0:1], scalar1=-1.0, scalar2=0.0,
                                            op0=ALU.mult, op1=ALU.add), reads=[rk_raw, k_tmp], writes=[k_tmp])
    P.op("dve", lambda e: e.scalar_tensor_tensor(out=dst[0:n, :], in0=tmp[0:n, :], scalar=mu_ap, in1=raw[0:n, :],
                                                  op0=ALU.mult, op1=ALU.add), reads=[rk_raw, k_tmp], writes=[k_dst])


def phase4b_rwkv_prep(K, cts=range(2)):
    nc, P = K.nc, K.P
    with contextlib.ExitStack() as st:
        def sb(name, shape, dt):
            return st.enter_context(nc.sbuf_tensor(name, shape, dt))
        txw = sb("txw", [96, S], BF16)
        xap = sb("xap", [96, S], BF16)
        sxg = sb("sxg", [128, 2, S], BF16)
        M01 = sb("M01", [128, S], BF16)
        wup = sb("wup", [96, 256], BF16)
        aup = sb("aup", [96, 256], BF16)
        gup = sb("gup", [128, 2, 256], BF16)
        wst = sb("wst4", [128, 2, 256], F32)
        bones = sb("bones", [128, 128], BF16)
        prm = sb("prm", [128, 12, 2], F32)
        mul = sb("mul", [128, 4], F32)
        PT = sb("PT", [128, S], F32)
        KK = sb("KK", [128, S], F32)
        KP = sb("KP", [128, S], F32)
        CL = sb("CL", [128, S], F32)
        RP = sb("RP", [128, S], BF16)
        VP = sb("VP", [128, S], BF16)
        AA = sb("AA", [128, S], BF16)
        K2 = sb("K2", [128, S], BF16)
        SQb = sb("SQb", [128, S], BF16)
        OUT = [sb("OUT%d" % i, [128, S], BF16) for i in range(2)]
        PCt = sb("PCt", [128, NCH], F32)
        ps = [st.enter_context(nc.psum_tensor("ps4_%d" % i, [128, 512], F32)) for i in range(4)]
        for i, ap in enumerate(K.rw_prm):
            P.dma("sp", prm[:, i, :], ap, writes=[("prm", i)])
        prk = [("prm", i) for i in range(10)]
        P.op("dve", lambda e: e.tensor_scalar(out=prm[:, 10, :], in0=prm[:, 6, :], scalar1=-1.0, scalar2=1.0,
                                               op0=ALU.mult, op1=ALU.add), reads=prk, writes=[("prm", 10)])
        prk = prk + [("prm", 10)]
        P.dma("sp", mul[:], K.rw_mul, writes=["mul"])
        P.op("pool", lambda e: e.memset(bones[:], 0.0), writes=["bones"])
        P.op("pool", lambda e: e.memset(bones[0:64, 0:64], 1.0), reads=["bones"], writes=["bones"])
        P.op("pool", lambda e: e.memset(bones[64:128, 64:128], 1.0), reads=["bones"], writes=["bones"])
        P.op("pool", lambda e: e.iota(PT[:].rearrange("p (c t) -> p c t", t=64), pattern=[[0, NCH], [1, 64]], base=0,
                                      channel_multiplier=0, allow_small_or_imprecise_dtypes=True), writes=["PT"])
        P.op("dve", lambda e: e.tensor_scalar(out=M01[:], in0=PT[:], scalar1=0.5, scalar2=None, op0=ALU.is_gt),
             reads=["PT"], writes=["M01"])
        P.dma("sp", wst[0:96, 0, :], K.rw_w_up, writes=["wst"])
        P.op("act", lambda e: e.copy(out=wup[:], in_=wst[0:96, 0, :]), reads=["wst"], writes=["wup"])
        P.dma("sp", wst[0:96, 1, :], K.rw_a_up, reads=[], writes=["wst1"])
        P.op("act", lambda e: e.copy(out=aup[:], in_=wst[0:96, 1, :]), reads=["wst1"], writes=["aup"])
        P.dma("sp", wst[:, :, :], K.rw_g_up.rearrange("(c p) n -> p c n", p=128), reads=[], writes=["wst", "wst1"])
        P.op("act", lambda e: e.copy(out=gup[:], in_=wst[:]), reads=["wst", "wst1"], writes=["gup"])
        for (r0, n, mcol, func, dst, kd) in ((768, 96, 0, AF.Tanh, txw[:, :], "txw"), (864, 96, 1, AF.Copy, xap[:, :], "xap"),
                                             (960, 128, 2, AF.Sigmoid, sxg[:, 0, :], "sxg0"),
                                             (1088, 128, 3, AF.Sigmoid, sxg[:, 1, :], "sxg1")):
            P.dma("sp", PT[0:n, :], K.yT_d[r0:r0 + n, :], writes=["PT"])
            tok_shift(P, KP, PT, KK, mul[0:n, mcol:mcol + 1], "PT", "KK", "KP", n=n)
            P.op("act", lambda e, n=n, func=func, dst=dst: e.activation(out=dst, in_=KP[0:n, :], func=func),
                 reads=["KP"], writes=[kd])
        lk = ["txw", "xap", "sxg0", "sxg1"]
        oc = 0
        for ct in cts:
            c0 = ct * 128
            P.dma("sp", PT[:], K.yT_d[c0:c0 + 128, :], writes=["PT"])
            tok_shift(P, RP, PT, KK, prm[:, 0, ct:ct + 1], "PT", "KK", "RP")
            P.dma("sp", PT[:], K.yT_d[256 + c0:256 + c0 + 128, :], writes=["PT"])
            tok_shift(P, KP, PT, KK, prm[:, 1, ct:ct + 1], "PT", "KK", "KP")
            P.dma("sp", PT[:], K.yT_d[512 + c0:512 + c0 + 128, :], writes=["PT"])
            tok_shift(P, VP, PT, KK, prm[:, 2, ct:ct + 1], "PT", "KK", "VP")
            P.dma("sp", K.vb_d[c0:c0 + 128, :], VP[:], reads=["VP"], writes=[("vb_d", ct)])
            for blk in range(8):
                bs = slice(blk * 512, (blk + 1) * 512)
                p0, p1, p2 = ps[0], ps[1], ps[2]
                P.op("pe", lambda e, bs=bs, c0=c0: e.matmul(ps[0][:, :], lhsT=wup[:, c0:c0 + 128], rhs=txw[:, bs],
                                                             start=True, stop=True), reads=["wup", "txw"], writes=[("ps4", 0)])
                P.op("act", lambda e, bs=bs, ct=ct: e.activation(out=CL[:, bs], in_=ps[0][:, :], func=AF.Sigmoid,
                                                                  bias=prm[:, 3, ct:ct + 1]),
                     reads=[("ps4", 0)] + prk, writes=["CL"])
                P.op("pe", lambda e, bs=bs, c0=c0: e.matmul(ps[1][:, :], lhsT=aup[:, c0:c0 + 128], rhs=xap[:, bs],
                                                             start=True, stop=True), reads=["aup", "xap"], writes=[("ps4", 1)])
                P.op("act", lambda e, bs=bs, ct=ct: e.activation(out=AA[:, bs], in_=ps[1][:, :], func=AF.Sigmoid,
                                                                  bias=prm[:, 4, ct:ct + 1]),
                     reads=[("ps4", 1)] + prk, writes=["AA"])
                for cc in range(2):
                    P.op("pe", lambda e, bs=bs, c0=c0, cc=cc: e.matmul(ps[2][:, :], lhsT=gup[:, cc, c0:c0 + 128],
                                                                       rhs=sxg[:, cc, bs], start=(cc == 0), stop=(cc == 1)),
                         reads=["gup", "sxg0", "sxg1"], writes=[("ps4", 2)])
                o = OUT[oc % 2]
                P.op("dve", lambda e, bs=bs, o=o: e.tensor_copy(out=o[:, bs], in_=ps[2][:, :]),
                     reads=[("ps4", 2)], writes=[("OUT", oc % 2)])
            P.dma("sp", K.G_d[c0:c0 + 128, :], OUT[oc % 2][:], reads=[("OUT", oc % 2)], writes=[("G_d", ct)])
            oc += 1
            P.op("dve", lambda e: e.tensor_scalar(out=CL[:], in0=CL[:], scalar1=-0.6065306597126334, scalar2=None,
                                                   op0=ALU.mult), reads=["CL"], writes=["CL"])
            P.op("dve", lambda e, ct=ct: e.tensor_scalar(out=KK[:], in0=KP[:], scalar1=prm[:, 5, ct:ct + 1], scalar2=None,
                                                          op0=ALU.mult), reads=["KP"] + prk, writes=["KK"])
            P.op("act", lambda e: e.activation(out=SQb[:], in_=KK[:], func=AF.Square), reads=["KK"], writes=["SQb"])
            for blk in range(8):
                bs = slice(blk * 512, (blk + 1) * 512)
                P.op("pe", lambda e, bs=bs: e.matmul(ps[3][:, :], lhsT=bones[:], rhs=SQb[:, bs], start=True, stop=True),
                     reads=["bones", "SQb"], writes=[("ps4", 3)])
                P.op("act", lambda e, bs=bs: e.activation(out=PT[:, bs], in_=ps[3][:, :], func=AF.Sqrt),
                     reads=[("ps4", 3)], writes=["PT"])
            P.op("dve", lambda e: e.tensor_scalar(out=PT[:], in0=PT[:], scalar1=1e-12, scalar2=None, op0=ALU.max),
                 reads=["PT"], writes=["PT"])
            P.op("dve", lambda e: e.reciprocal(out=PT[:], in_=PT[:]), reads=["PT"], writes=["PT"])
            P.op("dve", lambda e: e.tensor_tensor(out=KK[:], in0=KK[:], in1=PT[:], op=ALU.mult), reads=["KK", "PT"], writes=["KK"])
            P.op("dve", lambda e, ct=ct: e.tensor_scalar(out=PT[:], in0=AA[:], scalar1=prm[:, 6, ct:ct + 1],
                                                          scalar2=prm[:, 10, ct:ct + 1], op0=ALU.mult, op1=ALU.add),
                 reads=["AA", "PT"] + prk, writes=["PT"])
            P.op("dve", lambda e: e.tensor_tensor(out=K2[:], in0=KP[:], in1=PT[:], op=ALU.mult), reads=["KP", "PT"], writes=["K2"])
            P.op("dve", lambda e, ct=ct: e.scalar_tensor_tensor(out=SQb[:], in0=RP[:], scalar=prm[:, 7, ct:ct + 1], in1=K2[:],
                                                                 op0=ALU.mult, op1=ALU.mult),
                 reads=["RP", "K2", "SQb"] + prk, writes=["SQb"])
            o = OUT[oc % 2]
            for blk in range(8):
                bs = slice(blk * 512, (blk + 1) * 512)
                P.op("pe", lambda e, bs=bs: e.matmul(ps[3][:, :], lhsT=bones[:], rhs=SQb[:, bs], start=True, stop=True),
                     reads=["bones", "SQb"], writes=[("ps4", 3)])
                P.op("dve", lambda e, bs=bs, o=o: e.tensor_tensor(out=o[:, bs], in0=ps[3][:, :], in1=VP[:, bs], op=ALU.mult),
                     reads=[("ps4", 3), "VP"], writes=[("OUT", oc % 2)])
            P.dma("sp", K.BON_d[c0:c0 + 128, :], o[:], reads=[("OUT", oc % 2)], writes=[("BON_d", ct)])
            oc += 1
            P.op("dve", lambda e: e.tensor_tensor_scan(out=PT[:], data0=M01[:], data1=CL[:], initial=0.0,
                                                        op0=ALU.mult, op1=ALU.add), reads=["M01", "CL", "PT"], writes=["PT"])
            P.op("pool", lambda e: e.tensor_tensor(out=CL[:], in0=PT[:], in1=CL[:], op=ALU.subtract),
                 reads=["PT", "CL"], writes=["CL"])
            P.op("act", lambda e: e.activation(out=CL[:], in_=CL[:], func=AF.Exp), reads=["CL"], writes=["CL"])
            v3 = lambda t: t[:].rearrange("p (c t) -> p c t", t=64)
            o = OUT[oc % 2]
            P.op("dve", lambda e, o=o: e.scalar_tensor_tensor(out=o[:], in0=KK[:], scalar=-1.0, in1=CL[:],
                                                               op0=ALU.mult, op1=ALU.mult),
                 reads=["KK", "CL"], writes=[("OUT", oc % 2)])
            P.dma("sp", K.AH_d[c0:c0 + 128, :], o[:], reads=[("OUT", oc % 2)], writes=[("AH_d", ct)])
            oc += 1
            P.op("act", lambda e: e.activation(out=CL[:], in_=PT[:], func=AF.Exp), reads=["PT", "CL"], writes=["CL"])
            o = OUT[oc % 2]
            P.op("dve", lambda e, o=o: e.tensor_tensor(out=o[:], in0=RP[:], in1=CL[:], op=ALU.mult),
                 reads=["RP", "CL"], writes=[("OUT", oc % 2)])
            P.dma("sp", K.RH_d[c0:c0 + 128, :], o[:], reads=[("OUT", oc % 2)], writes=[("RH_d", ct)])
            oc += 1
            P.op("pool", lambda e: e.tensor_copy(out=PCt[:], in_=v3(CL)[:, :, 63]), reads=["CL"], writes=["PCt"])
            P.dma("sp", K.PC_d[c0:c0 + 128, :], PCt[:], reads=["PCt"], writes=[("PC_d", ct)])
            P.op("act", lambda e: e.activation(out=PT[:], in_=PT[:], func=AF.Exp, scale=-1.0), reads=["PT"], writes=["PT"])
            o = OUT[oc % 2]
            P.op("dve", lambda e, o=o: e.tensor_tensor(out=o[:], in0=K2[:], in1=PT[:], op=ALU.mult),
                 reads=["K2", "PT"], writes=[("OUT", oc % 2)])
            P.dma("sp", K.KH_d[c0:c0 + 128, :], o[:], reads=[("OUT", oc % 2)], writes=[("KH_d", ct)])
            oc += 1
            P.op("dve", lambda e: e.tensor_tensor(out=KK[:], in0=KK[:], in1=AA[:], op=ALU.mult), reads=["KK", "AA"], writes=["KK"])
            o = OUT[oc % 2]
            P.op("dve", lambda e, o=o: e.tensor_tensor(out=o[:], in0=KK[:], in1=PT[:], op=ALU.mult),
                 reads=["KK", "PT"], writes=[("OUT", oc % 2)])
            P.dma("sp", K.BH_d[c0:c0 + 128, :], o[:], reads=[("OUT", oc % 2)], writes=[("BH_d", ct)])
            oc += 1
        P.flush()

def phase4c_rwkv_scan(K, heads=range(4)):
    nc, P = K.nc, K.P
    with contextlib.ExitStack() as st:
        def sb(name, shape, dt):
            return st.enter_context(nc.sbuf_tensor(name, shape, dt))
        ident = sb("ident4", [128, 128], BF16)
        make_ident(K, P, ident)
        MaskG = sb("MaskG", [64, 4, 128], F32)
        MaskX = sb("MaskX", [64, 8, 64], F32)
        I8 = sb("I8", [64, 64], F32)
        ones = sb("ones4", [64, 64], F32)
        P.op("pool", lambda e: e.memset(ones[:], 1.0), writes=["ones"])
        for a in range(4):
            for cq in range(2):
                P.op("pool", lambda e, cq=cq, a=a: e.affine_select(
                    out=MaskG[:, a, cq * 64:(cq + 1) * 64], in_=ones[:], pattern=[[1, 64]],
                    compare_op=(ALU.is_gt if cq == 0 else ALU.is_ge), fill=0.0, base=0, channel_multiplier=-1),
                    reads=["ones"], writes=["MaskG"])
        for a in range(8):
            P.op("pool", lambda e, a=a: e.affine_select(out=MaskX[:, a, :], in_=ones[:], pattern=[[-1, 64]],
                                                         compare_op=ALU.is_gt, fill=0.0, base=0, channel_multiplier=1),
                 reads=["ones"], writes=["MaskX"])
        P.op("dve", lambda e: e.tensor_copy(out=I8[:], in_=ident[0:64, 0:64]), reads=["ident"], writes=["I8"])
        AH = sb("AH", [64, S], RD)
        RH = sb("RH", [64, S], RD)
        BH = sb("BH", [64, S], RD)
        KH = sb("KH", [64, S], RD)
        vb = sb("vb", [64, S], BF16)
        PC = sb("PC", [64, NCH], F32)
        ARh = sb("ARh", [64, NCH, 128], RD)
        BKh = sb("BKh", [64, NCH, 128], RD)
        GmB = sb("GmB", [64, NCH, 128], RD)
        GmK = sb("GmK", [64, NCH, 128], RD)
        Btok = sb("Btok", [64, NCH, 64], RD)
        Ktok = sb("Ktok", [64, NCH, 64], RD)
        Vtok = sb("Vtok", [64, NCH, 64], RD)
        X0 = sb("X0", [64, NCH, 64], RD)
        Pm = sb("Pm", [64, NCH, 64], RD)
        oT = sb("oT", [64, S], F32)
        Ast = sb("Ast", [64, 64], F32)
        Abf = sb("Abf", [64, 64], RD)
        Tt = sb("Tt", [64, 64], F32)
        Xs = sb("Xs", [64, 64], RD)
        Us = sb("Us", [64, 64], RD)
        PSb = st.enter_context(nc.psum_tensor("PSb", [128, 1024], BF16))
        PS = [st.enter_context(nc.psum_tensor("PS%d" % i, [128, 512], F32)) for i in range(7)]
        v3 = lambda t: t[:].rearrange("p (c t) -> p c t", t=64)
        Nb = [v3(AH), v3(RH)]
        Xb = [v3(BH), v3(KH)]
        Nk = ["AH", "RH"]
        Xk = ["BH", "KH"]
        for hd in heads:
            r0 = hd * 64
            P.dma("sp", AH[:], K.AH_d[r0:r0 + 64, :], writes=["AH"])
            P.dma("sp", RH[:], K.RH_d[r0:r0 + 64, :], writes=["RH"])
            P.dma("sp", BH[:], K.BH_d[r0:r0 + 64, :], writes=["BH"])
            P.dma("sp", KH[:], K.KH_d[r0:r0 + 64, :], writes=["KH"])
            P.dma("sp", vb[:], K.vb_d[r0:r0 + 64, :], writes=["vb"])
            P.dma("sp", PC[:], K.PC_d[r0:r0 + 64, :], writes=["PC"])
            P.op("dve", lambda e: e.tensor_copy(out=ARh[:, :, 0:64], in_=v3(AH)), reads=["AH"], writes=["ARh"])
            P.op("pool", lambda e: e.tensor_copy(out=ARh[:, :, 64:128], in_=v3(RH)), reads=["RH"], writes=["ARh"])
            P.op("dve", lambda e: e.tensor_copy(out=BKh[:, :, 0:64], in_=v3(BH)), reads=["BH"], writes=["BKh"])
            P.op("pool", lambda e: e.tensor_copy(out=BKh[:, :, 64:128], in_=v3(KH)), reads=["KH"], writes=["BKh"])
            for (src, srck, col0, dst, dk) in ((BKh, "BKh", 0, Btok, "Btok"), (BKh, "BKh", 64, Ktok, "Ktok"), (None, "vb", 0, Vtok, "Vtok")):
                for c16 in range(0, NCH, 16):
                    for cc in range(16):
                        c = c16 + cc
                        in_ = vb[:, c * 64:(c + 1) * 64] if src is None else src[:, c, col0:col0 + 64]
                        P.op("pe", lambda e, cc=cc, in_=in_: e.transpose(out=PSb[0:64, cc * 64:(cc + 1) * 64], in_=in_,
                                                                         identity=ident[0:64, 0:64]),
                             reads=[srck, "ident"], writes=["PSb"])
                    P.op("act", lambda e, c16=c16, dst=dst: e.copy(out=dst[:, c16:c16 + 16, :].rearrange("p c k -> p (c k)"),
                                                                    in_=PSb[0:64, :]), reads=["PSb"], writes=[dk])
            gi = 0
            for (col0, dst, dk) in ((0, GmB, "GmB"), (64, GmK, "GmK")):
                for c4 in range(0, NCH, 4):
                    b = gi % 2
                    gi += 1
                    for cc in range(4):
                        c = c4 + cc
                        P.op("pe", lambda e, c=c, cc=cc, b=b, col0=col0: e.matmul(
                            PS[b][0:64, cc * 128:(cc + 1) * 128], lhsT=BKh[:, c, col0:col0 + 64], rhs=ARh[:, c, :],
                            start=True, stop=True), reads=["BKh", "ARh"], writes=[("PS", b)])
                    P.op("dve", lambda e, c4=c4, dst=dst, b=b: e.tensor_tensor(
                        out=dst[:, c4:c4 + 4, :], in0=PS[b][0:64, :].rearrange("p (a t) -> p a t", t=128), in1=MaskG[:],
                        op=ALU.mult), reads=[("PS", b), "MaskG"], writes=[dk])
            for c8 in range(0, NCH, 8):
                for cc in range(8):
                    c = c8 + cc
                    P.op("pe", lambda e, c=c, cc=cc: e.matmul(PS[2][0:64, cc * 64:(cc + 1) * 64], lhsT=ARh[:, c, 0:64],
                                                               rhs=BKh[:, c, 0:64], start=True, stop=True),
                         reads=["ARh", "BKh"], writes=[("PS", 2)])
                P.op("dve", lambda e, c8=c8: e.tensor_tensor(
                    out=X0[:, c8:c8 + 8, :], in0=PS[2][0:64, :].rearrange("p (a t) -> p a t", t=64), in1=MaskX[:],
                    op=ALU.mult), reads=[("PS", 2), "MaskX"], writes=["X0"])
            N0 = GmB[:, :, 0:64]
            P.op("pool", lambda e, N0=N0: e.tensor_tensor(out=Pm[:], in0=N0, in1=I8[:].unsqueeze(1).to_broadcast([64, NCH, 64]),
                                                          op=ALU.add), reads=["GmB", "I8"], writes=["Pm"])
            curN, curNk = N0, "GmB"
            curX, curXk = X0[:], "X0"
            for lvl in range(1, 6):
                nX, nXk = Xb[lvl % 2], Xk[lvl % 2]
                nN, nNk = Nb[lvl % 2], Nk[lvl % 2]
                for c8 in range(0, NCH, 8):
                    for cc in range(8):
                        c = c8 + cc
                        P.op("pe", lambda e, c=c, cc=cc, curN=curN, curX=curX: e.matmul(
                            PS[3][0:64, cc * 64:(cc + 1) * 64], lhsT=curN[:, c, :], rhs=curX[:, c, :], start=True, stop=True),
                            reads=[curNk, curXk], writes=[("PS", 3)])
                    P.op("act", lambda e, c8=c8, nX=nX: e.copy(out=nX[:, c8:c8 + 8, :],
                                                                in_=PS[3][0:64, :].rearrange("p (a t) -> p a t", t=64)),
                         reads=[("PS", 3)], writes=[nXk])
                    if lvl < 5:
                        for cc in range(8):
                            c = c8 + cc
                            P.op("pe", lambda e, c=c, cc=cc, curN=curN, curX=curX: e.matmul(
                                PS[4][0:64, cc * 64:(cc + 1) * 64], lhsT=curX[:, c, :], rhs=curN[:, c, :], start=True, stop=True),
                                reads=[curNk, curXk], writes=[("PS", 4)])
                        P.op("dve", lambda e, c8=c8, nN=nN: e.tensor_copy(out=nN[:, c8:c8 + 8, :],
                                                                           in_=PS[4][0:64, :].rearrange("p (a t) -> p a t", t=64)),
                             reads=[("PS", 4)], writes=[nNk])
                    for cc in range(8):
                        c = c8 + cc
                        P.op("pe", lambda e, c=c, cc=cc, nX=nX: e.matmul(
                            PS[5][0:64, cc * 64:(cc + 1) * 64], lhsT=nX[:, c, :], rhs=Pm[:, c, :], start=True, stop=True),
                            reads=[nXk, "Pm"], writes=[("PS", 5)])
                    P.op("dve", lambda e, c8=c8: e.tensor_tensor(
                        out=Pm[:, c8:c8 + 8, :], in0=PS[5][0:64, :].rearrange("p (a t) -> p a t", t=64),
                        in1=Pm[:, c8:c8 + 8, :], op=ALU.add), reads=[("PS", 5), "Pm"], writes=["Pm"])
                curN, curNk, curX, curXk = nN, nNk, nX, nXk
            P.op("pool", lambda e: e.memset(Ast[:], 0.0), writes=["Ast"])
            P.op("pool", lambda e: e.memset(Abf[:], 0.0), writes=["Abf"])
            for c in range(NCH):
                P.op("pool", lambda e, c=c: e.tensor_scalar(out=Tt[:], in0=Ast[:], scalar1=PC[:, c:c + 1], scalar2=0.0,
                                                             op0=ALU.mult, op1=ALU.add), reads=["Ast", "PC"], writes=["Tt"])
                P.op("pe", lambda e, c=c: e.matmul(PS[0][0:64, 0:64], lhsT=ARh[:, c, 0:64], rhs=Abf[:], start=True, stop=False),
                     reads=["ARh", "Abf"], writes=[("PS", 0)])
                P.op("pe", lambda e, c=c: e.matmul(PS[0][0:64, 0:64], lhsT=GmK[:, c, 0:64], rhs=Vtok[:, c, :], start=False, stop=True),
                     reads=["GmK", "Vtok"], writes=[("PS", 0)])
                P.op("act", lambda e: e.copy(out=Xs[:], in_=PS[0][0:64, 0:64]), reads=[("PS", 0)], writes=["Xs"])
                P.op("pe", lambda e, c=c: e.matmul(PS[1][0:64, 0:64], lhsT=Pm[:, c, :], rhs=Xs[:], start=True, stop=True),
                     reads=["Pm", "Xs"], writes=[("PS", 1)])
                P.op("dve", lambda e: e.tensor_copy(out=Us[:], in_=PS[1][0:64, 0:64]), reads=[("PS", 1)], writes=["Us"])
                P.op("pe", lambda e, c=c: e.matmul(PS[6][0:64, 0:64], lhsT=Btok[:, c, :], rhs=Us[:], start=True, stop=False),
                     reads=["Btok", "Us"], writes=[("PS", 6)])
                P.op("pe", lambda e, c=c: e.matmul(PS[6][0:64, 0:64], lhsT=Ktok[:, c, :], rhs=Vtok[:, c, :], start=False, stop=True),
                     reads=["Ktok", "Vtok"], writes=[("PS", 6)])
                ob = 2 + (c % 2)
                P.op("pe", lambda e, c=c, ob=ob: e.matmul(PS[ob][0:64, 0:64], lhsT=Abf[:], rhs=ARh[:, c, 64:128], start=True, stop=False),
                     reads=["Abf", "ARh"], writes=[("PS", ob)])
                P.op("pe", lambda e, c=c, ob=ob: e.matmul(PS[ob][0:64, 0:64], lhsT=Us[:], rhs=GmB[:, c, 64:128], start=False, stop=False),
                     reads=["Us", "GmB"], writes=[("PS", ob)])
                P.op("pe", lambda e, c=c, ob=ob: e.matmul(PS[ob][0:64, 0:64], lhsT=Vtok[:, c, :], rhs=GmK[:, c, 64:128], start=False, stop=True),
                     reads=["Vtok", "GmK"], writes=[("PS", ob)])
                P.op("dve", lambda e, c=c: e.scalar_tensor_tensor(out=Abf[:], in0=PS[6][0:64, 0:64], scalar=PC[:, c:c + 1], in1=Tt[:],
                                                                   op0=ALU.mult, op1=ALU.add),
                     reads=[("PS", 6), "Tt", "PC"], writes=["Abf"])
                P.op("dve", lambda e, c=c: e.scalar_tensor_tensor(out=Ast[:], in0=PS[6][0:64, 0:64], scalar=PC[:, c:c + 1], in1=Tt[:],
                                                                   op0=ALU.mult, op1=ALU.add),
                     reads=[("PS", 6), "Tt", "PC"], writes=["Ast"])
                P.op("act", lambda e, c=c, ob=ob: e.copy(out=oT[:, c * 64:(c + 1) * 64], in_=PS[ob][0:64, 0:64]),
                     reads=[("PS", ob)], writes=[("oT", c // 8)])
            P.dma("sp", K.oT_d[r0:r0 + 64, :], oT[:], reads=[("oT", q) for q in range(8)], writes=[("oT_d", hd)])
        P.flush()

def phase4d_rwkv_post(K, cts=range(2)):
    nc, P = K.nc, K.P
    with contextlib.ExitStack() as st:
        def sb(name, shape, dt):
            return st.enter_context(nc.sbuf_tensor(name, shape, dt))
        ident = sb("ident4d", [128, 128], BF16)
        make_ident(K, P, ident)
        bonesf = sb("bonesf", [128, 128], F32)
        P.op("pool", lambda e: e.memset(bonesf[:], 0.0), writes=["bonesf"])
        P.op("pool", lambda e: e.memset(bonesf[0:64, 0:64], 1.0), reads=["bonesf"], writes=["bonesf"])
        P.op("pool", lambda e: e.memset(bonesf[64:128, 64:128], 1.0), reads=["bonesf"], writes=["bonesf"])
        prm = sb("prm4d", [128, 2, 2], F32)
        P.dma("sp", prm[:, 0, :], K.rw_prm[8], writes=["prm0"])
        P.dma("sp", prm[:, 1, :], K.rw_prm[9], writes=["prm1"])
        o = sb("o4d", [128, S], F32)
        osq = sb("osq", [128, S], F32)
        bon = sb("bon", [128, S], BF16)
        gg = sb("gg", [128, S], BF16)
        Mb = [sb("Mb%d" % i, [128, 512], F32) for i in range(2)]
        Vb = [sb("Vb%d" % i, [128, 512], F32) for i in range(2)]
        Yb = [sb("Yb%d" % i, [128, 512], F32) for i in range(2)]
        Ob = [sb("Ob%d" % i, [128, 512], BF16) for i in range(2)]
        Tk = [sb("Tk%d" % i, [128, 4, 128], BF16) for i in range(2)]
        ps = [st.enter_context(nc.psum_tensor("p4d_%d" % i, [128, 512], F32)) for i in range(4)]
        pst = [st.enter_context(nc.psum_tensor("p4dt_%d" % i, [128, 4, 128], BF16)) for i in range(2)]
        it = 0
        for ct in cts:
            c0 = ct * 128
            P.dma("sp", o[:], K.oT_d[c0:c0 + 128, :], writes=["o"])
            P.dma("sp", bon[:], K.BON_d[c0:c0 + 128, :], writes=["bon"])
            P.dma("sp", gg[:], K.G_d[c0:c0 + 128, :], writes=["gg"])
            P.op("act", lambda e: e.activation(out=osq[:], in_=o[:], func=AF.Square), reads=["o"], writes=["osq"])
            for blk in range(8):
                s2 = it % 2
                it += 1
                bs = slice(blk * 512, (blk + 1) * 512)
                P.op("pe", lambda e, bs=bs, s2=s2: e.matmul(ps[s2][:, :], lhsT=bonesf[:], rhs=o[:, bs], start=True, stop=True),
                     reads=["bonesf", "o"], writes=[("p4d", s2)])
                P.op("pe", lambda e, bs=bs, s2=s2: e.matmul(ps[2 + s2][:, :], lhsT=bonesf[:], rhs=osq[:, bs], start=True, stop=True),
                     reads=["bonesf", "osq"], writes=[("p4d", 2 + s2)])
                P.op("act", lambda e, s2=s2: e.activation(out=Mb[s2][:], in_=ps[s2][:, :], func=AF.Copy, scale=1.0 / 64),
                     reads=[("p4d", s2)], writes=[("Mb", s2)])
                P.op("pool", lambda e, s2=s2: e.tensor_tensor(out=Vb[s2][:], in0=Mb[s2][:], in1=Mb[s2][:], op=ALU.mult),
                     reads=[("Mb", s2)], writes=[("Vb", s2)])
                P.op("dve", lambda e, s2=s2: e.scalar_tensor_tensor(out=Vb[s2][:], in0=ps[2 + s2][:, :], scalar=1.0 / 64, in1=Vb[s2][:],
                                                                     op0=ALU.mult, op1=ALU.subtract),
                     reads=[("p4d", 2 + s2), ("Vb", s2)], writes=[("Vb", s2)])
                P.op("dve", lambda e, s2=s2: e.tensor_scalar(out=Vb[s2][:], in0=Vb[s2][:], scalar1=64e-5, scalar2=None, op0=ALU.add),
                     reads=[("Vb", s2)], writes=[("Vb", s2)])
                P.op("act", lambda e, s2=s2: e.activation(out=Vb[s2][:], in_=Vb[s2][:], func=AF.Sqrt),
                     reads=[("Vb", s2)], writes=[("Vb", s2)])
                P.op("dve", lambda e, s2=s2: e.reciprocal(out=Vb[s2][:], in_=Vb[s2][:]), reads=[("Vb", s2)], writes=[("Vb", s2)])
                P.op("pool", lambda e, s2=s2, bs=bs: e.tensor_tensor(out=Yb[s2][:], in0=o[:, bs], in1=Mb[s2][:], op=ALU.subtract),
                     reads=["o", ("Mb", s2)], writes=[("Yb", s2)])
                P.op("dve", lambda e, s2=s2: e.tensor_tensor(out=Yb[s2][:], in0=Yb[s2][:], in1=Vb[s2][:], op=ALU.mult),
                     reads=[("Yb", s2), ("Vb", s2)], writes=[("Yb", s2)])
                P.op("dve", lambda e, s2=s2, ct=ct: e.tensor_scalar(out=Yb[s2][:], in0=Yb[s2][:], scalar1=prm[:, 0, ct:ct + 1],
                                                                     scalar2=prm[:, 1, ct:ct + 1], op0=ALU.mult, op1=ALU.add),
                     reads=[("Yb", s2), "prm0", "prm1"], writes=[("Yb", s2)])
                P.op("pool", lambda e, s2=s2, bs=bs: e.tensor_tensor(out=Yb[s2][:], in0=Yb[s2][:], in1=bon[:, bs], op=ALU.add),
                     reads=[("Yb", s2), "bon"], writes=[("Yb", s2)])
                P.op("dve", lambda e, s2=s2, bs=bs: e.tensor_tensor(out=Ob[s2][:], in0=Yb[s2][:], in1=gg[:, bs], op=ALU.mult),
                     reads=[("Yb", s2), "gg"], writes=[("Ob", s2)])
                for q in range(4):
                    P.op("pe", lambda e, s2=s2, q=q: e.transpose(out=pst[s2][:, q, :], in_=Ob[s2][:, q * 128:(q + 1) * 128],
                                                                 identity=ident[:]),
                         reads=[("Ob", s2), "ident"], writes=[("p4dt", s2)])
                P.op("act", lambda e, s2=s2: e.copy(out=Tk[s2][:], in_=pst[s2][:]), reads=[("p4dt", s2)], writes=[("Tk", s2)])
                P.dma("sp", K.ro_loc_d[blk // 4].rearrange("(t p) c -> p t c", p=128)[:, (blk % 4) * 4:(blk % 4 + 1) * 4, c0:c0 + 128], Tk[s2][:],
                      reads=[("Tk", s2)], writes=[("ro_tok_d", ct, blk)])
        P.flush()


def phase4e_allgather(K):
    P = K.P
    for hh in range(2):
        P.coll(lambda e, hh=hh: e.collective_compute("AllGather", ALU.bypass, replica_groups=[[0, 1, 2, 3], [4, 5, 6, 7]],
                                                     ins=[K.ro_loc_d[hh].opt()], outs=[K.ro_all_d[hh].opt()]),
               reads=[("ro_loc", hh)], writes=[("ro_all", hh)])
    P.flush()


def phase5a_select(K):
    nc, P = K.nc, K.P
    with contextlib.ExitStack() as st:
        def sb(name, shape, dt):
            return st.enter_context(nc.sbuf_tensor(name, shape, dt))
        ro = sb("ro_tok", [128, 32, 1024], BF16)
        selT = sb("selT", [128, 32, 1024], BF16)
        qrow = sb("qrow", [128, 1024], F32)
        tki = sb("tki", [128, 32], I32)
        tkf = sb("tkf", [128, 32], F32)
        mo = [sb("mo%d" % i, [128, 512], BF16) for i in range(2)]
        at = sb("at5", [128, 8, 1024], BF16)
        ps = [st.enter_context(nc.psum_tensor("p5a_%d" % i, [128, 512], F32)) for i in range(2)]
        for q4 in range(4):
            for hh in range(2):
                P.dma("sp", ro[:, hh * 16:(hh + 1) * 16, q4 * 256:(q4 + 1) * 256],
                      K.ro_all_d[hh][q4 * 2048:(q4 + 1) * 2048, :].rearrange("(t p) c -> p t c", p=128), writes=[("ro", q4, hh)])
        rok = [("ro", q4, hh) for q4 in range(4) for hh in range(2)]
        P.dma("sp", qrow[:], bcast_rows(K.qpos_row, 1024), writes=["qrow"])
        P.op("pool", lambda e: e.iota(tki[:], pattern=[[128, 32]], base=0, channel_multiplier=1), writes=["tki"])
        P.op("dve", lambda e: e.tensor_copy(out=tkf[:], in_=tki[:]), reads=["tki"], writes=["tkf"])
        for T in range(32):
            P.op("dve", lambda e, T=T: e.tensor_scalar(out=selT[:, T, :], in0=qrow[:], scalar1=tkf[:, T:T + 1], scalar2=0.0,
                                                      op0=ALU.is_equal, op1=ALU.add), reads=["qrow", "tkf"], writes=[("selT", T)])
        sk = [("selT", T) for T in range(32)]
        P.dma("sp", at[:], K.attT_d.rearrange("h p t -> p h t"), writes=["at5"])
        P.dma("sp", K.mixT_d.rearrange("k p t -> p k t")[:, 0:8, :], at[:], reads=["at5"], writes=["mixa"])
        i = 0
        for m in range(8):
            for half in range(2):
                s2 = i % 2
                i += 1
                for T in range(32):
                    P.op("pe", lambda e, T=T, m=m, half=half, s2=s2: e.matmul(
                        ps[s2][:, :], lhsT=ro[:, T, m * 128:(m + 1) * 128], rhs=selT[:, T, half * 512:(half + 1) * 512],
                        start=(T == 0), stop=(T == 31)), reads=rok + sk, writes=[("p5a", s2)])
                P.op("act", lambda e, s2=s2: e.copy(out=mo[s2][:], in_=ps[s2][:, :]), reads=[("p5a", s2)], writes=[("mo", s2)])
                P.dma("sp", K.mixT_d[8 + m, :, half * 512:(half + 1) * 512], mo[s2][:], reads=[("mo", s2)], writes=[("mixr", m, half)])
        P.flush()


def phase5b_outproj(K):
    nc, P = K.nc, K.P
    with contextlib.ExitStack() as st:
        def sb(name, shape, dt):
            return st.enter_context(nc.sbuf_tensor(name, shape, dt))
        ident = sb("ident5", [128, 128], BF16)
        make_ident(K, P, ident)
        G2, SH2 = load_G_SH(K, P, st, 3, 4, K.norm2_g, "p5")
        GT1 = sb("GT1", [128, D], F32)
        P.dma("sp", GT1[:], bcast_rows(K.mod_d[2 * D:3 * D], D), writes=["GT1"])
        Wo = sb("Wo", [128, 16, D], BF16)
        stg = [sb("wstg5_%d" % i, [128, 4, 512], F32) for i in range(2)]
        wk = load_weight_bf16(K, P, stg, Wo, 0, K.w_out, D, "Wo")
        mixT = sb("mixT", [128, 16, 512], BF16)
        T = norm_tiles_alloc(K, st, "p5")
        x1 = T["xt"]
        hT = [sb("hT5_0", [128, 16, 512], BF16)] * 2
        xo = [sb("xo%d" % i, [128, D], F32) for i in range(2)]
        ps = [st.enter_context(nc.psum_tensor("p5b_%d" % i, [128, 512], F32)) for i in range(2)]
        ss, junk, hb, pT = T["ss"], T["junk"], T["hb"], T["pT"]
        gi = 0
        for blk in range(2):
            hs = 0
            P.dma("sp", mixT[:], K.mixT_d.rearrange("k p t -> p k t")[:, :, blk * 512:(blk + 1) * 512], writes=["mixT"])
            for ti in range(4):
                t = blk * 4 + ti
                xs = t % 2
                P.dma("sp", xo[xs][:], K.x_own[t * 128:(t + 1) * 128, :], writes=[("xo", xs)])
                for cg in range(4):
                    b = gi % 2
                    gi += 1
                    for k in range(16):
                        P.op("pe", lambda e, b=b, k=k, t=t, cg=cg: e.matmul(
                            ps[b][:, :], lhsT=mixT[:, k, (t % 4) * 128:(t % 4 + 1) * 128], rhs=Wo[:, k, cg * 512:(cg + 1) * 512],
                            start=(k == 0), stop=(k == 15)), reads=["mixT"] + wk, writes=[("p5b", b)])
                    cs = slice(cg * 512, (cg + 1) * 512)
                    P.op("dve", lambda e, b=b, xs=xs, cs=cs: e.tensor_tensor(out=x1[xs][:, cs], in0=ps[b][:, :], in1=GT1[:, cs], op=ALU.mult),
                         reads=[("p5b", b), "GT1"], writes=[("xt", xs)])
                    P.op("pool", lambda e, xs=xs, cs=cs: e.tensor_tensor(out=x1[xs][:, cs], in0=x1[xs][:, cs], in1=xo[xs][:, cs], op=ALU.add),
                         reads=[("xt", xs), ("xo", xs)], writes=[("xt", xs)])
                P.dma("sp", K.x1_d[t * 128:(t + 1) * 128, :], x1[xs][:], reads=[("xt", xs)], writes=[("x1_d", t)])
                P.op("act", lambda e, xs=xs: e.activation(out=junk[:], in_=x1[xs][:], func=AF.Square, accum_out=ss[:, 0:1]),
                     reads=[("xt", xs)], writes=["junk", "ss0"])
                P.op("dve", lambda e: e.tensor_scalar(out=ss[:, 1:2], in0=ss[:, 0:1], scalar1=1.0 / D, scalar2=1e-6,
                                                       op0=ALU.mult, op1=ALU.add), reads=["ss0"], writes=["ss1"])
                P.op("act", lambda e: e.activation(out=ss[:, 2:3], in_=ss[:, 1:2], func=AF.Sqrt), reads=["ss1"], writes=["ss2"])
                P.op("dve", lambda e: e.reciprocal(out=ss[:, 3:4], in_=ss[:, 2:3]), reads=["ss2"], writes=["ss3"])
                P.op("dve", lambda e, xs=xs: e.scalar_tensor_tensor(out=x1[xs][:], in0=x1[xs][:], scalar=ss[:, 3:4], in1=G2[:],
                                                                   op0=ALU.mult, op1=ALU.mult),
                     reads=[("xt", xs), "ss3", "G"], writes=[("xt", xs)])
                P.op("pool", lambda e, xs=xs: e.tensor_tensor(out=hb[xs][:], in0=x1[xs][:], in1=SH2[:], op=ALU.add),
                     reads=[("xt", xs), "SH"], writes=[("hb", xs)])
                for half in range(2):
                    for kk in range(8):
                        k = half * 8 + kk
                        P.op("pe", lambda e, k=k, kk=kk, half=half, xs=xs: e.transpose(
                            out=pT[half][:, kk, :], in_=hb[xs][:, k * 128:(k + 1) * 128], identity=ident[:]),
                            reads=[("hb", xs), "ident"], writes=[("pT", half)])
                    o_ = hT[hs][:, half * 8:(half + 1) * 8, ti * 128:(ti + 1) * 128]
                    if half == 0:
                        P.op("act", lambda e, o_=o_, half=half: e.copy(out=o_, in_=pT[half][:]), reads=[("pT", half)], writes=[("hT5", hs, ti, half)])
                    else:
                        P.op("dve", lambda e, o_=o_, half=half: e.tensor_copy(out=o_, in_=pT[half][:]), reads=[("pT", half)], writes=[("hT5", hs, ti, half)])
            P.dma("sp", K.h2T_d.rearrange("k p t -> p k t")[:, :, blk * 512:(blk + 1) * 512], hT[hs][:],
                  reads=[("hT5", hs, ti, half) for ti in range(4) for half in range(2)], writes=[("h2T_d", blk)])
        P.flush()


def phase5c_ffn(K):
    nc, P = K.nc, K.P
    NF = 5632 // 128
    with contextlib.ExitStack() as st:
        def sb(name, shape, dt):
            return st.enter_context(nc.sbuf_tensor(name, shape, dt))
        h2T = sb("h2T", [128, 16, OWN], BF16)
        P.dma("sp", h2T[:], K.h2T_d.rearrange("k p t -> p k t"), writes=["h2T"])
        ao = [sb("ao%d" % i, [128, 512], BF16) for i in range(2)]
        stg = [sb("wstg6_%d" % i, [128, 4, 512], F32) for i in range(4)]
        Wg = [sb("Wg%d" % i, [128, 16, 512], BF16) for i in range(2)]
        Wu = [sb("Wu%d" % i, [128, 16, 512], BF16) for i in range(2)]
        sg = [sb("sg%d" % i, [128, 512], F32) for i in range(2)]
        ps = [st.enter_context(nc.psum_tensor("p5c_%d" % i, [128, 512], F32)) for i in range(4)]
        gi = 0

        def load_group(fg, defer=None):
            ws = fg % 2
            load_weight_bf16(K, P, stg, Wg[ws], 0, K.w_ffn_gate[:, fg * 512:(fg + 1) * 512], 512, ("Wg", ws), defer=defer)
            load_weight_bf16(K, P, stg, Wu[ws], 0, K.w_ffn_up[:, fg * 512:(fg + 1) * 512], 512, ("Wu", ws), defer=defer)
        load_group(0)
        for fg in range(11):
            ws = fg % 2
            pend = []
            if fg + 1 < 11:
                load_group(fg + 1, defer=pend)
            for f4 in range(4):
                f = fg * 4 + f4
                for tb in range(2):
                    b = gi % 2
                    gi += 1
                    if pend:
                        pend.pop(0)()
                    for k in range(16):
                        P.op("pe", lambda e, b=b, k=k, f4=f4, tb=tb, ws=ws: e.matmul(
                            ps[b][:, :], lhsT=Wg[ws][:, k, f4 * 128:(f4 + 1) * 128], rhs=h2T[:, k, tb * 512:(tb + 1) * 512],
                            start=(k == 0), stop=(k == 15)), reads=["h2T", (("Wg", ws), 0, (k // 4) * 4)], writes=[("p5c", b)])
                    for k in range(16):
                        P.op("pe", lambda e, b=b, k=k, f4=f4, tb=tb, ws=ws: e.matmul(
                            ps[2 + b][:, :], lhsT=Wu[ws][:, k, f4 * 128:(f4 + 1) * 128], rhs=h2T[:, k, tb * 512:(tb + 1) * 512],
                            start=(k == 0), stop=(k == 15)), reads=["h2T", (("Wu", ws), 0, (k // 4) * 4)], writes=[("p5c", 2 + b)])
                    P.op("act", lambda e, b=b: e.activation(out=sg[b][:], in_=ps[b][:, :], func=AF.Silu),
                         reads=[("p5c", b)], writes=[("sg", b)])
                    P.op("dve", lambda e, b=b: e.tensor_tensor(out=ao[b][:], in0=ps[2 + b][:, :], in1=sg[b][:], op=ALU.mult),
                         reads=[("p5c", 2 + b), ("sg", b)], writes=[("ao", b)])
                    P.dma("sp", K.actT_d[f, :, tb * 512:(tb + 1) * 512], ao[b][:], reads=[("ao", b)], writes=[("actT_d", f, tb)])
        P.flush()
    with contextlib.ExitStack() as st:
        def sb(name, shape, dt):
            return st.enter_context(nc.sbuf_tensor(name, shape, dt))
        GT2 = sb("GT2", [128, D], F32)
        P.dma("sp", GT2[:], bcast_rows(K.mod_d[5 * D:6 * D], D), writes=["GT2"])
        actT = sb("actT", [128, NF, OWN], BF16)
        for q in range(4):
            P.dma("sp", actT[:, q * 11:(q + 1) * 11, :], K.actT_d.rearrange("f p t -> p f t")[:, q * 11:(q + 1) * 11, :], writes=[("actT", q)])
        ak = [("actT", q) for q in range(4)]
        stg = [sb("wstg7_%d" % i, [128, 4, 256], F32) for i in range(4)]
        ps = [st.enter_context(nc.psum_tensor("p5d_%d" % i, [128, 512], F32)) for i in range(2)]
        gi = 0
        Wd = [sb("Wd%d" % i, [128, NF, 256], BF16) for i in range(2)]
        x1 = [sb("x1_%d" % i, [128, 256], F32) for i in range(2)]
        yo = [sb("yo%d" % i, [128, 256], F32) for i in range(2)]
        wdv = K.w_ffn_down.rearrange("(k p) n -> p k n", p=128)
        engs = ["pool", "dve", "act"]

        def load_wd(cg, defer=None):
            wsl = cg % 2
            for k0 in range(0, NF, 4):
                if defer is not None:
                    defer.append(lambda k0=k0: load_wd_piece(cg, wsl, k0))
                else:
                    load_wd_piece(cg, wsl, k0)

        def load_wd_piece(cg, wsl, k0):
            if True:
                i = K.wcnt
                K.wcnt += 1
                sl = i % 4
                P.dma("sp", stg[sl][:, 0:4, 0:256], wdv[:, k0:k0 + 4, cg * 256:(cg + 1) * 256], writes=[("wstg", sl)])
                eng = engs[i % 3]
                o_ = Wd[wsl][:, k0:k0 + 4, :]
                if eng == "act":
                    P.op("act", lambda e, o_=o_, sl=sl: e.copy(out=o_, in_=stg[sl][:, 0:4, 0:256]), reads=[("wstg", sl)], writes=[("Wd", wsl, k0)])
                else:
                    P.op(eng, lambda e, o_=o_, sl=sl: e.tensor_copy(out=o_, in_=stg[sl][:, 0:4, 0:256]), reads=[("wstg", sl)], writes=[("Wd", wsl, k0)])
        load_wd(0)
        for cg in range(8):
            wsl = cg % 2
            cs = slice(cg * 256, (cg + 1) * 256)
            pend = []
            if cg + 1 < 8:
                load_wd(cg + 1, defer=pend)
            for t in range(8):
                b = gi % 2
                gi += 1
                for _ in range(2):
                    if pend:
                        pend.pop(0)()
                P.dma("sp", x1[b][:], K.x1_d[t * 128:(t + 1) * 128, cs], writes=[("x1", b)])
                for f in range(NF):
                    P.op("pe", lambda e, b=b, f=f, t=t, wsl=wsl: e.matmul(ps[b][:, 0:256], lhsT=actT[:, f, t * 128:(t + 1) * 128], rhs=Wd[wsl][:, f, :],
                                                                          start=(f == 0), stop=(f == NF - 1)),
                         reads=[("actT", f // 11), ("Wd", wsl, (f // 4) * 4)], writes=[("p5c", b)])
                P.op("dve", lambda e, b=b, cs=cs: e.tensor_tensor(out=yo[b][:], in0=ps[b][:, 0:256], in1=GT2[:, cs], op=ALU.mult),
                     reads=[("p5c", b), "GT2"], writes=[("yo", b)])
                P.op("pool", lambda e, b=b: e.tensor_tensor(out=yo[b][:], in0=yo[b][:], in1=x1[b][:], op=ALU.add),
                     reads=[("yo", b), ("x1", b)], writes=[("yo", b)])
                P.dma("sp", K.out[t * 128:(t + 1) * 128, cs], yo[b][:], reads=[("yo", b)], writes=[("out", t, cg)])
        P.flush()


def phase_final_copy(K):
    nc, P = K.nc, K.P
    with contextlib.ExitStack() as st:
        xt = [st.enter_context(nc.sbuf_tensor("fx%d" % i, [128, D], F32)) for i in range(2)]
        for t in range(8):
            s = t % 2
            P.dma("sp", xt[s][:], K.x_own[t * 128:(t + 1) * 128, :], writes=[("fx", s)])
            P.dma("sp", K.out[t * 128:(t + 1) * 128, :], xt[s][:], reads=[("fx", s)], writes=[("out", t)])
        P.flush()


def own_tiles(j):
    r = []
    for m in range(4):
        r += [8 * m + j, 8 * m + 7 - j]
    return r


def build_program(debug=False, stages=99, cts=range(2), dbg_list=None, skip_att=False):
    nc = bass.Bass("TRN2", target_bir_lowering=False)
    K = Ctx()
    K.stages = stages
    K.cts = cts
    K.skip_att = skip_att
    K.nc = nc
    K.dbg = {}
    K.wcnt = 0

    def inp(name, shape, dt=F32):
        return nc.dram_tensor(name, list(shape), dt, kind="ExternalInput").ap()

    def scratch(name, shape, dt):
        return nc.dram_tensor(name, list(shape), dt, kind="Internal").ap()

    K.x_full = inp("x_full", [S, D])
    K.x_own = inp("x_own", [OWN, D])
    K.c_arr = inp("c_arr", [128, 16])
    K.pos_full = inp("pos_full", [128, 32], I32)
    K.invf_att = inp("invf_att", [128, 16])
    K.invf_idx = inp("invf_idx", [128, 8])
    K.w_ada = inp("w_ada", [D, 3072])
    K.b_ada = inp("b_ada", [3072])
    K.norm1_g = inp("norm1_g", [D])
    K.k_norm_g = inp("k_norm_g", [128])
    K.q_norm_g = inp("q_norm_g", [128])
    K.pos_own = inp("pos_own", [128, 8], I32)
    K.qpos_own = inp("qpos_own", [128, 8])
    K.w_in = inp("w_in", [D, 4176])
    K.rw_prm = [inp("rwp%d" % i, [128, 2]) for i in range(10)]
    K.w_in_rw = inp("w_in_rw", [D, 1216])
    K.rw_mul = inp("rw_mul", [128, 4])
    K.rw_w_up = inp("rw_w_up", [96, 256])
    K.rw_a_up = inp("rw_a_up", [96, 256])
    K.rw_g_up = inp("rw_g_up", [256, 256])
    K.qpos_row = inp("qpos_row", [OWN])
    K.w_out = inp("w_out", [D, D])
    K.norm2_g = inp("norm2_g", [D])
    K.w_ffn_gate = inp("w_ffn_gate", [D, 5632])
    K.w_ffn_up = inp("w_ffn_up", [D, 5632])
    K.w_ffn_down = inp("w_ffn_down", [5632, D])
    K.out = nc.dram_tensor("y_own", [OWN, D], F32, kind="ExternalOutput").ap()
    K.modq_d = scratch("modq_d", [1, 3072], F32)
    K.mod4_d = scratch("mod4_d", [4, 3072], F32)
    K.mod_d = K.mod4_d.rearrange("a n -> (a n)")
    K.hT_d = scratch("hT_d", [16, 128, S], BF16)
    K.kT_d = scratch("kT_d", [8, 128, S], BF16)
    K.v_d = scratch("v_d", [S, 8 * 129], BF16)
    K.ikT_d = scratch("ikT_d", [64, S], BF16)
    K.yT_d = scratch("yT_d", [1216, S], F32)
    K.qT_d = scratch("qT_d", [8, 128, OWN], BF16)
    K.iqT_d = scratch("iqT_d", [64, OWN, 16], BF16)
    K.iw_d = scratch("iw_d", [OWN, 16], F32)
    K.attT_d = scratch("attT_d", [8, 128, OWN], BF16)
    for nm in ("vb_d", "G_d", "BON_d", "AH_d", "RH_d", "BH_d", "KH_d"):
        setattr(K, nm, scratch(nm, [256, S], BF16))
    K.PC_d = scratch("PC_d", [256, NCH], F32)
    K.oT_d = scratch("oT_d", [256, S], F32)
    K.ro_loc_d = [scratch("ro_loc%d_d" % i, [2048, 256], BF16) for i in range(2)]
    K.ro_all_d = [scratch("ro_all%d_d" % i, [8192, 256], BF16) for i in range(2)]
    K.mixT_d = scratch("mixT_d", [16, 128, OWN], BF16)
    K.x1_d = scratch("x1_d", [OWN, D], F32)
    K.h2T_d = scratch("h2T_d", [16, 128, OWN], BF16)
    K.actT_d = scratch("actT_d", [44, 128, OWN], BF16)
    with contextlib.ExitStack() as stack:
        K.P = Prog(nc, stack)
        phase0_adaln(K)
        phase1_kv(K)
        if K.stages >= 2:
            phase1b_rwkv_proj(K)
        if K.stages >= 3 and not getattr(K, "skip_att", False):
            phase2_own_proj(K)
            phase3_attention(K)
        if K.stages >= 4:
            phase4b_rwkv_prep(K, cts=K.cts)
            if K.stages >= 5:
                phase4c_rwkv_scan(K, heads=[h for ct in K.cts for h in (2 * ct, 2 * ct + 1)])
        if K.stages >= 6:
            phase4d_rwkv_post(K, cts=K.cts)
            phase4e_allgather(K)
        if K.stages >= 7:
            phase5a_select(K)
            phase5b_outproj(K)
            phase5c_ffn(K)
        else:
            phase_final_copy(K)
        if debug:
            P = K.P
            allc = (("dbg_mixT", K.mixT_d, [16, 128, OWN], BF16), ("dbg_x1", K.x1_d, [OWN, D], F32),
                    ("dbg_oT", K.oT_d, [256, S], F32), ("dbg_AH", K.AH_d, [256, S], BF16), ("dbg_BH", K.BH_d, [256, S], BF16),
                    ("dbg_KH", K.KH_d, [256, S], BF16), ("dbg_RH", K.RH_d, [256, S], BF16), ("dbg_PC", K.PC_d, [256, NCH], F32),
                    ("dbg_G", K.G_d, [256, S], BF16), ("dbg_BON", K.BON_d, [256, S], BF16), ("dbg_vb", K.vb_d, [256, S], BF16),
                    ("dbg_yT", K.yT_d, [1216, S], F32), ("dbg_attT", K.attT_d, [8, 128, OWN], BF16),
                                     ("dbg_qT", K.qT_d, [8, 128, OWN], BF16), ("dbg_iqT", K.iqT_d, [64, OWN, 16], BF16),
                                     ("dbg_iw", K.iw_d, [OWN, 16], F32))
            for nm, src, shp, dt in allc:
                if dbg_list is not None and nm not in dbg_list:
                    continue
                o = dbg_out(K, nm, shp, dt)
                P.dma("sp", o, src, writes=[nm])
            P.flush()
    return nc, K


def make_in_maps(inputs, cores=range(8)):
    x = np.asarray(inputs["x"], dtype=np.float32)
    c = np.asarray(inputs["c"], dtype=np.float32)
    pos = np.asarray(inputs["positions"], dtype=np.int32)
    invf_att = (np.float32(500000.0) ** (-np.arange(16, dtype=np.float32) / np.float32(16))).astype(np.float32)
    invf_idx = (np.float32(500000.0) ** (-np.arange(8, dtype=np.float32) / np.float32(8))).astype(np.float32)
    mu = np.asarray(inputs["rwkv_mu"][0], dtype=np.float32)

    vecs = [mu[0:1024], mu[1024:2048], mu[2048:3072], inputs["rwkv_w0"][0], inputs["rwkv_a0"][0], inputs["rwkv_k_k"][0],
            inputs["rwkv_k_a"][0], np.asarray(inputs["rwkv_r_k"][0]).reshape(-1), inputs["rwkv_lnx_g"][0], inputs["rwkv_lnx_b"][0]]
    w_in_full = np.asarray(inputs["w_in"][0], dtype=np.float32)
    rw_mul = np.zeros((128, 4), np.float32)
    rw_mul[:96, 0] = mu[3072:3168]
    rw_mul[:96, 1] = mu[3168:3264]
    rw_mul[:, 2] = mu[3264:3392]
    rw_mul[:, 3] = mu[3392:3520]
    maps = []
    for core in cores:
        b, j = core // 4, core % 4
        ch = slice(256 * j, 256 * j + 256)
        rwp = {"rwp%d" % i: np.ascontiguousarray(np.asarray(v, dtype=np.float32)[ch].reshape(2, 128).T) for i, v in enumerate(vecs)}
        R0 = 4176
        w_in_rw = np.ascontiguousarray(np.concatenate([w_in_full[:, R0 + 256 * j:R0 + 256 * j + 256],
                                                       w_in_full[:, R0 + 1024 + 256 * j:R0 + 1024 + 256 * j + 256],
                                                       w_in_full[:, R0 + 2048 + 256 * j:R0 + 2048 + 256 * j + 256],
                                                       w_in_full[:, R0 + 3072:R0 + 3520]], axis=1))
        tiles = own_tiles(j)
        idx = np.concatenate([np.arange(t * 128, (t + 1) * 128) for t in tiles])
        maps.append({
            "x_full": np.ascontiguousarray(x[b]),
            "x_own": np.ascontiguousarray(x[b][idx]),
            "c_arr": np.ascontiguousarray(c[b].reshape(16, 128).T),
            "pos_full": np.ascontiguousarray(pos[b].reshape(32, 128).T),
            "invf_att": np.ascontiguousarray(np.broadcast_to(invf_att, (128, 16))),
            "invf_idx": np.ascontiguousarray(np.broadcast_to(invf_idx, (128, 8))),
            "w_ada": np.ascontiguousarray(np.asarray(inputs["w_ada"][0], dtype=np.float32)[:, 3072 * j:3072 * (j + 1)]),
            "b_ada": np.ascontiguousarray(np.asarray(inputs["b_ada"][0], dtype=np.float32)[3072 * j:3072 * (j + 1)]),
            "norm1_g": np.asarray(inputs["norm1_g"][0], dtype=np.float32),
            "k_norm_g": np.asarray(inputs["k_norm_g"][0], dtype=np.float32),
            "q_norm_g": np.asarray(inputs["q_norm_g"][0], dtype=np.float32),
            "pos_own": np.ascontiguousarray(pos[b][idx].reshape(8, 128).T),
            "qpos_own": np.ascontiguousarray(idx.astype(np.float32).reshape(8, 128).T),
            "w_in": np.ascontiguousarray(w_in_full[:, 0:4176]),
            "qpos_row": idx.astype(np.float32),
            "w_out": np.asarray(inputs["w_out"][0], dtype=np.float32),
            "norm2_g": np.asarray(inputs["norm2_g"][0], dtype=np.float32),
            "w_ffn_gate": np.asarray(inputs["w_ffn_gate"][0], dtype=np.float32),
            "w_ffn_up": np.asarray(inputs["w_ffn_up"][0], dtype=np.float32),
            "w_ffn_down": np.asarray(inputs["w_ffn_down"][0], dtype=np.float32),
            "rw_w_up": np.ascontiguousarray(np.asarray(inputs["rwkv_w_up"][0], dtype=np.float32)[:, ch]),
            "rw_a_up": np.ascontiguousarray(np.asarray(inputs["rwkv_a_up"][0], dtype=np.float32)[:, ch]),
            "rw_g_up": np.ascontiguousarray(np.asarray(inputs["rwkv_g_up"][0], dtype=np.float32)[:, ch]),
            "w_in_rw": w_in_rw,
            "rw_mul": rw_mul,
            **rwp,
        })
    return maps


def kernel(**inputs):
    nc, K = build_program(debug=False)
    maps = make_in_maps(inputs)
    res = run_bass_kernel_spmd(nc, maps, core_ids=list(range(8)))
    out = np.zeros((2, S, D), dtype=np.float32)
    for core in range(8):
        b, j = core // 4, core % 4
        y = res.results[core]["y_own"]
        for i, t in enumerate(own_tiles(j)):
            out[b, t * 128:(t + 1) * 128] = y[i * 128:(i + 1) * 128]
    return out
```

```python
import contextlib
import numpy as np
import concourse.bass as bass
import concourse.mybir as mybir
from concourse.bass_utils import run_bass_kernel_spmd

F32 = mybir.dt.float32
BF16 = mybir.dt.bfloat16
I32 = mybir.dt.int32
AF = mybir.ActivationFunctionType
ALU = mybir.AluOpType
AX = mybir.AxisListType

D = 2048
S = 4096
NT = 32
OWN = 1024
ENGS = ("pe", "act", "dve", "pool", "sp")
DEBUG = {}


class _Op:
    __slots__ = ("eng", "fn", "deps", "needs_inc", "is_dma", "sem", "count", "idx", "prev_same_sem", "is_cc")

    def __init__(self, eng, fn, is_dma):
        self.eng = eng
        self.fn = fn
        self.deps = set()
        self.needs_inc = False
        self.is_dma = is_dma
        self.sem = None
        self.count = 0
        self.prev_same_sem = None
        self.is_cc = False


class Prog:
    def __init__(self, nc, stack, n_dma_sems=48):
        self.nc = nc
        self.n_dma_sems = n_dma_sems
        self.eng_sem = {e: stack.enter_context(nc.semaphore("s_" + e)) for e in ENGS}
        self.dma_sems = [stack.enter_context(nc.semaphore("d%d" % i)) for i in range(n_dma_sems)]
        self.bar_sem = stack.enter_context(nc.semaphore("bar"))
        self.cc_sem = stack.enter_context(nc.semaphore("ccs"))
        self.cc_cnt = 0
        self.cnt = {e: 0 for e in ENGS}
        self.dcnt = [0] * n_dma_sems
        self.rr = 0
        self.nbar = 0
        self._reset()

    def _reset(self):
        self.ops = []
        self.last_writer = {}
        self.readers = {}

    def _record(self, op, reads, writes):
        idx = len(self.ops)
        op.idx = idx
        deps = set()
        for k in reads:
            w = self.last_writer.get(k)
            if w is not None:
                deps.add(w)
        for k in writes:
            w = self.last_writer.get(k)
            if w is not None:
                deps.add(w)
            for r in self.readers.get(k, ()):
                deps.add(r)
        deps.discard(idx)
        op.deps = deps
        self.ops.append(op)
        for k in reads:
            self.readers.setdefault(k, []).append(idx)
        for k in writes:
            self.last_writer[k] = idx
            self.readers[k] = []
        return idx

    def op(self, eng, fn, reads=(), writes=()):
        return self._record(_Op(eng, fn, False), reads, writes)

    def dma(self, queue, out, in_, reads=(), writes=(), **kw):
        def fn(e, out=out, in_=in_, kw=kw):
            return e.dma_start(out=out, in_=in_, **kw)
        return self._record(_Op(queue, fn, True), reads, writes)

    def coll(self, fn, reads=(), writes=()):
        o = _Op("pool", fn, True)
        o.is_cc = True
        return self._record(o, reads, writes)

    def flush(self):
        nc = self.nc
        ops = self.ops
        for o in ops:
            nd = set()
            for d in o.deps:
                p = ops[d]
                if o.eng == "pe" and p.eng == "pe" and not p.is_dma and not o.is_dma:
                    continue
                nd.add(d)
                p.needs_inc = True
            o.deps = nd
        last_of = {}
        for o in ops:
            if not o.is_dma:
                last_of[o.eng] = o
        for o in last_of.values():
            o.needs_inc = True
        dlast = [None] * self.n_dma_sems
        for o in ops:
            if o.is_cc:
                self.cc_cnt += 1
                o.sem = self.cc_sem
                o.count = self.cc_cnt
            elif o.is_dma:
                s = self.rr % self.n_dma_sems
                self.rr += 1
                o.prev_same_sem = dlast[s]
                self.dcnt[s] += 16
                o.sem = self.dma_sems[s]
                o.count = self.dcnt[s]
                dlast[s] = o.idx
            elif o.needs_inc:
                self.cnt[o.eng] += 1
                o.sem = self.eng_sem[o.eng]
                o.count = self.cnt[o.eng]
        per_eng = {e: [o for o in ops if o.eng == e] for e in ENGS}
        final = [(self.dma_sems[s], self.dcnt[s]) for s in range(self.n_dma_sems) if self.dcnt[s] > 0]
        final += [(self.eng_sem[e], self.cnt[e]) for e in ENGS if self.cnt[e] > 0]
        if self.cc_cnt > 0:
            final.append((self.cc_sem, self.cc_cnt))
        self.nbar += 1
        nbar = self.nbar
        bar = self.bar_sem

        def run(e_name, eng):
            waited = {}
            for o in per_eng[e_name]:
                need = {}
                for d in o.deps:
                    p = ops[d]
                    if need.get(p.sem.num, (0, None))[0] < p.count:
                        need[p.sem.num] = (p.count, p.sem)
                if o.is_dma and o.prev_same_sem is not None:
                    p = ops[o.prev_same_sem]
                    if need.get(p.sem.num, (0, None))[0] < p.count:
                        need[p.sem.num] = (p.count, p.sem)
                for key, (c, s) in need.items():
                    if waited.get(key, 0) < c:
                        eng.wait_ge(s, c)
                        waited[key] = c
                ins = o.fn(eng)
                if o.is_cc:
                    ins.then_inc(o.sem)
                elif o.is_dma:
                    ins.then_inc(o.sem, 16)
                elif o.needs_inc:
                    ins.then_inc(o.sem, 1)
            if e_name == "sp":
                for s, c in final:
                    eng.wait_ge(s, c)
                eng.sem_inc(bar, 1)
            eng.wait_ge(bar, nbar)

        with nc.Block() as block:
            @block.tensor
            def _(e):
                run("pe", e)

            @block.scalar
            def _(e):
                run("act", e)

            @block.vector
            def _(e):
                run("dve", e)

            @block.gpsimd
            def _(e):
                run("pool", e)

            @block.sync
            def _(e):
                run("sp", e)
        self._reset()


class Ctx:
    pass


def bcast_rows(ap1d, n):
    return bass.AP(ap1d.tensor, ap1d.offset, [[0, 128], [1, n]])


def dbg_out(K, name, shape, dtype=F32):
    t = K.nc.dram_tensor(name, list(shape), dtype, kind="ExternalOutput")
    K.dbg[name] = t
    return t.ap()


def make_ident(K, P, ident):
    P.op("pool", lambda e: e.memset(ident[:], 0.0), writes=["ident"])
    P.op("pool", lambda e: e.affine_select(out=ident[:], in_=ident[:], pattern=[[-1, 128]],
                                           compare_op=ALU.not_equal, fill=1.0, base=0,
                                           channel_multiplier=1),
         reads=["ident"], writes=["ident"])


def phase0_adaln(K):
    nc, P = K.nc, K.P
    NQ = 3072
    with contextlib.ExitStack() as st:
        c_sb = st.enter_context(nc.sbuf_tensor("c_sb", [128, 16], F32))
        cact = st.enter_context(nc.sbuf_tensor("cact", [128, 16], F32))
        wst = [st.enter_context(nc.sbuf_tensor("wst%d" % i, [128, 16, 512], F32)) for i in range(2)]
        modrow = st.enter_context(nc.sbuf_tensor("modrow", [1, NQ], F32))
        brow = st.enter_context(nc.sbuf_tensor("brow", [1, NQ], F32))
        ps = [st.enter_context(nc.psum_tensor("ps0_%d" % i, [1, 512], F32)) for i in range(2)]
        P.dma("sp", c_sb[:], K.c_arr, writes=["c_sb"])
        P.dma("sp", brow[:], K.b_ada.rearrange("(o n) -> o n", o=1), writes=["brow"])
        P.op("act", lambda e: e.activation(out=cact[:], in_=c_sb[:], func=AF.Silu),
             reads=["c_sb"], writes=["cact"])
        wv = K.w_ada.rearrange("(k p) n -> p k n", p=128)
        for nt in range(NQ // 512):
            sl = nt % 2
            for hh in range(2):
                P.dma("sp", wst[sl][:, hh * 8:(hh + 1) * 8, :],
                      wv[:, hh * 8:(hh + 1) * 8, nt * 512:(nt + 1) * 512],
                      writes=[("wst", sl, hh)])
            for k in range(16):
                P.op("pe", lambda e, k=k, sl=sl: e.matmul(ps[sl][:, :], lhsT=cact[:, k:k + 1],
                                                         rhs=wst[sl][:, k, :], start=(k == 0), stop=(k == 15)),
                     reads=["cact", ("wst", sl, k // 8)], writes=[("ps0", sl)])
            P.op("dve", lambda e, nt=nt, sl=sl: e.tensor_tensor(
                out=modrow[0:1, nt * 512:(nt + 1) * 512], in0=ps[sl][:, :],
                in1=brow[0:1, nt * 512:(nt + 1) * 512], op=ALU.add),
                reads=[("ps0", sl), "brow"], writes=[("modrow", nt)])
        P.dma("sp", K.modq_d, modrow[:],
              reads=[("modrow", nt) for nt in range(NQ // 512)], writes=["modq_d"])
        P.flush()
    P.coll(lambda e: e.collective_compute("AllGather", ALU.bypass, replica_groups=[[0, 1, 2, 3], [4, 5, 6, 7]],
                                          ins=[K.modq_d.opt()], outs=[K.mod4_d.opt()]), reads=["modq_d"], writes=["mod4"])
    P.flush()


def load_mod_rows(K, P, tile, which, gain_ap=None, key=None):
    src = K.mod_d[which * D:(which + 1) * D]
    P.dma("sp", tile[:], bcast_rows(src, D), writes=[key])


def bc(ap, shape):
    return ap.to_broadcast(list(shape))


def load_weight_bf16(K, P, st_tiles, dst, c_dst, src2d, ncols, tag, defer=None):
    wv = src2d.rearrange("(k p) n -> p k n", p=128)
    nk = wv.shape[1]
    engs = ["pool", "dve", "act"]
    for c0 in range(0, ncols, 512):
        n = min(512, ncols - c0)
        for k0 in range(0, nk, 4):
            kn = min(4, nk - k0)
            if defer is not None:
                defer.append(lambda c0=c0, n=n, k0=k0, kn=kn: _load_piece(K, P, st_tiles, dst, c_dst, wv, tag, engs, c0, n, k0, kn))
                continue
            _load_piece(K, P, st_tiles, dst, c_dst, wv, tag, engs, c0, n, k0, kn)
    return [(tag, c0, k0) for c0 in range(0, ncols, 512) for k0 in range(0, nk, 4)]


def _load_piece(K, P, st_tiles, dst, c_dst, wv, tag, engs, c0, n, k0, kn):
    if True:
        if True:
            i = K.wcnt
            K.wcnt += 1
            sl = i % len(st_tiles)
            stg = st_tiles[sl]
            P.dma("sp", stg[:, 0:kn, 0:n], wv[:, k0:k0 + kn, c0:c0 + n], writes=[("wstg", sl)])
            eng = engs[i % 3]
            o = dst[:, k0:k0 + kn, c_dst + c0:c_dst + c0 + n]
            if eng == "act":
                P.op("act", lambda e, o=o, stg=stg, kn=kn, n=n: e.copy(out=o, in_=stg[:, 0:kn, 0:n]),
                     reads=[("wstg", sl)], writes=[(tag, c0, k0)])
            else:
                P.op(eng, lambda e, o=o, stg=stg, kn=kn, n=n: e.tensor_copy(out=o, in_=stg[:, 0:kn, 0:n]),
                     reads=[("wstg", sl)], writes=[(tag, c0, k0)])


def rope_tables(K, P, st, pos_arr, ntile, invf_att, invf_idx, tag):
    nc = K.nc
    posi = st.enter_context(nc.sbuf_tensor(tag + "posi", [128, ntile], I32))
    posf = st.enter_context(nc.sbuf_tensor(tag + "posf", [128, ntile], F32))
    iva = st.enter_context(nc.sbuf_tensor(tag + "iva", [128, 16], F32))
    ivi = st.enter_context(nc.sbuf_tensor(tag + "ivi", [128, 8], F32))
    P.dma("sp", posi[:], pos_arr, writes=[tag + "posi"])
    P.dma("sp", iva[:], invf_att, writes=[tag + "iva"])
    P.dma("sp", ivi[:], invf_idx, writes=[tag + "ivi"])
    P.op("dve", lambda e: e.tensor_copy(out=posf[:], in_=posi[:]), reads=[tag + "posi"], writes=[tag + "posf"])
    out = {}
    for nm, iv, h in (("a", iva, 16), ("i", ivi, 8)):
        u = st.enter_context(nc.sbuf_tensor(tag + "u" + nm, [128, ntile, h], F32))
        ui = st.enter_context(nc.sbuf_tensor(tag + "ui" + nm, [128, ntile, h], I32))
        uf = st.enter_context(nc.sbuf_tensor(tag + "uf" + nm, [128, ntile, h], F32))
        for fn, off in (("sin", 0.0), ("cos", 0.25)):
            tb = st.enter_context(nc.sbuf_tensor(tag + fn + nm, [128, ntile, h], F32))
            kk = tag + fn + nm
            P.op("dve", lambda e, u=u, iv=iv, h=h: e.tensor_tensor(
                out=u[:], in0=bc(posf[:].unsqueeze(2), [128, ntile, h]),
                in1=bc(iv[:].unsqueeze(1), [128, ntile, h]), op=ALU.mult),
                reads=[tag + "posf", tag + "iv" + nm], writes=[tag + "U" + nm])
            P.op("dve", lambda e, u=u, off=off: e.tensor_scalar(
                out=u[:], in0=u[:], scalar1=float(1.0 / (2 * np.pi)), scalar2=off, op0=ALU.mult, op1=ALU.add),
                reads=[tag + "U" + nm], writes=[tag + "U" + nm])
            P.op("dve", lambda e, u=u, ui=ui: e.tensor_copy(out=ui[:], in_=u[:]), reads=[tag + "U" + nm], writes=[tag + "UI" + nm])
            P.op("dve", lambda e, uf=uf, ui=ui: e.tensor_copy(out=uf[:], in_=ui[:]), reads=[tag + "UI" + nm], writes=[tag + "UF" + nm])
            P.op("dve", lambda e, u=u, uf=uf: e.tensor_tensor(out=u[:], in0=u[:], in1=uf[:], op=ALU.subtract),
                 reads=[tag + "U" + nm, tag + "UF" + nm], writes=[tag + "U" + nm])
            P.op("dve", lambda e, u=u: e.tensor_scalar(out=u[:], in0=u[:], scalar1=-0.5, scalar2=0.5,
                                                        op0=ALU.max, op1=ALU.min),
                 reads=[tag + "U" + nm], writes=[tag + "U" + nm])
            P.op("act", lambda e, u=u, tb=tb: e.activation(out=tb[:], in_=u[:], func=AF.Sin,
                                                            scale=float(2 * np.pi)),
                 reads=[tag + "U" + nm], writes=[kk])
            out[fn + nm] = (tb, kk)
    return out


def apply_rope(P, eng, x4, cos, sin, t, half, tmp, rk, wk, sfx=""):
    ctb, ck = cos
    stb, sk = sin
    H = x4.shape[1]
    x1 = x4[:, :, 0:half]
    x2 = x4[:, :, half:2 * half]
    cb = bc(ctb[:, t, :].unsqueeze(1), [128, H, half])
    sb = bc(stb[:, t, :].unsqueeze(1), [128, H, half])
    a, b2, c, d = tmp
    P.op(eng, lambda e: e.tensor_tensor(out=a[:, 0:H, 0:half], in0=x1, in1=cb, op=ALU.mult), reads=rk + [ck], writes=["rtmpA" + sfx])
    P.op(eng, lambda e: e.tensor_tensor(out=b2[:, 0:H, 0:half], in0=x2, in1=sb, op=ALU.mult), reads=rk + [sk], writes=["rtmpB" + sfx])
    P.op(eng, lambda e: e.tensor_tensor(out=c[:, 0:H, 0:half], in0=x2, in1=cb, op=ALU.mult), reads=rk + [ck], writes=["rtmpC" + sfx])
    P.op(eng, lambda e: e.tensor_tensor(out=d[:, 0:H, 0:half], in0=x1, in1=sb, op=ALU.mult), reads=rk + [sk], writes=["rtmpD" + sfx])
    P.op(eng, lambda e: e.tensor_tensor(out=x1, in0=a[:, 0:H, 0:half], in1=b2[:, 0:H, 0:half], op=ALU.subtract),
         reads=["rtmpA" + sfx, "rtmpB" + sfx, "rtmpC" + sfx, "rtmpD" + sfx] + rk, writes=rk)
    P.op(eng, lambda e: e.tensor_tensor(out=x2, in0=c[:, 0:H, 0:half], in1=d[:, 0:H, 0:half], op=ALU.add),
         reads=["rtmpC" + sfx, "rtmpD" + sfx] + rk, writes=rk)


def head_rmsnorm(P, x3, gain, sq, ssum, rk, wk, gk=None, sqk=None):
    P.op("pool", lambda e: e.tensor_tensor(out=sq[:], in0=x3, in1=x3, op=ALU.mult), reads=rk, writes=[sqk or (wk + "sq")])
    P.op("dve", lambda e: e.tensor_reduce(out=ssum[:, 0:8], in_=sq[:], axis=AX.X, op=ALU.add),
         reads=[sqk or (wk + "sq")], writes=[wk + "s0"])
    P.op("dve", lambda e: e.tensor_scalar(out=ssum[:, 8:16], in0=ssum[:, 0:8], scalar1=1.0 / 128, scalar2=1e-6,
                                           op0=ALU.mult, op1=ALU.add), reads=[wk + "s0"], writes=[wk + "s1"])
    P.op("act", lambda e: e.activation(out=ssum[:, 16:24], in_=ssum[:, 8:16], func=AF.Sqrt),
         reads=[wk + "s1"], writes=[wk + "s2"])
    P.op("dve", lambda e: e.reciprocal(out=ssum[:, 24:32], in_=ssum[:, 16:24]), reads=[wk + "s2"], writes=[wk + "s3"])
    P.op("dve", lambda e: e.tensor_tensor(out=x3, in0=x3, in1=bc(ssum[:, 24:32].unsqueeze(2), [128, 8, 128]),
                                           op=ALU.mult), reads=rk + [wk + "s3"], writes=rk)
    P.op("pool", lambda e: e.tensor_tensor(out=x3, in0=x3, in1=bc(gain[:].unsqueeze(1), [128, 8, 128]),
                                            op=ALU.mult), reads=rk + [gk or ("gain" + wk)], writes=rk)


def norm_load(K, P, T, x_src, t):
    xs = t % 2
    P.dma("sp", T["xt"][xs][:], x_src[t * 128:(t + 1) * 128, :], writes=[("xt", xs)])


def norm_block(K, P, T, x_src, t, G1, SH1, ident, blk_hT, ti, load=True, hname="hT"):
    xs = t % 2
    xt, hb, ss, junk, pT = T["xt"], T["hb"], T["ss"], T["junk"], T["pT"]
    if load:
        norm_load(K, P, T, x_src, t)
    P.op("act", lambda e: e.activation(out=junk[:], in_=xt[xs][:], func=AF.Square, accum_out=ss[:, 0:1]),
         reads=[("xt", xs)], writes=["junk", "ss0"])
    P.op("dve", lambda e: e.tensor_scalar(out=ss[:, 1:2], in0=ss[:, 0:1], scalar1=1.0 / D, scalar2=1e-6,
                                           op0=ALU.mult, op1=ALU.add), reads=["ss0"], writes=["ss1"])
    P.op("act", lambda e: e.activation(out=ss[:, 2:3], in_=ss[:, 1:2], func=AF.Sqrt), reads=["ss1"], writes=["ss2"])
    P.op("dve", lambda e: e.reciprocal(out=ss[:, 3:4], in_=ss[:, 2:3]), reads=["ss2"], writes=["ss3"])
    P.op("dve", lambda e: e.scalar_tensor_tensor(out=xt[xs][:], in0=xt[xs][:], scalar=ss[:, 3:4], in1=G1[:],
                                                  op0=ALU.mult, op1=ALU.mult),
         reads=[("xt", xs), "ss3", "G"], writes=[("xt", xs)])
    P.op("pool", lambda e: e.tensor_tensor(out=hb[xs][:], in0=xt[xs][:], in1=SH1[:], op=ALU.add),
         reads=[("xt", xs), "SH"], writes=[("hb", xs)])
    for half in range(2):
        for kk in range(8):
            k = half * 8 + kk
            P.op("pe", lambda e, k=k, kk=kk, half=half: e.transpose(
                out=pT[half][:, kk, :], in_=hb[xs][:, k * 128:(k + 1) * 128], identity=ident[:]),
                reads=[("hb", xs), "ident"], writes=[("pT", half)])
        o = blk_hT[:, half * 8:(half + 1) * 8, ti * 128:(ti + 1) * 128]
        if half == 0:
            P.op("act", lambda e, o=o, half=half: e.copy(out=o, in_=pT[half][:]),
                 reads=[("pT", half)], writes=[(hname, ti, half)])
        else:
            P.op("dve", lambda e, o=o, half=half: e.tensor_copy(out=o, in_=pT[half][:]),
                 reads=[("pT", half)], writes=[(hname, ti, half)])


def norm_tiles_alloc(K, st, tag):
    nc = K.nc
    T = {}
    T["xt"] = [st.enter_context(nc.sbuf_tensor(tag + "xt%d" % i, [128, D], F32)) for i in range(2)]
    T["hb"] = [st.enter_context(nc.sbuf_tensor(tag + "hb%d" % i, [128, D], BF16)) for i in range(2)]
    T["ss"] = st.enter_context(nc.sbuf_tensor(tag + "ss", [128, 4], F32))
    T["junk"] = st.enter_context(nc.sbuf_tensor(tag + "junk", [128, D], BF16))
    T["pT"] = [st.enter_context(nc.psum_tensor(tag + "pT%d" % i, [128, 8, 128], BF16)) for i in range(2)]
    return T


def load_G_SH(K, P, st, which_sh, which_sc, gain_vec, tag):
    nc = K.nc
    G = st.enter_context(nc.sbuf_tensor(tag + "G", [128, D], F32))
    SH = st.enter_context(nc.sbuf_tensor(tag + "SH", [128, D], F32))
    gtmp = st.enter_context(nc.sbuf_tensor(tag + "gtmp", [128, D], F32))
    P.dma("sp", SH[:], bcast_rows(K.mod_d[which_sh * D:(which_sh + 1) * D], D), writes=["SH"])
    P.dma("sp", G[:], bcast_rows(K.mod_d[which_sc * D:(which_sc + 1) * D], D), writes=["G"])
    P.dma("sp", gtmp[:], bcast_rows(gain_vec, D), writes=["gtmp"])
    P.op("dve", lambda e: e.scalar_tensor_tensor(out=G[:], in0=G[:], scalar=1.0, in1=gtmp[:],
                                                  op0=ALU.add, op1=ALU.mult), reads=["G", "gtmp"], writes=["G"])
    return G, SH


def phase1_kv(K):
    nc, P = K.nc, K.P
    with contextlib.ExitStack() as st:
        ident = st.enter_context(nc.sbuf_tensor("ident", [128, 128], BF16))
        make_ident(K, P, ident)
        G1, SH1 = load_G_SH(K, P, st, 0, 1, K.norm1_g, "p1")
        T = norm_tiles_alloc(K, st, "p1")
        hT = [st.enter_context(nc.sbuf_tensor("hT%d" % i, [128, 16, 512], BF16)) for i in range(2)]
        W = st.enter_context(nc.sbuf_tensor("Wkv", [128, 16, 2112], BF16))
        stg = [st.enter_context(nc.sbuf_tensor("wstg%d" % i, [128, 4, 512], F32)) for i in range(2)]
        wk_k = load_weight_bf16(K, P, stg, W, 0, K.w_in[:, 1024:2048], 1024, "Wk")
        wk_v = load_weight_bf16(K, P, stg, W, 1024, K.w_in[:, 2048:3072], 1024, "Wv")
        wk_i = load_weight_bf16(K, P, stg, W, 2048, K.w_in[:, 4096:4160], 64, "Wi")
        rt = rope_tables(K, P, st, K.pos_full, 32, K.invf_att, K.invf_idx, "rf")
        gain = st.enter_context(nc.sbuf_tensor("kgain", [128, 128], F32))
        P.dma("sp", gain[:], bcast_rows(K.k_norm_g, 128), writes=["gainK"])
        def two(name, shape, dt):
            return [st.enter_context(nc.sbuf_tensor(name + str(i), shape, dt)) for i in range(2)]
        ksb2 = two("ksb", [128, 8, 128], F32)
        kbf2 = two("kbf", [128, 8, 128], BF16)
        sq2 = [st.enter_context(nc.sbuf_tensor("sq", [128, 8, 128], F32))] * 2
        ssum2 = two("ssum", [128, 32], F32)
        rtmp2 = [[st.enter_context(nc.sbuf_tensor("rtmp%d" % i, [128, 8, 16], F32)) for i in range(4)]] * 2
        vsb2 = two("vsb", [128, 8, 129], BF16)
        iksb2 = two("iksb", [128, 1, 64], F32)
        ikbf2 = two("ikbf", [128, 64], BF16)
        kTs2 = [st.enter_context(nc.sbuf_tensor("kTs", [128, 8, 128], BF16))] * 2
        ikTs2 = two("ikTs", [64, 128], BF16)
        pm = [st.enter_context(nc.psum_tensor("pm%d" % i, [128, 512], F32)) for i in range(3)]
        pk = st.enter_context(nc.psum_tensor("pk", [128, 8, 128], BF16))
        for s_ in range(2):
            P.op("pool", lambda e, s_=s_: e.memset(vsb2[s_][:], 1.0), writes=["vsb%d" % s_])
        norm_load(K, P, T, K.x_full, 0)

        def norm_tile(blk, ti):
            tt_ = blk * 4 + ti
            if tt_ + 1 < 32:
                norm_load(K, P, T, K.x_full, tt_ + 1)
            norm_block(K, P, T, K.x_full, tt_, G1, SH1, ident, hT[blk % 2], ti, load=False, hname=("hT", blk % 2))

        def store_hT(blk):
            hs = blk % 2
            hkeys = [(("hT", hs), ti, half) for ti in range(4) for half in range(2)]
            P.dma("sp", K.hT_d.rearrange("k p t -> p k t")[:, :, blk * 512:(blk + 1) * 512], hT[hs][:],
                  reads=hkeys, writes=[("hT_d", blk)])

        def bufs(t):
            u = t % 2
            return (str(u), ksb2[u], kbf2[u], sq2[u], ssum2[u], rtmp2[u], vsb2[u], iksb2[u], ikbf2[u], kTs2[u], ikTs2[u])

        def mm_tile(blk, ti):
            t = blk * 4 + ti
            hs = blk % 2
            hk = [(("hT", hs), ti, 0), (("hT", hs), ti, 1)]
            us, ksb, kbf, sq, ssum, rtmp, vsb, iksb, ikbf, kTs, ikTs = bufs(t)
            for gi, (c0, n, wkeys) in enumerate([(0, 512, wk_k), (512, 512, wk_k), (1024, 512, wk_v),
                                                 (1536, 512, wk_v), (2048, 64, wk_i)]):
                pb = pm[gi % 3]
                for k in range(16):
                    P.op("pe", lambda e, pb=pb, k=k, c0=c0, n=n, ti=ti, hs=hs: e.matmul(
                        pb[:, 0:n], lhsT=hT[hs][:, k, ti * 128:(ti + 1) * 128], rhs=W[:, k, c0:c0 + n],
                        start=(k == 0), stop=(k == 15)), reads=hk + wkeys, writes=[("pm", gi % 3)])
                if gi < 2:
                    P.op("act", lambda e, pb=pb, gi=gi, ksb=ksb: e.copy(out=ksb[:, gi * 4:(gi + 1) * 4, :], in_=pb[:, 0:512]),
                         reads=[("pm", gi % 3)], writes=["ksb" + us])
                elif gi < 4:
                    g2 = gi - 2
                    P.op("act", lambda e, pb=pb, g2=g2, vsb=vsb: e.copy(out=vsb[:, g2 * 4:(g2 + 1) * 4, 0:128], in_=pb[:, 0:512]),
                         reads=[("pm", gi % 3)], writes=["vsb" + us])
                else:
                    P.op("act", lambda e, pb=pb, iksb=iksb: e.copy(out=iksb[:, 0, :], in_=pb[:, 0:64]),
                         reads=[("pm", gi % 3)], writes=["iksb" + us])
            P.dma("sp", K.v_d[t * 128:(t + 1) * 128, :], vsb[:].rearrange("p h d -> p (h d)"),
                  reads=["vsb" + us], writes=[("v_d", t)])

        def post1(blk, ti):
            t = blk * 4 + ti
            us, ksb, kbf, sq, ssum, rtmp, vsb, iksb, ikbf, kTs, ikTs = bufs(t)
            head_rmsnorm(P, ksb[:], gain, sq, ssum, ["ksb" + us], "K" + us, gk="gainK", sqk="Ksq")
            apply_rope(P, "dve", ksb[:], rt["cosa"], rt["sina"], t, 16, rtmp, ["ksb" + us], "rK")
            P.op("act", lambda e, kbf=kbf, ksb=ksb: e.copy(out=kbf[:], in_=ksb[:]), reads=["ksb" + us], writes=["kbf" + us])
            apply_rope(P, "pool", iksb[:], rt["cosi"], rt["sini"], t, 8, rtmp, ["iksb" + us], "rI")
            P.op("act", lambda e, ikbf=ikbf, iksb=iksb: e.copy(out=ikbf[:], in_=iksb[:, 0, :]), reads=["iksb" + us], writes=["ikbf" + us])

        def post2(blk, ti):
            t = blk * 4 + ti
            us, ksb, kbf, sq, ssum, rtmp, vsb, iksb, ikbf, kTs, ikTs = bufs(t)
            for h in range(8):
                P.op("pe", lambda e, h=h, kbf=kbf: e.transpose(out=pk[:, h, :], in_=kbf[:, h, :], identity=ident[:]),
                     reads=["kbf" + us, "ident"], writes=["pk"])
            P.op("dve", lambda e, kTs=kTs: e.tensor_copy(out=kTs[:], in_=pk[:]), reads=["pk"], writes=["kTs"])
            P.dma("sp", K.kT_d.rearrange("h p t -> p h t")[:, :, t * 128:(t + 1) * 128], kTs[:],
                  reads=["kTs"], writes=[("kT_d", t)])
            P.op("pe", lambda e, ikbf=ikbf: e.transpose(out=pk[0:64, 0, :], in_=ikbf[:], identity=ident[:]),
                 reads=["ikbf" + us, "ident"], writes=["pk"])
            P.op("dve", lambda e, ikTs=ikTs: e.tensor_copy(out=ikTs[:], in_=pk[0:64, 0, :]), reads=["pk"], writes=["ikTs" + us])
            P.dma("sp", K.ikT_d[:, t * 128:(t + 1) * 128], ikTs[:], reads=["ikTs" + us], writes=[("ikT_d", t)])

        for ti in range(4):
            norm_tile(0, ti)
        store_hT(0)
        prev = None
        for blk in range(8):
            for ti in range(4):
                mm_tile(blk, ti)
                if blk + 1 < 8:
                    norm_tile(blk + 1, ti)
                post1(blk, ti)
                if prev is not None:
                    post2(*prev)
                prev = (blk, ti)
            if blk + 1 < 8:
                store_hT(blk + 1)
        post2(*prev)
        P.flush()

RW0 = 4176
NRW = 1216
RW_GROUPS = [(i * 128, 128) for i in range(6)] + [(768, 96), (864, 96), (960, 128), (1088, 128)]


def phase1b_rwkv_proj(K):
    nc, P = K.nc, K.P
    with contextlib.ExitStack() as st:
        W = st.enter_context(nc.sbuf_tensor("Wr", [128, 16, NRW], BF16))
        stg = [st.enter_context(nc.sbuf_tensor("wstgb%d" % i, [128, 4, 512], F32)) for i in range(2)]
        hT = [st.enter_context(nc.sbuf_tensor("hTb%d" % i, [128, 16, 512], BF16)) for i in range(2)]
        ost = [st.enter_context(nc.sbuf_tensor("ost%d" % i, [128, 512], F32)) for i in range(4)]
        pm = [st.enter_context(nc.psum_tensor("pmb%d" % i, [128, 512], F32)) for i in range(4)]
        wkeys = load_weight_bf16(K, P, stg, W, 0, K.w_in_rw, NRW, "Wr")
        cnt = 0
        for blk in range(8):
            hs = blk % 2
            P.dma("sp", hT[hs][:], K.hT_d.rearrange("k p t -> p k t")[:, :, blk * 512:(blk + 1) * 512],
                  writes=[("hTb", hs)])
            for (r0, m) in RW_GROUPS:
                s4 = cnt % 4
                cnt += 1
                for k in range(16):
                    P.op("pe", lambda e, k=k, r0=r0, m=m, hs=hs, s4=s4: e.matmul(
                        pm[s4][0:m, :], lhsT=W[:, k, r0:r0 + m], rhs=hT[hs][:, k, :],
                        start=(k == 0), stop=(k == 15)), reads=[("hTb", hs)] + wkeys, writes=[("pmb", s4)])
                if cnt % 2 == 0:
                    P.op("act", lambda e, m=m, s4=s4: e.copy(out=ost[s4][0:m, :], in_=pm[s4][0:m, :]),
                         reads=[("pmb", s4)], writes=[("ost", s4)])
                else:
                    P.op("dve", lambda e, m=m, s4=s4: e.tensor_copy(out=ost[s4][0:m, :], in_=pm[s4][0:m, :]),
                         reads=[("pmb", s4)], writes=[("ost", s4)])
                P.dma("sp", K.yT_d[r0:r0 + m, blk * 512:(blk + 1) * 512], ost[s4][0:m, :],
                      reads=[("ost", s4)], writes=[("yT_d", r0, blk)])
        P.flush()


def phase2_own_proj(K):
    nc, P = K.nc, K.P
    with contextlib.ExitStack() as st:
        ident = st.enter_context(nc.sbuf_tensor("ident2", [128, 128], BF16))
        make_ident(K, P, ident)
        G1, SH1 = load_G_SH(K, P, st, 0, 1, K.norm1_g, "p2")
        T = norm_tiles_alloc(K, st, "p2")
        hT = [st.enter_context(nc.sbuf_tensor("hTo%d" % i, [128, 16, 512], BF16)) for i in range(2)]
        W = st.enter_context(nc.sbuf_tensor("Wq", [128, 16, 2064], BF16))
        stg = [st.enter_context(nc.sbuf_tensor("wstgq%d" % i, [128, 4, 512], F32)) for i in range(2)]
        wk_q = load_weight_bf16(K, P, stg, W, 0, K.w_in[:, 0:1024], 1024, "Wq")
        wk_iq = load_weight_bf16(K, P, stg, W, 1024, K.w_in[:, 3072:4096], 1024, "Wiq")
        wk_iw = load_weight_bf16(K, P, stg, W, 2048, K.w_in[:, 4160:4176], 16, "Wiw")
        rt = rope_tables(K, P, st, K.pos_own, 8, K.invf_att, K.invf_idx, "ro")
        gain = st.enter_context(nc.sbuf_tensor("qgain", [128, 128], F32))
        P.dma("sp", gain[:], bcast_rows(K.q_norm_g, 128), writes=["gainQ"])
        qsb = st.enter_context(nc.sbuf_tensor("qsb", [128, 8, 128], F32))
        qbf = st.enter_context(nc.sbuf_tensor("qbf", [128, 8, 128], BF16))
        sq = st.enter_context(nc.sbuf_tensor("sq2", [128, 8, 128], F32))
        ssum = st.enter_context(nc.sbuf_tensor("ssum2", [128, 32], F32))
        rtmp = [st.enter_context(nc.sbuf_tensor("rtmpq%d" % i, [128, 16, 16], F32)) for i in range(4)]
        iqsb = st.enter_context(nc.sbuf_tensor("iqsb", [128, 16, 64], F32))
        iqbf = st.enter_context(nc.sbuf_tensor("iqbf", [128, 16, 64], BF16))
        iwsb = st.enter_context(nc.sbuf_tensor("iwsb", [128, 16], F32))
        qTs = st.enter_context(nc.sbuf_tensor("qTs", [128, 8, 128], BF16))
        iqTs = st.enter_context(nc.sbuf_tensor("iqTs", [64, 128, 16], BF16))
        pm = [st.enter_context(nc.psum_tensor("pmq%d" % i, [128, 512], F32)) for i in range(3)]
        pk = st.enter_context(nc.psum_tensor("pkq", [128, 8, 128], BF16))
        for blk in range(2):
            hs = blk % 2
            for ti in range(4):
                norm_block(K, P, T, K.x_own, blk * 4 + ti, G1, SH1, ident, hT[hs], ti)
            for ti in range(4):
                t = blk * 4 + ti
                hk = [("hT", ti, 0), ("hT", ti, 1)]
                for gi, (c0, n, wkeys) in enumerate([(0, 512, wk_q), (512, 512, wk_q), (1024, 512, wk_iq),
                                                     (1536, 512, wk_iq), (2048, 16, wk_iw)]):
                    pb = pm[gi % 3]
                    for k in range(16):
                        P.op("pe", lambda e, pb=pb, k=k, c0=c0, n=n, ti=ti, hs=hs: e.matmul(
                            pb[:, 0:n], lhsT=hT[hs][:, k, ti * 128:(ti + 1) * 128], rhs=W[:, k, c0:c0 + n],
                            start=(k == 0), stop=(k == 15)), reads=hk + wkeys, writes=[("pmq", gi % 3)])
                    if gi < 2:
                        P.op("act", lambda e, pb=pb, gi=gi: e.copy(out=qsb[:, gi * 4:(gi + 1) * 4, :], in_=pb[:, 0:512]),
                             reads=[("pmq", gi % 3)], writes=["qsb"])
                    elif gi < 4:
                        g2 = gi - 2
                        P.op("act", lambda e, pb=pb, g2=g2: e.copy(out=iqsb[:, g2 * 8:(g2 + 1) * 8, :], in_=pb[:, 0:512]),
                             reads=[("pmq", gi % 3)], writes=["iqsb"])
                    else:
                        P.op("act", lambda e, pb=pb: e.activation(out=iwsb[:], in_=pb[:, 0:16], func=AF.Copy, scale=0.25),
                             reads=[("pmq", gi % 3)], writes=["iwsb"])
                P.dma("sp", K.iw_d[t * 128:(t + 1) * 128, :], iwsb[:], reads=["iwsb"], writes=[("iw_d", t)])
                head_rmsnorm(P, qsb[:], gain, sq, ssum, ["qsb"], "Q")
                apply_rope(P, "dve", qsb[:], rt["cosa"], rt["sina"], t, 16, rtmp, ["qsb"], "rQ")
                P.op("act", lambda e: e.copy(out=qbf[:], in_=qsb[:]), reads=["qsb"], writes=["qbf"])
                for h in range(8):
                    P.op("pe", lambda e, h=h: e.transpose(out=pk[:, h, :], in_=qbf[:, h, :], identity=ident[:]),
                         reads=["qbf", "ident"], writes=["pkq"])
                P.op("dve", lambda e: e.tensor_copy(out=qTs[:], in_=pk[:]), reads=["pkq"], writes=["qTs"])
                P.dma("sp", K.qT_d.rearrange("h p t -> p h t")[:, :, t * 128:(t + 1) * 128], qTs[:],
                      reads=["qTs"], writes=[("qT_d", t)])
                apply_rope(P, "pool", iqsb[:], rt["cosi"], rt["sini"], t, 8, rtmp, ["iqsb"], "rIQ")
                P.op("act", lambda e: e.activation(out=iqbf[:], in_=iqsb[:], func=AF.Copy, scale=0.125),
                     reads=["iqsb"], writes=["iqbf"])
                for half in range(2):
                    for hh in range(8):
                        h = half * 8 + hh
                        P.op("pe", lambda e, h=h, hh=hh: e.transpose(out=pk[0:64, hh, :], in_=iqbf[:, h, :],
                                                                      identity=ident[:]),
                             reads=["iqbf", "ident"], writes=["pkq"])
                    P.op("dve", lambda e, half=half: e.tensor_copy(
                        out=iqTs[:, :, half * 8:(half + 1) * 8].rearrange("p t h -> p h t"), in_=pk[0:64, :, :]),
                         reads=["pkq"], writes=["iqTs"])
                P.dma("sp", K.iqT_d[:, t * 128:(t + 1) * 128, :], iqTs[:], reads=["iqTs"], writes=[("iqT_d", t)])
        P.flush()


NIT = 22
SLOT_NK = [4, 8, 12, 16, 20, 24, 28, 32]


def phase3_attention(K):
    nc, P = K.nc, K.P
    with contextlib.ExitStack() as st:
        def sb(name, shape, dt):
            return st.enter_context(nc.sbuf_tensor(name, shape, dt))
        ident = sb("ident3", [128, 128], BF16)
        kposf = sb("kposf", [128, 512], F32)
        bias = sb("cbias", [128, 512], F32)
        identf = kposf[:, 0:128]
        make_ident(K, P, ident)
        P.op("dve", lambda e: e.tensor_copy(out=identf, in_=ident[:]), reads=["ident"], writes=["kposf"])
        kT = sb("kTall", [128, 8, S], BF16)
        V = sb("Vall", [128, 32, 1032], BF16)
        ikT = sb("ikTall", [64, S], BF16)
        for h in range(8):
            P.dma("sp", kT[:, h, :], K.kT_d[h], writes=[("kT", h)])
        for q4 in range(4):
            P.dma("sp", V[:, q4 * 8:(q4 + 1) * 8, :],
                  K.v_d.rearrange("(t p) c -> p t c", p=128)[:, q4 * 8:(q4 + 1) * 8, :], writes=[("V", q4)])
        P.dma("sp", ikT[:], K.ikT_d, writes=["ikT"])
        kTk = [("kT", h) for h in range(8)]
        Vk = [("V", q4) for q4 in range(4)]
        Sel = sb("Sel", [128, 16, 128], BF16)
        pidx = sb("pidx", [128, 1], I32)
        pidf = sb("pidf", [128, 1], F32)
        score = sb("score", [128, S], F32)
        self_ = score[:, 0:2048].rearrange("p (g t) -> p g t", g=16)
        sk4 = [("score", q) for q in range(4)]
        P.op("pool", lambda e: e.iota(self_, pattern=[[-8, 16], [1, 128]], base=0, channel_multiplier=0, allow_small_or_imprecise_dtypes=True), writes=sk4)
        P.op("pool", lambda e: e.iota(pidx[:], pattern=[[0, 1]], base=0, channel_multiplier=1), writes=["pidx"])
        P.op("dve", lambda e: e.tensor_scalar(out=pidx[:], in0=pidx[:], scalar1=4, scalar2=None,
                                               op0=ALU.arith_shift_right), reads=["pidx"], writes=["pidx"])
        P.op("dve", lambda e: e.tensor_copy(out=pidf[:], in_=pidx[:]), reads=["pidx"], writes=["pidf"])
        P.op("dve", lambda e: e.tensor_scalar(out=Sel[:], in0=self_, scalar1=pidf[:, 0:1], scalar2=None,
                                               op0=ALU.is_equal), reads=sk4 + ["pidf"], writes=["Sel"])
        qpos = sb("qpos", [128, 8], F32)
        P.dma("sp", qpos[:], K.qpos_own, writes=["qpos"])
        iwg = bias[:, 0:128]
        wcol = sb("wcol", [128, 128], F32)
        P.dma("sp", iwg, K.iw_d.rearrange("(g t) h -> g (t h)", t=8), writes=["bias"])
        A = [st.enter_context(nc.psum_tensor("A%d" % i, [128, 512], F32)) for i in range(2)]
        B = [st.enter_context(nc.psum_tensor("B%d" % i, [128, 512], F32)) for i in range(2)]
        C = st.enter_context(nc.psum_tensor("C3", [128, 8, 128], BF16))
        P.op("pe", lambda e: e.transpose(out=A[0][:, 0:128], in_=iwg, identity=identf),
             reads=["bias", "kposf"], writes=[("A", 0)])
        P.op("dve", lambda e: e.tensor_copy(out=wcol[:], in_=A[0][:, 0:128]), reads=[("A", 0)], writes=["wcol"])
        mask01 = sb("mask01", [128, S], BF16)
        maskT = sb("maskT", [128, 32, 128], BF16)
        R = [sb("R%d" % i, [128, 512], BF16) for i in range(2)]
        pexp = [sb("pexp%d" % i, [128, 512], BF16) for i in range(2)]
        pmk = [sb("pmk%d" % i, [128, 512], BF16) for i in range(2)]
        iqTs = sb("iqTs3", [64, 128, 16], BF16)
        qTs = sb("qTs3", [128, 8, 128], BF16)
        att = sb("att", [128, 8, 128], BF16)
        attTs = sb("attTs", [128, 8, 128], BF16)
        c2 = sb("c2", [128, NIT], F32)
        steps = sb("steps", [128, NIT], F32)
        sm = sb("sm3", [128, 8], F32)
        for k in range(NIT):
            P.op("pool", lambda e, k=k: e.memset(c2[:, k:k + 1], float(2.0 ** -(k + 1))), writes=["c2"])
        maskT2 = [maskT, sb("maskTb", [128, 32, 128], BF16)]
        Wbd = sb("Wbd", [128, 16, 128], BF16)
        qTs2 = [qTs, sb("qTs3b", [128, 8, 128], BF16)]
        rcp2 = sb("rcp2", [128, 2], F32)

        def stageA(i):
            nk = SLOT_NK[i]
            nb = nk // 4
            P.dma("sp", iqTs[:], K.iqT_d[:, i * 128:(i + 1) * 128, :], writes=["iqTs"])
            P.dma("sp", qTs2[i % 2][:], K.qT_d.rearrange("h p t -> p h t")[:, :, i * 128:(i + 1) * 128], writes=[("qTs", i % 2)])
            isteps = [(sbk, g) for sbk in range(nb) for g in range(16)]
            P.op("pool", lambda e, i=i: e.tensor_tensor(out=Wbd[:], in0=Sel[:],
                                                         in1=wcol[:, i * 16:(i + 1) * 16].unsqueeze(2).to_broadcast([128, 16, 128]),
                                                         op=ALU.mult), reads=["Sel", "wcol"], writes=["Wbd"])

            def dots(si):
                sbk, g = isteps[si]
                a_ = si % 2
                lhsT = iqTs[:, g * 8:(g + 1) * 8, :].rearrange("p t h -> p (t h)")
                P.op("pe", lambda e, a_=a_, lhsT=lhsT, sbk=sbk: e.matmul(
                    A[a_][:, :], lhsT=lhsT, rhs=ikT[:, sbk * 512:(sbk + 1) * 512], start=True, stop=True),
                    reads=["iqTs", "ikT"], writes=[("A", a_)])
            dots(0)
            for si, (sbk, g) in enumerate(isteps):
                a_ = si % 2
                bsl = sbk % 2
                if si + 1 < len(isteps):
                    dots(si + 1)
                if si % 2 == 0:
                    P.op("act", lambda e, a_=a_: e.activation(out=R[a_][:], in_=A[a_][:, :], func=AF.Relu),
                         reads=[("A", a_)], writes=[("R", a_)])
                else:
                    P.op("dve", lambda e, a_=a_: e.tensor_scalar(out=R[a_][:], in0=A[a_][:, :], scalar1=0.0, scalar2=None,
                                                                  op0=ALU.max), reads=[("A", a_)], writes=[("R", a_)])
                P.op("pe", lambda e, a_=a_, g=g, bsl=bsl: e.matmul(
                    B[bsl][:, :], lhsT=Wbd[:, g, :], rhs=R[a_][:], start=(g == 0), stop=(g == 15)),
                    reads=[("R", a_), "Wbd"], writes=[("B", bsl)])
                if g == 15:
                    P.op("dve", lambda e, bsl=bsl, sbk=sbk: e.tensor_copy(out=score[:, sbk * 512:(sbk + 1) * 512], in_=B[bsl][:, :]),
                         reads=[("B", bsl)], writes=[("score", sbk)])

        def stageBdve(i):
            nk = SLOT_NK[i]
            nb = nk // 4
            L = nk * 128
            sck = [("score", sbk) for sbk in range(nb)]
            P.op("dve", lambda e, L=L: e.tensor_reduce(out=sm[:, 0:1], in_=score[:, 0:L], axis=AX.X, op=ALU.max,
                                                        apply_absolute_value=True), reads=sck, writes=["sm0"])
            P.op("pool", lambda e, nb=nb: e.iota(kposf[:], pattern=[[1, 512]], base=(nb - 1) * 512, channel_multiplier=0,
                                                 allow_small_or_imprecise_dtypes=True), writes=["kposf"])
            P.op("dve", lambda e, i=i: e.tensor_scalar(out=bias[:], in0=kposf[:], scalar1=qpos[:, i:i + 1],
                                                        scalar2=-1e30, op0=ALU.is_gt, op1=ALU.mult),
                 reads=["kposf", "qpos"], writes=["bias"])
            P.op("dve", lambda e, nb=nb: e.tensor_tensor(out=score[:, (nb - 1) * 512:nb * 512],
                                                          in0=score[:, (nb - 1) * 512:nb * 512], in1=bias[:], op=ALU.add),
                 reads=["bias", ("score", nb - 1), "sm0"], writes=[("score", nb - 1)])
            P.op("dve", lambda e: e.tensor_scalar(out=sm[:, 1:2], in0=sm[:, 0:1], scalar1=-1.0, scalar2=-1.0,
                                                   op0=ALU.mult, op1=ALU.add), reads=["sm0"], writes=["lo"])
            P.op("dve", lambda e: e.tensor_scalar(out=sm[:, 5:6], in0=sm[:, 0:1], scalar1=2.0, scalar2=2.0,
                                                   op0=ALU.mult, op1=ALU.add), reads=["sm0"], writes=["d0"])
            P.op("dve", lambda e: e.tensor_scalar(out=steps[:], in0=c2[:], scalar1=sm[:, 5:6], scalar2=None,
                                                   op0=ALU.mult), reads=["d0", "c2"], writes=["steps"])
            P.op("dve", lambda e: e.tensor_tensor(out=sm[:, 2:3], in0=sm[:, 1:2], in1=steps[:, 0:1], op=ALU.add),
                 reads=["lo", "steps"], writes=["mid"])
            for k in range(NIT):
                P.op("dve", lambda e, L=L: e.tensor_scalar(out=mask01[:, 0:L], in0=score[:, 0:L], scalar1=sm[:, 2:3],
                                                            scalar2=None, op0=ALU.is_ge, op1=ALU.add,
                                                            accum_out=sm[:, 3:4]),
                     reads=sck + ["mid"], writes=["mask01", "cnt"])
                P.op("dve", lambda e: e.tensor_scalar(out=sm[:, 4:5], in0=sm[:, 3:4], scalar1=255.5, scalar2=-0.5,
                                                       op0=ALU.is_ge, op1=ALU.add), reads=["cnt"], writes=["inc"])
                P.op("dve", lambda e, k=k: e.scalar_tensor_tensor(out=sm[:, 2:3], in0=steps[:, k:k + 1], scalar=sm[:, 4:5],
                                                                   in1=sm[:, 2:3], op0=ALU.mult, op1=ALU.add),
                     reads=["inc", "steps", "mid"], writes=["mid"])
            P.op("dve", lambda e: e.scalar_tensor_tensor(out=sm[:, 1:2], in0=steps[:, NIT - 1:NIT], scalar=-0.5, in1=sm[:, 2:3],
                                                          op0=ALU.mult, op1=ALU.add), reads=["steps", "mid"], writes=["lo"])
            P.op("dve", lambda e, L=L: e.tensor_scalar(out=mask01[:, 0:L], in0=score[:, 0:L], scalar1=sm[:, 1:2],
                                                        scalar2=None, op0=ALU.is_ge), reads=sck + ["lo"], writes=["mask01"])

        def stageBpe(i):
            nk = SLOT_NK[i]
            mT = maskT2[i % 2]
            for kt in range(nk):
                P.op("pe", lambda e, kt=kt: e.transpose(out=C[:, kt % 8, :], in_=mask01[:, kt * 128:(kt + 1) * 128],
                                                         identity=ident[:]), reads=["mask01", "ident"], writes=["C"])
                if kt % 8 == 7 or kt == nk - 1:
                    k0 = (kt // 8) * 8
                    n8 = kt - k0 + 1
                    P.op("dve", lambda e, k0=k0, n8=n8, mT=mT: e.tensor_copy(out=mT[:, k0:k0 + n8, :], in_=C[:, 0:n8, :]),
                         reads=["C"], writes=[("maskT", i % 2, k0 // 8)])

        def stageC(i):
            nk = SLOT_NK[i]
            nb = nk // 4
            mT = maskT2[i % 2]
            qT_ = qTs2[i % 2]
            mk = [("maskT", i % 2, q) for q in range((nk + 7) // 8)]
            asteps = [(h, kg) for h in range(8) for kg in range(nb)]

            def qk(si):
                h, kg = asteps[si]
                a_ = si % 2
                for j4 in range(4):
                    kt = kg * 4 + j4
                    P.op("pe", lambda e, a_=a_, j4=j4, kt=kt, h=h: e.matmul(
                        A[a_][:, j4 * 128:(j4 + 1) * 128], lhsT=kT[:, h, kt * 128:(kt + 1) * 128], rhs=qT_[:, h, :],
                        start=True, stop=True), reads=kTk + [("qTs", i % 2)], writes=[("A", a_)])
            qk(0)
            for si, (h, kg) in enumerate(asteps):
                a_ = si % 2
                bsl = h % 2
                if si + 1 < len(asteps):
                    qk(si + 1)
                P.op("act", lambda e, a_=a_: e.activation(out=pexp[a_][:], in_=A[a_][:, :], func=AF.Exp,
                                                           scale=float(128 ** -0.5)),
                     reads=[("A", a_)], writes=[("pexp", a_)])
                P.op("pool", lambda e, a_=a_, kg=kg: e.tensor_tensor(
                    out=pmk[a_][:], in0=pexp[a_][:], in1=mT[:, kg * 4:(kg + 1) * 4, :].rearrange("p a t -> p (a t)"),
                    op=ALU.mult), reads=[("pexp", a_)] + mk, writes=[("pmk", a_)])
                for j4 in range(4):
                    kt = kg * 4 + j4
                    P.op("pe", lambda e, a_=a_, j4=j4, kt=kt, h=h, bsl=bsl, kg=kg, nb=nb: e.matmul(
                        B[bsl][:, 0:129], lhsT=pmk[a_][:, j4 * 128:(j4 + 1) * 128], rhs=V[:, kt, h * 129:(h + 1) * 129],
                        start=(kg == 0 and j4 == 0), stop=(kg == nb - 1 and j4 == 3)),
                        reads=[("pmk", a_)] + Vk, writes=[("B", bsl)])
                if kg == nb - 1:
                    P.op("act", lambda e, bsl=bsl: e.activation(out=rcp2[:, 0:1], in_=B[bsl][:, 128:129], func=AF.Ln),
                         reads=[("B", bsl)], writes=["rcpa"])
                    P.op("act", lambda e: e.activation(out=rcp2[:, 1:2], in_=rcp2[:, 0:1], func=AF.Exp, scale=-1.0),
                         reads=["rcpa"], writes=["rcpb"])
                    P.op("act", lambda e, bsl=bsl, h=h: e.activation(out=att[:, h, :], in_=B[bsl][:, 0:128], func=AF.Copy,
                                                                      scale=rcp2[:, 1:2]),
                         reads=[("B", bsl), "rcpb"], writes=["att"])
            for h in range(8):
                P.op("pe", lambda e, h=h: e.transpose(out=C[:, h, :], in_=att[:, h, :], identity=ident[:]),
                     reads=["att", "ident"], writes=["C"])
            P.op("act", lambda e: e.copy(out=attTs[:], in_=C[:]), reads=["C"], writes=["attTs"])
            P.dma("sp", K.attT_d.rearrange("h p t -> p h t")[:, :, i * 128:(i + 1) * 128], attTs[:],
                  reads=["attTs"], writes=[("attT_d", i)])

        stageA(0)
        stageBdve(0)
        stageBpe(0)
        for i in range(8):
            if i + 1 < 8:
                stageA(i + 1)
                stageBdve(i + 1)
            stageC(i)
            if i + 1 < 8:
                stageBpe(i + 1)
        P.flush()

RD = BF16
NCH = 64


def tok_shift(P, dst, raw, tmp, mu_ap, rk_raw, k_tmp, k_dst, n=128):
    P.op("pool", lambda e: e.tensor_tensor(out=tmp[0:n, 1:S], in0=raw[0:n, 0:S - 1], in1=raw[0:n, 1:S], op=ALU.subtract),
         reads=[rk_raw], writes=[k_tmp])
    P.op("pool", lambda e: e.tensor_scalar(out=tmp[0:n, 0:1], in0=raw[0:n, 0:1], scalar1=-1.0, scalar2=0.0,
                                            op0=ALU.mult, op1=ALU.add), reads=[rk_raw, k_tmp], writes=[k_tmp])
    P.op("dve", lambda e: e.scalar_tensor_tensor(out=dst[0:n, :], in0=tmp[0:n, :], scalar=mu_ap, in1=raw[0:n, :],
                                                  op0=ALU.mult, op1=ALU.add), reads=[rk_raw, k_tmp], writes=[k_dst])


def phase4b_rwkv_prep(K, cts=range(2)):
    nc, P = K.nc, K.P
    with contextlib.ExitStack() as st:
        def sb(name, shape, dt):
            return st.enter_context(nc.sbuf_tensor(name, shape, dt))
        txw = sb("txw", [96, S], BF16)
        xap = sb("xap", [96, S], BF16)
        sxg = sb("sxg", [128, 2, S], BF16)
        M01 = sb("M01", [128, S], BF16)
        wup = sb("wup", [96, 256], BF16)
        aup = sb("aup", [96, 256], BF16)
        gup = sb("gup", [128, 2, 256], BF16)
        wst = sb("wst4", [128, 2, 256], F32)
        bones = sb("bones", [128, 128], BF16)
        prm = sb("prm", [128, 12, 2], F32)
        mul = sb("mul", [128, 4], F32)
        PT = sb("PT", [128, S], F32)
        KK = sb("KK", [128, S], F32)
        KP = sb("KP", [128, S], F32)
        CL = sb("CL", [128, S], F32)
        RP = sb("RP", [128, S], BF16)
        VP = sb("VP", [128, S], BF16)
        AA = sb("AA", [128, S], BF16)
        K2 = sb("K2", [128, S], BF16)
        SQb = sb("SQb", [128, S], BF16)
        OUT = [sb("OUT%d" % i, [128, S], BF16) for i in range(2)]
        PCt = sb("PCt", [128, NCH], F32)
        ps = [st.enter_context(nc.psum_tensor("ps4_%d" % i, [128, 512], F32)) for i in range(4)]
        for i, ap in enumerate(K.rw_prm):
            P.dma("sp", prm[:, i, :], ap, writes=[("prm", i)])
        prk = [("prm", i) for i in range(10)]
        P.op("dve", lambda e: e.tensor_scalar(out=prm[:, 10, :], in0=prm[:, 6, :], scalar1=-1.0, scalar2=1.0,
                                               op0=ALU.mult, op1=ALU.add), reads=prk, writes=[("prm", 10)])
        prk = prk + [("prm", 10)]
        P.dma("sp", mul[:], K.rw_mul, writes=["mul"])
        P.op("pool", lambda e: e.memset(bones[:], 0.0), writes=["bones"])
        P.op("pool", lambda e: e.memset(bones[0:64, 0:64], 1.0), reads=["bones"], writes=["bones"])
        P.op("pool", lambda e: e.memset(bones[64:128, 64:128], 1.0), reads=["bones"], writes=["bones"])
        P.op("pool", lambda e: e.iota(PT[:].rearrange("p (c t) -> p c t", t=64), pattern=[[0, NCH], [1, 64]], base=0,
                                      channel_multiplier=0, allow_small_or_imprecise_dtypes=True), writes=["PT"])
        P.op("dve", lambda e: e.tensor_scalar(out=M01[:], in0=PT[:], scalar1=0.5, scalar2=None, op0=ALU.is_gt),
             reads=["PT"], writes=["M01"])
        P.dma("sp", wst[0:96, 0, :], K.rw_w_up, writes=["wst"])
        P.op("act", lambda e: e.copy(out=wup[:], in_=wst[0:96, 0, :]), reads=["wst"], writes=["wup"])
        P.dma("sp", wst[0:96, 1, :], K.rw_a_up, reads=[], writes=["wst1"])
        P.op("act", lambda e: e.copy(out=aup[:], in_=wst[0:96, 1, :]), reads=["wst1"], writes=["aup"])
        P.dma("sp", wst[:, :, :], K.rw_g_up.rearrange("(c p) n -> p c n", p=128), reads=[], writes=["wst", "wst1"])
        P.op("act", lambda e: e.copy(out=gup[:], in_=wst[:]), reads=["wst", "wst1"], writes=["gup"])
        for (r0, n, mcol, func, dst, kd) in ((768, 96, 0, AF.Tanh, txw[:, :], "txw"), (864, 96, 1, AF.Copy, xap[:, :], "xap"),
                                             (960, 128, 2, AF.Sigmoid, sxg[:, 0, :], "sxg0"),
                                             (1088, 128, 3, AF.Sigmoid, sxg[:, 1, :], "sxg1")):
            P.dma("sp", PT[0:n, :], K.yT_d[r0:r0 + n, :], writes=["PT"])
            tok_shift(P, KP, PT, KK, mul[0:n, mcol:mcol + 1], "PT", "KK", "KP", n=n)
            P.op("act", lambda e, n=n, func=func, dst=dst: e.activation(out=dst, in_=KP[0:n, :], func=func),
                 reads=["KP"], writes=[kd])
        lk = ["txw", "xap", "sxg0", "sxg1"]
        oc = 0
        for ct in cts:
            c0 = ct * 128
            P.dma("sp", PT[:], K.yT_d[c0:c0 + 128, :], writes=["PT"])
            tok_shift(P, RP, PT, KK, prm[:, 0, ct:ct + 1], "PT", "KK", "RP")
            P.dma("sp", PT[:], K.yT_d[256 + c0:256 + c0 + 128, :], writes=["PT"])
            tok_shift(P, KP, PT, KK, prm[:, 1, ct:ct + 1], "PT", "KK", "KP")
            P.dma("sp", PT[:], K.yT_d[512 + c0:512 + c0 + 128, :], writes=["PT"])
            tok_shift(P, VP, PT, KK, prm[:, 2, ct:ct + 1], "PT", "KK", "VP")
            P.dma("sp", K.vb_d[c0:c0 + 128, :], VP[:], reads=["VP"], writes=[("vb_d", ct)])
            for blk in range(8):
                bs = slice(blk * 512, (blk + 1) * 512)
                p0, p1, p2 = ps[0], ps[1], ps[2]
                P.op("pe", lambda e, bs=bs, c0=c0: e.matmul(ps[0][:, :], lhsT=wup[:, c0:c0 + 128], rhs=txw[:, bs],
                                                             start=True, stop=True), reads=["wup", "txw"], writes=[("ps4", 0)])
                P.op("act", lambda e, bs=bs, ct=ct: e.activation(out=CL[:, bs], in_=ps[0][:, :], func=AF.Sigmoid,
                                                                  bias=prm[:, 3, ct:ct + 1]),
                     reads=[("ps4", 0)] + prk, writes=["CL"])
                P.op("pe", lambda e, bs=bs, c0=c0: e.matmul(ps[1][:, :], lhsT=aup[:, c0:c0 + 128], rhs=xap[:, bs],
                                                             start=True, stop=True), reads=["aup", "xap"], writes=[("ps4", 1)])
                P.op("act", lambda e, bs=bs, ct=ct: e.activation(out=AA[:, bs], in_=ps[1][:, :], func=AF.Sigmoid,
                                                                  bias=prm[:, 4, ct:ct + 1]),
                     reads=[("ps4", 1)] + prk, writes=["AA"])
                for cc in range(2):
                    P.op("pe", lambda e, bs=bs, c0=c0, cc=cc: e.matmul(ps[2][:, :], lhsT=gup[:, cc, c0:c0 + 128],
                                                                       rhs=sxg[:, cc, bs], start=(cc == 0), stop=(cc == 1)),
                         reads=["gup", "sxg0", "sxg1"], writes=[("ps4", 2)])
                o = OUT[oc % 2]
                P.op("dve", lambda e, bs=bs, o=o: e.tensor_copy(out=o[:, bs], in_=ps[2][:, :]),
                     reads=[("ps4", 2)], writes=[("OUT", oc % 2)])
            P.dma("sp", K.G_d[c0:c0 + 128, :], OUT[oc % 2][:], reads=[("OUT", oc % 2)], writes=[("G_d", ct)])
            oc += 1
            P.op("dve", lambda e: e.tensor_scalar(out=CL[:], in0=CL[:], scalar1=-0.6065306597126334, scalar2=None,
                                                   op0=ALU.mult), reads=["CL"], writes=["CL"])
            P.op("dve", lambda e, ct=ct: e.tensor_scalar(out=KK[:], in0=KP[:], scalar1=prm[:, 5, ct:ct + 1], scalar2=None,
                                                          op0=ALU.mult), reads=["KP"] + prk, writes=["KK"])
            P.op("act", lambda e: e.activation(out=SQb[:], in_=KK[:], func=AF.Square), reads=["KK"], writes=["SQb"])
            for blk in range(8):
                bs = slice(blk * 512, (blk + 1) * 512)
                P.op("pe", lambda e, bs=bs: e.matmul(ps[3][:, :], lhsT=bones[:], rhs=SQb[:, bs], start=True, stop=True),
                     reads=["bones", "SQb"], writes=[("ps4", 3)])
                P.op("act", lambda e, bs=bs: e.activation(out=PT[:, bs], in_=ps[3][:, :], func=AF.Sqrt),
                     reads=[("ps4", 3)], writes=["PT"])
            P.op("dve", lambda e: e.tensor_scalar(out=PT[:], in0=PT[:], scalar1=1e-12, scalar2=None, op0=ALU.max),
                 reads=["PT"], writes=["PT"])
            P.op("dve", lambda e: e.reciprocal(out=PT[:], in_=PT[:]), reads=["PT"], writes=["PT"])
            P.op("dve", lambda e: e.tensor_tensor(out=KK[:], in0=KK[:], in1=PT[:], op=ALU.mult), reads=["KK", "PT"], writes=["KK"])
            P.op("dve", lambda e, ct=ct: e.tensor_scalar(out=PT[:], in0=AA[:], scalar1=prm[:, 6, ct:ct + 1],
                                                          scalar2=prm[:, 10, ct:ct + 1], op0=ALU.mult, op1=ALU.add),
                 reads=["AA", "PT"] + prk, writes=["PT"])
            P.op("dve", lambda e: e.tensor_tensor(out=K2[:], in0=KP[:], in1=PT[:], op=ALU.mult), reads=["KP", "PT"], writes=["K2"])
            P.op("dve", lambda e, ct=ct: e.scalar_tensor_tensor(out=SQb[:], in0=RP[:], scalar=prm[:, 7, ct:ct + 1], in1=K2[:],
                                                                 op0=ALU.mult, op1=ALU.mult),
                 reads=["RP", "K2", "SQb"] + prk, writes=["SQb"])
            o = OUT[oc % 2]
            for blk in range(8):
                bs = slice(blk * 512, (blk + 1) * 512)
                P.op("pe", lambda e, bs=bs: e.matmul(ps[3][:, :], lhsT=bones[:], rhs=SQb[:, bs], start=True, stop=True),
                     reads=["bones", "SQb"], writes=[("ps4", 3)])
                P.op("dve", lambda e, bs=bs, o=o: e.tensor_tensor(out=o[:, bs], in0=ps[3][:, :], in1=VP[:, bs], op=ALU.mult),
                     reads=[("ps4", 3), "VP"], writes=[("OUT", oc % 2)])
            P.dma("sp", K.BON_d[c0:c0 + 128, :], o[:], reads=[("OUT", oc % 2)], writes=[("BON_d", ct)])
            oc += 1
            P.op("dve", lambda e: e.tensor_tensor_scan(out=PT[:], data0=M01[:], data1=CL[:], initial=0.0,
                                                        op0=ALU.mult, op1=ALU.add), reads=["M01", "CL", "PT"], writes=["PT"])
            P.op("pool", lambda e: e.tensor_tensor(out=CL[:], in0=PT[:], in1=CL[:], op=ALU.subtract),
                 reads=["PT", "CL"], writes=["CL"])
            P.op("act", lambda e: e.activation(out=CL[:], in_=CL[:], func=AF.Exp), reads=["CL"], writes=["CL"])
            v3 = lambda t: t[:].rearrange("p (c t) -> p c t", t=64)
            o = OUT[oc % 2]
            P.op("dve", lambda e, o=o: e.scalar_tensor_tensor(out=o[:], in0=KK[:], scalar=-1.0, in1=CL[:],
                                                               op0=ALU.mult, op1=ALU.mult),
                 reads=["KK", "CL"], writes=[("OUT", oc % 2)])
            P.dma("sp", K.AH_d[c0:c0 + 128, :], o[:], reads=[("OUT", oc % 2)], writes=[("AH_d", ct)])
            oc += 1
            P.op("act", lambda e: e.activation(out=CL[:], in_=PT[:], func=AF.Exp), reads=["PT", "CL"], writes=["CL"])
            o = OUT[oc % 2]
            P.op("dve", lambda e, o=o: e.tensor_tensor(out=o[:], in0=RP[:], in1=CL[:], op=ALU.mult),
                 reads=["RP", "CL"], writes=[("OUT", oc % 2)])
            P.dma("sp", K.RH_d[c0:c0 + 128, :], o[:], reads=[("OUT", oc % 2)], writes=[("RH_d", ct)])
            oc += 1
            P.op("pool", lambda e: e.tensor_copy(out=PCt[:], in_=v3(CL)[:, :, 63]), reads=["CL"], writes=["PCt"])
            P.dma("sp", K.PC_d[c0:c0 + 128, :], PCt[:], reads=["PCt"], writes=[("PC_d", ct)])
            P.op("act", lambda e: e.activation(out=PT[:], in_=PT[:], func=AF.Exp, scale=-1.0), reads=["PT"], writes=["PT"])
            o = OUT[oc % 2]
            P.op("dve", lambda e, o=o: e.tensor_tensor(out=o[:], in0=K2[:], in1=PT[:], op=ALU.mult),
                 reads=["K2", "PT"], writes=[("OUT", oc % 2)])
            P.dma("sp", K.KH_d[c0:c0 + 128, :], o[:], reads=[("OUT", oc % 2)], writes=[("KH_d", ct)])
            oc += 1
            P.op("dve", lambda e: e.tensor_tensor(out=KK[:], in0=KK[:], in1=AA[:], op=ALU.mult), reads=["KK", "AA"], writes=["KK"])
            o = OUT[oc % 2]
            P.op("dve", lambda e, o=o: e.tensor_tensor(out=o[:], in0=KK[:], in1=PT[:], op=ALU.mult),
                 reads=["KK", "PT"], writes=[("OUT", oc % 2)])
            P.dma("sp", K.BH_d[c0:c0 + 128, :], o[:], reads=[("OUT", oc % 2)], writes=[("BH_d", ct)])
            oc += 1
        P.flush()

def phase4c_rwkv_scan(K, heads=range(4)):
    nc, P = K.nc, K.P
    with contextlib.ExitStack() as st:
        def sb(name, shape, dt):
            return st.enter_context(nc.sbuf_tensor(name, shape, dt))
        ident = sb("ident4", [128, 128], BF16)
        make_ident(K, P, ident)
        MaskG = sb("MaskG", [64, 4, 128], F32)
        MaskX = sb("MaskX", [64, 8, 64], F32)
        I8 = sb("I8", [64, 64], F32)
        ones = sb("ones4", [64, 64], F32)
        P.op("pool", lambda e: e.memset(ones[:], 1.0), writes=["ones"])
        for a in range(4):
            for cq in range(2):
                P.op("pool", lambda e, cq=cq, a=a: e.affine_select(
                    out=MaskG[:, a, cq * 64:(cq + 1) * 64], in_=ones[:], pattern=[[1, 64]],
                    compare_op=(ALU.is_gt if cq == 0 else ALU.is_ge), fill=0.0, base=0, channel_multiplier=-1),
                    reads=["ones"], writes=["MaskG"])
        for a in range(8):
            P.op("pool", lambda e, a=a: e.affine_select(out=MaskX[:, a, :], in_=ones[:], pattern=[[-1, 64]],
                                                         compare_op=ALU.is_gt, fill=0.0, base=0, channel_multiplier=1),
                 reads=["ones"], writes=["MaskX"])
        P.op("dve", lambda e: e.tensor_copy(out=I8[:], in_=ident[0:64, 0:64]), reads=["ident"], writes=["I8"])
        AH = sb("AH", [64, S], RD)
        RH = sb("RH", [64, S], RD)
        BH = sb("BH", [64, S], RD)
        KH = sb("KH", [64, S], RD)
        vb = sb("vb", [64, S], BF16)
        PC = sb("PC", [64, NCH], F32)
        ARh = sb("ARh", [64, NCH, 128], RD)
        BKh = sb("BKh", [64, NCH, 128], RD)
        GmB = sb("GmB", [64, NCH, 128], RD)
        GmK = sb("GmK", [64, NCH, 128], RD)
        Btok = sb("Btok", [64, NCH, 64], RD)
        Ktok = sb("Ktok", [64, NCH, 64], RD)
        Vtok = sb("Vtok", [64, NCH, 64], RD)
        X0 = sb("X0", [64, NCH, 64], RD)
        Pm = sb("Pm", [64, NCH, 64], RD)
        oT = sb("oT", [64, S], F32)
        Ast = sb("Ast", [64, 64], F32)
        Abf = sb("Abf", [64, 64], RD)
        Tt = sb("Tt", [64, 64], F32)
        Xs = sb("Xs", [64, 64], RD)
        Us = sb("Us", [64, 64], RD)
        PSb = st.enter_context(nc.psum_tensor("PSb", [128, 1024], BF16))
        PS = [st.enter_context(nc.psum_tensor("PS%d" % i, [128, 512], F32)) for i in range(7)]
        v3 = lambda t: t[:].rearrange("p (c t) -> p c t", t=64)
        Nb = [v3(AH), v3(RH)]
        Xb = [v3(BH), v3(KH)]
        Nk = ["AH", "RH"]
        Xk = ["BH", "KH"]
        for hd in heads:
            r0 = hd * 64
            P.dma("sp", AH[:], K.AH_d[r0:r0 + 64, :], writes=["AH"])
            P.dma("sp", RH[:], K.RH_d[r0:r0 + 64, :], writes=["RH"])
            P.dma("sp", BH[:], K.BH_d[r0:r0 + 64, :], writes=["BH"])
            P.dma("sp", KH[:], K.KH_d[r0:r0 + 64, :], writes=["KH"])
            P.dma("sp", vb[:], K.vb_d[r0:r0 + 64, :], writes=["vb"])
            P.dma("sp", PC[:], K.PC_d[r0:r0 + 64, :], writes=["PC"])
            P.op("dve", lambda e: e.tensor_copy(out=ARh[:, :, 0:64], in_=v3(AH)), reads=["AH"], writes=["ARh"])
            P.op("pool", lambda e: e.tensor_copy(out=ARh[:, :, 64:128], in_=v3(RH)), reads=["RH"], writes=["ARh"])
            P.op("dve", lambda e: e.tensor_copy(out=BKh[:, :, 0:64], in_=v3(BH)), reads=["BH"], writes=["BKh"])
            P.op("pool", lambda e: e.tensor_copy(out=BKh[:, :, 64:128], in_=v3(KH)), reads=["KH"], writes=["BKh"])
            for (src, srck, col0, dst, dk) in ((BKh, "BKh", 0, Btok, "Btok"), (BKh, "BKh", 64, Ktok, "Ktok"), (None, "vb", 0, Vtok, "Vtok")):
                for c16 in range(0, NCH, 16):
                    for cc in range(16):
                        c = c16 + cc
                        in_ = vb[:, c * 64:(c + 1) * 64] if src is None else src[:, c, col0:col0 + 64]
                        P.op("pe", lambda e, cc=cc, in_=in_: e.transpose(out=PSb[0:64, cc * 64:(cc + 1) * 64], in_=in_,
                                                                         identity=ident[0:64, 0:64]),
                             reads=[srck, "ident"], writes=["PSb"])
                    P.op("act", lambda e, c16=c16, dst=dst: e.copy(out=dst[:, c16:c16 + 16, :].rearrange("p c k -> p (c k)"),
                                                                    in_=PSb[0:64, :]), reads=["PSb"], writes=[dk])
            gi = 0
            for (col0, dst, dk) in ((0, GmB, "GmB"), (64, GmK, "GmK")):
                for c4 in range(0, NCH, 4):
                    b = gi % 2
                    gi += 1
                    for cc in range(4):
                        c = c4 + cc
                        P.op("pe", lambda e, c=c, cc=cc, b=b, col0=col0: e.matmul(
                            PS[b][0:64, cc * 128:(cc + 1) * 128], lhsT=BKh[:, c, col0:col0 + 64], rhs=ARh[:, c, :],
                            start=True, stop=True), reads=["BKh", "ARh"], writes=[("PS", b)])
                    P.op("dve", lambda e, c4=c4, dst=dst, b=b: e.tensor_tensor(
                        out=dst[:, c4:c4 + 4, :], in0=PS[b][0:64, :].rearrange("p (a t) -> p a t", t=128), in1=MaskG[:],
                        op=ALU.mult), reads=[("PS", b), "MaskG"], writes=[dk])
            for c8 in range(0, NCH, 8):
                for cc in range(8):
                    c = c8 + cc
                    P.op("pe", lambda e, c=c, cc=cc: e.matmul(PS[2][0:64, cc * 64:(cc + 1) * 64], lhsT=ARh[:, c, 0:64],
                                                               rhs=BKh[:, c, 0:64], start=True, stop=True),
                         reads=["ARh", "BKh"], writes=[("PS", 2)])
                P.op("dve", lambda e, c8=c8: e.tensor_tensor(
                    out=X0[:, c8:c8 + 8, :], in0=PS[2][0:64, :].rearrange("p (a t) -> p a t", t=64), in1=MaskX[:],
                    op=ALU.mult), reads=[("PS", 2), "MaskX"], writes=["X0"])
            N0 = GmB[:, :, 0:64]
            P.op("pool", lambda e, N0=N0: e.tensor_tensor(out=Pm[:], in0=N0, in1=I8[:].unsqueeze(1).to_broadcast([64, NCH, 64]),
                                                          op=ALU.add), reads=["GmB", "I8"], writes=["Pm"])
            curN, curNk = N0, "GmB"
            curX, curXk = X0[:], "X0"
            for lvl in range(1, 6):
                nX, nXk = Xb[lvl % 2], Xk[lvl % 2]
                nN, nNk = Nb[lvl % 2], Nk[lvl % 2]
                for c8 in range(0, NCH, 8):
                    for cc in range(8):
                        c = c8 + cc
                        P.op("pe", lambda e, c=c, cc=cc, curN=curN, curX=curX: e.matmul(
                            PS[3][0:64, cc * 64:(cc + 1) * 64], lhsT=curN[:, c, :], rhs=curX[:, c, :], start=True, stop=True),
                            reads=[curNk, curXk], writes=[("PS", 3)])
                    P.op("act", lambda e, c8=c8, nX=nX: e.copy(out=nX[:, c8:c8 + 8, :],
                                                                in_=PS[3][0:64, :].rearrange("p (a t) -> p a t", t=64)),
                         reads=[("PS", 3)], writes=[nXk])
                    if lvl < 5:
                        for cc in range(8):
                            c = c8 + cc
                            P.op("pe", lambda e, c=c, cc=cc, curN=curN, curX=curX: e.matmul(
                                PS[4][0:64, cc * 64:(cc + 1) * 64], lhsT=curX[:, c, :], rhs=curN[:, c, :], start=True, stop=True),
                                reads=[curNk, curXk], writes=[("PS", 4)])
                        P.op("dve", lambda e, c8=c8, nN=nN: e.tensor_copy(out=nN[:, c8:c8 + 8, :],
                                                                           in_=PS[4][0:64, :].rearrange("p (a t) -> p a t", t=64)),
                             reads=[("PS", 4)], writes=[nNk])
                    for cc in range(8):
                        c = c8 + cc
                        P.op("pe", lambda e, c=c, cc=cc, nX=nX: e.matmul(
                            PS[5][0:64, cc * 64:(cc + 1) * 64], lhsT=nX[:, c, :], rhs=Pm[:, c, :], start=True, stop=True),
                            reads=[nXk, "Pm"], writes=[("PS", 5)])
                    P.op("dve", lambda e, c8=c8: e.tensor_tensor(
                        out=Pm[:, c8:c8 + 8, :], in0=PS[5][0:64, :].rearrange("p (a t) -> p a t", t=64),
                        in1=Pm[:, c8:c8 + 8, :], op=ALU.add), reads=[("PS", 5), "Pm"], writes=["Pm"])
                curN, curNk, curX, curXk = nN, nNk, nX, nXk
            P.op("pool", lambda e: e.memset(Ast[:], 0.0), writes=["Ast"])
            P.op("pool", lambda e: e.memset(Abf[:], 0.0), writes=["Abf"])
            for c in range(NCH):
                P.op("pool", lambda e, c=c: e.tensor_scalar(out=Tt[:], in0=Ast[:], scalar1=PC[:, c:c + 1], scalar2=0.0,
                                                             op0=ALU.mult, op1=ALU.add), reads=["Ast", "PC"], writes=["Tt"])
                P.op("pe", lambda e, c=c: e.matmul(PS[0][0:64, 0:64], lhsT=ARh[:, c, 0:64], rhs=Abf[:], start=True, stop=False),
                     reads=["ARh", "Abf"], writes=[("PS", 0)])
                P.op("pe", lambda e, c=c: e.matmul(PS[0][0:64, 0:64], lhsT=GmK[:, c, 0:64], rhs=Vtok[:, c, :], start=False, stop=True),
                     reads=["GmK", "Vtok"], writes=[("PS", 0)])
                P.op("act", lambda e: e.copy(out=Xs[:], in_=PS[0][0:64, 0:64]), reads=[("PS", 0)], writes=["Xs"])
                P.op("pe", lambda e, c=c: e.matmul(PS[1][0:64, 0:64], lhsT=Pm[:, c, :], rhs=Xs[:], start=True, stop=True),
                     reads=["Pm", "Xs"], writes=[("PS", 1)])
                P.op("dve", lambda e: e.tensor_copy(out=Us[:], in_=PS[1][0:64, 0:64]), reads=[("PS", 1)], writes=["Us"])
                P.op("pe", lambda e, c=c: e.matmul(PS[6][0:64, 0:64], lhsT=Btok[:, c, :], rhs=Us[:], start=True, stop=False),
                     reads=["Btok", "Us"], writes=[("PS", 6)])
                P.op("pe", lambda e, c=c: e.matmul(PS[6][0:64, 0:64], lhsT=Ktok[:, c, :], rhs=Vtok[:, c, :], start=False, stop=True),
                     reads=["Ktok", "Vtok"], writes=[("PS", 6)])
                ob = 2 + (c % 2)
                P.op("pe", lambda e, c=c, ob=ob: e.matmul(PS[ob][0:64, 0:64], lhsT=Abf[:], rhs=ARh[:, c, 64:128], start=True, stop=False),
                     reads=["Abf", "ARh"], writes=[("PS", ob)])
                P.op("pe", lambda e, c=c, ob=ob: e.matmul(PS[ob][0:64, 0:64], lhsT=Us[:], rhs=GmB[:, c, 64:128], start=False, stop=False),
                     reads=["Us", "GmB"], writes=[("PS", ob)])
                P.op("pe", lambda e, c=c, ob=ob: e.matmul(PS[ob][0:64, 0:64], lhsT=Vtok[:, c, :], rhs=GmK[:, c, 64:128], start=False, stop=True),
                     reads=["Vtok", "GmK"], writes=[("PS", ob)])
                P.op("dve", lambda e, c=c: e.scalar_tensor_tensor(out=Abf[:], in0=PS[6][0:64, 0:64], scalar=PC[:, c:c + 1], in1=Tt[:],
                                                                   op0=ALU.mult, op1=ALU.add),
                     reads=[("PS", 6), "Tt", "PC"], writes=["Abf"])
                P.op("dve", lambda e, c=c: e.scalar_tensor_tensor(out=Ast[:], in0=PS[6][0:64, 0:64], scalar=PC[:, c:c + 1], in1=Tt[:],
                                                                   op0=ALU.mult, op1=ALU.add),
                     reads=[("PS", 6), "Tt", "PC"], writes=["Ast"])
                P.op("act", lambda e, c=c, ob=ob: e.copy(out=oT[:, c * 64:(c + 1) * 64], in_=PS[ob][0:64, 0:64]),
                     reads=[("PS", ob)], writes=[("oT", c // 8)])
            P.dma("sp", K.oT_d[r0:r0 + 64, :], oT[:], reads=[("oT", q) for q in range(8)], writes=[("oT_d", hd)])
        P.flush()

def phase4d_rwkv_post(K, cts=range(2)):
    nc, P = K.nc, K.P
    with contextlib.ExitStack() as st:
        def sb(name, shape, dt):
            return st.enter_context(nc.sbuf_tensor(name, shape, dt))
        ident = sb("ident4d", [128, 128], BF16)
        make_ident(K, P, ident)
        bonesf = sb("bonesf", [128, 128], F32)
        P.op("pool", lambda e: e.memset(bonesf[:], 0.0), writes=["bonesf"])
        P.op("pool", lambda e: e.memset(bonesf[0:64, 0:64], 1.0), reads=["bonesf"], writes=["bonesf"])
        P.op("pool", lambda e: e.memset(bonesf[64:128, 64:128], 1.0), reads=["bonesf"], writes=["bonesf"])
        prm = sb("prm4d", [128, 2, 2], F32)
        P.dma("sp", prm[:, 0, :], K.rw_prm[8], writes=["prm0"])
        P.dma("sp", prm[:, 1, :], K.rw_prm[9], writes=["prm1"])
        o = sb("o4d", [128, S], F32)
        osq = sb("osq", [128, S], F32)
        bon = sb("bon", [128, S], BF16)
        gg = sb("gg", [128, S], BF16)
        Mb = [sb("Mb%d" % i, [128, 512], F32) for i in range(2)]
        Vb = [sb("Vb%d" % i, [128, 512], F32) for i in range(2)]
        Yb = [sb("Yb%d" % i, [128, 512], F32) for i in range(2)]
        Ob = [sb("Ob%d" % i, [128, 512], BF16) for i in range(2)]
        Tk = [sb("Tk%d" % i, [128, 4, 128], BF16) for i in range(2)]
        ps = [st.enter_context(nc.psum_tensor("p4d_%d" % i, [128, 512], F32)) for i in range(4)]
        pst = [st.enter_context(nc.psum_tensor("p4dt_%d" % i, [128, 4, 128], BF16)) for i in range(2)]
        it = 0
        for ct in cts:
            c0 = ct * 128
            P.dma("sp", o[:], K.oT_d[c0:c0 + 128, :], writes=["o"])
            P.dma("sp", bon[:], K.BON_d[c0:c0 + 128, :], writes=["bon"])
            P.dma("sp", gg[:], K.G_d[c0:c0 + 128, :], writes=["gg"])
            P.op("act", lambda e: e.activation(out=osq[:], in_=o[:], func=AF.Square), reads=["o"], writes=["osq"])
            for blk in range(8):
                s2 = it % 2
                it += 1
                bs = slice(blk * 512, (blk + 1) * 512)
                P.op("pe", lambda e, bs=bs, s2=s2: e.matmul(ps[s2][:, :], lhsT=bonesf[:], rhs=o[:, bs], start=True, stop=True),
                     reads=["bonesf", "o"], writes=[("p4d", s2)])
                P.op("pe", lambda e, bs=bs, s2=s2: e.matmul(ps[2 + s2][:, :], lhsT=bonesf[:], rhs=osq[:, bs], start=True, stop=True),
                     reads=["bonesf", "osq"], writes=[("p4d", 2 + s2)])
                P.op("act", lambda e, s2=s2: e.activation(out=Mb[s2][:], in_=ps[s2][:, :], func=AF.Copy, scale=1.0 / 64),
                     reads=[("p4d", s2)], writes=[("Mb", s2)])
                P.op("pool", lambda e, s2=s2: e.tensor_tensor(out=Vb[s2][:], in0=Mb[s2][:], in1=Mb[s2][:], op=ALU.mult),
                     reads=[("Mb", s2)], writes=[("Vb", s2)])
                P.op("dve", lambda e, s2=s2: e.scalar_tensor_tensor(out=Vb[s2][:], in0=ps[2 + s2][:, :], scalar=1.0 / 64, in1=Vb[s2][:],
                                                                     op0=ALU.mult, op1=ALU.subtract),
                     reads=[("p4d", 2 + s2), ("Vb", s2)], writes=[("Vb", s2)])
                P.op("dve", lambda e, s2=s2: e.tensor_scalar(out=Vb[s2][:], in0=Vb[s2][:], scalar1=64e-5, scalar2=None, op0=ALU.add),
                     reads=[("Vb", s2)], writes=[("Vb", s2)])
                P.op("act", lambda e, s2=s2: e.activation(out=Vb[s2][:], in_=Vb[s2][:], func=AF.Sqrt),
                     reads=[("Vb", s2)], writes=[("Vb", s2)])
                P.op("dve", lambda e, s2=s2: e.reciprocal(out=Vb[s2][:], in_=Vb[s2][:]), reads=[("Vb", s2)], writes=[("Vb", s2)])
                P.op("pool", lambda e, s2=s2, bs=bs: e.tensor_tensor(out=Yb[s2][:], in0=o[:, bs], in1=Mb[s2][:], op=ALU.subtract),
                     reads=["o", ("Mb", s2)], writes=[("Yb", s2)])
                P.op("dve", lambda e, s2=s2: e.tensor_tensor(out=Yb[s2][:], in0=Yb[s2][:], in1=Vb[s2][:], op=ALU.mult),
                     reads=[("Yb", s2), ("Vb", s2)], writes=[("Yb", s2)])
                P.op("dve", lambda e, s2=s2, ct=ct: e.tensor_scalar(out=Yb[s2][:], in0=Yb[s2][:], scalar1=prm[:, 0, ct:ct + 1],
                                                                     scalar2=prm[:, 1, ct:ct + 1], op0=ALU.mult, op1=ALU.add),
                     reads=[("Yb", s2), "prm0", "prm1"], writes=[("Yb", s2)])
                P.op("pool", lambda e, s2=s2, bs=bs: e.tensor_tensor(out=Yb[s2][:], in0=Yb[s2][:], in1=bon[:, bs], op=ALU.add),
                     reads=[("Yb", s2), "bon"], writes=[("Yb", s2)])
                P.op("dve", lambda e, s2=s2, bs=bs: e.tensor_tensor(out=Ob[s2][:], in0=Yb[s2][:], in1=gg[:, bs], op=ALU.mult),
                     reads=[("Yb", s2), "gg"], writes=[("Ob", s2)])
                for q in range(4):
                    P.op("pe", lambda e, s2=s2, q=q: e.transpose(out=pst[s2][:, q, :], in_=Ob[s2][:, q * 128:(q + 1) * 128],
                                                                 identity=ident[:]),
                         reads=[("Ob", s2), "ident"], writes=[("p4dt", s2)])
                P.op("act", lambda e, s2=s2: e.copy(out=Tk[s2][:], in_=pst[s2][:]), reads=[("p4dt", s2)], writes=[("Tk", s2)])
                P.dma("sp", K.ro_loc_d[blk // 4].rearrange("(t p) c -> p t c", p=128)[:, (blk % 4) * 4:(blk % 4 + 1) * 4, c0:c0 + 128], Tk[s2][:],
                      reads=[("Tk", s2)], writes=[("ro_tok_d", ct, blk)])
        P.flush()


def phase4e_allgather(K):
    P = K.P
    for hh in range(2):
        P.coll(lambda e, hh=hh: e.collective_compute("AllGather", ALU.bypass, replica_groups=[[0, 1, 2, 3], [4, 5, 6, 7]],
                                                     ins=[K.ro_loc_d[hh].opt()], outs=[K.ro_all_d[hh].opt()]),
               reads=[("ro_loc", hh)], writes=[("ro_all", hh)])
    P.flush()


def phase5a_select(K):
    nc, P = K.nc, K.P
    with contextlib.ExitStack() as st:
        def sb(name, shape, dt):
            return st.enter_context(nc.sbuf_tensor(name, shape, dt))
        ro = sb("ro_tok", [128, 32, 1024], BF16)
        selT = sb("selT", [128, 32, 1024], BF16)
        qrow = sb("qrow", [128, 1024], F32)
        tki = sb("tki", [128, 32], I32)
        tkf = sb("tkf", [128, 32], F32)
        mo = [sb("mo%d" % i, [128, 512], BF16) for i in range(2)]
        at = sb("at5", [128, 8, 1024], BF16)
        ps = [st.enter_context(nc.psum_tensor("p5a_%d" % i, [128, 512], F32)) for i in range(2)]
        for q4 in range(4):
            for hh in range(2):
                P.dma("sp", ro[:, hh * 16:(hh + 1) * 16, q4 * 256:(q4 + 1) * 256],
                      K.ro_all_d[hh][q4 * 2048:(q4 + 1) * 2048, :].rearrange("(t p) c -> p t c", p=128), writes=[("ro", q4, hh)])
        rok = [("ro", q4, hh) for q4 in range(4) for hh in range(2)]
        P.dma("sp", qrow[:], bcast_rows(K.qpos_row, 1024), writes=["qrow"])
        P.op("pool", lambda e: e.iota(tki[:], pattern=[[128, 32]], base=0, channel_multiplier=1), writes=["tki"])
        P.op("dve", lambda e: e.tensor_copy(out=tkf[:], in_=tki[:]), reads=["tki"], writes=["tkf"])
        for T in range(32):
            P.op("dve", lambda e, T=T: e.tensor_scalar(out=selT[:, T, :], in0=qrow[:], scalar1=tkf[:, T:T + 1], scalar2=0.0,
                                                      op0=ALU.is_equal, op1=ALU.add), reads=["qrow", "tkf"], writes=[("selT", T)])
        sk = [("selT", T) for T in range(32)]
        P.dma("sp", at[:], K.attT_d.rearrange("h p t -> p h t"), writes=["at5"])
        P.dma("sp", K.mixT_d.rearrange("k p t -> p k t")[:, 0:8, :], at[:], reads=["at5"], writes=["mixa"])
        i = 0
        for m in range(8):
            for half in range(2):
                s2 = i % 2
                i += 1
                for T in range(32):
                    P.op("pe", lambda e, T=T, m=m, half=half, s2=s2: e.matmul(
                        ps[s2][:, :], lhsT=ro[:, T, m * 128:(m + 1) * 128], rhs=selT[:, T, half * 512:(half + 1) * 512],
                        start=(T == 0), stop=(T == 31)), reads=rok + sk, writes=[("p5a", s2)])
                P.op("act", lambda e, s2=s2: e.copy(out=mo[s2][:], in_=ps[s2][:, :]), reads=[("p5a", s2)], writes=[("mo", s2)])
                P.dma("sp", K.mixT_d[8 + m, :, half * 512:(half + 1) * 512], mo[s2][:], reads=[("mo", s2)], writes=[("mixr", m, half)])
        P.flush()


def phase5b_outproj(K):
    nc, P = K.nc, K.P
    with contextlib.ExitStack() as st:
        def sb(name, shape, dt):
            return st.enter_context(nc.sbuf_tensor(name, shape, dt))
        ident = sb("ident5", [128, 128], BF16)
        make_ident(K, P, ident)
        G2, SH2 = load_G_SH(K, P, st, 3, 4, K.norm2_g, "p5")
        GT1 = sb("GT1", [128, D], F32)
        P.dma("sp", GT1[:], bcast_rows(K.mod_d[2 * D:3 * D], D), writes=["GT1"])
        Wo = sb("Wo", [128, 16, D], BF16)
        stg = [sb("wstg5_%d" % i, [128, 4, 512], F32) for i in range(2)]
        wk = load_weight_bf16(K, P, stg, Wo, 0, K.w_out, D, "Wo")
        mixT = sb("mixT", [128, 16, 512], BF16)
        T = norm_tiles_alloc(K, st, "p5")
        x1 = T["xt"]
        hT = [sb("hT5_0", [128, 16, 512], BF16)] * 2
        xo = [sb("xo%d" % i, [128, D], F32) for i in range(2)]
        ps = [st.enter_context(nc.psum_tensor("p5b_%d" % i, [128, 512], F32)) for i in range(2)]
        ss, junk, hb, pT = T["ss"], T["junk"], T["hb"], T["pT"]
        gi = 0
        for blk in range(2):
            hs = 0
            P.dma("sp", mixT[:], K.mixT_d.rearrange("k p t -> p k t")[:, :, blk * 512:(blk + 1) * 512], writes=["mixT"])
            for ti in range(4):
                t = blk * 4 + ti
                xs = t % 2
                P.dma("sp", xo[xs][:], K.x_own[t * 128:(t + 1) * 128, :], writes=[("xo", xs)])
                for cg in range(4):
                    b = gi % 2
                    gi += 1
                    for k in range(16):
                        P.op("pe", lambda e, b=b, k=k, t=t, cg=cg: e.matmul(
                            ps[b][:, :], lhsT=mixT[:, k, (t % 4) * 128:(t % 4 + 1) * 128], rhs=Wo[:, k, cg * 512:(cg + 1) * 512],
                            start=(k == 0), stop=(k == 15)), reads=["mixT"] + wk, writes=[("p5b", b)])
                    cs = slice(cg * 512, (cg + 1) * 512)
                    P.op("dve", lambda e, b=b, xs=xs, cs=cs: e.tensor_tensor(out=x1[xs][:, cs], in0=ps[b][:, :], in1=GT1[:, cs], op=ALU.mult),
                         reads=[("p5b", b), "GT1"], writes=[("xt", xs)])
                    P.op("pool", lambda e, xs=xs, cs=cs: e.tensor_tensor(out=x1[xs][:, cs], in0=x1[xs][:, cs], in1=xo[xs][:, cs], op=ALU.add),
                         reads=[("xt", xs), ("xo", xs)], writes=[("xt", xs)])
                P.dma("sp", K.x1_d[t * 128:(t + 1) * 128, :], x1[xs][:], reads=[("xt", xs)], writes=[("x1_d", t)])
                P.op("act", lambda e, xs=xs: e.activation(out=junk[:], in_=x1[xs][:], func=AF.Square, accum_out=ss[:, 0:1]),
                     reads=[("xt", xs)], writes=["junk", "ss0"])
                P.op("dve", lambda e: e.tensor_scalar(out=ss[:, 1:2], in0=ss[:, 0:1], scalar1=1.0 / D, scalar2=1e-6,
                                                       op0=ALU.mult, op1=ALU.add), reads=["ss0"], writes=["ss1"])
                P.op("act", lambda e: e.activation(out=ss[:, 2:3], in_=ss[:, 1:2], func=AF.Sqrt), reads=["ss1"], writes=["ss2"])
                P.op("dve", lambda e: e.reciprocal(out=ss[:, 3:4], in_=ss[:, 2:3]), reads=["ss2"], writes=["ss3"])
                P.op("dve", lambda e, xs=xs: e.scalar_tensor_tensor(out=x1[xs][:], in0=x1[xs][:], scalar=ss[:, 3:4], in1=G2[:],
                                                                   op0=ALU.mult, op1=ALU.mult),
                     reads=[("xt", xs), "ss3", "G"], writes=[("xt", xs)])
                P.op("pool", lambda e, xs=xs: e.tensor_tensor(out=hb[xs][:], in0=x1[xs][:], in1=SH2[:], op=ALU.add),
                     reads=[("xt", xs), "SH"], writes=[("hb", xs)])
                for half in range(2):
                    for kk in range(8):
                        k = half * 8 + kk
                        P.op("pe", lambda e, k=k, kk=kk, half=half, xs=xs: e.transpose(
                            out=pT[half][:, kk, :], in_=hb[xs][:, k * 128:(k + 1) * 128], identity=ident[:]),
                            reads=[("hb", xs), "ident"], writes=[("pT", half)])
                    o_ = hT[hs][:, half * 8:(half + 1) * 8, ti * 128:(ti + 1) * 128]
                    if half == 0:
                        P.op("act", lambda e, o_=o_, half=half: e.copy(out=o_, in_=pT[half][:]), reads=[("pT", half)], writes=[("hT5", hs, ti, half)])
                    else:
                        P.op("dve", lambda e, o_=o_, half=half: e.tensor_copy(out=o_, in_=pT[half][:]), reads=[("pT", half)], writes=[("hT5", hs, ti, half)])
            P.dma("sp", K.h2T_d.rearrange("k p t -> p k t")[:, :, blk * 512:(blk + 1) * 512], hT[hs][:],
                  reads=[("hT5", hs, ti, half) for ti in range(4) for half in range(2)], writes=[("h2T_d", blk)])
        P.flush()


def phase5c_ffn(K):
    nc, P = K.nc, K.P
    NF = 5632 // 128
    with contextlib.ExitStack() as st:
        def sb(name, shape, dt):
            return st.enter_context(nc.sbuf_tensor(name, shape, dt))
        h2T = sb("h2T", [128, 16, OWN], BF16)
        P.dma("sp", h2T[:], K.h2T_d.rearrange("k p t -> p k t"), writes=["h2T"])
        ao = [sb("ao%d" % i, [128, 512], BF16) for i in range(2)]
        stg = [sb("wstg6_%d" % i, [128, 4, 512], F32) for i in range(4)]
        Wg = [sb("Wg%d" % i, [128, 16, 512], BF16) for i in range(2)]
        Wu = [sb("Wu%d" % i, [128, 16, 512], BF16) for i in range(2)]
        sg = [sb("sg%d" % i, [128, 512], F32) for i in range(2)]
        ps = [st.enter_context(nc.psum_tensor("p5c_%d" % i, [128, 512], F32)) for i in range(4)]
        gi = 0

        def load_group(fg, defer=None):
            ws = fg % 2
            load_weight_bf16(K, P, stg, Wg[ws], 0, K.w_ffn_gate[:, fg * 512:(fg + 1) * 512], 512, ("Wg", ws), defer=defer)
            load_weight_bf16(K, P, stg, Wu[ws], 0, K.w_ffn_up[:, fg * 512:(fg + 1) * 512], 512, ("Wu", ws), defer=defer)
        load_group(0)
        for fg in range(11):
            ws = fg % 2
            pend = []
            if fg + 1 < 11:
                load_group(fg + 1, defer=pend)
            for f4 in range(4):
                f = fg * 4 + f4
                for tb in range(2):
                    b = gi % 2
                    gi += 1
                    if pend:
                        pend.pop(0)()
                    for k in range(16):
                        P.op("pe", lambda e, b=b, k=k, f4=f4, tb=tb, ws=ws: e.matmul(
                            ps[b][:, :], lhsT=Wg[ws][:, k, f4 * 128:(f4 + 1) * 128], rhs=h2T[:, k, tb * 512:(tb + 1) * 512],
                            start=(k == 0), stop=(k == 15)), reads=["h2T", (("Wg", ws), 0, (k // 4) * 4)], writes=[("p5c", b)])
                    for k in range(16):
                        P.op("pe", lambda e, b=b, k=k, f4=f4, tb=tb, ws=ws: e.matmul(
                            ps[2 + b][:, :], lhsT=Wu[ws][:, k, f4 * 128:(f4 + 1) * 128], rhs=h2T[:, k, tb * 512:(tb + 1) * 512],
                            start=(k == 0), stop=(k == 15)), reads=["h2T", (("Wu", ws), 0, (k // 4) * 4)], writes=[("p5c", 2 + b)])
                    P.op("act", lambda e, b=b: e.activation(out=sg[b][:], in_=ps[b][:, :], func=AF.Silu),
                         reads=[("p5c", b)], writes=[("sg", b)])
                    P.op("dve", lambda e, b=b: e.tensor_tensor(out=ao[b][:], in0=ps[2 + b][:, :], in1=sg[b][:], op=ALU.mult),
                         reads=[("p5c", 2 + b), ("sg", b)], writes=[("ao", b)])
                    P.dma("sp", K.actT_d[f, :, tb * 512:(tb + 1) * 512], ao[b][:], reads=[("ao", b)], writes=[("actT_d", f, tb)])
        P.flush()
    with contextlib.ExitStack() as st:
        def sb(name, shape, dt):
            return st.enter_context(nc.sbuf_tensor(name, shape, dt))
        GT2 = sb("GT2", [128, D], F32)
        P.dma("sp", GT2[:], bcast_rows(K.mod_d[5 * D:6 * D], D), writes=["GT2"])
        actT = sb("actT", [128, NF, OWN], BF16)
        for q in range(4):
            P.dma("sp", actT[:, q * 11:(q + 1) * 11, :], K.actT_d.rearrange("f p t -> p f t")[:, q * 11:(q + 1) * 11, :], writes=[("actT", q)])
        ak = [("actT", q) for q in range(4)]
        stg = [sb("wstg7_%d" % i, [128, 4, 256], F32) for i in range(4)]
        ps = [st.enter_context(nc.psum_tensor("p5d_%d" % i, [128, 512], F32)) for i in range(2)]
        gi = 0
        Wd = [sb("Wd%d" % i, [128, NF, 256], BF16) for i in range(2)]
        x1 = [sb("x1_%d" % i, [128, 256], F32) for i in range(2)]
        yo = [sb("yo%d" % i, [128, 256], F32) for i in range(2)]
        wdv = K.w_ffn_down.rearrange("(k p) n -> p k n", p=128)
        engs = ["pool", "dve", "act"]

        def load_wd(cg, defer=None):
            wsl = cg % 2
            for k0 in range(0, NF, 4):
                if defer is not None:
                    defer.append(lambda k0=k0: load_wd_piece(cg, wsl, k0))
                else:
                    load_wd_piece(cg, wsl, k0)

        def load_wd_piece(cg, wsl, k0):
            if True:
                i = K.wcnt
                K.wcnt += 1
                sl = i % 4
                P.dma("sp", stg[sl][:, 0:4, 0:256], wdv[:, k0:k0 + 4, cg * 256:(cg + 1) * 256], writes=[("wstg", sl)])
                eng = engs[i % 3]
                o_ = Wd[wsl][:, k0:k0 + 4, :]
                if eng == "act":
                    P.op("act", lambda e, o_=o_, sl=sl: e.copy(out=o_, in_=stg[sl][:, 0:4, 0:256]), reads=[("wstg", sl)], writes=[("Wd", wsl, k0)])
                else:
                    P.op(eng, lambda e, o_=o_, sl=sl: e.tensor_copy(out=o_, in_=stg[sl][:, 0:4, 0:256]), reads=[("wstg", sl)], writes=[("Wd", wsl, k0)])
        load_wd(0)
        for cg in range(8):
            wsl = cg % 2
            cs = slice(cg * 256, (cg + 1) * 256)
            pend = []
            if cg + 1 < 8:
                load_wd(cg + 1, defer=pend)
            for t in range(8):
                b = gi % 2
                gi += 1
                for _ in range(2):
                    if pend:
                        pend.pop(0)()
                P.dma("sp", x1[b][:], K.x1_d[t * 128:(t + 1) * 128, cs], writes=[("x1", b)])
                for f in range(NF):
                    P.op("pe", lambda e, b=b, f=f, t=t, wsl=wsl: e.matmul(ps[b][:, 0:256], lhsT=actT[:, f, t * 128:(t + 1) * 128], rhs=Wd[wsl][:, f, :],
                                                                          start=(f == 0), stop=(f == NF - 1)),
                         reads=[("actT", f // 11), ("Wd", wsl, (f // 4) * 4)], writes=[("p5c", b)])
                P.op("dve", lambda e, b=b, cs=cs: e.tensor_tensor(out=yo[b][:], in0=ps[b][:, 0:256], in1=GT2[:, cs], op=ALU.mult),
                     reads=[("p5c", b), "GT2"], writes=[("yo", b)])
                P.op("pool", lambda e, b=b: e.tensor_tensor(out=yo[b][:], in0=yo[b][:], in1=x1[b][:], op=ALU.add),
                     reads=[("yo", b), ("x1", b)], writes=[("yo", b)])
                P.dma("sp", K.out[t * 128:(t + 1) * 128, cs], yo[b][:], reads=[("yo", b)], writes=[("out", t, cg)])
        P.flush()


def phase_final_copy(K):
    nc, P = K.nc, K.P
    with contextlib.ExitStack() as st:
        xt = [st.enter_context(nc.sbuf_tensor("fx%d" % i, [128, D], F32)) for i in range(2)]
        for t in range(8):
            s = t % 2
            P.dma("sp", xt[s][:], K.x_own[t * 128:(t + 1) * 128, :], writes=[("fx", s)])
            P.dma("sp", K.out[t * 128:(t + 1) * 128, :], xt[s][:], reads=[("fx", s)], writes=[("out", t)])
        P.flush()


def own_tiles(j):
    r = []
    for m in range(4):
        r += [8 * m + j, 8 * m + 7 - j]
    return r


def build_program(debug=False, stages=99, cts=range(2), dbg_list=None, skip_att=False):
    nc = bass.Bass("TRN2", target_bir_lowering=False)
    K = Ctx()
    K.stages = stages
    K.cts = cts
    K.skip_att = skip_att
    K.nc = nc
    K.dbg = {}
    K.wcnt = 0

    def inp(name, shape, dt=F32):
        return nc.dram_tensor(name, list(shape), dt, kind="ExternalInput").ap()

    def scratch(name, shape, dt):
        return nc.dram_tensor(name, list(shape), dt, kind="Internal").ap()

    K.x_full = inp("x_full", [S, D])
    K.x_own = inp("x_own", [OWN, D])
    K.c_arr = inp("c_arr", [128, 16])
    K.pos_full = inp("pos_full", [128, 32], I32)
    K.invf_att = inp("invf_att", [128, 16])
    K.invf_idx = inp("invf_idx", [128, 8])
    K.w_ada = inp("w_ada", [D, 3072])
    K.b_ada = inp("b_ada", [3072])
    K.norm1_g = inp("norm1_g", [D])
    K.k_norm_g = inp("k_norm_g", [128])
    K.q_norm_g = inp("q_norm_g", [128])
    K.pos_own = inp("pos_own", [128, 8], I32)
    K.qpos_own = inp("qpos_own", [128, 8])
    K.w_in = inp("w_in", [D, 4176])
    K.rw_prm = [inp("rwp%d" % i, [128, 2]) for i in range(10)]
    K.w_in_rw = inp("w_in_rw", [D, 1216])
    K.rw_mul = inp("rw_mul", [128, 4])
    K.rw_w_up = inp("rw_w_up", [96, 256])
    K.rw_a_up = inp("rw_a_up", [96, 256])
    K.rw_g_up = inp("rw_g_up", [256, 256])
    K.qpos_row = inp("qpos_row", [OWN])
    K.w_out = inp("w_out", [D, D])
    K.norm2_g = inp("norm2_g", [D])
    K.w_ffn_gate = inp("w_ffn_gate", [D, 5632])
    K.w_ffn_up = inp("w_ffn_up", [D, 5632])
    K.w_ffn_down = inp("w_ffn_down", [5632, D])
    K.out = nc.dram_tensor("y_own", [OWN, D], F32, kind="ExternalOutput").ap()
    K.modq_d = scratch("modq_d", [1, 3072], F32)
    K.mod4_d = scratch("mod4_d", [4, 3072], F32)
    K.mod_d = K.mod4_d.rearrange("a n -> (a n)")
    K.hT_d = scratch("hT_d", [16, 128, S], BF16)
    K.kT_d = scratch("kT_d", [8, 128, S], BF16)
    K.v_d = scratch("v_d", [S, 8 * 129], BF16)
    K.ikT_d = scratch("ikT_d", [64, S], BF16)
    K.yT_d = scratch("yT_d", [1216, S], F32)
    K.qT_d = scratch("qT_d", [8, 128, OWN], BF16)
    K.iqT_d = scratch("iqT_d", [64, OWN, 16], BF16)
    K.iw_d = scratch("iw_d", [OWN, 16], F32)
    K.attT_d = scratch("attT_d", [8, 128, OWN], BF16)
    for nm in ("vb_d", "G_d", "BON_d", "AH_d", "RH_d", "BH_d", "KH_d"):
        setattr(K, nm, scratch(nm, [256, S], BF16))
    K.PC_d = scratch("PC_d", [256, NCH], F32)
    K.oT_d = scratch("oT_d", [256, S], F32)
    K.ro_loc_d = [scratch("ro_loc%d_d" % i, [2048, 256], BF16) for i in range(2)]
    K.ro_all_d = [scratch("ro_all%d_d" % i, [8192, 256], BF16) for i in range(2)]
    K.mixT_d = scratch("mixT_d", [16, 128, OWN], BF16)
    K.x1_d = scratch("x1_d", [OWN, D], F32)
    K.h2T_d = scratch("h2T_d", [16, 128, OWN], BF16)
    K.actT_d = scratch("actT_d", [44, 128, OWN], BF16)
    with contextlib.ExitStack() as stack:
        K.P = Prog(nc, stack)
        phase0_adaln(K)
        phase1_kv(K)
        if K.stages >= 2:
            phase1b_rwkv_proj(K)
        if K.stages >= 3 and not getattr(K, "skip_att", False):
            phase2_own_proj(K)
            phase3_attention(K)
        if K.stages >= 4:
            phase4b_rwkv_prep(K, cts=K.cts)
            if K.stages >= 5:
                phase4c_rwkv_scan(K, heads=[h for ct in K.cts for h in (2 * ct, 2 * ct + 1)])
        if K.stages >= 6:
            phase4d_rwkv_post(K, cts=K.cts)
            phase4e_allgather(K)
        if K.stages >= 7:
            phase5a_select(K)
            phase5b_outproj(K)
            phase5c_ffn(K)
        else:
            phase_final_copy(K)
        if debug:
            P = K.P
            allc = (("dbg_mixT", K.mixT_d, [16, 128, OWN], BF16), ("dbg_x1", K.x1_d, [OWN, D], F32),
                    ("dbg_oT", K.oT_d, [256, S], F32), ("dbg_AH", K.AH_d, [256, S], BF16), ("dbg_BH", K.BH_d, [256, S], BF16),
                    ("dbg_KH", K.KH_d, [256, S], BF16), ("dbg_RH", K.RH_d, [256, S], BF16), ("dbg_PC", K.PC_d, [256, NCH], F32),
                    ("dbg_G", K.G_d, [256, S], BF16), ("dbg_BON", K.BON_d, [256, S], BF16), ("dbg_vb", K.vb_d, [256, S], BF16),
                    ("dbg_yT", K.yT_d, [1216, S], F32), ("dbg_attT", K.attT_d, [8, 128, OWN], BF16),
                                     ("dbg_qT", K.qT_d, [8, 128, OWN], BF16), ("dbg_iqT", K.iqT_d, [64, OWN, 16], BF16),
                                     ("dbg_iw", K.iw_d, [OWN, 16], F32))
            for nm, src, shp, dt in allc:
                if dbg_list is not None and nm not in dbg_list:
                    continue
                o = dbg_out(K, nm, shp, dt)
                P.dma("sp", o, src, writes=[nm])
            P.flush()
    return nc, K


def make_in_maps(inputs, cores=range(8)):
    x = np.asarray(inputs["x"], dtype=np.float32)
    c = np.asarray(inputs["c"], dtype=np.float32)
    pos = np.asarray(inputs["positions"], dtype=np.int32)
    invf_att = (np.float32(500000.0) ** (-np.arange(16, dtype=np.float32) / np.float32(16))).astype(np.float32)
    invf_idx = (np.float32(500000.0) ** (-np.arange(8, dtype=np.float32) / np.float32(8))).astype(np.float32)
    mu = np.asarray(inputs["rwkv_mu"][0], dtype=np.float32)

    vecs = [mu[0:1024], mu[1024:2048], mu[2048:3072], inputs["rwkv_w0"][0], inputs["rwkv_a0"][0], inputs["rwkv_k_k"][0],
            inputs["rwkv_k_a"][0], np.asarray(inputs["rwkv_r_k"][0]).reshape(-1), inputs["rwkv_lnx_g"][0], inputs["rwkv_lnx_b"][0]]
    w_in_full = np.asarray(inputs["w_in"][0], dtype=np.float32)
    rw_mul = np.zeros((128, 4), np.float32)
    rw_mul[:96, 0] = mu[3072:3168]
    rw_mul[:96, 1] = mu[3168:3264]
    rw_mul[:, 2] = mu[3264:3392]
    rw_mul[:, 3] = mu[3392:3520]
    maps = []
    for core in cores:
        b, j = core // 4, core % 4
        ch = slice(256 * j, 256 * j + 256)
        rwp = {"rwp%d" % i: np.ascontiguousarray(np.asarray(v, dtype=np.float32)[ch].reshape(2, 128).T) for i, v in enumerate(vecs)}
        R0 = 4176
        w_in_rw = np.ascontiguousarray(np.concatenate([w_in_full[:, R0 + 256 * j:R0 + 256 * j + 256],
                                                       w_in_full[:, R0 + 1024 + 256 * j:R0 + 1024 + 256 * j + 256],
                                                       w_in_full[:, R0 + 2048 + 256 * j:R0 + 2048 + 256 * j + 256],
                                                       w_in_full[:, R0 + 3072:R0 + 3520]], axis=1))
        tiles = own_tiles(j)
        idx = np.concatenate([np.arange(t * 128, (t + 1) * 128) for t in tiles])
        maps.append({
            "x_full": np.ascontiguousarray(x[b]),
            "x_own": np.ascontiguousarray(x[b][idx]),
            "c_arr": np.ascontiguousarray(c[b].reshape(16, 128).T),
            "pos_full": np.ascontiguousarray(pos[b].reshape(32, 128).T),
            "invf_att": np.ascontiguousarray(np.broadcast_to(invf_att, (128, 16))),
            "invf_idx": np.ascontiguousarray(np.broadcast_to(invf_idx, (128, 8))),
            "w_ada": np.ascontiguousarray(np.asarray(inputs["w_ada"][0], dtype=np.float32)[:, 3072 * j:3072 * (j + 1)]),
            "b_ada": np.ascontiguousarray(np.asarray(inputs["b_ada"][0], dtype=np.float32)[3072 * j:3072 * (j + 1)]),
            "norm1_g": np.asarray(inputs["norm1_g"][0], dtype=np.float32),
            "k_norm_g": np.asarray(inputs["k_norm_g"][0], dtype=np.float32),
            "q_norm_g": np.asarray(inputs["q_norm_g"][0], dtype=np.float32),
            "pos_own": np.ascontiguousarray(pos[b][idx].reshape(8, 128).T),
            "qpos_own": np.ascontiguousarray(idx.astype(np.float32).reshape(8, 128).T),
            "w_in": np.ascontiguousarray(w_in_full[:, 0:4176]),
            "qpos_row": idx.astype(np.float32),
            "w_out": np.asarray(inputs["w_out"][0], dtype=np.float32),
            "norm2_g": np.asarray(inputs["norm2_g"][0], dtype=np.float32),
            "w_ffn_gate": np.asarray(inputs["w_ffn_gate"][0], dtype=np.float32),
            "w_ffn_up": np.asarray(inputs["w_ffn_up"][0], dtype=np.float32),
            "w_ffn_down": np.asarray(inputs["w_ffn_down"][0], dtype=np.float32),
            "rw_w_up": np.ascontiguousarray(np.asarray(inputs["rwkv_w_up"][0], dtype=np.float32)[:, ch]),
            "rw_a_up": np.ascontiguousarray(np.asarray(inputs["rwkv_a_up"][0], dtype=np.float32)[:, ch]),
            "rw_g_up": np.ascontiguousarray(np.asarray(inputs["rwkv_g_up"][0], dtype=np.float32)[:, ch]),
            "w_in_rw": w_in_rw,
            "rw_mul": rw_mul,
            **rwp,
        })
    return maps


def kernel(**inputs):
    nc, K = build_program(debug=False)
    maps = make_in_maps(inputs)
    res = run_bass_kernel_spmd(nc, maps, core_ids=list(range(8)))
    out = np.zeros((2, S, D), dtype=np.float32)
    for core in range(8):
        b, j = core // 4, core % 4
        y = res.results[core]["y_own"]
        for i, t in enumerate(own_tiles(j)):
            out[b, t * 128:(t + 1) * 128] = y[i * 128:(i + 1) * 128]
    return out
```

```python
import contextlib
import numpy as np
import concourse.bass as bass
import concourse.mybir as mybir
from concourse.bass_utils import run_bass_kernel_spmd

F32 = mybir.dt.float32
BF16 = mybir.dt.bfloat16
I32 = mybir.dt.int32
AF = mybir.ActivationFunctionType
ALU = mybir.AluOpType
AX = mybir.AxisListType

D = 2048
S = 4096
NT = 32
OWN = 1024
ENGS = ("pe", "act", "dve", "pool", "sp")
DEBUG = {}


class _Op:
    __slots__ = ("eng", "fn", "deps", "needs_inc", "is_dma", "sem", "count", "idx", "prev_same_sem", "is_cc")

    def __init__(self, eng, fn, is_dma):
        self.eng = eng
        self.fn = fn
        self.deps = set()
        self.needs_inc = False
        self.is_dma = is_dma
        self.sem = None
        self.count = 0
        self.prev_same_sem = None
        self.is_cc = False


class Prog:
    def __init__(self, nc, stack, n_dma_sems=48):
        self.nc = nc
        self.n_dma_sems = n_dma_sems
        self.eng_sem = {e: stack.enter_context(nc.semaphore("s_" + e)) for e in ENGS}
        self.dma_sems = [stack.enter_context(nc.semaphore("d%d" % i)) for i in range(n_dma_sems)]
        self.bar_sem = stack.enter_context(nc.semaphore("bar"))
        self.cc_sem = stack.enter_context(nc.semaphore("ccs"))
        self.cc_cnt = 0
        self.cnt = {e: 0 for e in ENGS}
        self.dcnt = [0] * n_dma_sems
        self.rr = 0
        self.nbar = 0
        self._reset()

    def _reset(self):
        self.ops = []
        self.last_writer = {}
        self.readers = {}

    def _record(self, op, reads, writes):
        idx = len(self.ops)
        op.idx = idx
        deps = set()
        for k in reads:
            w = self.last_writer.get(k)
            if w is not None:
                deps.add(w)
        for k in writes:
            w = self.last_writer.get(k)
            if w is not None:
                deps.add(w)
            for r in self.readers.get(k, ()):
                deps.add(r)
        deps.discard(idx)
        op.deps = deps
        self.ops.append(op)
        for k in reads:
            self.readers.setdefault(k, []).append(idx)
        for k in writes:
            self.last_writer[k] = idx
            self.readers[k] = []
        return idx

    def op(self, eng, fn, reads=(), writes=()):
        return self._record(_Op(eng, fn, False), reads, writes)

    def dma(self, queue, out, in_, reads=(), writes=(), **kw):
        def fn(e, out=out, in_=in_, kw=kw):
            return e.dma_start(out=out, in_=in_, **kw)
        return self._record(_Op(queue, fn, True), reads, writes)

    def coll(self, fn, reads=(), writes=()):
        o = _Op("pool", fn, True)
        o.is_cc = True
        return self._record(o, reads, writes)

    def flush(self):
        nc = self.nc
        ops = self.ops
        for o in ops:
            nd = set()
            for d in o.deps:
                p = ops[d]
                if o.eng == "pe" and p.eng == "pe" and not p.is_dma and not o.is_dma:
                    continue
                nd.add(d)
                p.needs_inc = True
            o.deps = nd
        last_of = {}
        for o in ops:
            if not o.is_dma:
                last_of[o.eng] = o
        for o in last_of.values():
            o.needs_inc = True
        dlast = [None] * self.n_dma_sems
        for o in ops:
            if o.is_cc:
                self.cc_cnt += 1
                o.sem = self.cc_sem
                o.count = self.cc_cnt
            elif o.is_dma:
                s = self.rr % self.n_dma_sems
                self.rr += 1
                o.prev_same_sem = dlast[s]
                self.dcnt[s] += 16
                o.sem = self.dma_sems[s]
                o.count = self.dcnt[s]
                dlast[s] = o.idx
            elif o.needs_inc:
                self.cnt[o.eng] += 1
                o.sem = self.eng_sem[o.eng]
                o.count = self.cnt[o.eng]
        per_eng = {e: [o for o in ops if o.eng == e] for e in ENGS}
        final = [(self.dma_sems[s], self.dcnt[s]) for s in range(self.n_dma_sems) if self.dcnt[s] > 0]
        final += [(self.eng_sem[e], self.cnt[e]) for e in ENGS if self.cnt[e] > 0]
        if self.cc_cnt > 0:
            final.append((self.cc_sem, self.cc_cnt))
        self.nbar += 1
        nbar = self.nbar
        bar = self.bar_sem

        def run(e_name, eng):
            waited = {}
            for o in per_eng[e_name]:
                need = {}
                for d in o.deps:
                    p = ops[d]
                    if need.get(p.sem.num, (0, None))[0] < p.count:
                        need[p.sem.num] = (p.count, p.sem)
                if o.is_dma and o.prev_same_sem is not None:
                    p = ops[o.prev_same_sem]
                    if need.get(p.sem.num, (0, None))[0] < p.count:
                        need[p.sem.num] = (p.count, p.sem)
                for key, (c, s) in need.items():
                    if waited.get(key, 0) < c:
                        eng.wait_ge(s, c)
                        waited[key] = c
                ins = o.fn(eng)
                if o.is_cc:
                    ins.then_inc(o.sem)
                elif o.is_dma:
                    ins.then_inc(o.sem, 16)
                elif o.needs_inc:
                    ins.then_inc(o.sem, 1)
            if e_name == "sp":
                for s, c in final:
                    eng.wait_ge(s, c)
                eng.sem_inc(bar, 1)
            eng.wait_ge(bar, nbar)

        with nc.Block() as block:
            @block.tensor
            def _(e):
                run("pe", e)

            @block.scalar
            def _(e):
                run("act", e)

            @block.vector
            def _(e):
                run("dve", e)

            @block.gpsimd
            def _(e):
                run("pool", e)

            @block.sync
            def _(e):
                run("sp", e)
        self._reset()


class Ctx:
    pass


def bcast_rows(ap1d, n):
    return bass.AP(ap1d.tensor, ap1d.offset, [[0, 128], [1, n]])


def dbg_out(K, name, shape, dtype=F32):
    t = K.nc.dram_tensor(name, list(shape), dtype, kind="ExternalOutput")
    K.dbg[name] = t
    return t.ap()


def make_ident(K, P, ident):
    P.op("pool", lambda e: e.memset(ident[:], 0.0), writes=["ident"])
    P.op("pool", lambda e: e.affine_select(out=ident[:], in_=ident[:], pattern=[[-1, 128]],
                                           compare_op=ALU.not_equal, fill=1.0, base=0,
                                           channel_multiplier=1),
         reads=["ident"], writes=["ident"])


def phase0_adaln(K):
    nc, P = K.nc, K.P
    NQ = 3072
    with contextlib.ExitStack() as st:
        c_sb = st.enter_context(nc.sbuf_tensor("c_sb", [128, 16], F32))
        cact = st.enter_context(nc.sbuf_tensor("cact", [128, 16], F32))
        wst = [st.enter_context(nc.sbuf_tensor("wst%d" % i, [128, 16, 512], F32)) for i in range(2)]
        modrow = st.enter_context(nc.sbuf_tensor("modrow", [1, NQ], F32))
        brow = st.enter_context(nc.sbuf_tensor("brow", [1, NQ], F32))
        ps = [st.enter_context(nc.psum_tensor("ps0_%d" % i, [1, 512], F32)) for i in range(2)]
        P.dma("sp", c_sb[:], K.c_arr, writes=["c_sb"])
        P.dma("sp", brow[:], K.b_ada.rearrange("(o n) -> o n", o=1), writes=["brow"])
        P.op("act", lambda e: e.activation(out=cact[:], in_=c_sb[:], func=AF.Silu),
             reads=["c_sb"], writes=["cact"])
        wv = K.w_ada.rearrange("(k p) n -> p k n", p=128)
        for nt in range(NQ // 512):
            sl = nt % 2
            for hh in range(2):
                P.dma("sp", wst[sl][:, hh * 8:(hh + 1) * 8, :],
                      wv[:, hh * 8:(hh + 1) * 8, nt * 512:(nt + 1) * 512],
                      writes=[("wst", sl, hh)])
            for k in range(16):
                P.op("pe", lambda e, k=k, sl=sl: e.matmul(ps[sl][:, :], lhsT=cact[:, k:k + 1],
                                                         rhs=wst[sl][:, k, :], start=(k == 0), stop=(k == 15)),
                     reads=["cact", ("wst", sl, k // 8)], writes=[("ps0", sl)])
            P.op("dve", lambda e, nt=nt, sl=sl: e.tensor_tensor(
                out=modrow[0:1, nt * 512:(nt + 1) * 512], in0=ps[sl][:, :],
                in1=brow[0:1, nt * 512:(nt + 1) * 512], op=ALU.add),
                reads=[("ps0", sl), "brow"], writes=[("modrow", nt)])
        P.dma("sp", K.modq_d, modrow[:],
              reads=[("modrow", nt) for nt in range(NQ // 512)], writes=["modq_d"])
        P.flush()
    P.coll(lambda e: e.collective_compute("AllGather", ALU.bypass, replica_groups=[[0, 1, 2, 3], [4, 5, 6, 7]],
                                          ins=[K.modq_d.opt()], outs=[K.mod4_d.opt()]), reads=["modq_d"], writes=["mod4"])
    P.flush()


def load_mod_rows(K, P, tile, which, gain_ap=None, key=None):
    src = K.mod_d[which * D:(which + 1) * D]
    P.dma("sp", tile[:], bcast_rows(src, D), writes=[key])


def bc(ap, shape):
    return ap.to_broadcast(list(shape))


def load_weight_bf16(K, P, st_tiles, dst, c_dst, src2d, ncols, tag, defer=None):
    wv = src2d.rearrange("(k p) n -> p k n", p=128)
    nk = wv.shape[1]
    engs = ["pool", "dve", "act"]
    for c0 in range(0, ncols, 512):
        n = min(512, ncols - c0)
        for k0 in range(0, nk, 4):
            kn = min(4, nk - k0)
            if defer is not None:
                defer.append(lambda c0=c0, n=n, k0=k0, kn=kn: _load_piece(K, P, st_tiles, dst, c_dst, wv, tag, engs, c0, n, k0, kn))
                continue
            _load_piece(K, P, st_tiles, dst, c_dst, wv, tag, engs, c0, n, k0, kn)
    return [(tag, c0, k0) for c0 in range(0, ncols, 512) for k0 in range(0, nk, 4)]


def _load_piece(K, P, st_tiles, dst, c_dst, wv, tag, engs, c0, n, k0, kn):
    if True:
        if True:
            i = K.wcnt
            K.wcnt += 1
            sl = i % len(st_tiles)
            stg = st_tiles[sl]
            P.dma("sp", stg[:, 0:kn, 0:n], wv[:, k0:k0 + kn, c0:c0 + n], writes=[("wstg", sl)])
            eng = engs[i % 3]
            o = dst[:, k0:k0 + kn, c_dst + c0:c_dst + c0 + n]
            if eng == "act":
                P.op("act", lambda e, o=o, stg=stg, kn=kn, n=n: e.copy(out=o, in_=stg[:, 0:kn, 0:n]),
                     reads=[("wstg", sl)], writes=[(tag, c0, k0)])
            else:
                P.op(eng, lambda e, o=o, stg=stg, kn=kn, n=n: e.tensor_copy(out=o, in_=stg[:, 0:kn, 0:n]),
                     reads=[("wstg", sl)], writes=[(tag, c0, k0)])


def rope_tables(K, P, st, pos_arr, ntile, invf_att, invf_idx, tag):
    nc = K.nc
    posi = st.enter_context(nc.sbuf_tensor(tag + "posi", [128, ntile], I32))
    posf = st.enter_context(nc.sbuf_tensor(tag + "posf", [128, ntile], F32))
    iva = st.enter_context(nc.sbuf_tensor(tag + "iva", [128, 16], F32))
    ivi = st.enter_context(nc.sbuf_tensor(tag + "ivi", [128, 8], F32))
    P.dma("sp", posi[:], pos_arr, writes=[tag + "posi"])
    P.dma("sp", iva[:], invf_att, writes=[tag + "iva"])
    P.dma("sp", ivi[:], invf_idx, writes=[tag + "ivi"])
    P.op("dve", lambda e: e.tensor_copy(out=posf[:], in_=posi[:]), reads=[tag + "posi"], writes=[tag + "posf"])
    out = {}
    for nm, iv, h in (("a", iva, 16), ("i", ivi, 8)):
        u = st.enter_context(nc.sbuf_tensor(tag + "u" + nm, [128, ntile, h], F32))
        ui = st.enter_context(nc.sbuf_tensor(tag + "ui" + nm, [128, ntile, h], I32))
        uf = st.enter_context(nc.sbuf_tensor(tag + "uf" + nm, [128, ntile, h], F32))
        for fn, off in (("sin", 0.0), ("cos", 0.25)):
            tb = st.enter_context(nc.sbuf_tensor(tag + fn + nm, [128, ntile, h], F32))
            kk = tag + fn + nm
            P.op("dve", lambda e, u=u, iv=iv, h=h: e.tensor_tensor(
                out=u[:], in0=bc(posf[:].unsqueeze(2), [128, ntile, h]),
                in1=bc(iv[:].unsqueeze(1), [128, ntile, h]), op=ALU.mult),
                reads=[tag + "posf", tag + "iv" + nm], writes=[tag + "U" + nm])
            P.op("dve", lambda e, u=u, off=off: e.tensor_scalar(
                out=u[:], in0=u[:], scalar1=float(1.0 / (2 * np.pi)), scalar2=off, op0=ALU.mult, op1=ALU.add),
                reads=[tag + "U" + nm], writes=[tag + "U" + nm])
            P.op("dve", lambda e, u=u, ui=ui: e.tensor_copy(out=ui[:], in_=u[:]), reads=[tag + "U" + nm], writes=[tag + "UI" + nm])
            P.op("dve", lambda e, uf=uf, ui=ui: e.tensor_copy(out=uf[:], in_=ui[:]), reads=[tag + "UI" + nm], writes=[tag + "UF" + nm])
            P.op("dve", lambda e, u=u, uf=uf: e.tensor_tensor(out=u[:], in0=u[:], in1=uf[:], op=ALU.subtract),
                 reads=[tag + "U" + nm, tag + "UF" + nm], writes=[tag + "U" + nm])
            P.op("dve", lambda e, u=u: e.tensor_scalar(out=u[:], in0=u[:], scalar1=-0.5, scalar2=0.5,
                                                        op0=ALU.max, op1=ALU.min),
                 reads=[tag + "U" + nm], writes=[tag + "U" + nm])
            P.op("act", lambda e, u=u, tb=tb: e.activation(out=tb[:], in_=u[:], func=AF.Sin,
                                                            scale=float(2 * np.pi)),
                 reads=[tag + "U" + nm], writes=[kk])
            out[fn + nm] = (tb, kk)
    return out


def apply_rope(P, eng, x4, cos, sin, t, half, tmp, rk, wk, sfx=""):
    ctb, ck = cos
    stb, sk = sin
    H = x4.shape[1]
    x1 = x4[:, :, 0:half]
    x2 = x4[:, :, half:2 * half]
    cb = bc(ctb[:, t, :].unsqueeze(1), [128, H, half])
    sb = bc(stb[:, t, :].unsqueeze(1), [128, H, half])
    a, b2, c, d = tmp
    P.op(eng, lambda e: e.tensor_tensor(out=a[:, 0:H, 0:half], in0=x1, in1=cb, op=ALU.mult), reads=rk + [ck], writes=["rtmpA" + sfx])
    P.op(eng, lambda e: e.tensor_tensor(out=b2[:, 0:H, 0:half], in0=x2, in1=sb, op=ALU.mult), reads=rk + [sk], writes=["rtmpB" + sfx])
    P.op(eng, lambda e: e.tensor_tensor(out=c[:, 0:H, 0:half], in0=x2, in1=cb, op=ALU.mult), reads=rk + [ck], writes=["rtmpC" + sfx])
    P.op(eng, lambda e: e.tensor_tensor(out=d[:, 0:H, 0:half], in0=x1, in1=sb, op=ALU.mult), reads=rk + [sk], writes=["rtmpD" + sfx])
    P.op(eng, lambda e: e.tensor_tensor(out=x1, in0=a[:, 0:H, 0:half], in1=b2[:, 0:H, 0:half], op=ALU.subtract),
         reads=["rtmpA" + sfx, "rtmpB" + sfx, "rtmpC" + sfx, "rtmpD" + sfx] + rk, writes=rk)
    P.op(eng, lambda e: e.tensor_tensor(out=x2, in0=c[:, 0:H, 0:half], in1=d[:, 0:H, 0:half], op=ALU.add),
         reads=["rtmpC" + sfx, "rtmpD" + sfx] + rk, writes=rk)


def head_rmsnorm(P, x3, gain, sq, ssum, rk, wk, gk=None, sqk=None):
    P.op("pool", lambda e: e.tensor_tensor(out=sq[:], in0=x3, in1=x3, op=ALU.mult), reads=rk, writes=[sqk or (wk + "sq")])
    P.op("dve", lambda e: e.tensor_reduce(out=ssum[:, 0:8], in_=sq[:], axis=AX.X, op=ALU.add),
         reads=[sqk or (wk + "sq")], writes=[wk + "s0"])
    P.op("dve", lambda e: e.tensor_scalar(out=ssum[:, 8:16], in0=ssum[:, 0:8], scalar1=1.0 / 128, scalar2=1e-6,
                                           op0=ALU.mult, op1=ALU.add), reads=[wk + "s0"], writes=[wk + "s1"])
    P.op("act", lambda e: e.activation(out=ssum[:, 16:24], in_=ssum[:, 8:16], func=AF.Sqrt),
         reads=[wk + "s1"], writes=[wk + "s2"])
    P.op("dve", lambda e: e.reciprocal(out=ssum[:, 24:32], in_=ssum[:, 16:24]), reads=[wk + "s2"], writes=[wk + "s3"])
    P.op("dve", lambda e: e.tensor_tensor(out=x3, in0=x3, in1=bc(ssum[:, 24:32].unsqueeze(2), [128, 8, 128]),
                                           op=ALU.mult), reads=rk + [wk + "s3"], writes=rk)
    P.op("pool", lambda e: e.tensor_tensor(out=x3, in0=x3, in1=bc(gain[:].unsqueeze(1), [128, 8, 128]),
                                            op=ALU.mult), reads=rk + [gk or ("gain" + wk)], writes=rk)


def norm_load(K, P, T, x_src, t):
    xs = t % 2
    P.dma("sp", T["xt"][xs][:], x_src[t * 128:(t + 1) * 128, :], writes=[("xt", xs)])


def norm_chain(K, P, T, x_src, t, G1, SH1, load=True, xg=None):
    xs = t % 2
    xt, hb, ss, junk = T["xt"], T["hb"], T["ss"], T["junk"]
    if load:
        norm_load(K, P, T, x_src, t)
    if xg is not None:
        xg_ap, xg_k = xg[xs]
        P.op("pool", lambda e: e.tensor_tensor(out=xg_ap, in0=xt[xs][:], in1=G1[:], op=ALU.mult),
             reads=[("xt", xs), "G"], writes=[xg_k])
    P.op("act", lambda e: e.activation(out=junk[:], in_=xt[xs][:], func=AF.Square, accum_out=ss[:, 0:1]),
         reads=[("xt", xs)], writes=["junk", "ss0"])
    P.op("dve", lambda e: e.tensor_scalar(out=ss[:, 1:2], in0=ss[:, 0:1], scalar1=1.0 / D, scalar2=1e-6,
                                           op0=ALU.mult, op1=ALU.add), reads=["ss0"], writes=["ss1"])
    P.op("act", lambda e: e.activation(out=ss[:, 2:3], in_=ss[:, 1:2], func=AF.Sqrt), reads=["ss1"], writes=["ss2"])
    P.op("dve", lambda e: e.reciprocal(out=ss[:, 3:4], in_=ss[:, 2:3]), reads=["ss2"], writes=["ss3"])
    if xg is not None:
        P.op("dve", lambda e: e.scalar_tensor_tensor(out=hb[xs][:], in0=xg_ap, scalar=ss[:, 3:4], in1=SH1[:],
                                                      op0=ALU.mult, op1=ALU.add),
             reads=[xg_k, "ss3", "SH"], writes=[("hb", xs)])
    else:
        P.op("dve", lambda e: e.scalar_tensor_tensor(out=xt[xs][:], in0=xt[xs][:], scalar=ss[:, 3:4], in1=G1[:],
                                                      op0=ALU.mult, op1=ALU.mult),
             reads=[("xt", xs), "ss3", "G"], writes=[("xt", xs)])
        P.op("pool", lambda e: e.tensor_tensor(out=hb[xs][:], in0=xt[xs][:], in1=SH1[:], op=ALU.add),
             reads=[("xt", xs), "SH"], writes=[("hb", xs)])


def norm_pe(K, P, T, t, ident, blk_hT, ti, hname="hT"):
    xs = t % 2
    hb, pT = T["hb"], T["pT"]
    for half in range(2):
        for kk in range(8):
            k = half * 8 + kk
            P.op("pe", lambda e, k=k, kk=kk, half=half: e.transpose(
                out=pT[half][:, kk, :], in_=hb[xs][:, k * 128:(k + 1) * 128], identity=ident[:]),
                reads=[("hb", xs), "ident"], writes=[("pT", half)])
        o = blk_hT[:, half * 8:(half + 1) * 8, ti * 128:(ti + 1) * 128]
        if half == 0:
            P.op("act", lambda e, o=o, half=half: e.copy(out=o, in_=pT[half][:]),
                 reads=[("pT", half)], writes=[(hname, ti, half)])
        else:
            P.op("dve", lambda e, o=o, half=half: e.tensor_copy(out=o, in_=pT[half][:]),
                 reads=[("pT", half)], writes=[(hname, ti, half)])


def norm_block(K, P, T, x_src, t, G1, SH1, ident, blk_hT, ti, load=True, hname="hT"):
    norm_chain(K, P, T, x_src, t, G1, SH1, load=load)
    norm_pe(K, P, T, t, ident, blk_hT, ti, hname=hname)


def norm_tiles_alloc(K, st, tag):
    nc = K.nc
    T = {}
    T["xt"] = [st.enter_context(nc.sbuf_tensor(tag + "xt%d" % i, [128, D], F32)) for i in range(2)]
    T["hb"] = [st.enter_context(nc.sbuf_tensor(tag + "hb%d" % i, [128, D], BF16)) for i in range(2)]
    T["ss"] = st.enter_context(nc.sbuf_tensor(tag + "ss", [128, 4], F32))
    T["junk"] = st.enter_context(nc.sbuf_tensor(tag + "junk", [128, D], BF16))
    T["pT"] = [st.enter_context(nc.psum_tensor(tag + "pT%d" % i, [128, 8, 128], BF16)) for i in range(2)]
    return T


def load_G_SH(K, P, st, which_sh, which_sc, gain_vec, tag):
    nc = K.nc
    G = st.enter_context(nc.sbuf_tensor(tag + "G", [128, D], F32))
    SH = st.enter_context(nc.sbuf_tensor(tag + "SH", [128, D], F32))
    gtmp = st.enter_context(nc.sbuf_tensor(tag + "gtmp", [128, D], F32))
    P.dma("sp", SH[:], bcast_rows(K.mod_d[which_sh * D:(which_sh + 1) * D], D), writes=["SH"])
    P.dma("sp", G[:], bcast_rows(K.mod_d[which_sc * D:(which_sc + 1) * D], D), writes=["G"])
    P.dma("sp", gtmp[:], bcast_rows(gain_vec, D), writes=["gtmp"])
    P.op("dve", lambda e: e.scalar_tensor_tensor(out=G[:], in0=G[:], scalar=1.0, in1=gtmp[:],
                                                  op0=ALU.add, op1=ALU.mult), reads=["G", "gtmp"], writes=["G"])
    K.last_gtmp = gtmp
    return G, SH


def phase1_kv(K):
    nc, P = K.nc, K.P
    with contextlib.ExitStack() as st:
        ident = st.enter_context(nc.sbuf_tensor("ident", [128, 128], BF16))
        make_ident(K, P, ident)
        G1, SH1 = load_G_SH(K, P, st, 0, 1, K.norm1_g, "p1")
        T = norm_tiles_alloc(K, st, "p1")
        hT = [st.enter_context(nc.sbuf_tensor("hT%d" % i, [128, 16, 512], BF16)) for i in range(2)]
        W = st.enter_context(nc.sbuf_tensor("Wkv", [128, 16, 2112], BF16))
        stg = [st.enter_context(nc.sbuf_tensor("wstg%d" % i, [128, 4, 512], F32)) for i in range(2)]
        wk_k = load_weight_bf16(K, P, stg, W, 0, K.w_in[:, 1024:2048], 1024, "Wk")
        wk_v = load_weight_bf16(K, P, stg, W, 1024, K.w_in[:, 2048:3072], 1024, "Wv")
        wk_i = load_weight_bf16(K, P, stg, W, 2048, K.w_in[:, 4096:4160], 64, "Wi")
        rt = rope_tables(K, P, st, K.pos_full, 32, K.invf_att, K.invf_idx, "rf")
        gain = st.enter_context(nc.sbuf_tensor("kgain", [128, 128], F32))
        P.dma("sp", gain[:], bcast_rows(K.k_norm_g, 128), writes=["gainK"])
        def two(name, shape, dt):
            return [st.enter_context(nc.sbuf_tensor(name + str(i), shape, dt)) for i in range(2)]
        ksb2 = two("ksb", [128, 8, 128], F32)
        kbf2 = two("kbf", [128, 8, 128], BF16)
        sq2 = [st.enter_context(nc.sbuf_tensor("sq", [128, 8, 128], F32))] * 2
        ssum2 = two("ssum", [128, 32], F32)
        rtmp2 = [[st.enter_context(nc.sbuf_tensor("rtmp%d" % i, [128, 8, 16], F32)) for i in range(4)]] * 2
        vsb2 = two("vsb", [128, 8, 129], BF16)
        iksb2 = two("iksb", [128, 1, 64], F32)
        ikbf2 = two("ikbf", [128, 64], BF16)
        kTs2 = [st.enter_context(nc.sbuf_tensor("kTs", [128, 8, 128], BF16))] * 2
        ikTs2 = two("ikTs", [64, 128], BF16)
        pm = [st.enter_context(nc.psum_tensor("pm%d" % i, [128, 512], F32)) for i in range(3)]
        pk = st.enter_context(nc.psum_tensor("pk", [128, 8, 128], BF16))
        pk2 = st.enter_context(nc.psum_tensor("pk2", [64, 128], BF16))
        for s_ in range(2):
            P.op("pool", lambda e, s_=s_: e.memset(vsb2[s_][:], 1.0), writes=["vsb%d" % s_])
        norm_load(K, P, T, K.x_full, 0)

        xg = [(K.last_gtmp[:], "gtmp"), (stg[0][:].rearrange("p a b -> p (a b)"), ("wstg", 0))]

        def norm_tile_chain(blk, ti):
            tt_ = blk * 4 + ti
            if tt_ + 1 < 32:
                norm_load(K, P, T, K.x_full, tt_ + 1)
            norm_chain(K, P, T, K.x_full, tt_, G1, SH1, load=False, xg=xg)

        def norm_tile_pe(blk, ti):
            norm_pe(K, P, T, blk * 4 + ti, ident, hT[blk % 2], ti, hname=("hT", blk % 2))

        def norm_tile(blk, ti):
            norm_tile_chain(blk, ti)
            norm_tile_pe(blk, ti)

        def store_hT(blk):
            hs = blk % 2
            hkeys = [(("hT", hs), ti, half) for ti in range(4) for half in range(2)]
            P.dma("sp", K.hT_d.rearrange("k p t -> p k t")[:, :, blk * 512:(blk + 1) * 512], hT[hs][:],
                  reads=hkeys, writes=[("hT_d", blk)])

        def bufs(t):
            u = t % 2
            return (str(u), ksb2[u], kbf2[u], sq2[u], ssum2[u], rtmp2[u], vsb2[u], iksb2[u], ikbf2[u], kTs2[u], ikTs2[u])

        def mm_tile(blk, ti):
            t = blk * 4 + ti
            hs = blk % 2
            hk = [(("hT", hs), ti, 0), (("hT", hs), ti, 1)]
            us, ksb, kbf, sq, ssum, rtmp, vsb, iksb, ikbf, kTs, ikTs = bufs(t)
            for gi, (c0, n, wkeys) in enumerate([(0, 512, wk_k), (512, 512, wk_k), (1024, 512, wk_v),
                                                 (1536, 512, wk_v), (2048, 64, wk_i)]):
                pb = pm[gi % 3]
                for k in range(16):
                    P.op("pe", lambda e, pb=pb, k=k, c0=c0, n=n, ti=ti, hs=hs: e.matmul(
                        pb[:, 0:n], lhsT=hT[hs][:, k, ti * 128:(ti + 1) * 128], rhs=W[:, k, c0:c0 + n],
                        start=(k == 0), stop=(k == 15)), reads=hk + wkeys, writes=[("pm", gi % 3)])
                if gi < 2:
                    P.op("act", lambda e, pb=pb, gi=gi, ksb=ksb: e.copy(out=ksb[:, gi * 4:(gi + 1) * 4, :], in_=pb[:, 0:512]),
                         reads=[("pm", gi % 3)], writes=["ksb" + us])
                elif gi < 4:
                    g2 = gi - 2
                    P.op("act", lambda e, pb=pb, g2=g2, vsb=vsb: e.copy(out=vsb[:, g2 * 4:(g2 + 1) * 4, 0:128], in_=pb[:, 0:512]),
                         reads=[("pm", gi % 3)], writes=["vsb" + us])
                else:
                    P.op("act", lambda e, pb=pb, iksb=iksb: e.copy(out=iksb[:, 0, :], in_=pb[:, 0:64]),
                         reads=[("pm", gi % 3)], writes=["iksb" + us])
            P.dma("sp", K.v_d[t * 128:(t + 1) * 128, :], vsb[:].rearrange("p h d -> p (h d)"),
                  reads=["vsb" + us], writes=[("v_d", t)])

        def post1(blk, ti):
            t = blk * 4 + ti
            us, ksb, kbf, sq, ssum, rtmp, vsb, iksb, ikbf, kTs, ikTs = bufs(t)
            head_rmsnorm(P, ksb[:], gain, sq, ssum, ["ksb" + us], "K" + us, gk="gainK", sqk="Ksq")
            apply_rope(P, "dve", ksb[:], rt["cosa"], rt["sina"], t, 16, rtmp, ["ksb" + us], "rK")
            P.op("act", lambda e, kbf=kbf, ksb=ksb: e.copy(out=kbf[:], in_=ksb[:]), reads=["ksb" + us], writes=["kbf" + us])
            apply_rope(P, "pool", iksb[:], rt["cosi"], rt["sini"], t, 8, rtmp, ["iksb" + us], "rI")
            P.op("act", lambda e, ikbf=ikbf, iksb=iksb: e.copy(out=ikbf[:], in_=iksb[:, 0, :]), reads=["iksb" + us], writes=["ikbf" + us])

        def post2(blk, ti):
            t = blk * 4 + ti
            us, ksb, kbf, sq, ssum, rtmp, vsb, iksb, ikbf, kTs, ikTs = bufs(t)
            for h in range(8):
                P.op("pe", lambda e, h=h, kbf=kbf: e.transpose(out=pk[:, h, :], in_=kbf[:, h, :], identity=ident[:]),
                     reads=["kbf" + us, "ident"], writes=["pk"])
            P.op("dve", lambda e, kTs=kTs: e.tensor_copy(out=kTs[:], in_=pk[:]), reads=["pk"], writes=["kTs"])
            P.dma("sp", K.kT_d.rearrange("h p t -> p h t")[:, :, t * 128:(t + 1) * 128], kTs[:],
                  reads=["kTs"], writes=[("kT_d", t)])
            P.op("pe", lambda e, ikbf=ikbf: e.transpose(out=pk2[:, :], in_=ikbf[:], identity=ident[:]),
                 reads=["ikbf" + us, "ident"], writes=["pk2"])
            P.op("dve", lambda e, ikTs=ikTs: e.tensor_copy(out=ikTs[:], in_=pk2[:, :]), reads=["pk2"], writes=["ikTs" + us])
            P.dma("sp", K.ikT_d[:, t * 128:(t + 1) * 128], ikTs[:], reads=["ikTs" + us], writes=[("ikT_d", t)])

        for ti in range(4):
            norm_tile(0, ti)
        store_hT(0)
        prev = None
        for blk in range(8):
            for ti in range(4):
                if blk + 1 < 8:
                    norm_tile_chain(blk + 1, ti)
                mm_tile(blk, ti)
                if prev is not None:
                    post2(*prev)
                if blk + 1 < 8:
                    norm_tile_pe(blk + 1, ti)
                post1(blk, ti)
                prev = (blk, ti)
            if blk + 1 < 8:
                store_hT(blk + 1)
        post2(*prev)
        P.flush()

RW0 = 4176
NRW = 1216
RW_GROUPS = [(i * 128, 128) for i in range(6)] + [(768, 96), (864, 96), (960, 128), (1088, 128)]


def phase1b_rwkv_proj(K):
    nc, P = K.nc, K.P
    with contextlib.ExitStack() as st:
        W = st.enter_context(nc.sbuf_tensor("Wr", [128, 16, NRW], BF16))
        stg = [st.enter_context(nc.sbuf_tensor("wstgb%d" % i, [128, 4, 512], F32)) for i in range(2)]
        hT = [st.enter_context(nc.sbuf_tensor("hTb%d" % i, [128, 16, 512], BF16)) for i in range(2)]
        ost = [st.enter_context(nc.sbuf_tensor("ost%d" % i, [128, 512], F32)) for i in range(4)]
        pm = [st.enter_context(nc.psum_tensor("pmb%d" % i, [128, 512], F32)) for i in range(4)]
        wkeys = load_weight_bf16(K, P, stg, W, 0, K.w_in_rw, NRW, "Wr")
        cnt = 0
        for blk in range(8):
            hs = blk % 2
            P.dma("sp", hT[hs][:], K.hT_d.rearrange("k p t -> p k t")[:, :, blk * 512:(blk + 1) * 512],
                  writes=[("hTb", hs)])
            for (r0, m) in RW_GROUPS:
                s4 = cnt % 4
                cnt += 1
                for k in range(16):
                    P.op("pe", lambda e, k=k, r0=r0, m=m, hs=hs, s4=s4: e.matmul(
                        pm[s4][0:m, :], lhsT=W[:, k, r0:r0 + m], rhs=hT[hs][:, k, :],
                        start=(k == 0), stop=(k == 15)), reads=[("hTb", hs)] + wkeys, writes=[("pmb", s4)])
                if cnt % 2 == 0:
                    P.op("act", lambda e, m=m, s4=s4: e.copy(out=ost[s4][0:m, :], in_=pm[s4][0:m, :]),
                         reads=[("pmb", s4)], writes=[("ost", s4)])
                else:
                    P.op("dve", lambda e, m=m, s4=s4: e.tensor_copy(out=ost[s4][0:m, :], in_=pm[s4][0:m, :]),
                         reads=[("pmb", s4)], writes=[("ost", s4)])
                P.dma("sp", K.yT_d[r0:r0 + m, blk * 512:(blk + 1) * 512], ost[s4][0:m, :],
                      reads=[("ost", s4)], writes=[("yT_d", r0, blk)])
        P.flush()


def phase2_own_proj(K):
    nc, P = K.nc, K.P
    with contextlib.ExitStack() as st:
        ident = st.enter_context(nc.sbuf_tensor("ident2", [128, 128], BF16))
        make_ident(K, P, ident)
        G1, SH1 = load_G_SH(K, P, st, 0, 1, K.norm1_g, "p2")
        T = norm_tiles_alloc(K, st, "p2")
        hT = [st.enter_context(nc.sbuf_tensor("hTo%d" % i, [128, 16, 512], BF16)) for i in range(2)]
        W = st.enter_context(nc.sbuf_tensor("Wq", [128, 16, 2064], BF16))
        stg = [st.enter_context(nc.sbuf_tensor("wstgq%d" % i, [128, 4, 512], F32)) for i in range(2)]
        wk_q = load_weight_bf16(K, P, stg, W, 0, K.w_in[:, 0:1024], 1024, "Wq")
        wk_iq = load_weight_bf16(K, P, stg, W, 1024, K.w_in[:, 3072:4096], 1024, "Wiq")
        wk_iw = load_weight_bf16(K, P, stg, W, 2048, K.w_in[:, 4160:4176], 16, "Wiw")
        rt = rope_tables(K, P, st, K.pos_own, 8, K.invf_att, K.invf_idx, "ro")
        gain = st.enter_context(nc.sbuf_tensor("qgain", [128, 128], F32))
        P.dma("sp", gain[:], bcast_rows(K.q_norm_g, 128), writes=["gainQ"])
        qsb = st.enter_context(nc.sbuf_tensor("qsb", [128, 8, 128], F32))
        qbf = st.enter_context(nc.sbuf_tensor("qbf", [128, 8, 128], BF16))
        sq = st.enter_context(nc.sbuf_tensor("sq2", [128, 8, 128], F32))
        ssum = st.enter_context(nc.sbuf_tensor("ssum2", [128, 32], F32))
        rtmp = [st.enter_context(nc.sbuf_tensor("rtmpq%d" % i, [128, 16, 16], F32)) for i in range(4)]
        iqsb = st.enter_context(nc.sbuf_tensor("iqsb", [128, 16, 64], F32))
        iqbf = st.enter_context(nc.sbuf_tensor("iqbf", [128, 16, 64], BF16))
        iwsb = st.enter_context(nc.sbuf_tensor("iwsb", [128, 16], F32))
        qTs = st.enter_context(nc.sbuf_tensor("qTs", [128, 8, 128], BF16))
        iqTs = st.enter_context(nc.sbuf_tensor("iqTs", [64, 128, 16], BF16))
        pm = [st.enter_context(nc.psum_tensor("pmq%d" % i, [128, 512], F32)) for i in range(3)]
        pk = st.enter_context(nc.psum_tensor("pkq", [128, 8, 128], BF16))
        for blk in range(2):
            hs = blk % 2
            for ti in range(4):
                norm_block(K, P, T, K.x_own, blk * 4 + ti, G1, SH1, ident, hT[hs], ti)
            for ti in range(4):
                t = blk * 4 + ti
                hk = [("hT", ti, 0), ("hT", ti, 1)]
                for gi, (c0, n, wkeys) in enumerate([(0, 512, wk_q), (512, 512, wk_q), (1024, 512, wk_iq),
                                                     (1536, 512, wk_iq), (2048, 16, wk_iw)]):
                    pb = pm[gi % 3]
                    for k in range(16):
                        P.op("pe", lambda e, pb=pb, k=k, c0=c0, n=n, ti=ti, hs=hs: e.matmul(
                            pb[:, 0:n], lhsT=hT[hs][:, k, ti * 128:(ti + 1) * 128], rhs=W[:, k, c0:c0 + n],
                            start=(k == 0), stop=(k == 15)), reads=hk + wkeys, writes=[("pmq", gi % 3)])
                    if gi < 2:
                        P.op("act", lambda e, pb=pb, gi=gi: e.copy(out=qsb[:, gi * 4:(gi + 1) * 4, :], in_=pb[:, 0:512]),
                             reads=[("pmq", gi % 3)], writes=["qsb"])
                    elif gi < 4:
                        g2 = gi - 2
                        P.op("act", lambda e, pb=pb, g2=g2: e.copy(out=iqsb[:, g2 * 8:(g2 + 1) * 8, :], in_=pb[:, 0:512]),
                             reads=[("pmq", gi % 3)], writes=["iqsb"])
                    else:
                        P.op("act", lambda e, pb=pb: e.activation(out=iwsb[:], in_=pb[:, 0:16], func=AF.Copy, scale=0.25),
                             reads=[("pmq", gi % 3)], writes=["iwsb"])
                P.dma("sp", K.iw_d[t * 128:(t + 1) * 128, :], iwsb[:], reads=["iwsb"], writes=[("iw_d", t)])
                head_rmsnorm(P, qsb[:], gain, sq, ssum, ["qsb"], "Q")
                apply_rope(P, "dve", qsb[:], rt["cosa"], rt["sina"], t, 16, rtmp, ["qsb"], "rQ")
                P.op("act", lambda e: e.copy(out=qbf[:], in_=qsb[:]), reads=["qsb"], writes=["qbf"])
                for h in range(8):
                    P.op("pe", lambda e, h=h: e.transpose(out=pk[:, h, :], in_=qbf[:, h, :], identity=ident[:]),
                         reads=["qbf", "ident"], writes=["pkq"])
                P.op("dve", lambda e: e.tensor_copy(out=qTs[:], in_=pk[:]), reads=["pkq"], writes=["qTs"])
                P.dma("sp", K.qT_d.rearrange("h p t -> p h t")[:, :, t * 128:(t + 1) * 128], qTs[:],
                      reads=["qTs"], writes=[("qT_d", t)])
                apply_rope(P, "pool", iqsb[:], rt["cosi"], rt["sini"], t, 8, rtmp, ["iqsb"], "rIQ")
                P.op("act", lambda e: e.activation(out=iqbf[:], in_=iqsb[:], func=AF.Copy, scale=0.125),
                     reads=["iqsb"], writes=["iqbf"])
                for half in range(2):
                    for hh in range(8):
                        h = half * 8 + hh
                        P.op("pe", lambda e, h=h, hh=hh: e.transpose(out=pk[0:64, hh, :], in_=iqbf[:, h, :],
                                                                      identity=ident[:]),
                             reads=["iqbf", "ident"], writes=["pkq"])
                    P.op("dve", lambda e, half=half: e.tensor_copy(
                        out=iqTs[:, :, half * 8:(half + 1) * 8].rearrange("p t h -> p h t"), in_=pk[0:64, :, :]),
                         reads=["pkq"], writes=["iqTs"])
                P.dma("sp", K.iqT_d[:, t * 128:(t + 1) * 128, :], iqTs[:], reads=["iqTs"], writes=[("iqT_d", t)])
        P.flush()


NIT = 22
SLOT_NK = [4, 8, 12, 16, 20, 24, 28, 32]


def phase3_attention(K):
    nc, P = K.nc, K.P
    with contextlib.ExitStack() as st:
        def sb(name, shape, dt):
            return st.enter_context(nc.sbuf_tensor(name, shape, dt))
        ident = sb("ident3", [128, 128], BF16)
        kposf = sb("kposf", [128, 512], F32)
        bias = sb("cbias", [128, 512], F32)
        identf = kposf[:, 0:128]
        make_ident(K, P, ident)
        P.op("dve", lambda e: e.tensor_copy(out=identf, in_=ident[:]), reads=["ident"], writes=["kposf"])
        kT = sb("kTall", [128, 8, S], BF16)
        V = sb("Vall", [128, 32, 1032], BF16)
        ikT = sb("ikTall", [64, S], BF16)
        for h in range(8):
            P.dma("sp", kT[:, h, :], K.kT_d[h], writes=[("kT", h)])
        for q4 in range(4):
            P.dma("sp", V[:, q4 * 8:(q4 + 1) * 8, :],
                  K.v_d.rearrange("(t p) c -> p t c", p=128)[:, q4 * 8:(q4 + 1) * 8, :], writes=[("V", q4)])
        P.dma("sp", ikT[:], K.ikT_d, writes=["ikT"])
        kTk = [("kT", h) for h in range(8)]
        Vk = [("V", q4) for q4 in range(4)]
        Sel = sb("Sel", [128, 16, 128], BF16)
        pidx = sb("pidx", [128, 1], I32)
        pidf = sb("pidf", [128, 1], F32)
        score = sb("score", [128, S], F32)
        self_ = score[:, 0:2048].rearrange("p (g t) -> p g t", g=16)
        sk4 = [("score", q) for q in range(4)]
        P.op("pool", lambda e: e.iota(self_, pattern=[[-8, 16], [1, 128]], base=0, channel_multiplier=0, allow_small_or_imprecise_dtypes=True), writes=sk4)
        P.op("pool", lambda e: e.iota(pidx[:], pattern=[[0, 1]], base=0, channel_multiplier=1), writes=["pidx"])
        P.op("dve", lambda e: e.tensor_scalar(out=pidx[:], in0=pidx[:], scalar1=4, scalar2=None,
                                               op0=ALU.arith_shift_right), reads=["pidx"], writes=["pidx"])
        P.op("dve", lambda e: e.tensor_copy(out=pidf[:], in_=pidx[:]), reads=["pidx"], writes=["pidf"])
        P.op("dve", lambda e: e.tensor_scalar(out=Sel[:], in0=self_, scalar1=pidf[:, 0:1], scalar2=None,
                                               op0=ALU.is_equal), reads=sk4 + ["pidf"], writes=["Sel"])
        qpos = sb("qpos", [128, 8], F32)
        P.dma("sp", qpos[:], K.qpos_own, writes=["qpos"])
        iwg = bias[:, 0:128]
        wcol = sb("wcol", [128, 128], F32)
        P.dma("sp", iwg, K.iw_d.rearrange("(g t) h -> g (t h)", t=8), writes=["bias"])
        A = [st.enter_context(nc.psum_tensor("A%d" % i, [128, 512], F32)) for i in range(2)]
        B = [st.enter_context(nc.psum_tensor("B%d" % i, [128, 512], F32)) for i in range(2)]
        C = st.enter_context(nc.psum_tensor("C3", [128, 8, 128], BF16))
        P.op("pe", lambda e: e.transpose(out=A[0][:, 0:128], in_=iwg, identity=identf),
             reads=["bias", "kposf"], writes=[("A", 0)])
        P.op("dve", lambda e: e.tensor_copy(out=wcol[:], in_=A[0][:, 0:128]), reads=[("A", 0)], writes=["wcol"])
        mask01 = sb("mask01", [128, S], BF16)
        maskT = sb("maskT", [128, 32, 128], BF16)
        R = [sb("R%d" % i, [128, 512], BF16) for i in range(2)]
        pexp = [sb("pexp%d" % i, [128, 512], BF16) for i in range(2)]
        pmk = [sb("pmk%d" % i, [128, 512], BF16) for i in range(2)]
        iqTs = sb("iqTs3", [64, 128, 16], BF16)
        qTs = sb("qTs3", [128, 8, 128], BF16)
        att = sb("att", [128, 8, 128], BF16)
        attTs = sb("attTs", [128, 8, 128], BF16)
        c2 = sb("c2", [128, NIT], F32)
        steps = sb("steps", [128, NIT], F32)
        sm = sb("sm3", [128, 8], F32)
        for k in range(NIT):
            P.op("pool", lambda e, k=k: e.memset(c2[:, k:k + 1], float(2.0 ** -(k + 1))), writes=["c2"])
        maskT2 = [maskT, sb("maskTb", [128, 32, 128], BF16)]
        Wbd = sb("Wbd", [128, 16, 128], BF16)
        qTs2 = [qTs, sb("qTs3b", [128, 8, 128], BF16)]
        rcp2 = sb("rcp2", [128, 2], F32)

        def stageA(i):
            nk = SLOT_NK[i]
            nb = nk // 4
            P.dma("sp", iqTs[:], K.iqT_d[:, i * 128:(i + 1) * 128, :], writes=["iqTs"])
            P.dma("sp", qTs2[i % 2][:], K.qT_d.rearrange("h p t -> p h t")[:, :, i * 128:(i + 1) * 128], writes=[("qTs", i % 2)])
            isteps = [(sbk, g) for sbk in range(nb) for g in range(16)]
            P.op("pool", lambda e, i=i: e.tensor_tensor(out=Wbd[:], in0=Sel[:],
                                                         in1=wcol[:, i * 16:(i + 1) * 16].unsqueeze(2).to_broadcast([128, 16, 128]),
                                                         op=ALU.mult), reads=["Sel", "wcol"], writes=["Wbd"])

            def dots(si):
                sbk, g = isteps[si]
                a_ = si % 2
                lhsT = iqTs[:, g * 8:(g + 1) * 8, :].rearrange("p t h -> p (t h)")
                P.op("pe", lambda e, a_=a_, lhsT=lhsT, sbk=sbk: e.matmul(
                    A[a_][:, :], lhsT=lhsT, rhs=ikT[:, sbk * 512:(sbk + 1) * 512], start=True, stop=True),
                    reads=["iqTs", "ikT"], writes=[("A", a_)])
            dots(0)
            for si, (sbk, g) in enumerate(isteps):
                a_ = si % 2
                bsl = sbk % 2
                if si + 1 < len(isteps):
                    dots(si + 1)
                if si % 2 == 0:
                    P.op("act", lambda e, a_=a_: e.activation(out=R[a_][:], in_=A[a_][:, :], func=AF.Relu),
                         reads=[("A", a_)], writes=[("R", a_)])
                else:
                    P.op("dve", lambda e, a_=a_: e.tensor_scalar(out=R[a_][:], in0=A[a_][:, :], scalar1=0.0, scalar2=None,
                                                                  op0=ALU.max), reads=[("A", a_)], writes=[("R", a_)])
                P.op("pe", lambda e, a_=a_, g=g, bsl=bsl: e.matmul(
                    B[bsl][:, :], lhsT=Wbd[:, g, :], rhs=R[a_][:], start=(g == 0), stop=(g == 15)),
                    reads=[("R", a_), "Wbd"], writes=[("B", bsl)])
                if g == 15:
                    P.op("dve", lambda e, bsl=bsl, sbk=sbk: e.tensor_copy(out=score[:, sbk * 512:(sbk + 1) * 512], in_=B[bsl][:, :]),
                         reads=[("B", bsl)], writes=[("score", sbk)])

        def stageBdve(i):
            nk = SLOT_NK[i]
            nb = nk // 4
            L = nk * 128
            sck = [("score", sbk) for sbk in range(nb)]
            P.op("dve", lambda e, L=L: e.tensor_reduce(out=sm[:, 0:1], in_=score[:, 0:L], axis=AX.X, op=ALU.max,
                                                        apply_absolute_value=True), reads=sck, writes=["sm0"])
            P.op("pool", lambda e, nb=nb: e.iota(kposf[:], pattern=[[1, 512]], base=(nb - 1) * 512, channel_multiplier=0,
                                                 allow_small_or_imprecise_dtypes=True), writes=["kposf"])
            P.op("dve", lambda e, i=i: e.tensor_scalar(out=bias[:], in0=kposf[:], scalar1=qpos[:, i:i + 1],
                                                        scalar2=-1e30, op0=ALU.is_gt, op1=ALU.mult),
                 reads=["kposf", "qpos"], writes=["bias"])
            P.op("dve", lambda e, nb=nb: e.tensor_tensor(out=score[:, (nb - 1) * 512:nb * 512],
                                                          in0=score[:, (nb - 1) * 512:nb * 512], in1=bias[:], op=ALU.add),
                 reads=["bias", ("score", nb - 1), "sm0"], writes=[("score", nb - 1)])
            P.op("dve", lambda e: e.tensor_scalar(out=sm[:, 1:2], in0=sm[:, 0:1], scalar1=-1.0, scalar2=-1.0,
                                                   op0=ALU.mult, op1=ALU.add), reads=["sm0"], writes=["lo"])
            P.op("dve", lambda e: e.tensor_scalar(out=sm[:, 5:6], in0=sm[:, 0:1], scalar1=2.0, scalar2=2.0,
                                                   op0=ALU.mult, op1=ALU.add), reads=["sm0"], writes=["d0"])
            P.op("dve", lambda e: e.tensor_scalar(out=steps[:], in0=c2[:], scalar1=sm[:, 5:6], scalar2=None,
                                                   op0=ALU.mult), reads=["d0", "c2"], writes=["steps"])
            P.op("dve", lambda e: e.tensor_tensor(out=sm[:, 2:3], in0=sm[:, 1:2], in1=steps[:, 0:1], op=ALU.add),
                 reads=["lo", "steps"], writes=["mid"])
            for k in range(NIT):
                P.op("dve", lambda e, L=L: e.tensor_scalar(out=mask01[:, 0:L], in0=score[:, 0:L], scalar1=sm[:, 2:3],
                                                            scalar2=None, op0=ALU.is_ge, op1=ALU.add,
                                                            accum_out=sm[:, 3:4]),
                     reads=sck + ["mid"], writes=["mask01", "cnt"])
                P.op("dve", lambda e: e.tensor_scalar(out=sm[:, 4:5], in0=sm[:, 3:4], scalar1=255.5, scalar2=-0.5,
                                                       op0=ALU.is_ge, op1=ALU.add), reads=["cnt"], writes=["inc"])
                P.op("dve", lambda e, k=k: e.scalar_tensor_tensor(out=sm[:, 2:3], in0=steps[:, k:k + 1], scalar=sm[:, 4:5],
                                                                   in1=sm[:, 2:3], op0=ALU.mult, op1=ALU.add),
                     reads=["inc", "steps", "mid"], writes=["mid"])
            P.op("dve", lambda e: e.scalar_tensor_tensor(out=sm[:, 1:2], in0=steps[:, NIT - 1:NIT], scalar=-0.5, in1=sm[:, 2:3],
                                                          op0=ALU.mult, op1=ALU.add), reads=["steps", "mid"], writes=["lo"])
            P.op("dve", lambda e, L=L: e.tensor_scalar(out=mask01[:, 0:L], in0=score[:, 0:L], scalar1=sm[:, 1:2],
                                                        scalar2=None, op0=ALU.is_ge), reads=sck + ["lo"], writes=["mask01"])

        def stageBpe(i):
            nk = SLOT_NK[i]
            mT = maskT2[i % 2]
            for kt in range(nk):
                P.op("pe", lambda e, kt=kt: e.transpose(out=C[:, kt % 8, :], in_=mask01[:, kt * 128:(kt + 1) * 128],
                                                         identity=ident[:]), reads=["mask01", "ident"], writes=["C"])
                if kt % 8 == 7 or kt == nk - 1:
                    k0 = (kt // 8) * 8
                    n8 = kt - k0 + 1
                    P.op("dve", lambda e, k0=k0, n8=n8, mT=mT: e.tensor_copy(out=mT[:, k0:k0 + n8, :], in_=C[:, 0:n8, :]),
                         reads=["C"], writes=[("maskT", i % 2, k0 // 8)])

        def stageC(i):
            nk = SLOT_NK[i]
            nb = nk // 4
            mT = maskT2[i % 2]
            qT_ = qTs2[i % 2]
            mk = [("maskT", i % 2, q) for q in range((nk + 7) // 8)]
            asteps = [(h, kg) for h in range(8) for kg in range(nb)]

            def qk(si):
                h, kg = asteps[si]
                a_ = si % 2
                for j4 in range(4):
                    kt = kg * 4 + j4
                    P.op("pe", lambda e, a_=a_, j4=j4, kt=kt, h=h: e.matmul(
                        A[a_][:, j4 * 128:(j4 + 1) * 128], lhsT=kT[:, h, kt * 128:(kt + 1) * 128], rhs=qT_[:, h, :],
                        start=True, stop=True), reads=kTk + [("qTs", i % 2)], writes=[("A", a_)])
            qk(0)
            for si, (h, kg) in enumerate(asteps):
                a_ = si % 2
                bsl = h % 2
                if si + 1 < len(asteps):
                    qk(si + 1)
                P.op("act", lambda e, a_=a_: e.activation(out=pexp[a_][:], in_=A[a_][:, :], func=AF.Exp,
                                                           scale=float(128 ** -0.5)),
                     reads=[("A", a_)], writes=[("pexp", a_)])
                P.op("pool", lambda e, a_=a_, kg=kg: e.tensor_tensor(
                    out=pmk[a_][:], in0=pexp[a_][:], in1=mT[:, kg * 4:(kg + 1) * 4, :].rearrange("p a t -> p (a t)"),
                    op=ALU.mult), reads=[("pexp", a_)] + mk, writes=[("pmk", a_)])
                for j4 in range(4):
                    kt = kg * 4 + j4
                    P.op("pe", lambda e, a_=a_, j4=j4, kt=kt, h=h, bsl=bsl, kg=kg, nb=nb: e.matmul(
                        B[bsl][:, 0:129], lhsT=pmk[a_][:, j4 * 128:(j4 + 1) * 128], rhs=V[:, kt, h * 129:(h + 1) * 129],
                        start=(kg == 0 and j4 == 0), stop=(kg == nb - 1 and j4 == 3)),
                        reads=[("pmk", a_)] + Vk, writes=[("B", bsl)])
                if kg == nb - 1:
                    P.op("act", lambda e, bsl=bsl: e.activation(out=rcp2[:, 0:1], in_=B[bsl][:, 128:129], func=AF.Ln),
                         reads=[("B", bsl)], writes=["rcpa"])
                    P.op("act", lambda e: e.activation(out=rcp2[:, 1:2], in_=rcp2[:, 0:1], func=AF.Exp, scale=-1.0),
                         reads=["rcpa"], writes=["rcpb"])
                    P.op("act", lambda e, bsl=bsl, h=h: e.activation(out=att[:, h, :], in_=B[bsl][:, 0:128], func=AF.Copy,
                                                                      scale=rcp2[:, 1:2]),
                         reads=[("B", bsl), "rcpb"], writes=["att"])
            for h in range(8):
                P.op("pe", lambda e, h=h: e.transpose(out=C[:, h, :], in_=att[:, h, :], identity=ident[:]),
                     reads=["att", "ident"], writes=["C"])
            P.op("act", lambda e: e.copy(out=attTs[:], in_=C[:]), reads=["C"], writes=["attTs"])
            P.dma("sp", K.attT_d.rearrange("h p t -> p h t")[:, :, i * 128:(i + 1) * 128], attTs[:],
                  reads=["attTs"], writes=[("attT_d", i)])

        stageA(0)
        stageBdve(0)
        stageBpe(0)
        for i in range(8):
            if i + 1 < 8:
                stageA(i + 1)
                stageBdve(i + 1)
            stageC(i)
            if i + 1 < 8:
                stageBpe(i + 1)
        P.flush()

RD = BF16
NCH = 64


def tok_shift(P, dst, raw, tmp, mu_ap, rk_raw, k_tmp, k_dst, n=128):
    P.op("pool", lambda e: e.tensor_tensor(out=tmp[0:n, 1:S], in0=raw[0:n, 0:S - 1], in1=raw[0:n, 1:S], op=ALU.subtract),
         reads=[rk_raw], writes=[k_tmp])
    P.op("pool", lambda e: e.tensor_scalar(out=tmp[0:n, 0:1], in0=raw[0:n, 0:1], scalar1=-1.0, scalar2=0.0,
                                            op0=ALU.mult, op1=ALU.add), reads=[rk_raw, k_tmp], writes=[k_tmp])
    P.op("dve", lambda e: e.scalar_tensor_tensor(out=dst[0:n, :], in0=tmp[0:n, :], scalar=mu_ap, in1=raw[0:n, :],
                                                  op0=ALU.mult, op1=ALU.add), reads=[rk_raw, k_tmp], writes=[k_dst])


def phase4b_rwkv_prep(K, cts=range(2)):
    nc, P = K.nc, K.P
    with contextlib.ExitStack() as st:
        def sb(name, shape, dt):
            return st.enter_context(nc.sbuf_tensor(name, shape, dt))
        txw = sb("txw", [96, S], BF16)
        xap = sb("xap", [96, S], BF16)
        sxg = sb("sxg", [128, 2, S], BF16)
        M01 = sb("M01", [128, S], BF16)
        wup = sb("wup", [96, 256], BF16)
        aup = sb("aup", [96, 256], BF16)
        gup = sb("gup", [128, 2, 256], BF16)
        wst = sb("wst4", [128, 2, 256], F32)
        bones = sb("bones", [128, 128], BF16)
        prm = sb("prm", [128, 12, 2], F32)
        mul = sb("mul", [128, 4], F32)
        PT = sb("PT", [128, S], F32)
        KK = sb("KK", [128, S], F32)
        KP = sb("KP", [128, S], F32)
        CL = sb("CL", [128, S], F32)
        RP = sb("RP", [128, S], BF16)
        VP = sb("VP", [128, S], BF16)
        AA = sb("AA", [128, S], BF16)
        K2 = sb("K2", [128, S], BF16)
        SQb = sb("SQb", [128, S], BF16)
        OUT = [sb("OUT%d" % i, [128, S], BF16) for i in range(2)]
        PCt = sb("PCt", [128, NCH], F32)
        ps = [st.enter_context(nc.psum_tensor("ps4_%d" % i, [128, 512], F32)) for i in range(4)]
        for i, ap in enumerate(K.rw_prm):
            P.dma("sp", prm[:, i, :], ap, writes=[("prm", i)])
        prk = [("prm", i) for i in range(10)]
        P.op("dve", lambda e: e.tensor_scalar(out=prm[:, 10, :], in0=prm[:, 6, :], scalar1=-1.0, scalar2=1.0,
                                               op0=ALU.mult, op1=ALU.add), reads=prk, writes=[("prm", 10)])
        prk = prk + [("prm", 10)]
        P.dma("sp", mul[:], K.rw_mul, writes=["mul"])
        P.op("pool", lambda e: e.memset(bones[:], 0.0), writes=["bones"])
        P.op("pool", lambda e: e.memset(bones[0:64, 0:64], 1.0), reads=["bones"], writes=["bones"])
        P.op("pool", lambda e: e.memset(bones[64:128, 64:128], 1.0), reads=["bones"], writes=["bones"])
        P.op("pool", lambda e: e.iota(PT[:].rearrange("p (c t) -> p c t", t=64), pattern=[[0, NCH], [1, 64]], base=0,
                                      channel_multiplier=0, allow_small_or_imprecise_dtypes=True), writes=["PT"])
        P.op("dve", lambda e: e.tensor_scalar(out=M01[:], in0=PT[:], scalar1=0.5, scalar2=None, op0=ALU.is_gt),
             reads=["PT"], writes=["M01"])
        P.dma("sp", wst[0:96, 0, :], K.rw_w_up, writes=["wst"])
        P.op("act", lambda e: e.copy(out=wup[:], in_=wst[0:96, 0, :]), reads=["wst"], writes=["wup"])
        P.dma("sp", wst[0:96, 1, :], K.rw_a_up, reads=[], writes=["wst1"])
        P.op("act", lambda e: e.copy(out=aup[:], in_=wst[0:96, 1, :]), reads=["wst1"], writes=["aup"])
        P.dma("sp", wst[:, :, :], K.rw_g_up.rearrange("(c p) n -> p c n", p=128), reads=[], writes=["wst", "wst1"])
        P.op("act", lambda e: e.copy(out=gup[:], in_=wst[:]), reads=["wst", "wst1"], writes=["gup"])
        for (r0, n, mcol, func, dst, kd) in ((768, 96, 0, AF.Tanh, txw[:, :], "txw"), (864, 96, 1, AF.Copy, xap[:, :], "xap"),
                                             (960, 128, 2, AF.Sigmoid, sxg[:, 0, :], "sxg0"),
                                             (1088, 128, 3, AF.Sigmoid, sxg[:, 1, :], "sxg1")):
            P.dma("sp", PT[0:n, :], K.yT_d[r0:r0 + n, :], writes=["PT"])
            tok_shift(P, KP, PT, KK, mul[0:n, mcol:mcol + 1], "PT", "KK", "KP", n=n)
            P.op("act", lambda e, n=n, func=func, dst=dst: e.activation(out=dst, in_=KP[0:n, :], func=func),
                 reads=["KP"], writes=[kd])
        lk = ["txw", "xap", "sxg0", "sxg1"]
        oc = 0
        for ct in cts:
            c0 = ct * 128
            P.dma("sp", PT[:], K.yT_d[c0:c0 + 128, :], writes=["PT"])
            tok_shift(P, RP, PT, KK, prm[:, 0, ct:ct + 1], "PT", "KK", "RP")
            P.dma("sp", PT[:], K.yT_d[256 + c0:256 + c0 + 128, :], writes=["PT"])
            tok_shift(P, KP, PT, KK, prm[:, 1, ct:ct + 1], "PT", "KK", "KP")
            P.dma("sp", PT[:], K.yT_d[512 + c0:512 + c0 + 128, :], writes=["PT"])
            tok_shift(P, VP, PT, KK, prm[:, 2, ct:ct + 1], "PT", "KK", "VP")
            P.dma("sp", K.vb_d[c0:c0 + 128, :], VP[:], reads=["VP"], writes=[("vb_d", ct)])
            for blk in range(8):
                bs = slice(blk * 512, (blk + 1) * 512)
                p0, p1, p2 = ps[0], ps[1], ps[2]
                P.op("pe", lambda e, bs=bs, c0=c0: e.matmul(ps[0][:, :], lhsT=wup[:, c0:c0 + 128], rhs=txw[:, bs],
                                                             start=True, stop=True), reads=["wup", "txw"], writes=[("ps4", 0)])
                P.op("act", lambda e, bs=bs, ct=ct: e.activation(out=CL[:, bs], in_=ps[0][:, :], func=AF.Sigmoid,
                                                                  bias=prm[:, 3, ct:ct + 1]),
                     reads=[("ps4", 0)] + prk, writes=["CL"])
                P.op("pe", lambda e, bs=bs, c0=c0: e.matmul(ps[1][:, :], lhsT=aup[:, c0:c0 + 128], rhs=xap[:, bs],
                                                             start=True, stop=True), reads=["aup", "xap"], writes=[("ps4", 1)])
                P.op("act", lambda e, bs=bs, ct=ct: e.activation(out=AA[:, bs], in_=ps[1][:, :], func=AF.Sigmoid,
                                                                  bias=prm[:, 4, ct:ct + 1]),
                     reads=[("ps4", 1)] + prk, writes=["AA"])
                for cc in range(2):
                    P.op("pe", lambda e, bs=bs, c0=c0, cc=cc: e.matmul(ps[2][:, :], lhsT=gup[:, cc, c0:c0 + 128],
                                                                       rhs=sxg[:, cc, bs], start=(cc == 0), stop=(cc == 1)),
                         reads=["gup", "sxg0", "sxg1"], writes=[("ps4", 2)])
                o = OUT[oc % 2]
                P.op("dve", lambda e, bs=bs, o=o: e.tensor_copy(out=o[:, bs], in_=ps[2][:, :]),
                     reads=[("ps4", 2)], writes=[("OUT", oc % 2)])
            P.dma("sp", K.G_d[c0:c0 + 128, :], OUT[oc % 2][:], reads=[("OUT", oc % 2)], writes=[("G_d", ct)])
            oc += 1
            P.op("dve", lambda e: e.tensor_scalar(out=CL[:], in0=CL[:], scalar1=-0.6065306597126334, scalar2=None,
                                                   op0=ALU.mult), reads=["CL"], writes=["CL"])
            P.op("dve", lambda e, ct=ct: e.tensor_scalar(out=KK[:], in0=KP[:], scalar1=prm[:, 5, ct:ct + 1], scalar2=None,
                                                          op0=ALU.mult), reads=["KP"] + prk, writes=["KK"])
            P.op("act", lambda e: e.activation(out=SQb[:], in_=KK[:], func=AF.Square), reads=["KK"], writes=["SQb"])
            for blk in range(8):
                bs = slice(blk * 512, (blk + 1) * 512)
                P.op("pe", lambda e, bs=bs: e.matmul(ps[3][:, :], lhsT=bones[:], rhs=SQb[:, bs], start=True, stop=True),
                     reads=["bones", "SQb"], writes=[("ps4", 3)])
                P.op("act", lambda e, bs=bs: e.activation(out=PT[:, bs], in_=ps[3][:, :], func=AF.Sqrt),
                     reads=[("ps4", 3)], writes=["PT"])
            P.op("dve", lambda e: e.tensor_scalar(out=PT[:], in0=PT[:], scalar1=1e-12, scalar2=None, op0=ALU.max),
                 reads=["PT"], writes=["PT"])
            P.op("dve", lambda e: e.reciprocal(out=PT[:], in_=PT[:]), reads=["PT"], writes=["PT"])
            P.op("dve", lambda e: e.tensor_tensor(out=KK[:], in0=KK[:], in1=PT[:], op=ALU.mult), reads=["KK", "PT"], writes=["KK"])
            P.op("dve", lambda e, ct=ct: e.tensor_scalar(out=PT[:], in0=AA[:], scalar1=prm[:, 6, ct:ct + 1],
                                                          scalar2=prm[:, 10, ct:ct + 1], op0=ALU.mult, op1=ALU.add),
                 reads=["AA", "PT"] + prk, writes=["PT"])
            P.op("dve", lambda e: e.tensor_tensor(out=K2[:], in0=KP[:], in1=PT[:], op=ALU.mult), reads=["KP", "PT"], writes=["K2"])
            P.op("dve", lambda e, ct=ct: e.scalar_tensor_tensor(out=SQb[:], in0=RP[:], scalar=prm[:, 7, ct:ct + 1], in1=K2[:],
                                                                 op0=ALU.mult, op1=ALU.mult),
                 reads=["RP", "K2", "SQb"] + prk, writes=["SQb"])
            o = OUT[oc % 2]
            for blk in range(8):
                bs = slice(blk * 512, (blk + 1) * 512)
                P.op("pe", lambda e, bs=bs: e.matmul(ps[3][:, :], lhsT=bones[:], rhs=SQb[:, bs], start=True, stop=True),
                     reads=["bones", "SQb"], writes=[("ps4", 3)])
                P.op("dve", lambda e, bs=bs, o=o: e.tensor_tensor(out=o[:, bs], in0=ps[3][:, :], in1=VP[:, bs], op=ALU.mult),
                     reads=[("ps4", 3), "VP"], writes=[("OUT", oc % 2)])
            P.dma("sp", K.BON_d[c0:c0 + 128, :], o[:], reads=[("OUT", oc % 2)], writes=[("BON_d", ct)])
            oc += 1
            P.op("dve", lambda e: e.tensor_tensor_scan(out=PT[:], data0=M01[:], data1=CL[:], initial=0.0,
                                                        op0=ALU.mult, op1=ALU.add), reads=["M01", "CL", "PT"], writes=["PT"])
            P.op("pool", lambda e: e.tensor_tensor(out=CL[:], in0=PT[:], in1=CL[:], op=ALU.subtract),
                 reads=["PT", "CL"], writes=["CL"])
            P.op("act", lambda e: e.activation(out=CL[:], in_=CL[:], func=AF.Exp), reads=["CL"], writes=["CL"])
            v3 = lambda t: t[:].rearrange("p (c t) -> p c t", t=64)
            o = OUT[oc % 2]
            P.op("dve", lambda e, o=o: e.scalar_tensor_tensor(out=o[:], in0=KK[:], scalar=-1.0, in1=CL[:],
                                                               op0=ALU.mult, op1=ALU.mult),
                 reads=["KK", "CL"], writes=[("OUT", oc % 2)])
            P.dma("sp", K.AH_d[c0:c0 + 128, :], o[:], reads=[("OUT", oc % 2)], writes=[("AH_d", ct)])
            oc += 1
            P.op("act", lambda e: e.activation(out=CL[:], in_=PT[:], func=AF.Exp), reads=["PT", "CL"], writes=["CL"])
            o = OUT[oc % 2]
            P.op("dve", lambda e, o=o: e.tensor_tensor(out=o[:], in0=RP[:], in1=CL[:], op=ALU.mult),
                 reads=["RP", "CL"], writes=[("OUT", oc % 2)])
            P.dma("sp", K.RH_d[c0:c0 + 128, :], o[:], reads=[("OUT", oc % 2)], writes=[("RH_d", ct)])
            oc += 1
            P.op("pool", lambda e: e.tensor_copy(out=PCt[:], in_=v3(CL)[:, :, 63]), reads=["CL"], writes=["PCt"])
            P.dma("sp", K.PC_d[c0:c0 + 128, :], PCt[:], reads=["PCt"], writes=[("PC_d", ct)])
            P.op("act", lambda e: e.activation(out=PT[:], in_=PT[:], func=AF.Exp, scale=-1.0), reads=["PT"], writes=["PT"])
            o = OUT[oc % 2]
            P.op("dve", lambda e, o=o: e.tensor_tensor(out=o[:], in0=K2[:], in1=PT[:], op=ALU.mult),
                 reads=["K2", "PT"], writes=[("OUT", oc % 2)])
            P.dma("sp", K.KH_d[c0:c0 + 128, :], o[:], reads=[("OUT", oc % 2)], writes=[("KH_d", ct)])
            oc += 1
            P.op("dve", lambda e: e.tensor_tensor(out=KK[:], in0=KK[:], in1=AA[:], op=ALU.mult), reads=["KK", "AA"], writes=["KK"])
            o = OUT[oc % 2]
            P.op("dve", lambda e, o=o: e.tensor_tensor(out=o[:], in0=KK[:], in1=PT[:], op=ALU.mult),
                 reads=["KK", "PT"], writes=[("OUT", oc % 2)])
            P.dma("sp", K.BH_d[c0:c0 + 128, :], o[:], reads=[("OUT", oc % 2)], writes=[("BH_d", ct)])
            oc += 1
        P.flush()

def phase4c_rwkv_scan(K, heads=range(4)):
    nc, P = K.nc, K.P
    with contextlib.ExitStack() as st:
        def sb(name, shape, dt):
            return st.enter_context(nc.sbuf_tensor(name, shape, dt))
        ident = sb("ident4", [128, 128], BF16)
        make_ident(K, P, ident)
        MaskG = sb("MaskG", [64, 4, 128], F32)
        MaskX = sb("MaskX", [64, 8, 64], F32)
        I8 = sb("I8", [64, 64], F32)
        ones = sb("ones4", [64, 64], F32)
        P.op("pool", lambda e: e.memset(ones[:], 1.0), writes=["ones"])
        for a in range(4):
            for cq in range(2):
                P.op("pool", lambda e, cq=cq, a=a: e.affine_select(
                    out=MaskG[:, a, cq * 64:(cq + 1) * 64], in_=ones[:], pattern=[[1, 64]],
                    compare_op=(ALU.is_gt if cq == 0 else ALU.is_ge), fill=0.0, base=0, channel_multiplier=-1),
                    reads=["ones"], writes=["MaskG"])
        for a in range(8):
            P.op("pool", lambda e, a=a: e.affine_select(out=MaskX[:, a, :], in_=ones[:], pattern=[[-1, 64]],
                                                         compare_op=ALU.is_gt, fill=0.0, base=0, channel_multiplier=1),
                 reads=["ones"], writes=["MaskX"])
        P.op("dve", lambda e: e.tensor_copy(out=I8[:], in_=ident[0:64, 0:64]), reads=["ident"], writes=["I8"])
        AH = sb("AH", [64, S], RD)
        RH = sb("RH", [64, S], RD)
        BH = sb("BH", [64, S], RD)
        KH = sb("KH", [64, S], RD)
        vb = sb("vb", [64, S], BF16)
        PC = sb("PC", [64, NCH], F32)
        ARh = sb("ARh", [64, NCH, 128], RD)
        BKh = sb("BKh", [64, NCH, 128], RD)
        GmB = sb("GmB", [64, NCH, 128], RD)
        GmK = sb("GmK", [64, NCH, 128], RD)
        Btok = sb("Btok", [64, NCH, 64], RD)
        Ktok = sb("Ktok", [64, NCH, 64], RD)
        Vtok = sb("Vtok", [64, NCH, 64], RD)
        X0 = sb("X0", [64, NCH, 64], RD)
        Pm = sb("Pm", [64, NCH, 64], RD)
        oT = sb("oT", [64, S], F32)
        Ast = sb("Ast", [64, 64], F32)
        Abf = sb("Abf", [64, 64], RD)
        Tt = sb("Tt", [64, 64], F32)
        Xs = sb("Xs", [64, 64], RD)
        Us = sb("Us", [64, 64], RD)
        PSb = st.enter_context(nc.psum_tensor("PSb", [128, 1024], BF16))
        PS = [st.enter_context(nc.psum_tensor("PS%d" % i, [128, 512], F32)) for i in range(7)]
        v3 = lambda t: t[:].rearrange("p (c t) -> p c t", t=64)
        Nb = [v3(AH), v3(RH)]
        Xb = [v3(BH), v3(KH)]
        Nk = ["AH", "RH"]
        Xk = ["BH", "KH"]
        for hd in heads:
            r0 = hd * 64
            P.dma("sp", AH[:], K.AH_d[r0:r0 + 64, :], writes=["AH"])
            P.dma("sp", RH[:], K.RH_d[r0:r0 + 64, :], writes=["RH"])
            P.dma("sp", BH[:], K.BH_d[r0:r0 + 64, :], writes=["BH"])
            P.dma("sp", KH[:], K.KH_d[r0:r0 + 64, :], writes=["KH"])
            P.dma("sp", vb[:], K.vb_d[r0:r0 + 64, :], writes=["vb"])
            P.dma("sp", PC[:], K.PC_d[r0:r0 + 64, :], writes=["PC"])
            P.op("dve", lambda e: e.tensor_copy(out=ARh[:, :, 0:64], in_=v3(AH)), reads=["AH"], writes=["ARh"])
            P.op("pool", lambda e: e.tensor_copy(out=ARh[:, :, 64:128], in_=v3(RH)), reads=["RH"], writes=["ARh"])
            P.op("dve", lambda e: e.tensor_copy(out=BKh[:, :, 0:64], in_=v3(BH)), reads=["BH"], writes=["BKh"])
            P.op("pool", lambda e: e.tensor_copy(out=BKh[:, :, 64:128], in_=v3(KH)), reads=["KH"], writes=["BKh"])
            for (src, srck, col0, dst, dk) in ((BKh, "BKh", 0, Btok, "Btok"), (BKh, "BKh", 64, Ktok, "Ktok"), (None, "vb", 0, Vtok, "Vtok")):
                for c16 in range(0, NCH, 16):
                    for cc in range(16):
                        c = c16 + cc
                        in_ = vb[:, c * 64:(c + 1) * 64] if src is None else src[:, c, col0:col0 + 64]
                        P.op("pe", lambda e, cc=cc, in_=in_: e.transpose(out=PSb[0:64, cc * 64:(cc + 1) * 64], in_=in_,
                                                                         identity=ident[0:64, 0:64]),
                             reads=[srck, "ident"], writes=["PSb"])
                    P.op("act", lambda e, c16=c16, dst=dst: e.copy(out=dst[:, c16:c16 + 16, :].rearrange("p c k -> p (c k)"),
                                                                    in_=PSb[0:64, :]), reads=["PSb"], writes=[dk])
            gi = 0
            for (col0, dst, dk) in ((0, GmB, "GmB"), (64, GmK, "GmK")):
                for c4 in range(0, NCH, 4):
                    b = gi % 2
                    gi += 1
                    for cc in range(4):
                        c = c4 + cc
                        P.op("pe", lambda e, c=c, cc=cc, b=b, col0=col0: e.matmul(
                            PS[b][0:64, cc * 128:(cc + 1) * 128], lhsT=BKh[:, c, col0:col0 + 64], rhs=ARh[:, c, :],
                            start=True, stop=True), reads=["BKh", "ARh"], writes=[("PS", b)])
                    P.op("dve", lambda e, c4=c4, dst=dst, b=b: e.tensor_tensor(
                        out=dst[:, c4:c4 + 4, :], in0=PS[b][0:64, :].rearrange("p (a t) -> p a t", t=128), in1=MaskG[:],
                        op=ALU.mult), reads=[("PS", b), "MaskG"], writes=[dk])
            for c8 in range(0, NCH, 8):
                for cc in range(8):
                    c = c8 + cc
                    P.op("pe", lambda e, c=c, cc=cc: e.matmul(PS[2][0:64, cc * 64:(cc + 1) * 64], lhsT=ARh[:, c, 0:64],
                                                               rhs=BKh[:, c, 0:64], start=True, stop=True),
                         reads=["ARh", "BKh"], writes=[("PS", 2)])
                P.op("dve", lambda e, c8=c8: e.tensor_tensor(
                    out=X0[:, c8:c8 + 8, :], in0=PS[2][0:64, :].rearrange("p (a t) -> p a t", t=64), in1=MaskX[:],
                    op=ALU.mult), reads=[("PS", 2), "MaskX"], writes=["X0"])
            N0 = GmB[:, :, 0:64]
            P.op("pool", lambda e, N0=N0: e.tensor_tensor(out=Pm[:], in0=N0, in1=I8[:].unsqueeze(1).to_broadcast([64, NCH, 64]),
                                                          op=ALU.add), reads=["GmB", "I8"], writes=["Pm"])
            curN, curNk = N0, "GmB"
            curX, curXk = X0[:], "X0"
            for lvl in range(1, 6):
                nX, nXk = Xb[lvl % 2], Xk[lvl % 2]
                nN, nNk = Nb[lvl % 2], Nk[lvl % 2]
                for c8 in range(0, NCH, 8):
                    pb_ = (c8 // 8) % 2
                    for cc in range(8):
                        c = c8 + cc
                        P.op("pe", lambda e, c=c, cc=cc, curN=curN, curX=curX, pb_=pb_: e.matmul(
                            PS[0 + pb_][0:64, cc * 64:(cc + 1) * 64], lhsT=curN[:, c, :], rhs=curX[:, c, :], start=True, stop=True),
                            reads=[curNk, curXk], writes=[("PS", 0 + pb_)])
                    P.op("act", lambda e, c8=c8, nX=nX, pb_=pb_: e.copy(out=nX[:, c8:c8 + 8, :],
                                                                in_=PS[0 + pb_][0:64, :].rearrange("p (a t) -> p a t", t=64)),
                         reads=[("PS", 0 + pb_)], writes=[nXk])
                    if lvl < 5:
                        for cc in range(8):
                            c = c8 + cc
                            P.op("pe", lambda e, c=c, cc=cc, curN=curN, curX=curX, pb_=pb_: e.matmul(
                                PS[2 + pb_][0:64, cc * 64:(cc + 1) * 64], lhsT=curX[:, c, :], rhs=curN[:, c, :], start=True, stop=True),
                                reads=[curNk, curXk], writes=[("PS", 2 + pb_)])
                        P.op("dve", lambda e, c8=c8, nN=nN, pb_=pb_: e.tensor_copy(out=nN[:, c8:c8 + 8, :],
                                                                           in_=PS[2 + pb_][0:64, :].rearrange("p (a t) -> p a t", t=64)),
                             reads=[("PS", 2 + pb_)], writes=[nNk])
                    for cc in range(8):
                        c = c8 + cc
                        P.op("pe", lambda e, c=c, cc=cc, nX=nX, pb_=pb_: e.matmul(
                            PS[4 + pb_][0:64, cc * 64:(cc + 1) * 64], lhsT=nX[:, c, :], rhs=Pm[:, c, :], start=True, stop=True),
                            reads=[nXk, "Pm"], writes=[("PS", 4 + pb_)])
                    P.op("dve", lambda e, c8=c8, pb_=pb_: e.tensor_tensor(
                        out=Pm[:, c8:c8 + 8, :], in0=PS[4 + pb_][0:64, :].rearrange("p (a t) -> p a t", t=64),
                        in1=Pm[:, c8:c8 + 8, :], op=ALU.add), reads=[("PS", 4 + pb_), "Pm"], writes=["Pm"])
                curN, curNk, curX, curXk = nN, nNk, nX, nXk
            P.op("pool", lambda e: e.memset(Ast[:], 0.0), writes=["Ast"])
            P.op("pool", lambda e: e.memset(Abf[:], 0.0), writes=["Abf"])
            for c in range(NCH):
                P.op("pool", lambda e, c=c: e.tensor_scalar(out=Tt[:], in0=Ast[:], scalar1=PC[:, c:c + 1], scalar2=0.0,
                                                             op0=ALU.mult, op1=ALU.add), reads=["Ast", "PC"], writes=["Tt"])
                P.op("pe", lambda e, c=c: e.matmul(PS[0][0:64, 0:64], lhsT=ARh[:, c, 0:64], rhs=Abf[:], start=True, stop=False),
                     reads=["ARh", "Abf"], writes=[("PS", 0)])
                P.op("pe", lambda e, c=c: e.matmul(PS[0][0:64, 0:64], lhsT=GmK[:, c, 0:64], rhs=Vtok[:, c, :], start=False, stop=True),
                     reads=["GmK", "Vtok"], writes=[("PS", 0)])
                P.op("act", lambda e: e.copy(out=Xs[:], in_=PS[0][0:64, 0:64]), reads=[("PS", 0)], writes=["Xs"])
                P.op("pe", lambda e, c=c: e.matmul(PS[1][0:64, 0:64], lhsT=Pm[:, c, :], rhs=Xs[:], start=True, stop=True),
                     reads=["Pm", "Xs"], writes=[("PS", 1)])
                P.op("dve", lambda e: e.tensor_copy(out=Us[:], in_=PS[1][0:64, 0:64]), reads=[("PS", 1)], writes=["Us"])
                P.op("pe", lambda e, c=c: e.matmul(PS[6][0:64, 0:64], lhsT=Btok[:, c, :], rhs=Us[:], start=True, stop=False),
                     reads=["Btok", "Us"], writes=[("PS", 6)])
                P.op("pe", lambda e, c=c: e.matmul(PS[6][0:64, 0:64], lhsT=Ktok[:, c, :], rhs=Vtok[:, c, :], start=False, stop=True),
                     reads=["Ktok", "Vtok"], writes=[("PS", 6)])
                ob = 2 + (c % 2)
                P.op("pe", lambda e, c=c, ob=ob: e.matmul(PS[ob][0:64, 0:64], lhsT=Abf[:], rhs=ARh[:, c, 64:128], start=True, stop=False),
                     reads=["Abf", "ARh"], writes=[("PS", ob)])
                P.op("pe", lambda e, c=c, ob=ob: e.matmul(PS[ob][0:64, 0:64], lhsT=Us[:], rhs=GmB[:, c, 64:128], start=False, stop=False),
                     reads=["Us", "GmB"], writes=[("PS", ob)])
                P.op("pe", lambda e, c=c, ob=ob: e.matmul(PS[ob][0:64, 0:64], lhsT=Vtok[:, c, :], rhs=GmK[:, c, 64:128], start=False, stop=True),
                     reads=["Vtok", "GmK"], writes=[("PS", ob)])
                P.op("dve", lambda e, c=c: e.scalar_tensor_tensor(out=Abf[:], in0=PS[6][0:64, 0:64], scalar=PC[:, c:c + 1], in1=Tt[:],
                                                                   op0=ALU.mult, op1=ALU.add),
                     reads=[("PS", 6), "Tt", "PC"], writes=["Abf"])
                P.op("dve", lambda e, c=c: e.scalar_tensor_tensor(out=Ast[:], in0=PS[6][0:64, 0:64], scalar=PC[:, c:c + 1], in1=Tt[:],
                                                                   op0=ALU.mult, op1=ALU.add),
                     reads=[("PS", 6), "Tt", "PC"], writes=["Ast"])
                P.op("act", lambda e, c=c, ob=ob: e.copy(out=oT[:, c * 64:(c + 1) * 64], in_=PS[ob][0:64, 0:64]),
                     reads=[("PS", ob)], writes=[("oT", c // 8)])
            P.dma("sp", K.oT_d[r0:r0 + 64, :], oT[:], reads=[("oT", q) for q in range(8)], writes=[("oT_d", hd)])
        P.flush()

def phase4d_rwkv_post(K, cts=range(2)):
    nc, P = K.nc, K.P
    with contextlib.ExitStack() as st:
        def sb(name, shape, dt):
            return st.enter_context(nc.sbuf_tensor(name, shape, dt))
        ident = sb("ident4d", [128, 128], BF16)
        make_ident(K, P, ident)
        bonesf = sb("bonesf", [128, 128], F32)
        P.op("pool", lambda e: e.memset(bonesf[:], 0.0), writes=["bonesf"])
        P.op("pool", lambda e: e.memset(bonesf[0:64, 0:64], 1.0), reads=["bonesf"], writes=["bonesf"])
        P.op("pool", lambda e: e.memset(bonesf[64:128, 64:128], 1.0), reads=["bonesf"], writes=["bonesf"])
        prm = sb("prm4d", [128, 2, 2], F32)
        P.dma("sp", prm[:, 0, :], K.rw_prm[8], writes=["prm0"])
        P.dma("sp", prm[:, 1, :], K.rw_prm[9], writes=["prm1"])
        o = sb("o4d", [128, S], F32)
        osq = sb("osq", [128, S], F32)
        bon = sb("bon", [128, S], BF16)
        gg = sb("gg", [128, S], BF16)
        Mb = [sb("Mb%d" % i, [128, 512], F32) for i in range(2)]
        Vb = [sb("Vb%d" % i, [128, 512], F32) for i in range(2)]
        Yb = [sb("Yb%d" % i, [128, 512], F32) for i in range(2)]
        Ob = [sb("Ob%d" % i, [128, 512], BF16) for i in range(2)]
        Tk = [sb("Tk%d" % i, [128, 4, 128], BF16) for i in range(2)]
        ps = [st.enter_context(nc.psum_tensor("p4d_%d" % i, [128, 512], F32)) for i in range(4)]
        pst = [st.enter_context(nc.psum_tensor("p4dt_%d" % i, [128, 4, 128], BF16)) for i in range(2)]
        it = 0
        for ct in cts:
            c0 = ct * 128
            P.dma("sp", o[:], K.oT_d[c0:c0 + 128, :], writes=["o"])
            P.dma("sp", bon[:], K.BON_d[c0:c0 + 128, :], writes=["bon"])
            P.dma("sp", gg[:], K.G_d[c0:c0 + 128, :], writes=["gg"])
            P.op("act", lambda e: e.activation(out=osq[:], in_=o[:], func=AF.Square), reads=["o"], writes=["osq"])
            for blk in range(8):
                s2 = it % 2
                it += 1
                bs = slice(blk * 512, (blk + 1) * 512)
                P.op("pe", lambda e, bs=bs, s2=s2: e.matmul(ps[s2][:, :], lhsT=bonesf[:], rhs=o[:, bs], start=True, stop=True),
                     reads=["bonesf", "o"], writes=[("p4d", s2)])
                P.op("pe", lambda e, bs=bs, s2=s2: e.matmul(ps[2 + s2][:, :], lhsT=bonesf[:], rhs=osq[:, bs], start=True, stop=True),
                     reads=["bonesf", "osq"], writes=[("p4d", 2 + s2)])
                P.op("act", lambda e, s2=s2: e.activation(out=Mb[s2][:], in_=ps[s2][:, :], func=AF.Copy, scale=1.0 / 64),
                     reads=[("p4d", s2)], writes=[("Mb", s2)])
                P.op("pool", lambda e, s2=s2: e.tensor_tensor(out=Vb[s2][:], in0=Mb[s2][:], in1=Mb[s2][:], op=ALU.mult),
                     reads=[("Mb", s2)], writes=[("Vb", s2)])
                P.op("dve", lambda e, s2=s2: e.scalar_tensor_tensor(out=Vb[s2][:], in0=ps[2 + s2][:, :], scalar=1.0 / 64, in1=Vb[s2][:],
                                                                     op0=ALU.mult, op1=ALU.subtract),
                     reads=[("p4d", 2 + s2), ("Vb", s2)], writes=[("Vb", s2)])
                P.op("dve", lambda e, s2=s2: e.tensor_scalar(out=Vb[s2][:], in0=Vb[s2][:], scalar1=64e-5, scalar2=None, op0=ALU.add),
                     reads=[("Vb", s2)], writes=[("Vb", s2)])
                P.op("act", lambda e, s2=s2: e.activation(out=Vb[s2][:], in_=Vb[s2][:], func=AF.Sqrt),
                     reads=[("Vb", s2)], writes=[("Vb", s2)])
                P.op("dve", lambda e, s2=s2: e.reciprocal(out=Vb[s2][:], in_=Vb[s2][:]), reads=[("Vb", s2)], writes=[("Vb", s2)])
                P.op("pool", lambda e, s2=s2, bs=bs: e.tensor_tensor(out=Yb[s2][:], in0=o[:, bs], in1=Mb[s2][:], op=ALU.subtract),
                     reads=["o", ("Mb", s2)], writes=[("Yb", s2)])
                P.op("dve", lambda e, s2=s2: e.tensor_tensor(out=Yb[s2][:], in0=Yb[s2][:], in1=Vb[s2][:], op=ALU.mult),
                     reads=[("Yb", s2), ("Vb", s2)], writes=[("Yb", s2)])
                P.op("dve", lambda e, s2=s2, ct=ct: e.tensor_scalar(out=Yb[s2][:], in0=Yb[s2][:], scalar1=prm[:, 0, ct:ct + 1],
                                                                     scalar2=prm[:, 1, ct:ct + 1], op0=ALU.mult, op1=ALU.add),
                     reads=[("Yb", s2), "prm0", "prm1"], writes=[("Yb", s2)])
                P.op("pool", lambda e, s2=s2, bs=bs: e.tensor_tensor(out=Yb[s2][:], in0=Yb[s2][:], in1=bon[:, bs], op=ALU.add),
                     reads=[("Yb", s2), "bon"], writes=[("Yb", s2)])
                P.op("dve", lambda e, s2=s2, bs=bs: e.tensor_tensor(out=Ob[s2][:], in0=Yb[s2][:], in1=gg[:, bs], op=ALU.mult),
                     reads=[("Yb", s2), "gg"], writes=[("Ob", s2)])
                for q in range(4):
                    P.op("pe", lambda e, s2=s2, q=q: e.transpose(out=pst[s2][:, q, :], in_=Ob[s2][:, q * 128:(q + 1) * 128],
                                                                 identity=ident[:]),
                         reads=[("Ob", s2), "ident"], writes=[("p4dt", s2)])
                P.op("act", lambda e, s2=s2: e.copy(out=Tk[s2][:], in_=pst[s2][:]), reads=[("p4dt", s2)], writes=[("Tk", s2)])
                P.dma("sp", K.ro_loc_d[blk // 4].rearrange("(t p) c -> p t c", p=128)[:, (blk % 4) * 4:(blk % 4 + 1) * 4, c0:c0 + 128], Tk[s2][:],
                      reads=[("Tk", s2)], writes=[("ro_tok_d", ct, blk)])
        P.flush()


def phase4e_allgather(K):
    P = K.P
    for hh in range(2):
        P.coll(lambda e, hh=hh: e.collective_compute("AllGather", ALU.bypass, replica_groups=[[0, 1, 2, 3], [4, 5, 6, 7]],
                                                     ins=[K.ro_loc_d[hh].opt()], outs=[K.ro_all_d[hh].opt()]),
               reads=[("ro_loc", hh)], writes=[("ro_all", hh)])
    P.flush()


def phase5a_select(K):
    nc, P = K.nc, K.P
    with contextlib.ExitStack() as st:
        def sb(name, shape, dt):
            return st.enter_context(nc.sbuf_tensor(name, shape, dt))
        ro = sb("ro_tok", [128, 32, 1024], BF16)
        selT = sb("selT", [128, 32, 1024], BF16)
        qrow = sb("qrow", [128, 1024], F32)
        tki = sb("tki", [128, 32], I32)
        tkf = sb("tkf", [128, 32], F32)
        mo = [sb("mo%d" % i, [128, 512], BF16) for i in range(2)]
        at = sb("at5", [128, 8, 1024], BF16)
        ps = [st.enter_context(nc.psum_tensor("p5a_%d" % i, [128, 512], F32)) for i in range(2)]
        for q4 in range(4):
            for hh in range(2):
                P.dma("sp", ro[:, hh * 16:(hh + 1) * 16, q4 * 256:(q4 + 1) * 256],
                      K.ro_all_d[hh][q4 * 2048:(q4 + 1) * 2048, :].rearrange("(t p) c -> p t c", p=128), writes=[("ro", q4, hh)])
        rok = [("ro", q4, hh) for q4 in range(4) for hh in range(2)]
        P.dma("sp", qrow[:], bcast_rows(K.qpos_row, 1024), writes=["qrow"])
        P.op("pool", lambda e: e.iota(tki[:], pattern=[[128, 32]], base=0, channel_multiplier=1), writes=["tki"])
        P.op("dve", lambda e: e.tensor_copy(out=tkf[:], in_=tki[:]), reads=["tki"], writes=["tkf"])
        for T in range(32):
            P.op("dve", lambda e, T=T: e.tensor_scalar(out=selT[:, T, :], in0=qrow[:], scalar1=tkf[:, T:T + 1], scalar2=0.0,
                                                      op0=ALU.is_equal, op1=ALU.add), reads=["qrow", "tkf"], writes=[("selT", T)])
        sk = [("selT", T) for T in range(32)]
        P.dma("sp", at[:], K.attT_d.rearrange("h p t -> p h t"), writes=["at5"])
        P.dma("sp", K.mixT_d.rearrange("k p t -> p k t")[:, 0:8, :], at[:], reads=["at5"], writes=["mixa"])
        i = 0
        for m in range(8):
            for half in range(2):
                s2 = i % 2
                i += 1
                for T in range(32):
                    P.op("pe", lambda e, T=T, m=m, half=half, s2=s2: e.matmul(
                        ps[s2][:, :], lhsT=ro[:, T, m * 128:(m + 1) * 128], rhs=selT[:, T, half * 512:(half + 1) * 512],
                        start=(T == 0), stop=(T == 31)), reads=rok + sk, writes=[("p5a", s2)])
                P.op("act", lambda e, s2=s2: e.copy(out=mo[s2][:], in_=ps[s2][:, :]), reads=[("p5a", s2)], writes=[("mo", s2)])
                P.dma("sp", K.mixT_d[8 + m, :, half * 512:(half + 1) * 512], mo[s2][:], reads=[("mo", s2)], writes=[("mixr", m, half)])
        P.flush()


def phase5b_outproj(K):
    nc, P = K.nc, K.P
    with contextlib.ExitStack() as st:
        def sb(name, shape, dt):
            return st.enter_context(nc.sbuf_tensor(name, shape, dt))
        ident = sb("ident5", [128, 128], BF16)
        make_ident(K, P, ident)
        G2, SH2 = load_G_SH(K, P, st, 3, 4, K.norm2_g, "p5")
        GT1 = sb("GT1", [128, D], F32)
        P.dma("sp", GT1[:], bcast_rows(K.mod_d[2 * D:3 * D], D), writes=["GT1"])
        Wo = sb("Wo", [128, 16, D], BF16)
        stg = [sb("wstg5_%d" % i, [128, 4, 512], F32) for i in range(2)]
        wk = load_weight_bf16(K, P, stg, Wo, 0, K.w_out, D, "Wo")
        mixT = sb("mixT", [128, 16, 512], BF16)
        T = norm_tiles_alloc(K, st, "p5")
        x1 = T["xt"]
        hT = [sb("hT5_0", [128, 16, 512], BF16)] * 2
        xo = [sb("xo%d" % i, [128, D], F32) for i in range(2)]
        ps = [st.enter_context(nc.psum_tensor("p5b_%d" % i, [128, 512], F32)) for i in range(2)]
        ss, junk, hb, pT = T["ss"], T["junk"], T["hb"], T["pT"]
        gi = 0
        for blk in range(2):
            hs = 0
            P.dma("sp", mixT[:], K.mixT_d.rearrange("k p t -> p k t")[:, :, blk * 512:(blk + 1) * 512], writes=["mixT"])
            for ti in range(4):
                t = blk * 4 + ti
                xs = t % 2
                P.dma("sp", xo[xs][:], K.x_own[t * 128:(t + 1) * 128, :], writes=[("xo", xs)])
                for cg in range(4):
                    b = gi % 2
                    gi += 1
                    for k in range(16):
                        P.op("pe", lambda e, b=b, k=k, t=t, cg=cg: e.matmul(
                            ps[b][:, :], lhsT=mixT[:, k, (t % 4) * 128:(t % 4 + 1) * 128], rhs=Wo[:, k, cg * 512:(cg + 1) * 512],
                            start=(k == 0), stop=(k == 15)), reads=["mixT"] + wk, writes=[("p5b", b)])
                    cs = slice(cg * 512, (cg + 1) * 512)
                    P.op("dve", lambda e, b=b, xs=xs, cs=cs: e.tensor_tensor(out=x1[xs][:, cs], in0=ps[b][:, :], in1=GT1[:, cs], op=ALU.mult),
                         reads=[("p5b", b), "GT1"], writes=[("xt", xs)])
                    P.op("pool", lambda e, xs=xs, cs=cs: e.tensor_tensor(out=x1[xs][:, cs], in0=x1[xs][:, cs], in1=xo[xs][:, cs], op=ALU.add),
                         reads=[("xt", xs), ("xo", xs)], writes=[("xt", xs)])
                P.dma("sp", K.x1_d[t * 128:(t + 1) * 128, :], x1[xs][:], reads=[("xt", xs)], writes=[("x1_d", t)])
                P.op("act", lambda e, xs=xs: e.activation(out=junk[:], in_=x1[xs][:], func=AF.Square, accum_out=ss[:, 0:1]),
                     reads=[("xt", xs)], writes=["junk", "ss0"])
                P.op("dve", lambda e: e.tensor_scalar(out=ss[:, 1:2], in0=ss[:, 0:1], scalar1=1.0 / D, scalar2=1e-6,
                                                       op0=ALU.mult, op1=ALU.add), reads=["ss0"], writes=["ss1"])
                P.op("act", lambda e: e.activation(out=ss[:, 2:3], in_=ss[:, 1:2], func=AF.Sqrt), reads=["ss1"], writes=["ss2"])
                P.op("dve", lambda e: e.reciprocal(out=ss[:, 3:4], in_=ss[:, 2:3]), reads=["ss2"], writes=["ss3"])
                P.op("dve", lambda e, xs=xs: e.scalar_tensor_tensor(out=x1[xs][:], in0=x1[xs][:], scalar=ss[:, 3:4], in1=G2[:],
                                                                   op0=ALU.mult, op1=ALU.mult),
                     reads=[("xt", xs), "ss3", "G"], writes=[("xt", xs)])
                P.op("pool", lambda e, xs=xs: e.tensor_tensor(out=hb[xs][:], in0=x1[xs][:], in1=SH2[:], op=ALU.add),
                     reads=[("xt", xs), "SH"], writes=[("hb", xs)])
                for half in range(2):
                    for kk in range(8):
                        k = half * 8 + kk
                        P.op("pe", lambda e, k=k, kk=kk, half=half, xs=xs: e.transpose(
                            out=pT[half][:, kk, :], in_=hb[xs][:, k * 128:(k + 1) * 128], identity=ident[:]),
                            reads=[("hb", xs), "ident"], writes=[("pT", half)])
                    o_ = hT[hs][:, half * 8:(half + 1) * 8, ti * 128:(ti + 1) * 128]
                    if half == 0:
                        P.op("act", lambda e, o_=o_, half=half: e.copy(out=o_, in_=pT[half][:]), reads=[("pT", half)], writes=[("hT5", hs, ti, half)])
                    else:
                        P.op("dve", lambda e, o_=o_, half=half: e.tensor_copy(out=o_, in_=pT[half][:]), reads=[("pT", half)], writes=[("hT5", hs, ti, half)])
            P.dma("sp", K.h2T_d.rearrange("k p t -> p k t")[:, :, blk * 512:(blk + 1) * 512], hT[hs][:],
                  reads=[("hT5", hs, ti, half) for ti in range(4) for half in range(2)], writes=[("h2T_d", blk)])
        P.flush()


def phase5c_ffn(K):
    nc, P = K.nc, K.P
    NF = 5632 // 128
    with contextlib.ExitStack() as st:
        def sb(name, shape, dt):
            return st.enter_context(nc.sbuf_tensor(name, shape, dt))
        h2T = sb("h2T", [128, 16, OWN], BF16)
        P.dma("sp", h2T[:], K.h2T_d.rearrange("k p t -> p k t"), writes=["h2T"])
        ao = [sb("ao%d" % i, [128, 512], BF16) for i in range(2)]
        stg = [sb("wstg6_%d" % i, [128, 4, 512], F32) for i in range(4)]
        Wg = [sb("Wg%d" % i, [128, 16, 512], BF16) for i in range(2)]
        Wu = [sb("Wu%d" % i, [128, 16, 512], BF16) for i in range(2)]
        sg = [sb("sg%d" % i, [128, 512], F32) for i in range(2)]
        ps = [st.enter_context(nc.psum_tensor("p5c_%d" % i, [128, 512], F32)) for i in range(4)]
        gi = 0

        def load_group(fg, defer=None):
            ws = fg % 2
            load_weight_bf16(K, P, stg, Wg[ws], 0, K.w_ffn_gate[:, fg * 512:(fg + 1) * 512], 512, ("Wg", ws), defer=defer)
            load_weight_bf16(K, P, stg, Wu[ws], 0, K.w_ffn_up[:, fg * 512:(fg + 1) * 512], 512, ("Wu", ws), defer=defer)
        load_group(0)
        for fg in range(11):
            ws = fg % 2
            pend = []
            if fg + 1 < 11:
                load_group(fg + 1, defer=pend)
            for f4 in range(4):
                f = fg * 4 + f4
                for tb in range(2):
                    b = gi % 2
                    gi += 1
                    if pend:
                        pend.pop(0)()
                    for k in range(16):
                        P.op("pe", lambda e, b=b, k=k, f4=f4, tb=tb, ws=ws: e.matmul(
                            ps[b][:, :], lhsT=Wg[ws][:, k, f4 * 128:(f4 + 1) * 128], rhs=h2T[:, k, tb * 512:(tb + 1) * 512],
                            start=(k == 0), stop=(k == 15)), reads=["h2T", (("Wg", ws), 0, (k // 4) * 4)], writes=[("p5c", b)])
                    for k in range(16):
                        P.op("pe", lambda e, b=b, k=k, f4=f4, tb=tb, ws=ws: e.matmul(
                            ps[2 + b][:, :], lhsT=Wu[ws][:, k, f4 * 128:(f4 + 1) * 128], rhs=h2T[:, k, tb * 512:(tb + 1) * 512],
                            start=(k == 0), stop=(k == 15)), reads=["h2T", (("Wu", ws), 0, (k // 4) * 4)], writes=[("p5c", 2 + b)])
                    P.op("act", lambda e, b=b: e.activation(out=sg[b][:], in_=ps[b][:, :], func=AF.Silu),
                         reads=[("p5c", b)], writes=[("sg", b)])
                    P.op("dve", lambda e, b=b: e.tensor_tensor(out=ao[b][:], in0=ps[2 + b][:, :], in1=sg[b][:], op=ALU.mult),
                         reads=[("p5c", 2 + b), ("sg", b)], writes=[("ao", b)])
                    P.dma("sp", K.actT_d[f, :, tb * 512:(tb + 1) * 512], ao[b][:], reads=[("ao", b)], writes=[("actT_d", f, tb)])
        P.flush()
    with contextlib.ExitStack() as st:
        def sb(name, shape, dt):
            return st.enter_context(nc.sbuf_tensor(name, shape, dt))
        GT2 = sb("GT2", [128, D], F32)
        P.dma("sp", GT2[:], bcast_rows(K.mod_d[5 * D:6 * D], D), writes=["GT2"])
        actT = sb("actT", [128, NF, OWN], BF16)
        for q in range(4):
            P.dma("sp", actT[:, q * 11:(q + 1) * 11, :], K.actT_d.rearrange("f p t -> p f t")[:, q * 11:(q + 1) * 11, :], writes=[("actT", q)])
        ak = [("actT", q) for q in range(4)]
        stg = [sb("wstg7_%d" % i, [128, 4, 256], F32) for i in range(4)]
        ps = [st.enter_context(nc.psum_tensor("p5d_%d" % i, [128, 512], F32)) for i in range(2)]
        gi = 0
        Wd = [sb("Wd%d" % i, [128, NF, 256], BF16) for i in range(2)]
        x1 = [sb("x1_%d" % i, [128, 256], F32) for i in range(2)]
        yo = [sb("yo%d" % i, [128, 256], F32) for i in range(2)]
        wdv = K.w_ffn_down.rearrange("(k p) n -> p k n", p=128)
        engs = ["pool", "dve", "act"]

        def load_wd(cg, defer=None):
            wsl = cg % 2
            for k0 in range(0, NF, 4):
                if defer is not None:
                    defer.append(lambda k0=k0: load_wd_piece(cg, wsl, k0))
                else:
                    load_wd_piece(cg, wsl, k0)

        def load_wd_piece(cg, wsl, k0):
            if True:
                i = K.wcnt
                K.wcnt += 1
                sl = i % 4
                P.dma("sp", stg[sl][:, 0:4, 0:256], wdv[:, k0:k0 + 4, cg * 256:(cg + 1) * 256], writes=[("wstg", sl)])
                eng = engs[i % 3]
                o_ = Wd[wsl][:, k0:k0 + 4, :]
                if eng == "act":
                    P.op("act", lambda e, o_=o_, sl=sl: e.copy(out=o_, in_=stg[sl][:, 0:4, 0:256]), reads=[("wstg", sl)], writes=[("Wd", wsl, k0)])
                else:
                    P.op(eng, lambda e, o_=o_, sl=sl: e.tensor_copy(out=o_, in_=stg[sl][:, 0:4, 0:256]), reads=[("wstg", sl)], writes=[("Wd", wsl, k0)])
        load_wd(0)
        for cg in range(8):
            wsl = cg % 2
            cs = slice(cg * 256, (cg + 1) * 256)
            pend = []
            if cg + 1 < 8:
                load_wd(cg + 1, defer=pend)
            for t in range(8):
                b = gi % 2
                gi += 1
                for _ in range(2):
                    if pend:
                        pend.pop(0)()
                P.dma("sp", x1[b][:], K.x1_d[t * 128:(t + 1) * 128, cs], writes=[("x1", b)])
                for f in range(NF):
                    P.op("pe", lambda e, b=b, f=f, t=t, wsl=wsl: e.matmul(ps[b][:, 0:256], lhsT=actT[:, f, t * 128:(t + 1) * 128], rhs=Wd[wsl][:, f, :],
                                                                          start=(f == 0), stop=(f == NF - 1)),
                         reads=[("actT", f // 11), ("Wd", wsl, (f // 4) * 4)], writes=[("p5c", b)])
                P.op("dve", lambda e, b=b, cs=cs: e.tensor_tensor(out=yo[b][:], in0=ps[b][:, 0:256], in1=GT2[:, cs], op=ALU.mult),
                     reads=[("p5c", b), "GT2"], writes=[("yo", b)])
                P.op("pool", lambda e, b=b: e.tensor_tensor(out=yo[b][:], in0=yo[b][:], in1=x1[b][:], op=ALU.add),
                     reads=[("yo", b), ("x1", b)], writes=[("yo", b)])
                P.dma("sp", K.out[t * 128:(t + 1) * 128, cs], yo[b][:], reads=[("yo", b)], writes=[("out", t, cg)])
        P.flush()


def phase_final_copy(K):
    nc, P = K.nc, K.P
    with contextlib.ExitStack() as st:
        xt = [st.enter_context(nc.sbuf_tensor("fx%d" % i, [128, D], F32)) for i in range(2)]
        for t in range(8):
            s = t % 2
            P.dma("sp", xt[s][:], K.x_own[t * 128:(t + 1) * 128, :], writes=[("fx", s)])
            P.dma("sp", K.out[t * 128:(t + 1) * 128, :], xt[s][:], reads=[("fx", s)], writes=[("out", t)])
        P.flush()


def own_tiles(j):
    r = []
    for m in range(4):
        r += [8 * m + j, 8 * m + 7 - j]
    return r


def build_program(debug=False, stages=99, cts=range(2), dbg_list=None, skip_att=False):
    nc = bass.Bass("TRN2", target_bir_lowering=False)
    K = Ctx()
    K.stages = stages
    K.cts = cts
    K.skip_att = skip_att
    K.nc = nc
    K.dbg = {}
    K.wcnt = 0

    def inp(name, shape, dt=F32):
        return nc.dram_tensor(name, list(shape), dt, kind="ExternalInput").ap()

    def scratch(name, shape, dt):
        return nc.dram_tensor(name, list(shape), dt, kind="Internal").ap()

    K.x_full = inp("x_full", [S, D])
    K.x_own = inp("x_own", [OWN, D])
    K.c_arr = inp("c_arr", [128, 16])
    K.pos_full = inp("pos_full", [128, 32], I32)
    K.invf_att = inp("invf_att", [128, 16])
    K.invf_idx = inp("invf_idx", [128, 8])
    K.w_ada = inp("w_ada", [D, 3072])
    K.b_ada = inp("b_ada", [3072])
    K.norm1_g = inp("norm1_g", [D])
    K.k_norm_g = inp("k_norm_g", [128])
    K.q_norm_g = inp("q_norm_g", [128])
    K.pos_own = inp("pos_own", [128, 8], I32)
    K.qpos_own = inp("qpos_own", [128, 8])
    K.w_in = inp("w_in", [D, 4176])
    K.rw_prm = [inp("rwp%d" % i, [128, 2]) for i in range(10)]
    K.w_in_rw = inp("w_in_rw", [D, 1216])
    K.rw_mul = inp("rw_mul", [128, 4])
    K.rw_w_up = inp("rw_w_up", [96, 256])
    K.rw_a_up = inp("rw_a_up", [96, 256])
    K.rw_g_up = inp("rw_g_up", [256, 256])
    K.qpos_row = inp("qpos_row", [OWN])
    K.w_out = inp("w_out", [D, D])
    K.norm2_g = inp("norm2_g", [D])
    K.w_ffn_gate = inp("w_ffn_gate", [D, 5632])
    K.w_ffn_up = inp("w_ffn_up", [D, 5632])
    K.w_ffn_down = inp("w_ffn_down", [5632, D])
    K.out = nc.dram_tensor("y_own", [OWN, D], F32, kind="ExternalOutput").ap()
    K.modq_d = scratch("modq_d", [1, 3072], F32)
    K.mod4_d = scratch("mod4_d", [4, 3072], F32)
    K.mod_d = K.mod4_d.rearrange("a n -> (a n)")
    K.hT_d = scratch("hT_d", [16, 128, S], BF16)
    K.kT_d = scratch("kT_d", [8, 128, S], BF16)
    K.v_d = scratch("v_d", [S, 8 * 129], BF16)
    K.ikT_d = scratch("ikT_d", [64, S], BF16)
    K.yT_d = scratch("yT_d", [1216, S], F32)
    K.qT_d = scratch("qT_d", [8, 128, OWN], BF16)
    K.iqT_d = scratch("iqT_d", [64, OWN, 16], BF16)
    K.iw_d = scratch("iw_d", [OWN, 16], F32)
    K.attT_d = scratch("attT_d", [8, 128, OWN], BF16)
    for nm in ("vb_d", "G_d", "BON_d", "AH_d", "RH_d", "BH_d", "KH_d"):
        setattr(K, nm, scratch(nm, [256, S], BF16))
    K.PC_d = scratch("PC_d", [256, NCH], F32)
    K.oT_d = scratch("oT_d", [256, S], F32)
    K.ro_loc_d = [scratch("ro_loc%d_d" % i, [2048, 256], BF16) for i in range(2)]
    K.ro_all_d = [scratch("ro_all%d_d" % i, [8192, 256], BF16) for i in range(2)]
    K.mixT_d = scratch("mixT_d", [16, 128, OWN], BF16)
    K.x1_d = scratch("x1_d", [OWN, D], F32)
    K.h2T_d = scratch("h2T_d", [16, 128, OWN], BF16)
    K.actT_d = scratch("actT_d", [44, 128, OWN], BF16)
    with contextlib.ExitStack() as stack:
        K.P = Prog(nc, stack)
        phase0_adaln(K)
        phase1_kv(K)
        if K.stages >= 2:
            phase1b_rwkv_proj(K)
        if K.stages >= 3 and not getattr(K, "skip_att", False):
            phase2_own_proj(K)
            phase3_attention(K)
        if K.stages >= 4:
            phase4b_rwkv_prep(K, cts=K.cts)
            if K.stages >= 5:
                phase4c_rwkv_scan(K, heads=[h for ct in K.cts for h in (2 * ct, 2 * ct + 1)])
        if K.stages >= 6:
            phase4d_rwkv_post(K, cts=K.cts)
            phase4e_allgather(K)
        if K.stages >= 7:
            phase5a_select(K)
            phase5b_outproj(K)
            phase5c_ffn(K)
        else:
            phase_final_copy(K)
        if debug:
            P = K.P
            allc = (("dbg_mixT", K.mixT_d, [16, 128, OWN], BF16), ("dbg_x1", K.x1_d, [OWN, D], F32),
                    ("dbg_oT", K.oT_d, [256, S], F32), ("dbg_AH", K.AH_d, [256, S], BF16), ("dbg_BH", K.BH_d, [256, S], BF16),
                    ("dbg_KH", K.KH_d, [256, S], BF16), ("dbg_RH", K.RH_d, [256, S], BF16), ("dbg_PC", K.PC_d, [256, NCH], F32),
                    ("dbg_G", K.G_d, [256, S], BF16), ("dbg_BON", K.BON_d, [256, S], BF16), ("dbg_vb", K.vb_d, [256, S], BF16),
                    ("dbg_yT", K.yT_d, [1216, S], F32), ("dbg_attT", K.attT_d, [8, 128, OWN], BF16),
                                     ("dbg_qT", K.qT_d, [8, 128, OWN], BF16), ("dbg_iqT", K.iqT_d, [64, OWN, 16], BF16),
                                     ("dbg_iw", K.iw_d, [OWN, 16], F32))
            for nm, src, shp, dt in allc:
                if dbg_list is not None and nm not in dbg_list:
                    continue
                o = dbg_out(K, nm, shp, dt)
                P.dma("sp", o, src, writes=[nm])
            P.flush()
    return nc, K


def make_in_maps(inputs, cores=range(8)):
    x = np.asarray(inputs["x"], dtype=np.float32)
    c = np.asarray(inputs["c"], dtype=np.float32)
    pos = np.asarray(inputs["positions"], dtype=np.int32)
    invf_att = (np.float32(500000.0) ** (-np.arange(16, dtype=np.float32) / np.float32(16))).astype(np.float32)
    invf_idx = (np.float32(500000.0) ** (-np.arange(8, dtype=np.float32) / np.float32(8))).astype(np.float32)
    mu = np.asarray(inputs["rwkv_mu"][0], dtype=np.float32)

    vecs = [mu[0:1024], mu[1024:2048], mu[2048:3072], inputs["rwkv_w0"][0], inputs["rwkv_a0"][0], inputs["rwkv_k_k"][0],
            inputs["rwkv_k_a"][0], np.asarray(inputs["rwkv_r_k"][0]).reshape(-1), inputs["rwkv_lnx_g"][0], inputs["rwkv_lnx_b"][0]]
    w_in_full = np.asarray(inputs["w_in"][0], dtype=np.float32)
    rw_mul = np.zeros((128, 4), np.float32)
    rw_mul[:96, 0] = mu[3072:3168]
    rw_mul[:96, 1] = mu[3168:3264]
    rw_mul[:, 2] = mu[3264:3392]
    rw_mul[:, 3] = mu[3392:3520]
    maps = []
    for core in cores:
        b, j = core // 4, core % 4
        ch = slice(256 * j, 256 * j + 256)
        rwp = {"rwp%d" % i: np.ascontiguousarray(np.asarray(v, dtype=np.float32)[ch].reshape(2, 128).T) for i, v in enumerate(vecs)}
        R0 = 4176
        w_in_rw = np.ascontiguousarray(np.concatenate([w_in_full[:, R0 + 256 * j:R0 + 256 * j + 256],
                                                       w_in_full[:, R0 + 1024 + 256 * j:R0 + 1024 + 256 * j + 256],
                                                       w_in_full[:, R0 + 2048 + 256 * j:R0 + 2048 + 256 * j + 256],
                                                       w_in_full[:, R0 + 3072:R0 + 3520]], axis=1))
        tiles = own_tiles(j)
        idx = np.concatenate([np.arange(t * 128, (t + 1) * 128) for t in tiles])
        maps.append({
            "x_full": np.ascontiguousarray(x[b]),
            "x_own": np.ascontiguousarray(x[b][idx]),
            "c_arr": np.ascontiguousarray(c[b].reshape(16, 128).T),
            "pos_full": np.ascontiguousarray(pos[b].reshape(32, 128).T),
            "invf_att": np.ascontiguousarray(np.broadcast_to(invf_att, (128, 16))),
            "invf_idx": np.ascontiguousarray(np.broadcast_to(invf_idx, (128, 8))),
            "w_ada": np.ascontiguousarray(np.asarray(inputs["w_ada"][0], dtype=np.float32)[:, 3072 * j:3072 * (j + 1)]),
            "b_ada": np.ascontiguousarray(np.asarray(inputs["b_ada"][0], dtype=np.float32)[3072 * j:3072 * (j + 1)]),
            "norm1_g": np.asarray(inputs["norm1_g"][0], dtype=np.float32),
            "k_norm_g": np.asarray(inputs["k_norm_g"][0], dtype=np.float32),
            "q_norm_g": np.asarray(inputs["q_norm_g"][0], dtype=np.float32),
            "pos_own": np.ascontiguousarray(pos[b][idx].reshape(8, 128).T),
            "qpos_own": np.ascontiguousarray(idx.astype(np.float32).reshape(8, 128).T),
            "w_in": np.ascontiguousarray(w_in_full[:, 0:4176]),
            "qpos_row": idx.astype(np.float32),
            "w_out": np.asarray(inputs["w_out"][0], dtype=np.float32),
            "norm2_g": np.asarray(inputs["norm2_g"][0], dtype=np.float32),
            "w_ffn_gate": np.asarray(inputs["w_ffn_gate"][0], dtype=np.float32),
            "w_ffn_up": np.asarray(inputs["w_ffn_up"][0], dtype=np.float32),
            "w_ffn_down": np.asarray(inputs["w_ffn_down"][0], dtype=np.float32),
            "rw_w_up": np.ascontiguousarray(np.asarray(inputs["rwkv_w_up"][0], dtype=np.float32)[:, ch]),
            "rw_a_up": np.ascontiguousarray(np.asarray(inputs["rwkv_a_up"][0], dtype=np.float32)[:, ch]),
            "rw_g_up": np.ascontiguousarray(np.asarray(inputs["rwkv_g_up"][0], dtype=np.float32)[:, ch]),
            "w_in_rw": w_in_rw,
            "rw_mul": rw_mul,
            **rwp,
        })
    return maps


def kernel(**inputs):
    nc, K = build_program(debug=False)
    maps = make_in_maps(inputs)
    res = run_bass_kernel_spmd(nc, maps, core_ids=list(range(8)))
    out = np.zeros((2, S, D), dtype=np.float32)
    for core in range(8):
        b, j = core // 4, core % 4
        y = res.results[core]["y_own"]
        for i, t in enumerate(own_tiles(j)):
            out[b, t * 128:(t + 1) * 128] = y[i * 128:(i + 1) * 128]
    return out
```

```python
import contextlib
import numpy as np
import concourse.bass as bass
import concourse.mybir as mybir
from concourse.bass_utils import run_bass_kernel_spmd

F32 = mybir.dt.float32
BF16 = mybir.dt.bfloat16
I32 = mybir.dt.int32
AF = mybir.ActivationFunctionType
ALU = mybir.AluOpType
AX = mybir.AxisListType

D = 2048
S = 4096
NT = 32
OWN = 1024
ENGS = ("pe", "act", "dve", "pool", "sp")
DEBUG = {}


class _Op:
    __slots__ = ("eng", "fn", "deps", "needs_inc", "is_dma", "sem", "count", "idx", "prev_same_sem", "is_cc")

    def __init__(self, eng, fn, is_dma):
        self.eng = eng
        self.fn = fn
        self.deps = set()
        self.needs_inc = False
        self.is_dma = is_dma
        self.sem = None
        self.count = 0
        self.prev_same_sem = None
        self.is_cc = False


class Prog:
    def __init__(self, nc, stack, n_dma_sems=48):
        self.nc = nc
        self.n_dma_sems = n_dma_sems
        self.eng_sem = {e: stack.enter_context(nc.semaphore("s_" + e)) for e in ENGS}
        self.dma_sems = [stack.enter_context(nc.semaphore("d%d" % i)) for i in range(n_dma_sems)]
        self.bar_sem = stack.enter_context(nc.semaphore("bar"))
        self.cc_sem = stack.enter_context(nc.semaphore("ccs"))
        self.cc_cnt = 0
        self.cnt = {e: 0 for e in ENGS}
        self.dcnt = [0] * n_dma_sems
        self.rr = 0
        self.nbar = 0
        self._reset()

    def _reset(self):
        self.ops = []
        self.last_writer = {}
        self.readers = {}

    def _record(self, op, reads, writes):
        idx = len(self.ops)
        op.idx = idx
        deps = set()
        for k in reads:
            w = self.last_writer.get(k)
            if w is not None:
                deps.add(w)
        for k in writes:
            w = self.last_writer.get(k)
            if w is not None:
                deps.add(w)
            for r in self.readers.get(k, ()):
                deps.add(r)
        deps.discard(idx)
        op.deps = deps
        self.ops.append(op)
        for k in reads:
            self.readers.setdefault(k, []).append(idx)
        for k in writes:
            self.last_writer[k] = idx
            self.readers[k] = []
        return idx

    def op(self, eng, fn, reads=(), writes=()):
        return self._record(_Op(eng, fn, False), reads, writes)

    def dma(self, queue, out, in_, reads=(), writes=(), **kw):
        def fn(e, out=out, in_=in_, kw=kw):
            return e.dma_start(out=out, in_=in_, **kw)
        return self._record(_Op(queue, fn, True), reads, writes)

    def coll(self, fn, reads=(), writes=()):
        o = _Op("pool", fn, True)
        o.is_cc = True
        return self._record(o, reads, writes)

    def flush(self):
        nc = self.nc
        ops = self.ops
        for o in ops:
            nd = set()
            for d in o.deps:
                p = ops[d]
                if o.eng == "pe" and p.eng == "pe" and not p.is_dma and not o.is_dma:
                    continue
                nd.add(d)
                p.needs_inc = True
            o.deps = nd
        last_of = {}
        for o in ops:
            if not o.is_dma:
                last_of[o.eng] = o
        for o in last_of.values():
            o.needs_inc = True
        dlast = [None] * self.n_dma_sems
        for o in ops:
            if o.is_cc:
                self.cc_cnt += 1
                o.sem = self.cc_sem
                o.count = self.cc_cnt
            elif o.is_dma:
                s = self.rr % self.n_dma_sems
                self.rr += 1
                o.prev_same_sem = dlast[s]
                self.dcnt[s] += 16
                o.sem = self.dma_sems[s]
                o.count = self.dcnt[s]
                dlast[s] = o.idx
            elif o.needs_inc:
                self.cnt[o.eng] += 1
                o.sem = self.eng_sem[o.eng]
                o.count = self.cnt[o.eng]
        per_eng = {e: [o for o in ops if o.eng == e] for e in ENGS}
        final = [(self.dma_sems[s], self.dcnt[s]) for s in range(self.n_dma_sems) if self.dcnt[s] > 0]
        final += [(self.eng_sem[e], self.cnt[e]) for e in ENGS if self.cnt[e] > 0]
        if self.cc_cnt > 0:
            final.append((self.cc_sem, self.cc_cnt))
        self.nbar += 1
        nbar = self.nbar
        bar = self.bar_sem

        def run(e_name, eng):
            waited = {}
            for o in per_eng[e_name]:
                need = {}
                for d in o.deps:
                    p = ops[d]
                    if need.get(p.sem.num, (0, None))[0] < p.count:
                        need[p.sem.num] = (p.count, p.sem)
                if o.is_dma and o.prev_same_sem is not None:
                    p = ops[o.prev_same_sem]
                    if need.get(p.sem.num, (0, None))[0] < p.count:
                        need[p.sem.num] = (p.count, p.sem)
                for key, (c, s) in need.items():
                    if waited.get(key, 0) < c:
                        eng.wait_ge(s, c)
                        waited[key] = c
                ins = o.fn(eng)
                if o.is_cc:
                    ins.then_inc(o.sem)
                elif o.is_dma:
                    ins.then_inc(o.sem, 16)
                elif o.needs_inc:
                    ins.then_inc(o.sem, 1)
            if e_name == "sp":
                for s, c in final:
                    eng.wait_ge(s, c)
                eng.sem_inc(bar, 1)
            eng.wait_ge(bar, nbar)

        with nc.Block() as block:
            @block.tensor
            def _(e):
                run("pe", e)

            @block.scalar
            def _(e):
                run("act", e)

            @block.vector
            def _(e):
                run("dve", e)

            @block.gpsimd
            def _(e):
                run("pool", e)

            @block.sync
            def _(e):
                run("sp", e)
        self._reset()


class Ctx:
    pass


def bcast_rows(ap1d, n):
    return bass.AP(ap1d.tensor, ap1d.offset, [[0, 128], [1, n]])


def dbg_out(K, name, shape, dtype=F32):
    t = K.nc.dram_tensor(name, list(shape), dtype, kind="ExternalOutput")
    K.dbg[name] = t
    return t.ap()


def make_ident(K, P, ident):
    P.op("pool", lambda e: e.memset(ident[:], 0.0), writes=["ident"])
    P.op("pool", lambda e: e.affine_select(out=ident[:], in_=ident[:], pattern=[[-1, 128]],
                                           compare_op=ALU.not_equal, fill=1.0, base=0,
                                           channel_multiplier=1),
         reads=["ident"], writes=["ident"])


def phase0_adaln(K):
    nc, P = K.nc, K.P
    NQ = 3072
    with contextlib.ExitStack() as st:
        c_sb = st.enter_context(nc.sbuf_tensor("c_sb", [128, 16], F32))
        cact = st.enter_context(nc.sbuf_tensor("cact", [128, 16], F32))
        wst = [st.enter_context(nc.sbuf_tensor("wst%d" % i, [128, 16, 512], F32)) for i in range(2)]
        modrow = st.enter_context(nc.sbuf_tensor("modrow", [1, NQ], F32))
        brow = st.enter_context(nc.sbuf_tensor("brow", [1, NQ], F32))
        ps = [st.enter_context(nc.psum_tensor("ps0_%d" % i, [1, 512], F32)) for i in range(2)]
        P.dma("sp", c_sb[:], K.c_arr, writes=["c_sb"])
        P.dma("sp", brow[:], K.b_ada.rearrange("(o n) -> o n", o=1), writes=["brow"])
        P.op("act", lambda e: e.activation(out=cact[:], in_=c_sb[:], func=AF.Silu),
             reads=["c_sb"], writes=["cact"])
        wv = K.w_ada.rearrange("(k p) n -> p k n", p=128)
        for nt in range(NQ // 512):
            sl = nt % 2
            for hh in range(2):
                P.dma("sp", wst[sl][:, hh * 8:(hh + 1) * 8, :],
                      wv[:, hh * 8:(hh + 1) * 8, nt * 512:(nt + 1) * 512],
                      writes=[("wst", sl, hh)])
            for k in range(16):
                P.op("pe", lambda e, k=k, sl=sl: e.matmul(ps[sl][:, :], lhsT=cact[:, k:k + 1],
                                                         rhs=wst[sl][:, k, :], start=(k == 0), stop=(k == 15)),
                     reads=["cact", ("wst", sl, k // 8)], writes=[("ps0", sl)])
            P.op("dve", lambda e, nt=nt, sl=sl: e.tensor_tensor(
                out=modrow[0:1, nt * 512:(nt + 1) * 512], in0=ps[sl][:, :],
                in1=brow[0:1, nt * 512:(nt + 1) * 512], op=ALU.add),
                reads=[("ps0", sl), "brow"], writes=[("modrow", nt)])
        P.dma("sp", K.modq_d, modrow[:],
              reads=[("modrow", nt) for nt in range(NQ // 512)], writes=["modq_d"])
        P.flush()
    P.coll(lambda e: e.collective_compute("AllGather", ALU.bypass, replica_groups=[[0, 1, 2, 3], [4, 5, 6, 7]],
                                          ins=[K.modq_d.opt()], outs=[K.mod4_d.opt()]), reads=["modq_d"], writes=["mod4"])
    P.flush()


def load_mod_rows(K, P, tile, which, gain_ap=None, key=None):
    src = K.mod_d[which * D:(which + 1) * D]
    P.dma("sp", tile[:], bcast_rows(src, D), writes=[key])


def bc(ap, shape):
    return ap.to_broadcast(list(shape))


def load_weight_bf16(K, P, st_tiles, dst, c_dst, src2d, ncols, tag, defer=None):
    wv = src2d.rearrange("(k p) n -> p k n", p=128)
    nk = wv.shape[1]
    engs = ["pool", "dve", "act"]
    for c0 in range(0, ncols, 512):
        n = min(512, ncols - c0)
        for k0 in range(0, nk, 4):
            kn = min(4, nk - k0)
            if defer is not None:
                defer.append(lambda c0=c0, n=n, k0=k0, kn=kn: _load_piece(K, P, st_tiles, dst, c_dst, wv, tag, engs, c0, n, k0, kn))
                continue
            _load_piece(K, P, st_tiles, dst, c_dst, wv, tag, engs, c0, n, k0, kn)
    return [(tag, c0, k0) for c0 in range(0, ncols, 512) for k0 in range(0, nk, 4)]


def _load_piece(K, P, st_tiles, dst, c_dst, wv, tag, engs, c0, n, k0, kn):
    if True:
        if True:
            i = K.wcnt
            K.wcnt += 1
            sl = i % len(st_tiles)
            stg = st_tiles[sl]
            P.dma("sp", stg[:, 0:kn, 0:n], wv[:, k0:k0 + kn, c0:c0 + n], writes=[("wstg", sl)])
            eng = engs[i % 3]
            o = dst[:, k0:k0 + kn, c_dst + c0:c_dst + c0 + n]
            if eng == "act":
                P.op("act", lambda e, o=o, stg=stg, kn=kn, n=n: e.copy(out=o, in_=stg[:, 0:kn, 0:n]),
                     reads=[("wstg", sl)], writes=[(tag, c0, k0)])
            else:
                P.op(eng, lambda e, o=o, stg=stg, kn=kn, n=n: e.tensor_copy(out=o, in_=stg[:, 0:kn, 0:n]),
                     reads=[("wstg", sl)], writes=[(tag, c0, k0)])


def rope_tables(K, P, st, pos_arr, ntile, invf_att, invf_idx, tag):
    nc = K.nc
    posi = st.enter_context(nc.sbuf_tensor(tag + "posi", [128, ntile], I32))
    posf = st.enter_context(nc.sbuf_tensor(tag + "posf", [128, ntile], F32))
    iva = st.enter_context(nc.sbuf_tensor(tag + "iva", [128, 16], F32))
    ivi = st.enter_context(nc.sbuf_tensor(tag + "ivi", [128, 8], F32))
    P.dma("sp", posi[:], pos_arr, writes=[tag + "posi"])
    P.dma("sp", iva[:], invf_att, writes=[tag + "iva"])
    P.dma("sp", ivi[:], invf_idx, writes=[tag + "ivi"])
    P.op("dve", lambda e: e.tensor_copy(out=posf[:], in_=posi[:]), reads=[tag + "posi"], writes=[tag + "posf"])
    out = {}
    for nm, iv, h in (("a", iva, 16), ("i", ivi, 8)):
        u = st.enter_context(nc.sbuf_tensor(tag + "u" + nm, [128, ntile, h], F32))
        ui = st.enter_context(nc.sbuf_tensor(tag + "ui" + nm, [128, ntile, h], I32))
        uf = st.enter_context(nc.sbuf_tensor(tag + "uf" + nm, [128, ntile, h], F32))
        for fn, off in (("sin", 0.0), ("cos", 0.25)):
            tb = st.enter_context(nc.sbuf_tensor(tag + fn + nm, [128, ntile, h], F32))
            kk = tag + fn + nm
            P.op("dve", lambda e, u=u, iv=iv, h=h: e.tensor_tensor(
                out=u[:], in0=bc(posf[:].unsqueeze(2), [128, ntile, h]),
                in1=bc(iv[:].unsqueeze(1), [128, ntile, h]), op=ALU.mult),
                reads=[tag + "posf", tag + "iv" + nm], writes=[tag + "U" + nm])
            P.op("dve", lambda e, u=u, off=off: e.tensor_scalar(
                out=u[:], in0=u[:], scalar1=float(1.0 / (2 * np.pi)), scalar2=off, op0=ALU.mult, op1=ALU.add),
                reads=[tag + "U" + nm], writes=[tag + "U" + nm])
            P.op("dve", lambda e, u=u, ui=ui: e.tensor_copy(out=ui[:], in_=u[:]), reads=[tag + "U" + nm], writes=[tag + "UI" + nm])
            P.op("dve", lambda e, uf=uf, ui=ui: e.tensor_copy(out=uf[:], in_=ui[:]), reads=[tag + "UI" + nm], writes=[tag + "UF" + nm])
            P.op("dve", lambda e, u=u, uf=uf: e.tensor_tensor(out=u[:], in0=u[:], in1=uf[:], op=ALU.subtract),
                 reads=[tag + "U" + nm, tag + "UF" + nm], writes=[tag + "U" + nm])
            P.op("dve", lambda e, u=u: e.tensor_scalar(out=u[:], in0=u[:], scalar1=-0.5, scalar2=0.5,
                                                        op0=ALU.max, op1=ALU.min),
                 reads=[tag + "U" + nm], writes=[tag + "U" + nm])
            P.op("act", lambda e, u=u, tb=tb: e.activation(out=tb[:], in_=u[:], func=AF.Sin,
                                                            scale=float(2 * np.pi)),
                 reads=[tag + "U" + nm], writes=[kk])
            out[fn + nm] = (tb, kk)
    return out


def apply_rope(P, eng, x4, cos, sin, t, half, tmp, rk, wk, sfx=""):
    ctb, ck = cos
    stb, sk = sin
    H = x4.shape[1]
    x1 = x4[:, :, 0:half]
    x2 = x4[:, :, half:2 * half]
    cb = bc(ctb[:, t, :].unsqueeze(1), [128, H, half])
    sb = bc(stb[:, t, :].unsqueeze(1), [128, H, half])
    a, b2, c, d = tmp
    P.op(eng, lambda e: e.tensor_tensor(out=a[:, 0:H, 0:half], in0=x1, in1=cb, op=ALU.mult), reads=rk + [ck], writes=["rtmpA" + sfx])
    P.op(eng, lambda e: e.tensor_tensor(out=b2[:, 0:H, 0:half], in0=x2, in1=sb, op=ALU.mult), reads=rk + [sk], writes=["rtmpB" + sfx])
    P.op(eng, lambda e: e.tensor_tensor(out=c[:, 0:H, 0:half], in0=x2, in1=cb, op=ALU.mult), reads=rk + [ck], writes=["rtmpC" + sfx])
    P.op(eng, lambda e: e.tensor_tensor(out=d[:, 0:H, 0:half], in0=x1, in1=sb, op=ALU.mult), reads=rk + [sk], writes=["rtmpD" + sfx])
    P.op(eng, lambda e: e.tensor_tensor(out=x1, in0=a[:, 0:H, 0:half], in1=b2[:, 0:H, 0:half], op=ALU.subtract),
         reads=["rtmpA" + sfx, "rtmpB" + sfx, "rtmpC" + sfx, "rtmpD" + sfx] + rk, writes=rk)
    P.op(eng, lambda e: e.tensor_tensor(out=x2, in0=c[:, 0:H, 0:half], in1=d[:, 0:H, 0:half], op=ALU.add),
         reads=["rtmpC" + sfx, "rtmpD" + sfx] + rk, writes=rk)


def head_rmsnorm(P, x3, gain, sq, ssum, rk, wk, gk=None, sqk=None):
    P.op("pool", lambda e: e.tensor_tensor(out=sq[:], in0=x3, in1=x3, op=ALU.mult), reads=rk, writes=[sqk or (wk + "sq")])
    P.op("dve", lambda e: e.tensor_reduce(out=ssum[:, 0:8], in_=sq[:], axis=AX.X, op=ALU.add),
         reads=[sqk or (wk + "sq")], writes=[wk + "s0"])
    P.op("dve", lambda e: e.tensor_scalar(out=ssum[:, 8:16], in0=ssum[:, 0:8], scalar1=1.0 / 128, scalar2=1e-6,
                                           op0=ALU.mult, op1=ALU.add), reads=[wk + "s0"], writes=[wk + "s1"])
    P.op("act", lambda e: e.activation(out=ssum[:, 16:24], in_=ssum[:, 8:16], func=AF.Sqrt),
         reads=[wk + "s1"], writes=[wk + "s2"])
    P.op("dve", lambda e: e.reciprocal(out=ssum[:, 24:32], in_=ssum[:, 16:24]), reads=[wk + "s2"], writes=[wk + "s3"])
    P.op("dve", lambda e: e.tensor_tensor(out=x3, in0=x3, in1=bc(ssum[:, 24:32].unsqueeze(2), [128, 8, 128]),
                                           op=ALU.mult), reads=rk + [wk + "s3"], writes=rk)
    P.op("pool", lambda e: e.tensor_tensor(out=x3, in0=x3, in1=bc(gain[:].unsqueeze(1), [128, 8, 128]),
                                            op=ALU.mult), reads=rk + [gk or ("gain" + wk)], writes=rk)


def norm_load(K, P, T, x_src, t):
    xs = t % 2
    P.dma("sp", T["xt"][xs][:], x_src[t * 128:(t + 1) * 128, :], writes=[("xt", xs)])


def norm_chain(K, P, T, x_src, t, G1, SH1, load=True, xg=None):
    xs = t % 2
    xt, hb, ss, junk = T["xt"], T["hb"], T["ss"], T["junk"]
    if load:
        norm_load(K, P, T, x_src, t)
    if xg is not None:
        xg_ap, xg_k = xg[xs]
        P.op("pool", lambda e: e.tensor_tensor(out=xg_ap, in0=xt[xs][:], in1=G1[:], op=ALU.mult),
             reads=[("xt", xs), "G"], writes=[xg_k])
    P.op("act", lambda e: e.activation(out=junk[:], in_=xt[xs][:], func=AF.Square, accum_out=ss[:, 0:1]),
         reads=[("xt", xs)], writes=["junk", "ss0"])
    P.op("dve", lambda e: e.tensor_scalar(out=ss[:, 1:2], in0=ss[:, 0:1], scalar1=1.0 / D, scalar2=1e-6,
                                           op0=ALU.mult, op1=ALU.add), reads=["ss0"], writes=["ss1"])
    P.op("act", lambda e: e.activation(out=ss[:, 2:3], in_=ss[:, 1:2], func=AF.Sqrt), reads=["ss1"], writes=["ss2"])
    P.op("dve", lambda e: e.reciprocal(out=ss[:, 3:4], in_=ss[:, 2:3]), reads=["ss2"], writes=["ss3"])
    if xg is not None:
        P.op("dve", lambda e: e.scalar_tensor_tensor(out=hb[xs][:], in0=xg_ap, scalar=ss[:, 3:4], in1=SH1[:],
                                                      op0=ALU.mult, op1=ALU.add),
             reads=[xg_k, "ss3", "SH"], writes=[("hb", xs)])
    else:
        P.op("dve", lambda e: e.scalar_tensor_tensor(out=xt[xs][:], in0=xt[xs][:], scalar=ss[:, 3:4], in1=G1[:],
                                                      op0=ALU.mult, op1=ALU.mult),
             reads=[("xt", xs), "ss3", "G"], writes=[("xt", xs)])
        P.op("pool", lambda e: e.tensor_tensor(out=hb[xs][:], in0=xt[xs][:], in1=SH1[:], op=ALU.add),
             reads=[("xt", xs), "SH"], writes=[("hb", xs)])


def norm_pe(K, P, T, t, ident, blk_hT, ti, hname="hT"):
    xs = t % 2
    hb, pT = T["hb"], T["pT"]
    for half in range(2):
        for kk in range(8):
            k = half * 8 + kk
            P.op("pe", lambda e, k=k, kk=kk, half=half: e.transpose(
                out=pT[half][:, kk, :], in_=hb[xs][:, k * 128:(k + 1) * 128], identity=ident[:]),
                reads=[("hb", xs), "ident"], writes=[("pT", half)])
        o = blk_hT[:, half * 8:(half + 1) * 8, ti * 128:(ti + 1) * 128]
        if half == 0:
            P.op("act", lambda e, o=o, half=half: e.copy(out=o, in_=pT[half][:]),
                 reads=[("pT", half)], writes=[(hname, ti, half)])
        else:
            P.op("dve", lambda e, o=o, half=half: e.tensor_copy(out=o, in_=pT[half][:]),
                 reads=[("pT", half)], writes=[(hname, ti, half)])


def norm_block(K, P, T, x_src, t, G1, SH1, ident, blk_hT, ti, load=True, hname="hT"):
    norm_chain(K, P, T, x_src, t, G1, SH1, load=load)
    norm_pe(K, P, T, t, ident, blk_hT, ti, hname=hname)


def norm_tiles_alloc(K, st, tag):
    nc = K.nc
    T = {}
    T["xt"] = [st.enter_context(nc.sbuf_tensor(tag + "xt%d" % i, [128, D], F32)) for i in range(2)]
    T["hb"] = [st.enter_context(nc.sbuf_tensor(tag + "hb%d" % i, [128, D], BF16)) for i in range(2)]
    T["ss"] = st.enter_context(nc.sbuf_tensor(tag + "ss", [128, 4], F32))
    T["junk"] = st.enter_context(nc.sbuf_tensor(tag + "junk", [128, D], BF16))
    T["pT"] = [st.enter_context(nc.psum_tensor(tag + "pT%d" % i, [128, 8, 128], BF16)) for i in range(2)]
    return T


def load_G_SH(K, P, st, which_sh, which_sc, gain_vec, tag):
    nc = K.nc
    G = st.enter_context(nc.sbuf_tensor(tag + "G", [128, D], F32))
    SH = st.enter_context(nc.sbuf_tensor(tag + "SH", [128, D], F32))
    gtmp = st.enter_context(nc.sbuf_tensor(tag + "gtmp", [128, D], F32))
    P.dma("sp", SH[:], bcast_rows(K.mod_d[which_sh * D:(which_sh + 1) * D], D), writes=["SH"])
    P.dma("sp", G[:], bcast_rows(K.mod_d[which_sc * D:(which_sc + 1) * D], D), writes=["G"])
    P.dma("sp", gtmp[:], bcast_rows(gain_vec, D), writes=["gtmp"])
    P.op("dve", lambda e: e.scalar_tensor_tensor(out=G[:], in0=G[:], scalar=1.0, in1=gtmp[:],
                                                  op0=ALU.add, op1=ALU.mult), reads=["G", "gtmp"], writes=["G"])
    K.last_gtmp = gtmp
    return G, SH


def phase1_kv(K):
    nc, P = K.nc, K.P
    with contextlib.ExitStack() as st:
        ident = st.enter_context(nc.sbuf_tensor("ident", [128, 128], BF16))
        make_ident(K, P, ident)
        G1, SH1 = load_G_SH(K, P, st, 0, 1, K.norm1_g, "p1")
        T = norm_tiles_alloc(K, st, "p1")
        hT = [st.enter_context(nc.sbuf_tensor("hT%d" % i, [128, 16, 512], BF16)) for i in range(2)]
        W = st.enter_context(nc.sbuf_tensor("Wkv", [128, 16, 2112], BF16))
        stg = [st.enter_context(nc.sbuf_tensor("wstg%d" % i, [128, 4, 512], F32)) for i in range(2)]
        wk_k = load_weight_bf16(K, P, stg, W, 0, K.w_in[:, 1024:2048], 1024, "Wk")
        wk_v = load_weight_bf16(K, P, stg, W, 1024, K.w_in[:, 2048:3072], 1024, "Wv")
        wk_i = load_weight_bf16(K, P, stg, W, 2048, K.w_in[:, 4096:4160], 64, "Wi")
        rt = rope_tables(K, P, st, K.pos_full, 32, K.invf_att, K.invf_idx, "rf")
        gain = st.enter_context(nc.sbuf_tensor("kgain", [128, 128], F32))
        P.dma("sp", gain[:], bcast_rows(K.k_norm_g, 128), writes=["gainK"])
        def two(name, shape, dt):
            return [st.enter_context(nc.sbuf_tensor(name + str(i), shape, dt)) for i in range(2)]
        ksb2 = two("ksb", [128, 8, 128], F32)
        kbf2 = two("kbf", [128, 8, 128], BF16)
        sq2 = [st.enter_context(nc.sbuf_tensor("sq", [128, 8, 128], F32))] * 2
        ssum2 = two("ssum", [128, 32], F32)
        rtmp2 = [[st.enter_context(nc.sbuf_tensor("rtmp%d" % i, [128, 8, 16], F32)) for i in range(4)]] * 2
        vsb2 = two("vsb", [128, 8, 129], BF16)
        iksb2 = two("iksb", [128, 1, 64], F32)
        ikbf2 = two("ikbf", [128, 64], BF16)
        kTs2 = [st.enter_context(nc.sbuf_tensor("kTs", [128, 8, 128], BF16))] * 2
        ikTs2 = two("ikTs", [64, 128], BF16)
        pm = [st.enter_context(nc.psum_tensor("pm%d" % i, [128, 512], F32)) for i in range(3)]
        pk = st.enter_context(nc.psum_tensor("pk", [128, 8, 128], BF16))
        pk2 = st.enter_context(nc.psum_tensor("pk2", [64, 128], BF16))
        for s_ in range(2):
            P.op("pool", lambda e, s_=s_: e.memset(vsb2[s_][:], 1.0), writes=["vsb%d" % s_])
        norm_load(K, P, T, K.x_full, 0)

        xg = [(K.last_gtmp[:], "gtmp"), (stg[0][:].rearrange("p a b -> p (a b)"), ("wstg", 0))]

        def norm_tile_chain(blk, ti):
            tt_ = blk * 4 + ti
            if tt_ + 1 < 32:
                norm_load(K, P, T, K.x_full, tt_ + 1)
            norm_chain(K, P, T, K.x_full, tt_, G1, SH1, load=False, xg=xg)

        def norm_tile_pe(blk, ti):
            norm_pe(K, P, T, blk * 4 + ti, ident, hT[blk % 2], ti, hname=("hT", blk % 2))

        def norm_tile(blk, ti):
            norm_tile_chain(blk, ti)
            norm_tile_pe(blk, ti)

        def store_hT(blk):
            hs = blk % 2
            hkeys = [(("hT", hs), ti, half) for ti in range(4) for half in range(2)]
            P.dma("sp", K.hT_d.rearrange("k p t -> p k t")[:, :, blk * 512:(blk + 1) * 512], hT[hs][:],
                  reads=hkeys, writes=[("hT_d", blk)])

        def bufs(t):
            u = t % 2
            return (str(u), ksb2[u], kbf2[u], sq2[u], ssum2[u], rtmp2[u], vsb2[u], iksb2[u], ikbf2[u], kTs2[u], ikTs2[u])

        def mm_tile(blk, ti):
            t = blk * 4 + ti
            hs = blk % 2
            hk = [(("hT", hs), ti, 0), (("hT", hs), ti, 1)]
            us, ksb, kbf, sq, ssum, rtmp, vsb, iksb, ikbf, kTs, ikTs = bufs(t)
            for gi, (c0, n, wkeys) in enumerate([(0, 512, wk_k), (512, 512, wk_k), (1024, 512, wk_v),
                                                 (1536, 512, wk_v), (2048, 64, wk_i)]):
                pb = pm[gi % 3]
                for k in range(16):
                    P.op("pe", lambda e, pb=pb, k=k, c0=c0, n=n, ti=ti, hs=hs: e.matmul(
                        pb[:, 0:n], lhsT=hT[hs][:, k, ti * 128:(ti + 1) * 128], rhs=W[:, k, c0:c0 + n],
                        start=(k == 0), stop=(k == 15)), reads=hk + wkeys, writes=[("pm", gi % 3)])
                if gi < 2:
                    P.op("act", lambda e, pb=pb, gi=gi, ksb=ksb: e.copy(out=ksb[:, gi * 4:(gi + 1) * 4, :], in_=pb[:, 0:512]),
                         reads=[("pm", gi % 3)], writes=["ksb" + us])
                elif gi < 4:
                    g2 = gi - 2
                    P.op("act", lambda e, pb=pb, g2=g2, vsb=vsb: e.copy(out=vsb[:, g2 * 4:(g2 + 1) * 4, 0:128], in_=pb[:, 0:512]),
                         reads=[("pm", gi % 3)], writes=["vsb" + us])
                else:
                    P.op("act", lambda e, pb=pb, iksb=iksb: e.copy(out=iksb[:, 0, :], in_=pb[:, 0:64]),
                         reads=[("pm", gi % 3)], writes=["iksb" + us])
            P.dma("sp", K.v_d[t * 128:(t + 1) * 128, :], vsb[:].rearrange("p h d -> p (h d)"),
                  reads=["vsb" + us], writes=[("v_d", t)])

        def post1(blk, ti):
            t = blk * 4 + ti
            us, ksb, kbf, sq, ssum, rtmp, vsb, iksb, ikbf, kTs, ikTs = bufs(t)
            head_rmsnorm(P, ksb[:], gain, sq, ssum, ["ksb" + us], "K" + us, gk="gainK", sqk="Ksq")
            apply_rope(P, "dve", ksb[:], rt["cosa"], rt["sina"], t, 16, rtmp, ["ksb" + us], "rK")
            P.op("act", lambda e, kbf=kbf, ksb=ksb: e.copy(out=kbf[:], in_=ksb[:]), reads=["ksb" + us], writes=["kbf" + us])
            apply_rope(P, "pool", iksb[:], rt["cosi"], rt["sini"], t, 8, rtmp, ["iksb" + us], "rI")
            P.op("act", lambda e, ikbf=ikbf, iksb=iksb: e.copy(out=ikbf[:], in_=iksb[:, 0, :]), reads=["iksb" + us], writes=["ikbf" + us])

        def post2(blk, ti):
            t = blk * 4 + ti
            us, ksb, kbf, sq, ssum, rtmp, vsb, iksb, ikbf, kTs, ikTs = bufs(t)
            for h in range(8):
                P.op("pe", lambda e, h=h, kbf=kbf: e.transpose(out=pk[:, h, :], in_=kbf[:, h, :], identity=ident[:]),
                     reads=["kbf" + us, "ident"], writes=["pk"])
            P.op("dve", lambda e, kTs=kTs: e.tensor_copy(out=kTs[:], in_=pk[:]), reads=["pk"], writes=["kTs"])
            P.dma("sp", K.kT_d.rearrange("h p t -> p h t")[:, :, t * 128:(t + 1) * 128], kTs[:],
                  reads=["kTs"], writes=[("kT_d", t)])
            P.op("pe", lambda e, ikbf=ikbf: e.transpose(out=pk2[:, :], in_=ikbf[:], identity=ident[:]),
                 reads=["ikbf" + us, "ident"], writes=["pk2"])
            P.op("dve", lambda e, ikTs=ikTs: e.tensor_copy(out=ikTs[:], in_=pk2[:, :]), reads=["pk2"], writes=["ikTs" + us])
            P.dma("sp", K.ikT_d[:, t * 128:(t + 1) * 128], ikTs[:], reads=["ikTs" + us], writes=[("ikT_d", t)])

        for ti in range(4):
            norm_tile(0, ti)
        store_hT(0)
        prev = None
        for blk in range(8):
            for ti in range(4):
                if blk + 1 < 8:
                    norm_tile_chain(blk + 1, ti)
                mm_tile(blk, ti)
                if prev is not None:
                    post2(*prev)
                if blk + 1 < 8:
                    norm_tile_pe(blk + 1, ti)
                post1(blk, ti)
                prev = (blk, ti)
            if blk + 1 < 8:
                store_hT(blk + 1)
        post2(*prev)
        P.flush()

RW0 = 4176
NRW = 1216
RW_GROUPS = [(i * 128, 128) for i in range(6)] + [(768, 96), (864, 96), (960, 128), (1088, 128)]


def phase1b_rwkv_proj(K):
    nc, P = K.nc, K.P
    with contextlib.ExitStack() as st:
        W = st.enter_context(nc.sbuf_tensor("Wr", [128, 16, NRW], BF16))
        stg = [st.enter_context(nc.sbuf_tensor("wstgb%d" % i, [128, 4, 512], F32)) for i in range(2)]
        hT = [st.enter_context(nc.sbuf_tensor("hTb%d" % i, [128, 16, 512], BF16)) for i in range(2)]
        ost = [st.enter_context(nc.sbuf_tensor("ost%d" % i, [128, 512], F32)) for i in range(4)]
        pm = [st.enter_context(nc.psum_tensor("pmb%d" % i, [128, 512], F32)) for i in range(4)]
        wkeys = load_weight_bf16(K, P, stg, W, 0, K.w_in_rw, NRW, "Wr")
        cnt = 0
        for blk in range(8):
            hs = blk % 2
            P.dma("sp", hT[hs][:], K.hT_d.rearrange("k p t -> p k t")[:, :, blk * 512:(blk + 1) * 512],
                  writes=[("hTb", hs)])
            for (r0, m) in RW_GROUPS:
                s4 = cnt % 4
                cnt += 1
                for k in range(16):
                    P.op("pe", lambda e, k=k, r0=r0, m=m, hs=hs, s4=s4: e.matmul(
                        pm[s4][0:m, :], lhsT=W[:, k, r0:r0 + m], rhs=hT[hs][:, k, :],
                        start=(k == 0), stop=(k == 15)), reads=[("hTb", hs)] + wkeys, writes=[("pmb", s4)])
                if cnt % 2 == 0:
                    P.op("act", lambda e, m=m, s4=s4: e.copy(out=ost[s4][0:m, :], in_=pm[s4][0:m, :]),
                         reads=[("pmb", s4)], writes=[("ost", s4)])
                else:
                    P.op("dve", lambda e, m=m, s4=s4: e.tensor_copy(out=ost[s4][0:m, :], in_=pm[s4][0:m, :]),
                         reads=[("pmb", s4)], writes=[("ost", s4)])
                P.dma("sp", K.yT_d[r0:r0 + m, blk * 512:(blk + 1) * 512], ost[s4][0:m, :],
                      reads=[("ost", s4)], writes=[("yT_d", r0, blk)])
        P.flush()


def phase2_own_proj(K):
    nc, P = K.nc, K.P
    with contextlib.ExitStack() as st:
        ident = st.enter_context(nc.sbuf_tensor("ident2", [128, 128], BF16))
        make_ident(K, P, ident)
        G1, SH1 = load_G_SH(K, P, st, 0, 1, K.norm1_g, "p2")
        T = norm_tiles_alloc(K, st, "p2")
        hT = [st.enter_context(nc.sbuf_tensor("hTo%d" % i, [128, 16, 512], BF16)) for i in range(2)]
        W = st.enter_context(nc.sbuf_tensor("Wq", [128, 16, 2064], BF16))
        stg = [st.enter_context(nc.sbuf_tensor("wstgq%d" % i, [128, 4, 512], F32)) for i in range(2)]
        wk_q = load_weight_bf16(K, P, stg, W, 0, K.w_in[:, 0:1024], 1024, "Wq")
        wk_iq = load_weight_bf16(K, P, stg, W, 1024, K.w_in[:, 3072:4096], 1024, "Wiq")
        wk_iw = load_weight_bf16(K, P, stg, W, 2048, K.w_in[:, 4160:4176], 16, "Wiw")
        rt = rope_tables(K, P, st, K.pos_own, 8, K.invf_att, K.invf_idx, "ro")
        gain = st.enter_context(nc.sbuf_tensor("qgain", [128, 128], F32))
        P.dma("sp", gain[:], bcast_rows(K.q_norm_g, 128), writes=["gainQ"])
        qsb = st.enter_context(nc.sbuf_tensor("qsb", [128, 8, 128], F32))
        qbf = st.enter_context(nc.sbuf_tensor("qbf", [128, 8, 128], BF16))
        sq = st.enter_context(nc.sbuf_tensor("sq2", [128, 8, 128], F32))
        ssum = st.enter_context(nc.sbuf_tensor("ssum2", [128, 32], F32))
        rtmp = [st.enter_context(nc.sbuf_tensor("rtmpq%d" % i, [128, 16, 16], F32)) for i in range(4)]
        iqsb = st.enter_context(nc.sbuf_tensor("iqsb", [128, 16, 64], F32))
        iqbf = st.enter_context(nc.sbuf_tensor("iqbf", [128, 16, 64], BF16))
        iwsb = st.enter_context(nc.sbuf_tensor("iwsb", [128, 16], F32))
        qTs = st.enter_context(nc.sbuf_tensor("qTs", [128, 8, 128], BF16))
        iqTs = st.enter_context(nc.sbuf_tensor("iqTs", [64, 128, 16], BF16))
        pm = [st.enter_context(nc.psum_tensor("pmq%d" % i, [128, 512], F32)) for i in range(3)]
        pk = st.enter_context(nc.psum_tensor("pkq", [128, 8, 128], BF16))
        xg = [(K.last_gtmp[:], "gtmp"), (stg[0][:].rearrange("p a b -> p (a b)"), ("wstg", 0))]

        def nchain(blk, ti):
            tt_ = blk * 4 + ti
            if tt_ + 1 < 8:
                norm_load(K, P, T, K.x_own, tt_ + 1)
            norm_chain(K, P, T, K.x_own, tt_, G1, SH1, load=False, xg=xg)

        def npe(blk, ti):
            norm_pe(K, P, T, blk * 4 + ti, ident, hT[blk % 2], ti, hname=("hT", blk % 2))
        norm_load(K, P, T, K.x_own, 0)
        for ti in range(4):
            nchain(0, ti)
            npe(0, ti)
        for blk in range(2):
            hs = blk % 2
            for ti in range(4):
                t = blk * 4 + ti
                hk = [(("hT", hs), ti, 0), (("hT", hs), ti, 1)]
                if blk + 1 < 2:
                    nchain(blk + 1, ti)
                for gi, (c0, n, wkeys) in enumerate([(0, 512, wk_q), (512, 512, wk_q), (1024, 512, wk_iq),
                                                     (1536, 512, wk_iq), (2048, 16, wk_iw)]):
                    pb = pm[gi % 3]
                    for k in range(16):
                        P.op("pe", lambda e, pb=pb, k=k, c0=c0, n=n, ti=ti, hs=hs: e.matmul(
                            pb[:, 0:n], lhsT=hT[hs][:, k, ti * 128:(ti + 1) * 128], rhs=W[:, k, c0:c0 + n],
                            start=(k == 0), stop=(k == 15)), reads=hk + wkeys, writes=[("pmq", gi % 3)])
                    if gi < 2:
                        P.op("act", lambda e, pb=pb, gi=gi: e.copy(out=qsb[:, gi * 4:(gi + 1) * 4, :], in_=pb[:, 0:512]),
                             reads=[("pmq", gi % 3)], writes=["qsb"])
                    elif gi < 4:
                        g2 = gi - 2
                        P.op("act", lambda e, pb=pb, g2=g2: e.copy(out=iqsb[:, g2 * 8:(g2 + 1) * 8, :], in_=pb[:, 0:512]),
                             reads=[("pmq", gi % 3)], writes=["iqsb"])
                    else:
                        P.op("act", lambda e, pb=pb: e.activation(out=iwsb[:], in_=pb[:, 0:16], func=AF.Copy, scale=0.25),
                             reads=[("pmq", gi % 3)], writes=["iwsb"])
                P.dma("sp", K.iw_d[t * 128:(t + 1) * 128, :], iwsb[:], reads=["iwsb"], writes=[("iw_d", t)])
                head_rmsnorm(P, qsb[:], gain, sq, ssum, ["qsb"], "Q")
                apply_rope(P, "dve", qsb[:], rt["cosa"], rt["sina"], t, 16, rtmp, ["qsb"], "rQ")
                P.op("act", lambda e: e.copy(out=qbf[:], in_=qsb[:]), reads=["qsb"], writes=["qbf"])
                for h in range(8):
                    P.op("pe", lambda e, h=h: e.transpose(out=pk[:, h, :], in_=qbf[:, h, :], identity=ident[:]),
                         reads=["qbf", "ident"], writes=["pkq"])
                P.op("dve", lambda e: e.tensor_copy(out=qTs[:], in_=pk[:]), reads=["pkq"], writes=["qTs"])
                P.dma("sp", K.qT_d.rearrange("h p t -> p h t")[:, :, t * 128:(t + 1) * 128], qTs[:],
                      reads=["qTs"], writes=[("qT_d", t)])
                apply_rope(P, "pool", iqsb[:], rt["cosi"], rt["sini"], t, 8, rtmp, ["iqsb"], "rIQ")
                P.op("act", lambda e: e.activation(out=iqbf[:], in_=iqsb[:], func=AF.Copy, scale=0.125),
                     reads=["iqsb"], writes=["iqbf"])
                for half in range(2):
                    for hh in range(8):
                        h = half * 8 + hh
                        P.op("pe", lambda e, h=h, hh=hh: e.transpose(out=pk[0:64, hh, :], in_=iqbf[:, h, :],
                                                                      identity=ident[:]),
                             reads=["iqbf", "ident"], writes=["pkq"])
                    P.op("dve", lambda e, half=half: e.tensor_copy(
                        out=iqTs[:, :, half * 8:(half + 1) * 8].rearrange("p t h -> p h t"), in_=pk[0:64, :, :]),
                         reads=["pkq"], writes=["iqTs"])
                P.dma("sp", K.iqT_d[:, t * 128:(t + 1) * 128, :], iqTs[:], reads=["iqTs"], writes=[("iqT_d", t)])
                if blk + 1 < 2:
                    npe(blk + 1, ti)
        P.flush()


NIT = 22
SLOT_NK = [4, 8, 12, 16, 20, 24, 28, 32]


def phase3_attention(K):
    nc, P = K.nc, K.P
    with contextlib.ExitStack() as st:
        def sb(name, shape, dt):
            return st.enter_context(nc.sbuf_tensor(name, shape, dt))
        ident = sb("ident3", [128, 128], BF16)
        kposf = sb("kposf", [128, 512], F32)
        bias = sb("cbias", [128, 512], F32)
        identf = kposf[:, 0:128]
        make_ident(K, P, ident)
        P.op("dve", lambda e: e.tensor_copy(out=identf, in_=ident[:]), reads=["ident"], writes=["kposf"])
        kT = sb("kTall", [128, 8, S], BF16)
        V = sb("Vall", [128, 32, 1032], BF16)
        ikT = sb("ikTall", [64, S], BF16)
        for h in range(8):
            P.dma("sp", kT[:, h, :], K.kT_d[h], writes=[("kT", h)])
        for q4 in range(4):
            P.dma("sp", V[:, q4 * 8:(q4 + 1) * 8, :],
                  K.v_d.rearrange("(t p) c -> p t c", p=128)[:, q4 * 8:(q4 + 1) * 8, :], writes=[("V", q4)])
        P.dma("sp", ikT[:], K.ikT_d, writes=["ikT"])
        kTk = [("kT", h) for h in range(8)]
        Vk = [("V", q4) for q4 in range(4)]
        Sel = sb("Sel", [128, 16, 128], BF16)
        pidx = sb("pidx", [128, 1], I32)
        pidf = sb("pidf", [128, 1], F32)
        score = sb("score", [128, S], F32)
        self_ = score[:, 0:2048].rearrange("p (g t) -> p g t", g=16)
        sk4 = [("score", q) for q in range(4)]
        P.op("pool", lambda e: e.iota(self_, pattern=[[-8, 16], [1, 128]], base=0, channel_multiplier=0, allow_small_or_imprecise_dtypes=True), writes=sk4)
        P.op("pool", lambda e: e.iota(pidx[:], pattern=[[0, 1]], base=0, channel_multiplier=1), writes=["pidx"])
        P.op("dve", lambda e: e.tensor_scalar(out=pidx[:], in0=pidx[:], scalar1=4, scalar2=None,
                                               op0=ALU.arith_shift_right), reads=["pidx"], writes=["pidx"])
        P.op("dve", lambda e: e.tensor_copy(out=pidf[:], in_=pidx[:]), reads=["pidx"], writes=["pidf"])
        P.op("dve", lambda e: e.tensor_scalar(out=Sel[:], in0=self_, scalar1=pidf[:, 0:1], scalar2=None,
                                               op0=ALU.is_equal), reads=sk4 + ["pidf"], writes=["Sel"])
        qpos = sb("qpos", [128, 8], F32)
        P.dma("sp", qpos[:], K.qpos_own, writes=["qpos"])
        iwg = bias[:, 0:128]
        wcol = sb("wcol", [128, 128], F32)
        P.dma("sp", iwg, K.iw_d.rearrange("(g t) h -> g (t h)", t=8), writes=["bias"])
        A = [st.enter_context(nc.psum_tensor("A%d" % i, [128, 512], F32)) for i in range(2)]
        B = [st.enter_context(nc.psum_tensor("B%d" % i, [128, 512], F32)) for i in range(2)]
        C = st.enter_context(nc.psum_tensor("C3", [128, 8, 128], BF16))
        P.op("pe", lambda e: e.transpose(out=A[0][:, 0:128], in_=iwg, identity=identf),
             reads=["bias", "kposf"], writes=[("A", 0)])
        P.op("dve", lambda e: e.tensor_copy(out=wcol[:], in_=A[0][:, 0:128]), reads=[("A", 0)], writes=["wcol"])
        mask01 = sb("mask01", [128, S], BF16)
        maskT = sb("maskT", [128, 32, 128], BF16)
        R = [sb("R%d" % i, [128, 512], BF16) for i in range(2)]
        pexp = [sb("pexp%d" % i, [128, 512], BF16) for i in range(2)]
        pmk = [sb("pmk%d" % i, [128, 512], BF16) for i in range(2)]
        iqTs = sb("iqTs3", [64, 128, 16], BF16)
        qTs = sb("qTs3", [128, 8, 128], BF16)
        att = sb("att", [128, 8, 128], BF16)
        attTs = sb("attTs", [128, 8, 128], BF16)
        c2 = sb("c2", [128, NIT], F32)
        steps = sb("steps", [128, NIT], F32)
        sm = sb("sm3", [128, 8], F32)
        for k in range(NIT):
            P.op("pool", lambda e, k=k: e.memset(c2[:, k:k + 1], float(2.0 ** -(k + 1))), writes=["c2"])
        maskT2 = [maskT, sb("maskTb", [128, 32, 128], BF16)]
        Wbd = sb("Wbd", [128, 16, 128], BF16)
        qTs2 = [qTs, sb("qTs3b", [128, 8, 128], BF16)]
        rcp2 = sb("rcp2", [128, 2], F32)

        def stageA(i):
            nk = SLOT_NK[i]
            nb = nk // 4
            P.dma("sp", iqTs[:], K.iqT_d[:, i * 128:(i + 1) * 128, :], writes=["iqTs"])
            P.dma("sp", qTs2[i % 2][:], K.qT_d.rearrange("h p t -> p h t")[:, :, i * 128:(i + 1) * 128], writes=[("qTs", i % 2)])
            isteps = [(sbk, g) for sbk in range(nb) for g in range(16)]
            P.op("pool", lambda e, i=i: e.tensor_tensor(out=Wbd[:], in0=Sel[:],
                                                         in1=wcol[:, i * 16:(i + 1) * 16].unsqueeze(2).to_broadcast([128, 16, 128]),
                                                         op=ALU.mult), reads=["Sel", "wcol"], writes=["Wbd"])

            def dots(si):
                sbk, g = isteps[si]
                a_ = si % 2
                lhsT = iqTs[:, g * 8:(g + 1) * 8, :].rearrange("p t h -> p (t h)")
                P.op("pe", lambda e, a_=a_, lhsT=lhsT, sbk=sbk: e.matmul(
                    A[a_][:, :], lhsT=lhsT, rhs=ikT[:, sbk * 512:(sbk + 1) * 512], start=True, stop=True),
                    reads=["iqTs", "ikT"], writes=[("A", a_)])
            dots(0)
            for si, (sbk, g) in enumerate(isteps):
                a_ = si % 2
                bsl = sbk % 2
                if si + 1 < len(isteps):
                    dots(si + 1)
                if si % 2 == 0:
                    P.op("act", lambda e, a_=a_: e.activation(out=R[a_][:], in_=A[a_][:, :], func=AF.Relu),
                         reads=[("A", a_)], writes=[("R", a_)])
                else:
                    P.op("dve", lambda e, a_=a_: e.tensor_scalar(out=R[a_][:], in0=A[a_][:, :], scalar1=0.0, scalar2=None,
                                                                  op0=ALU.max), reads=[("A", a_)], writes=[("R", a_)])
                P.op("pe", lambda e, a_=a_, g=g, bsl=bsl: e.matmul(
                    B[bsl][:, :], lhsT=Wbd[:, g, :], rhs=R[a_][:], start=(g == 0), stop=(g == 15)),
                    reads=[("R", a_), "Wbd"], writes=[("B", bsl)])
                if g == 15:
                    P.op("dve", lambda e, bsl=bsl, sbk=sbk: e.tensor_copy(out=score[:, sbk * 512:(sbk + 1) * 512], in_=B[bsl][:, :]),
                         reads=[("B", bsl)], writes=[("score", sbk)])

        def stageBdve(i):
            nk = SLOT_NK[i]
            nb = nk // 4
            L = nk * 128
            sck = [("score", sbk) for sbk in range(nb)]
            P.op("dve", lambda e, L=L: e.tensor_reduce(out=sm[:, 0:1], in_=score[:, 0:L], axis=AX.X, op=ALU.max,
                                                        apply_absolute_value=True), reads=sck, writes=["sm0"])
            P.op("pool", lambda e, nb=nb: e.iota(kposf[:], pattern=[[1, 512]], base=(nb - 1) * 512, channel_multiplier=0,
                                                 allow_small_or_imprecise_dtypes=True), writes=["kposf"])
            P.op("dve", lambda e, i=i: e.tensor_scalar(out=bias[:], in0=kposf[:], scalar1=qpos[:, i:i + 1],
                                                        scalar2=-1e30, op0=ALU.is_gt, op1=ALU.mult),
                 reads=["kposf", "qpos"], writes=["bias"])
            P.op("dve", lambda e, nb=nb: e.tensor_tensor(out=score[:, (nb - 1) * 512:nb * 512],
                                                          in0=score[:, (nb - 1) * 512:nb * 512], in1=bias[:], op=ALU.add),
                 reads=["bias", ("score", nb - 1), "sm0"], writes=[("score", nb - 1)])
            P.op("dve", lambda e: e.tensor_scalar(out=sm[:, 1:2], in0=sm[:, 0:1], scalar1=-1.0, scalar2=-1.0,
                                                   op0=ALU.mult, op1=ALU.add), reads=["sm0"], writes=["lo"])
            P.op("dve", lambda e: e.tensor_scalar(out=sm[:, 5:6], in0=sm[:, 0:1], scalar1=2.0, scalar2=2.0,
                                                   op0=ALU.mult, op1=ALU.add), reads=["sm0"], writes=["d0"])
            P.op("dve", lambda e: e.tensor_scalar(out=steps[:], in0=c2[:], scalar1=sm[:, 5:6], scalar2=None,
                                                   op0=ALU.mult), reads=["d0", "c2"], writes=["steps"])
            P.op("dve", lambda e: e.tensor_tensor(out=sm[:, 2:3], in0=sm[:, 1:2], in1=steps[:, 0:1], op=ALU.add),
                 reads=["lo", "steps"], writes=["mid"])
            for k in range(NIT):
                P.op("dve", lambda e, L=L: e.tensor_scalar(out=mask01[:, 0:L], in0=score[:, 0:L], scalar1=sm[:, 2:3],
                                                            scalar2=None, op0=ALU.is_ge, op1=ALU.add,
                                                            accum_out=sm[:, 3:4]),
                     reads=sck + ["mid"], writes=["mask01", "cnt"])
                P.op("dve", lambda e: e.tensor_scalar(out=sm[:, 4:5], in0=sm[:, 3:4], scalar1=255.5, scalar2=-0.5,
                                                       op0=ALU.is_ge, op1=ALU.add), reads=["cnt"], writes=["inc"])
                P.op("dve", lambda e, k=k: e.scalar_tensor_tensor(out=sm[:, 2:3], in0=steps[:, k:k + 1], scalar=sm[:, 4:5],
                                                                   in1=sm[:, 2:3], op0=ALU.mult, op1=ALU.add),
                     reads=["inc", "steps", "mid"], writes=["mid"])
            P.op("dve", lambda e: e.scalar_tensor_tensor(out=sm[:, 1:2], in0=steps[:, NIT - 1:NIT], scalar=-0.5, in1=sm[:, 2:3],
                                                          op0=ALU.mult, op1=ALU.add), reads=["steps", "mid"], writes=["lo"])
            P.op("dve", lambda e, L=L: e.tensor_scalar(out=mask01[:, 0:L], in0=score[:, 0:L], scalar1=sm[:, 1:2],
                                                        scalar2=None, op0=ALU.is_ge), reads=sck + ["lo"], writes=["mask01"])

        def stageBpe(i):
            nk = SLOT_NK[i]
            mT = maskT2[i % 2]
            for kt in range(nk):
                P.op("pe", lambda e, kt=kt: e.transpose(out=C[:, kt % 8, :], in_=mask01[:, kt * 128:(kt + 1) * 128],
                                                         identity=ident[:]), reads=["mask01", "ident"], writes=["C"])
                if kt % 8 == 7 or kt == nk - 1:
                    k0 = (kt // 8) * 8
                    n8 = kt - k0 + 1
                    P.op("dve", lambda e, k0=k0, n8=n8, mT=mT: e.tensor_copy(out=mT[:, k0:k0 + n8, :], in_=C[:, 0:n8, :]),
                         reads=["C"], writes=[("maskT", i % 2, k0 // 8)])

        def stageC(i):
            nk = SLOT_NK[i]
            nb = nk // 4
            mT = maskT2[i % 2]
            qT_ = qTs2[i % 2]
            mk = [("maskT", i % 2, q) for q in range((nk + 7) // 8)]
            asteps = [(h, kg) for h in range(8) for kg in range(nb)]

            def qk(si):
                h, kg = asteps[si]
                a_ = si % 2
                for j4 in range(4):
                    kt = kg * 4 + j4
                    P.op("pe", lambda e, a_=a_, j4=j4, kt=kt, h=h: e.matmul(
                        A[a_][:, j4 * 128:(j4 + 1) * 128], lhsT=kT[:, h, kt * 128:(kt + 1) * 128], rhs=qT_[:, h, :],
                        start=True, stop=True), reads=kTk + [("qTs", i % 2)], writes=[("A", a_)])
            qk(0)
            for si, (h, kg) in enumerate(asteps):
                a_ = si % 2
                bsl = h % 2
                if si + 1 < len(asteps):
                    qk(si + 1)
                P.op("act", lambda e, a_=a_: e.activation(out=pexp[a_][:], in_=A[a_][:, :], func=AF.Exp,
                                                           scale=float(128 ** -0.5)),
                     reads=[("A", a_)], writes=[("pexp", a_)])
                P.op("pool", lambda e, a_=a_, kg=kg: e.tensor_tensor(
                    out=pmk[a_][:], in0=pexp[a_][:], in1=mT[:, kg * 4:(kg + 1) * 4, :].rearrange("p a t -> p (a t)"),
                    op=ALU.mult), reads=[("pexp", a_)] + mk, writes=[("pmk", a_)])
                for j4 in range(4):
                    kt = kg * 4 + j4
                    P.op("pe", lambda e, a_=a_, j4=j4, kt=kt, h=h, bsl=bsl, kg=kg, nb=nb: e.matmul(
                        B[bsl][:, 0:129], lhsT=pmk[a_][:, j4 * 128:(j4 + 1) * 128], rhs=V[:, kt, h * 129:(h + 1) * 129],
                        start=(kg == 0 and j4 == 0), stop=(kg == nb - 1 and j4 == 3)),
                        reads=[("pmk", a_)] + Vk, writes=[("B", bsl)])
                if kg == nb - 1:
                    P.op("act", lambda e, bsl=bsl: e.activation(out=rcp2[:, 0:1], in_=B[bsl][:, 128:129], func=AF.Ln),
                         reads=[("B", bsl)], writes=["rcpa"])
                    P.op("act", lambda e: e.activation(out=rcp2[:, 1:2], in_=rcp2[:, 0:1], func=AF.Exp, scale=-1.0),
                         reads=["rcpa"], writes=["rcpb"])
                    P.op("act", lambda e, bsl=bsl, h=h: e.activation(out=att[:, h, :], in_=B[bsl][:, 0:128], func=AF.Copy,
                                                                      scale=rcp2[:, 1:2]),
                         reads=[("B", bsl), "rcpb"], writes=["att"])
            for h in range(8):
                P.op("pe", lambda e, h=h: e.transpose(out=C[:, h, :], in_=att[:, h, :], identity=ident[:]),
                     reads=["att", "ident"], writes=["C"])
            P.op("act", lambda e: e.copy(out=attTs[:], in_=C[:]), reads=["C"], writes=["attTs"])
            P.dma("sp", K.attT_d.rearrange("h p t -> p h t")[:, :, i * 128:(i + 1) * 128], attTs[:],
                  reads=["attTs"], writes=[("attT_d", i)])

        stageA(0)
        stageBdve(0)
        stageBpe(0)
        for i in range(8):
            if i + 1 < 8:
                stageA(i + 1)
                stageBdve(i + 1)
            stageC(i)
            if i + 1 < 8:
                stageBpe(i + 1)
        P.flush()

RD = BF16
NCH = 64


def tok_shift(P, dst, raw, tmp, mu_ap, rk_raw, k_tmp, k_dst, n=128):
    P.op("pool", lambda e: e.tensor_tensor(out=tmp[0:n, 1:S], in0=raw[0:n, 0:S - 1], in1=raw[0:n, 1:S], op=ALU.subtract),
         reads=[rk_raw], writes=[k_tmp])
    P.op("pool", lambda e: e.tensor_scalar(out=tmp[0:n, 0:1], in0=raw[0:n, 0:1], scalar1=-1.0, scalar2=0.0,
                                            op0=ALU.mult, op1=ALU.add), reads=[rk_raw, k_tmp], writes=[k_tmp])
    P.op("dve", lambda e: e.scalar_tensor_tensor(out=dst[0:n, :], in0=tmp[0:n, :], scalar=mu_ap, in1=raw[0:n, :],
                                                  op0=ALU.mult, op1=ALU.add), reads=[rk_raw, k_tmp], writes=[k_dst])


def phase4b_rwkv_prep(K, cts=range(2)):
    nc, P = K.nc, K.P
    with contextlib.ExitStack() as st:
        def sb(name, shape, dt):
            return st.enter_context(nc.sbuf_tensor(name, shape, dt))
        txw = sb("txw", [96, S], BF16)
        xap = sb("xap", [96, S], BF16)
        sxg = sb("sxg", [128, 2, S], BF16)
        M01 = sb("M01", [128, S], BF16)
        wup = sb("wup", [96, 256], BF16)
        aup = sb("aup", [96, 256], BF16)
        gup = sb("gup", [128, 2, 256], BF16)
        wst = sb("wst4", [128, 2, 256], F32)
        bones = sb("bones", [128, 128], BF16)
        prm = sb("prm", [128, 12, 2], F32)
        mul = sb("mul", [128, 4], F32)
        PT = sb("PT", [128, S], F32)
        KK = sb("KK", [128, S], F32)
        KP = sb("KP", [128, S], F32)
        CL = sb("CL", [128, S], F32)
        RP = sb("RP", [128, S], BF16)
        VP = sb("VP", [128, S], BF16)
        AA = sb("AA", [128, S], BF16)
        K2 = sb("K2", [128, S], BF16)
        SQb = sb("SQb", [128, S], BF16)
        OUT = [sb("OUT%d" % i, [128, S], BF16) for i in range(2)]
        PCt = sb("PCt", [128, NCH], F32)
        ps = [st.enter_context(nc.psum_tensor("ps4_%d" % i, [128, 512], F32)) for i in range(4)]
        for i, ap in enumerate(K.rw_prm):
            P.dma("sp", prm[:, i, :], ap, writes=[("prm", i)])
        prk = [("prm", i) for i in range(10)]
        P.op("dve", lambda e: e.tensor_scalar(out=prm[:, 10, :], in0=prm[:, 6, :], scalar1=-1.0, scalar2=1.0,
                                               op0=ALU.mult, op1=ALU.add), reads=prk, writes=[("prm", 10)])
        prk = prk + [("prm", 10)]
        P.dma("sp", mul[:], K.rw_mul, writes=["mul"])
        P.op("pool", lambda e: e.memset(bones[:], 0.0), writes=["bones"])
        P.op("pool", lambda e: e.memset(bones[0:64, 0:64], 1.0), reads=["bones"], writes=["bones"])
        P.op("pool", lambda e: e.memset(bones[64:128, 64:128], 1.0), reads=["bones"], writes=["bones"])
        P.op("pool", lambda e: e.iota(PT[:].rearrange("p (c t) -> p c t", t=64), pattern=[[0, NCH], [1, 64]], base=0,
                                      channel_multiplier=0, allow_small_or_imprecise_dtypes=True), writes=["PT"])
        P.op("dve", lambda e: e.tensor_scalar(out=M01[:], in0=PT[:], scalar1=0.5, scalar2=None, op0=ALU.is_gt),
             reads=["PT"], writes=["M01"])
        P.dma("sp", wst[0:96, 0, :], K.rw_w_up, writes=["wst"])
        P.op("act", lambda e: e.copy(out=wup[:], in_=wst[0:96, 0, :]), reads=["wst"], writes=["wup"])
        P.dma("sp", wst[0:96, 1, :], K.rw_a_up, reads=[], writes=["wst1"])
        P.op("act", lambda e: e.copy(out=aup[:], in_=wst[0:96, 1, :]), reads=["wst1"], writes=["aup"])
        P.dma("sp", wst[:, :, :], K.rw_g_up.rearrange("(c p) n -> p c n", p=128), reads=[], writes=["wst", "wst1"])
        P.op("act", lambda e: e.copy(out=gup[:], in_=wst[:]), reads=["wst", "wst1"], writes=["gup"])
        for (r0, n, mcol, func, dst, kd) in ((768, 96, 0, AF.Tanh, txw[:, :], "txw"), (864, 96, 1, AF.Copy, xap[:, :], "xap"),
                                             (960, 128, 2, AF.Sigmoid, sxg[:, 0, :], "sxg0"),
                                             (1088, 128, 3, AF.Sigmoid, sxg[:, 1, :], "sxg1")):
            P.dma("sp", PT[0:n, :], K.yT_d[r0:r0 + n, :], writes=["PT"])
            tok_shift(P, KP, PT, KK, mul[0:n, mcol:mcol + 1], "PT", "KK", "KP", n=n)
            P.op("act", lambda e, n=n, func=func, dst=dst: e.activation(out=dst, in_=KP[0:n, :], func=func),
                 reads=["KP"], writes=[kd])
        lk = ["txw", "xap", "sxg0", "sxg1"]
        oc = 0
        for ct in cts:
            c0 = ct * 128
            P.dma("sp", PT[:], K.yT_d[c0:c0 + 128, :], writes=["PT"])
            tok_shift(P, RP, PT, KK, prm[:, 0, ct:ct + 1], "PT", "KK", "RP")
            P.dma("sp", PT[:], K.yT_d[256 + c0:256 + c0 + 128, :], writes=["PT"])
            tok_shift(P, KP, PT, KK, prm[:, 1, ct:ct + 1], "PT", "KK", "KP")
            P.dma("sp", PT[:], K.yT_d[512 + c0:512 + c0 + 128, :], writes=["PT"])
            tok_shift(P, VP, PT, KK, prm[:, 2, ct:ct + 1], "PT", "KK", "VP")
            P.dma("sp", K.vb_d[c0:c0 + 128, :], VP[:], reads=["VP"], writes=[("vb_d", ct)])
            for blk in range(8):
                bs = slice(blk * 512, (blk + 1) * 512)
                p0, p1, p2 = ps[0], ps[1], ps[2]
                P.op("pe", lambda e, bs=bs, c0=c0: e.matmul(ps[0][:, :], lhsT=wup[:, c0:c0 + 128], rhs=txw[:, bs],
                                                             start=True, stop=True), reads=["wup", "txw"], writes=[("ps4", 0)])
                P.op("act", lambda e, bs=bs, ct=ct: e.activation(out=CL[:, bs], in_=ps[0][:, :], func=AF.Sigmoid,
                                                                  bias=prm[:, 3, ct:ct + 1]),
                     reads=[("ps4", 0)] + prk, writes=["CL"])
                P.op("pe", lambda e, bs=bs, c0=c0: e.matmul(ps[1][:, :], lhsT=aup[:, c0:c0 + 128], rhs=xap[:, bs],
                                                             start=True, stop=True), reads=["aup", "xap"], writes=[("ps4", 1)])
                P.op("act", lambda e, bs=bs, ct=ct: e.activation(out=AA[:, bs], in_=ps[1][:, :], func=AF.Sigmoid,
                                                                  bias=prm[:, 4, ct:ct + 1]),
                     reads=[("ps4", 1)] + prk, writes=["AA"])
                for cc in range(2):
                    P.op("pe", lambda e, bs=bs, c0=c0, cc=cc: e.matmul(ps[2][:, :], lhsT=gup[:, cc, c0:c0 + 128],
                                                                       rhs=sxg[:, cc, bs], start=(cc == 0), stop=(cc == 1)),
                         reads=["gup", "sxg0", "sxg1"], writes=[("ps4", 2)])
                o = OUT[oc % 2]
                P.op("dve", lambda e, bs=bs, o=o: e.tensor_copy(out=o[:, bs], in_=ps[2][:, :]),
                     reads=[("ps4", 2)], writes=[("OUT", oc % 2)])
            P.dma("sp", K.G_d[c0:c0 + 128, :], OUT[oc % 2][:], reads=[("OUT", oc % 2)], writes=[("G_d", ct)])
            oc += 1
            P.op("dve", lambda e: e.tensor_scalar(out=CL[:], in0=CL[:], scalar1=-0.6065306597126334, scalar2=None,
                                                   op0=ALU.mult), reads=["CL"], writes=["CL"])
            P.op("dve", lambda e, ct=ct: e.tensor_scalar(out=KK[:], in0=KP[:], scalar1=prm[:, 5, ct:ct + 1], scalar2=None,
                                                          op0=ALU.mult), reads=["KP"] + prk, writes=["KK"])
            P.op("act", lambda e: e.activation(out=SQb[:], in_=KK[:], func=AF.Square), reads=["KK"], writes=["SQb"])
            for blk in range(8):
                bs = slice(blk * 512, (blk + 1) * 512)
                P.op("pe", lambda e, bs=bs: e.matmul(ps[3][:, :], lhsT=bones[:], rhs=SQb[:, bs], start=True, stop=True),
                     reads=["bones", "SQb"], writes=[("ps4", 3)])
                P.op("act", lambda e, bs=bs: e.activation(out=PT[:, bs], in_=ps[3][:, :], func=AF.Sqrt),
                     reads=[("ps4", 3)], writes=["PT"])
            P.op("dve", lambda e: e.tensor_scalar(out=PT[:], in0=PT[:], scalar1=1e-12, scalar2=None, op0=ALU.max),
                 reads=["PT"], writes=["PT"])
            P.op("dve", lambda e: e.reciprocal(out=PT[:], in_=PT[:]), reads=["PT"], writes=["PT"])
            P.op("dve", lambda e: e.tensor_tensor(out=KK[:], in0=KK[:], in1=PT[:], op=ALU.mult), reads=["KK", "PT"], writes=["KK"])
            P.op("dve", lambda e, ct=ct: e.tensor_scalar(out=PT[:], in0=AA[:], scalar1=prm[:, 6, ct:ct + 1],
                                                          scalar2=prm[:, 10, ct:ct + 1], op0=ALU.mult, op1=ALU.add),
                 reads=["AA", "PT"] + prk, writes=["PT"])
            P.op("dve", lambda e: e.tensor_tensor(out=K2[:], in0=KP[:], in1=PT[:], op=ALU.mult), reads=["KP", "PT"], writes=["K2"])
            P.op("dve", lambda e, ct=ct: e.scalar_tensor_tensor(out=SQb[:], in0=RP[:], scalar=prm[:, 7, ct:ct + 1], in1=K2[:],
                                                                 op0=ALU.mult, op1=ALU.mult),
                 reads=["RP", "K2", "SQb"] + prk, writes=["SQb"])
            o = OUT[oc % 2]
            for blk in range(8):
                bs = slice(blk * 512, (blk + 1) * 512)
                P.op("pe", lambda e, bs=bs: e.matmul(ps[3][:, :], lhsT=bones[:], rhs=SQb[:, bs], start=True, stop=True),
                     reads=["bones", "SQb"], writes=[("ps4", 3)])
                P.op("dve", lambda e, bs=bs, o=o: e.tensor_tensor(out=o[:, bs], in0=ps[3][:, :], in1=VP[:, bs], op=ALU.mult),
                     reads=[("ps4", 3), "VP"], writes=[("OUT", oc % 2)])
            P.dma("sp", K.BON_d[c0:c0 + 128, :], o[:], reads=[("OUT", oc % 2)], writes=[("BON_d", ct)])
            oc += 1
            P.op("dve", lambda e: e.tensor_tensor_scan(out=PT[:], data0=M01[:], data1=CL[:], initial=0.0,
                                                        op0=ALU.mult, op1=ALU.add), reads=["M01", "CL", "PT"], writes=["PT"])
            P.op("pool", lambda e: e.tensor_tensor(out=CL[:], in0=PT[:], in1=CL[:], op=ALU.subtract),
                 reads=["PT", "CL"], writes=["CL"])
            P.op("act", lambda e: e.activation(out=CL[:], in_=CL[:], func=AF.Exp), reads=["CL"], writes=["CL"])
            v3 = lambda t: t[:].rearrange("p (c t) -> p c t", t=64)
            o = OUT[oc % 2]
            P.op("dve", lambda e, o=o: e.scalar_tensor_tensor(out=o[:], in0=KK[:], scalar=-1.0, in1=CL[:],
                                                               op0=ALU.mult, op1=ALU.mult),
                 reads=["KK", "CL"], writes=[("OUT", oc % 2)])
            P.dma("sp", K.AH_d[c0:c0 + 128, :], o[:], reads=[("OUT", oc % 2)], writes=[("AH_d", ct)])
            oc += 1
            P.op("act", lambda e: e.activation(out=CL[:], in_=PT[:], func=AF.Exp), reads=["PT", "CL"], writes=["CL"])
            o = OUT[oc % 2]
            P.op("dve", lambda e, o=o: e.tensor_tensor(out=o[:], in0=RP[:], in1=CL[:], op=ALU.mult),
                 reads=["RP", "CL"], writes=[("OUT", oc % 2)])
            P.dma("sp", K.RH_d[c0:c0 + 128, :], o[:], reads=[("OUT", oc % 2)], writes=[("RH_d", ct)])
            oc += 1
            P.op("pool", lambda e: e.tensor_copy(out=PCt[:], in_=v3(CL)[:, :, 63]), reads=["CL"], writes=["PCt"])
            P.dma("sp", K.PC_d[c0:c0 + 128, :], PCt[:], reads=["PCt"], writes=[("PC_d", ct)])
            P.op("act", lambda e: e.activation(out=PT[:], in_=PT[:], func=AF.Exp, scale=-1.0), reads=["PT"], writes=["PT"])
            o = OUT[oc % 2]
            P.op("dve", lambda e, o=o: e.tensor_tensor(out=o[:], in0=K2[:], in1=PT[:], op=ALU.mult),
                 reads=["K2", "PT"], writes=[("OUT", oc % 2)])
            P.dma("sp", K.KH_d[c0:c0 + 128, :], o[:], reads=[("OUT", oc % 2)], writes=[("KH_d", ct)])
            oc += 1
            P.op("dve", lambda e: e.tensor_tensor(out=KK[:], in0=KK[:], in1=AA[:], op=ALU.mult), reads=["KK", "AA"], writes=["KK"])
            o = OUT[oc % 2]
            P.op("dve", lambda e, o=o: e.tensor_tensor(out=o[:], in0=KK[:], in1=PT[:], op=ALU.mult),
                 reads=["KK", "PT"], writes=[("OUT", oc % 2)])
            P.dma("sp", K.BH_d[c0:c0 + 128, :], o[:], reads=[("OUT", oc % 2)], writes=[("BH_d", ct)])
            oc += 1
        P.flush()

def phase4c_rwkv_scan(K, heads=range(4)):
    nc, P = K.nc, K.P
    with contextlib.ExitStack() as st:
        def sb(name, shape, dt):
            return st.enter_context(nc.sbuf_tensor(name, shape, dt))
        ident = sb("ident4", [128, 128], BF16)
        make_ident(K, P, ident)
        MaskG = sb("MaskG", [64, 4, 128], F32)
        MaskX = sb("MaskX", [64, 8, 64], F32)
        I8 = sb("I8", [64, 64], F32)
        ones = sb("ones4", [64, 64], F32)
        P.op("pool", lambda e: e.memset(ones[:], 1.0), writes=["ones"])
        for a in range(4):
            for cq in range(2):
                P.op("pool", lambda e, cq=cq, a=a: e.affine_select(
                    out=MaskG[:, a, cq * 64:(cq + 1) * 64], in_=ones[:], pattern=[[1, 64]],
                    compare_op=(ALU.is_gt if cq == 0 else ALU.is_ge), fill=0.0, base=0, channel_multiplier=-1),
                    reads=["ones"], writes=["MaskG"])
        for a in range(8):
            P.op("pool", lambda e, a=a: e.affine_select(out=MaskX[:, a, :], in_=ones[:], pattern=[[-1, 64]],
                                                         compare_op=ALU.is_gt, fill=0.0, base=0, channel_multiplier=1),
                 reads=["ones"], writes=["MaskX"])
        P.op("dve", lambda e: e.tensor_copy(out=I8[:], in_=ident[0:64, 0:64]), reads=["ident"], writes=["I8"])
        AH = sb("AH", [64, S], RD)
        RH = sb("RH", [64, S], RD)
        BH = sb("BH", [64, S], RD)
        KH = sb("KH", [64, S], RD)
        vb = sb("vb", [64, S], BF16)
        PC = sb("PC", [64, NCH], F32)
        ARh = sb("ARh", [64, NCH, 128], RD)
        BKh = sb("BKh", [64, NCH, 128], RD)
        GmB = sb("GmB", [64, NCH, 128], RD)
        GmK = sb("GmK", [64, NCH, 128], RD)
        Btok = sb("Btok", [64, NCH, 64], RD)
        Ktok = sb("Ktok", [64, NCH, 64], RD)
        Vtok = sb("Vtok", [64, NCH, 64], RD)
        X0 = sb("X0", [64, NCH, 64], RD)
        Pm = sb("Pm", [64, NCH, 64], RD)
        oT = sb("oT", [64, S], F32)
        Ast = sb("Ast", [64, 64], F32)
        Abf = sb("Abf", [64, 64], RD)
        Tt = sb("Tt", [64, 64], F32)
        Xs = sb("Xs", [64, 64], RD)
        Us = sb("Us", [64, 64], RD)
        PSb = st.enter_context(nc.psum_tensor("PSb", [128, 1024], BF16))
        PS = [st.enter_context(nc.psum_tensor("PS%d" % i, [128, 512], F32)) for i in range(7)]
        v3 = lambda t: t[:].rearrange("p (c t) -> p c t", t=64)
        Nb = [v3(AH), v3(RH)]
        Xb = [v3(BH), v3(KH)]
        Nk = ["AH", "RH"]
        Xk = ["BH", "KH"]
        for hd in heads:
            r0 = hd * 64
            P.dma("sp", AH[:], K.AH_d[r0:r0 + 64, :], writes=["AH"])
            P.dma("sp", RH[:], K.RH_d[r0:r0 + 64, :], writes=["RH"])
            P.dma("sp", BH[:], K.BH_d[r0:r0 + 64, :], writes=["BH"])
            P.dma("sp", KH[:], K.KH_d[r0:r0 + 64, :], writes=["KH"])
            P.dma("sp", vb[:], K.vb_d[r0:r0 + 64, :], writes=["vb"])
            P.dma("sp", PC[:], K.PC_d[r0:r0 + 64, :], writes=["PC"])
            P.op("dve", lambda e: e.tensor_copy(out=ARh[:, :, 0:64], in_=v3(AH)), reads=["AH"], writes=["ARh"])
            P.op("pool", lambda e: e.tensor_copy(out=ARh[:, :, 64:128], in_=v3(RH)), reads=["RH"], writes=["ARh"])
            P.op("dve", lambda e: e.tensor_copy(out=BKh[:, :, 0:64], in_=v3(BH)), reads=["BH"], writes=["BKh"])
            P.op("pool", lambda e: e.tensor_copy(out=BKh[:, :, 64:128], in_=v3(KH)), reads=["KH"], writes=["BKh"])
            for (src, srck, col0, dst, dk) in ((BKh, "BKh", 0, Btok, "Btok"), (BKh, "BKh", 64, Ktok, "Ktok"), (None, "vb", 0, Vtok, "Vtok")):
                for c16 in range(0, NCH, 16):
                    for cc in range(16):
                        c = c16 + cc
                        in_ = vb[:, c * 64:(c + 1) * 64] if src is None else src[:, c, col0:col0 + 64]
                        P.op("pe", lambda e, cc=cc, in_=in_: e.transpose(out=PSb[0:64, cc * 64:(cc + 1) * 64], in_=in_,
                                                                         identity=ident[0:64, 0:64]),
                             reads=[srck, "ident"], writes=["PSb"])
                    P.op("act", lambda e, c16=c16, dst=dst: e.copy(out=dst[:, c16:c16 + 16, :].rearrange("p c k -> p (c k)"),
                                                                    in_=PSb[0:64, :]), reads=["PSb"], writes=[dk])
            gi = 0
            for (col0, dst, dk) in ((0, GmB, "GmB"), (64, GmK, "GmK")):
                for c4 in range(0, NCH, 4):
                    b = gi % 2
                    gi += 1
                    for cc in range(4):
                        c = c4 + cc
                        P.op("pe", lambda e, c=c, cc=cc, b=b, col0=col0: e.matmul(
                            PS[b][0:64, cc * 128:(cc + 1) * 128], lhsT=BKh[:, c, col0:col0 + 64], rhs=ARh[:, c, :],
                            start=True, stop=True), reads=["BKh", "ARh"], writes=[("PS", b)])
                    P.op("dve", lambda e, c4=c4, dst=dst, b=b: e.tensor_tensor(
                        out=dst[:, c4:c4 + 4, :], in0=PS[b][0:64, :].rearrange("p (a t) -> p a t", t=128), in1=MaskG[:],
                        op=ALU.mult), reads=[("PS", b), "MaskG"], writes=[dk])
            for c8 in range(0, NCH, 8):
                for cc in range(8):
                    c = c8 + cc
                    P.op("pe", lambda e, c=c, cc=cc: e.matmul(PS[2][0:64, cc * 64:(cc + 1) * 64], lhsT=ARh[:, c, 0:64],
                                                               rhs=BKh[:, c, 0:64], start=True, stop=True),
                         reads=["ARh", "BKh"], writes=[("PS", 2)])
                P.op("dve", lambda e, c8=c8: e.tensor_tensor(
                    out=X0[:, c8:c8 + 8, :], in0=PS[2][0:64, :].rearrange("p (a t) -> p a t", t=64), in1=MaskX[:],
                    op=ALU.mult), reads=[("PS", 2), "MaskX"], writes=["X0"])
            N0 = GmB[:, :, 0:64]
            P.op("dve", lambda e, N0=N0: e.tensor_tensor(out=Pm[:], in0=N0, in1=I8[:].unsqueeze(1).to_broadcast([64, NCH, 64]),
                                                         op=ALU.add), reads=["GmB", "I8"], writes=["Pm"])
            curN, curNk = N0, "GmB"
            curX, curXk = X0[:], "X0"
            for lvl in range(1, 6):
                nX, nXk = Xb[lvl % 2], Xk[lvl % 2]
                nN, nNk = Nb[lvl % 2], Nk[lvl % 2]
                for c8 in range(0, NCH, 8):
                    pb_ = (c8 // 8) % 2
                    for cc in range(8):
                        c = c8 + cc
                        P.op("pe", lambda e, c=c, cc=cc, curN=curN, curX=curX, pb_=pb_: e.matmul(
                            PS[0 + pb_][0:64, cc * 64:(cc + 1) * 64], lhsT=curN[:, c, :], rhs=curX[:, c, :], start=True, stop=True),
                            reads=[curNk, curXk], writes=[("PS", 0 + pb_)])
                    P.op("act", lambda e, c8=c8, nX=nX, pb_=pb_: e.copy(out=nX[:, c8:c8 + 8, :],
                                                                in_=PS[0 + pb_][0:64, :].rearrange("p (a t) -> p a t", t=64)),
                         reads=[("PS", 0 + pb_)], writes=[nXk])
                    if lvl < 5:
                        for cc in range(8):
                            c = c8 + cc
                            P.op("pe", lambda e, c=c, cc=cc, curN=curN, curX=curX, pb_=pb_: e.matmul(
                                PS[2 + pb_][0:64, cc * 64:(cc + 1) * 64], lhsT=curX[:, c, :], rhs=curN[:, c, :], start=True, stop=True),
                                reads=[curNk, curXk], writes=[("PS", 2 + pb_)])
                        P.op("act", lambda e, c8=c8, nN=nN, pb_=pb_: e.copy(out=nN[:, c8:c8 + 8, :],
                                                                    in_=PS[2 + pb_][0:64, :].rearrange("p (a t) -> p a t", t=64)),
                             reads=[("PS", 2 + pb_)], writes=[nNk])
                    for cc in range(8):
                        c = c8 + cc
                        P.op("pe", lambda e, c=c, cc=cc, nX=nX, pb_=pb_: e.matmul(
                            PS[4 + pb_][0:64, cc * 64:(cc + 1) * 64], lhsT=nX[:, c, :], rhs=Pm[:, c, :], start=True, stop=True),
                            reads=[nXk, "Pm"], writes=[("PS", 4 + pb_)])
                    P.op("dve", lambda e, c8=c8, pb_=pb_: e.tensor_tensor(
                        out=Pm[:, c8:c8 + 8, :], in0=PS[4 + pb_][0:64, :].rearrange("p (a t) -> p a t", t=64),
                        in1=Pm[:, c8:c8 + 8, :], op=ALU.add), reads=[("PS", 4 + pb_), "Pm"], writes=["Pm"])
                curN, curNk, curX, curXk = nN, nNk, nX, nXk
            P.op("pool", lambda e: e.memset(Ast[:], 0.0), writes=["Ast"])
            P.op("pool", lambda e: e.memset(Abf[:], 0.0), writes=["Abf"])
            for c in range(NCH):
                P.op("pool", lambda e, c=c: e.tensor_scalar(out=Tt[:], in0=Ast[:], scalar1=PC[:, c:c + 1], scalar2=0.0,
                                                             op0=ALU.mult, op1=ALU.add), reads=["Ast", "PC"], writes=["Tt"])
                P.op("pe", lambda e, c=c: e.matmul(PS[0][0:64, 0:64], lhsT=ARh[:, c, 0:64], rhs=Abf[:], start=True, stop=False),
                     reads=["ARh", "Abf"], writes=[("PS", 0)])
                P.op("pe", lambda e, c=c: e.matmul(PS[0][0:64, 0:64], lhsT=GmK[:, c, 0:64], rhs=Vtok[:, c, :], start=False, stop=True),
                     reads=["GmK", "Vtok"], writes=[("PS", 0)])
                P.op("act", lambda e: e.copy(out=Xs[:], in_=PS[0][0:64, 0:64]), reads=[("PS", 0)], writes=["Xs"])
                P.op("pe", lambda e, c=c: e.matmul(PS[1][0:64, 0:64], lhsT=Pm[:, c, :], rhs=Xs[:], start=True, stop=True),
                     reads=["Pm", "Xs"], writes=[("PS", 1)])
                P.op("dve", lambda e: e.tensor_copy(out=Us[:], in_=PS[1][0:64, 0:64]), reads=[("PS", 1)], writes=["Us"])
                P.op("pe", lambda e, c=c: e.matmul(PS[6][0:64, 0:64], lhsT=Btok[:, c, :], rhs=Us[:], start=True, stop=False),
                     reads=["Btok", "Us"], writes=[("PS", 6)])
                P.op("pe", lambda e, c=c: e.matmul(PS[6][0:64, 0:64], lhsT=Ktok[:, c, :], rhs=Vtok[:, c, :], start=False, stop=True),
                     reads=["Ktok", "Vtok"], writes=[("PS", 6)])
                ob = 2 + (c % 2)
                P.op("pe", lambda e, c=c, ob=ob: e.matmul(PS[ob][0:64, 0:64], lhsT=Abf[:], rhs=ARh[:, c, 64:128], start=True, stop=False),
                     reads=["Abf", "ARh"], writes=[("PS", ob)])
                P.op("pe", lambda e, c=c, ob=ob: e.matmul(PS[ob][0:64, 0:64], lhsT=Us[:], rhs=GmB[:, c, 64:128], start=False, stop=False),
                     reads=["Us", "GmB"], writes=[("PS", ob)])
                P.op("pe", lambda e, c=c, ob=ob: e.matmul(PS[ob][0:64, 0:64], lhsT=Vtok[:, c, :], rhs=GmK[:, c, 64:128], start=False, stop=True),
                     reads=["Vtok", "GmK"], writes=[("PS", ob)])
                P.op("dve", lambda e, c=c: e.scalar_tensor_tensor(out=Abf[:], in0=PS[6][0:64, 0:64], scalar=PC[:, c:c + 1], in1=Tt[:],
                                                                   op0=ALU.mult, op1=ALU.add),
                     reads=[("PS", 6), "Tt", "PC"], writes=["Abf"])
                P.op("dve", lambda e, c=c: e.scalar_tensor_tensor(out=Ast[:], in0=PS[6][0:64, 0:64], scalar=PC[:, c:c + 1], in1=Tt[:],
                                                                   op0=ALU.mult, op1=ALU.add),
                     reads=[("PS", 6), "Tt", "PC"], writes=["Ast"])
                P.op("act", lambda e, c=c, ob=ob: e.copy(out=oT[:, c * 64:(c + 1) * 64], in_=PS[ob][0:64, 0:64]),
                     reads=[("PS", ob)], writes=[("oT", c // 8)])
            P.dma("sp", K.oT_d[r0:r0 + 64, :], oT[:], reads=[("oT", q) for q in range(8)], writes=[("oT_d", hd)])
        P.flush()

def phase4d_rwkv_post(K, cts=range(2)):
    nc, P = K.nc, K.P
    with contextlib.ExitStack() as st:
        def sb(name, shape, dt):
            return st.enter_context(nc.sbuf_tensor(name, shape, dt))
        ident = sb("ident4d", [128, 128], BF16)
        make_ident(K, P, ident)
        bonesf = sb("bonesf", [128, 128], F32)
        P.op("pool", lambda e: e.memset(bonesf[:], 0.0), writes=["bonesf"])
        P.op("pool", lambda e: e.memset(bonesf[0:64, 0:64], 1.0), reads=["bonesf"], writes=["bonesf"])
        P.op("pool", lambda e: e.memset(bonesf[64:128, 64:128], 1.0), reads=["bonesf"], writes=["bonesf"])
        prm = sb("prm4d", [128, 2, 2], F32)
        P.dma("sp", prm[:, 0, :], K.rw_prm[8], writes=["prm0"])
        P.dma("sp", prm[:, 1, :], K.rw_prm[9], writes=["prm1"])
        o = sb("o4d", [128, S], F32)
        osq = sb("osq", [128, S], F32)
        bon = sb("bon", [128, S], BF16)
        gg = sb("gg", [128, S], BF16)
        Mb = [sb("Mb%d" % i, [128, 512], F32) for i in range(2)]
        Vb = [sb("Vb%d" % i, [128, 512], F32) for i in range(2)]
        Yb = [sb("Yb%d" % i, [128, 512], F32) for i in range(2)]
        Ob = [sb("Ob%d" % i, [128, 512], BF16) for i in range(2)]
        Tk = [sb("Tk%d" % i, [128, 4, 128], BF16) for i in range(2)]
        ps = [st.enter_context(nc.psum_tensor("p4d_%d" % i, [128, 512], F32)) for i in range(4)]
        pst = [st.enter_context(nc.psum_tensor("p4dt_%d" % i, [128, 4, 128], BF16)) for i in range(2)]
        it = 0
        for ct in cts:
            c0 = ct * 128
            P.dma("sp", o[:], K.oT_d[c0:c0 + 128, :], writes=["o"])
            P.dma("sp", bon[:], K.BON_d[c0:c0 + 128, :], writes=["bon"])
            P.dma("sp", gg[:], K.G_d[c0:c0 + 128, :], writes=["gg"])
            P.op("act", lambda e: e.activation(out=osq[:], in_=o[:], func=AF.Square), reads=["o"], writes=["osq"])
            for blk in range(8):
                s2 = it % 2
                it += 1
                bs = slice(blk * 512, (blk + 1) * 512)
                P.op("pe", lambda e, bs=bs, s2=s2: e.matmul(ps[s2][:, :], lhsT=bonesf[:], rhs=o[:, bs], start=True, stop=True),
                     reads=["bonesf", "o"], writes=[("p4d", s2)])
                P.op("pe", lambda e, bs=bs, s2=s2: e.matmul(ps[2 + s2][:, :], lhsT=bonesf[:], rhs=osq[:, bs], start=True, stop=True),
                     reads=["bonesf", "osq"], writes=[("p4d", 2 + s2)])
                P.op("act", lambda e, s2=s2: e.activation(out=Mb[s2][:], in_=ps[s2][:, :], func=AF.Copy, scale=1.0 / 64),
                     reads=[("p4d", s2)], writes=[("Mb", s2)])
                P.op("pool", lambda e, s2=s2: e.tensor_tensor(out=Vb[s2][:], in0=Mb[s2][:], in1=Mb[s2][:], op=ALU.mult),
                     reads=[("Mb", s2)], writes=[("Vb", s2)])
                P.op("dve", lambda e, s2=s2: e.scalar_tensor_tensor(out=Vb[s2][:], in0=ps[2 + s2][:, :], scalar=1.0 / 64, in1=Vb[s2][:],
                                                                     op0=ALU.mult, op1=ALU.subtract),
                     reads=[("p4d", 2 + s2), ("Vb", s2)], writes=[("Vb", s2)])
                P.op("dve", lambda e, s2=s2: e.tensor_scalar(out=Vb[s2][:], in0=Vb[s2][:], scalar1=64e-5, scalar2=None, op0=ALU.add),
                     reads=[("Vb", s2)], writes=[("Vb", s2)])
                P.op("act", lambda e, s2=s2: e.activation(out=Vb[s2][:], in_=Vb[s2][:], func=AF.Sqrt),
                     reads=[("Vb", s2)], writes=[("Vb", s2)])
                P.op("dve", lambda e, s2=s2: e.reciprocal(out=Vb[s2][:], in_=Vb[s2][:]), reads=[("Vb", s2)], writes=[("Vb", s2)])
                P.op("pool", lambda e, s2=s2, bs=bs: e.tensor_tensor(out=Yb[s2][:], in0=o[:, bs], in1=Mb[s2][:], op=ALU.subtract),
                     reads=["o", ("Mb", s2)], writes=[("Yb", s2)])
                P.op("dve", lambda e, s2=s2: e.tensor_tensor(out=Yb[s2][:], in0=Yb[s2][:], in1=Vb[s2][:], op=ALU.mult),
                     reads=[("Yb", s2), ("Vb", s2)], writes=[("Yb", s2)])
                P.op("dve", lambda e, s2=s2, ct=ct: e.tensor_scalar(out=Yb[s2][:], in0=Yb[s2][:], scalar1=prm[:, 0, ct:ct + 1],
                                                                     scalar2=prm[:, 1, ct:ct + 1], op0=ALU.mult, op1=ALU.add),
                     reads=[("Yb", s2), "prm0", "prm1"], writes=[("Yb", s2)])
                P.op("pool", lambda e, s2=s2, bs=bs: e.tensor_tensor(out=Yb[s2][:], in0=Yb[s2][:], in1=bon[:, bs], op=ALU.add),
                     reads=[("Yb", s2), "bon"], writes=[("Yb", s2)])
                P.op("dve", lambda e, s2=s2, bs=bs: e.tensor_tensor(out=Ob[s2][:], in0=Yb[s2][:], in1=gg[:, bs], op=ALU.mult),
                     reads=[("Yb", s2), "gg"], writes=[("Ob", s2)])
                for q in range(4):
                    P.op("pe", lambda e, s2=s2, q=q: e.transpose(out=pst[s2][:, q, :], in_=Ob[s2][:, q * 128:(q + 1) * 128],
                                                                 identity=ident[:]),
                         reads=[("Ob", s2), "ident"], writes=[("p4dt", s2)])
                P.op("act", lambda e, s2=s2: e.copy(out=Tk[s2][:], in_=pst[s2][:]), reads=[("p4dt", s2)], writes=[("Tk", s2)])
                P.dma("sp", K.ro_loc_d[blk // 4].rearrange("(t p) c -> p t c", p=128)[:, (blk % 4) * 4:(blk % 4 + 1) * 4, c0:c0 + 128], Tk[s2][:],
                      reads=[("Tk", s2)], writes=[("ro_tok_d", ct, blk)])
        P.flush()


def phase4e_allgather(K):
    P = K.P
    for hh in range(2):
        P.coll(lambda e, hh=hh: e.collective_compute("AllGather", ALU.bypass, replica_groups=[[0, 1, 2, 3], [4, 5, 6, 7]],
                                                     ins=[K.ro_loc_d[hh].opt()], outs=[K.ro_all_d[hh].opt()]),
               reads=[("ro_loc", hh)], writes=[("ro_all", hh)])
    P.flush()


def phase5a_select(K):
    nc, P = K.nc, K.P
    with contextlib.ExitStack() as st:
        def sb(name, shape, dt):
            return st.enter_context(nc.sbuf_tensor(name, shape, dt))
        ro = sb("ro_tok", [128, 32, 1024], BF16)
        selT = sb("selT", [128, 32, 1024], BF16)
        qrow = sb("qrow", [128, 1024], F32)
        tki = sb("tki", [128, 32], I32)
        tkf = sb("tkf", [128, 32], F32)
        mo = [sb("mo%d" % i, [128, 512], BF16) for i in range(2)]
        at = sb("at5", [128, 8, 1024], BF16)
        ps = [st.enter_context(nc.psum_tensor("p5a_%d" % i, [128, 512], F32)) for i in range(2)]
        for q4 in range(4):
            for hh in range(2):
                P.dma("sp", ro[:, hh * 16:(hh + 1) * 16, q4 * 256:(q4 + 1) * 256],
                      K.ro_all_d[hh][q4 * 2048:(q4 + 1) * 2048, :].rearrange("(t p) c -> p t c", p=128), writes=[("ro", q4, hh)])
        rok = [("ro", q4, hh) for q4 in range(4) for hh in range(2)]
        P.dma("sp", qrow[:], bcast_rows(K.qpos_row, 1024), writes=["qrow"])
        P.op("pool", lambda e: e.iota(tki[:], pattern=[[128, 32]], base=0, channel_multiplier=1), writes=["tki"])
        P.op("dve", lambda e: e.tensor_copy(out=tkf[:], in_=tki[:]), reads=["tki"], writes=["tkf"])
        for T in range(32):
            P.op("dve", lambda e, T=T: e.tensor_scalar(out=selT[:, T, :], in0=qrow[:], scalar1=tkf[:, T:T + 1], scalar2=0.0,
                                                      op0=ALU.is_equal, op1=ALU.add), reads=["qrow", "tkf"], writes=[("selT", T)])
        sk = [("selT", T) for T in range(32)]
        P.dma("sp", at[:], K.attT_d.rearrange("h p t -> p h t"), writes=["at5"])
        P.dma("sp", K.mixT_d.rearrange("k p t -> p k t")[:, 0:8, :], at[:], reads=["at5"], writes=["mixa"])
        i = 0
        for m in range(8):
            for half in range(2):
                s2 = i % 2
                i += 1
                for T in range(32):
                    P.op("pe", lambda e, T=T, m=m, half=half, s2=s2: e.matmul(
                        ps[s2][:, :], lhsT=ro[:, T, m * 128:(m + 1) * 128], rhs=selT[:, T, half * 512:(half + 1) * 512],
                        start=(T == 0), stop=(T == 31)), reads=rok + sk, writes=[("p5a", s2)])
                P.op("act", lambda e, s2=s2: e.copy(out=mo[s2][:], in_=ps[s2][:, :]), reads=[("p5a", s2)], writes=[("mo", s2)])
                P.dma("sp", K.mixT_d[8 + m, :, half * 512:(half + 1) * 512], mo[s2][:], reads=[("mo", s2)], writes=[("mixr", m, half)])
        P.flush()


def phase5b_outproj(K):
    nc, P = K.nc, K.P
    with contextlib.ExitStack() as st:
        def sb(name, shape, dt):
            return st.enter_context(nc.sbuf_tensor(name, shape, dt))
        ident = sb("ident5", [128, 128], BF16)
        make_ident(K, P, ident)
        G2, SH2 = load_G_SH(K, P, st, 3, 4, K.norm2_g, "p5")
        GT1 = sb("GT1", [128, D], F32)
        P.dma("sp", GT1[:], bcast_rows(K.mod_d[2 * D:3 * D], D), writes=["GT1"])
        Wo = sb("Wo", [128, 16, D], BF16)
        stg = [sb("wstg5_%d" % i, [128, 4, 512], F32) for i in range(2)]
        wk = load_weight_bf16(K, P, stg, Wo, 0, K.w_out, D, "Wo")
        mixT = sb("mixT", [128, 16, 512], BF16)
        T = norm_tiles_alloc(K, st, "p5")
        x1 = T["xt"]
        hT = [sb("hT5_0", [128, 16, 512], BF16)] * 2
        xo = [sb("xo%d" % i, [128, D], F32) for i in range(2)]
        ps = [st.enter_context(nc.psum_tensor("p5b_%d" % i, [128, 512], F32)) for i in range(2)]
        ss, junk, hb, pT = T["ss"], T["junk"], T["hb"], T["pT"]
        gi = 0
        for blk in range(2):
            hs = 0
            P.dma("sp", mixT[:], K.mixT_d.rearrange("k p t -> p k t")[:, :, blk * 512:(blk + 1) * 512], writes=["mixT"])
            for ti in range(4):
                t = blk * 4 + ti
                xs = t % 2
                P.dma("sp", xo[xs][:], K.x_own[t * 128:(t + 1) * 128, :], writes=[("xo", xs)])
                for cg in range(4):
                    b = gi % 2
                    gi += 1
                    for k in range(16):
                        P.op("pe", lambda e, b=b, k=k, t=t, cg=cg: e.matmul(
                            ps[b][:, :], lhsT=mixT[:, k, (t % 4) * 128:(t % 4 + 1) * 128], rhs=Wo[:, k, cg * 512:(cg + 1) * 512],
                            start=(k == 0), stop=(k == 15)), reads=["mixT"] + wk, writes=[("p5b", b)])
                    cs = slice(cg * 512, (cg + 1) * 512)
                    P.op("dve", lambda e, b=b, xs=xs, cs=cs: e.tensor_tensor(out=x1[xs][:, cs], in0=ps[b][:, :], in1=GT1[:, cs], op=ALU.mult),
                         reads=[("p5b", b), "GT1"], writes=[("xt", xs)])
                    P.op("pool", lambda e, xs=xs, cs=cs: e.tensor_tensor(out=x1[xs][:, cs], in0=x1[xs][:, cs], in1=xo[xs][:, cs], op=ALU.add),
                         reads=[("xt", xs), ("xo", xs)], writes=[("xt", xs)])
                P.dma("sp", K.x1_d[t * 128:(t + 1) * 128, :], x1[xs][:], reads=[("xt", xs)], writes=[("x1_d", t)])
                P.op("act", lambda e, xs=xs: e.activation(out=junk[:], in_=x1[xs][:], func=AF.Square, accum_out=ss[:, 0:1]),
                     reads=[("xt", xs)], writes=["junk", "ss0"])
                P.op("dve", lambda e: e.tensor_scalar(out=ss[:, 1:2], in0=ss[:, 0:1], scalar1=1.0 / D, scalar2=1e-6,
                                                       op0=ALU.mult, op1=ALU.add), reads=["ss0"], writes=["ss1"])
                P.op("act", lambda e: e.activation(out=ss[:, 2:3], in_=ss[:, 1:2], func=AF.Sqrt), reads=["ss1"], writes=["ss2"])
                P.op("dve", lambda e: e.reciprocal(out=ss[:, 3:4], in_=ss[:, 2:3]), reads=["ss2"], writes=["ss3"])
                P.op("dve", lambda e, xs=xs: e.scalar_tensor_tensor(out=x1[xs][:], in0=x1[xs][:], scalar=ss[:, 3:4], in1=G2[:],
                                                                   op0=ALU.mult, op1=ALU.mult),
                     reads=[("xt", xs), "ss3", "G"], writes=[("xt", xs)])
                P.op("pool", lambda e, xs=xs: e.tensor_tensor(out=hb[xs][:], in0=x1[xs][:], in1=SH2[:], op=ALU.add),
                     reads=[("xt", xs), "SH"], writes=[("hb", xs)])
                for half in range(2):
                    for kk in range(8):
                        k = half * 8 + kk
                        P.op("pe", lambda e, k=k, kk=kk, half=half, xs=xs: e.transpose(
                            out=pT[half][:, kk, :], in_=hb[xs][:, k * 128:(k + 1) * 128], identity=ident[:]),
                            reads=[("hb", xs), "ident"], writes=[("pT", half)])
                    o_ = hT[hs][:, half * 8:(half + 1) * 8, ti * 128:(ti + 1) * 128]
                    if half == 0:
                        P.op("act", lambda e, o_=o_, half=half: e.copy(out=o_, in_=pT[half][:]), reads=[("pT", half)], writes=[("hT5", hs, ti, half)])
                    else:
                        P.op("dve", lambda e, o_=o_, half=half: e.tensor_copy(out=o_, in_=pT[half][:]), reads=[("pT", half)], writes=[("hT5", hs, ti, half)])
            P.dma("sp", K.h2T_d.rearrange("k p t -> p k t")[:, :, blk * 512:(blk + 1) * 512], hT[hs][:],
                  reads=[("hT5", hs, ti, half) for ti in range(4) for half in range(2)], writes=[("h2T_d", blk)])
        P.flush()


def phase5c_ffn(K):
    nc, P = K.nc, K.P
    NF = 5632 // 128
    with contextlib.ExitStack() as st:
        def sb(name, shape, dt):
            return st.enter_context(nc.sbuf_tensor(name, shape, dt))
        h2T = sb("h2T", [128, 16, OWN], BF16)
        P.dma("sp", h2T[:], K.h2T_d.rearrange("k p t -> p k t"), writes=["h2T"])
        ao = [sb("ao%d" % i, [128, 512], BF16) for i in range(2)]
        stg = [sb("wstg6_%d" % i, [128, 4, 512], F32) for i in range(4)]
        Wg = [sb("Wg%d" % i, [128, 16, 512], BF16) for i in range(2)]
        Wu = [sb("Wu%d" % i, [128, 16, 512], BF16) for i in range(2)]
        sg = [sb("sg%d" % i, [128, 512], F32) for i in range(2)]
        ps = [st.enter_context(nc.psum_tensor("p5c_%d" % i, [128, 512], F32)) for i in range(4)]
        gi = 0

        def load_group(fg, defer=None):
            ws = fg % 2
            load_weight_bf16(K, P, stg, Wg[ws], 0, K.w_ffn_gate[:, fg * 512:(fg + 1) * 512], 512, ("Wg", ws), defer=defer)
            load_weight_bf16(K, P, stg, Wu[ws], 0, K.w_ffn_up[:, fg * 512:(fg + 1) * 512], 512, ("Wu", ws), defer=defer)
        load_group(0)
        for fg in range(11):
            ws = fg % 2
            pend = []
            if fg + 1 < 11:
                load_group(fg + 1, defer=pend)
            for f4 in range(4):
                f = fg * 4 + f4
                for tb in range(2):
                    b = gi % 2
                    gi += 1
                    if pend:
                        pend.pop(0)()
                    for k in range(16):
                        P.op("pe", lambda e, b=b, k=k, f4=f4, tb=tb, ws=ws: e.matmul(
                            ps[b][:, :], lhsT=Wg[ws][:, k, f4 * 128:(f4 + 1) * 128], rhs=h2T[:, k, tb * 512:(tb + 1) * 512],
                            start=(k == 0), stop=(k == 15)), reads=["h2T", (("Wg", ws), 0, (k // 4) * 4)], writes=[("p5c", b)])
                    for k in range(16):
                        P.op("pe", lambda e, b=b, k=k, f4=f4, tb=tb, ws=ws: e.matmul(
                            ps[2 + b][:, :], lhsT=Wu[ws][:, k, f4 * 128:(f4 + 1) * 128], rhs=h2T[:, k, tb * 512:(tb + 1) * 512],
                            start=(k == 0), stop=(k == 15)), reads=["h2T", (("Wu", ws), 0, (k // 4) * 4)], writes=[("p5c", 2 + b)])
                    P.op("act", lambda e, b=b: e.activation(out=sg[b][:], in_=ps[b][:, :], func=AF.Silu),
                         reads=[("p5c", b)], writes=[("sg", b)])
                    P.op("dve", lambda e, b=b: e.tensor_tensor(out=ao[b][:], in0=ps[2 + b][:, :], in1=sg[b][:], op=ALU.mult),
                         reads=[("p5c", 2 + b), ("sg", b)], writes=[("ao", b)])
                    P.dma("sp", K.actT_d[f, :, tb * 512:(tb + 1) * 512], ao[b][:], reads=[("ao", b)], writes=[("actT_d", f, tb)])
        P.flush()
    with contextlib.ExitStack() as st:
        def sb(name, shape, dt):
            return st.enter_context(nc.sbuf_tensor(name, shape, dt))
        GT2 = sb("GT2", [128, D], F32)
        P.dma("sp", GT2[:], bcast_rows(K.mod_d[5 * D:6 * D], D), writes=["GT2"])
        actT = sb("actT", [128, NF, OWN], BF16)
        for q in range(4):
            P.dma("sp", actT[:, q * 11:(q + 1) * 11, :], K.actT_d.rearrange("f p t -> p f t")[:, q * 11:(q + 1) * 11, :], writes=[("actT", q)])
        ak = [("actT", q) for q in range(4)]
        stg = [sb("wstg7_%d" % i, [128, 4, 256], F32) for i in range(4)]
        ps = [st.enter_context(nc.psum_tensor("p5d_%d" % i, [128, 512], F32)) for i in range(2)]
        gi = 0
        Wd = [sb("Wd%d" % i, [128, NF, 256], BF16) for i in range(2)]
        x1 = [sb("x1_%d" % i, [128, 256], F32) for i in range(2)]
        yo = [sb("yo%d" % i, [128, 256], F32) for i in range(2)]
        wdv = K.w_ffn_down.rearrange("(k p) n -> p k n", p=128)
        engs = ["pool", "dve", "act"]

        def load_wd(cg, defer=None):
            wsl = cg % 2
            for k0 in range(0, NF, 4):
                if defer is not None:
                    defer.append(lambda k0=k0: load_wd_piece(cg, wsl, k0))
                else:
                    load_wd_piece(cg, wsl, k0)

        def load_wd_piece(cg, wsl, k0):
            if True:
                i = K.wcnt
                K.wcnt += 1
                sl = i % 4
                P.dma("sp", stg[sl][:, 0:4, 0:256], wdv[:, k0:k0 + 4, cg * 256:(cg + 1) * 256], writes=[("wstg", sl)])
                eng = engs[i % 3]
                o_ = Wd[wsl][:, k0:k0 + 4, :]
                if eng == "act":
                    P.op("act", lambda e, o_=o_, sl=sl: e.copy(out=o_, in_=stg[sl][:, 0:4, 0:256]), reads=[("wstg", sl)], writes=[("Wd", wsl, k0)])
                else:
                    P.op(eng, lambda e, o_=o_, sl=sl: e.tensor_copy(out=o_, in_=stg[sl][:, 0:4, 0:256]), reads=[("wstg", sl)], writes=[("Wd", wsl, k0)])
        load_wd(0)
        for cg in range(8):
            wsl = cg % 2
            cs = slice(cg * 256, (cg + 1) * 256)
            pend = []
            if cg + 1 < 8:
                load_wd(cg + 1, defer=pend)
            for t in range(8):
                b = gi % 2
                gi += 1
                for _ in range(2):
                    if pend:
                        pend.pop(0)()
                P.dma("sp", x1[b][:], K.x1_d[t * 128:(t + 1) * 128, cs], writes=[("x1", b)])
                for f in range(NF):
                    P.op("pe", lambda e, b=b, f=f, t=t, wsl=wsl: e.matmul(ps[b][:, 0:256], lhsT=actT[:, f, t * 128:(t + 1) * 128], rhs=Wd[wsl][:, f, :],
                                                                          start=(f == 0), stop=(f == NF - 1)),
                         reads=[("actT", f // 11), ("Wd", wsl, (f // 4) * 4)], writes=[("p5c", b)])
                P.op("dve", lambda e, b=b, cs=cs: e.tensor_tensor(out=yo[b][:], in0=ps[b][:, 0:256], in1=GT2[:, cs], op=ALU.mult),
                     reads=[("p5c", b), "GT2"], writes=[("yo", b)])
                P.op("pool", lambda e, b=b: e.tensor_tensor(out=yo[b][:], in0=yo[b][:], in1=x1[b][:], op=ALU.add),
                     reads=[("yo", b), ("x1", b)], writes=[("yo", b)])
                P.dma("sp", K.out[t * 128:(t + 1) * 128, cs], yo[b][:], reads=[("yo", b)], writes=[("out", t, cg)])
        P.flush()


def phase_final_copy(K):
    nc, P = K.nc, K.P
    with contextlib.ExitStack() as st:
        xt = [st.enter_context(nc.sbuf_tensor("fx%d" % i, [128, D], F32)) for i in range(2)]
        for t in range(8):
            s = t % 2
            P.dma("sp", xt[s][:], K.x_own[t * 128:(t + 1) * 128, :], writes=[("fx", s)])
            P.dma("sp", K.out[t * 128:(t + 1) * 128, :], xt[s][:], reads=[("fx", s)], writes=[("out", t)])
        P.flush()


def own_tiles(j):
    r = []
    for m in range(4):
        r += [8 * m + j, 8 * m + 7 - j]
    return r


def build_program(debug=False, stages=99, cts=range(2), dbg_list=None, skip_att=False):
    nc = bass.Bass("TRN2", target_bir_lowering=False)
    K = Ctx()
    K.stages = stages
    K.cts = cts
    K.skip_att = skip_att
    K.nc = nc
    K.dbg = {}
    K.wcnt = 0

    def inp(name, shape, dt=F32):
        return nc.dram_tensor(name, list(shape), dt, kind="ExternalInput").ap()

    def scratch(name, shape, dt):
        return nc.dram_tensor(name, list(shape), dt, kind="Internal").ap()

    K.x_full = inp("x_full", [S, D])
    K.x_own = inp("x_own", [OWN, D])
    K.c_arr = inp("c_arr", [128, 16])
    K.pos_full = inp("pos_full", [128, 32], I32)
    K.invf_att = inp("invf_att", [128, 16])
    K.invf_idx = inp("invf_idx", [128, 8])
    K.w_ada = inp("w_ada", [D, 3072])
    K.b_ada = inp("b_ada", [3072])
    K.norm1_g = inp("norm1_g", [D])
    K.k_norm_g = inp("k_norm_g", [128])
    K.q_norm_g = inp("q_norm_g", [128])
    K.pos_own = inp("pos_own", [128, 8], I32)
    K.qpos_own = inp("qpos_own", [128, 8])
    K.w_in = inp("w_in", [D, 4176])
    K.rw_prm = [inp("rwp%d" % i, [128, 2]) for i in range(10)]
    K.w_in_rw = inp("w_in_rw", [D, 1216])
    K.rw_mul = inp("rw_mul", [128, 4])
    K.rw_w_up = inp("rw_w_up", [96, 256])
    K.rw_a_up = inp("rw_a_up", [96, 256])
    K.rw_g_up = inp("rw_g_up", [256, 256])
    K.qpos_row = inp("qpos_row", [OWN])
    K.w_out = inp("w_out", [D, D])
    K.norm2_g = inp("norm2_g", [D])
    K.w_ffn_gate = inp("w_ffn_gate", [D, 5632])
    K.w_ffn_up = inp("w_ffn_up", [D, 5632])
    K.w_ffn_down = inp("w_ffn_down", [5632, D])
    K.out = nc.dram_tensor("y_own", [OWN, D], F32, kind="ExternalOutput").ap()
    K.modq_d = scratch("modq_d", [1, 3072], F32)
    K.mod4_d = scratch("mod4_d", [4, 3072], F32)
    K.mod_d = K.mod4_d.rearrange("a n -> (a n)")
    K.hT_d = scratch("hT_d", [16, 128, S], BF16)
    K.kT_d = scratch("kT_d", [8, 128, S], BF16)
    K.v_d = scratch("v_d", [S, 8 * 129], BF16)
    K.ikT_d = scratch("ikT_d", [64, S], BF16)
    K.yT_d = scratch("yT_d", [1216, S], F32)
    K.qT_d = scratch("qT_d", [8, 128, OWN], BF16)
    K.iqT_d = scratch("iqT_d", [64, OWN, 16], BF16)
    K.iw_d = scratch("iw_d", [OWN, 16], F32)
    K.attT_d = scratch("attT_d", [8, 128, OWN], BF16)
    for nm in ("vb_d", "G_d", "BON_d", "AH_d", "RH_d", "BH_d", "KH_d"):
        setattr(K, nm, scratch(nm, [256, S], BF16))
    K.PC_d = scratch("PC_d", [256, NCH], F32)
    K.oT_d = scratch("oT_d", [256, S], F32)
    K.ro_loc_d = [scratch("ro_loc%d_d" % i, [2048, 256], BF16) for i in range(2)]
    K.ro_all_d = [scratch("ro_all%d_d" % i, [8192, 256], BF16) for i in range(2)]
    K.mixT_d = scratch("mixT_d", [16, 128, OWN], BF16)
    K.x1_d = scratch("x1_d", [OWN, D], F32)
    K.h2T_d = scratch("h2T_d", [16, 128, OWN], BF16)
    K.actT_d = scratch("actT_d", [44, 128, OWN], BF16)
    with contextlib.ExitStack() as stack:
        K.P = Prog(nc, stack)
        phase0_adaln(K)
        phase1_kv(K)
        if K.stages >= 2:
            phase1b_rwkv_proj(K)
        if K.stages >= 3 and not getattr(K, "skip_att", False):
            phase2_own_proj(K)
            phase3_attention(K)
        if K.stages >= 4:
            phase4b_rwkv_prep(K, cts=K.cts)
            if K.stages >= 5:
                phase4c_rwkv_scan(K, heads=[h for ct in K.cts for h in (2 * ct, 2 * ct + 1)])
        if K.stages >= 6:
            phase4d_rwkv_post(K, cts=K.cts)
            phase4e_allgather(K)
        if K.stages >= 7:
            phase5a_select(K)
            phase5b_outproj(K)
            phase5c_ffn(K)
        else:
            phase_final_copy(K)
        if debug:
            P = K.P
            allc = (("dbg_mixT", K.mixT_d, [16, 128, OWN], BF16), ("dbg_x1", K.x1_d, [OWN, D], F32),
                    ("dbg_oT", K.oT_d, [256, S], F32), ("dbg_AH", K.AH_d, [256, S], BF16), ("dbg_BH", K.BH_d, [256, S], BF16),
                    ("dbg_KH", K.KH_d, [256, S], BF16), ("dbg_RH", K.RH_d, [256, S], BF16), ("dbg_PC", K.PC_d, [256, NCH], F32),
                    ("dbg_G", K.G_d, [256, S], BF16), ("dbg_BON", K.BON_d, [256, S], BF16), ("dbg_vb", K.vb_d, [256, S], BF16),
                    ("dbg_yT", K.yT_d, [1216, S], F32), ("dbg_attT", K.attT_d, [8, 128, OWN], BF16),
                                     ("dbg_qT", K.qT_d, [8, 128, OWN], BF16), ("dbg_iqT", K.iqT_d, [64, OWN, 16], BF16),
                                     ("dbg_iw", K.iw_d, [OWN, 16], F32))
            for nm, src, shp, dt in allc:
                if dbg_list is not None and nm not in dbg_list:
                    continue
                o = dbg_out(K, nm, shp, dt)
                P.dma("sp", o, src, writes=[nm])
            P.flush()
    return nc, K


def make_in_maps(inputs, cores=range(8)):
    x = np.asarray(inputs["x"], dtype=np.float32)
    c = np.asarray(inputs["c"], dtype=np.float32)
    pos = np.asarray(inputs["positions"], dtype=np.int32)
    invf_att = (np.float32(500000.0) ** (-np.arange(16, dtype=np.float32) / np.float32(16))).astype(np.float32)
    invf_idx = (np.float32(500000.0) ** (-np.arange(8, dtype=np.float32) / np.float32(8))).astype(np.float32)
    mu = np.asarray(inputs["rwkv_mu"][0], dtype=np.float32)

    vecs = [mu[0:1024], mu[1024:2048], mu[2048:3072], inputs["rwkv_w0"][0], inputs["rwkv_a0"][0], inputs["rwkv_k_k"][0],
            inputs["rwkv_k_a"][0], np.asarray(inputs["rwkv_r_k"][0]).reshape(-1), inputs["rwkv_lnx_g"][0], inputs["rwkv_lnx_b"][0]]
    w_in_full = np.asarray(inputs["w_in"][0], dtype=np.float32)
    rw_mul = np.zeros((128, 4), np.float32)
    rw_mul[:96, 0] = mu[3072:3168]
    rw_mul[:96, 1] = mu[3168:3264]
    rw_mul[:, 2] = mu[3264:3392]
    rw_mul[:, 3] = mu[3392:3520]
    maps = []
    for core in cores:
        b, j = core // 4, core % 4
        ch = slice(256 * j, 256 * j + 256)
        rwp = {"rwp%d" % i: np.ascontiguousarray(np.asarray(v, dtype=np.float32)[ch].reshape(2, 128).T) for i, v in enumerate(vecs)}
        R0 = 4176
        w_in_rw = np.ascontiguousarray(np.concatenate([w_in_full[:, R0 + 256 * j:R0 + 256 * j + 256],
                                                       w_in_full[:, R0 + 1024 + 256 * j:R0 + 1024 + 256 * j + 256],
                                                       w_in_full[:, R0 + 2048 + 256 * j:R0 + 2048 + 256 * j + 256],
                                                       w_in_full[:, R0 + 3072:R0 + 3520]], axis=1))
        tiles = own_tiles(j)
        idx = np.concatenate([np.arange(t * 128, (t + 1) * 128) for t in tiles])
        maps.append({
            "x_full": np.ascontiguousarray(x[b]),
            "x_own": np.ascontiguousarray(x[b][idx]),
            "c_arr": np.ascontiguousarray(c[b].reshape(16, 128).T),
            "pos_full": np.ascontiguousarray(pos[b].reshape(32, 128).T),
            "invf_att": np.ascontiguousarray(np.broadcast_to(invf_att, (128, 16))),
            "invf_idx": np.ascontiguousarray(np.broadcast_to(invf_idx, (128, 8))),
            "w_ada": np.ascontiguousarray(np.asarray(inputs["w_ada"][0], dtype=np.float32)[:, 3072 * j:3072 * (j + 1)]),
            "b_ada": np.ascontiguousarray(np.asarray(inputs["b_ada"][0], dtype=np.float32)[3072 * j:3072 * (j + 1)]),
            "norm1_g": np.asarray(inputs["norm1_g"][0], dtype=np.float32),
            "k_norm_g": np.asarray(inputs["k_norm_g"][0], dtype=np.float32),
            "q_norm_g": np.asarray(inputs["q_norm_g"][0], dtype=np.float32),
            "pos_own": np.ascontiguousarray(pos[b][idx].reshape(8, 128).T),
            "qpos_own": np.ascontiguousarray(idx.astype(np.float32).reshape(8, 128).T),
            "w_in": np.ascontiguousarray(w_in_full[:, 0:4176]),
            "qpos_row": idx.astype(np.float32),
            "w_out": np.asarray(inputs["w_out"][0], dtype=np.float32),
            "norm2_g": np.asarray(inputs["norm2_g"][0], dtype=np.float32),
            "w_ffn_gate": np.asarray(inputs["w_ffn_gate"][0], dtype=np.float32),
            "w_ffn_up": np.asarray(inputs["w_ffn_up"][0], dtype=np.float32),
            "w_ffn_down": np.asarray(inputs["w_ffn_down"][0], dtype=np.float32),
            "rw_w_up": np.ascontiguousarray(np.asarray(inputs["rwkv_w_up"][0], dtype=np.float32)[:, ch]),
            "rw_a_up": np.ascontiguousarray(np.asarray(inputs["rwkv_a_up"][0], dtype=np.float32)[:, ch]),
            "rw_g_up": np.ascontiguousarray(np.asarray(inputs["rwkv_g_up"][0], dtype=np.float32)[:, ch]),
            "w_in_rw": w_in_rw,
            "rw_mul": rw_mul,
            **rwp,
        })
    return maps


def kernel(**inputs):
    nc, K = build_program(debug=False)
    maps = make_in_maps(inputs)
    res = run_bass_kernel_spmd(nc, maps, core_ids=list(range(8)))
    out = np.zeros((2, S, D), dtype=np.float32)
    for core in range(8):
        b, j = core // 4, core % 4
        y = res.results[core]["y_own"]
        for i, t in enumerate(own_tiles(j)):
            out[b, t * 128:(t + 1) * 128] = y[i * 128:(i + 1) * 128]
    return out
```

```python
import contextlib
import numpy as np
import concourse.bass as bass
import concourse.mybir as mybir
from concourse.bass_utils import run_bass_kernel_spmd

F32 = mybir.dt.float32
BF16 = mybir.dt.bfloat16
I32 = mybir.dt.int32
AF = mybir.ActivationFunctionType
ALU = mybir.AluOpType
AX = mybir.AxisListType

D = 2048
S = 4096
NT = 32
OWN = 1024
ENGS = ("pe", "act", "dve", "pool", "sp")
DEBUG = {}


class _Op:
    __slots__ = ("eng", "fn", "deps", "needs_inc", "is_dma", "sem", "count", "idx", "prev_same_sem", "is_cc")

    def __init__(self, eng, fn, is_dma):
        self.eng = eng
        self.fn = fn
        self.deps = set()
        self.needs_inc = False
        self.is_dma = is_dma
        self.sem = None
        self.count = 0
        self.prev_same_sem = None
        self.is_cc = False


class Prog:
    def __init__(self, nc, stack, n_dma_sems=48):
        self.nc = nc
        self.n_dma_sems = n_dma_sems
        self.eng_sem = {e: stack.enter_context(nc.semaphore("s_" + e)) for e in ENGS}
        self.dma_sems = [stack.enter_context(nc.semaphore("d%d" % i)) for i in range(n_dma_sems)]
        self.bar_sem = stack.enter_context(nc.semaphore("bar"))
        self.cc_sem = stack.enter_context(nc.semaphore("ccs"))
        self.cc_cnt = 0
        self.cnt = {e: 0 for e in ENGS}
        self.dcnt = [0] * n_dma_sems
        self.rr = 0
        self.nbar = 0
        self._reset()

    def _reset(self):
        self.ops = []
        self.last_writer = {}
        self.readers = {}

    def _record(self, op, reads, writes):
        idx = len(self.ops)
        op.idx = idx
        deps = set()
        for k in reads:
            w = self.last_writer.get(k)
            if w is not None:
                deps.add(w)
        for k in writes:
            w = self.last_writer.get(k)
            if w is not None:
                deps.add(w)
            for r in self.readers.get(k, ()):
                deps.add(r)
        deps.discard(idx)
        op.deps = deps
        self.ops.append(op)
        for k in reads:
            self.readers.setdefault(k, []).append(idx)
        for k in writes:
            self.last_writer[k] = idx
            self.readers[k] = []
        return idx

    def op(self, eng, fn, reads=(), writes=()):
        return self._record(_Op(eng, fn, False), reads, writes)

    def dma(self, queue, out, in_, reads=(), writes=(), **kw):
        def fn(e, out=out, in_=in_, kw=kw):
            return e.dma_start(out=out, in_=in_, **kw)
        return self._record(_Op(queue, fn, True), reads, writes)

    def coll(self, fn, reads=(), writes=()):
        o = _Op("pool", fn, True)
        o.is_cc = True
        return self._record(o, reads, writes)

    def flush(self):
        nc = self.nc
        ops = self.ops
        for o in ops:
            nd = set()
            for d in o.deps:
                p = ops[d]
                if o.eng == "pe" and p.eng == "pe" and not p.is_dma and not o.is_dma:
                    continue
                nd.add(d)
                p.needs_inc = True
            o.deps = nd
        last_of = {}
        for o in ops:
            if not o.is_dma:
                last_of[o.eng] = o
        for o in last_of.values():
            o.needs_inc = True
        dlast = [None] * self.n_dma_sems
        for o in ops:
            if o.is_cc:
                self.cc_cnt += 1
                o.sem = self.cc_sem
                o.count = self.cc_cnt
            elif o.is_dma:
                s = self.rr % self.n_dma_sems
                self.rr += 1
                o.prev_same_sem = dlast[s]
                self.dcnt[s] += 16
                o.sem = self.dma_sems[s]
                o.count = self.dcnt[s]
                dlast[s] = o.idx
            elif o.needs_inc:
                self.cnt[o.eng] += 1
                o.sem = self.eng_sem[o.eng]
                o.count = self.cnt[o.eng]
        per_eng = {e: [o for o in ops if o.eng == e] for e in ENGS}
        final = [(self.dma_sems[s], self.dcnt[s]) for s in range(self.n_dma_sems) if self.dcnt[s] > 0]
        final += [(self.eng_sem[e], self.cnt[e]) for e in ENGS if self.cnt[e] > 0]
        if self.cc_cnt > 0:
            final.append((self.cc_sem, self.cc_cnt))
        self.nbar += 1
        nbar = self.nbar
        bar = self.bar_sem

        def run(e_name, eng):
            waited = {}
            for o in per_eng[e_name]:
                need = {}
                for d in o.deps:
                    p = ops[d]
                    if need.get(p.sem.num, (0, None))[0] < p.count:
                        need[p.sem.num] = (p.count, p.sem)
                if o.is_dma and o.prev_same_sem is not None:
                    p = ops[o.prev_same_sem]
                    if need.get(p.sem.num, (0, None))[0] < p.count:
                        need[p.sem.num] = (p.count, p.sem)
                for key, (c, s) in need.items():
                    if waited.get(key, 0) < c:
                        eng.wait_ge(s, c)
                        waited[key] = c
                ins = o.fn(eng)
                if o.is_cc:
                    ins.then_inc(o.sem)
                elif o.is_dma:
                    ins.then_inc(o.sem, 16)
                elif o.needs_inc:
                    ins.then_inc(o.sem, 1)
            if e_name == "sp":
                for s, c in final:
                    eng.wait_ge(s, c)
                eng.sem_inc(bar, 1)
            eng.wait_ge(bar, nbar)

        with nc.Block() as block:
            @block.tensor
            def _(e):
                run("pe", e)

            @block.scalar
            def _(e):
                run("act", e)

            @block.vector
            def _(e):
                run("dve", e)

            @block.gpsimd
            def _(e):
                run("pool", e)

            @block.sync
            def _(e):
                run("sp", e)
        self._reset()


class Ctx:
    pass


def bcast_rows(ap1d, n):
    return bass.AP(ap1d.tensor, ap1d.offset, [[0, 128], [1, n]])


def dbg_out(K, name, shape, dtype=F32):
    t = K.nc.dram_tensor(name, list(shape), dtype, kind="ExternalOutput")
    K.dbg[name] = t
    return t.ap()


def make_ident(K, P, ident):
    P.op("pool", lambda e: e.memset(ident[:], 0.0), writes=["ident"])
    P.op("pool", lambda e: e.affine_select(out=ident[:], in_=ident[:], pattern=[[-1, 128]],
                                           compare_op=ALU.not_equal, fill=1.0, base=0,
                                           channel_multiplier=1),
         reads=["ident"], writes=["ident"])


def phase0_adaln(K):
    nc, P = K.nc, K.P
    NQ = 3072
    with contextlib.ExitStack() as st:
        c_sb = st.enter_context(nc.sbuf_tensor("c_sb", [128, 16], F32))
        cact = st.enter_context(nc.sbuf_tensor("cact", [128, 16], F32))
        wst = [st.enter_context(nc.sbuf_tensor("wst%d" % i, [128, 16, 512], F32)) for i in range(2)]
        modrow = st.enter_context(nc.sbuf_tensor("modrow", [1, NQ], F32))
        brow = st.enter_context(nc.sbuf_tensor("brow", [1, NQ], F32))
        ps = [st.enter_context(nc.psum_tensor("ps0_%d" % i, [1, 512], F32)) for i in range(2)]
        P.dma("sp", c_sb[:], K.c_arr, writes=["c_sb"])
        P.dma("sp", brow[:], K.b_ada.rearrange("(o n) -> o n", o=1), writes=["brow"])
        P.op("act", lambda e: e.activation(out=cact[:], in_=c_sb[:], func=AF.Silu),
             reads=["c_sb"], writes=["cact"])
        wv = K.w_ada.rearrange("(k p) n -> p k n", p=128)
        for nt in range(NQ // 512):
            sl = nt % 2
            for hh in range(2):
                P.dma("sp", wst[sl][:, hh * 8:(hh + 1) * 8, :],
                      wv[:, hh * 8:(hh + 1) * 8, nt * 512:(nt + 1) * 512],
                      writes=[("wst", sl, hh)])
            for k in range(16):
                P.op("pe", lambda e, k=k, sl=sl: e.matmul(ps[sl][:, :], lhsT=cact[:, k:k + 1],
                                                         rhs=wst[sl][:, k, :], start=(k == 0), stop=(k == 15)),
                     reads=["cact", ("wst", sl, k // 8)], writes=[("ps0", sl)])
            P.op("dve", lambda e, nt=nt, sl=sl: e.tensor_tensor(
                out=modrow[0:1, nt * 512:(nt + 1) * 512], in0=ps[sl][:, :],
                in1=brow[0:1, nt * 512:(nt + 1) * 512], op=ALU.add),
                reads=[("ps0", sl), "brow"], writes=[("modrow", nt)])
        P.dma("sp", K.modq_d, modrow[:],
              reads=[("modrow", nt) for nt in range(NQ // 512)], writes=["modq_d"])
        P.flush()
    P.coll(lambda e: e.collective_compute("AllGather", ALU.bypass, replica_groups=[[0, 1, 2, 3], [4, 5, 6, 7]],
                                          ins=[K.modq_d.opt()], outs=[K.mod4_d.opt()]), reads=["modq_d"], writes=["mod4"])
    P.flush()


def load_mod_rows(K, P, tile, which, gain_ap=None, key=None):
    src = K.mod_d[which * D:(which + 1) * D]
    P.dma("sp", tile[:], bcast_rows(src, D), writes=[key])


def bc(ap, shape):
    return ap.to_broadcast(list(shape))


def load_weight_bf16(K, P, st_tiles, dst, c_dst, src2d, ncols, tag, defer=None):
    wv = src2d.rearrange("(k p) n -> p k n", p=128)
    nk = wv.shape[1]
    engs = ["pool", "dve", "act"]
    for c0 in range(0, ncols, 512):
        n = min(512, ncols - c0)
        for k0 in range(0, nk, 4):
            kn = min(4, nk - k0)
            if defer is not None:
                defer.append(lambda c0=c0, n=n, k0=k0, kn=kn: _load_piece(K, P, st_tiles, dst, c_dst, wv, tag, engs, c0, n, k0, kn))
                continue
            _load_piece(K, P, st_tiles, dst, c_dst, wv, tag, engs, c0, n, k0, kn)
    return [(tag, c0, k0) for c0 in range(0, ncols, 512) for k0 in range(0, nk, 4)]


def _load_piece(K, P, st_tiles, dst, c_dst, wv, tag, engs, c0, n, k0, kn):
    if True:
        if True:
            i = K.wcnt
            K.wcnt += 1
            sl = i % len(st_tiles)
            stg = st_tiles[sl]
            P.dma("sp", stg[:, 0:kn, 0:n], wv[:, k0:k0 + kn, c0:c0 + n], writes=[("wstg", sl)])
            eng = engs[i % 3]
            o = dst[:, k0:k0 + kn, c_dst + c0:c_dst + c0 + n]
            if eng == "act":
                P.op("act", lambda e, o=o, stg=stg, kn=kn, n=n: e.copy(out=o, in_=stg[:, 0:kn, 0:n]),
                     reads=[("wstg", sl)], writes=[(tag, c0, k0)])
            else:
                P.op(eng, lambda e, o=o, stg=stg, kn=kn, n=n: e.tensor_copy(out=o, in_=stg[:, 0:kn, 0:n]),
                     reads=[("wstg", sl)], writes=[(tag, c0, k0)])


def rope_tables(K, P, st, pos_arr, ntile, invf_att, invf_idx, tag):
    nc = K.nc
    posi = st.enter_context(nc.sbuf_tensor(tag + "posi", [128, ntile], I32))
    posf = st.enter_context(nc.sbuf_tensor(tag + "posf", [128, ntile], F32))
    iva = st.enter_context(nc.sbuf_tensor(tag + "iva", [128, 16], F32))
    ivi = st.enter_context(nc.sbuf_tensor(tag + "ivi", [128, 8], F32))
    P.dma("sp", posi[:], pos_arr, writes=[tag + "posi"])
    P.dma("sp", iva[:], invf_att, writes=[tag + "iva"])
    P.dma("sp", ivi[:], invf_idx, writes=[tag + "ivi"])
    P.op("dve", lambda e: e.tensor_copy(out=posf[:], in_=posi[:]), reads=[tag + "posi"], writes=[tag + "posf"])
    out = {}
    for nm, iv, h in (("a", iva, 16), ("i", ivi, 8)):
        u = st.enter_context(nc.sbuf_tensor(tag + "u" + nm, [128, ntile, h], F32))
        ui = st.enter_context(nc.sbuf_tensor(tag + "ui" + nm, [128, ntile, h], I32))
        uf = st.enter_context(nc.sbuf_tensor(tag + "uf" + nm, [128, ntile, h], F32))
        for fn, off in (("sin", 0.0), ("cos", 0.25)):
            tb = st.enter_context(nc.sbuf_tensor(tag + fn + nm, [128, ntile, h], F32))
            kk = tag + fn + nm
            P.op("dve", lambda e, u=u, iv=iv, h=h: e.tensor_tensor(
                out=u[:], in0=bc(posf[:].unsqueeze(2), [128, ntile, h]),
                in1=bc(iv[:].unsqueeze(1), [128, ntile, h]), op=ALU.mult),
                reads=[tag + "posf", tag + "iv" + nm], writes=[tag + "U" + nm])
            P.op("dve", lambda e, u=u, off=off: e.tensor_scalar(
                out=u[:], in0=u[:], scalar1=float(1.0 / (2 * np.pi)), scalar2=off, op0=ALU.mult, op1=ALU.add),
                reads=[tag + "U" + nm], writes=[tag + "U" + nm])
            P.op("dve", lambda e, u=u, ui=ui: e.tensor_copy(out=ui[:], in_=u[:]), reads=[tag + "U" + nm], writes=[tag + "UI" + nm])
            P.op("dve", lambda e, uf=uf, ui=ui: e.tensor_copy(out=uf[:], in_=ui[:]), reads=[tag + "UI" + nm], writes=[tag + "UF" + nm])
            P.op("dve", lambda e, u=u, uf=uf: e.tensor_tensor(out=u[:], in0=u[:], in1=uf[:], op=ALU.subtract),
                 reads=[tag + "U" + nm, tag + "UF" + nm], writes=[tag + "U" + nm])
            P.op("dve", lambda e, u=u: e.tensor_scalar(out=u[:], in0=u[:], scalar1=-0.5, scalar2=0.5,
                                                        op0=ALU.max, op1=ALU.min),
                 reads=[tag + "U" + nm], writes=[tag + "U" + nm])
            P.op("act", lambda e, u=u, tb=tb: e.activation(out=tb[:], in_=u[:], func=AF.Sin,
                                                            scale=float(2 * np.pi)),
                 reads=[tag + "U" + nm], writes=[kk])
            out[fn + nm] = (tb, kk)
    return out


def apply_rope(P, eng, x4, cos, sin, t, half, tmp, rk, wk, sfx=""):
    ctb, ck = cos
    stb, sk = sin
    H = x4.shape[1]
    x1 = x4[:, :, 0:half]
    x2 = x4[:, :, half:2 * half]
    cb = bc(ctb[:, t, :].unsqueeze(1), [128, H, half])
    sb = bc(stb[:, t, :].unsqueeze(1), [128, H, half])
    a, b2, c, d = tmp
    P.op(eng, lambda e: e.tensor_tensor(out=a[:, 0:H, 0:half], in0=x1, in1=cb, op=ALU.mult), reads=rk + [ck], writes=["rtmpA" + sfx])
    P.op(eng, lambda e: e.tensor_tensor(out=b2[:, 0:H, 0:half], in0=x2, in1=sb, op=ALU.mult), reads=rk + [sk], writes=["rtmpB" + sfx])
    P.op(eng, lambda e: e.tensor_tensor(out=c[:, 0:H, 0:half], in0=x2, in1=cb, op=ALU.mult), reads=rk + [ck], writes=["rtmpC" + sfx])
    P.op(eng, lambda e: e.tensor_tensor(out=d[:, 0:H, 0:half], in0=x1, in1=sb, op=ALU.mult), reads=rk + [sk], writes=["rtmpD" + sfx])
    P.op(eng, lambda e: e.tensor_tensor(out=x1, in0=a[:, 0:H, 0:half], in1=b2[:, 0:H, 0:half], op=ALU.subtract),
         reads=["rtmpA" + sfx, "rtmpB" + sfx, "rtmpC" + sfx, "rtmpD" + sfx] + rk, writes=rk)
    P.op(eng, lambda e: e.tensor_tensor(out=x2, in0=c[:, 0:H, 0:half], in1=d[:, 0:H, 0:half], op=ALU.add),
         reads=["rtmpC" + sfx, "rtmpD" + sfx] + rk, writes=rk)


def head_rmsnorm(P, x3, gain, sq, ssum, rk, wk, gk=None, sqk=None):
    P.op("pool", lambda e: e.tensor_tensor(out=sq[:], in0=x3, in1=x3, op=ALU.mult), reads=rk, writes=[sqk or (wk + "sq")])
    P.op("dve", lambda e: e.tensor_reduce(out=ssum[:, 0:8], in_=sq[:], axis=AX.X, op=ALU.add),
         reads=[sqk or (wk + "sq")], writes=[wk + "s0"])
    P.op("dve", lambda e: e.tensor_scalar(out=ssum[:, 8:16], in0=ssum[:, 0:8], scalar1=1.0 / 128, scalar2=1e-6,
                                           op0=ALU.mult, op1=ALU.add), reads=[wk + "s0"], writes=[wk + "s1"])
    P.op("act", lambda e: e.activation(out=ssum[:, 16:24], in_=ssum[:, 8:16], func=AF.Sqrt),
         reads=[wk + "s1"], writes=[wk + "s2"])
    P.op("dve", lambda e: e.reciprocal(out=ssum[:, 24:32], in_=ssum[:, 16:24]), reads=[wk + "s2"], writes=[wk + "s3"])
    P.op("dve", lambda e: e.tensor_tensor(out=x3, in0=x3, in1=bc(ssum[:, 24:32].unsqueeze(2), [128, 8, 128]),
                                           op=ALU.mult), reads=rk + [wk + "s3"], writes=rk)
    P.op("pool", lambda e: e.tensor_tensor(out=x3, in0=x3, in1=bc(gain[:].unsqueeze(1), [128, 8, 128]),
                                            op=ALU.mult), reads=rk + [gk or ("gain" + wk)], writes=rk)


def norm_load(K, P, T, x_src, t):
    xs = t % 2
    P.dma("sp", T["xt"][xs][:], x_src[t * 128:(t + 1) * 128, :], writes=[("xt", xs)])


def norm_chain(K, P, T, x_src, t, G1, SH1, load=True, xg=None):
    xs = t % 2
    xt, hb, ss, junk = T["xt"], T["hb"], T["ss"], T["junk"]
    if load:
        norm_load(K, P, T, x_src, t)
    if xg is not None:
        xg_ap, xg_k = xg[xs]
        P.op("pool", lambda e: e.tensor_tensor(out=xg_ap, in0=xt[xs][:], in1=G1[:], op=ALU.mult),
             reads=[("xt", xs), "G"], writes=[xg_k])
    P.op("act", lambda e: e.activation(out=junk[:], in_=xt[xs][:], func=AF.Square, accum_out=ss[:, 0:1]),
         reads=[("xt", xs)], writes=["junk", "ss0"])
    P.op("dve", lambda e: e.tensor_scalar(out=ss[:, 1:2], in0=ss[:, 0:1], scalar1=1.0 / D, scalar2=1e-6,
                                           op0=ALU.mult, op1=ALU.add), reads=["ss0"], writes=["ss1"])
    P.op("act", lambda e: e.activation(out=ss[:, 2:3], in_=ss[:, 1:2], func=AF.Sqrt), reads=["ss1"], writes=["ss2"])
    P.op("dve", lambda e: e.reciprocal(out=ss[:, 3:4], in_=ss[:, 2:3]), reads=["ss2"], writes=["ss3"])
    if xg is not None:
        P.op("dve", lambda e: e.scalar_tensor_tensor(out=hb[xs][:], in0=xg_ap, scalar=ss[:, 3:4], in1=SH1[:],
                                                      op0=ALU.mult, op1=ALU.add),
             reads=[xg_k, "ss3", "SH"], writes=[("hb", xs)])
    else:
        P.op("dve", lambda e: e.scalar_tensor_tensor(out=xt[xs][:], in0=xt[xs][:], scalar=ss[:, 3:4], in1=G1[:],
                                                      op0=ALU.mult, op1=ALU.mult),
             reads=[("xt", xs), "ss3", "G"], writes=[("xt", xs)])
        P.op("pool", lambda e: e.tensor_tensor(out=hb[xs][:], in0=xt[xs][:], in1=SH1[:], op=ALU.add),
             reads=[("xt", xs), "SH"], writes=[("hb", xs)])


def norm_pe(K, P, T, t, ident, blk_hT, ti, hname="hT"):
    xs = t % 2
    hb, pT = T["hb"], T["pT"]
    for half in range(2):
        for kk in range(8):
            k = half * 8 + kk
            P.op("pe", lambda e, k=k, kk=kk, half=half: e.transpose(
                out=pT[half][:, kk, :], in_=hb[xs][:, k * 128:(k + 1) * 128], identity=ident[:]),
                reads=[("hb", xs), "ident"], writes=[("pT", half)])
        o = blk_hT[:, half * 8:(half + 1) * 8, ti * 128:(ti + 1) * 128]
        if half == 0:
            P.op("act", lambda e, o=o, half=half: e.copy(out=o, in_=pT[half][:]),
                 reads=[("pT", half)], writes=[(hname, ti, half)])
        else:
            P.op("dve", lambda e, o=o, half=half: e.tensor_copy(out=o, in_=pT[half][:]),
                 reads=[("pT", half)], writes=[(hname, ti, half)])


def norm_block(K, P, T, x_src, t, G1, SH1, ident, blk_hT, ti, load=True, hname="hT"):
    norm_chain(K, P, T, x_src, t, G1, SH1, load=load)
    norm_pe(K, P, T, t, ident, blk_hT, ti, hname=hname)


def norm_tiles_alloc(K, st, tag):
    nc = K.nc
    T = {}
    T["xt"] = [st.enter_context(nc.sbuf_tensor(tag + "xt%d" % i, [128, D], F32)) for i in range(2)]
    T["hb"] = [st.enter_context(nc.sbuf_tensor(tag + "hb%d" % i, [128, D], BF16)) for i in range(2)]
    T["ss"] = st.enter_context(nc.sbuf_tensor(tag + "ss", [128, 4], F32))
    T["junk"] = st.enter_context(nc.sbuf_tensor(tag + "junk", [128, D], BF16))
    T["pT"] = [st.enter_context(nc.psum_tensor(tag + "pT%d" % i, [128, 8, 128], BF16)) for i in range(2)]
    return T


def load_G_SH(K, P, st, which_sh, which_sc, gain_vec, tag):
    nc = K.nc
    G = st.enter_context(nc.sbuf_tensor(tag + "G", [128, D], F32))
    SH = st.enter_context(nc.sbuf_tensor(tag + "SH", [128, D], F32))
    gtmp = st.enter_context(nc.sbuf_tensor(tag + "gtmp", [128, D], F32))
    P.dma("sp", SH[:], bcast_rows(K.mod_d[which_sh * D:(which_sh + 1) * D], D), writes=["SH"])
    P.dma("sp", G[:], bcast_rows(K.mod_d[which_sc * D:(which_sc + 1) * D], D), writes=["G"])
    P.dma("sp", gtmp[:], bcast_rows(gain_vec, D), writes=["gtmp"])
    P.op("dve", lambda e: e.scalar_tensor_tensor(out=G[:], in0=G[:], scalar=1.0, in1=gtmp[:],
                                                  op0=ALU.add, op1=ALU.mult), reads=["G", "gtmp"], writes=["G"])
    K.last_gtmp = gtmp
    return G, SH


def phase1_kv(K):
    nc, P = K.nc, K.P
    with contextlib.ExitStack() as st:
        ident = st.enter_context(nc.sbuf_tensor("ident", [128, 128], BF16))
        make_ident(K, P, ident)
        G1, SH1 = load_G_SH(K, P, st, 0, 1, K.norm1_g, "p1")
        T = norm_tiles_alloc(K, st, "p1")
        hT = [st.enter_context(nc.sbuf_tensor("hT%d" % i, [128, 16, 512], BF16)) for i in range(2)]
        W = st.enter_context(nc.sbuf_tensor("Wkv", [128, 16, 2112], BF16))
        stg = [st.enter_context(nc.sbuf_tensor("wstg%d" % i, [128, 4, 512], F32)) for i in range(2)]
        wk_k = load_weight_bf16(K, P, stg, W, 0, K.w_in[:, 1024:2048], 1024, "Wk")
        wk_v = load_weight_bf16(K, P, stg, W, 1024, K.w_in[:, 2048:3072], 1024, "Wv")
        wk_i = load_weight_bf16(K, P, stg, W, 2048, K.w_in[:, 4096:4160], 64, "Wi")
        rt = rope_tables(K, P, st, K.pos_full, 32, K.invf_att, K.invf_idx, "rf")
        gain = st.enter_context(nc.sbuf_tensor("kgain", [128, 128], F32))
        P.dma("sp", gain[:], bcast_rows(K.k_norm_g, 128), writes=["gainK"])
        def two(name, shape, dt):
            return [st.enter_context(nc.sbuf_tensor(name + str(i), shape, dt)) for i in range(2)]
        ksb2 = two("ksb", [128, 8, 128], F32)
        kbf2 = two("kbf", [128, 8, 128], BF16)
        sq2 = [st.enter_context(nc.sbuf_tensor("sq", [128, 8, 128], F32))] * 2
        ssum2 = two("ssum", [128, 32], F32)
        rtmp2 = [[st.enter_context(nc.sbuf_tensor("rtmp%d" % i, [128, 8, 16], F32)) for i in range(4)]] * 2
        vsb2 = two("vsb", [128, 8, 129], BF16)
        iksb2 = two("iksb", [128, 1, 64], F32)
        ikbf2 = two("ikbf", [128, 64], BF16)
        kTs2 = [st.enter_context(nc.sbuf_tensor("kTs", [128, 8, 128], BF16))] * 2
        ikTs2 = two("ikTs", [64, 128], BF16)
        pm = [st.enter_context(nc.psum_tensor("pm%d" % i, [128, 512], F32)) for i in range(3)]
        pk = st.enter_context(nc.psum_tensor("pk", [128, 8, 128], BF16))
        pk2 = st.enter_context(nc.psum_tensor("pk2", [64, 128], BF16))
        for s_ in range(2):
            P.op("pool", lambda e, s_=s_: e.memset(vsb2[s_][:], 1.0), writes=["vsb%d" % s_])
        norm_load(K, P, T, K.x_full, 0)

        xg = [(K.last_gtmp[:], "gtmp"), (stg[0][:].rearrange("p a b -> p (a b)"), ("wstg", 0))]

        def norm_tile_chain(blk, ti):
            tt_ = blk * 4 + ti
            if tt_ + 1 < 32:
                norm_load(K, P, T, K.x_full, tt_ + 1)
            norm_chain(K, P, T, K.x_full, tt_, G1, SH1, load=False, xg=xg)

        def norm_tile_pe(blk, ti):
            norm_pe(K, P, T, blk * 4 + ti, ident, hT[blk % 2], ti, hname=("hT", blk % 2))

        def norm_tile(blk, ti):
            norm_tile_chain(blk, ti)
            norm_tile_pe(blk, ti)

        def store_hT(blk):
            hs = blk % 2
            hkeys = [(("hT", hs), ti, half) for ti in range(4) for half in range(2)]
            P.dma("sp", K.hT_d.rearrange("k p t -> p k t")[:, :, blk * 512:(blk + 1) * 512], hT[hs][:],
                  reads=hkeys, writes=[("hT_d", blk)])

        def bufs(t):
            u = t % 2
            return (str(u), ksb2[u], kbf2[u], sq2[u], ssum2[u], rtmp2[u], vsb2[u], iksb2[u], ikbf2[u], kTs2[u], ikTs2[u])

        def mm_tile(blk, ti):
            t = blk * 4 + ti
            hs = blk % 2
            hk = [(("hT", hs), ti, 0), (("hT", hs), ti, 1)]
            us, ksb, kbf, sq, ssum, rtmp, vsb, iksb, ikbf, kTs, ikTs = bufs(t)
            for gi, (c0, n, wkeys) in enumerate([(0, 512, wk_k), (512, 512, wk_k), (1024, 512, wk_v),
                                                 (1536, 512, wk_v), (2048, 64, wk_i)]):
                pb = pm[gi % 3]
                wtag, wc0 = [("Wk", 0), ("Wk", 512), ("Wv", 0), ("Wv", 512), ("Wi", 0)][gi]
                for k in range(16):
                    wkeys = [(wtag, wc0, 4 * (k // 4))]
                    P.op("pe", lambda e, pb=pb, k=k, c0=c0, n=n, ti=ti, hs=hs: e.matmul(
                        pb[:, 0:n], lhsT=hT[hs][:, k, ti * 128:(ti + 1) * 128], rhs=W[:, k, c0:c0 + n],
                        start=(k == 0), stop=(k == 15)), reads=hk + wkeys, writes=[("pm", gi % 3)])
                if gi < 2:
                    P.op("act", lambda e, pb=pb, gi=gi, ksb=ksb: e.copy(out=ksb[:, gi * 4:(gi + 1) * 4, :], in_=pb[:, 0:512]),
                         reads=[("pm", gi % 3)], writes=["ksb" + us])
                elif gi < 4:
                    g2 = gi - 2
                    P.op("act", lambda e, pb=pb, g2=g2, vsb=vsb: e.copy(out=vsb[:, g2 * 4:(g2 + 1) * 4, 0:128], in_=pb[:, 0:512]),
                         reads=[("pm", gi % 3)], writes=["vsb" + us])
                else:
                    P.op("act", lambda e, pb=pb, iksb=iksb: e.copy(out=iksb[:, 0, :], in_=pb[:, 0:64]),
                         reads=[("pm", gi % 3)], writes=["iksb" + us])
            P.dma("sp", K.v_d[t * 128:(t + 1) * 128, :], vsb[:].rearrange("p h d -> p (h d)"),
                  reads=["vsb" + us], writes=[("v_d", t)])

        def post1(blk, ti):
            t = blk * 4 + ti
            us, ksb, kbf, sq, ssum, rtmp, vsb, iksb, ikbf, kTs, ikTs = bufs(t)
            head_rmsnorm(P, ksb[:], gain, sq, ssum, ["ksb" + us], "K" + us, gk="gainK", sqk="Ksq")
            apply_rope(P, "dve", ksb[:], rt["cosa"], rt["sina"], t, 16, rtmp, ["ksb" + us], "rK")
            P.op("act", lambda e, kbf=kbf, ksb=ksb: e.copy(out=kbf[:], in_=ksb[:]), reads=["ksb" + us], writes=["kbf" + us])
            apply_rope(P, "pool", iksb[:], rt["cosi"], rt["sini"], t, 8, rtmp, ["iksb" + us], "rI")
            P.op("act", lambda e, ikbf=ikbf, iksb=iksb: e.copy(out=ikbf[:], in_=iksb[:, 0, :]), reads=["iksb" + us], writes=["ikbf" + us])

        def post2(blk, ti):
            t = blk * 4 + ti
            us, ksb, kbf, sq, ssum, rtmp, vsb, iksb, ikbf, kTs, ikTs = bufs(t)
            for h in range(8):
                P.op("pe", lambda e, h=h, kbf=kbf: e.transpose(out=pk[:, h, :], in_=kbf[:, h, :], identity=ident[:]),
                     reads=["kbf" + us, "ident"], writes=["pk"])
            P.op("dve", lambda e, kTs=kTs: e.tensor_copy(out=kTs[:], in_=pk[:]), reads=["pk"], writes=["kTs"])
            P.dma("sp", K.kT_d.rearrange("h p t -> p h t")[:, :, t * 128:(t + 1) * 128], kTs[:],
                  reads=["kTs"], writes=[("kT_d", t)])
            P.op("pe", lambda e, ikbf=ikbf: e.transpose(out=pk2[:, :], in_=ikbf[:], identity=ident[:]),
                 reads=["ikbf" + us, "ident"], writes=["pk2"])
            P.op("dve", lambda e, ikTs=ikTs: e.tensor_copy(out=ikTs[:], in_=pk2[:, :]), reads=["pk2"], writes=["ikTs" + us])
            P.dma("sp", K.ikT_d[:, t * 128:(t + 1) * 128], ikTs[:], reads=["ikTs" + us], writes=[("ikT_d", t)])

        for ti in range(4):
            norm_tile(0, ti)
        store_hT(0)
        prev = None
        for blk in range(8):
            for ti in range(4):
                if blk + 1 < 8:
                    norm_tile_chain(blk + 1, ti)
                mm_tile(blk, ti)
                if prev is not None:
                    post2(*prev)
                if blk + 1 < 8:
                    norm_tile_pe(blk + 1, ti)
                post1(blk, ti)
                prev = (blk, ti)
            if blk + 1 < 8:
                store_hT(blk + 1)
        post2(*prev)
        P.flush()

RW0 = 4176
NRW = 1216
RW_GROUPS = [(i * 128, 128) for i in range(6)] + [(768, 96), (864, 96), (960, 128), (1088, 128)]


def phase1b_rwkv_proj(K):
    nc, P = K.nc, K.P
    with contextlib.ExitStack() as st:
        W = st.enter_context(nc.sbuf_tensor("Wr", [128, 16, NRW], BF16))
        stg = [st.enter_context(nc.sbuf_tensor("wstgb%d" % i, [128, 4, 512], F32)) for i in range(2)]
        hT = [st.enter_context(nc.sbuf_tensor("hTb%d" % i, [128, 16, 512], BF16)) for i in range(2)]
        ost = [st.enter_context(nc.sbuf_tensor("ost%d" % i, [128, 512], F32)) for i in range(4)]
        pm = [st.enter_context(nc.psum_tensor("pmb%d" % i, [128, 512], F32)) for i in range(4)]
        wkeys = load_weight_bf16(K, P, stg, W, 0, K.w_in_rw, NRW, "Wr")
        cnt = 0
        for blk in range(8):
            hs = blk % 2
            P.dma("sp", hT[hs][:], K.hT_d.rearrange("k p t -> p k t")[:, :, blk * 512:(blk + 1) * 512],
                  writes=[("hTb", hs)])
            for (r0, m) in RW_GROUPS:
                s4 = cnt % 4
                cnt += 1
                for k in range(16):
                    wkeys = [("Wr", c0_, 4 * (k // 4)) for c0_ in sorted({512 * (r0 // 512), 512 * ((r0 + m - 1) // 512)})]
                    P.op("pe", lambda e, k=k, r0=r0, m=m, hs=hs, s4=s4: e.matmul(
                        pm[s4][0:m, :], lhsT=W[:, k, r0:r0 + m], rhs=hT[hs][:, k, :],
                        start=(k == 0), stop=(k == 15)), reads=[("hTb", hs)] + wkeys, writes=[("pmb", s4)])
                if cnt % 2 == 0:
                    P.op("act", lambda e, m=m, s4=s4: e.copy(out=ost[s4][0:m, :], in_=pm[s4][0:m, :]),
                         reads=[("pmb", s4)], writes=[("ost", s4)])
                else:
                    P.op("dve", lambda e, m=m, s4=s4: e.tensor_copy(out=ost[s4][0:m, :], in_=pm[s4][0:m, :]),
                         reads=[("pmb", s4)], writes=[("ost", s4)])
                P.dma("sp", K.yT_d[r0:r0 + m, blk * 512:(blk + 1) * 512], ost[s4][0:m, :],
                      reads=[("ost", s4)], writes=[("yT_d", r0, blk)])
        P.flush()


def phase2_own_proj(K):
    nc, P = K.nc, K.P
    with contextlib.ExitStack() as st:
        ident = st.enter_context(nc.sbuf_tensor("ident2", [128, 128], BF16))
        make_ident(K, P, ident)
        G1, SH1 = load_G_SH(K, P, st, 0, 1, K.norm1_g, "p2")
        T = norm_tiles_alloc(K, st, "p2")
        hT = [st.enter_context(nc.sbuf_tensor("hTo%d" % i, [128, 16, 512], BF16)) for i in range(2)]
        W = st.enter_context(nc.sbuf_tensor("Wq", [128, 16, 2064], BF16))
        stg = [st.enter_context(nc.sbuf_tensor("wstgq%d" % i, [128, 4, 512], F32)) for i in range(2)]
        wk_q = load_weight_bf16(K, P, stg, W, 0, K.w_in[:, 0:1024], 1024, "Wq")
        wk_iq = load_weight_bf16(K, P, stg, W, 1024, K.w_in[:, 3072:4096], 1024, "Wiq")
        wk_iw = load_weight_bf16(K, P, stg, W, 2048, K.w_in[:, 4160:4176], 16, "Wiw")
        rt = rope_tables(K, P, st, K.pos_own, 8, K.invf_att, K.invf_idx, "ro")
        gain = st.enter_context(nc.sbuf_tensor("qgain", [128, 128], F32))
        P.dma("sp", gain[:], bcast_rows(K.q_norm_g, 128), writes=["gainQ"])
        qsb = st.enter_context(nc.sbuf_tensor("qsb", [128, 8, 128], F32))
        qbf = st.enter_context(nc.sbuf_tensor("qbf", [128, 8, 128], BF16))
        sq = st.enter_context(nc.sbuf_tensor("sq2", [128, 8, 128], F32))
        ssum = st.enter_context(nc.sbuf_tensor("ssum2", [128, 32], F32))
        rtmp = [st.enter_context(nc.sbuf_tensor("rtmpq%d" % i, [128, 16, 16], F32)) for i in range(4)]
        iqsb = st.enter_context(nc.sbuf_tensor("iqsb", [128, 16, 64], F32))
        iqbf = st.enter_context(nc.sbuf_tensor("iqbf", [128, 16, 64], BF16))
        iwsb = st.enter_context(nc.sbuf_tensor("iwsb", [128, 16], F32))
        qTs = st.enter_context(nc.sbuf_tensor("qTs", [128, 8, 128], BF16))
        iqTs = st.enter_context(nc.sbuf_tensor("iqTs", [64, 128, 16], BF16))
        pm = [st.enter_context(nc.psum_tensor("pmq%d" % i, [128, 512], F32)) for i in range(3)]
        pk = st.enter_context(nc.psum_tensor("pkq", [128, 8, 128], BF16))
        xg = [(K.last_gtmp[:], "gtmp"), (stg[0][:].rearrange("p a b -> p (a b)"), ("wstg", 0))]

        def nchain(blk, ti):
            tt_ = blk * 4 + ti
            if tt_ + 1 < 8:
                norm_load(K, P, T, K.x_own, tt_ + 1)
            norm_chain(K, P, T, K.x_own, tt_, G1, SH1, load=False, xg=xg)

        def npe(blk, ti):
            norm_pe(K, P, T, blk * 4 + ti, ident, hT[blk % 2], ti, hname=("hT", blk % 2))
        norm_load(K, P, T, K.x_own, 0)
        for ti in range(4):
            nchain(0, ti)
            npe(0, ti)
        for blk in range(2):
            hs = blk % 2
            for ti in range(4):
                t = blk * 4 + ti
                hk = [(("hT", hs), ti, 0), (("hT", hs), ti, 1)]
                if blk + 1 < 2:
                    nchain(blk + 1, ti)
                for gi, (c0, n, wkeys) in enumerate([(0, 512, wk_q), (512, 512, wk_q), (1024, 512, wk_iq),
                                                     (1536, 512, wk_iq), (2048, 16, wk_iw)]):
                    pb = pm[gi % 3]
                    wtag, wc0 = [("Wq", 0), ("Wq", 512), ("Wiq", 0), ("Wiq", 512), ("Wiw", 0)][gi]
                    for k in range(16):
                        wkeys = [(wtag, wc0, 4 * (k // 4))]
                        P.op("pe", lambda e, pb=pb, k=k, c0=c0, n=n, ti=ti, hs=hs: e.matmul(
                            pb[:, 0:n], lhsT=hT[hs][:, k, ti * 128:(ti + 1) * 128], rhs=W[:, k, c0:c0 + n],
                            start=(k == 0), stop=(k == 15)), reads=hk + wkeys, writes=[("pmq", gi % 3)])
                    if gi < 2:
                        P.op("act", lambda e, pb=pb, gi=gi: e.copy(out=qsb[:, gi * 4:(gi + 1) * 4, :], in_=pb[:, 0:512]),
                             reads=[("pmq", gi % 3)], writes=["qsb"])
                    elif gi < 4:
                        g2 = gi - 2
                        P.op("act", lambda e, pb=pb, g2=g2: e.copy(out=iqsb[:, g2 * 8:(g2 + 1) * 8, :], in_=pb[:, 0:512]),
                             reads=[("pmq", gi % 3)], writes=["iqsb"])
                    else:
                        P.op("act", lambda e, pb=pb: e.activation(out=iwsb[:], in_=pb[:, 0:16], func=AF.Copy, scale=0.25),
                             reads=[("pmq", gi % 3)], writes=["iwsb"])
                P.dma("sp", K.iw_d[t * 128:(t + 1) * 128, :], iwsb[:], reads=["iwsb"], writes=[("iw_d", t)])
                head_rmsnorm(P, qsb[:], gain, sq, ssum, ["qsb"], "Q")
                apply_rope(P, "dve", qsb[:], rt["cosa"], rt["sina"], t, 16, rtmp, ["qsb"], "rQ")
                P.op("act", lambda e: e.copy(out=qbf[:], in_=qsb[:]), reads=["qsb"], writes=["qbf"])
                for h in range(8):
                    P.op("pe", lambda e, h=h: e.transpose(out=pk[:, h, :], in_=qbf[:, h, :], identity=ident[:]),
                         reads=["qbf", "ident"], writes=["pkq"])
                P.op("dve", lambda e: e.tensor_copy(out=qTs[:], in_=pk[:]), reads=["pkq"], writes=["qTs"])
                P.dma("sp", K.qT_d.rearrange("h p t -> p h t")[:, :, t * 128:(t + 1) * 128], qTs[:],
                      reads=["qTs"], writes=[("qT_d", t)])
                apply_rope(P, "pool", iqsb[:], rt["cosi"], rt["sini"], t, 8, rtmp, ["iqsb"], "rIQ")
                P.op("act", lambda e: e.activation(out=iqbf[:], in_=iqsb[:], func=AF.Copy, scale=0.125),
                     reads=["iqsb"], writes=["iqbf"])
                for half in range(2):
                    for hh in range(8):
                        h = half * 8 + hh
                        P.op("pe", lambda e, h=h, hh=hh: e.transpose(out=pk[0:64, hh, :], in_=iqbf[:, h, :],
                                                                      identity=ident[:]),
                             reads=["iqbf", "ident"], writes=["pkq"])
                    P.op("dve", lambda e, half=half: e.tensor_copy(
                        out=iqTs[:, :, half * 8:(half + 1) * 8].rearrange("p t h -> p h t"), in_=pk[0:64, :, :]),
                         reads=["pkq"], writes=["iqTs"])
                P.dma("sp", K.iqT_d[:, t * 128:(t + 1) * 128, :], iqTs[:], reads=["iqTs"], writes=[("iqT_d", t)])
                if blk + 1 < 2:
                    npe(blk + 1, ti)
        P.flush()


NIT = 22
SLOT_NK = [4, 8, 12, 16, 20, 24, 28, 32]


def phase3_attention(K):
    nc, P = K.nc, K.P
    with contextlib.ExitStack() as st:
        def sb(name, shape, dt):
            return st.enter_context(nc.sbuf_tensor(name, shape, dt))
        ident = sb("ident3", [128, 128], BF16)
        kposf = sb("kposf", [128, 512], F32)
        bias = sb("cbias", [128, 512], F32)
        identf = kposf[:, 0:128]
        make_ident(K, P, ident)
        P.op("dve", lambda e: e.tensor_copy(out=identf, in_=ident[:]), reads=["ident"], writes=["kposf"])
        kT = sb("kTall", [128, 8, S], BF16)
        V = sb("Vall", [128, 32, 1032], BF16)
        ikT = sb("ikTall", [64, S], BF16)
        for h in range(8):
            P.dma("sp", kT[:, h, :], K.kT_d[h], writes=[("kT", h)])
        for q4 in range(4):
            P.dma("sp", V[:, q4 * 8:(q4 + 1) * 8, :],
                  K.v_d.rearrange("(t p) c -> p t c", p=128)[:, q4 * 8:(q4 + 1) * 8, :], writes=[("V", q4)])
        P.dma("sp", ikT[:], K.ikT_d, writes=["ikT"])
        kTk = [("kT", h) for h in range(8)]
        Vk = [("V", q4) for q4 in range(4)]
        Sel = sb("Sel", [128, 16, 128], BF16)
        pidx = sb("pidx", [128, 1], I32)
        pidf = sb("pidf", [128, 1], F32)
        score = sb("score", [128, S], F32)
        self_ = score[:, 0:2048].rearrange("p (g t) -> p g t", g=16)
        sk4 = [("score", q) for q in range(4)]
        P.op("pool", lambda e: e.iota(self_, pattern=[[-8, 16], [1, 128]], base=0, channel_multiplier=0, allow_small_or_imprecise_dtypes=True), writes=sk4)
        P.op("pool", lambda e: e.iota(pidx[:], pattern=[[0, 1]], base=0, channel_multiplier=1), writes=["pidx"])
        P.op("dve", lambda e: e.tensor_scalar(out=pidx[:], in0=pidx[:], scalar1=4, scalar2=None,
                                               op0=ALU.arith_shift_right), reads=["pidx"], writes=["pidx"])
        P.op("dve", lambda e: e.tensor_copy(out=pidf[:], in_=pidx[:]), reads=["pidx"], writes=["pidf"])
        P.op("dve", lambda e: e.tensor_scalar(out=Sel[:], in0=self_, scalar1=pidf[:, 0:1], scalar2=None,
                                               op0=ALU.is_equal), reads=sk4 + ["pidf"], writes=["Sel"])
        qpos = sb("qpos", [128, 8], F32)
        P.dma("sp", qpos[:], K.qpos_own, writes=["qpos"])
        iwg = bias[:, 0:128]
        wcol = sb("wcol", [128, 128], F32)
        P.dma("sp", iwg, K.iw_d.rearrange("(g t) h -> g (t h)", t=8), writes=["bias"])
        A = [st.enter_context(nc.psum_tensor("A%d" % i, [128, 512], F32)) for i in range(2)]
        B = [st.enter_context(nc.psum_tensor("B%d" % i, [128, 512], F32)) for i in range(2)]
        C = st.enter_context(nc.psum_tensor("C3", [128, 8, 128], BF16))
        P.op("pe", lambda e: e.transpose(out=A[0][:, 0:128], in_=iwg, identity=identf),
             reads=["bias", "kposf"], writes=[("A", 0)])
        P.op("dve", lambda e: e.tensor_copy(out=wcol[:], in_=A[0][:, 0:128]), reads=[("A", 0)], writes=["wcol"])
        mask01 = sb("mask01", [128, S], BF16)
        maskT = sb("maskT", [128, 32, 128], BF16)
        R = [sb("R%d" % i, [128, 512], BF16) for i in range(2)]
        pexp = [sb("pexp%d" % i, [128, 512], BF16) for i in range(2)]
        pmk = [sb("pmk%d" % i, [128, 512], BF16) for i in range(2)]
        iqTs = sb("iqTs3", [64, 128, 16], BF16)
        qTs = sb("qTs3", [128, 8, 128], BF16)
        att = sb("att", [128, 8, 128], BF16)
        attTs = sb("attTs", [128, 8, 128], BF16)
        c2 = sb("c2", [128, NIT], F32)
        steps = sb("steps", [128, NIT], F32)
        sm = sb("sm3", [128, 8], F32)
        for k in range(NIT):
            P.op("pool", lambda e, k=k: e.memset(c2[:, k:k + 1], float(2.0 ** -(k + 1))), writes=["c2"])
        maskT2 = [maskT, sb("maskTb", [128, 32, 128], BF16)]
        Wbd = sb("Wbd", [128, 16, 128], BF16)
        qTs2 = [qTs, sb("qTs3b", [128, 8, 128], BF16)]
        rcp2 = sb("rcp2", [128, 2], F32)

        def stageA(i):
            nk = SLOT_NK[i]
            nb = nk // 4
            P.dma("sp", iqTs[:], K.iqT_d[:, i * 128:(i + 1) * 128, :], writes=["iqTs"])
            P.dma("sp", qTs2[i % 2][:], K.qT_d.rearrange("h p t -> p h t")[:, :, i * 128:(i + 1) * 128], writes=[("qTs", i % 2)])
            isteps = [(sbk, g) for sbk in range(nb) for g in range(16)]
            P.op("pool", lambda e, i=i: e.tensor_tensor(out=Wbd[:], in0=Sel[:],
                                                         in1=wcol[:, i * 16:(i + 1) * 16].unsqueeze(2).to_broadcast([128, 16, 128]),
                                                         op=ALU.mult), reads=["Sel", "wcol"], writes=["Wbd"])

            def dots(si):
                sbk, g = isteps[si]
                a_ = si % 2
                lhsT = iqTs[:, g * 8:(g + 1) * 8, :].rearrange("p t h -> p (t h)")
                P.op("pe", lambda e, a_=a_, lhsT=lhsT, sbk=sbk: e.matmul(
                    A[a_][:, :], lhsT=lhsT, rhs=ikT[:, sbk * 512:(sbk + 1) * 512], start=True, stop=True),
                    reads=["iqTs", "ikT"], writes=[("A", a_)])
            dots(0)
            for si, (sbk, g) in enumerate(isteps):
                a_ = si % 2
                bsl = sbk % 2
                if si + 1 < len(isteps):
                    dots(si + 1)
                if si % 2 == 0:
                    P.op("act", lambda e, a_=a_: e.activation(out=R[a_][:], in_=A[a_][:, :], func=AF.Relu),
                         reads=[("A", a_)], writes=[("R", a_)])
                else:
                    P.op("dve", lambda e, a_=a_: e.tensor_scalar(out=R[a_][:], in0=A[a_][:, :], scalar1=0.0, scalar2=None,
                                                                  op0=ALU.max), reads=[("A", a_)], writes=[("R", a_)])
                P.op("pe", lambda e, a_=a_, g=g, bsl=bsl: e.matmul(
                    B[bsl][:, :], lhsT=Wbd[:, g, :], rhs=R[a_][:], start=(g == 0), stop=(g == 15)),
                    reads=[("R", a_), "Wbd"], writes=[("B", bsl)])
                if g == 15:
                    P.op("dve", lambda e, bsl=bsl, sbk=sbk: e.tensor_copy(out=score[:, sbk * 512:(sbk + 1) * 512], in_=B[bsl][:, :]),
                         reads=[("B", bsl)], writes=[("score", sbk)])

        def stageBdve(i):
            nk = SLOT_NK[i]
            nb = nk // 4
            L = nk * 128
            sck = [("score", sbk) for sbk in range(nb)]
            P.op("dve", lambda e, L=L: e.tensor_reduce(out=sm[:, 0:1], in_=score[:, 0:L], axis=AX.X, op=ALU.max,
                                                        apply_absolute_value=True), reads=sck, writes=["sm0"])
            P.op("pool", lambda e, nb=nb: e.iota(kposf[:], pattern=[[1, 512]], base=(nb - 1) * 512, channel_multiplier=0,
                                                 allow_small_or_imprecise_dtypes=True), writes=["kposf"])
            P.op("dve", lambda e, i=i: e.tensor_scalar(out=bias[:], in0=kposf[:], scalar1=qpos[:, i:i + 1],
                                                        scalar2=-1e30, op0=ALU.is_gt, op1=ALU.mult),
                 reads=["kposf", "qpos"], writes=["bias"])
            P.op("dve", lambda e, nb=nb: e.tensor_tensor(out=score[:, (nb - 1) * 512:nb * 512],
                                                          in0=score[:, (nb - 1) * 512:nb * 512], in1=bias[:], op=ALU.add),
                 reads=["bias", ("score", nb - 1), "sm0"], writes=[("score", nb - 1)])
            P.op("dve", lambda e: e.tensor_scalar(out=sm[:, 1:2], in0=sm[:, 0:1], scalar1=-1.0, scalar2=-1.0,
                                                   op0=ALU.mult, op1=ALU.add), reads=["sm0"], writes=["lo"])
            P.op("dve", lambda e: e.tensor_scalar(out=sm[:, 5:6], in0=sm[:, 0:1], scalar1=2.0, scalar2=2.0,
                                                   op0=ALU.mult, op1=ALU.add), reads=["sm0"], writes=["d0"])
            P.op("dve", lambda e: e.tensor_scalar(out=steps[:], in0=c2[:], scalar1=sm[:, 5:6], scalar2=None,
                                                   op0=ALU.mult), reads=["d0", "c2"], writes=["steps"])
            P.op("dve", lambda e: e.tensor_tensor(out=sm[:, 2:3], in0=sm[:, 1:2], in1=steps[:, 0:1], op=ALU.add),
                 reads=["lo", "steps"], writes=["mid"])
            for k in range(NIT):
                P.op("dve", lambda e, L=L: e.tensor_scalar(out=mask01[:, 0:L], in0=score[:, 0:L], scalar1=sm[:, 2:3],
                                                            scalar2=None, op0=ALU.is_ge, op1=ALU.add,
                                                            accum_out=sm[:, 3:4]),
                     reads=sck + ["mid"], writes=["mask01", "cnt"])
                P.op("dve", lambda e: e.tensor_scalar(out=sm[:, 4:5], in0=sm[:, 3:4], scalar1=255.5, scalar2=-0.5,
                                                       op0=ALU.is_ge, op1=ALU.add), reads=["cnt"], writes=["inc"])
                P.op("dve", lambda e, k=k: e.scalar_tensor_tensor(out=sm[:, 2:3], in0=steps[:, k:k + 1], scalar=sm[:, 4:5],
                                                                   in1=sm[:, 2:3], op0=ALU.mult, op1=ALU.add),
                     reads=["inc", "steps", "mid"], writes=["mid"])
            P.op("dve", lambda e: e.scalar_tensor_tensor(out=sm[:, 1:2], in0=steps[:, NIT - 1:NIT], scalar=-0.5, in1=sm[:, 2:3],
                                                          op0=ALU.mult, op1=ALU.add), reads=["steps", "mid"], writes=["lo"])
            P.op("dve", lambda e, L=L: e.tensor_scalar(out=mask01[:, 0:L], in0=score[:, 0:L], scalar1=sm[:, 1:2],
                                                        scalar2=None, op0=ALU.is_ge), reads=sck + ["lo"], writes=["mask01"])

        def stageBpe(i):
            nk = SLOT_NK[i]
            mT = maskT2[i % 2]
            for kt in range(nk):
                P.op("pe", lambda e, kt=kt: e.transpose(out=C[:, kt % 8, :], in_=mask01[:, kt * 128:(kt + 1) * 128],
                                                         identity=ident[:]), reads=["mask01", "ident"], writes=["C"])
                if kt % 8 == 7 or kt == nk - 1:
                    k0 = (kt // 8) * 8
                    n8 = kt - k0 + 1
                    P.op("dve", lambda e, k0=k0, n8=n8, mT=mT: e.tensor_copy(out=mT[:, k0:k0 + n8, :], in_=C[:, 0:n8, :]),
                         reads=["C"], writes=[("maskT", i % 2, k0 // 8)])

        def stageC(i):
            nk = SLOT_NK[i]
            nb = nk // 4
            mT = maskT2[i % 2]
            qT_ = qTs2[i % 2]
            mk = [("maskT", i % 2, q) for q in range((nk + 7) // 8)]
            asteps = [(h, kg) for h in range(8) for kg in range(nb)]

            def qk(si):
                h, kg = asteps[si]
                a_ = si % 2
                for j4 in range(4):
                    kt = kg * 4 + j4
                    P.op("pe", lambda e, a_=a_, j4=j4, kt=kt, h=h: e.matmul(
                        A[a_][:, j4 * 128:(j4 + 1) * 128], lhsT=kT[:, h, kt * 128:(kt + 1) * 128], rhs=qT_[:, h, :],
                        start=True, stop=True), reads=kTk + [("qTs", i % 2)], writes=[("A", a_)])
            qk(0)
            for si, (h, kg) in enumerate(asteps):
                a_ = si % 2
                bsl = h % 2
                if si + 1 < len(asteps):
                    qk(si + 1)
                P.op("act", lambda e, a_=a_: e.activation(out=pexp[a_][:], in_=A[a_][:, :], func=AF.Exp,
                                                           scale=float(128 ** -0.5)),
                     reads=[("A", a_)], writes=[("pexp", a_)])
                P.op("pool", lambda e, a_=a_, kg=kg: e.tensor_tensor(
                    out=pmk[a_][:], in0=pexp[a_][:], in1=mT[:, kg * 4:(kg + 1) * 4, :].rearrange("p a t -> p (a t)"),
                    op=ALU.mult), reads=[("pexp", a_)] + mk, writes=[("pmk", a_)])
                for j4 in range(4):
                    kt = kg * 4 + j4
                    P.op("pe", lambda e, a_=a_, j4=j4, kt=kt, h=h, bsl=bsl, kg=kg, nb=nb: e.matmul(
                        B[bsl][:, 0:129], lhsT=pmk[a_][:, j4 * 128:(j4 + 1) * 128], rhs=V[:, kt, h * 129:(h + 1) * 129],
                        start=(kg == 0 and j4 == 0), stop=(kg == nb - 1 and j4 == 3)),
                        reads=[("pmk", a_)] + Vk, writes=[("B", bsl)])
                if kg == nb - 1:
                    P.op("act", lambda e, bsl=bsl: e.activation(out=rcp2[:, 0:1], in_=B[bsl][:, 128:129], func=AF.Ln),
                         reads=[("B", bsl)], writes=["rcpa"])
                    P.op("act", lambda e: e.activation(out=rcp2[:, 1:2], in_=rcp2[:, 0:1], func=AF.Exp, scale=-1.0),
                         reads=["rcpa"], writes=["rcpb"])
                    P.op("act", lambda e, bsl=bsl, h=h: e.activation(out=att[:, h, :], in_=B[bsl][:, 0:128], func=AF.Copy,
                                                                      scale=rcp2[:, 1:2]),
                         reads=[("B", bsl), "rcpb"], writes=["att"])
            for h in range(8):
                P.op("pe", lambda e, h=h: e.transpose(out=C[:, h, :], in_=att[:, h, :], identity=ident[:]),
                     reads=["att", "ident"], writes=["C"])
            P.op("act", lambda e: e.copy(out=attTs[:], in_=C[:]), reads=["C"], writes=["attTs"])
            P.dma("sp", K.attT_d.rearrange("h p t -> p h t")[:, :, i * 128:(i + 1) * 128], attTs[:],
                  reads=["attTs"], writes=[("attT_d", i)])

        stageA(0)
        stageBdve(0)
        stageBpe(0)
        for i in range(8):
            if i + 1 < 8:
                stageA(i + 1)
                stageBdve(i + 1)
            stageC(i)
            if i + 1 < 8:
                stageBpe(i + 1)
        P.flush()

RD = BF16
NCH = 64


def tok_shift(P, dst, raw, tmp, mu_ap, rk_raw, k_tmp, k_dst, n=128):
    P.op("pool", lambda e: e.tensor_tensor(out=tmp[0:n, 1:S], in0=raw[0:n, 0:S - 1], in1=raw[0:n, 1:S], op=ALU.subtract),
         reads=[rk_raw], writes=[k_tmp])
    P.op("pool", lambda e: e.tensor_scalar(out=tmp[0:n, 0:1], in0=raw[0:n, 0:1], scalar1=-1.0, scalar2=0.0,
                                            op0=ALU.mult, op1=ALU.add), reads=[rk_raw, k_tmp], writes=[k_tmp])
    P.op("dve", lambda e: e.scalar_tensor_tensor(out=dst[0:n, :], in0=tmp[0:n, :], scalar=mu_ap, in1=raw[0:n, :],
                                                  op0=ALU.mult, op1=ALU.add), reads=[rk_raw, k_tmp], writes=[k_dst])


def phase4b_rwkv_prep(K, cts=range(2)):
    nc, P = K.nc, K.P
    with contextlib.ExitStack() as st:
        def sb(name, shape, dt):
            return st.enter_context(nc.sbuf_tensor(name, shape, dt))
        txw = sb("txw", [96, S], BF16)
        xap = sb("xap", [96, S], BF16)
        sxg = sb("sxg", [128, 2, S], BF16)
        M01 = sb("M01", [128, S], BF16)
        wup = sb("wup", [96, 256], BF16)
        aup = sb("aup", [96, 256], BF16)
        gup = sb("gup", [128, 2, 256], BF16)
        wst = sb("wst4", [128, 2, 256], F32)
        bones = sb("bones", [128, 128], BF16)
        prm = sb("prm", [128, 12, 2], F32)
        mul = sb("mul", [128, 4], F32)
        PT = sb("PT", [128, S], F32)
        KK = sb("KK", [128, S], F32)
        KP = sb("KP", [128, S], F32)
        CL = sb("CL", [128, S], F32)
        RP = sb("RP", [128, S], BF16)
        VP = sb("VP", [128, S], BF16)
        AA = sb("AA", [128, S], BF16)
        K2 = sb("K2", [128, S], BF16)
        SQb = sb("SQb", [128, S], BF16)
        OUT = [sb("OUT%d" % i, [128, S], BF16) for i in range(2)]
        PCt = sb("PCt", [128, NCH], F32)
        ps = [st.enter_context(nc.psum_tensor("ps4_%d" % i, [128, 512], F32)) for i in range(4)]
        for i, ap in enumerate(K.rw_prm):
            P.dma("sp", prm[:, i, :], ap, writes=[("prm", i)])
        prk = [("prm", i) for i in range(10)]
        P.op("dve", lambda e: e.tensor_scalar(out=prm[:, 10, :], in0=prm[:, 6, :], scalar1=-1.0, scalar2=1.0,
                                               op0=ALU.mult, op1=ALU.add), reads=prk, writes=[("prm", 10)])
        prk = prk + [("prm", 10)]
        P.dma("sp", mul[:], K.rw_mul, writes=["mul"])
        P.op("pool", lambda e: e.memset(bones[:], 0.0), writes=["bones"])
        P.op("pool", lambda e: e.memset(bones[0:64, 0:64], 1.0), reads=["bones"], writes=["bones"])
        P.op("pool", lambda e: e.memset(bones[64:128, 64:128], 1.0), reads=["bones"], writes=["bones"])
        P.op("pool", lambda e: e.iota(PT[:].rearrange("p (c t) -> p c t", t=64), pattern=[[0, NCH], [1, 64]], base=0,
                                      channel_multiplier=0, allow_small_or_imprecise_dtypes=True), writes=["PT"])
        P.op("dve", lambda e: e.tensor_scalar(out=M01[:], in0=PT[:], scalar1=0.5, scalar2=None, op0=ALU.is_gt),
             reads=["PT"], writes=["M01"])
        P.dma("sp", wst[0:96, 0, :], K.rw_w_up, writes=["wst"])
        P.op("act", lambda e: e.copy(out=wup[:], in_=wst[0:96, 0, :]), reads=["wst"], writes=["wup"])
        P.dma("sp", wst[0:96, 1, :], K.rw_a_up, reads=[], writes=["wst1"])
        P.op("act", lambda e: e.copy(out=aup[:], in_=wst[0:96, 1, :]), reads=["wst1"], writes=["aup"])
        P.dma("sp", wst[:, :, :], K.rw_g_up.rearrange("(c p) n -> p c n", p=128), reads=[], writes=["wst", "wst1"])
        P.op("act", lambda e: e.copy(out=gup[:], in_=wst[:]), reads=["wst", "wst1"], writes=["gup"])
        for (r0, n, mcol, func, dst, kd) in ((768, 96, 0, AF.Tanh, txw[:, :], "txw"), (864, 96, 1, AF.Copy, xap[:, :], "xap"),
                                             (960, 128, 2, AF.Sigmoid, sxg[:, 0, :], "sxg0"),
                                             (1088, 128, 3, AF.Sigmoid, sxg[:, 1, :], "sxg1")):
            P.dma("sp", PT[0:n, :], K.yT_d[r0:r0 + n, :], writes=["PT"])
            tok_shift(P, KP, PT, KK, mul[0:n, mcol:mcol + 1], "PT", "KK", "KP", n=n)
            P.op("act", lambda e, n=n, func=func, dst=dst: e.activation(out=dst, in_=KP[0:n, :], func=func),
                 reads=["KP"], writes=[kd])
        lk = ["txw", "xap", "sxg0", "sxg1"]
        oc = 0
        for ct in cts:
            c0 = ct * 128
            P.dma("sp", PT[:], K.yT_d[c0:c0 + 128, :], writes=["PT"])
            tok_shift(P, RP, PT, KK, prm[:, 0, ct:ct + 1], "PT", "KK", "RP")
            P.dma("sp", PT[:], K.yT_d[256 + c0:256 + c0 + 128, :], writes=["PT"])
            tok_shift(P, KP, PT, KK, prm[:, 1, ct:ct + 1], "PT", "KK", "KP")
            P.dma("sp", PT[:], K.yT_d[512 + c0:512 + c0 + 128, :], writes=["PT"])
            tok_shift(P, VP, PT, KK, prm[:, 2, ct:ct + 1], "PT", "KK", "VP")
            P.dma("sp", K.vb_d[c0:c0 + 128, :], VP[:], reads=["VP"], writes=[("vb_d", ct)])
            for blk in range(8):
                bs = slice(blk * 512, (blk + 1) * 512)
                p0, p1, p2 = ps[0], ps[1], ps[2]
                P.op("pe", lambda e, bs=bs, c0=c0: e.matmul(ps[0][:, :], lhsT=wup[:, c0:c0 + 128], rhs=txw[:, bs],
                                                             start=True, stop=True), reads=["wup", "txw"], writes=[("ps4", 0)])
                P.op("act", lambda e, bs=bs, ct=ct: e.activation(out=CL[:, bs], in_=ps[0][:, :], func=AF.Sigmoid,
                                                                  bias=prm[:, 3, ct:ct + 1]),
                     reads=[("ps4", 0)] + prk, writes=["CL"])
                P.op("pe", lambda e, bs=bs, c0=c0: e.matmul(ps[1][:, :], lhsT=aup[:, c0:c0 + 128], rhs=xap[:, bs],
                                                             start=True, stop=True), reads=["aup", "xap"], writes=[("ps4", 1)])
                P.op("act", lambda e, bs=bs, ct=ct: e.activation(out=AA[:, bs], in_=ps[1][:, :], func=AF.Sigmoid,
                                                                  bias=prm[:, 4, ct:ct + 1]),
                     reads=[("ps4", 1)] + prk, writes=["AA"])
                for cc in range(2):
                    P.op("pe", lambda e, bs=bs, c0=c0, cc=cc: e.matmul(ps[2][:, :], lhsT=gup[:, cc, c0:c0 + 128],
                                                                       rhs=sxg[:, cc, bs], start=(cc == 0), stop=(cc == 1)),
                         reads=["gup", "sxg0", "sxg1"], writes=[("ps4", 2)])
                o = OUT[oc % 2]
                P.op("dve", lambda e, bs=bs, o=o: e.tensor_copy(out=o[:, bs], in_=ps[2][:, :]),
                     reads=[("ps4", 2)], writes=[("OUT", oc % 2)])
            P.dma("sp", K.G_d[c0:c0 + 128, :], OUT[oc % 2][:], reads=[("OUT", oc % 2)], writes=[("G_d", ct)])
            oc += 1
            P.op("dve", lambda e: e.tensor_scalar(out=CL[:], in0=CL[:], scalar1=-0.6065306597126334, scalar2=None,
                                                   op0=ALU.mult), reads=["CL"], writes=["CL"])
            P.op("dve", lambda e, ct=ct: e.tensor_scalar(out=KK[:], in0=KP[:], scalar1=prm[:, 5, ct:ct + 1], scalar2=None,
                                                          op0=ALU.mult), reads=["KP"] + prk, writes=["KK"])
            P.op("act", lambda e: e.activation(out=SQb[:], in_=KK[:], func=AF.Square), reads=["KK"], writes=["SQb"])
            for blk in range(8):
                bs = slice(blk * 512, (blk + 1) * 512)
                P.op("pe", lambda e, bs=bs: e.matmul(ps[3][:, :], lhsT=bones[:], rhs=SQb[:, bs], start=True, stop=True),
                     reads=["bones", "SQb"], writes=[("ps4", 3)])
                P.op("act", lambda e, bs=bs: e.activation(out=PT[:, bs], in_=ps[3][:, :], func=AF.Sqrt),
                     reads=[("ps4", 3)], writes=["PT"])
            P.op("dve", lambda e: e.tensor_scalar(out=PT[:], in0=PT[:], scalar1=1e-12, scalar2=None, op0=ALU.max),
                 reads=["PT"], writes=["PT"])
            P.op("dve", lambda e: e.reciprocal(out=PT[:], in_=PT[:]), reads=["PT"], writes=["PT"])
            P.op("dve", lambda e: e.tensor_tensor(out=KK[:], in0=KK[:], in1=PT[:], op=ALU.mult), reads=["KK", "PT"], writes=["KK"])
            P.op("dve", lambda e, ct=ct: e.tensor_scalar(out=PT[:], in0=AA[:], scalar1=prm[:, 6, ct:ct + 1],
                                                          scalar2=prm[:, 10, ct:ct + 1], op0=ALU.mult, op1=ALU.add),
                 reads=["AA", "PT"] + prk, writes=["PT"])
            P.op("dve", lambda e: e.tensor_tensor(out=K2[:], in0=KP[:], in1=PT[:], op=ALU.mult), reads=["KP", "PT"], writes=["K2"])
            P.op("dve", lambda e, ct=ct: e.scalar_tensor_tensor(out=SQb[:], in0=RP[:], scalar=prm[:, 7, ct:ct + 1], in1=K2[:],
                                                                 op0=ALU.mult, op1=ALU.mult),
                 reads=["RP", "K2", "SQb"] + prk, writes=["SQb"])
            o = OUT[oc % 2]
            for blk in range(8):
                bs = slice(blk * 512, (blk + 1) * 512)
                P.op("pe", lambda e, bs=bs: e.matmul(ps[3][:, :], lhsT=bones[:], rhs=SQb[:, bs], start=True, stop=True),
                     reads=["bones", "SQb"], writes=[("ps4", 3)])
                P.op("dve", lambda e, bs=bs, o=o: e.tensor_tensor(out=o[:, bs], in0=ps[3][:, :], in1=VP[:, bs], op=ALU.mult),
                     reads=[("ps4", 3), "VP"], writes=[("OUT", oc % 2)])
            P.dma("sp", K.BON_d[c0:c0 + 128, :], o[:], reads=[("OUT", oc % 2)], writes=[("BON_d", ct)])
            oc += 1
            P.op("dve", lambda e: e.tensor_tensor_scan(out=PT[:], data0=M01[:], data1=CL[:], initial=0.0,
                                                        op0=ALU.mult, op1=ALU.add), reads=["M01", "CL", "PT"], writes=["PT"])
            P.op("pool", lambda e: e.tensor_tensor(out=CL[:], in0=PT[:], in1=CL[:], op=ALU.subtract),
                 reads=["PT", "CL"], writes=["CL"])
            P.op("act", lambda e: e.activation(out=CL[:], in_=CL[:], func=AF.Exp), reads=["CL"], writes=["CL"])
            v3 = lambda t: t[:].rearrange("p (c t) -> p c t", t=64)
            o = OUT[oc % 2]
            P.op("dve", lambda e, o=o: e.scalar_tensor_tensor(out=o[:], in0=KK[:], scalar=-1.0, in1=CL[:],
                                                               op0=ALU.mult, op1=ALU.mult),
                 reads=["KK", "CL"], writes=[("OUT", oc % 2)])
            P.dma("sp", K.AH_d[c0:c0 + 128, :], o[:], reads=[("OUT", oc % 2)], writes=[("AH_d", ct)])
            oc += 1
            P.op("act", lambda e: e.activation(out=CL[:], in_=PT[:], func=AF.Exp), reads=["PT", "CL"], writes=["CL"])
            o = OUT[oc % 2]
            P.op("dve", lambda e, o=o: e.tensor_tensor(out=o[:], in0=RP[:], in1=CL[:], op=ALU.mult),
                 reads=["RP", "CL"], writes=[("OUT", oc % 2)])
            P.dma("sp", K.RH_d[c0:c0 + 128, :], o[:], reads=[("OUT", oc % 2)], writes=[("RH_d", ct)])
            oc += 1
            P.op("pool", lambda e: e.tensor_copy(out=PCt[:], in_=v3(CL)[:, :, 63]), reads=["CL"], writes=["PCt"])
            P.dma("sp", K.PC_d[c0:c0 + 128, :], PCt[:], reads=["PCt"], writes=[("PC_d", ct)])
            P.op("act", lambda e: e.activation(out=PT[:], in_=PT[:], func=AF.Exp, scale=-1.0), reads=["PT"], writes=["PT"])
            o = OUT[oc % 2]
            P.op("dve", lambda e, o=o: e.tensor_tensor(out=o[:], in0=K2[:], in1=PT[:], op=ALU.mult),
                 reads=["K2", "PT"], writes=[("OUT", oc % 2)])
            P.dma("sp", K.KH_d[c0:c0 + 128, :], o[:], reads=[("OUT", oc % 2)], writes=[("KH_d", ct)])
            oc += 1
            P.op("dve", lambda e: e.tensor_tensor(out=KK[:], in0=KK[:], in1=AA[:], op=ALU.mult), reads=["KK", "AA"], writes=["KK"])
            o = OUT[oc % 2]
            P.op("dve", lambda e, o=o: e.tensor_tensor(out=o[:], in0=KK[:], in1=PT[:], op=ALU.mult),
                 reads=["KK", "PT"], writes=[("OUT", oc % 2)])
            P.dma("sp", K.BH_d[c0:c0 + 128, :], o[:], reads=[("OUT", oc % 2)], writes=[("BH_d", ct)])
            oc += 1
        P.flush()

def phase4c_rwkv_scan(K, heads=range(4)):
    nc, P = K.nc, K.P
    with contextlib.ExitStack() as st:
        def sb(name, shape, dt):
            return st.enter_context(nc.sbuf_tensor(name, shape, dt))
        ident = sb("ident4", [128, 128], BF16)
        make_ident(K, P, ident)
        MaskG = sb("MaskG", [64, 4, 128], F32)
        MaskX = sb("MaskX", [64, 8, 64], F32)
        I8 = sb("I8", [64, 64], F32)
        ones = sb("ones4", [64, 64], F32)
        P.op("pool", lambda e: e.memset(ones[:], 1.0), writes=["ones"])
        for a in range(4):
            for cq in range(2):
                P.op("pool", lambda e, cq=cq, a=a: e.affine_select(
                    out=MaskG[:, a, cq * 64:(cq + 1) * 64], in_=ones[:], pattern=[[1, 64]],
                    compare_op=(ALU.is_gt if cq == 0 else ALU.is_ge), fill=0.0, base=0, channel_multiplier=-1),
                    reads=["ones"], writes=["MaskG"])
        for a in range(8):
            P.op("pool", lambda e, a=a: e.affine_select(out=MaskX[:, a, :], in_=ones[:], pattern=[[-1, 64]],
                                                         compare_op=ALU.is_gt, fill=0.0, base=0, channel_multiplier=1),
                 reads=["ones"], writes=["MaskX"])
        P.op("dve", lambda e: e.tensor_copy(out=I8[:], in_=ident[0:64, 0:64]), reads=["ident"], writes=["I8"])
        AH = sb("AH", [64, S], RD)
        RH = sb("RH", [64, S], RD)
        BH = sb("BH", [64, S], RD)
        KH = sb("KH", [64, S], RD)
        vb = sb("vb", [64, S], BF16)
        PC = sb("PC", [64, NCH], F32)
        ARh = sb("ARh", [64, NCH, 128], RD)
        BKh = sb("BKh", [64, NCH, 128], RD)
        GmB = sb("GmB", [64, NCH, 128], RD)
        GmK = sb("GmK", [64, NCH, 128], RD)
        Btok = sb("Btok", [64, NCH, 64], RD)
        Ktok = sb("Ktok", [64, NCH, 64], RD)
        Vtok = sb("Vtok", [64, NCH, 64], RD)
        X0 = sb("X0", [64, NCH, 64], RD)
        Pm = sb("Pm", [64, NCH, 64], RD)
        oT = sb("oT", [64, S], F32)
        Ast = sb("Ast", [64, 64], F32)
        Abf = sb("Abf", [64, 64], RD)
        Tt = sb("Tt", [64, 64], F32)
        Xs = sb("Xs", [64, 64], RD)
        Us = sb("Us", [64, 64], RD)
        PSb = st.enter_context(nc.psum_tensor("PSb", [128, 1024], BF16))
        PS = [st.enter_context(nc.psum_tensor("PS%d" % i, [128, 512], F32)) for i in range(7)]
        v3 = lambda t: t[:].rearrange("p (c t) -> p c t", t=64)
        Nb = [v3(AH), v3(RH)]
        Xb = [v3(BH), v3(KH)]
        Nk = ["AH", "RH"]
        Xk = ["BH", "KH"]
        for hd in heads:
            r0 = hd * 64
            P.dma("sp", AH[:], K.AH_d[r0:r0 + 64, :], writes=["AH"])
            P.dma("sp", RH[:], K.RH_d[r0:r0 + 64, :], writes=["RH"])
            P.dma("sp", BH[:], K.BH_d[r0:r0 + 64, :], writes=["BH"])
            P.dma("sp", KH[:], K.KH_d[r0:r0 + 64, :], writes=["KH"])
            P.dma("sp", vb[:], K.vb_d[r0:r0 + 64, :], writes=["vb"])
            P.dma("sp", PC[:], K.PC_d[r0:r0 + 64, :], writes=["PC"])
            P.op("dve", lambda e: e.tensor_copy(out=ARh[:, :, 0:64], in_=v3(AH)), reads=["AH"], writes=["ARh"])
            P.op("pool", lambda e: e.tensor_copy(out=ARh[:, :, 64:128], in_=v3(RH)), reads=["RH"], writes=["ARh"])
            P.op("dve", lambda e: e.tensor_copy(out=BKh[:, :, 0:64], in_=v3(BH)), reads=["BH"], writes=["BKh"])
            P.op("pool", lambda e: e.tensor_copy(out=BKh[:, :, 64:128], in_=v3(KH)), reads=["KH"], writes=["BKh"])
            for (src, srck, col0, dst, dk) in ((BKh, "BKh", 0, Btok, "Btok"), (BKh, "BKh", 64, Ktok, "Ktok"), (None, "vb", 0, Vtok, "Vtok")):
                for c16 in range(0, NCH, 16):
                    for cc in range(16):
                        c = c16 + cc
                        in_ = vb[:, c * 64:(c + 1) * 64] if src is None else src[:, c, col0:col0 + 64]
                        P.op("pe", lambda e, cc=cc, in_=in_: e.transpose(out=PSb[0:64, cc * 64:(cc + 1) * 64], in_=in_,
                                                                         identity=ident[0:64, 0:64]),
                             reads=[srck, "ident"], writes=["PSb"])
                    P.op("act", lambda e, c16=c16, dst=dst: e.copy(out=dst[:, c16:c16 + 16, :].rearrange("p c k -> p (c k)"),
                                                                    in_=PSb[0:64, :]), reads=["PSb"], writes=[dk])
            gi = 0
            for (col0, dst, dk) in ((0, GmB, "GmB"), (64, GmK, "GmK")):
                for c4 in range(0, NCH, 4):
                    b = gi % 2
                    gi += 1
                    for cc in range(4):
                        c = c4 + cc
                        P.op("pe", lambda e, c=c, cc=cc, b=b, col0=col0: e.matmul(
                            PS[b][0:64, cc * 128:(cc + 1) * 128], lhsT=BKh[:, c, col0:col0 + 64], rhs=ARh[:, c, :],
                            start=True, stop=True), reads=["BKh", "ARh"], writes=[("PS", b)])
                    P.op("dve", lambda e, c4=c4, dst=dst, b=b: e.tensor_tensor(
                        out=dst[:, c4:c4 + 4, :], in0=PS[b][0:64, :].rearrange("p (a t) -> p a t", t=128), in1=MaskG[:],
                        op=ALU.mult), reads=[("PS", b), "MaskG"], writes=[dk])
            for c8 in range(0, NCH, 8):
                for cc in range(8):
                    c = c8 + cc
                    P.op("pe", lambda e, c=c, cc=cc: e.matmul(PS[2][0:64, cc * 64:(cc + 1) * 64], lhsT=ARh[:, c, 0:64],
                                                               rhs=BKh[:, c, 0:64], start=True, stop=True),
                         reads=["ARh", "BKh"], writes=[("PS", 2)])
                P.op("dve", lambda e, c8=c8: e.tensor_tensor(
                    out=X0[:, c8:c8 + 8, :], in0=PS[2][0:64, :].rearrange("p (a t) -> p a t", t=64), in1=MaskX[:],
                    op=ALU.mult), reads=[("PS", 2), "MaskX"], writes=["X0"])
            N0 = GmB[:, :, 0:64]
            P.op("dve", lambda e, N0=N0: e.tensor_tensor(out=Pm[:], in0=N0, in1=I8[:].unsqueeze(1).to_broadcast([64, NCH, 64]),
                                                         op=ALU.add), reads=["GmB", "I8"], writes=["Pm"])
            curN, curNk = N0, "GmB"
            curX, curXk = X0[:], "X0"
            for lvl in range(1, 6):
                nX, nXk = Xb[lvl % 2], Xk[lvl % 2]
                nN, nNk = Nb[lvl % 2], Nk[lvl % 2]
                for c8 in range(0, NCH, 8):
                    pb_ = (c8 // 8) % 2
                    for cc in range(8):
                        c = c8 + cc
                        P.op("pe", lambda e, c=c, cc=cc, curN=curN, curX=curX, pb_=pb_: e.matmul(
                            PS[0 + pb_][0:64, cc * 64:(cc + 1) * 64], lhsT=curN[:, c, :], rhs=curX[:, c, :], start=True, stop=True),
                            reads=[curNk, curXk], writes=[("PS", 0 + pb_)])
                    P.op("act", lambda e, c8=c8, nX=nX, pb_=pb_: e.copy(out=nX[:, c8:c8 + 8, :],
                                                                in_=PS[0 + pb_][0:64, :].rearrange("p (a t) -> p a t", t=64)),
                         reads=[("PS", 0 + pb_)], writes=[nXk])
                    if lvl < 5:
                        for cc in range(8):
                            c = c8 + cc
                            P.op("pe", lambda e, c=c, cc=cc, curN=curN, curX=curX, pb_=pb_: e.matmul(
                                PS[2 + pb_][0:64, cc * 64:(cc + 1) * 64], lhsT=curX[:, c, :], rhs=curN[:, c, :], start=True, stop=True),
                                reads=[curNk, curXk], writes=[("PS", 2 + pb_)])
                        P.op("act", lambda e, c8=c8, nN=nN, pb_=pb_: e.copy(out=nN[:, c8:c8 + 8, :],
                                                                    in_=PS[2 + pb_][0:64, :].rearrange("p (a t) -> p a t", t=64)),
                             reads=[("PS", 2 + pb_)], writes=[nNk])
                    for cc in range(8):
                        c = c8 + cc
                        P.op("pe", lambda e, c=c, cc=cc, nX=nX, pb_=pb_: e.matmul(
                            PS[4 + pb_][0:64, cc * 64:(cc + 1) * 64], lhsT=nX[:, c, :], rhs=Pm[:, c, :], start=True, stop=True),
                            reads=[nXk, "Pm"], writes=[("PS", 4 + pb_)])
                    P.op("dve", lambda e, c8=c8, pb_=pb_: e.tensor_tensor(
                        out=Pm[:, c8:c8 + 8, :], in0=PS[4 + pb_][0:64, :].rearrange("p (a t) -> p a t", t=64),
                        in1=Pm[:, c8:c8 + 8, :], op=ALU.add), reads=[("PS", 4 + pb_), "Pm"], writes=["Pm"])
                curN, curNk, curX, curXk = nN, nNk, nX, nXk
            P.op("pool", lambda e: e.memset(Ast[:], 0.0), writes=["Ast"])
            P.op("pool", lambda e: e.memset(Abf[:], 0.0), writes=["Abf"])
            for c in range(NCH):
                P.op("pool", lambda e, c=c: e.tensor_scalar(out=Tt[:], in0=Ast[:], scalar1=PC[:, c:c + 1], scalar2=0.0,
                                                             op0=ALU.mult, op1=ALU.add), reads=["Ast", "PC"], writes=["Tt"])
                P.op("pe", lambda e, c=c: e.matmul(PS[0][0:64, 0:64], lhsT=ARh[:, c, 0:64], rhs=Abf[:], start=True, stop=False),
                     reads=["ARh", "Abf"], writes=[("PS", 0)])
                P.op("pe", lambda e, c=c: e.matmul(PS[0][0:64, 0:64], lhsT=GmK[:, c, 0:64], rhs=Vtok[:, c, :], start=False, stop=True),
                     reads=["GmK", "Vtok"], writes=[("PS", 0)])
                P.op("act", lambda e: e.copy(out=Xs[:], in_=PS[0][0:64, 0:64]), reads=[("PS", 0)], writes=["Xs"])
                P.op("pe", lambda e, c=c: e.matmul(PS[1][0:64, 0:64], lhsT=Pm[:, c, :], rhs=Xs[:], start=True, stop=True),
                     reads=["Pm", "Xs"], writes=[("PS", 1)])
                P.op("dve", lambda e: e.tensor_copy(out=Us[:], in_=PS[1][0:64, 0:64]), reads=[("PS", 1)], writes=["Us"])
                P.op("pe", lambda e, c=c: e.matmul(PS[6][0:64, 0:64], lhsT=Btok[:, c, :], rhs=Us[:], start=True, stop=False),
                     reads=["Btok", "Us"], writes=[("PS", 6)])
                P.op("pe", lambda e, c=c: e.matmul(PS[6][0:64, 0:64], lhsT=Ktok[:, c, :], rhs=Vtok[:, c, :], start=False, stop=True),
                     reads=["Ktok", "Vtok"], writes=[("PS", 6)])
                ob = 2 + (c % 2)
                P.op("pe", lambda e, c=c, ob=ob: e.matmul(PS[ob][0:64, 0:64], lhsT=Abf[:], rhs=ARh[:, c, 64:128], start=True, stop=False),
                     reads=["Abf", "ARh"], writes=[("PS", ob)])
                P.op("pe", lambda e, c=c, ob=ob: e.matmul(PS[ob][0:64, 0:64], lhsT=Us[:], rhs=GmB[:, c, 64:128], start=False, stop=False),
                     reads=["Us", "GmB"], writes=[("PS", ob)])
                P.op("pe", lambda e, c=c, ob=ob: e.matmul(PS[ob][0:64, 0:64], lhsT=Vtok[:, c, :], rhs=GmK[:, c, 64:128], start=False, stop=True),
                     reads=["Vtok", "GmK"], writes=[("PS", ob)])
                P.op("dve", lambda e, c=c: e.scalar_tensor_tensor(out=Abf[:], in0=PS[6][0:64, 0:64], scalar=PC[:, c:c + 1], in1=Tt[:],
                                                                   op0=ALU.mult, op1=ALU.add),
                     reads=[("PS", 6), "Tt", "PC"], writes=["Abf"])
                P.op("dve", lambda e, c=c: e.scalar_tensor_tensor(out=Ast[:], in0=PS[6][0:64, 0:64], scalar=PC[:, c:c + 1], in1=Tt[:],
                                                                   op0=ALU.mult, op1=ALU.add),
                     reads=[("PS", 6), "Tt", "PC"], writes=["Ast"])
                P.op("act", lambda e, c=c, ob=ob: e.copy(out=oT[:, c * 64:(c + 1) * 64], in_=PS[ob][0:64, 0:64]),
                     reads=[("PS", ob)], writes=[("oT", c // 8)])
            P.dma("sp", K.oT_d[r0:r0 + 64, :], oT[:], reads=[("oT", q) for q in range(8)], writes=[("oT_d", hd)])
        P.flush()

def phase4d_rwkv_post(K, cts=range(2)):
    nc, P = K.nc, K.P
    with contextlib.ExitStack() as st:
        def sb(name, shape, dt):
            return st.enter_context(nc.sbuf_tensor(name, shape, dt))
        ident = sb("ident4d", [128, 128], BF16)
        make_ident(K, P, ident)
        bonesf = sb("bonesf", [128, 128], F32)
        P.op("pool", lambda e: e.memset(bonesf[:], 0.0), writes=["bonesf"])
        P.op("pool", lambda e: e.memset(bonesf[0:64, 0:64], 1.0), reads=["bonesf"], writes=["bonesf"])
        P.op("pool", lambda e: e.memset(bonesf[64:128, 64:128], 1.0), reads=["bonesf"], writes=["bonesf"])
        prm = sb("prm4d", [128, 2, 2], F32)
        P.dma("sp", prm[:, 0, :], K.rw_prm[8], writes=["prm0"])
        P.dma("sp", prm[:, 1, :], K.rw_prm[9], writes=["prm1"])
        o = sb("o4d", [128, S], F32)
        osq = sb("osq", [128, S], F32)
        bon = sb("bon", [128, S], BF16)
        gg = sb("gg", [128, S], BF16)
        Mb = [sb("Mb%d" % i, [128, 512], F32) for i in range(2)]
        Vb = [sb("Vb%d" % i, [128, 512], F32) for i in range(2)]
        Yb = [sb("Yb%d" % i, [128, 512], F32) for i in range(2)]
        Ob = [sb("Ob%d" % i, [128, 512], BF16) for i in range(2)]
        Tk = [sb("Tk%d" % i, [128, 4, 128], BF16) for i in range(2)]
        ps = [st.enter_context(nc.psum_tensor("p4d_%d" % i, [128, 512], F32)) for i in range(4)]
        pst = [st.enter_context(nc.psum_tensor("p4dt_%d" % i, [128, 4, 128], BF16)) for i in range(2)]
        it = 0
        for ct in cts:
            c0 = ct * 128
            P.dma("sp", o[:], K.oT_d[c0:c0 + 128, :], writes=["o"])
            P.dma("sp", bon[:], K.BON_d[c0:c0 + 128, :], writes=["bon"])
            P.dma("sp", gg[:], K.G_d[c0:c0 + 128, :], writes=["gg"])
            P.op("act", lambda e: e.activation(out=osq[:], in_=o[:], func=AF.Square), reads=["o"], writes=["osq"])
            for blk in range(8):
                s2 = it % 2
                it += 1
                bs = slice(blk * 512, (blk + 1) * 512)
                P.op("pe", lambda e, bs=bs, s2=s2: e.matmul(ps[s2][:, :], lhsT=bonesf[:], rhs=o[:, bs], start=True, stop=True),
                     reads=["bonesf", "o"], writes=[("p4d", s2)])
                P.op("pe", lambda e, bs=bs, s2=s2: e.matmul(ps[2 + s2][:, :], lhsT=bonesf[:], rhs=osq[:, bs], start=True, stop=True),
                     reads=["bonesf", "osq"], writes=[("p4d", 2 + s2)])
                P.op("act", lambda e, s2=s2: e.activation(out=Mb[s2][:], in_=ps[s2][:, :], func=AF.Copy, scale=1.0 / 64),
                     reads=[("p4d", s2)], writes=[("Mb", s2)])
                P.op("pool", lambda e, s2=s2: e.tensor_tensor(out=Vb[s2][:], in0=Mb[s2][:], in1=Mb[s2][:], op=ALU.mult),
                     reads=[("Mb", s2)], writes=[("Vb", s2)])
                P.op("dve", lambda e, s2=s2: e.scalar_tensor_tensor(out=Vb[s2][:], in0=ps[2 + s2][:, :], scalar=1.0 / 64, in1=Vb[s2][:],
                                                                     op0=ALU.mult, op1=ALU.subtract),
                     reads=[("p4d", 2 + s2), ("Vb", s2)], writes=[("Vb", s2)])
                P.op("dve", lambda e, s2=s2: e.tensor_scalar(out=Vb[s2][:], in0=Vb[s2][:], scalar1=64e-5, scalar2=None, op0=ALU.add),
                     reads=[("Vb", s2)], writes=[("Vb", s2)])
                P.op("act", lambda e, s2=s2: e.activation(out=Vb[s2][:], in_=Vb[s2][:], func=AF.Sqrt),
                     reads=[("Vb", s2)], writes=[("Vb", s2)])
                P.op("dve", lambda e, s2=s2: e.reciprocal(out=Vb[s2][:], in_=Vb[s2][:]), reads=[("Vb", s2)], writes=[("Vb", s2)])
                P.op("pool", lambda e, s2=s2, bs=bs: e.tensor_tensor(out=Yb[s2][:], in0=o[:, bs], in1=Mb[s2][:], op=ALU.subtract),
                     reads=["o", ("Mb", s2)], writes=[("Yb", s2)])
                P.op("dve", lambda e, s2=s2: e.tensor_tensor(out=Yb[s2][:], in0=Yb[s2][:], in1=Vb[s2][:], op=ALU.mult),
                     reads=[("Yb", s2), ("Vb", s2)], writes=[("Yb", s2)])
                P.op("dve", lambda e, s2=s2, ct=ct: e.tensor_scalar(out=Yb[s2][:], in0=Yb[s2][:], scalar1=prm[:, 0, ct:ct + 1],
                                                                     scalar2=prm[:, 1, ct:ct + 1], op0=ALU.mult, op1=ALU.add),
                     reads=[("Yb", s2), "prm0", "prm1"], writes=[("Yb", s2)])
                P.op("pool", lambda e, s2=s2, bs=bs: e.tensor_tensor(out=Yb[s2][:], in0=Yb[s2][:], in1=bon[:, bs], op=ALU.add),
                     reads=[("Yb", s2), "bon"], writes=[("Yb", s2)])
                P.op("dve", lambda e, s2=s2, bs=bs: e.tensor_tensor(out=Ob[s2][:], in0=Yb[s2][:], in1=gg[:, bs], op=ALU.mult),
                     reads=[("Yb", s2), "gg"], writes=[("Ob", s2)])
                for q in range(4):
                    P.op("pe", lambda e, s2=s2, q=q: e.transpose(out=pst[s2][:, q, :], in_=Ob[s2][:, q * 128:(q + 1) * 128],
                                                                 identity=ident[:]),
                         reads=[("Ob", s2), "ident"], writes=[("p4dt", s2)])
                P.op("act", lambda e, s2=s2: e.copy(out=Tk[s2][:], in_=pst[s2][:]), reads=[("p4dt", s2)], writes=[("Tk", s2)])
                P.dma("sp", K.ro_loc_d[blk // 4].rearrange("(t p) c -> p t c", p=128)[:, (blk % 4) * 4:(blk % 4 + 1) * 4, c0:c0 + 128], Tk[s2][:],
                      reads=[("Tk", s2)], writes=[("ro_tok_d", ct, blk)])
        P.flush()


def phase4e_allgather(K):
    P = K.P
    for hh in range(2):
        P.coll(lambda e, hh=hh: e.collective_compute("AllGather", ALU.bypass, replica_groups=[[0, 1, 2, 3], [4, 5, 6, 7]],
                                                     ins=[K.ro_loc_d[hh].opt()], outs=[K.ro_all_d[hh].opt()]),
               reads=[("ro_loc", hh)], writes=[("ro_all", hh)])
    P.flush()


def phase5a_select(K):
    nc, P = K.nc, K.P
    with contextlib.ExitStack() as st:
        def sb(name, shape, dt):
            return st.enter_context(nc.sbuf_tensor(name, shape, dt))
        ro = sb("ro_tok", [128, 32, 1024], BF16)
        selT = sb("selT", [128, 32, 1024], BF16)
        qrow = sb("qrow", [128, 1024], F32)
        tki = sb("tki", [128, 32], I32)
        tkf = sb("tkf", [128, 32], F32)
        mo = [sb("mo%d" % i, [128, 512], BF16) for i in range(2)]
        at = sb("at5", [128, 8, 1024], BF16)
        ps = [st.enter_context(nc.psum_tensor("p5a_%d" % i, [128, 512], F32)) for i in range(2)]
        for q4 in range(4):
            for hh in range(2):
                P.dma("sp", ro[:, hh * 16:(hh + 1) * 16, q4 * 256:(q4 + 1) * 256],
                      K.ro_all_d[hh][q4 * 2048:(q4 + 1) * 2048, :].rearrange("(t p) c -> p t c", p=128), writes=[("ro", q4, hh)])
        rok = [("ro", q4, hh) for q4 in range(4) for hh in range(2)]
        P.dma("sp", qrow[:], bcast_rows(K.qpos_row, 1024), writes=["qrow"])
        P.op("pool", lambda e: e.iota(tki[:], pattern=[[128, 32]], base=0, channel_multiplier=1), writes=["tki"])
        P.op("dve", lambda e: e.tensor_copy(out=tkf[:], in_=tki[:]), reads=["tki"], writes=["tkf"])
        for T in range(32):
            P.op("dve", lambda e, T=T: e.tensor_scalar(out=selT[:, T, :], in0=qrow[:], scalar1=tkf[:, T:T + 1], scalar2=0.0,
                                                      op0=ALU.is_equal, op1=ALU.add), reads=["qrow", "tkf"], writes=[("selT", T)])
        sk = [("selT", T) for T in range(32)]
        P.dma("sp", at[:], K.attT_d.rearrange("h p t -> p h t"), writes=["at5"])
        P.dma("sp", K.mixT_d.rearrange("k p t -> p k t")[:, 0:8, :], at[:], reads=["at5"], writes=["mixa"])
        i = 0
        for m in range(8):
            for half in range(2):
                s2 = i % 2
                i += 1
                for T in range(32):
                    P.op("pe", lambda e, T=T, m=m, half=half, s2=s2: e.matmul(
                        ps[s2][:, :], lhsT=ro[:, T, m * 128:(m + 1) * 128], rhs=selT[:, T, half * 512:(half + 1) * 512],
                        start=(T == 0), stop=(T == 31)), reads=rok + sk, writes=[("p5a", s2)])
                P.op("act", lambda e, s2=s2: e.copy(out=mo[s2][:], in_=ps[s2][:, :]), reads=[("p5a", s2)], writes=[("mo", s2)])
                P.dma("sp", K.mixT_d[8 + m, :, half * 512:(half + 1) * 512], mo[s2][:], reads=[("mo", s2)], writes=[("mixr", m, half)])
        P.flush()


def phase5b_outproj(K):
    nc, P = K.nc, K.P
    with contextlib.ExitStack() as st:
        def sb(name, shape, dt):
            return st.enter_context(nc.sbuf_tensor(name, shape, dt))
        ident = sb("ident5", [128, 128], BF16)
        make_ident(K, P, ident)
        G2, SH2 = load_G_SH(K, P, st, 3, 4, K.norm2_g, "p5")
        GT1 = sb("GT1", [128, D], F32)
        P.dma("sp", GT1[:], bcast_rows(K.mod_d[2 * D:3 * D], D), writes=["GT1"])
        Wo = sb("Wo", [128, 16, D], BF16)
        stg = [sb("wstg5_%d" % i, [128, 4, 512], F32) for i in range(2)]
        wk = load_weight_bf16(K, P, stg, Wo, 0, K.w_out, D, "Wo")
        mixT = sb("mixT", [128, 16, 512], BF16)
        T = norm_tiles_alloc(K, st, "p5")
        x1 = T["xt"]
        hT = [sb("hT5_0", [128, 16, 512], BF16)] * 2
        xo = [sb("xo%d" % i, [128, D], F32) for i in range(2)]
        ps = [st.enter_context(nc.psum_tensor("p5b_%d" % i, [128, 512], F32)) for i in range(2)]
        ss, junk, hb, pT = T["ss"], T["junk"], T["hb"], T["pT"]
        gi = 0
        for blk in range(2):
            hs = 0
            P.dma("sp", mixT[:], K.mixT_d.rearrange("k p t -> p k t")[:, :, blk * 512:(blk + 1) * 512], writes=["mixT"])
            for ti in range(4):
                t = blk * 4 + ti
                xs = t % 2
                P.dma("sp", xo[xs][:], K.x_own[t * 128:(t + 1) * 128, :], writes=[("xo", xs)])
                for cg in range(4):
                    b = gi % 2
                    gi += 1
                    for k in range(16):
                        P.op("pe", lambda e, b=b, k=k, t=t, cg=cg: e.matmul(
                            ps[b][:, :], lhsT=mixT[:, k, (t % 4) * 128:(t % 4 + 1) * 128], rhs=Wo[:, k, cg * 512:(cg + 1) * 512],
                            start=(k == 0), stop=(k == 15)), reads=["mixT", ("Wo", cg * 512, 4 * (k // 4))], writes=[("p5b", b)])
                    cs = slice(cg * 512, (cg + 1) * 512)
                    P.op("dve", lambda e, b=b, xs=xs, cs=cs: e.tensor_tensor(out=x1[xs][:, cs], in0=ps[b][:, :], in1=GT1[:, cs], op=ALU.mult),
                         reads=[("p5b", b), "GT1"], writes=[("xt", xs)])
                    P.op("pool", lambda e, xs=xs, cs=cs: e.tensor_tensor(out=x1[xs][:, cs], in0=x1[xs][:, cs], in1=xo[xs][:, cs], op=ALU.add),
                         reads=[("xt", xs), ("xo", xs)], writes=[("xt", xs)])
                P.dma("sp", K.x1_d[t * 128:(t + 1) * 128, :], x1[xs][:], reads=[("xt", xs)], writes=[("x1_d", t)])
                P.op("act", lambda e, xs=xs: e.activation(out=junk[:], in_=x1[xs][:], func=AF.Square, accum_out=ss[:, 0:1]),
                     reads=[("xt", xs)], writes=["junk", "ss0"])
                P.op("dve", lambda e: e.tensor_scalar(out=ss[:, 1:2], in0=ss[:, 0:1], scalar1=1.0 / D, scalar2=1e-6,
                                                       op0=ALU.mult, op1=ALU.add), reads=["ss0"], writes=["ss1"])
                P.op("act", lambda e: e.activation(out=ss[:, 2:3], in_=ss[:, 1:2], func=AF.Sqrt), reads=["ss1"], writes=["ss2"])
                P.op("dve", lambda e: e.reciprocal(out=ss[:, 3:4], in_=ss[:, 2:3]), reads=["ss2"], writes=["ss3"])
                P.op("dve", lambda e, xs=xs: e.scalar_tensor_tensor(out=x1[xs][:], in0=x1[xs][:], scalar=ss[:, 3:4], in1=G2[:],
                                                                   op0=ALU.mult, op1=ALU.mult),
                     reads=[("xt", xs), "ss3", "G"], writes=[("xt", xs)])
                P.op("pool", lambda e, xs=xs: e.tensor_tensor(out=hb[xs][:], in0=x1[xs][:], in1=SH2[:], op=ALU.add),
                     reads=[("xt", xs), "SH"], writes=[("hb", xs)])
                for half in range(2):
                    for kk in range(8):
                        k = half * 8 + kk
                        P.op("pe", lambda e, k=k, kk=kk, half=half, xs=xs: e.transpose(
                            out=pT[half][:, kk, :], in_=hb[xs][:, k * 128:(k + 1) * 128], identity=ident[:]),
                            reads=[("hb", xs), "ident"], writes=[("pT", half)])
                    o_ = hT[hs][:, half * 8:(half + 1) * 8, ti * 128:(ti + 1) * 128]
                    if half == 0:
                        P.op("act", lambda e, o_=o_, half=half: e.copy(out=o_, in_=pT[half][:]), reads=[("pT", half)], writes=[("hT5", hs, ti, half)])
                    else:
                        P.op("dve", lambda e, o_=o_, half=half: e.tensor_copy(out=o_, in_=pT[half][:]), reads=[("pT", half)], writes=[("hT5", hs, ti, half)])
            P.dma("sp", K.h2T_d.rearrange("k p t -> p k t")[:, :, blk * 512:(blk + 1) * 512], hT[hs][:],
                  reads=[("hT5", hs, ti, half) for ti in range(4) for half in range(2)], writes=[("h2T_d", blk)])
        P.flush()


def phase5c_ffn(K):
    nc, P = K.nc, K.P
    NF = 5632 // 128
    with contextlib.ExitStack() as st:
        def sb(name, shape, dt):
            return st.enter_context(nc.sbuf_tensor(name, shape, dt))
        h2T = sb("h2T", [128, 16, OWN], BF16)
        P.dma("sp", h2T[:], K.h2T_d.rearrange("k p t -> p k t"), writes=["h2T"])
        ao = [sb("ao%d" % i, [128, 512], BF16) for i in range(2)]
        stg = [sb("wstg6_%d" % i, [128, 4, 512], F32) for i in range(4)]
        Wg = [sb("Wg%d" % i, [128, 16, 512], BF16) for i in range(2)]
        Wu = [sb("Wu%d" % i, [128, 16, 512], BF16) for i in range(2)]
        sg = [sb("sg%d" % i, [128, 512], F32) for i in range(2)]
        ps = [st.enter_context(nc.psum_tensor("p5c_%d" % i, [128, 512], F32)) for i in range(4)]
        gi = 0

        def load_group(fg, defer=None):
            ws = fg % 2
            load_weight_bf16(K, P, stg, Wg[ws], 0, K.w_ffn_gate[:, fg * 512:(fg + 1) * 512], 512, ("Wg", ws), defer=defer)
            load_weight_bf16(K, P, stg, Wu[ws], 0, K.w_ffn_up[:, fg * 512:(fg + 1) * 512], 512, ("Wu", ws), defer=defer)
        load_group(0)
        for fg in range(11):
            ws = fg % 2
            pend = []
            if fg + 1 < 11:
                load_group(fg + 1, defer=pend)
            for f4 in range(4):
                f = fg * 4 + f4
                for tb in range(2):
                    b = gi % 2
                    gi += 1
                    if pend:
                        pend.pop(0)()
                    for k in range(16):
                        P.op("pe", lambda e, b=b, k=k, f4=f4, tb=tb, ws=ws: e.matmul(
                            ps[b][:, :], lhsT=Wg[ws][:, k, f4 * 128:(f4 + 1) * 128], rhs=h2T[:, k, tb * 512:(tb + 1) * 512],
                            start=(k == 0), stop=(k == 15)), reads=["h2T", (("Wg", ws), 0, (k // 4) * 4)], writes=[("p5c", b)])
                    for k in range(16):
                        P.op("pe", lambda e, b=b, k=k, f4=f4, tb=tb, ws=ws: e.matmul(
                            ps[2 + b][:, :], lhsT=Wu[ws][:, k, f4 * 128:(f4 + 1) * 128], rhs=h2T[:, k, tb * 512:(tb + 1) * 512],
                            start=(k == 0), stop=(k == 15)), reads=["h2T", (("Wu", ws), 0, (k // 4) * 4)], writes=[("p5c", 2 + b)])
                    P.op("act", lambda e, b=b: e.activation(out=sg[b][:], in_=ps[b][:, :], func=AF.Silu),
                         reads=[("p5c", b)], writes=[("sg", b)])
                    P.op("dve", lambda e, b=b: e.tensor_tensor(out=ao[b][:], in0=ps[2 + b][:, :], in1=sg[b][:], op=ALU.mult),
                         reads=[("p5c", 2 + b), ("sg", b)], writes=[("ao", b)])
                    P.dma("sp", K.actT_d[f, :, tb * 512:(tb + 1) * 512], ao[b][:], reads=[("ao", b)], writes=[("actT_d", f, tb)])
        P.flush()
    with contextlib.ExitStack() as st:
        def sb(name, shape, dt):
            return st.enter_context(nc.sbuf_tensor(name, shape, dt))
        GT2 = sb("GT2", [128, D], F32)
        P.dma("sp", GT2[:], bcast_rows(K.mod_d[5 * D:6 * D], D), writes=["GT2"])
        actT = sb("actT", [128, NF, OWN], BF16)
        for q in range(4):
            P.dma("sp", actT[:, q * 11:(q + 1) * 11, :], K.actT_d.rearrange("f p t -> p f t")[:, q * 11:(q + 1) * 11, :], writes=[("actT", q)])
        ak = [("actT", q) for q in range(4)]
        stg = [sb("wstg7_%d" % i, [128, 4, 256], F32) for i in range(4)]
        ps = [st.enter_context(nc.psum_tensor("p5d_%d" % i, [128, 512], F32)) for i in range(2)]
        gi = 0
        Wd = [sb("Wd%d" % i, [128, NF, 256], BF16) for i in range(2)]
        x1 = [sb("x1_%d" % i, [128, 256], F32) for i in range(2)]
        yo = [sb("yo%d" % i, [128, 256], F32) for i in range(2)]
        wdv = K.w_ffn_down.rearrange("(k p) n -> p k n", p=128)
        engs = ["pool", "dve", "act"]

        def load_wd(cg, defer=None):
            wsl = cg % 2
            for k0 in range(0, NF, 4):
                if defer is not None:
                    defer.append(lambda k0=k0: load_wd_piece(cg, wsl, k0))
                else:
                    load_wd_piece(cg, wsl, k0)

        def load_wd_piece(cg, wsl, k0):
            if True:
                i = K.wcnt
                K.wcnt += 1
                sl = i % 4
                P.dma("sp", stg[sl][:, 0:4, 0:256], wdv[:, k0:k0 + 4, cg * 256:(cg + 1) * 256], writes=[("wstg", sl)])
                eng = engs[i % 3]
                o_ = Wd[wsl][:, k0:k0 + 4, :]
                if eng == "act":
                    P.op("act", lambda e, o_=o_, sl=sl: e.copy(out=o_, in_=stg[sl][:, 0:4, 0:256]), reads=[("wstg", sl)], writes=[("Wd", wsl, k0)])
                else:
                    P.op(eng, lambda e, o_=o_, sl=sl: e.tensor_copy(out=o_, in_=stg[sl][:, 0:4, 0:256]), reads=[("wstg", sl)], writes=[("Wd", wsl, k0)])
        load_wd(0)
        for cg in range(8):
            wsl = cg % 2
            cs = slice(cg * 256, (cg + 1) * 256)
            pend = []
            if cg + 1 < 8:
                load_wd(cg + 1, defer=pend)
            for t in range(8):
                b = gi % 2
                gi += 1
                for _ in range(2):
                    if pend:
                        pend.pop(0)()
                P.dma("sp", x1[b][:], K.x1_d[t * 128:(t + 1) * 128, cs], writes=[("x1", b)])
                for f in range(NF):
                    P.op("pe", lambda e, b=b, f=f, t=t, wsl=wsl: e.matmul(ps[b][:, 0:256], lhsT=actT[:, f, t * 128:(t + 1) * 128], rhs=Wd[wsl][:, f, :],
                                                                          start=(f == 0), stop=(f == NF - 1)),
                         reads=[("actT", f // 11), ("Wd", wsl, (f // 4) * 4)], writes=[("p5c", b)])
                P.op("dve", lambda e, b=b, cs=cs: e.tensor_tensor(out=yo[b][:], in0=ps[b][:, 0:256], in1=GT2[:, cs], op=ALU.mult),
                     reads=[("p5c", b), "GT2"], writes=[("yo", b)])
                P.op("pool", lambda e, b=b: e.tensor_tensor(out=yo[b][:], in0=yo[b][:], in1=x1[b][:], op=ALU.add),
                     reads=[("yo", b), ("x1", b)], writes=[("yo", b)])
                P.dma("sp", K.out[t * 128:(t + 1) * 128, cs], yo[b][:], reads=[("yo", b)], writes=[("out", t, cg)])
        P.flush()


def phase_final_copy(K):
    nc, P = K.nc, K.P
    with contextlib.ExitStack() as st:
        xt = [st.enter_context(nc.sbuf_tensor("fx%d" % i, [128, D], F32)) for i in range(2)]
        for t in range(8):
            s = t % 2
            P.dma("sp", xt[s][:], K.x_own[t * 128:(t + 1) * 128, :], writes=[("fx", s)])
            P.dma("sp", K.out[t * 128:(t + 1) * 128, :], xt[s][:], reads=[("fx", s)], writes=[("out", t)])
        P.flush()


def own_tiles(j):
    r = []
    for m in range(4):
        r += [8 * m + j, 8 * m + 7 - j]
    return r


def build_program(debug=False, stages=99, cts=range(2), dbg_list=None, skip_att=False):
    nc = bass.Bass("TRN2", target_bir_lowering=False)
    K = Ctx()
    K.stages = stages
    K.cts = cts
    K.skip_att = skip_att
    K.nc = nc
    K.dbg = {}
    K.wcnt = 0

    def inp(name, shape, dt=F32):
        return nc.dram_tensor(name, list(shape), dt, kind="ExternalInput").ap()

    def scratch(name, shape, dt):
        return nc.dram_tensor(name, list(shape), dt, kind="Internal").ap()

    K.x_full = inp("x_full", [S, D])
    K.x_own = inp("x_own", [OWN, D])
    K.c_arr = inp("c_arr", [128, 16])
    K.pos_full = inp("pos_full", [128, 32], I32)
    K.invf_att = inp("invf_att", [128, 16])
    K.invf_idx = inp("invf_idx", [128, 8])
    K.w_ada = inp("w_ada", [D, 3072])
    K.b_ada = inp("b_ada", [3072])
    K.norm1_g = inp("norm1_g", [D])
    K.k_norm_g = inp("k_norm_g", [128])
    K.q_norm_g = inp("q_norm_g", [128])
    K.pos_own = inp("pos_own", [128, 8], I32)
    K.qpos_own = inp("qpos_own", [128, 8])
    K.w_in = inp("w_in", [D, 4176])
    K.rw_prm = [inp("rwp%d" % i, [128, 2]) for i in range(10)]
    K.w_in_rw = inp("w_in_rw", [D, 1216])
    K.rw_mul = inp("rw_mul", [128, 4])
    K.rw_w_up = inp("rw_w_up", [96, 256])
    K.rw_a_up = inp("rw_a_up", [96, 256])
    K.rw_g_up = inp("rw_g_up", [256, 256])
    K.qpos_row = inp("qpos_row", [OWN])
    K.w_out = inp("w_out", [D, D])
    K.norm2_g = inp("norm2_g", [D])
    K.w_ffn_gate = inp("w_ffn_gate", [D, 5632])
    K.w_ffn_up = inp("w_ffn_up", [D, 5632])
    K.w_ffn_down = inp("w_ffn_down", [5632, D])
    K.out = nc.dram_tensor("y_own", [OWN, D], F32, kind="ExternalOutput").ap()
    K.modq_d = scratch("modq_d", [1, 3072], F32)
    K.mod4_d = scratch("mod4_d", [4, 3072], F32)
    K.mod_d = K.mod4_d.rearrange("a n -> (a n)")
    K.hT_d = scratch("hT_d", [16, 128, S], BF16)
    K.kT_d = scratch("kT_d", [8, 128, S], BF16)
    K.v_d = scratch("v_d", [S, 8 * 129], BF16)
    K.ikT_d = scratch("ikT_d", [64, S], BF16)
    K.yT_d = scratch("yT_d", [1216, S], F32)
    K.qT_d = scratch("qT_d", [8, 128, OWN], BF16)
    K.iqT_d = scratch("iqT_d", [64, OWN, 16], BF16)
    K.iw_d = scratch("iw_d", [OWN, 16], F32)
    K.attT_d = scratch("attT_d", [8, 128, OWN], BF16)
    for nm in ("vb_d", "G_d", "BON_d", "AH_d", "RH_d", "BH_d", "KH_d"):
        setattr(K, nm, scratch(nm, [256, S], BF16))
    K.PC_d = scratch("PC_d", [256, NCH], F32)
    K.oT_d = scratch("oT_d", [256, S], F32)
    K.ro_loc_d = [scratch("ro_loc%d_d" % i, [2048, 256], BF16) for i in range(2)]
    K.ro_all_d = [scratch("ro_all%d_d" % i, [8192, 256], BF16) for i in range(2)]
    K.mixT_d = scratch("mixT_d", [16, 128, OWN], BF16)
    K.x1_d = scratch("x1_d", [OWN, D], F32)
    K.h2T_d = scratch("h2T_d", [16, 128, OWN], BF16)
    K.actT_d = scratch("actT_d", [44, 128, OWN], BF16)
    with contextlib.ExitStack() as stack:
        K.P = Prog(nc, stack)
        phase0_adaln(K)
        phase1_kv(K)
        if K.stages >= 2:
            phase1b_rwkv_proj(K)
        if K.stages >= 3 and not getattr(K, "skip_att", False):
            phase2_own_proj(K)
            phase3_attention(K)
        if K.stages >= 4:
            phase4b_rwkv_prep(K, cts=K.cts)
            if K.stages >= 5:
                phase4c_rwkv_scan(K, heads=[h for ct in K.cts for h in (2 * ct, 2 * ct + 1)])
        if K.stages >= 6:
            phase4d_rwkv_post(K, cts=K.cts)
            phase4e_allgather(K)
        if K.stages >= 7:
            phase5a_select(K)
            phase5b_outproj(K)
            phase5c_ffn(K)
        else:
            phase_final_copy(K)
        if debug:
            P = K.P
            allc = (("dbg_mixT", K.mixT_d, [16, 128, OWN], BF16), ("dbg_x1", K.x1_d, [OWN, D], F32),
                    ("dbg_oT", K.oT_d, [256, S], F32), ("dbg_AH", K.AH_d, [256, S], BF16), ("dbg_BH", K.BH_d, [256, S], BF16),
                    ("dbg_KH", K.KH_d, [256, S], BF16), ("dbg_RH", K.RH_d, [256, S], BF16), ("dbg_PC", K.PC_d, [256, NCH], F32),
                    ("dbg_G", K.G_d, [256, S], BF16), ("dbg_BON", K.BON_d, [256, S], BF16), ("dbg_vb", K.vb_d, [256, S], BF16),
                    ("dbg_yT", K.yT_d, [1216, S], F32), ("dbg_attT", K.attT_d, [8, 128, OWN], BF16),
                                     ("dbg_qT", K.qT_d, [8, 128, OWN], BF16), ("dbg_iqT", K.iqT_d, [64, OWN, 16], BF16),
                                     ("dbg_iw", K.iw_d, [OWN, 16], F32))
            for nm, src, shp, dt in allc:
                if dbg_list is not None and nm not in dbg_list:
                    continue
                o = dbg_out(K, nm, shp, dt)
                P.dma("sp", o, src, writes=[nm])
            P.flush()
    return nc, K


def make_in_maps(inputs, cores=range(8)):
    x = np.asarray(inputs["x"], dtype=np.float32)
    c = np.asarray(inputs["c"], dtype=np.float32)
    pos = np.asarray(inputs["positions"], dtype=np.int32)
    invf_att = (np.float32(500000.0) ** (-np.arange(16, dtype=np.float32) / np.float32(16))).astype(np.float32)
    invf_idx = (np.float32(500000.0) ** (-np.arange(8, dtype=np.float32) / np.float32(8))).astype(np.float32)
    mu = np.asarray(inputs["rwkv_mu"][0], dtype=np.float32)

    vecs = [mu[0:1024], mu[1024:2048], mu[2048:3072], inputs["rwkv_w0"][0], inputs["rwkv_a0"][0], inputs["rwkv_k_k"][0],
            inputs["rwkv_k_a"][0], np.asarray(inputs["rwkv_r_k"][0]).reshape(-1), inputs["rwkv_lnx_g"][0], inputs["rwkv_lnx_b"][0]]
    w_in_full = np.asarray(inputs["w_in"][0], dtype=np.float32)
    rw_mul = np.zeros((128, 4), np.float32)
    rw_mul[:96, 0] = mu[3072:3168]
    rw_mul[:96, 1] = mu[3168:3264]
    rw_mul[:, 2] = mu[3264:3392]
    rw_mul[:, 3] = mu[3392:3520]
    maps = []
    for core in cores:
        b, j = core // 4, core % 4
        ch = slice(256 * j, 256 * j + 256)
        rwp = {"rwp%d" % i: np.ascontiguousarray(np.asarray(v, dtype=np.float32)[ch].reshape(2, 128).T) for i, v in enumerate(vecs)}
        R0 = 4176
        w_in_rw = np.ascontiguousarray(np.concatenate([w_in_full[:, R0 + 256 * j:R0 + 256 * j + 256],
                                                       w_in_full[:, R0 + 1024 + 256 * j:R0 + 1024 + 256 * j + 256],
                                                       w_in_full[:, R0 + 2048 + 256 * j:R0 + 2048 + 256 * j + 256],
                                                       w_in_full[:, R0 + 3072:R0 + 3520]], axis=1))
        tiles = own_tiles(j)
        idx = np.concatenate([np.arange(t * 128, (t + 1) * 128) for t in tiles])
        maps.append({
            "x_full": np.ascontiguousarray(x[b]),
            "x_own": np.ascontiguousarray(x[b][idx]),
            "c_arr": np.ascontiguousarray(c[b].reshape(16, 128).T),
            "pos_full": np.ascontiguousarray(pos[b].reshape(32, 128).T),
            "invf_att": np.ascontiguousarray(np.broadcast_to(invf_att, (128, 16))),
            "invf_idx": np.ascontiguousarray(np.broadcast_to(invf_idx, (128, 8))),
            "w_ada": np.ascontiguousarray(np.asarray(inputs["w_ada"][0], dtype=np.float32)[:, 3072 * j:3072 * (j + 1)]),
            "b_ada": np.ascontiguousarray(np.asarray(inputs["b_ada"][0], dtype=np.float32)[3072 * j:3072 * (j + 1)]),
            "norm1_g": np.asarray(inputs["norm1_g"][0], dtype=np.float32),
            "k_norm_g": np.asarray(inputs["k_norm_g"][0], dtype=np.float32),
            "q_norm_g": np.asarray(inputs["q_norm_g"][0], dtype=np.float32),
            "pos_own": np.ascontiguousarray(pos[b][idx].reshape(8, 128).T),
            "qpos_own": np.ascontiguousarray(idx.astype(np.float32).reshape(8, 128).T),
            "w_in": np.ascontiguousarray(w_in_full[:, 0:4176]),
            "qpos_row": idx.astype(np.float32),
            "w_out": np.asarray(inputs["w_out"][0], dtype=np.float32),
            "norm2_g": np.asarray(inputs["norm2_g"][0], dtype=np.float32),
            "w_ffn_gate": np.asarray(inputs["w_ffn_gate"][0], dtype=np.float32),
            "w_ffn_up": np.asarray(inputs["w_ffn_up"][0], dtype=np.float32),
            "w_ffn_down": np.asarray(inputs["w_ffn_down"][0], dtype=np.float32),
            "rw_w_up": np.ascontiguousarray(np.asarray(inputs["rwkv_w_up"][0], dtype=np.float32)[:, ch]),
            "rw_a_up": np.ascontiguousarray(np.asarray(inputs["rwkv_a_up"][0], dtype=np.float32)[:, ch]),
            "rw_g_up": np.ascontiguousarray(np.asarray(inputs["rwkv_g_up"][0], dtype=np.float32)[:, ch]),
            "w_in_rw": w_in_rw,
            "rw_mul": rw_mul,
            **rwp,
        })
    return maps


def kernel(**inputs):
    nc, K = build_program(debug=False)
    maps = make_in_maps(inputs)
    res = run_bass_kernel_spmd(nc, maps, core_ids=list(range(8)))
    out = np.zeros((2, S, D), dtype=np.float32)
    for core in range(8):
        b, j = core // 4, core % 4
        y = res.results[core]["y_own"]
        for i, t in enumerate(own_tiles(j)):
            out[b, t * 128:(t + 1) * 128] = y[i * 128:(i + 1) * 128]
    return out
```

```python
import contextlib
import numpy as np
import concourse.bass as bass
import concourse.mybir as mybir
from concourse.bass_utils import run_bass_kernel_spmd

F32 = mybir.dt.float32
BF16 = mybir.dt.bfloat16
I32 = mybir.dt.int32
AF = mybir.ActivationFunctionType
ALU = mybir.AluOpType
AX = mybir.AxisListType

D = 2048
S = 4096
NT = 32
OWN = 1024
ENGS = ("pe", "act", "dve", "pool", "sp")
DEBUG = {}


class _Op:
    __slots__ = ("eng", "fn", "deps", "needs_inc", "is_dma", "sem", "count", "idx", "prev_same_sem", "is_cc")

    def __init__(self, eng, fn, is_dma):
        self.eng = eng
        self.fn = fn
        self.deps = set()
        self.needs_inc = False
        self.is_dma = is_dma
        self.sem = None
        self.count = 0
        self.prev_same_sem = None
        self.is_cc = False


class Prog:
    def __init__(self, nc, stack, n_dma_sems=48):
        self.nc = nc
        self.n_dma_sems = n_dma_sems
        self.eng_sem = {e: stack.enter_context(nc.semaphore("s_" + e)) for e in ENGS}
        self.dma_sems = [stack.enter_context(nc.semaphore("d%d" % i)) for i in range(n_dma_sems)]
        self.bar_sem = stack.enter_context(nc.semaphore("bar"))
        self.cc_sem = stack.enter_context(nc.semaphore("ccs"))
        self.cc_cnt = 0
        self.cnt = {e: 0 for e in ENGS}
        self.dcnt = [0] * n_dma_sems
        self.rr = 0
        self.nbar = 0
        self._reset()

    def _reset(self):
        self.ops = []
        self.last_writer = {}
        self.readers = {}

    def _record(self, op, reads, writes):
        idx = len(self.ops)
        op.idx = idx
        deps = set()
        for k in reads:
            w = self.last_writer.get(k)
            if w is not None:
                deps.add(w)
        for k in writes:
            w = self.last_writer.get(k)
            if w is not None:
                deps.add(w)
            for r in self.readers.get(k, ()):
                deps.add(r)
        deps.discard(idx)
        op.deps = deps
        self.ops.append(op)
        for k in reads:
            self.readers.setdefault(k, []).append(idx)
        for k in writes:
            self.last_writer[k] = idx
            self.readers[k] = []
        return idx

    def op(self, eng, fn, reads=(), writes=()):
        return self._record(_Op(eng, fn, False), reads, writes)

    def dma(self, queue, out, in_, reads=(), writes=(), **kw):
        def fn(e, out=out, in_=in_, kw=kw):
            return e.dma_start(out=out, in_=in_, **kw)
        return self._record(_Op(queue, fn, True), reads, writes)

    def coll(self, fn, reads=(), writes=()):
        o = _Op("pool", fn, True)
        o.is_cc = True
        return self._record(o, reads, writes)

    def flush(self):
        nc = self.nc
        ops = self.ops
        for o in ops:
            nd = set()
            for d in o.deps:
                p = ops[d]
                if o.eng == "pe" and p.eng == "pe" and not p.is_dma and not o.is_dma:
                    continue
                nd.add(d)
                p.needs_inc = True
            o.deps = nd
        last_of = {}
        for o in ops:
            if not o.is_dma:
                last_of[o.eng] = o
        for o in last_of.values():
            o.needs_inc = True
        dlast = [None] * self.n_dma_sems
        for o in ops:
            if o.is_cc:
                self.cc_cnt += 1
                o.sem = self.cc_sem
                o.count = self.cc_cnt
            elif o.is_dma:
                s = self.rr % self.n_dma_sems
                self.rr += 1
                o.prev_same_sem = dlast[s]
                self.dcnt[s] += 16
                o.sem = self.dma_sems[s]
                o.count = self.dcnt[s]
                dlast[s] = o.idx
            elif o.needs_inc:
                self.cnt[o.eng] += 1
                o.sem = self.eng_sem[o.eng]
                o.count = self.cnt[o.eng]
        per_eng = {e: [o for o in ops if o.eng == e] for e in ENGS}
        final = [(self.dma_sems[s], self.dcnt[s]) for s in range(self.n_dma_sems) if self.dcnt[s] > 0]
        final += [(self.eng_sem[e], self.cnt[e]) for e in ENGS if self.cnt[e] > 0]
        if self.cc_cnt > 0:
            final.append((self.cc_sem, self.cc_cnt))
        self.nbar += 1
        nbar = self.nbar
        bar = self.bar_sem

        def run(e_name, eng):
            waited = {}
            for o in per_eng[e_name]:
                need = {}
                for d in o.deps:
                    p = ops[d]
                    if need.get(p.sem.num, (0, None))[0] < p.count:
                        need[p.sem.num] = (p.count, p.sem)
                if o.is_dma and o.prev_same_sem is not None:
                    p = ops[o.prev_same_sem]
                    if need.get(p.sem.num, (0, None))[0] < p.count:
                        need[p.sem.num] = (p.count, p.sem)
                for key, (c, s) in need.items():
                    if waited.get(key, 0) < c:
                        eng.wait_ge(s, c)
                        waited[key] = c
                ins = o.fn(eng)
                if o.is_cc:
                    ins.then_inc(o.sem)
                elif o.is_dma:
                    ins.then_inc(o.sem, 16)
                elif o.needs_inc:
                    ins.then_inc(o.sem, 1)
            if e_name == "sp":
                for s, c in final:
                    eng.wait_ge(s, c)
                eng.sem_inc(bar, 1)
            eng.wait_ge(bar, nbar)

        with nc.Block() as block:
            @block.tensor
            def _(e):
                run("pe", e)

            @block.scalar
            def _(e):
                run("act", e)

            @block.vector
            def _(e):
                run("dve", e)

            @block.gpsimd
            def _(e):
                run("pool", e)

            @block.sync
            def _(e):
                run("sp", e)
        self._reset()


class Ctx:
    pass


def bcast_rows(ap1d, n):
    return bass.AP(ap1d.tensor, ap1d.offset, [[0, 128], [1, n]])


def dbg_out(K, name, shape, dtype=F32):
    t = K.nc.dram_tensor(name, list(shape), dtype, kind="ExternalOutput")
    K.dbg[name] = t
    return t.ap()


def make_ident(K, P, ident):
    P.op("pool", lambda e: e.memset(ident[:], 0.0), writes=["ident"])
    P.op("pool", lambda e: e.affine_select(out=ident[:], in_=ident[:], pattern=[[-1, 128]],
                                           compare_op=ALU.not_equal, fill=1.0, base=0,
                                           channel_multiplier=1),
         reads=["ident"], writes=["ident"])


def phase0_adaln(K):
    nc, P = K.nc, K.P
    NQ = 3072
    with contextlib.ExitStack() as st:
        c_sb = st.enter_context(nc.sbuf_tensor("c_sb", [128, 16], F32))
        cact = st.enter_context(nc.sbuf_tensor("cact", [128, 16], F32))
        wst = [st.enter_context(nc.sbuf_tensor("wst%d" % i, [128, 16, 512], F32)) for i in range(2)]
        modrow = st.enter_context(nc.sbuf_tensor("modrow", [1, NQ], F32))
        brow = st.enter_context(nc.sbuf_tensor("brow", [1, NQ], F32))
        ps = [st.enter_context(nc.psum_tensor("ps0_%d" % i, [1, 512], F32)) for i in range(2)]
        P.dma("sp", c_sb[:], K.c_arr, writes=["c_sb"])
        P.dma("sp", brow[:], K.b_ada.rearrange("(o n) -> o n", o=1), writes=["brow"])
        P.op("act", lambda e: e.activation(out=cact[:], in_=c_sb[:], func=AF.Silu),
             reads=["c_sb"], writes=["cact"])
        wv = K.w_ada.rearrange("(k p) n -> p k n", p=128)
        for nt in range(NQ // 512):
            sl = nt % 2
            for hh in range(2):
                P.dma("sp", wst[sl][:, hh * 8:(hh + 1) * 8, :],
                      wv[:, hh * 8:(hh + 1) * 8, nt * 512:(nt + 1) * 512],
                      writes=[("wst", sl, hh)])
            for k in range(16):
                P.op("pe", lambda e, k=k, sl=sl: e.matmul(ps[sl][:, :], lhsT=cact[:, k:k + 1],
                                                         rhs=wst[sl][:, k, :], start=(k == 0), stop=(k == 15)),
                     reads=["cact", ("wst", sl, k // 8)], writes=[("ps0", sl)])
            P.op("dve", lambda e, nt=nt, sl=sl: e.tensor_tensor(
                out=modrow[0:1, nt * 512:(nt + 1) * 512], in0=ps[sl][:, :],
                in1=brow[0:1, nt * 512:(nt + 1) * 512], op=ALU.add),
                reads=[("ps0", sl), "brow"], writes=[("modrow", nt)])
        P.dma("sp", K.modq_d, modrow[:],
              reads=[("modrow", nt) for nt in range(NQ // 512)], writes=["modq_d"])
        P.flush()
    P.coll(lambda e: e.collective_compute("AllGather", ALU.bypass, replica_groups=[[0, 1, 2, 3], [4, 5, 6, 7]],
                                          ins=[K.modq_d.opt()], outs=[K.mod4_d.opt()]), reads=["modq_d"], writes=["mod4"])
    P.flush()


def load_mod_rows(K, P, tile, which, gain_ap=None, key=None):
    src = K.mod_d[which * D:(which + 1) * D]
    P.dma("sp", tile[:], bcast_rows(src, D), writes=[key])


def bc(ap, shape):
    return ap.to_broadcast(list(shape))


def load_weight_bf16(K, P, st_tiles, dst, c_dst, src2d, ncols, tag, defer=None):
    wv = src2d.rearrange("(k p) n -> p k n", p=128)
    nk = wv.shape[1]
    engs = ["pool", "dve", "act"]
    for c0 in range(0, ncols, 512):
        n = min(512, ncols - c0)
        for k0 in range(0, nk, 4):
            kn = min(4, nk - k0)
            if defer is not None:
                defer.append(lambda c0=c0, n=n, k0=k0, kn=kn: _load_piece(K, P, st_tiles, dst, c_dst, wv, tag, engs, c0, n, k0, kn))
                continue
            _load_piece(K, P, st_tiles, dst, c_dst, wv, tag, engs, c0, n, k0, kn)
    return [(tag, c0, k0) for c0 in range(0, ncols, 512) for k0 in range(0, nk, 4)]


def _load_piece(K, P, st_tiles, dst, c_dst, wv, tag, engs, c0, n, k0, kn):
    if True:
        if True:
            i = K.wcnt
            K.wcnt += 1
            sl = i % len(st_tiles)
            stg = st_tiles[sl]
            P.dma("sp", stg[:, 0:kn, 0:n], wv[:, k0:k0 + kn, c0:c0 + n], writes=[("wstg", sl)])
            eng = engs[i % 3]
            o = dst[:, k0:k0 + kn, c_dst + c0:c_dst + c0 + n]
            if eng == "act":
                P.op("act", lambda e, o=o, stg=stg, kn=kn, n=n: e.copy(out=o, in_=stg[:, 0:kn, 0:n]),
                     reads=[("wstg", sl)], writes=[(tag, c0, k0)])
            else:
                P.op(eng, lambda e, o=o, stg=stg, kn=kn, n=n: e.tensor_copy(out=o, in_=stg[:, 0:kn, 0:n]),
                     reads=[("wstg", sl)], writes=[(tag, c0, k0)])


def rope_tables(K, P, st, pos_arr, ntile, invf_att, invf_idx, tag):
    nc = K.nc
    posi = st.enter_context(nc.sbuf_tensor(tag + "posi", [128, ntile], I32))
    posf = st.enter_context(nc.sbuf_tensor(tag + "posf", [128, ntile], F32))
    iva = st.enter_context(nc.sbuf_tensor(tag + "iva", [128, 16], F32))
    ivi = st.enter_context(nc.sbuf_tensor(tag + "ivi", [128, 8], F32))
    P.dma("sp", posi[:], pos_arr, writes=[tag + "posi"])
    P.dma("sp", iva[:], invf_att, writes=[tag + "iva"])
    P.dma("sp", ivi[:], invf_idx, writes=[tag + "ivi"])
    P.op("dve", lambda e: e.tensor_copy(out=posf[:], in_=posi[:]), reads=[tag + "posi"], writes=[tag + "posf"])
    out = {}
    for nm, iv, h in (("a", iva, 16), ("i", ivi, 8)):
        u = st.enter_context(nc.sbuf_tensor(tag + "u" + nm, [128, ntile, h], F32))
        ui = st.enter_context(nc.sbuf_tensor(tag + "ui" + nm, [128, ntile, h], I32))
        uf = st.enter_context(nc.sbuf_tensor(tag + "uf" + nm, [128, ntile, h], F32))
        for fn, off in (("sin", 0.0), ("cos", 0.25)):
            tb = st.enter_context(nc.sbuf_tensor(tag + fn + nm, [128, ntile, h], F32))
            kk = tag + fn + nm
            P.op("dve", lambda e, u=u, iv=iv, h=h: e.tensor_tensor(
                out=u[:], in0=bc(posf[:].unsqueeze(2), [128, ntile, h]),
                in1=bc(iv[:].unsqueeze(1), [128, ntile, h]), op=ALU.mult),
                reads=[tag + "posf", tag + "iv" + nm], writes=[tag + "U" + nm])
            P.op("dve", lambda e, u=u, off=off: e.tensor_scalar(
                out=u[:], in0=u[:], scalar1=float(1.0 / (2 * np.pi)), scalar2=off, op0=ALU.mult, op1=ALU.add),
                reads=[tag + "U" + nm], writes=[tag + "U" + nm])
            P.op("dve", lambda e, u=u, ui=ui: e.tensor_copy(out=ui[:], in_=u[:]), reads=[tag + "U" + nm], writes=[tag + "UI" + nm])
            P.op("dve", lambda e, uf=uf, ui=ui: e.tensor_copy(out=uf[:], in_=ui[:]), reads=[tag + "UI" + nm], writes=[tag + "UF" + nm])
            P.op("dve", lambda e, u=u, uf=uf: e.tensor_tensor(out=u[:], in0=u[:], in1=uf[:], op=ALU.subtract),
                 reads=[tag + "U" + nm, tag + "UF" + nm], writes=[tag + "U" + nm])
            P.op("dve", lambda e, u=u: e.tensor_scalar(out=u[:], in0=u[:], scalar1=-0.5, scalar2=0.5,
                                                        op0=ALU.max, op1=ALU.min),
                 reads=[tag + "U" + nm], writes=[tag + "U" + nm])
            P.op("act", lambda e, u=u, tb=tb: e.activation(out=tb[:], in_=u[:], func=AF.Sin,
                                                            scale=float(2 * np.pi)),
                 reads=[tag + "U" + nm], writes=[kk])
            out[fn + nm] = (tb, kk)
    return out


def apply_rope(P, eng, x4, cos, sin, t, half, tmp, rk, wk, sfx=""):
    ctb, ck = cos
    stb, sk = sin
    H = x4.shape[1]
    x1 = x4[:, :, 0:half]
    x2 = x4[:, :, half:2 * half]
    cb = bc(ctb[:, t, :].unsqueeze(1), [128, H, half])
    sb = bc(stb[:, t, :].unsqueeze(1), [128, H, half])
    a, b2, c, d = tmp
    P.op(eng, lambda e: e.tensor_tensor(out=a[:, 0:H, 0:half], in0=x1, in1=cb, op=ALU.mult), reads=rk + [ck], writes=["rtmpA" + sfx])
    P.op(eng, lambda e: e.tensor_tensor(out=b2[:, 0:H, 0:half], in0=x2, in1=sb, op=ALU.mult), reads=rk + [sk], writes=["rtmpB" + sfx])
    P.op(eng, lambda e: e.tensor_tensor(out=c[:, 0:H, 0:half], in0=x2, in1=cb, op=ALU.mult), reads=rk + [ck], writes=["rtmpC" + sfx])
    P.op(eng, lambda e: e.tensor_tensor(out=d[:, 0:H, 0:half], in0=x1, in1=sb, op=ALU.mult), reads=rk + [sk], writes=["rtmpD" + sfx])
    P.op(eng, lambda e: e.tensor_tensor(out=x1, in0=a[:, 0:H, 0:half], in1=b2[:, 0:H, 0:half], op=ALU.subtract),
         reads=["rtmpA" + sfx, "rtmpB" + sfx, "rtmpC" + sfx, "rtmpD" + sfx] + rk, writes=rk)
    P.op(eng, lambda e: e.tensor_tensor(out=x2, in0=c[:, 0:H, 0:half], in1=d[:, 0:H, 0:half], op=ALU.add),
         reads=["rtmpC" + sfx, "rtmpD" + sfx] + rk, writes=rk)


def head_rmsnorm(P, x3, gain, sq, ssum, rk, wk, gk=None, sqk=None):
    P.op("pool", lambda e: e.tensor_tensor(out=sq[:], in0=x3, in1=x3, op=ALU.mult), reads=rk, writes=[sqk or (wk + "sq")])
    P.op("dve", lambda e: e.tensor_reduce(out=ssum[:, 0:8], in_=sq[:], axis=AX.X, op=ALU.add),
         reads=[sqk or (wk + "sq")], writes=[wk + "s0"])
    P.op("dve", lambda e: e.tensor_scalar(out=ssum[:, 8:16], in0=ssum[:, 0:8], scalar1=1.0 / 128, scalar2=1e-6,
                                           op0=ALU.mult, op1=ALU.add), reads=[wk + "s0"], writes=[wk + "s1"])
    P.op("act", lambda e: e.activation(out=ssum[:, 16:24], in_=ssum[:, 8:16], func=AF.Sqrt),
         reads=[wk + "s1"], writes=[wk + "s2"])
    P.op("dve", lambda e: e.reciprocal(out=ssum[:, 24:32], in_=ssum[:, 16:24]), reads=[wk + "s2"], writes=[wk + "s3"])
    P.op("dve", lambda e: e.tensor_tensor(out=x3, in0=x3, in1=bc(ssum[:, 24:32].unsqueeze(2), [128, 8, 128]),
                                           op=ALU.mult), reads=rk + [wk + "s3"], writes=rk)
    P.op("pool", lambda e: e.tensor_tensor(out=x3, in0=x3, in1=bc(gain[:].unsqueeze(1), [128, 8, 128]),
                                            op=ALU.mult), reads=rk + [gk or ("gain" + wk)], writes=rk)


def norm_load(K, P, T, x_src, t):
    xs = t % 2
    P.dma("sp", T["xt"][xs][:], x_src[t * 128:(t + 1) * 128, :], writes=[("xt", xs)])


def norm_chain(K, P, T, x_src, t, G1, SH1, load=True, xg=None):
    xs = t % 2
    xt, hb, ss, junk = T["xt"], T["hb"], T["ss"], T["junk"]
    if load:
        norm_load(K, P, T, x_src, t)
    if xg is not None:
        xg_ap, xg_k = xg[xs]
        P.op("pool", lambda e: e.tensor_tensor(out=xg_ap, in0=xt[xs][:], in1=G1[:], op=ALU.mult),
             reads=[("xt", xs), "G"], writes=[xg_k])
    P.op("act", lambda e: e.activation(out=junk[:], in_=xt[xs][:], func=AF.Square, accum_out=ss[:, 0:1]),
         reads=[("xt", xs)], writes=["junk", "ss0"])
    P.op("dve", lambda e: e.tensor_scalar(out=ss[:, 1:2], in0=ss[:, 0:1], scalar1=1.0 / D, scalar2=1e-6,
                                           op0=ALU.mult, op1=ALU.add), reads=["ss0"], writes=["ss1"])
    P.op("act", lambda e: e.activation(out=ss[:, 2:3], in_=ss[:, 1:2], func=AF.Sqrt), reads=["ss1"], writes=["ss2"])
    P.op("dve", lambda e: e.reciprocal(out=ss[:, 3:4], in_=ss[:, 2:3]), reads=["ss2"], writes=["ss3"])
    if xg is not None:
        P.op("dve", lambda e: e.scalar_tensor_tensor(out=hb[xs][:], in0=xg_ap, scalar=ss[:, 3:4], in1=SH1[:],
                                                      op0=ALU.mult, op1=ALU.add),
             reads=[xg_k, "ss3", "SH"], writes=[("hb", xs)])
    else:
        P.op("dve", lambda e: e.scalar_tensor_tensor(out=xt[xs][:], in0=xt[xs][:], scalar=ss[:, 3:4], in1=G1[:],
                                                      op0=ALU.mult, op1=ALU.mult),
             reads=[("xt", xs), "ss3", "G"], writes=[("xt", xs)])
        P.op("pool", lambda e: e.tensor_tensor(out=hb[xs][:], in0=xt[xs][:], in1=SH1[:], op=ALU.add),
             reads=[("xt", xs), "SH"], writes=[("hb", xs)])


def norm_pe(K, P, T, t, ident, blk_hT, ti, hname="hT"):
    xs = t % 2
    hb, pT = T["hb"], T["pT"]
    for half in range(2):
        for kk in range(8):
            k = half * 8 + kk
            P.op("pe", lambda e, k=k, kk=kk, half=half: e.transpose(
                out=pT[half][:, kk, :], in_=hb[xs][:, k * 128:(k + 1) * 128], identity=ident[:]),
                reads=[("hb", xs), "ident"], writes=[("pT", half)])
        o = blk_hT[:, half * 8:(half + 1) * 8, ti * 128:(ti + 1) * 128]
        if half == 0:
            P.op("act", lambda e, o=o, half=half: e.copy(out=o, in_=pT[half][:]),
                 reads=[("pT", half)], writes=[(hname, ti, half)])
        else:
            P.op("dve", lambda e, o=o, half=half: e.tensor_copy(out=o, in_=pT[half][:]),
                 reads=[("pT", half)], writes=[(hname, ti, half)])


def norm_block(K, P, T, x_src, t, G1, SH1, ident, blk_hT, ti, load=True, hname="hT"):
    norm_chain(K, P, T, x_src, t, G1, SH1, load=load)
    norm_pe(K, P, T, t, ident, blk_hT, ti, hname=hname)


def norm_tiles_alloc(K, st, tag):
    nc = K.nc
    T = {}
    T["xt"] = [st.enter_context(nc.sbuf_tensor(tag + "xt%d" % i, [128, D], F32)) for i in range(2)]
    T["hb"] = [st.enter_context(nc.sbuf_tensor(tag + "hb%d" % i, [128, D], BF16)) for i in range(2)]
    T["ss"] = st.enter_context(nc.sbuf_tensor(tag + "ss", [128, 4], F32))
    T["junk"] = st.enter_context(nc.sbuf_tensor(tag + "junk", [128, D], BF16))
    T["pT"] = [st.enter_context(nc.psum_tensor(tag + "pT%d" % i, [128, 8, 128], BF16)) for i in range(2)]
    return T


def load_G_SH(K, P, st, which_sh, which_sc, gain_vec, tag):
    nc = K.nc
    G = st.enter_context(nc.sbuf_tensor(tag + "G", [128, D], F32))
    SH = st.enter_context(nc.sbuf_tensor(tag + "SH", [128, D], F32))
    gtmp = st.enter_context(nc.sbuf_tensor(tag + "gtmp", [128, D], F32))
    P.dma("sp", SH[:], bcast_rows(K.mod_d[which_sh * D:(which_sh + 1) * D], D), writes=["SH"])
    P.dma("sp", G[:], bcast_rows(K.mod_d[which_sc * D:(which_sc + 1) * D], D), writes=["G"])
    P.dma("sp", gtmp[:], bcast_rows(gain_vec, D), writes=["gtmp"])
    P.op("dve", lambda e: e.scalar_tensor_tensor(out=G[:], in0=G[:], scalar=1.0, in1=gtmp[:],
                                                  op0=ALU.add, op1=ALU.mult), reads=["G", "gtmp"], writes=["G"])
    K.last_gtmp = gtmp
    return G, SH


def phase1_kv(K):
    nc, P = K.nc, K.P
    with contextlib.ExitStack() as st:
        ident = st.enter_context(nc.sbuf_tensor("ident", [128, 128], BF16))
        make_ident(K, P, ident)
        G1, SH1 = load_G_SH(K, P, st, 0, 1, K.norm1_g, "p1")
        T = norm_tiles_alloc(K, st, "p1")
        hT = [st.enter_context(nc.sbuf_tensor("hT%d" % i, [128, 16, 512], BF16)) for i in range(2)]
        W = st.enter_context(nc.sbuf_tensor("Wkv", [128, 16, 2112], BF16))
        stg = [st.enter_context(nc.sbuf_tensor("wstg%d" % i, [128, 4, 512], F32)) for i in range(2)]
        wk_k = load_weight_bf16(K, P, stg, W, 0, K.w_in[:, 1024:2048], 1024, "Wk")
        wk_v = load_weight_bf16(K, P, stg, W, 1024, K.w_in[:, 2048:3072], 1024, "Wv")
        wk_i = load_weight_bf16(K, P, stg, W, 2048, K.w_in[:, 4096:4160], 64, "Wi")
        rt = rope_tables(K, P, st, K.pos_full, 32, K.invf_att, K.invf_idx, "rf")
        gain = st.enter_context(nc.sbuf_tensor("kgain", [128, 128], F32))
        P.dma("sp", gain[:], bcast_rows(K.k_norm_g, 128), writes=["gainK"])
        def two(name, shape, dt):
            return [st.enter_context(nc.sbuf_tensor(name + str(i), shape, dt)) for i in range(2)]
        ksb2 = two("ksb", [128, 8, 128], F32)
        kbf2 = two("kbf", [128, 8, 128], BF16)
        sq2 = [st.enter_context(nc.sbuf_tensor("sq", [128, 8, 128], F32))] * 2
        ssum2 = two("ssum", [128, 32], F32)
        rtmp2 = [[st.enter_context(nc.sbuf_tensor("rtmp%d" % i, [128, 8, 16], F32)) for i in range(4)]] * 2
        vsb2 = two("vsb", [128, 8, 129], BF16)
        iksb2 = two("iksb", [128, 1, 64], F32)
        ikbf2 = two("ikbf", [128, 64], BF16)
        kTs2 = [st.enter_context(nc.sbuf_tensor("kTs", [128, 8, 128], BF16))] * 2
        ikTs2 = two("ikTs", [64, 128], BF16)
        pm = [st.enter_context(nc.psum_tensor("pm%d" % i, [128, 512], F32)) for i in range(3)]
        pk = st.enter_context(nc.psum_tensor("pk", [128, 8, 128], BF16))
        pk2 = st.enter_context(nc.psum_tensor("pk2", [64, 128], BF16))
        for s_ in range(2):
            P.op("pool", lambda e, s_=s_: e.memset(vsb2[s_][:], 1.0), writes=["vsb%d" % s_])
        norm_load(K, P, T, K.x_full, 0)

        xg = [(K.last_gtmp[:], "gtmp"), (stg[0][:].rearrange("p a b -> p (a b)"), ("wstg", 0))]

        def norm_tile_chain(blk, ti):
            tt_ = blk * 4 + ti
            if tt_ + 1 < 32:
                norm_load(K, P, T, K.x_full, tt_ + 1)
            norm_chain(K, P, T, K.x_full, tt_, G1, SH1, load=False, xg=xg)

        def norm_tile_pe(blk, ti):
            norm_pe(K, P, T, blk * 4 + ti, ident, hT[blk % 2], ti, hname=("hT", blk % 2))

        def norm_tile(blk, ti):
            norm_tile_chain(blk, ti)
            norm_tile_pe(blk, ti)

        def store_hT(blk):
            hs = blk % 2
            hkeys = [(("hT", hs), ti, half) for ti in range(4) for half in range(2)]
            P.dma("sp", K.hT_d.rearrange("k p t -> p k t")[:, :, blk * 512:(blk + 1) * 512], hT[hs][:],
                  reads=hkeys, writes=[("hT_d", blk)])

        def bufs(t):
            u = t % 2
            return (str(u), ksb2[u], kbf2[u], sq2[u], ssum2[u], rtmp2[u], vsb2[u], iksb2[u], ikbf2[u], kTs2[u], ikTs2[u])

        def mm_tile(blk, ti):
            t = blk * 4 + ti
            hs = blk % 2
            hk = [(("hT", hs), ti, 0), (("hT", hs), ti, 1)]
            us, ksb, kbf, sq, ssum, rtmp, vsb, iksb, ikbf, kTs, ikTs = bufs(t)
            for gi, (c0, n, wkeys) in enumerate([(0, 512, wk_k), (512, 512, wk_k), (1024, 512, wk_v),
                                                 (1536, 512, wk_v), (2048, 64, wk_i)]):
                pb = pm[gi % 3]
                wtag, wc0 = [("Wk", 0), ("Wk", 512), ("Wv", 0), ("Wv", 512), ("Wi", 0)][gi]
                for k in range(16):
                    wkeys = [(wtag, wc0, 4 * (k // 4))]
                    P.op("pe", lambda e, pb=pb, k=k, c0=c0, n=n, ti=ti, hs=hs: e.matmul(
                        pb[:, 0:n], lhsT=hT[hs][:, k, ti * 128:(ti + 1) * 128], rhs=W[:, k, c0:c0 + n],
                        start=(k == 0), stop=(k == 15)), reads=hk + wkeys, writes=[("pm", gi % 3)])
                if gi < 2:
                    P.op("act", lambda e, pb=pb, gi=gi, ksb=ksb: e.copy(out=ksb[:, gi * 4:(gi + 1) * 4, :], in_=pb[:, 0:512]),
                         reads=[("pm", gi % 3)], writes=["ksb" + us])
                elif gi < 4:
                    g2 = gi - 2
                    P.op("act", lambda e, pb=pb, g2=g2, vsb=vsb: e.copy(out=vsb[:, g2 * 4:(g2 + 1) * 4, 0:128], in_=pb[:, 0:512]),
                         reads=[("pm", gi % 3)], writes=["vsb" + us])
                else:
                    P.op("act", lambda e, pb=pb, iksb=iksb: e.copy(out=iksb[:, 0, :], in_=pb[:, 0:64]),
                         reads=[("pm", gi % 3)], writes=["iksb" + us])
            P.dma("sp", K.v_d[t * 128:(t + 1) * 128, :], vsb[:].rearrange("p h d -> p (h d)"),
                  reads=["vsb" + us], writes=[("v_d", t)])

        def post1(blk, ti):
            t = blk * 4 + ti
            us, ksb, kbf, sq, ssum, rtmp, vsb, iksb, ikbf, kTs, ikTs = bufs(t)
            head_rmsnorm(P, ksb[:], gain, sq, ssum, ["ksb" + us], "K" + us, gk="gainK", sqk="Ksq")
            apply_rope(P, "dve", ksb[:], rt["cosa"], rt["sina"], t, 16, rtmp, ["ksb" + us], "rK")
            P.op("act", lambda e, kbf=kbf, ksb=ksb: e.copy(out=kbf[:], in_=ksb[:]), reads=["ksb" + us], writes=["kbf" + us])
            apply_rope(P, "pool", iksb[:], rt["cosi"], rt["sini"], t, 8, rtmp, ["iksb" + us], "rI")
            P.op("act", lambda e, ikbf=ikbf, iksb=iksb: e.copy(out=ikbf[:], in_=iksb[:, 0, :]), reads=["iksb" + us], writes=["ikbf" + us])

        def post2(blk, ti):
            t = blk * 4 + ti
            us, ksb, kbf, sq, ssum, rtmp, vsb, iksb, ikbf, kTs, ikTs = bufs(t)
            for h in range(8):
                P.op("pe", lambda e, h=h, kbf=kbf: e.transpose(out=pk[:, h, :], in_=kbf[:, h, :], identity=ident[:]),
                     reads=["kbf" + us, "ident"], writes=["pk"])
            P.op("dve", lambda e, kTs=kTs: e.tensor_copy(out=kTs[:], in_=pk[:]), reads=["pk"], writes=["kTs"])
            P.dma("sp", K.kT_d.rearrange("h p t -> p h t")[:, :, t * 128:(t + 1) * 128], kTs[:],
                  reads=["kTs"], writes=[("kT_d", t)])
            P.op("pe", lambda e, ikbf=ikbf: e.transpose(out=pk2[:, :], in_=ikbf[:], identity=ident[:]),
                 reads=["ikbf" + us, "ident"], writes=["pk2"])
            P.op("dve", lambda e, ikTs=ikTs: e.tensor_copy(out=ikTs[:], in_=pk2[:, :]), reads=["pk2"], writes=["ikTs" + us])
            P.dma("sp", K.ikT_d[:, t * 128:(t + 1) * 128], ikTs[:], reads=["ikTs" + us], writes=[("ikT_d", t)])

        for ti in range(4):
            norm_tile(0, ti)
        store_hT(0)
        prev = None
        for blk in range(8):
            for ti in range(4):
                if blk + 1 < 8:
                    norm_tile_chain(blk + 1, ti)
                mm_tile(blk, ti)
                if prev is not None:
                    post2(*prev)
                if blk + 1 < 8:
                    norm_tile_pe(blk + 1, ti)
                post1(blk, ti)
                prev = (blk, ti)
            if blk + 1 < 8:
                store_hT(blk + 1)
        post2(*prev)
        P.flush()

RW0 = 4176
NRW = 1216
RW_GROUPS = [(i * 128, 128) for i in range(6)] + [(768, 96), (864, 96), (960, 128), (1088, 128)]


def phase1b_rwkv_proj(K):
    nc, P = K.nc, K.P
    with contextlib.ExitStack() as st:
        W = st.enter_context(nc.sbuf_tensor("Wr", [128, 16, NRW], BF16))
        stg = [st.enter_context(nc.sbuf_tensor("wstgb%d" % i, [128, 4, 512], F32)) for i in range(2)]
        hT = [st.enter_context(nc.sbuf_tensor("hTb%d" % i, [128, 16, 512], BF16)) for i in range(2)]
        ost = [st.enter_context(nc.sbuf_tensor("ost%d" % i, [128, 512], F32)) for i in range(4)]
        pm = [st.enter_context(nc.psum_tensor("pmb%d" % i, [128, 512], F32)) for i in range(4)]
        wkeys = load_weight_bf16(K, P, stg, W, 0, K.w_in_rw, NRW, "Wr")
        cnt = 0
        for blk in range(8):
            hs = blk % 2
            P.dma("sp", hT[hs][:], K.hT_d.rearrange("k p t -> p k t")[:, :, blk * 512:(blk + 1) * 512],
                  writes=[("hTb", hs)])
            for (r0, m) in RW_GROUPS:
                s4 = cnt % 4
                cnt += 1
                for k in range(16):
                    wkeys = [("Wr", c0_, 4 * (k // 4)) for c0_ in sorted({512 * (r0 // 512), 512 * ((r0 + m - 1) // 512)})]
                    P.op("pe", lambda e, k=k, r0=r0, m=m, hs=hs, s4=s4: e.matmul(
                        pm[s4][0:m, :], lhsT=W[:, k, r0:r0 + m], rhs=hT[hs][:, k, :],
                        start=(k == 0), stop=(k == 15)), reads=[("hTb", hs)] + wkeys, writes=[("pmb", s4)])
                if cnt % 2 == 0:
                    P.op("act", lambda e, m=m, s4=s4: e.copy(out=ost[s4][0:m, :], in_=pm[s4][0:m, :]),
                         reads=[("pmb", s4)], writes=[("ost", s4)])
                else:
                    P.op("dve", lambda e, m=m, s4=s4: e.tensor_copy(out=ost[s4][0:m, :], in_=pm[s4][0:m, :]),
                         reads=[("pmb", s4)], writes=[("ost", s4)])
                P.dma("sp", K.yT_d[r0:r0 + m, blk * 512:(blk + 1) * 512], ost[s4][0:m, :],
                      reads=[("ost", s4)], writes=[("yT_d", r0, blk)])
        P.flush()


def phase2_own_proj(K):
    nc, P = K.nc, K.P
    with contextlib.ExitStack() as st:
        ident = st.enter_context(nc.sbuf_tensor("ident2", [128, 128], BF16))
        make_ident(K, P, ident)
        G1, SH1 = load_G_SH(K, P, st, 0, 1, K.norm1_g, "p2")
        T = norm_tiles_alloc(K, st, "p2")
        hT = [st.enter_context(nc.sbuf_tensor("hTo%d" % i, [128, 16, 512], BF16)) for i in range(2)]
        W = st.enter_context(nc.sbuf_tensor("Wq", [128, 16, 2064], BF16))
        stg = [st.enter_context(nc.sbuf_tensor("wstgq%d" % i, [128, 4, 512], F32)) for i in range(2)]
        wk_q = load_weight_bf16(K, P, stg, W, 0, K.w_in[:, 0:1024], 1024, "Wq")
        wk_iq = load_weight_bf16(K, P, stg, W, 1024, K.w_in[:, 3072:4096], 1024, "Wiq")
        wk_iw = load_weight_bf16(K, P, stg, W, 2048, K.w_in[:, 4160:4176], 16, "Wiw")
        rt = rope_tables(K, P, st, K.pos_own, 8, K.invf_att, K.invf_idx, "ro")
        gain = st.enter_context(nc.sbuf_tensor("qgain", [128, 128], F32))
        P.dma("sp", gain[:], bcast_rows(K.q_norm_g, 128), writes=["gainQ"])
        qsb = st.enter_context(nc.sbuf_tensor("qsb", [128, 8, 128], F32))
        qbf = st.enter_context(nc.sbuf_tensor("qbf", [128, 8, 128], BF16))
        sq = st.enter_context(nc.sbuf_tensor("sq2", [128, 8, 128], F32))
        ssum = st.enter_context(nc.sbuf_tensor("ssum2", [128, 32], F32))
        rtmp = [st.enter_context(nc.sbuf_tensor("rtmpq%d" % i, [128, 16, 16], F32)) for i in range(4)]
        iqsb = st.enter_context(nc.sbuf_tensor("iqsb", [128, 16, 64], F32))
        iqbf = st.enter_context(nc.sbuf_tensor("iqbf", [128, 16, 64], BF16))
        iwsb = st.enter_context(nc.sbuf_tensor("iwsb", [128, 16], F32))
        qTs = st.enter_context(nc.sbuf_tensor("qTs", [128, 8, 128], BF16))
        iqTs = st.enter_context(nc.sbuf_tensor("iqTs", [64, 128, 16], BF16))
        pm = [st.enter_context(nc.psum_tensor("pmq%d" % i, [128, 512], F32)) for i in range(3)]
        pk = st.enter_context(nc.psum_tensor("pkq", [128, 8, 128], BF16))
        xg = [(K.last_gtmp[:], "gtmp"), (stg[0][:].rearrange("p a b -> p (a b)"), ("wstg", 0))]

        def nchain(blk, ti):
            tt_ = blk * 4 + ti
            if tt_ + 1 < 8:
                norm_load(K, P, T, K.x_own, tt_ + 1)
            norm_chain(K, P, T, K.x_own, tt_, G1, SH1, load=False, xg=xg)

        def npe(blk, ti):
            norm_pe(K, P, T, blk * 4 + ti, ident, hT[blk % 2], ti, hname=("hT", blk % 2))
        norm_load(K, P, T, K.x_own, 0)
        for ti in range(4):
            nchain(0, ti)
            npe(0, ti)
        for blk in range(2):
            hs = blk % 2
            for ti in range(4):
                t = blk * 4 + ti
                hk = [(("hT", hs), ti, 0), (("hT", hs), ti, 1)]
                if blk + 1 < 2:
                    nchain(blk + 1, ti)
                for gi, (c0, n, wkeys) in enumerate([(0, 512, wk_q), (512, 512, wk_q), (1024, 512, wk_iq),
                                                     (1536, 512, wk_iq), (2048, 16, wk_iw)]):
                    pb = pm[gi % 3]
                    wtag, wc0 = [("Wq", 0), ("Wq", 512), ("Wiq", 0), ("Wiq", 512), ("Wiw", 0)][gi]
                    for k in range(16):
                        wkeys = [(wtag, wc0, 4 * (k // 4))]
                        P.op("pe", lambda e, pb=pb, k=k, c0=c0, n=n, ti=ti, hs=hs: e.matmul(
                            pb[:, 0:n], lhsT=hT[hs][:, k, ti * 128:(ti + 1) * 128], rhs=W[:, k, c0:c0 + n],
                            start=(k == 0), stop=(k == 15)), reads=hk + wkeys, writes=[("pmq", gi % 3)])
                    if gi < 2:
                        P.op("act", lambda e, pb=pb, gi=gi: e.copy(out=qsb[:, gi * 4:(gi + 1) * 4, :], in_=pb[:, 0:512]),
                             reads=[("pmq", gi % 3)], writes=["qsb"])
                    elif gi < 4:
                        g2 = gi - 2
                        P.op("act", lambda e, pb=pb, g2=g2: e.copy(out=iqsb[:, g2 * 8:(g2 + 1) * 8, :], in_=pb[:, 0:512]),
                             reads=[("pmq", gi % 3)], writes=["iqsb"])
                    else:
                        P.op("act", lambda e, pb=pb: e.activation(out=iwsb[:], in_=pb[:, 0:16], func=AF.Copy, scale=0.25),
                             reads=[("pmq", gi % 3)], writes=["iwsb"])
                P.dma("sp", K.iw_d[t * 128:(t + 1) * 128, :], iwsb[:], reads=["iwsb"], writes=[("iw_d", t)])
                head_rmsnorm(P, qsb[:], gain, sq, ssum, ["qsb"], "Q")
                apply_rope(P, "dve", qsb[:], rt["cosa"], rt["sina"], t, 16, rtmp, ["qsb"], "rQ")
                P.op("act", lambda e: e.copy(out=qbf[:], in_=qsb[:]), reads=["qsb"], writes=["qbf"])
                for h in range(8):
                    P.op("pe", lambda e, h=h: e.transpose(out=pk[:, h, :], in_=qbf[:, h, :], identity=ident[:]),
                         reads=["qbf", "ident"], writes=["pkq"])
                P.op("dve", lambda e: e.tensor_copy(out=qTs[:], in_=pk[:]), reads=["pkq"], writes=["qTs"])
                P.dma("sp", K.qT_d.rearrange("h p t -> p h t")[:, :, t * 128:(t + 1) * 128], qTs[:],
                      reads=["qTs"], writes=[("qT_d", t)])
                apply_rope(P, "pool", iqsb[:], rt["cosi"], rt["sini"], t, 8, rtmp, ["iqsb"], "rIQ")
                P.op("act", lambda e: e.activation(out=iqbf[:], in_=iqsb[:], func=AF.Copy, scale=0.125),
                     reads=["iqsb"], writes=["iqbf"])
                for half in range(2):
                    for hh in range(8):
                        h = half * 8 + hh
                        P.op("pe", lambda e, h=h, hh=hh: e.transpose(out=pk[0:64, hh, :], in_=iqbf[:, h, :],
                                                                      identity=ident[:]),
                             reads=["iqbf", "ident"], writes=["pkq"])
                    P.op("dve", lambda e, half=half: e.tensor_copy(
                        out=iqTs[:, :, half * 8:(half + 1) * 8].rearrange("p t h -> p h t"), in_=pk[0:64, :, :]),
                         reads=["pkq"], writes=["iqTs"])
                P.dma("sp", K.iqT_d[:, t * 128:(t + 1) * 128, :], iqTs[:], reads=["iqTs"], writes=[("iqT_d", t)])
                if blk + 1 < 2:
                    npe(blk + 1, ti)
        P.flush()


NIT = 20
SLOT_NK = [4, 8, 12, 16, 20, 24, 28, 32]


def phase3_attention(K):
    nc, P = K.nc, K.P
    with contextlib.ExitStack() as st:
        def sb(name, shape, dt):
            return st.enter_context(nc.sbuf_tensor(name, shape, dt))
        ident = sb("ident3", [128, 128], BF16)
        kposf = sb("kposf", [128, 512], F32)
        bias = sb("cbias", [128, 512], F32)
        identf = kposf[:, 0:128]
        make_ident(K, P, ident)
        P.op("dve", lambda e: e.tensor_copy(out=identf, in_=ident[:]), reads=["ident"], writes=["kposf"])
        kT = sb("kTall", [128, 8, S], BF16)
        V = sb("Vall", [128, 32, 1032], BF16)
        ikT = sb("ikTall", [64, S], BF16)
        for h in range(8):
            P.dma("sp", kT[:, h, :], K.kT_d[h], writes=[("kT", h)])
        for q4 in range(4):
            P.dma("sp", V[:, q4 * 8:(q4 + 1) * 8, :],
                  K.v_d.rearrange("(t p) c -> p t c", p=128)[:, q4 * 8:(q4 + 1) * 8, :], writes=[("V", q4)])
        P.dma("sp", ikT[:], K.ikT_d, writes=["ikT"])
        kTk = [("kT", h) for h in range(8)]
        Vk = [("V", q4) for q4 in range(4)]
        Sel = sb("Sel", [128, 16, 128], BF16)
        pidx = sb("pidx", [128, 1], I32)
        pidf = sb("pidf", [128, 1], F32)
        score = sb("score", [128, S], F32)
        self_ = score[:, 0:2048].rearrange("p (g t) -> p g t", g=16)
        sk4 = [("score", q) for q in range(4)]
        P.op("pool", lambda e: e.iota(self_, pattern=[[-8, 16], [1, 128]], base=0, channel_multiplier=0, allow_small_or_imprecise_dtypes=True), writes=sk4)
        P.op("pool", lambda e: e.iota(pidx[:], pattern=[[0, 1]], base=0, channel_multiplier=1), writes=["pidx"])
        P.op("dve", lambda e: e.tensor_scalar(out=pidx[:], in0=pidx[:], scalar1=4, scalar2=None,
                                               op0=ALU.arith_shift_right), reads=["pidx"], writes=["pidx"])
        P.op("dve", lambda e: e.tensor_copy(out=pidf[:], in_=pidx[:]), reads=["pidx"], writes=["pidf"])
        P.op("dve", lambda e: e.tensor_scalar(out=Sel[:], in0=self_, scalar1=pidf[:, 0:1], scalar2=None,
                                               op0=ALU.is_equal), reads=sk4 + ["pidf"], writes=["Sel"])
        qpos = sb("qpos", [128, 8], F32)
        P.dma("sp", qpos[:], K.qpos_own, writes=["qpos"])
        iwg = bias[:, 0:128]
        wcol = sb("wcol", [128, 128], F32)
        P.dma("sp", iwg, K.iw_d.rearrange("(g t) h -> g (t h)", t=8), writes=["bias"])
        A = [st.enter_context(nc.psum_tensor("A%d" % i, [128, 512], F32)) for i in range(2)]
        B = [st.enter_context(nc.psum_tensor("B%d" % i, [128, 512], F32)) for i in range(2)]
        C = st.enter_context(nc.psum_tensor("C3", [128, 8, 128], BF16))
        P.op("pe", lambda e: e.transpose(out=A[0][:, 0:128], in_=iwg, identity=identf),
             reads=["bias", "kposf"], writes=[("A", 0)])
        P.op("dve", lambda e: e.tensor_copy(out=wcol[:], in_=A[0][:, 0:128]), reads=[("A", 0)], writes=["wcol"])
        mask01 = sb("mask01", [128, S], BF16)
        maskT = sb("maskT", [128, 32, 128], BF16)
        R = [sb("R%d" % i, [128, 512], BF16) for i in range(2)]
        pexp = [sb("pexp%d" % i, [128, 512], BF16) for i in range(2)]
        pmk = [sb("pmk%d" % i, [128, 512], BF16) for i in range(2)]
        iqTs = sb("iqTs3", [64, 128, 16], BF16)
        qTs = sb("qTs3", [128, 8, 128], BF16)
        att = sb("att", [128, 8, 128], BF16)
        attTs = sb("attTs", [128, 8, 128], BF16)
        c2 = sb("c2", [128, NIT], F32)
        steps = sb("steps", [128, NIT], F32)
        sm = sb("sm3", [128, 8], F32)
        for k in range(NIT):
            P.op("pool", lambda e, k=k: e.memset(c2[:, k:k + 1], float(2.0 ** -(k + 1))), writes=["c2"])
        maskT2 = [maskT, sb("maskTb", [128, 32, 128], BF16)]
        Wbd = sb("Wbd", [128, 16, 128], BF16)
        qTs2 = [qTs, sb("qTs3b", [128, 8, 128], BF16)]
        rcp2 = sb("rcp2", [128, 2], F32)

        def stageA(i):
            nk = SLOT_NK[i]
            nb = nk // 4
            P.dma("sp", iqTs[:], K.iqT_d[:, i * 128:(i + 1) * 128, :], writes=["iqTs"])
            P.dma("sp", qTs2[i % 2][:], K.qT_d.rearrange("h p t -> p h t")[:, :, i * 128:(i + 1) * 128], writes=[("qTs", i % 2)])
            isteps = [(sbk, g) for sbk in range(nb) for g in range(16)]
            P.op("pool", lambda e, i=i: e.tensor_tensor(out=Wbd[:], in0=Sel[:],
                                                         in1=wcol[:, i * 16:(i + 1) * 16].unsqueeze(2).to_broadcast([128, 16, 128]),
                                                         op=ALU.mult), reads=["Sel", "wcol"], writes=["Wbd"])

            def dots(si):
                sbk, g = isteps[si]
                a_ = si % 2
                lhsT = iqTs[:, g * 8:(g + 1) * 8, :].rearrange("p t h -> p (t h)")
                P.op("pe", lambda e, a_=a_, lhsT=lhsT, sbk=sbk: e.matmul(
                    A[a_][:, :], lhsT=lhsT, rhs=ikT[:, sbk * 512:(sbk + 1) * 512], start=True, stop=True),
                    reads=["iqTs", "ikT"], writes=[("A", a_)])
            dots(0)
            for si, (sbk, g) in enumerate(isteps):
                a_ = si % 2
                bsl = sbk % 2
                if si + 1 < len(isteps):
                    dots(si + 1)
                if si % 2 == 0:
                    P.op("act", lambda e, a_=a_: e.activation(out=R[a_][:], in_=A[a_][:, :], func=AF.Relu),
                         reads=[("A", a_)], writes=[("R", a_)])
                else:
                    P.op("dve", lambda e, a_=a_: e.tensor_scalar(out=R[a_][:], in0=A[a_][:, :], scalar1=0.0, scalar2=None,
                                                                  op0=ALU.max), reads=[("A", a_)], writes=[("R", a_)])
                P.op("pe", lambda e, a_=a_, g=g, bsl=bsl: e.matmul(
                    B[bsl][:, :], lhsT=Wbd[:, g, :], rhs=R[a_][:], start=(g == 0), stop=(g == 15)),
                    reads=[("R", a_), "Wbd"], writes=[("B", bsl)])
                if g == 15:
                    P.op("dve", lambda e, bsl=bsl, sbk=sbk: e.tensor_copy(out=score[:, sbk * 512:(sbk + 1) * 512], in_=B[bsl][:, :]),
                         reads=[("B", bsl)], writes=[("score", sbk)])

        def stageBdve(i):
            nk = SLOT_NK[i]
            nb = nk // 4
            L = nk * 128
            sck = [("score", sbk) for sbk in range(nb)]
            P.op("dve", lambda e, L=L: e.tensor_reduce(out=sm[:, 0:1], in_=score[:, 0:L], axis=AX.X, op=ALU.max,
                                                        apply_absolute_value=True), reads=sck, writes=["sm0"])
            P.op("pool", lambda e, nb=nb: e.iota(kposf[:], pattern=[[1, 512]], base=(nb - 1) * 512, channel_multiplier=0,
                                                 allow_small_or_imprecise_dtypes=True), writes=["kposf"])
            P.op("dve", lambda e, i=i: e.tensor_scalar(out=bias[:], in0=kposf[:], scalar1=qpos[:, i:i + 1],
                                                        scalar2=-1e30, op0=ALU.is_gt, op1=ALU.mult),
                 reads=["kposf", "qpos"], writes=["bias"])
            P.op("dve", lambda e, nb=nb: e.tensor_tensor(out=score[:, (nb - 1) * 512:nb * 512],
                                                          in0=score[:, (nb - 1) * 512:nb * 512], in1=bias[:], op=ALU.add),
                 reads=["bias", ("score", nb - 1), "sm0"], writes=[("score", nb - 1)])
            P.op("dve", lambda e: e.tensor_scalar(out=sm[:, 1:2], in0=sm[:, 0:1], scalar1=-1.0, scalar2=-1.0,
                                                   op0=ALU.mult, op1=ALU.add), reads=["sm0"], writes=["lo"])
            P.op("dve", lambda e: e.tensor_scalar(out=sm[:, 5:6], in0=sm[:, 0:1], scalar1=2.0, scalar2=2.0,
                                                   op0=ALU.mult, op1=ALU.add), reads=["sm0"], writes=["d0"])
            P.op("dve", lambda e: e.tensor_scalar(out=steps[:], in0=c2[:], scalar1=sm[:, 5:6], scalar2=None,
                                                   op0=ALU.mult), reads=["d0", "c2"], writes=["steps"])
            P.op("dve", lambda e: e.tensor_tensor(out=sm[:, 2:3], in0=sm[:, 1:2], in1=steps[:, 0:1], op=ALU.add),
                 reads=["lo", "steps"], writes=["mid"])
            for k in range(NIT):
                P.op("dve", lambda e, L=L: e.tensor_scalar(out=mask01[:, 0:L], in0=score[:, 0:L], scalar1=sm[:, 2:3],
                                                            scalar2=None, op0=ALU.is_ge, op1=ALU.add,
                                                            accum_out=sm[:, 3:4]),
                     reads=sck + ["mid"], writes=["mask01", "cnt"])
                P.op("dve", lambda e: e.tensor_scalar(out=sm[:, 4:5], in0=sm[:, 3:4], scalar1=255.5, scalar2=-0.5,
                                                       op0=ALU.is_ge, op1=ALU.add), reads=["cnt"], writes=["inc"])
                P.op("dve", lambda e, k=k: e.scalar_tensor_tensor(out=sm[:, 2:3], in0=steps[:, k:k + 1], scalar=sm[:, 4:5],
                                                                   in1=sm[:, 2:3], op0=ALU.mult, op1=ALU.add),
                     reads=["inc", "steps", "mid"], writes=["mid"])
            P.op("dve", lambda e: e.scalar_tensor_tensor(out=sm[:, 1:2], in0=steps[:, NIT - 1:NIT], scalar=-0.5, in1=sm[:, 2:3],
                                                          op0=ALU.mult, op1=ALU.add), reads=["steps", "mid"], writes=["lo"])
            P.op("dve", lambda e, L=L: e.tensor_scalar(out=mask01[:, 0:L], in0=score[:, 0:L], scalar1=sm[:, 1:2],
                                                        scalar2=None, op0=ALU.is_ge), reads=sck + ["lo"], writes=["mask01"])

        def stageBpe(i):
            nk = SLOT_NK[i]
            mT = maskT2[i % 2]
            for kt in range(nk):
                P.op("pe", lambda e, kt=kt: e.transpose(out=C[:, kt % 8, :], in_=mask01[:, kt * 128:(kt + 1) * 128],
                                                         identity=ident[:]), reads=["mask01", "ident"], writes=["C"])
                if kt % 8 == 7 or kt == nk - 1:
                    k0 = (kt // 8) * 8
                    n8 = kt - k0 + 1
                    P.op("dve", lambda e, k0=k0, n8=n8, mT=mT: e.tensor_copy(out=mT[:, k0:k0 + n8, :], in_=C[:, 0:n8, :]),
                         reads=["C"], writes=[("maskT", i % 2, k0 // 8)])

        def stageC(i):
            nk = SLOT_NK[i]
            nb = nk // 4
            mT = maskT2[i % 2]
            qT_ = qTs2[i % 2]
            mk = [("maskT", i % 2, q) for q in range((nk + 7) // 8)]
            asteps = [(h, kg) for h in range(8) for kg in range(nb)]

            def qk(si):
                h, kg = asteps[si]
                a_ = si % 2
                for j4 in range(4):
                    kt = kg * 4 + j4
                    P.op("pe", lambda e, a_=a_, j4=j4, kt=kt, h=h: e.matmul(
                        A[a_][:, j4 * 128:(j4 + 1) * 128], lhsT=kT[:, h, kt * 128:(kt + 1) * 128], rhs=qT_[:, h, :],
                        start=True, stop=True), reads=kTk + [("qTs", i % 2)], writes=[("A", a_)])
            qk(0)
            for si, (h, kg) in enumerate(asteps):
                a_ = si % 2
                bsl = h % 2
                if si + 1 < len(asteps):
                    qk(si + 1)
                P.op("act", lambda e, a_=a_: e.activation(out=pexp[a_][:], in_=A[a_][:, :], func=AF.Exp,
                                                           scale=float(128 ** -0.5)),
                     reads=[("A", a_)], writes=[("pexp", a_)])
                P.op("pool", lambda e, a_=a_, kg=kg: e.tensor_tensor(
                    out=pmk[a_][:], in0=pexp[a_][:], in1=mT[:, kg * 4:(kg + 1) * 4, :].rearrange("p a t -> p (a t)"),
                    op=ALU.mult), reads=[("pexp", a_)] + mk, writes=[("pmk", a_)])
                for j4 in range(4):
                    kt = kg * 4 + j4
                    P.op("pe", lambda e, a_=a_, j4=j4, kt=kt, h=h, bsl=bsl, kg=kg, nb=nb: e.matmul(
                        B[bsl][:, 0:129], lhsT=pmk[a_][:, j4 * 128:(j4 + 1) * 128], rhs=V[:, kt, h * 129:(h + 1) * 129],
                        start=(kg == 0 and j4 == 0), stop=(kg == nb - 1 and j4 == 3)),
                        reads=[("pmk", a_)] + Vk, writes=[("B", bsl)])
                if kg == nb - 1:
                    P.op("act", lambda e, bsl=bsl: e.activation(out=rcp2[:, 0:1], in_=B[bsl][:, 128:129], func=AF.Ln),
                         reads=[("B", bsl)], writes=["rcpa"])
                    P.op("act", lambda e: e.activation(out=rcp2[:, 1:2], in_=rcp2[:, 0:1], func=AF.Exp, scale=-1.0),
                         reads=["rcpa"], writes=["rcpb"])
                    P.op("act", lambda e, bsl=bsl, h=h: e.activation(out=att[:, h, :], in_=B[bsl][:, 0:128], func=AF.Copy,
                                                                      scale=rcp2[:, 1:2]),
                         reads=[("B", bsl), "rcpb"], writes=["att"])
            for h in range(8):
                P.op("pe", lambda e, h=h: e.transpose(out=C[:, h, :], in_=att[:, h, :], identity=ident[:]),
                     reads=["att", "ident"], writes=["C"])
            P.op("act", lambda e: e.copy(out=attTs[:], in_=C[:]), reads=["C"], writes=["attTs"])
            P.dma("sp", K.attT_d.rearrange("h p t -> p h t")[:, :, i * 128:(i + 1) * 128], attTs[:],
                  reads=["attTs"], writes=[("attT_d", i)])

        stageA(0)
        stageBdve(0)
        stageBpe(0)
        for i in range(8):
            if i + 1 < 8:
                stageA(i + 1)
                stageBdve(i + 1)
            stageC(i)
            if i + 1 < 8:
                stageBpe(i + 1)
        P.flush()

RD = BF16
NCH = 64


def tok_shift(P, dst, raw, tmp, mu_ap, rk_raw, k_tmp, k_dst, n=128):
    P.op("pool", lambda e: e.tensor_tensor(out=tmp[0:n, 1:S], in0=raw[0:n, 0:S - 1], in1=raw[0:n, 1:S], op=ALU.subtract),
         reads=[rk_raw], writes=[k_tmp])
    P.op("pool", lambda e: e.tensor_scalar(out=tmp[0:n, 0:1], in0=raw[0:n, 0:1], scalar1=-1.0, scalar2=0.0,
                                            op0=ALU.mult, op1=ALU.add), reads=[rk_raw, k_tmp], writes=[k_tmp])
    P.op("dve", lambda e: e.scalar_tensor_tensor(out=dst[0:n, :], in0=tmp[0:n, :], scalar=mu_ap, in1=raw[0:n, :],
                                                  op0=ALU.mult, op1=ALU.add), reads=[rk_raw, k_tmp], writes=[k_dst])


def phase4b_rwkv_prep(K, cts=range(2)):
    nc, P = K.nc, K.P
    with contextlib.ExitStack() as st:
        def sb(name, shape, dt):
            return st.enter_context(nc.sbuf_tensor(name, shape, dt))
        txw = sb("txw", [96, S], BF16)
        xap = sb("xap", [96, S], BF16)
        sxg = sb("sxg", [128, 2, S], BF16)
        M01 = sb("M01", [128, S], BF16)
        wup = sb("wup", [96, 256], BF16)
        aup = sb("aup", [96, 256], BF16)
        gup = sb("gup", [128, 2, 256], BF16)
        wst = sb("wst4", [128, 2, 256], F32)
        bones = sb("bones", [128, 128], BF16)
        prm = sb("prm", [128, 12, 2], F32)
        mul = sb("mul", [128, 4], F32)
        PT = sb("PT", [128, S], F32)
        KK = sb("KK", [128, S], F32)
        KP = sb("KP", [128, S], F32)
        CL = sb("CL", [128, S], F32)
        RP = sb("RP", [128, S], BF16)
        VP = sb("VP", [128, S], BF16)
        AA = sb("AA", [128, S], BF16)
        K2 = sb("K2", [128, S], BF16)
        SQb = sb("SQb", [128, S], BF16)
        OUT = [sb("OUT%d" % i, [128, S], BF16) for i in range(2)]
        PCt = sb("PCt", [128, NCH], F32)
        ps = [st.enter_context(nc.psum_tensor("ps4_%d" % i, [128, 512], F32)) for i in range(4)]
        for i, ap in enumerate(K.rw_prm):
            P.dma("sp", prm[:, i, :], ap, writes=[("prm", i)])
        prk = [("prm", i) for i in range(10)]
        P.op("dve", lambda e: e.tensor_scalar(out=prm[:, 10, :], in0=prm[:, 6, :], scalar1=-1.0, scalar2=1.0,
                                               op0=ALU.mult, op1=ALU.add), reads=prk, writes=[("prm", 10)])
        prk = prk + [("prm", 10)]
        P.dma("sp", mul[:], K.rw_mul, writes=["mul"])
        P.op("pool", lambda e: e.memset(bones[:], 0.0), writes=["bones"])
        P.op("pool", lambda e: e.memset(bones[0:64, 0:64], 1.0), reads=["bones"], writes=["bones"])
        P.op("pool", lambda e: e.memset(bones[64:128, 64:128], 1.0), reads=["bones"], writes=["bones"])
        P.op("pool", lambda e: e.iota(PT[:].rearrange("p (c t) -> p c t", t=64), pattern=[[0, NCH], [1, 64]], base=0,
                                      channel_multiplier=0, allow_small_or_imprecise_dtypes=True), writes=["PT"])
        P.op("dve", lambda e: e.tensor_scalar(out=M01[:], in0=PT[:], scalar1=0.5, scalar2=None, op0=ALU.is_gt),
             reads=["PT"], writes=["M01"])
        P.dma("sp", wst[0:96, 0, :], K.rw_w_up, writes=["wst"])
        P.op("act", lambda e: e.copy(out=wup[:], in_=wst[0:96, 0, :]), reads=["wst"], writes=["wup"])
        P.dma("sp", wst[0:96, 1, :], K.rw_a_up, reads=[], writes=["wst1"])
        P.op("act", lambda e: e.copy(out=aup[:], in_=wst[0:96, 1, :]), reads=["wst1"], writes=["aup"])
        P.dma("sp", wst[:, :, :], K.rw_g_up.rearrange("(c p) n -> p c n", p=128), reads=[], writes=["wst", "wst1"])
        P.op("act", lambda e: e.copy(out=gup[:], in_=wst[:]), reads=["wst", "wst1"], writes=["gup"])
        for (r0, n, mcol, func, dst, kd) in ((768, 96, 0, AF.Tanh, txw[:, :], "txw"), (864, 96, 1, AF.Copy, xap[:, :], "xap"),
                                             (960, 128, 2, AF.Sigmoid, sxg[:, 0, :], "sxg0"),
                                             (1088, 128, 3, AF.Sigmoid, sxg[:, 1, :], "sxg1")):
            P.dma("sp", PT[0:n, :], K.yT_d[r0:r0 + n, :], writes=["PT"])
            tok_shift(P, KP, PT, KK, mul[0:n, mcol:mcol + 1], "PT", "KK", "KP", n=n)
            P.op("act", lambda e, n=n, func=func, dst=dst: e.activation(out=dst, in_=KP[0:n, :], func=func),
                 reads=["KP"], writes=[kd])
        lk = ["txw", "xap", "sxg0", "sxg1"]
        oc = 0
        for ct in cts:
            c0 = ct * 128
            P.dma("sp", PT[:], K.yT_d[c0:c0 + 128, :], writes=["PT"])
            tok_shift(P, RP, PT, KK, prm[:, 0, ct:ct + 1], "PT", "KK", "RP")
            P.dma("sp", PT[:], K.yT_d[256 + c0:256 + c0 + 128, :], writes=["PT"])
            tok_shift(P, KP, PT, KK, prm[:, 1, ct:ct + 1], "PT", "KK", "KP")
            P.dma("sp", PT[:], K.yT_d[512 + c0:512 + c0 + 128, :], writes=["PT"])
            tok_shift(P, VP, PT, KK, prm[:, 2, ct:ct + 1], "PT", "KK", "VP")
            P.dma("sp", K.vb_d[c0:c0 + 128, :], VP[:], reads=["VP"], writes=[("vb_d", ct)])
            for blk in range(8):
                bs = slice(blk * 512, (blk + 1) * 512)
                p0, p1, p2 = ps[0], ps[1], ps[2]
                P.op("pe", lambda e, bs=bs, c0=c0: e.matmul(ps[0][:, :], lhsT=wup[:, c0:c0 + 128], rhs=txw[:, bs],
                                                             start=True, stop=True), reads=["wup", "txw"], writes=[("ps4", 0)])
                P.op("act", lambda e, bs=bs, ct=ct: e.activation(out=CL[:, bs], in_=ps[0][:, :], func=AF.Sigmoid,
                                                                  bias=prm[:, 3, ct:ct + 1]),
                     reads=[("ps4", 0)] + prk, writes=["CL"])
                P.op("pe", lambda e, bs=bs, c0=c0: e.matmul(ps[1][:, :], lhsT=aup[:, c0:c0 + 128], rhs=xap[:, bs],
                                                             start=True, stop=True), reads=["aup", "xap"], writes=[("ps4", 1)])
                P.op("act", lambda e, bs=bs, ct=ct: e.activation(out=AA[:, bs], in_=ps[1][:, :], func=AF.Sigmoid,
                                                                  bias=prm[:, 4, ct:ct + 1]),
                     reads=[("ps4", 1)] + prk, writes=["AA"])
                for cc in range(2):
                    P.op("pe", lambda e, bs=bs, c0=c0, cc=cc: e.matmul(ps[2][:, :], lhsT=gup[:, cc, c0:c0 + 128],
                                                                       rhs=sxg[:, cc, bs], start=(cc == 0), stop=(cc == 1)),
                         reads=["gup", "sxg0", "sxg1"], writes=[("ps4", 2)])
                o = OUT[oc % 2]
                P.op("dve", lambda e, bs=bs, o=o: e.tensor_copy(out=o[:, bs], in_=ps[2][:, :]),
                     reads=[("ps4", 2)], writes=[("OUT", oc % 2)])
            P.dma("sp", K.G_d[c0:c0 + 128, :], OUT[oc % 2][:], reads=[("OUT", oc % 2)], writes=[("G_d", ct)])
            oc += 1
            P.op("dve", lambda e: e.tensor_scalar(out=CL[:], in0=CL[:], scalar1=-0.6065306597126334, scalar2=None,
                                                   op0=ALU.mult), reads=["CL"], writes=["CL"])
            P.op("dve", lambda e, ct=ct: e.tensor_scalar(out=KK[:], in0=KP[:], scalar1=prm[:, 5, ct:ct + 1], scalar2=None,
                                                          op0=ALU.mult), reads=["KP"] + prk, writes=["KK"])
            P.op("act", lambda e: e.activation(out=SQb[:], in_=KK[:], func=AF.Square), reads=["KK"], writes=["SQb"])
            for blk in range(8):
                bs = slice(blk * 512, (blk + 1) * 512)
                P.op("pe", lambda e, bs=bs: e.matmul(ps[3][:, :], lhsT=bones[:], rhs=SQb[:, bs], start=True, stop=True),
                     reads=["bones", "SQb"], writes=[("ps4", 3)])
                P.op("act", lambda e, bs=bs: e.activation(out=PT[:, bs], in_=ps[3][:, :], func=AF.Sqrt),
                     reads=[("ps4", 3)], writes=["PT"])
            P.op("dve", lambda e: e.tensor_scalar(out=PT[:], in0=PT[:], scalar1=1e-12, scalar2=None, op0=ALU.max),
                 reads=["PT"], writes=["PT"])
            P.op("dve", lambda e: e.reciprocal(out=PT[:], in_=PT[:]), reads=["PT"], writes=["PT"])
            P.op("dve", lambda e: e.tensor_tensor(out=KK[:], in0=KK[:], in1=PT[:], op=ALU.mult), reads=["KK", "PT"], writes=["KK"])
            P.op("dve", lambda e, ct=ct: e.tensor_scalar(out=PT[:], in0=AA[:], scalar1=prm[:, 6, ct:ct + 1],
                                                          scalar2=prm[:, 10, ct:ct + 1], op0=ALU.mult, op1=ALU.add),
                 reads=["AA", "PT"] + prk, writes=["PT"])
            P.op("dve", lambda e: e.tensor_tensor(out=K2[:], in0=KP[:], in1=PT[:], op=ALU.mult), reads=["KP", "PT"], writes=["K2"])
            P.op("dve", lambda e, ct=ct: e.scalar_tensor_tensor(out=SQb[:], in0=RP[:], scalar=prm[:, 7, ct:ct + 1], in1=K2[:],
                                                                 op0=ALU.mult, op1=ALU.mult),
                 reads=["RP", "K2", "SQb"] + prk, writes=["SQb"])
            o = OUT[oc % 2]
            for blk in range(8):
                bs = slice(blk * 512, (blk + 1) * 512)
                P.op("pe", lambda e, bs=bs: e.matmul(ps[3][:, :], lhsT=bones[:], rhs=SQb[:, bs], start=True, stop=True),
                     reads=["bones", "SQb"], writes=[("ps4", 3)])
                P.op("dve", lambda e, bs=bs, o=o: e.tensor_tensor(out=o[:, bs], in0=ps[3][:, :], in1=VP[:, bs], op=ALU.mult),
                     reads=[("ps4", 3), "VP"], writes=[("OUT", oc % 2)])
            P.dma("sp", K.BON_d[c0:c0 + 128, :], o[:], reads=[("OUT", oc % 2)], writes=[("BON_d", ct)])
            oc += 1
            P.op("dve", lambda e: e.tensor_tensor_scan(out=PT[:], data0=M01[:], data1=CL[:], initial=0.0,
                                                        op0=ALU.mult, op1=ALU.add), reads=["M01", "CL", "PT"], writes=["PT"])
            P.op("pool", lambda e: e.tensor_tensor(out=CL[:], in0=PT[:], in1=CL[:], op=ALU.subtract),
                 reads=["PT", "CL"], writes=["CL"])
            P.op("act", lambda e: e.activation(out=CL[:], in_=CL[:], func=AF.Exp), reads=["CL"], writes=["CL"])
            v3 = lambda t: t[:].rearrange("p (c t) -> p c t", t=64)
            o = OUT[oc % 2]
            P.op("dve", lambda e, o=o: e.scalar_tensor_tensor(out=o[:], in0=KK[:], scalar=-1.0, in1=CL[:],
                                                               op0=ALU.mult, op1=ALU.mult),
                 reads=["KK", "CL"], writes=[("OUT", oc % 2)])
            P.dma("sp", K.AH_d[c0:c0 + 128, :], o[:], reads=[("OUT", oc % 2)], writes=[("AH_d", ct)])
            oc += 1
            P.op("act", lambda e: e.activation(out=CL[:], in_=PT[:], func=AF.Exp), reads=["PT", "CL"], writes=["CL"])
            o = OUT[oc % 2]
            P.op("dve", lambda e, o=o: e.tensor_tensor(out=o[:], in0=RP[:], in1=CL[:], op=ALU.mult),
                 reads=["RP", "CL"], writes=[("OUT", oc % 2)])
            P.dma("sp", K.RH_d[c0:c0 + 128, :], o[:], reads=[("OUT", oc % 2)], writes=[("RH_d", ct)])
            oc += 1
            P.op("pool", lambda e: e.tensor_copy(out=PCt[:], in_=v3(CL)[:, :, 63]), reads=["CL"], writes=["PCt"])
            P.dma("sp", K.PC_d[c0:c0 + 128, :], PCt[:], reads=["PCt"], writes=[("PC_d", ct)])
            P.op("act", lambda e: e.activation(out=PT[:], in_=PT[:], func=AF.Exp, scale=-1.0), reads=["PT"], writes=["PT"])
            o = OUT[oc % 2]
            P.op("dve", lambda e, o=o: e.tensor_tensor(out=o[:], in0=K2[:], in1=PT[:], op=ALU.mult),
                 reads=["K2", "PT"], writes=[("OUT", oc % 2)])
            P.dma("sp", K.KH_d[c0:c0 + 128, :], o[:], reads=[("OUT", oc % 2)], writes=[("KH_d", ct)])
            oc += 1
            P.op("dve", lambda e: e.tensor_tensor(out=KK[:], in0=KK[:], in1=AA[:], op=ALU.mult), reads=["KK", "AA"], writes=["KK"])
            o = OUT[oc % 2]
            P.op("dve", lambda e, o=o: e.tensor_tensor(out=o[:], in0=KK[:], in1=PT[:], op=ALU.mult),
                 reads=["KK", "PT"], writes=[("OUT", oc % 2)])
            P.dma("sp", K.BH_d[c0:c0 + 128, :], o[:], reads=[("OUT", oc % 2)], writes=[("BH_d", ct)])
            oc += 1
        P.flush()

def phase4c_rwkv_scan(K, heads=range(4)):
    nc, P = K.nc, K.P
    with contextlib.ExitStack() as st:
        def sb(name, shape, dt):
            return st.enter_context(nc.sbuf_tensor(name, shape, dt))
        ident = sb("ident4", [128, 128], BF16)
        make_ident(K, P, ident)
        MaskG = sb("MaskG", [64, 4, 128], F32)
        MaskX = sb("MaskX", [64, 8, 64], F32)
        I8 = sb("I8", [64, 64], F32)
        ones = sb("ones4", [64, 64], F32)
        P.op("pool", lambda e: e.memset(ones[:], 1.0), writes=["ones"])
        for a in range(4):
            for cq in range(2):
                P.op("pool", lambda e, cq=cq, a=a: e.affine_select(
                    out=MaskG[:, a, cq * 64:(cq + 1) * 64], in_=ones[:], pattern=[[1, 64]],
                    compare_op=(ALU.is_gt if cq == 0 else ALU.is_ge), fill=0.0, base=0, channel_multiplier=-1),
                    reads=["ones"], writes=["MaskG"])
        for a in range(8):
            P.op("pool", lambda e, a=a: e.affine_select(out=MaskX[:, a, :], in_=ones[:], pattern=[[-1, 64]],
                                                         compare_op=ALU.is_gt, fill=0.0, base=0, channel_multiplier=1),
                 reads=["ones"], writes=["MaskX"])
        P.op("dve", lambda e: e.tensor_copy(out=I8[:], in_=ident[0:64, 0:64]), reads=["ident"], writes=["I8"])
        AH = sb("AH", [64, S], RD)
        RH = sb("RH", [64, S], RD)
        BH = sb("BH", [64, S], RD)
        KH = sb("KH", [64, S], RD)
        vb = sb("vb", [64, S], BF16)
        PC = sb("PC", [64, NCH], F32)
        ARh = sb("ARh", [64, NCH, 128], RD)
        BKh = sb("BKh", [64, NCH, 128], RD)
        GmB = sb("GmB", [64, NCH, 128], RD)
        GmK = sb("GmK", [64, NCH, 128], RD)
        Btok = sb("Btok", [64, NCH, 64], RD)
        Ktok = sb("Ktok", [64, NCH, 64], RD)
        Vtok = sb("Vtok", [64, NCH, 64], RD)
        X0 = sb("X0", [64, NCH, 64], RD)
        Pm = sb("Pm", [64, NCH, 64], RD)
        oT = sb("oT", [64, S], F32)
        Ast = sb("Ast", [64, 64], F32)
        Abf = sb("Abf", [64, 64], RD)
        Tt = sb("Tt", [64, 64], F32)
        Xs = sb("Xs", [64, 64], RD)
        Us = sb("Us", [64, 64], RD)
        PSb2 = [st.enter_context(nc.psum_tensor("PSb%d" % i, [128, 1024], BF16)) for i in range(2)]
        PS = [st.enter_context(nc.psum_tensor("PS%d" % i, [128, 512], F32)) for i in range(6)]
        v3 = lambda t: t[:].rearrange("p (c t) -> p c t", t=64)
        Nb = [v3(AH), v3(RH)]
        Xb = [v3(BH), v3(KH)]
        Nk = ["AH", "RH"]
        Xk = ["BH", "KH"]
        K.tcnt = 0
        for hd in heads:
            r0 = hd * 64
            P.dma("sp", AH[:], K.AH_d[r0:r0 + 64, :], writes=["AH"])
            P.dma("sp", RH[:], K.RH_d[r0:r0 + 64, :], writes=["RH"])
            P.dma("sp", BH[:], K.BH_d[r0:r0 + 64, :], writes=["BH"])
            P.dma("sp", KH[:], K.KH_d[r0:r0 + 64, :], writes=["KH"])
            P.dma("sp", vb[:], K.vb_d[r0:r0 + 64, :], writes=["vb"])
            P.dma("sp", PC[:], K.PC_d[r0:r0 + 64, :], writes=["PC"])
            P.op("dve", lambda e: e.tensor_copy(out=ARh[:, :, 0:64], in_=v3(AH)), reads=["AH"], writes=["ARh"])
            P.op("pool", lambda e: e.tensor_copy(out=ARh[:, :, 64:128], in_=v3(RH)), reads=["RH"], writes=["ARh"])
            P.op("dve", lambda e: e.tensor_copy(out=BKh[:, :, 0:64], in_=v3(BH)), reads=["BH"], writes=["BKh"])
            P.op("pool", lambda e: e.tensor_copy(out=BKh[:, :, 64:128], in_=v3(KH)), reads=["KH"], writes=["BKh"])
            for (src, srck, col0, dst, dk) in ((BKh, "BKh", 0, Btok, "Btok"), (BKh, "BKh", 64, Ktok, "Ktok"), (None, "vb", 0, Vtok, "Vtok")):
                for c16 in range(0, NCH, 16):
                    tb_ = K.tcnt % 2
                    K.tcnt += 1
                    PSb = PSb2[tb_]
                    for cc in range(16):
                        c = c16 + cc
                        in_ = vb[:, c * 64:(c + 1) * 64] if src is None else src[:, c, col0:col0 + 64]
                        P.op("pe", lambda e, cc=cc, in_=in_, PSb=PSb: e.transpose(out=PSb[0:64, cc * 64:(cc + 1) * 64], in_=in_,
                                                                                  identity=ident[0:64, 0:64]),
                             reads=[srck, "ident"], writes=[("PSb", tb_)])
                    P.op("act", lambda e, c16=c16, dst=dst, PSb=PSb: e.copy(out=dst[:, c16:c16 + 16, :].rearrange("p c k -> p (c k)"),
                                                                             in_=PSb[0:64, :]), reads=[("PSb", tb_)], writes=[dk])
            gi = 0
            for (col0, dst, dk) in ((0, GmB, "GmB"), (64, GmK, "GmK")):
                for c4 in range(0, NCH, 4):
                    b = gi % 2
                    gi += 1
                    for cc in range(4):
                        c = c4 + cc
                        P.op("pe", lambda e, c=c, cc=cc, b=b, col0=col0: e.matmul(
                            PS[b][0:64, cc * 128:(cc + 1) * 128], lhsT=BKh[:, c, col0:col0 + 64], rhs=ARh[:, c, :],
                            start=True, stop=True), reads=["BKh", "ARh"], writes=[("PS", b)])
                    P.op("dve", lambda e, c4=c4, dst=dst, b=b: e.tensor_tensor(
                        out=dst[:, c4:c4 + 4, :], in0=PS[b][0:64, :].rearrange("p (a t) -> p a t", t=128), in1=MaskG[:],
                        op=ALU.mult), reads=[("PS", b), "MaskG"], writes=[dk])
            for c8 in range(0, NCH, 8):
                for cc in range(8):
                    c = c8 + cc
                    P.op("pe", lambda e, c=c, cc=cc: e.matmul(PS[2][0:64, cc * 64:(cc + 1) * 64], lhsT=ARh[:, c, 0:64],
                                                               rhs=BKh[:, c, 0:64], start=True, stop=True),
                         reads=["ARh", "BKh"], writes=[("PS", 2)])
                P.op("dve", lambda e, c8=c8: e.tensor_tensor(
                    out=X0[:, c8:c8 + 8, :], in0=PS[2][0:64, :].rearrange("p (a t) -> p a t", t=64), in1=MaskX[:],
                    op=ALU.mult), reads=[("PS", 2), "MaskX"], writes=["X0"])
            N0 = GmB[:, :, 0:64]
            P.op("dve", lambda e, N0=N0: e.tensor_tensor(out=Pm[:], in0=N0, in1=I8[:].unsqueeze(1).to_broadcast([64, NCH, 64]),
                                                         op=ALU.add), reads=["GmB", "I8"], writes=["Pm"])
            curN, curNk = N0, "GmB"
            curX, curXk = X0[:], "X0"
            for lvl in range(1, 6):
                nX, nXk = Xb[lvl % 2], Xk[lvl % 2]
                nN, nNk = Nb[lvl % 2], Nk[lvl % 2]
                for c8 in range(0, NCH, 8):
                    pb_ = (c8 // 8) % 2
                    for cc in range(8):
                        c = c8 + cc
                        P.op("pe", lambda e, c=c, cc=cc, curN=curN, curX=curX, pb_=pb_: e.matmul(
                            PS[0 + pb_][0:64, cc * 64:(cc + 1) * 64], lhsT=curN[:, c, :], rhs=curX[:, c, :], start=True, stop=True),
                            reads=[curNk, curXk], writes=[("PS", 0 + pb_)])
                    P.op("act", lambda e, c8=c8, nX=nX, pb_=pb_: e.copy(out=nX[:, c8:c8 + 8, :],
                                                                in_=PS[0 + pb_][0:64, :].rearrange("p (a t) -> p a t", t=64)),
                         reads=[("PS", 0 + pb_)], writes=[nXk])
                    if lvl < 5:
                        for cc in range(8):
                            c = c8 + cc
                            P.op("pe", lambda e, c=c, cc=cc, curN=curN, curX=curX, pb_=pb_: e.matmul(
                                PS[2 + pb_][0:64, cc * 64:(cc + 1) * 64], lhsT=curX[:, c, :], rhs=curN[:, c, :], start=True, stop=True),
                                reads=[curNk, curXk], writes=[("PS", 2 + pb_)])
                        P.op("act", lambda e, c8=c8, nN=nN, pb_=pb_: e.copy(out=nN[:, c8:c8 + 8, :],
                                                                    in_=PS[2 + pb_][0:64, :].rearrange("p (a t) -> p a t", t=64)),
                             reads=[("PS", 2 + pb_)], writes=[nNk])
                    for cc in range(8):
                        c = c8 + cc
                        P.op("pe", lambda e, c=c, cc=cc, nX=nX, pb_=pb_: e.matmul(
                            PS[4 + pb_][0:64, cc * 64:(cc + 1) * 64], lhsT=nX[:, c, :], rhs=Pm[:, c, :], start=True, stop=True),
                            reads=[nXk, "Pm"], writes=[("PS", 4 + pb_)])
                    P.op("dve", lambda e, c8=c8, pb_=pb_: e.tensor_tensor(
                        out=Pm[:, c8:c8 + 8, :], in0=PS[4 + pb_][0:64, :].rearrange("p (a t) -> p a t", t=64),
                        in1=Pm[:, c8:c8 + 8, :], op=ALU.add), reads=[("PS", 4 + pb_), "Pm"], writes=["Pm"])
                curN, curNk, curX, curXk = nN, nNk, nX, nXk
            P.op("pool", lambda e: e.memset(Ast[:], 0.0), writes=["Ast"])
            P.op("pool", lambda e: e.memset(Abf[:], 0.0), writes=["Abf"])
            for c in range(NCH):
                P.op("pool", lambda e, c=c: e.tensor_scalar(out=Tt[:], in0=Ast[:], scalar1=PC[:, c:c + 1], scalar2=0.0,
                                                             op0=ALU.mult, op1=ALU.add), reads=["Ast", "PC"], writes=["Tt"])
                P.op("pe", lambda e, c=c: e.matmul(PS[0][0:64, 0:64], lhsT=ARh[:, c, 0:64], rhs=Abf[:], start=True, stop=False),
                     reads=["ARh", "Abf"], writes=[("PS", 0)])
                P.op("pe", lambda e, c=c: e.matmul(PS[0][0:64, 0:64], lhsT=GmK[:, c, 0:64], rhs=Vtok[:, c, :], start=False, stop=True),
                     reads=["GmK", "Vtok"], writes=[("PS", 0)])
                P.op("act", lambda e: e.copy(out=Xs[:], in_=PS[0][0:64, 0:64]), reads=[("PS", 0)], writes=["Xs"])
                P.op("pe", lambda e, c=c: e.matmul(PS[1][0:64, 0:64], lhsT=Pm[:, c, :], rhs=Xs[:], start=True, stop=True),
                     reads=["Pm", "Xs"], writes=[("PS", 1)])
                P.op("dve", lambda e: e.tensor_copy(out=Us[:], in_=PS[1][0:64, 0:64]), reads=[("PS", 1)], writes=["Us"])
                P.op("pe", lambda e, c=c: e.matmul(PS[4][0:64, 0:64], lhsT=Btok[:, c, :], rhs=Us[:], start=True, stop=False),
                     reads=["Btok", "Us"], writes=[("PS", 4)])
                P.op("pe", lambda e, c=c: e.matmul(PS[4][0:64, 0:64], lhsT=Ktok[:, c, :], rhs=Vtok[:, c, :], start=False, stop=True),
                     reads=["Ktok", "Vtok"], writes=[("PS", 4)])
                ob = 2 + (c % 2)
                P.op("pe", lambda e, c=c, ob=ob: e.matmul(PS[ob][0:64, 0:64], lhsT=Abf[:], rhs=ARh[:, c, 64:128], start=True, stop=False),
                     reads=["Abf", "ARh"], writes=[("PS", ob)])
                P.op("pe", lambda e, c=c, ob=ob: e.matmul(PS[ob][0:64, 0:64], lhsT=Us[:], rhs=GmB[:, c, 64:128], start=False, stop=False),
                     reads=["Us", "GmB"], writes=[("PS", ob)])
                P.op("pe", lambda e, c=c, ob=ob: e.matmul(PS[ob][0:64, 0:64], lhsT=Vtok[:, c, :], rhs=GmK[:, c, 64:128], start=False, stop=True),
                     reads=["Vtok", "GmK"], writes=[("PS", ob)])
                P.op("dve", lambda e, c=c: e.scalar_tensor_tensor(out=Abf[:], in0=PS[4][0:64, 0:64], scalar=PC[:, c:c + 1], in1=Tt[:],
                                                                   op0=ALU.mult, op1=ALU.add),
                     reads=[("PS", 4), "Tt", "PC"], writes=["Abf"])
                P.op("dve", lambda e, c=c: e.scalar_tensor_tensor(out=Ast[:], in0=PS[4][0:64, 0:64], scalar=PC[:, c:c + 1], in1=Tt[:],
                                                                   op0=ALU.mult, op1=ALU.add),
                     reads=[("PS", 4), "Tt", "PC"], writes=["Ast"])
                P.op("act", lambda e, c=c, ob=ob: e.copy(out=oT[:, c * 64:(c + 1) * 64], in_=PS[ob][0:64, 0:64]),
                     reads=[("PS", ob)], writes=[("oT", c // 8)])
            P.dma("sp", K.oT_d[r0:r0 + 64, :], oT[:], reads=[("oT", q) for q in range(8)], writes=[("oT_d", hd)])
        P.flush()

def phase4d_rwkv_post(K, cts=range(2)):
    nc, P = K.nc, K.P
    with contextlib.ExitStack() as st:
        def sb(name, shape, dt):
            return st.enter_context(nc.sbuf_tensor(name, shape, dt))
        ident = sb("ident4d", [128, 128], BF16)
        make_ident(K, P, ident)
        bonesf = sb("bonesf", [128, 128], F32)
        P.op("pool", lambda e: e.memset(bonesf[:], 0.0), writes=["bonesf"])
        P.op("pool", lambda e: e.memset(bonesf[0:64, 0:64], 1.0), reads=["bonesf"], writes=["bonesf"])
        P.op("pool", lambda e: e.memset(bonesf[64:128, 64:128], 1.0), reads=["bonesf"], writes=["bonesf"])
        prm = sb("prm4d", [128, 2, 2], F32)
        P.dma("sp", prm[:, 0, :], K.rw_prm[8], writes=["prm0"])
        P.dma("sp", prm[:, 1, :], K.rw_prm[9], writes=["prm1"])
        o = sb("o4d", [128, S], F32)
        osq = sb("osq", [128, S], F32)
        bon = sb("bon", [128, S], BF16)
        gg = sb("gg", [128, S], BF16)
        Mb = [sb("Mb%d" % i, [128, 512], F32) for i in range(2)]
        Vb = [sb("Vb%d" % i, [128, 512], F32) for i in range(2)]
        Yb = [sb("Yb%d" % i, [128, 512], F32) for i in range(2)]
        Ob = [sb("Ob%d" % i, [128, 512], BF16) for i in range(2)]
        Tk = [sb("Tk%d" % i, [128, 4, 128], BF16) for i in range(2)]
        ps = [st.enter_context(nc.psum_tensor("p4d_%d" % i, [128, 512], F32)) for i in range(4)]
        pst = [st.enter_context(nc.psum_tensor("p4dt_%d" % i, [128, 4, 128], BF16)) for i in range(2)]
        it = 0
        for ct in cts:
            c0 = ct * 128
            P.dma("sp", o[:], K.oT_d[c0:c0 + 128, :], writes=["o"])
            P.dma("sp", bon[:], K.BON_d[c0:c0 + 128, :], writes=["bon"])
            P.dma("sp", gg[:], K.G_d[c0:c0 + 128, :], writes=["gg"])
            P.op("act", lambda e: e.activation(out=osq[:], in_=o[:], func=AF.Square), reads=["o"], writes=["osq"])
            for blk in range(8):
                s2 = it % 2
                it += 1
                bs = slice(blk * 512, (blk + 1) * 512)
                P.op("pe", lambda e, bs=bs, s2=s2: e.matmul(ps[s2][:, :], lhsT=bonesf[:], rhs=o[:, bs], start=True, stop=True),
                     reads=["bonesf", "o"], writes=[("p4d", s2)])
                P.op("pe", lambda e, bs=bs, s2=s2: e.matmul(ps[2 + s2][:, :], lhsT=bonesf[:], rhs=osq[:, bs], start=True, stop=True),
                     reads=["bonesf", "osq"], writes=[("p4d", 2 + s2)])
                P.op("act", lambda e, s2=s2: e.activation(out=Mb[s2][:], in_=ps[s2][:, :], func=AF.Copy, scale=1.0 / 64),
                     reads=[("p4d", s2)], writes=[("Mb", s2)])
                P.op("pool", lambda e, s2=s2: e.tensor_tensor(out=Vb[s2][:], in0=Mb[s2][:], in1=Mb[s2][:], op=ALU.mult),
                     reads=[("Mb", s2)], writes=[("Vb", s2)])
                P.op("dve", lambda e, s2=s2: e.scalar_tensor_tensor(out=Vb[s2][:], in0=ps[2 + s2][:, :], scalar=1.0 / 64, in1=Vb[s2][:],
                                                                     op0=ALU.mult, op1=ALU.subtract),
                     reads=[("p4d", 2 + s2), ("Vb", s2)], writes=[("Vb", s2)])
                P.op("dve", lambda e, s2=s2: e.tensor_scalar(out=Vb[s2][:], in0=Vb[s2][:], scalar1=64e-5, scalar2=None, op0=ALU.add),
                     reads=[("Vb", s2)], writes=[("Vb", s2)])
                P.op("act", lambda e, s2=s2: e.activation(out=Vb[s2][:], in_=Vb[s2][:], func=AF.Sqrt),
                     reads=[("Vb", s2)], writes=[("Vb", s2)])
                P.op("dve", lambda e, s2=s2: e.reciprocal(out=Vb[s2][:], in_=Vb[s2][:]), reads=[("Vb", s2)], writes=[("Vb", s2)])
                P.op("pool", lambda e, s2=s2, bs=bs: e.tensor_tensor(out=Yb[s2][:], in0=o[:, bs], in1=Mb[s2][:], op=ALU.subtract),
                     reads=["o", ("Mb", s2)], writes=[("Yb", s2)])
                P.op("dve", lambda e, s2=s2: e.tensor_tensor(out=Yb[s2][:], in0=Yb[s2][:], in1=Vb[s2][:], op=ALU.mult),
                     reads=[("Yb", s2), ("Vb", s2)], writes=[("Yb", s2)])
                P.op("dve", lambda e, s2=s2, ct=ct: e.tensor_scalar(out=Yb[s2][:], in0=Yb[s2][:], scalar1=prm[:, 0, ct:ct + 1],
                                                                     scalar2=prm[:, 1, ct:ct + 1], op0=ALU.mult, op1=ALU.add),
                     reads=[("Yb", s2), "prm0", "prm1"], writes=[("Yb", s2)])
                P.op("pool", lambda e, s2=s2, bs=bs: e.tensor_tensor(out=Yb[s2][:], in0=Yb[s2][:], in1=bon[:, bs], op=ALU.add),
                     reads=[("Yb", s2), "bon"], writes=[("Yb", s2)])
                P.op("dve", lambda e, s2=s2, bs=bs: e.tensor_tensor(out=Ob[s2][:], in0=Yb[s2][:], in1=gg[:, bs], op=ALU.mult),
                     reads=[("Yb", s2), "gg"], writes=[("Ob", s2)])
                for q in range(4):
                    P.op("pe", lambda e, s2=s2, q=q: e.transpose(out=pst[s2][:, q, :], in_=Ob[s2][:, q * 128:(q + 1) * 128],
                                                                 identity=ident[:]),
                         reads=[("Ob", s2), "ident"], writes=[("p4dt", s2)])
                P.op("act", lambda e, s2=s2: e.copy(out=Tk[s2][:], in_=pst[s2][:]), reads=[("p4dt", s2)], writes=[("Tk", s2)])
                P.dma("sp", K.ro_loc_d[blk // 4].rearrange("(t p) c -> p t c", p=128)[:, (blk % 4) * 4:(blk % 4 + 1) * 4, c0:c0 + 128], Tk[s2][:],
                      reads=[("Tk", s2)], writes=[("ro_tok_d", ct, blk)])
        P.flush()


def phase4e_allgather(K):
    P = K.P
    for hh in range(2):
        P.coll(lambda e, hh=hh: e.collective_compute("AllGather", ALU.bypass, replica_groups=[[0, 1, 2, 3], [4, 5, 6, 7]],
                                                     ins=[K.ro_loc_d[hh].opt()], outs=[K.ro_all_d[hh].opt()]),
               reads=[("ro_loc", hh)], writes=[("ro_all", hh)])
    P.flush()


def phase5a_select(K):
    nc, P = K.nc, K.P
    with contextlib.ExitStack() as st:
        def sb(name, shape, dt):
            return st.enter_context(nc.sbuf_tensor(name, shape, dt))
        ro = sb("ro_tok", [128, 32, 1024], BF16)
        selT = sb("selT", [128, 32, 1024], BF16)
        qrow = sb("qrow", [128, 1024], F32)
        tki = sb("tki", [128, 32], I32)
        tkf = sb("tkf", [128, 32], F32)
        mo = [sb("mo%d" % i, [128, 512], BF16) for i in range(2)]
        at = sb("at5", [128, 8, 1024], BF16)
        ps = [st.enter_context(nc.psum_tensor("p5a_%d" % i, [128, 512], F32)) for i in range(2)]
        for q4 in range(4):
            for hh in range(2):
                P.dma("sp", ro[:, hh * 16:(hh + 1) * 16, q4 * 256:(q4 + 1) * 256],
                      K.ro_all_d[hh][q4 * 2048:(q4 + 1) * 2048, :].rearrange("(t p) c -> p t c", p=128), writes=[("ro", q4, hh)])
        rok = [("ro", q4, hh) for q4 in range(4) for hh in range(2)]
        P.dma("sp", qrow[:], bcast_rows(K.qpos_row, 1024), writes=["qrow"])
        P.op("pool", lambda e: e.iota(tki[:], pattern=[[128, 32]], base=0, channel_multiplier=1), writes=["tki"])
        P.op("dve", lambda e: e.tensor_copy(out=tkf[:], in_=tki[:]), reads=["tki"], writes=["tkf"])
        for T in range(32):
            P.op("dve", lambda e, T=T: e.tensor_scalar(out=selT[:, T, :], in0=qrow[:], scalar1=tkf[:, T:T + 1], scalar2=0.0,
                                                      op0=ALU.is_equal, op1=ALU.add), reads=["qrow", "tkf"], writes=[("selT", T)])
        sk = [("selT", T) for T in range(32)]
        P.dma("sp", at[:], K.attT_d.rearrange("h p t -> p h t"), writes=["at5"])
        P.dma("sp", K.mixT_d.rearrange("k p t -> p k t")[:, 0:8, :], at[:], reads=["at5"], writes=["mixa"])
        i = 0
        for m in range(8):
            for half in range(2):
                s2 = i % 2
                i += 1
                for T in range(32):
                    P.op("pe", lambda e, T=T, m=m, half=half, s2=s2: e.matmul(
                        ps[s2][:, :], lhsT=ro[:, T, m * 128:(m + 1) * 128], rhs=selT[:, T, half * 512:(half + 1) * 512],
                        start=(T == 0), stop=(T == 31)), reads=rok + sk, writes=[("p5a", s2)])
                P.op("act", lambda e, s2=s2: e.copy(out=mo[s2][:], in_=ps[s2][:, :]), reads=[("p5a", s2)], writes=[("mo", s2)])
                P.dma("sp", K.mixT_d[8 + m, :, half * 512:(half + 1) * 512], mo[s2][:], reads=[("mo", s2)], writes=[("mixr", m, half)])
        P.flush()


def phase5b_outproj(K):
    nc, P = K.nc, K.P
    with contextlib.ExitStack() as st:
        def sb(name, shape, dt):
            return st.enter_context(nc.sbuf_tensor(name, shape, dt))
        ident = sb("ident5", [128, 128], BF16)
        make_ident(K, P, ident)
        G2, SH2 = load_G_SH(K, P, st, 3, 4, K.norm2_g, "p5")
        GT1 = sb("GT1", [128, D], F32)
        P.dma("sp", GT1[:], bcast_rows(K.mod_d[2 * D:3 * D], D), writes=["GT1"])
        Wo = sb("Wo", [128, 16, D], BF16)
        stg = [sb("wstg5_%d" % i, [128, 4, 512], F32) for i in range(2)]
        wk = load_weight_bf16(K, P, stg, Wo, 0, K.w_out, D, "Wo")
        mixT = sb("mixT", [128, 16, 512], BF16)
        T = norm_tiles_alloc(K, st, "p5")
        x1 = T["xt"]
        hT = [sb("hT5_0", [128, 16, 512], BF16)] * 2
        xo = [sb("xo%d" % i, [128, D], F32) for i in range(2)]
        ps = [st.enter_context(nc.psum_tensor("p5b_%d" % i, [128, 512], F32)) for i in range(2)]
        ss, junk, hb, pT = T["ss"], T["junk"], T["hb"], T["pT"]
        gi = 0
        for blk in range(2):
            hs = 0
            P.dma("sp", mixT[:], K.mixT_d.rearrange("k p t -> p k t")[:, :, blk * 512:(blk + 1) * 512], writes=["mixT"])
            for ti in range(4):
                t = blk * 4 + ti
                xs = t % 2
                P.dma("sp", xo[xs][:], K.x_own[t * 128:(t + 1) * 128, :], writes=[("xo", xs)])
                for cg in range(4):
                    b = gi % 2
                    gi += 1
                    for k in range(16):
                        P.op("pe", lambda e, b=b, k=k, t=t, cg=cg: e.matmul(
                            ps[b][:, :], lhsT=mixT[:, k, (t % 4) * 128:(t % 4 + 1) * 128], rhs=Wo[:, k, cg * 512:(cg + 1) * 512],
                            start=(k == 0), stop=(k == 15)), reads=["mixT", ("Wo", cg * 512, 4 * (k // 4))], writes=[("p5b", b)])
                    cs = slice(cg * 512, (cg + 1) * 512)
                    P.op("dve", lambda e, b=b, xs=xs, cs=cs: e.tensor_tensor(out=x1[xs][:, cs], in0=ps[b][:, :], in1=GT1[:, cs], op=ALU.mult),
                         reads=[("p5b", b), "GT1"], writes=[("xt", xs)])
                    P.op("pool", lambda e, xs=xs, cs=cs: e.tensor_tensor(out=x1[xs][:, cs], in0=x1[xs][:, cs], in1=xo[xs][:, cs], op=ALU.add),
                         reads=[("xt", xs), ("xo", xs)], writes=[("xt", xs)])
                P.dma("sp", K.x1_d[t * 128:(t + 1) * 128, :], x1[xs][:], reads=[("xt", xs)], writes=[("x1_d", t)])
                P.op("act", lambda e, xs=xs: e.activation(out=junk[:], in_=x1[xs][:], func=AF.Square, accum_out=ss[:, 0:1]),
                     reads=[("xt", xs)], writes=["junk", "ss0"])
                P.op("dve", lambda e: e.tensor_scalar(out=ss[:, 1:2], in0=ss[:, 0:1], scalar1=1.0 / D, scalar2=1e-6,
                                                       op0=ALU.mult, op1=ALU.add), reads=["ss0"], writes=["ss1"])
                P.op("act", lambda e: e.activation(out=ss[:, 2:3], in_=ss[:, 1:2], func=AF.Sqrt), reads=["ss1"], writes=["ss2"])
                P.op("dve", lambda e: e.reciprocal(out=ss[:, 3:4], in_=ss[:, 2:3]), reads=["ss2"], writes=["ss3"])
                P.op("dve", lambda e, xs=xs: e.scalar_tensor_tensor(out=x1[xs][:], in0=x1[xs][:], scalar=ss[:, 3:4], in1=G2[:],
                                                                   op0=ALU.mult, op1=ALU.mult),
                     reads=[("xt", xs), "ss3", "G"], writes=[("xt", xs)])
                P.op("pool", lambda e, xs=xs: e.tensor_tensor(out=hb[xs][:], in0=x1[xs][:], in1=SH2[:], op=ALU.add),
                     reads=[("xt", xs), "SH"], writes=[("hb", xs)])
                for half in range(2):
                    for kk in range(8):
                        k = half * 8 + kk
                        P.op("pe", lambda e, k=k, kk=kk, half=half, xs=xs: e.transpose(
                            out=pT[half][:, kk, :], in_=hb[xs][:, k * 128:(k + 1) * 128], identity=ident[:]),
                            reads=[("hb", xs), "ident"], writes=[("pT", half)])
                    o_ = hT[hs][:, half * 8:(half + 1) * 8, ti * 128:(ti + 1) * 128]
                    if half == 0:
                        P.op("act", lambda e, o_=o_, half=half: e.copy(out=o_, in_=pT[half][:]), reads=[("pT", half)], writes=[("hT5", hs, ti, half)])
                    else:
                        P.op("dve", lambda e, o_=o_, half=half: e.tensor_copy(out=o_, in_=pT[half][:]), reads=[("pT", half)], writes=[("hT5", hs, ti, half)])
            P.dma("sp", K.h2T_d.rearrange("k p t -> p k t")[:, :, blk * 512:(blk + 1) * 512], hT[hs][:],
                  reads=[("hT5", hs, ti, half) for ti in range(4) for half in range(2)], writes=[("h2T_d", blk)])
        P.flush()


def phase5c_ffn(K):
    nc, P = K.nc, K.P
    NF = 5632 // 128
    with contextlib.ExitStack() as st:
        def sb(name, shape, dt):
            return st.enter_context(nc.sbuf_tensor(name, shape, dt))
        h2T = sb("h2T", [128, 16, OWN], BF16)
        P.dma("sp", h2T[:], K.h2T_d.rearrange("k p t -> p k t"), writes=["h2T"])
        ao = [sb("ao%d" % i, [128, 512], BF16) for i in range(2)]
        stg = [sb("wstg6_%d" % i, [128, 4, 512], F32) for i in range(4)]
        Wg = [sb("Wg%d" % i, [128, 16, 512], BF16) for i in range(2)]
        Wu = [sb("Wu%d" % i, [128, 16, 512], BF16) for i in range(2)]
        sg = [sb("sg%d" % i, [128, 512], F32) for i in range(2)]
        ps = [st.enter_context(nc.psum_tensor("p5c_%d" % i, [128, 512], F32)) for i in range(4)]
        gi = 0

        def load_group(fg, defer=None):
            ws = fg % 2
            load_weight_bf16(K, P, stg, Wg[ws], 0, K.w_ffn_gate[:, fg * 512:(fg + 1) * 512], 512, ("Wg", ws), defer=defer)
            load_weight_bf16(K, P, stg, Wu[ws], 0, K.w_ffn_up[:, fg * 512:(fg + 1) * 512], 512, ("Wu", ws), defer=defer)
        load_group(0)
        for fg in range(11):
            ws = fg % 2
            pend = []
            if fg + 1 < 11:
                load_group(fg + 1, defer=pend)
            for f4 in range(4):
                f = fg * 4 + f4
                for tb in range(2):
                    b = gi % 2
                    gi += 1
                    if pend:
                        pend.pop(0)()
                    for k in range(16):
                        P.op("pe", lambda e, b=b, k=k, f4=f4, tb=tb, ws=ws: e.matmul(
                            ps[b][:, :], lhsT=Wg[ws][:, k, f4 * 128:(f4 + 1) * 128], rhs=h2T[:, k, tb * 512:(tb + 1) * 512],
                            start=(k == 0), stop=(k == 15)), reads=["h2T", (("Wg", ws), 0, (k // 4) * 4)], writes=[("p5c", b)])
                    for k in range(16):
                        P.op("pe", lambda e, b=b, k=k, f4=f4, tb=tb, ws=ws: e.matmul(
                            ps[2 + b][:, :], lhsT=Wu[ws][:, k, f4 * 128:(f4 + 1) * 128], rhs=h2T[:, k, tb * 512:(tb + 1) * 512],
                            start=(k == 0), stop=(k == 15)), reads=["h2T", (("Wu", ws), 0, (k // 4) * 4)], writes=[("p5c", 2 + b)])
                    P.op("act", lambda e, b=b: e.activation(out=sg[b][:], in_=ps[b][:, :], func=AF.Silu),
                         reads=[("p5c", b)], writes=[("sg", b)])
                    P.op("dve", lambda e, b=b: e.tensor_tensor(out=ao[b][:], in0=ps[2 + b][:, :], in1=sg[b][:], op=ALU.mult),
                         reads=[("p5c", 2 + b), ("sg", b)], writes=[("ao", b)])
                    P.dma("sp", K.actT_d[f, :, tb * 512:(tb + 1) * 512], ao[b][:], reads=[("ao", b)], writes=[("actT_d", f, tb)])
        P.flush()
    with contextlib.ExitStack() as st:
        def sb(name, shape, dt):
            return st.enter_context(nc.sbuf_tensor(name, shape, dt))
        GT2 = sb("GT2", [128, D], F32)
        P.dma("sp", GT2[:], bcast_rows(K.mod_d[5 * D:6 * D], D), writes=["GT2"])
        actT = sb("actT", [128, NF, OWN], BF16)
        for q in range(4):
            P.dma("sp", actT[:, q * 11:(q + 1) * 11, :], K.actT_d.rearrange("f p t -> p f t")[:, q * 11:(q + 1) * 11, :], writes=[("actT", q)])
        ak = [("actT", q) for q in range(4)]
        stg = [sb("wstg7_%d" % i, [128, 4, 256], F32) for i in range(4)]
        ps = [st.enter_context(nc.psum_tensor("p5d_%d" % i, [128, 512], F32)) for i in range(2)]
        gi = 0
        Wd = [sb("Wd%d" % i, [128, NF, 256], BF16) for i in range(2)]
        x1 = [sb("x1_%d" % i, [128, 256], F32) for i in range(2)]
        yo = [sb("yo%d" % i, [128, 256], F32) for i in range(2)]
        wdv = K.w_ffn_down.rearrange("(k p) n -> p k n", p=128)
        engs = ["pool", "dve", "act"]

        def load_wd(cg, defer=None):
            wsl = cg % 2
            for k0 in range(0, NF, 4):
                if defer is not None:
                    defer.append(lambda k0=k0: load_wd_piece(cg, wsl, k0))
                else:
                    load_wd_piece(cg, wsl, k0)

        def load_wd_piece(cg, wsl, k0):
            if True:
                i = K.wcnt
                K.wcnt += 1
                sl = i % 4
                P.dma("sp", stg[sl][:, 0:4, 0:256], wdv[:, k0:k0 + 4, cg * 256:(cg + 1) * 256], writes=[("wstg", sl)])
                eng = engs[i % 3]
                o_ = Wd[wsl][:, k0:k0 + 4, :]
                if eng == "act":
                    P.op("act", lambda e, o_=o_, sl=sl: e.copy(out=o_, in_=stg[sl][:, 0:4, 0:256]), reads=[("wstg", sl)], writes=[("Wd", wsl, k0)])
                else:
                    P.op(eng, lambda e, o_=o_, sl=sl: e.tensor_copy(out=o_, in_=stg[sl][:, 0:4, 0:256]), reads=[("wstg", sl)], writes=[("Wd", wsl, k0)])
        load_wd(0)
        for cg in range(8):
            wsl = cg % 2
            cs = slice(cg * 256, (cg + 1) * 256)
            pend = []
            if cg + 1 < 8:
                load_wd(cg + 1, defer=pend)
            for t in range(8):
                b = gi % 2
                gi += 1
                if cg == 0 and t == 0:
                    P.dma("sp", x1[b][:], K.x1_d[0:128, cs], writes=[("x1", b)])
                nt_, ncg_ = (t + 1, cg) if t + 1 < 8 else (0, cg + 1)
                if ncg_ < 8:
                    P.dma("sp", x1[1 - b][:], K.x1_d[nt_ * 128:(nt_ + 1) * 128, ncg_ * 256:(ncg_ + 1) * 256], writes=[("x1", 1 - b)])
                for _ in range(2):
                    if pend:
                        pend.pop(0)()
                for f in range(NF):
                    P.op("pe", lambda e, b=b, f=f, t=t, wsl=wsl: e.matmul(ps[b][:, 0:256], lhsT=actT[:, f, t * 128:(t + 1) * 128], rhs=Wd[wsl][:, f, :],
                                                                          start=(f == 0), stop=(f == NF - 1)),
                         reads=[("actT", f // 11), ("Wd", wsl, (f // 4) * 4)], writes=[("p5c", b)])
                P.op("dve", lambda e, b=b, cs=cs: e.tensor_tensor(out=yo[b][:], in0=ps[b][:, 0:256], in1=GT2[:, cs], op=ALU.mult),
                     reads=[("p5c", b), "GT2"], writes=[("yo", b)])
                P.op("pool", lambda e, b=b: e.tensor_tensor(out=yo[b][:], in0=yo[b][:], in1=x1[b][:], op=ALU.add),
                     reads=[("yo", b), ("x1", b)], writes=[("yo", b)])
                P.dma("sp", K.out[t * 128:(t + 1) * 128, cs], yo[b][:], reads=[("yo", b)], writes=[("out", t, cg)])
        P.flush()


def phase_final_copy(K):
    nc, P = K.nc, K.P
    with contextlib.ExitStack() as st:
        xt = [st.enter_context(nc.sbuf_tensor("fx%d" % i, [128, D], F32)) for i in range(2)]
        for t in range(8):
            s = t % 2
            P.dma("sp", xt[s][:], K.x_own[t * 128:(t + 1) * 128, :], writes=[("fx", s)])
            P.dma("sp", K.out[t * 128:(t + 1) * 128, :], xt[s][:], reads=[("fx", s)], writes=[("out", t)])
        P.flush()


def own_tiles(j):
    r = []
    for m in range(4):
        r += [8 * m + j, 8 * m + 7 - j]
    return r


def build_program(debug=False, stages=99, cts=range(2), dbg_list=None, skip_att=False):
    nc = bass.Bass("TRN2", target_bir_lowering=False)
    K = Ctx()
    K.stages = stages
    K.cts = cts
    K.skip_att = skip_att
    K.nc = nc
    K.dbg = {}
    K.wcnt = 0

    def inp(name, shape, dt=F32):
        return nc.dram_tensor(name, list(shape), dt, kind="ExternalInput").ap()

    def scratch(name, shape, dt):
        return nc.dram_tensor(name, list(shape), dt, kind="Internal").ap()

    K.x_full = inp("x_full", [S, D])
    K.x_own = inp("x_own", [OWN, D])
    K.c_arr = inp("c_arr", [128, 16])
    K.pos_full = inp("pos_full", [128, 32], I32)
    K.invf_att = inp("invf_att", [128, 16])
    K.invf_idx = inp("invf_idx", [128, 8])
    K.w_ada = inp("w_ada", [D, 3072])
    K.b_ada = inp("b_ada", [3072])
    K.norm1_g = inp("norm1_g", [D])
    K.k_norm_g = inp("k_norm_g", [128])
    K.q_norm_g = inp("q_norm_g", [128])
    K.pos_own = inp("pos_own", [128, 8], I32)
    K.qpos_own = inp("qpos_own", [128, 8])
    K.w_in = inp("w_in", [D, 4176])
    K.rw_prm = [inp("rwp%d" % i, [128, 2]) for i in range(10)]
    K.w_in_rw = inp("w_in_rw", [D, 1216])
    K.rw_mul = inp("rw_mul", [128, 4])
    K.rw_w_up = inp("rw_w_up", [96, 256])
    K.rw_a_up = inp("rw_a_up", [96, 256])
    K.rw_g_up = inp("rw_g_up", [256, 256])
    K.qpos_row = inp("qpos_row", [OWN])
    K.w_out = inp("w_out", [D, D])
    K.norm2_g = inp("norm2_g", [D])
    K.w_ffn_gate = inp("w_ffn_gate", [D, 5632])
    K.w_ffn_up = inp("w_ffn_up", [D, 5632])
    K.w_ffn_down = inp("w_ffn_down", [5632, D])
    K.out = nc.dram_tensor("y_own", [OWN, D], F32, kind="ExternalOutput").ap()
    K.modq_d = scratch("modq_d", [1, 3072], F32)
    K.mod4_d = scratch("mod4_d", [4, 3072], F32)
    K.mod_d = K.mod4_d.rearrange("a n -> (a n)")
    K.hT_d = scratch("hT_d", [16, 128, S], BF16)
    K.kT_d = scratch("kT_d", [8, 128, S], BF16)
    K.v_d = scratch("v_d", [S, 8 * 129], BF16)
    K.ikT_d = scratch("ikT_d", [64, S], BF16)
    K.yT_d = scratch("yT_d", [1216, S], F32)
    K.qT_d = scratch("qT_d", [8, 128, OWN], BF16)
    K.iqT_d = scratch("iqT_d", [64, OWN, 16], BF16)
    K.iw_d = scratch("iw_d", [OWN, 16], F32)
    K.attT_d = scratch("attT_d", [8, 128, OWN], BF16)
    for nm in ("vb_d", "G_d", "BON_d", "AH_d", "RH_d", "BH_d", "KH_d"):
        setattr(K, nm, scratch(nm, [256, S], BF16))
    K.PC_d = scratch("PC_d", [256, NCH], F32)
    K.oT_d = scratch("oT_d", [256, S], F32)
    K.ro_loc_d = [scratch("ro_loc%d_d" % i, [2048, 256], BF16) for i in range(2)]
    K.ro_all_d = [scratch("ro_all%d_d" % i, [8192, 256], BF16) for i in range(2)]
    K.mixT_d = scratch("mixT_d", [16, 128, OWN], BF16)
    K.x1_d = scratch("x1_d", [OWN, D], F32)
    K.h2T_d = scratch("h2T_d", [16, 128, OWN], BF16)
    K.actT_d = scratch("actT_d", [44, 128, OWN], BF16)
    with contextlib.ExitStack() as stack:
        K.P = Prog(nc, stack)
        phase0_adaln(K)
        phase1_kv(K)
        if K.stages >= 2:
            phase1b_rwkv_proj(K)
        if K.stages >= 3 and not getattr(K, "skip_att", False):
            phase2_own_proj(K)
            phase3_attention(K)
        if K.stages >= 4:
            phase4b_rwkv_prep(K, cts=K.cts)
            if K.stages >= 5:
                phase4c_rwkv_scan(K, heads=[h for ct in K.cts for h in (2 * ct, 2 * ct + 1)])
        if K.stages >= 6:
            phase4d_rwkv_post(K, cts=K.cts)
            phase4e_allgather(K)
        if K.stages >= 7:
            phase5a_select(K)
            phase5b_outproj(K)
            phase5c_ffn(K)
        else:
            phase_final_copy(K)
        if debug:
            P = K.P
            allc = (("dbg_mixT", K.mixT_d, [16, 128, OWN], BF16), ("dbg_x1", K.x1_d, [OWN, D], F32),
                    ("dbg_oT", K.oT_d, [256, S], F32), ("dbg_AH", K.AH_d, [256, S], BF16), ("dbg_BH", K.BH_d, [256, S], BF16),
                    ("dbg_KH", K.KH_d, [256, S], BF16), ("dbg_RH", K.RH_d, [256, S], BF16), ("dbg_PC", K.PC_d, [256, NCH], F32),
                    ("dbg_G", K.G_d, [256, S], BF16), ("dbg_BON", K.BON_d, [256, S], BF16), ("dbg_vb", K.vb_d, [256, S], BF16),
                    ("dbg_yT", K.yT_d, [1216, S], F32), ("dbg_attT", K.attT_d, [8, 128, OWN], BF16),
                                     ("dbg_qT", K.qT_d, [8, 128, OWN], BF16), ("dbg_iqT", K.iqT_d, [64, OWN, 16], BF16),
                                     ("dbg_iw", K.iw_d, [OWN, 16], F32))
            for nm, src, shp, dt in allc:
                if dbg_list is not None and nm not in dbg_list:
                    continue
                o = dbg_out(K, nm, shp, dt)
                P.dma("sp", o, src, writes=[nm])
            P.flush()
    return nc, K


def make_in_maps(inputs, cores=range(8)):
    x = np.asarray(inputs["x"], dtype=np.float32)
    c = np.asarray(inputs["c"], dtype=np.float32)
    pos = np.asarray(inputs["positions"], dtype=np.int32)
    invf_att = (np.float32(500000.0) ** (-np.arange(16, dtype=np.float32) / np.float32(16))).astype(np.float32)
    invf_idx = (np.float32(500000.0) ** (-np.arange(8, dtype=np.float32) / np.float32(8))).astype(np.float32)
    mu = np.asarray(inputs["rwkv_mu"][0], dtype=np.float32)

    vecs = [mu[0:1024], mu[1024:2048], mu[2048:3072], inputs["rwkv_w0"][0], inputs["rwkv_a0"][0], inputs["rwkv_k_k"][0],
            inputs["rwkv_k_a"][0], np.asarray(inputs["rwkv_r_k"][0]).reshape(-1), inputs["rwkv_lnx_g"][0], inputs["rwkv_lnx_b"][0]]
    w_in_full = np.asarray(inputs["w_in"][0], dtype=np.float32)
    rw_mul = np.zeros((128, 4), np.float32)
    rw_mul[:96, 0] = mu[3072:3168]
    rw_mul[:96, 1] = mu[3168:3264]
    rw_mul[:, 2] = mu[3264:3392]
    rw_mul[:, 3] = mu[3392:3520]
    maps = []
    for core in cores:
        b, j = core // 4, core % 4
        ch = slice(256 * j, 256 * j + 256)
        rwp = {"rwp%d" % i: np.ascontiguousarray(np.asarray(v, dtype=np.float32)[ch].reshape(2, 128).T) for i, v in enumerate(vecs)}
        R0 = 4176
        w_in_rw = np.ascontiguousarray(np.concatenate([w_in_full[:, R0 + 256 * j:R0 + 256 * j + 256],
                                                       w_in_full[:, R0 + 1024 + 256 * j:R0 + 1024 + 256 * j + 256],
                                                       w_in_full[:, R0 + 2048 + 256 * j:R0 + 2048 + 256 * j + 256],
                                                       w_in_full[:, R0 + 3072:R0 + 3520]], axis=1))
        tiles = own_tiles(j)
        idx = np.concatenate([np.arange(t * 128, (t + 1) * 128) for t in tiles])
        maps.append({
            "x_full": np.ascontiguousarray(x[b]),
            "x_own": np.ascontiguousarray(x[b][idx]),
            "c_arr": np.ascontiguousarray(c[b].reshape(16, 128).T),
            "pos_full": np.ascontiguousarray(pos[b].reshape(32, 128).T),
            "invf_att": np.ascontiguousarray(np.broadcast_to(invf_att, (128, 16))),
            "invf_idx": np.ascontiguousarray(np.broadcast_to(invf_idx, (128, 8))),
            "w_ada": np.ascontiguousarray(np.asarray(inputs["w_ada"][0], dtype=np.float32)[:, 3072 * j:3072 * (j + 1)]),
            "b_ada": np.ascontiguousarray(np.asarray(inputs["b_ada"][0], dtype=np.float32)[3072 * j:3072 * (j + 1)]),
            "norm1_g": np.asarray(inputs["norm1_g"][0], dtype=np.float32),
            "k_norm_g": np.asarray(inputs["k_norm_g"][0], dtype=np.float32),
            "q_norm_g": np.asarray(inputs["q_norm_g"][0], dtype=np.float32),
            "pos_own": np.ascontiguousarray(pos[b][idx].reshape(8, 128).T),
            "qpos_own": np.ascontiguousarray(idx.astype(np.float32).reshape(8, 128).T),
            "w_in": np.ascontiguousarray(w_in_full[:, 0:4176]),
            "qpos_row": idx.astype(np.float32),
            "w_out": np.asarray(inputs["w_out"][0], dtype=np.float32),
            "norm2_g": np.asarray(inputs["norm2_g"][0], dtype=np.float32),
            "w_ffn_gate": np.asarray(inputs["w_ffn_gate"][0], dtype=np.float32),
            "w_ffn_up": np.asarray(inputs["w_ffn_up"][0], dtype=np.float32),
            "w_ffn_down": np.asarray(inputs["w_ffn_down"][0], dtype=np.float32),
            "rw_w_up": np.ascontiguousarray(np.asarray(inputs["rwkv_w_up"][0], dtype=np.float32)[:, ch]),
            "rw_a_up": np.ascontiguousarray(np.asarray(inputs["rwkv_a_up"][0], dtype=np.float32)[:, ch]),
            "rw_g_up": np.ascontiguousarray(np.asarray(inputs["rwkv_g_up"][0], dtype=np.float32)[:, ch]),
            "w_in_rw": w_in_rw,
            "rw_mul": rw_mul,
            **rwp,
        })
    return maps


def kernel(**inputs):
    nc, K = build_program(debug=False)
    maps = make_in_maps(inputs)
    res = run_bass_kernel_spmd(nc, maps, core_ids=list(range(8)))
    out = np.zeros((2, S, D), dtype=np.float32)
    for core in range(8):
        b, j = core // 4, core % 4
        y = res.results[core]["y_own"]
        for i, t in enumerate(own_tiles(j)):
            out[b, t * 128:(t + 1) * 128] = y[i * 128:(i + 1) * 128]
    return out
```

```python
import contextlib
import numpy as np
import concourse.bass as bass
import concourse.mybir as mybir
from concourse.bass_utils import run_bass_kernel_spmd

F32 = mybir.dt.float32
BF16 = mybir.dt.bfloat16
I32 = mybir.dt.int32
AF = mybir.ActivationFunctionType
ALU = mybir.AluOpType
AX = mybir.AxisListType

D = 2048
S = 4096
NT = 32
OWN = 1024
ENGS = ("pe", "act", "dve", "pool", "sp")
DEBUG = {}


class _Op:
    __slots__ = ("eng", "fn", "deps", "needs_inc", "is_dma", "sem", "count", "idx", "prev_same_sem", "is_cc")

    def __init__(self, eng, fn, is_dma):
        self.eng = eng
        self.fn = fn
        self.deps = set()
        self.needs_inc = False
        self.is_dma = is_dma
        self.sem = None
        self.count = 0
        self.prev_same_sem = None
        self.is_cc = False


class Prog:
    def __init__(self, nc, stack, n_dma_sems=48):
        self.nc = nc
        self.n_dma_sems = n_dma_sems
        self.eng_sem = {e: stack.enter_context(nc.semaphore("s_" + e)) for e in ENGS}
        self.dma_sems = [stack.enter_context(nc.semaphore("d%d" % i)) for i in range(n_dma_sems)]
        self.bar_sem = stack.enter_context(nc.semaphore("bar"))
        self.cc_sem = stack.enter_context(nc.semaphore("ccs"))
        self.cc_cnt = 0
        self.cnt = {e: 0 for e in ENGS}
        self.dcnt = [0] * n_dma_sems
        self.rr = 0
        self.nbar = 0
        self._reset()

    def _reset(self):
        self.ops = []
        self.last_writer = {}
        self.readers = {}

    def _record(self, op, reads, writes):
        idx = len(self.ops)
        op.idx = idx
        deps = set()
        for k in reads:
            w = self.last_writer.get(k)
            if w is not None:
                deps.add(w)
        for k in writes:
            w = self.last_writer.get(k)
            if w is not None:
                deps.add(w)
            for r in self.readers.get(k, ()):
                deps.add(r)
        deps.discard(idx)
        op.deps = deps
        self.ops.append(op)
        for k in reads:
            self.readers.setdefault(k, []).append(idx)
        for k in writes:
            self.last_writer[k] = idx
            self.readers[k] = []
        return idx

    def op(self, eng, fn, reads=(), writes=()):
        return self._record(_Op(eng, fn, False), reads, writes)

    def dma(self, queue, out, in_, reads=(), writes=(), **kw):
        def fn(e, out=out, in_=in_, kw=kw):
            return e.dma_start(out=out, in_=in_, **kw)
        return self._record(_Op(queue, fn, True), reads, writes)

    def coll(self, fn, reads=(), writes=()):
        o = _Op("pool", fn, True)
        o.is_cc = True
        return self._record(o, reads, writes)

    def flush(self):
        nc = self.nc
        ops = self.ops
        for o in ops:
            nd = set()
            for d in o.deps:
                p = ops[d]
                if o.eng == "pe" and p.eng == "pe" and not p.is_dma and not o.is_dma:
                    continue
                nd.add(d)
                p.needs_inc = True
            o.deps = nd
        last_of = {}
        for o in ops:
            if not o.is_dma:
                last_of[o.eng] = o
        for o in last_of.values():
            o.needs_inc = True
        dlast = [None] * self.n_dma_sems
        for o in ops:
            if o.is_cc:
                self.cc_cnt += 1
                o.sem = self.cc_sem
                o.count = self.cc_cnt
            elif o.is_dma:
                s = self.rr % self.n_dma_sems
                self.rr += 1
                o.prev_same_sem = dlast[s]
                self.dcnt[s] += 16
                o.sem = self.dma_sems[s]
                o.count = self.dcnt[s]
                dlast[s] = o.idx
            elif o.needs_inc:
                self.cnt[o.eng] += 1
                o.sem = self.eng_sem[o.eng]
                o.count = self.cnt[o.eng]
        per_eng = {e: [o for o in ops if o.eng == e] for e in ENGS}
        final = [(self.dma_sems[s], self.dcnt[s]) for s in range(self.n_dma_sems) if self.dcnt[s] > 0]
        final += [(self.eng_sem[e], self.cnt[e]) for e in ENGS if self.cnt[e] > 0]
        if self.cc_cnt > 0:
            final.append((self.cc_sem, self.cc_cnt))
        self.nbar += 1
        nbar = self.nbar
        bar = self.bar_sem

        def run(e_name, eng):
            waited = {}
            for o in per_eng[e_name]:
                need = {}
                for d in o.deps:
                    p = ops[d]
                    if need.get(p.sem.num, (0, None))[0] < p.count:
                        need[p.sem.num] = (p.count, p.sem)
                if o.is_dma and o.prev_same_sem is not None:
                    p = ops[o.prev_same_sem]
                    if need.get(p.sem.num, (0, None))[0] < p.count:
                        need[p.sem.num] = (p.count, p.sem)
                for key, (c, s) in need.items():
                    if waited.get(key, 0) < c:
                        eng.wait_ge(s, c)
                        waited[key] = c
                ins = o.fn(eng)
                if o.is_cc:
                    ins.then_inc(o.sem)
                elif o.is_dma:
                    ins.then_inc(o.sem, 16)
                elif o.needs_inc:
                    ins.then_inc(o.sem, 1)
            if e_name == "sp":
                for s, c in final:
                    eng.wait_ge(s, c)
                eng.sem_inc(bar, 1)
            eng.wait_ge(bar, nbar)

        with nc.Block() as block:
            @block.tensor
            def _(e):
                run("pe", e)

            @block.scalar
            def _(e):
                run("act", e)

            @block.vector
            def _(e):
                run("dve", e)

            @block.gpsimd
            def _(e):
                run("pool", e)

            @block.sync
            def _(e):
                run("sp", e)
        self._reset()


class Ctx:
    pass


def bcast_rows(ap1d, n):
    return bass.AP(ap1d.tensor, ap1d.offset, [[0, 128], [1, n]])


def dbg_out(K, name, shape, dtype=F32):
    t = K.nc.dram_tensor(name, list(shape), dtype, kind="ExternalOutput")
    K.dbg[name] = t
    return t.ap()


def make_ident(K, P, ident):
    P.op("pool", lambda e: e.memset(ident[:], 0.0), writes=["ident"])
    P.op("pool", lambda e: e.affine_select(out=ident[:], in_=ident[:], pattern=[[-1, 128]],
                                           compare_op=ALU.not_equal, fill=1.0, base=0,
                                           channel_multiplier=1),
         reads=["ident"], writes=["ident"])


def phase0_adaln(K):
    nc, P = K.nc, K.P
    NQ = 3072
    with contextlib.ExitStack() as st:
        c_sb = st.enter_context(nc.sbuf_tensor("c_sb", [128, 16], F32))
        cact = st.enter_context(nc.sbuf_tensor("cact", [128, 16], F32))
        wst = [st.enter_context(nc.sbuf_tensor("wst%d" % i, [128, 16, 512], F32)) for i in range(2)]
        modrow = st.enter_context(nc.sbuf_tensor("modrow", [1, NQ], F32))
        brow = st.enter_context(nc.sbuf_tensor("brow", [1, NQ], F32))
        ps = [st.enter_context(nc.psum_tensor("ps0_%d" % i, [1, 512], F32)) for i in range(2)]
        P.dma("sp", c_sb[:], K.c_arr, writes=["c_sb"])
        P.dma("sp", brow[:], K.b_ada.rearrange("(o n) -> o n", o=1), writes=["brow"])
        P.op("act", lambda e: e.activation(out=cact[:], in_=c_sb[:], func=AF.Silu),
             reads=["c_sb"], writes=["cact"])
        wv = K.w_ada.rearrange("(k p) n -> p k n", p=128)
        for nt in range(NQ // 512):
            sl = nt % 2
            for hh in range(2):
                P.dma("sp", wst[sl][:, hh * 8:(hh + 1) * 8, :],
                      wv[:, hh * 8:(hh + 1) * 8, nt * 512:(nt + 1) * 512],
                      writes=[("wst", sl, hh)])
            for k in range(16):
                P.op("pe", lambda e, k=k, sl=sl: e.matmul(ps[sl][:, :], lhsT=cact[:, k:k + 1],
                                                         rhs=wst[sl][:, k, :], start=(k == 0), stop=(k == 15)),
                     reads=["cact", ("wst", sl, k // 8)], writes=[("ps0", sl)])
            P.op("dve", lambda e, nt=nt, sl=sl: e.tensor_tensor(
                out=modrow[0:1, nt * 512:(nt + 1) * 512], in0=ps[sl][:, :],
                in1=brow[0:1, nt * 512:(nt + 1) * 512], op=ALU.add),
                reads=[("ps0", sl), "brow"], writes=[("modrow", nt)])
        P.dma("sp", K.modq_d, modrow[:],
              reads=[("modrow", nt) for nt in range(NQ // 512)], writes=["modq_d"])
        P.flush()
    P.coll(lambda e: e.collective_compute("AllGather", ALU.bypass, replica_groups=[[0, 1, 2, 3], [4, 5, 6, 7]],
                                          ins=[K.modq_d.opt()], outs=[K.mod4_d.opt()]), reads=["modq_d"], writes=["mod4"])
    P.flush()


def load_mod_rows(K, P, tile, which, gain_ap=None, key=None):
    src = K.mod_d[which * D:(which + 1) * D]
    P.dma("sp", tile[:], bcast_rows(src, D), writes=[key])


def bc(ap, shape):
    return ap.to_broadcast(list(shape))


def load_weight_bf16(K, P, st_tiles, dst, c_dst, src2d, ncols, tag, defer=None):
    wv = src2d.rearrange("(k p) n -> p k n", p=128)
    nk = wv.shape[1]
    engs = ["pool", "dve", "act"]
    for c0 in range(0, ncols, 512):
        n = min(512, ncols - c0)
        for k0 in range(0, nk, 4):
            kn = min(4, nk - k0)
            if defer is not None:
                defer.append(lambda c0=c0, n=n, k0=k0, kn=kn: _load_piece(K, P, st_tiles, dst, c_dst, wv, tag, engs, c0, n, k0, kn))
                continue
            _load_piece(K, P, st_tiles, dst, c_dst, wv, tag, engs, c0, n, k0, kn)
    return [(tag, c0, k0) for c0 in range(0, ncols, 512) for k0 in range(0, nk, 4)]


def _load_piece(K, P, st_tiles, dst, c_dst, wv, tag, engs, c0, n, k0, kn):
    if True:
        if True:
            i = K.wcnt
            K.wcnt += 1
            sl = i % len(st_tiles)
            stg = st_tiles[sl]
            P.dma("sp", stg[:, 0:kn, 0:n], wv[:, k0:k0 + kn, c0:c0 + n], writes=[("wstg", sl)])
            eng = engs[i % 3]
            o = dst[:, k0:k0 + kn, c_dst + c0:c_dst + c0 + n]
            if eng == "act":
                P.op("act", lambda e, o=o, stg=stg, kn=kn, n=n: e.copy(out=o, in_=stg[:, 0:kn, 0:n]),
                     reads=[("wstg", sl)], writes=[(tag, c0, k0)])
            else:
                P.op(eng, lambda e, o=o, stg=stg, kn=kn, n=n: e.tensor_copy(out=o, in_=stg[:, 0:kn, 0:n]),
                     reads=[("wstg", sl)], writes=[(tag, c0, k0)])


def rope_tables(K, P, st, pos_arr, ntile, invf_att, invf_idx, tag):
    nc = K.nc
    posi = st.enter_context(nc.sbuf_tensor(tag + "posi", [128, ntile], I32))
    posf = st.enter_context(nc.sbuf_tensor(tag + "posf", [128, ntile], F32))
    iva = st.enter_context(nc.sbuf_tensor(tag + "iva", [128, 16], F32))
    ivi = st.enter_context(nc.sbuf_tensor(tag + "ivi", [128, 8], F32))
    P.dma("sp", posi[:], pos_arr, writes=[tag + "posi"])
    P.dma("sp", iva[:], invf_att, writes=[tag + "iva"])
    P.dma("sp", ivi[:], invf_idx, writes=[tag + "ivi"])
    P.op("dve", lambda e: e.tensor_copy(out=posf[:], in_=posi[:]), reads=[tag + "posi"], writes=[tag + "posf"])
    out = {}
    for nm, iv, h in (("a", iva, 16), ("i", ivi, 8)):
        u = st.enter_context(nc.sbuf_tensor(tag + "u" + nm, [128, ntile, h], F32))
        ui = st.enter_context(nc.sbuf_tensor(tag + "ui" + nm, [128, ntile, h], I32))
        uf = st.enter_context(nc.sbuf_tensor(tag + "uf" + nm, [128, ntile, h], F32))
        for fn, off in (("sin", 0.0), ("cos", 0.25)):
            tb = st.enter_context(nc.sbuf_tensor(tag + fn + nm, [128, ntile, h], F32))
            kk = tag + fn + nm
            P.op("dve", lambda e, u=u, iv=iv, h=h: e.tensor_tensor(
                out=u[:], in0=bc(posf[:].unsqueeze(2), [128, ntile, h]),
                in1=bc(iv[:].unsqueeze(1), [128, ntile, h]), op=ALU.mult),
                reads=[tag + "posf", tag + "iv" + nm], writes=[tag + "U" + nm])
            P.op("dve", lambda e, u=u, off=off: e.tensor_scalar(
                out=u[:], in0=u[:], scalar1=float(1.0 / (2 * np.pi)), scalar2=off, op0=ALU.mult, op1=ALU.add),
                reads=[tag + "U" + nm], writes=[tag + "U" + nm])
            P.op("dve", lambda e, u=u, ui=ui: e.tensor_copy(out=ui[:], in_=u[:]), reads=[tag + "U" + nm], writes=[tag + "UI" + nm])
            P.op("dve", lambda e, uf=uf, ui=ui: e.tensor_copy(out=uf[:], in_=ui[:]), reads=[tag + "UI" + nm], writes=[tag + "UF" + nm])
            P.op("dve", lambda e, u=u, uf=uf: e.tensor_tensor(out=u[:], in0=u[:], in1=uf[:], op=ALU.subtract),
                 reads=[tag + "U" + nm, tag + "UF" + nm], writes=[tag + "U" + nm])
            P.op("dve", lambda e, u=u: e.tensor_scalar(out=u[:], in0=u[:], scalar1=-0.5, scalar2=0.5,
                                                        op0=ALU.max, op1=ALU.min),
                 reads=[tag + "U" + nm], writes=[tag + "U" + nm])
            P.op("act", lambda e, u=u, tb=tb: e.activation(out=tb[:], in_=u[:], func=AF.Sin,
                                                            scale=float(2 * np.pi)),
                 reads=[tag + "U" + nm], writes=[kk])
            out[fn + nm] = (tb, kk)
    return out


def apply_rope(P, eng, x4, cos, sin, t, half, tmp, rk, wk, sfx=""):
    ctb, ck = cos
    stb, sk = sin
    H = x4.shape[1]
    x1 = x4[:, :, 0:half]
    x2 = x4[:, :, half:2 * half]
    cb = bc(ctb[:, t, :].unsqueeze(1), [128, H, half])
    sb = bc(stb[:, t, :].unsqueeze(1), [128, H, half])
    a, b2, c, d = tmp
    P.op(eng, lambda e: e.tensor_tensor(out=a[:, 0:H, 0:half], in0=x1, in1=cb, op=ALU.mult), reads=rk + [ck], writes=["rtmpA" + sfx])
    P.op(eng, lambda e: e.tensor_tensor(out=b2[:, 0:H, 0:half], in0=x2, in1=sb, op=ALU.mult), reads=rk + [sk], writes=["rtmpB" + sfx])
    P.op(eng, lambda e: e.tensor_tensor(out=c[:, 0:H, 0:half], in0=x2, in1=cb, op=ALU.mult), reads=rk + [ck], writes=["rtmpC" + sfx])
    P.op(eng, lambda e: e.tensor_tensor(out=d[:, 0:H, 0:half], in0=x1, in1=sb, op=ALU.mult), reads=rk + [sk], writes=["rtmpD" + sfx])
    P.op(eng, lambda e: e.tensor_tensor(out=x1, in0=a[:, 0:H, 0:half], in1=b2[:, 0:H, 0:half], op=ALU.subtract),
         reads=["rtmpA" + sfx, "rtmpB" + sfx, "rtmpC" + sfx, "rtmpD" + sfx] + rk, writes=rk)
    P.op(eng, lambda e: e.tensor_tensor(out=x2, in0=c[:, 0:H, 0:half], in1=d[:, 0:H, 0:half], op=ALU.add),
         reads=["rtmpC" + sfx, "rtmpD" + sfx] + rk, writes=rk)


def head_rmsnorm(P, x3, gain, sq, ssum, rk, wk, gk=None, sqk=None):
    P.op("pool", lambda e: e.tensor_tensor(out=sq[:], in0=x3, in1=x3, op=ALU.mult), reads=rk, writes=[sqk or (wk + "sq")])
    P.op("dve", lambda e: e.tensor_reduce(out=ssum[:, 0:8], in_=sq[:], axis=AX.X, op=ALU.add),
         reads=[sqk or (wk + "sq")], writes=[wk + "s0"])
    P.op("dve", lambda e: e.tensor_scalar(out=ssum[:, 8:16], in0=ssum[:, 0:8], scalar1=1.0 / 128, scalar2=1e-6,
                                           op0=ALU.mult, op1=ALU.add), reads=[wk + "s0"], writes=[wk + "s1"])
    P.op("act", lambda e: e.activation(out=ssum[:, 16:24], in_=ssum[:, 8:16], func=AF.Sqrt),
         reads=[wk + "s1"], writes=[wk + "s2"])
    P.op("dve", lambda e: e.reciprocal(out=ssum[:, 24:32], in_=ssum[:, 16:24]), reads=[wk + "s2"], writes=[wk + "s3"])
    P.op("dve", lambda e: e.tensor_tensor(out=x3, in0=x3, in1=bc(ssum[:, 24:32].unsqueeze(2), [128, 8, 128]),
                                           op=ALU.mult), reads=rk + [wk + "s3"], writes=rk)
    P.op("pool", lambda e: e.tensor_tensor(out=x3, in0=x3, in1=bc(gain[:].unsqueeze(1), [128, 8, 128]),
                                            op=ALU.mult), reads=rk + [gk or ("gain" + wk)], writes=rk)


def norm_load(K, P, T, x_src, t):
    xs = t % 2
    P.dma("sp", T["xt"][xs][:], x_src[t * 128:(t + 1) * 128, :], writes=[("xt", xs)])


def norm_chain(K, P, T, x_src, t, G1, SH1, load=True, xg=None):
    xs = t % 2
    xt, hb, ss, junk = T["xt"], T["hb"], T["ss"], T["junk"]
    if load:
        norm_load(K, P, T, x_src, t)
    if xg is not None:
        xg_ap, xg_k = xg[xs]
        P.op("pool", lambda e: e.tensor_tensor(out=xg_ap, in0=xt[xs][:], in1=G1[:], op=ALU.mult),
             reads=[("xt", xs), "G"], writes=[xg_k])
    P.op("act", lambda e: e.activation(out=junk[:], in_=xt[xs][:], func=AF.Square, accum_out=ss[:, 0:1]),
         reads=[("xt", xs)], writes=["junk", "ss0"])
    P.op("dve", lambda e: e.tensor_scalar(out=ss[:, 1:2], in0=ss[:, 0:1], scalar1=1.0 / D, scalar2=1e-6,
                                           op0=ALU.mult, op1=ALU.add), reads=["ss0"], writes=["ss1"])
    P.op("act", lambda e: e.activation(out=ss[:, 2:3], in_=ss[:, 1:2], func=AF.Sqrt), reads=["ss1"], writes=["ss2"])
    P.op("dve", lambda e: e.reciprocal(out=ss[:, 3:4], in_=ss[:, 2:3]), reads=["ss2"], writes=["ss3"])
    if xg is not None:
        P.op("dve", lambda e: e.scalar_tensor_tensor(out=hb[xs][:], in0=xg_ap, scalar=ss[:, 3:4], in1=SH1[:],
                                                      op0=ALU.mult, op1=ALU.add),
             reads=[xg_k, "ss3", "SH"], writes=[("hb", xs)])
    else:
        P.op("dve", lambda e: e.scalar_tensor_tensor(out=xt[xs][:], in0=xt[xs][:], scalar=ss[:, 3:4], in1=G1[:],
                                                      op0=ALU.mult, op1=ALU.mult),
             reads=[("xt", xs), "ss3", "G"], writes=[("xt", xs)])
        P.op("pool", lambda e: e.tensor_tensor(out=hb[xs][:], in0=xt[xs][:], in1=SH1[:], op=ALU.add),
             reads=[("xt", xs), "SH"], writes=[("hb", xs)])


def norm_pe(K, P, T, t, ident, blk_hT, ti, hname="hT"):
    xs = t % 2
    hb, pT = T["hb"], T["pT"]
    for half in range(2):
        for kk in range(8):
            k = half * 8 + kk
            P.op("pe", lambda e, k=k, kk=kk, half=half: e.transpose(
                out=pT[half][:, kk, :], in_=hb[xs][:, k * 128:(k + 1) * 128], identity=ident[:]),
                reads=[("hb", xs), "ident"], writes=[("pT", half)])
        o = blk_hT[:, half * 8:(half + 1) * 8, ti * 128:(ti + 1) * 128]
        if half == 0:
            P.op("act", lambda e, o=o, half=half: e.copy(out=o, in_=pT[half][:]),
                 reads=[("pT", half)], writes=[(hname, ti, half)])
        else:
            P.op("dve", lambda e, o=o, half=half: e.tensor_copy(out=o, in_=pT[half][:]),
                 reads=[("pT", half)], writes=[(hname, ti, half)])


def norm_block(K, P, T, x_src, t, G1, SH1, ident, blk_hT, ti, load=True, hname="hT"):
    norm_chain(K, P, T, x_src, t, G1, SH1, load=load)
    norm_pe(K, P, T, t, ident, blk_hT, ti, hname=hname)


def norm_tiles_alloc(K, st, tag):
    nc = K.nc
    T = {}
    T["xt"] = [st.enter_context(nc.sbuf_tensor(tag + "xt%d" % i, [128, D], F32)) for i in range(2)]
    T["hb"] = [st.enter_context(nc.sbuf_tensor(tag + "hb%d" % i, [128, D], BF16)) for i in range(2)]
    T["ss"] = st.enter_context(nc.sbuf_tensor(tag + "ss", [128, 4], F32))
    T["junk"] = st.enter_context(nc.sbuf_tensor(tag + "junk", [128, D], BF16))
    T["pT"] = [st.enter_context(nc.psum_tensor(tag + "pT%d" % i, [128, 8, 128], BF16)) for i in range(2)]
    return T


def load_G_SH(K, P, st, which_sh, which_sc, gain_vec, tag):
    nc = K.nc
    G = st.enter_context(nc.sbuf_tensor(tag + "G", [128, D], F32))
    SH = st.enter_context(nc.sbuf_tensor(tag + "SH", [128, D], F32))
    gtmp = st.enter_context(nc.sbuf_tensor(tag + "gtmp", [128, D], F32))
    P.dma("sp", SH[:], bcast_rows(K.mod_d[which_sh * D:(which_sh + 1) * D], D), writes=["SH"])
    P.dma("sp", G[:], bcast_rows(K.mod_d[which_sc * D:(which_sc + 1) * D], D), writes=["G"])
    P.dma("sp", gtmp[:], bcast_rows(gain_vec, D), writes=["gtmp"])
    P.op("dve", lambda e: e.scalar_tensor_tensor(out=G[:], in0=G[:], scalar=1.0, in1=gtmp[:],
                                                  op0=ALU.add, op1=ALU.mult), reads=["G", "gtmp"], writes=["G"])
    K.last_gtmp = gtmp
    return G, SH


def phase1_kv(K):
    nc, P = K.nc, K.P
    with contextlib.ExitStack() as st:
        ident = st.enter_context(nc.sbuf_tensor("ident", [128, 128], BF16))
        make_ident(K, P, ident)
        G1, SH1 = load_G_SH(K, P, st, 0, 1, K.norm1_g, "p1")
        T = norm_tiles_alloc(K, st, "p1")
        hT = [st.enter_context(nc.sbuf_tensor("hT%d" % i, [128, 16, 512], BF16)) for i in range(2)]
        W = st.enter_context(nc.sbuf_tensor("Wkv", [128, 16, 2112], BF16))
        stg = [st.enter_context(nc.sbuf_tensor("wstg%d" % i, [128, 4, 512], F32)) for i in range(2)]
        wk_k = load_weight_bf16(K, P, stg, W, 0, K.w_in[:, 1024:2048], 1024, "Wk")
        wk_v = load_weight_bf16(K, P, stg, W, 1024, K.w_in[:, 2048:3072], 1024, "Wv")
        wk_i = load_weight_bf16(K, P, stg, W, 2048, K.w_in[:, 4096:4160], 64, "Wi")
        rt = rope_tables(K, P, st, K.pos_full, 32, K.invf_att, K.invf_idx, "rf")
        gain = st.enter_context(nc.sbuf_tensor("kgain", [128, 128], F32))
        P.dma("sp", gain[:], bcast_rows(K.k_norm_g, 128), writes=["gainK"])
        def two(name, shape, dt):
            return [st.enter_context(nc.sbuf_tensor(name + str(i), shape, dt)) for i in range(2)]
        ksb2 = two("ksb", [128, 8, 128], F32)
        kbf2 = two("kbf", [128, 8, 128], BF16)
        sq2 = [st.enter_context(nc.sbuf_tensor("sq", [128, 8, 128], F32))] * 2
        ssum2 = two("ssum", [128, 32], F32)
        rtmp2 = [[st.enter_context(nc.sbuf_tensor("rtmp%d" % i, [128, 8, 16], F32)) for i in range(4)]] * 2
        vsb2 = two("vsb", [128, 8, 129], BF16)
        iksb2 = two("iksb", [128, 1, 64], F32)
        ikbf2 = two("ikbf", [128, 64], BF16)
        kTs2 = [st.enter_context(nc.sbuf_tensor("kTs", [128, 8, 128], BF16))] * 2
        ikTs2 = two("ikTs", [64, 128], BF16)
        pm = [st.enter_context(nc.psum_tensor("pm%d" % i, [128, 512], F32)) for i in range(3)]
        pk = st.enter_context(nc.psum_tensor("pk", [128, 8, 128], BF16))
        pk2 = st.enter_context(nc.psum_tensor("pk2", [64, 128], BF16))
        for s_ in range(2):
            P.op("pool", lambda e, s_=s_: e.memset(vsb2[s_][:], 1.0), writes=["vsb%d" % s_])
        norm_load(K, P, T, K.x_full, 0)

        xg = [(K.last_gtmp[:], "gtmp"), (stg[0][:].rearrange("p a b -> p (a b)"), ("wstg", 0))]

        def norm_tile_chain(blk, ti):
            tt_ = blk * 4 + ti
            if tt_ + 1 < 32:
                norm_load(K, P, T, K.x_full, tt_ + 1)
            norm_chain(K, P, T, K.x_full, tt_, G1, SH1, load=False, xg=xg)

        def norm_tile_pe(blk, ti):
            norm_pe(K, P, T, blk * 4 + ti, ident, hT[blk % 2], ti, hname=("hT", blk % 2))

        def norm_tile(blk, ti):
            norm_tile_chain(blk, ti)
            norm_tile_pe(blk, ti)

        def store_hT(blk):
            hs = blk % 2
            hkeys = [(("hT", hs), ti, half) for ti in range(4) for half in range(2)]
            P.dma("sp", K.hT_d.rearrange("k p t -> p k t")[:, :, blk * 512:(blk + 1) * 512], hT[hs][:],
                  reads=hkeys, writes=[("hT_d", blk)])

        def bufs(t):
            u = t % 2
            return (str(u), ksb2[u], kbf2[u], sq2[u], ssum2[u], rtmp2[u], vsb2[u], iksb2[u], ikbf2[u], kTs2[u], ikTs2[u])

        def mm_tile(blk, ti):
            t = blk * 4 + ti
            hs = blk % 2
            hk = [(("hT", hs), ti, 0), (("hT", hs), ti, 1)]
            us, ksb, kbf, sq, ssum, rtmp, vsb, iksb, ikbf, kTs, ikTs = bufs(t)
            for gi, (c0, n, wkeys) in enumerate([(0, 512, wk_k), (512, 512, wk_k), (1024, 512, wk_v),
                                                 (1536, 512, wk_v), (2048, 64, wk_i)]):
                pb = pm[gi % 3]
                wtag, wc0 = [("Wk", 0), ("Wk", 512), ("Wv", 0), ("Wv", 512), ("Wi", 0)][gi]
                for k in range(16):
                    wkeys = [(wtag, wc0, 4 * (k // 4))]
                    P.op("pe", lambda e, pb=pb, k=k, c0=c0, n=n, ti=ti, hs=hs: e.matmul(
                        pb[:, 0:n], lhsT=hT[hs][:, k, ti * 128:(ti + 1) * 128], rhs=W[:, k, c0:c0 + n],
                        start=(k == 0), stop=(k == 15)), reads=hk + wkeys, writes=[("pm", gi % 3)])
                if gi < 2:
                    P.op("act", lambda e, pb=pb, gi=gi, ksb=ksb: e.copy(out=ksb[:, gi * 4:(gi + 1) * 4, :], in_=pb[:, 0:512]),
                         reads=[("pm", gi % 3)], writes=["ksb" + us])
                elif gi < 4:
                    g2 = gi - 2
                    P.op("act", lambda e, pb=pb, g2=g2, vsb=vsb: e.copy(out=vsb[:, g2 * 4:(g2 + 1) * 4, 0:128], in_=pb[:, 0:512]),
                         reads=[("pm", gi % 3)], writes=["vsb" + us])
                else:
                    P.op("act", lambda e, pb=pb, iksb=iksb: e.copy(out=iksb[:, 0, :], in_=pb[:, 0:64]),
                         reads=[("pm", gi % 3)], writes=["iksb" + us])
            P.dma("sp", K.v_d[t * 128:(t + 1) * 128, :], vsb[:].rearrange("p h d -> p (h d)"),
                  reads=["vsb" + us], writes=[("v_d", t)])

        def post1(blk, ti):
            t = blk * 4 + ti
            us, ksb, kbf, sq, ssum, rtmp, vsb, iksb, ikbf, kTs, ikTs = bufs(t)
            head_rmsnorm(P, ksb[:], gain, sq, ssum, ["ksb" + us], "K" + us, gk="gainK", sqk="Ksq")
            apply_rope(P, "dve", ksb[:], rt["cosa"], rt["sina"], t, 16, rtmp, ["ksb" + us], "rK")
            P.op("act", lambda e, kbf=kbf, ksb=ksb: e.copy(out=kbf[:], in_=ksb[:]), reads=["ksb" + us], writes=["kbf" + us])
            apply_rope(P, "pool", iksb[:], rt["cosi"], rt["sini"], t, 8, rtmp, ["iksb" + us], "rI")
            P.op("act", lambda e, ikbf=ikbf, iksb=iksb: e.copy(out=ikbf[:], in_=iksb[:, 0, :]), reads=["iksb" + us], writes=["ikbf" + us])

        def post2(blk, ti):
            t = blk * 4 + ti
            us, ksb, kbf, sq, ssum, rtmp, vsb, iksb, ikbf, kTs, ikTs = bufs(t)
            for h in range(8):
                P.op("pe", lambda e, h=h, kbf=kbf: e.transpose(out=pk[:, h, :], in_=kbf[:, h, :], identity=ident[:]),
                     reads=["kbf" + us, "ident"], writes=["pk"])
            P.op("dve", lambda e, kTs=kTs: e.tensor_copy(out=kTs[:], in_=pk[:]), reads=["pk"], writes=["kTs"])
            P.dma("sp", K.kT_d.rearrange("h p t -> p h t")[:, :, t * 128:(t + 1) * 128], kTs[:],
                  reads=["kTs"], writes=[("kT_d", t)])
            P.op("pe", lambda e, ikbf=ikbf: e.transpose(out=pk2[:, :], in_=ikbf[:], identity=ident[:]),
                 reads=["ikbf" + us, "ident"], writes=["pk2"])
            P.op("dve", lambda e, ikTs=ikTs: e.tensor_copy(out=ikTs[:], in_=pk2[:, :]), reads=["pk2"], writes=["ikTs" + us])
            P.dma("sp", K.ikT_d[:, t * 128:(t + 1) * 128], ikTs[:], reads=["ikTs" + us], writes=[("ikT_d", t)])

        for ti in range(4):
            norm_tile(0, ti)
        store_hT(0)
        prev = None
        for blk in range(8):
            for ti in range(4):
                if blk + 1 < 8:
                    norm_tile_chain(blk + 1, ti)
                mm_tile(blk, ti)
                if prev is not None:
                    post2(*prev)
                if blk + 1 < 8:
                    norm_tile_pe(blk + 1, ti)
                post1(blk, ti)
                prev = (blk, ti)
            if blk + 1 < 8:
                store_hT(blk + 1)
        post2(*prev)
        P.flush()

RW0 = 4176
NRW = 1216
RW_GROUPS = [(i * 128, 128) for i in range(6)] + [(768, 96), (864, 96), (960, 128), (1088, 128)]


def phase1b_rwkv_proj(K):
    nc, P = K.nc, K.P
    with contextlib.ExitStack() as st:
        W = st.enter_context(nc.sbuf_tensor("Wr", [128, 16, NRW], BF16))
        stg = [st.enter_context(nc.sbuf_tensor("wstgb%d" % i, [128, 4, 512], F32)) for i in range(2)]
        hT = [st.enter_context(nc.sbuf_tensor("hTb%d" % i, [128, 16, 512], BF16)) for i in range(2)]
        ost = [st.enter_context(nc.sbuf_tensor("ost%d" % i, [128, 512], F32)) for i in range(4)]
        pm = [st.enter_context(nc.psum_tensor("pmb%d" % i, [128, 512], F32)) for i in range(4)]
        wkeys = load_weight_bf16(K, P, stg, W, 0, K.w_in_rw, NRW, "Wr")
        cnt = 0
        def load_h(blk):
            P.dma("sp", hT[blk % 2][:], K.hT_d.rearrange("k p t -> p k t")[:, :, blk * 512:(blk + 1) * 512],
                  writes=[("hTb", blk % 2)])
        load_h(0)
        for blk in range(8):
            hs = blk % 2
            if blk + 1 < 8:
                load_h(blk + 1)
            for (r0, m) in RW_GROUPS:
                s4 = cnt % 4
                cnt += 1
                for k in range(16):
                    wkeys = [("Wr", c0_, 4 * (k // 4)) for c0_ in sorted({512 * (r0 // 512), 512 * ((r0 + m - 1) // 512)})]
                    P.op("pe", lambda e, k=k, r0=r0, m=m, hs=hs, s4=s4: e.matmul(
                        pm[s4][0:m, :], lhsT=W[:, k, r0:r0 + m], rhs=hT[hs][:, k, :],
                        start=(k == 0), stop=(k == 15)), reads=[("hTb", hs)] + wkeys, writes=[("pmb", s4)])
                if cnt % 2 == 0:
                    P.op("act", lambda e, m=m, s4=s4: e.copy(out=ost[s4][0:m, :], in_=pm[s4][0:m, :]),
                         reads=[("pmb", s4)], writes=[("ost", s4)])
                else:
                    P.op("dve", lambda e, m=m, s4=s4: e.tensor_copy(out=ost[s4][0:m, :], in_=pm[s4][0:m, :]),
                         reads=[("pmb", s4)], writes=[("ost", s4)])
                P.dma("sp", K.yT_d[r0:r0 + m, blk * 512:(blk + 1) * 512], ost[s4][0:m, :],
                      reads=[("ost", s4)], writes=[("yT_d", r0, blk)])
        P.flush()


def phase2_own_proj(K):
    nc, P = K.nc, K.P
    with contextlib.ExitStack() as st:
        ident = st.enter_context(nc.sbuf_tensor("ident2", [128, 128], BF16))
        make_ident(K, P, ident)
        G1, SH1 = load_G_SH(K, P, st, 0, 1, K.norm1_g, "p2")
        T = norm_tiles_alloc(K, st, "p2")
        hT = [st.enter_context(nc.sbuf_tensor("hTo%d" % i, [128, 16, 512], BF16)) for i in range(2)]
        W = st.enter_context(nc.sbuf_tensor("Wq", [128, 16, 2064], BF16))
        stg = [st.enter_context(nc.sbuf_tensor("wstgq%d" % i, [128, 4, 512], F32)) for i in range(2)]
        wk_q = load_weight_bf16(K, P, stg, W, 0, K.w_in[:, 0:1024], 1024, "Wq")
        wk_iq = load_weight_bf16(K, P, stg, W, 1024, K.w_in[:, 3072:4096], 1024, "Wiq")
        wk_iw = load_weight_bf16(K, P, stg, W, 2048, K.w_in[:, 4160:4176], 16, "Wiw")
        rt = rope_tables(K, P, st, K.pos_own, 8, K.invf_att, K.invf_idx, "ro")
        gain = st.enter_context(nc.sbuf_tensor("qgain", [128, 128], F32))
        P.dma("sp", gain[:], bcast_rows(K.q_norm_g, 128), writes=["gainQ"])
        qsb = st.enter_context(nc.sbuf_tensor("qsb", [128, 8, 128], F32))
        qbf = st.enter_context(nc.sbuf_tensor("qbf", [128, 8, 128], BF16))
        sq = st.enter_context(nc.sbuf_tensor("sq2", [128, 8, 128], F32))
        ssum = st.enter_context(nc.sbuf_tensor("ssum2", [128, 32], F32))
        rtmp = [st.enter_context(nc.sbuf_tensor("rtmpq%d" % i, [128, 16, 16], F32)) for i in range(4)]
        iqsb = st.enter_context(nc.sbuf_tensor("iqsb", [128, 16, 64], F32))
        iqbf = st.enter_context(nc.sbuf_tensor("iqbf", [128, 16, 64], BF16))
        iwsb = st.enter_context(nc.sbuf_tensor("iwsb", [128, 16], F32))
        qTs = st.enter_context(nc.sbuf_tensor("qTs", [128, 8, 128], BF16))
        iqTs = st.enter_context(nc.sbuf_tensor("iqTs", [64, 128, 16], BF16))
        pm = [st.enter_context(nc.psum_tensor("pmq%d" % i, [128, 512], F32)) for i in range(3)]
        pk = st.enter_context(nc.psum_tensor("pkq", [128, 8, 128], BF16))
        xg = [(K.last_gtmp[:], "gtmp"), (stg[0][:].rearrange("p a b -> p (a b)"), ("wstg", 0))]

        def nchain(blk, ti):
            tt_ = blk * 4 + ti
            if tt_ + 1 < 8:
                norm_load(K, P, T, K.x_own, tt_ + 1)
            norm_chain(K, P, T, K.x_own, tt_, G1, SH1, load=False, xg=xg)

        def npe(blk, ti):
            norm_pe(K, P, T, blk * 4 + ti, ident, hT[blk % 2], ti, hname=("hT", blk % 2))
        norm_load(K, P, T, K.x_own, 0)
        for ti in range(4):
            nchain(0, ti)
            npe(0, ti)
        for blk in range(2):
            hs = blk % 2
            for ti in range(4):
                t = blk * 4 + ti
                hk = [(("hT", hs), ti, 0), (("hT", hs), ti, 1)]
                if blk + 1 < 2:
                    nchain(blk + 1, ti)
                for gi, (c0, n, wkeys) in enumerate([(0, 512, wk_q), (512, 512, wk_q), (1024, 512, wk_iq),
                                                     (1536, 512, wk_iq), (2048, 16, wk_iw)]):
                    pb = pm[gi % 3]
                    wtag, wc0 = [("Wq", 0), ("Wq", 512), ("Wiq", 0), ("Wiq", 512), ("Wiw", 0)][gi]
                    for k in range(16):
                        wkeys = [(wtag, wc0, 4 * (k // 4))]
                        P.op("pe", lambda e, pb=pb, k=k, c0=c0, n=n, ti=ti, hs=hs: e.matmul(
                            pb[:, 0:n], lhsT=hT[hs][:, k, ti * 128:(ti + 1) * 128], rhs=W[:, k, c0:c0 + n],
                            start=(k == 0), stop=(k == 15)), reads=hk + wkeys, writes=[("pmq", gi % 3)])
                    if gi < 2:
                        P.op("act", lambda e, pb=pb, gi=gi: e.copy(out=qsb[:, gi * 4:(gi + 1) * 4, :], in_=pb[:, 0:512]),
                             reads=[("pmq", gi % 3)], writes=["qsb"])
                    elif gi < 4:
                        g2 = gi - 2
                        P.op("act", lambda e, pb=pb, g2=g2: e.copy(out=iqsb[:, g2 * 8:(g2 + 1) * 8, :], in_=pb[:, 0:512]),
                             reads=[("pmq", gi % 3)], writes=["iqsb"])
                    else:
                        P.op("act", lambda e, pb=pb: e.activation(out=iwsb[:], in_=pb[:, 0:16], func=AF.Copy, scale=0.25),
                             reads=[("pmq", gi % 3)], writes=["iwsb"])
                P.dma("sp", K.iw_d[t * 128:(t + 1) * 128, :], iwsb[:], reads=["iwsb"], writes=[("iw_d", t)])
                head_rmsnorm(P, qsb[:], gain, sq, ssum, ["qsb"], "Q")
                apply_rope(P, "dve", qsb[:], rt["cosa"], rt["sina"], t, 16, rtmp, ["qsb"], "rQ")
                P.op("act", lambda e: e.copy(out=qbf[:], in_=qsb[:]), reads=["qsb"], writes=["qbf"])
                for h in range(8):
                    P.op("pe", lambda e, h=h: e.transpose(out=pk[:, h, :], in_=qbf[:, h, :], identity=ident[:]),
                         reads=["qbf", "ident"], writes=["pkq"])
                P.op("dve", lambda e: e.tensor_copy(out=qTs[:], in_=pk[:]), reads=["pkq"], writes=["qTs"])
                P.dma("sp", K.qT_d.rearrange("h p t -> p h t")[:, :, t * 128:(t + 1) * 128], qTs[:],
                      reads=["qTs"], writes=[("qT_d", t)])
                apply_rope(P, "pool", iqsb[:], rt["cosi"], rt["sini"], t, 8, rtmp, ["iqsb"], "rIQ")
                P.op("act", lambda e: e.activation(out=iqbf[:], in_=iqsb[:], func=AF.Copy, scale=0.125),
                     reads=["iqsb"], writes=["iqbf"])
                for half in range(2):
                    for hh in range(8):
                        h = half * 8 + hh
                        P.op("pe", lambda e, h=h, hh=hh: e.transpose(out=pk[0:64, hh, :], in_=iqbf[:, h, :],
                                                                      identity=ident[:]),
                             reads=["iqbf", "ident"], writes=["pkq"])
                    P.op("dve", lambda e, half=half: e.tensor_copy(
                        out=iqTs[:, :, half * 8:(half + 1) * 8].rearrange("p t h -> p h t"), in_=pk[0:64, :, :]),
                         reads=["pkq"], writes=["iqTs"])
                P.dma("sp", K.iqT_d[:, t * 128:(t + 1) * 128, :], iqTs[:], reads=["iqTs"], writes=[("iqT_d", t)])
                if blk + 1 < 2:
                    npe(blk + 1, ti)
        P.flush()


NIT = 20
SLOT_NK = [4, 8, 12, 16, 20, 24, 28, 32]


def phase3_attention(K):
    nc, P = K.nc, K.P
    with contextlib.ExitStack() as st:
        def sb(name, shape, dt):
            return st.enter_context(nc.sbuf_tensor(name, shape, dt))
        ident = sb("ident3", [128, 128], BF16)
        kposf = sb("kposf", [128, 512], F32)
        bias = sb("cbias", [128, 512], F32)
        identf = kposf[:, 0:128]
        make_ident(K, P, ident)
        P.op("dve", lambda e: e.tensor_copy(out=identf, in_=ident[:]), reads=["ident"], writes=["kposf"])
        kT = sb("kTall", [128, 8, S], BF16)
        V = sb("Vall", [128, 32, 1032], BF16)
        ikT = sb("ikTall", [64, S], BF16)
        for h in range(8):
            P.dma("sp", kT[:, h, :], K.kT_d[h], writes=[("kT", h)])
        for q4 in range(4):
            P.dma("sp", V[:, q4 * 8:(q4 + 1) * 8, :],
                  K.v_d.rearrange("(t p) c -> p t c", p=128)[:, q4 * 8:(q4 + 1) * 8, :], writes=[("V", q4)])
        P.dma("sp", ikT[:], K.ikT_d, writes=["ikT"])
        kTk = [("kT", h) for h in range(8)]
        Vk = [("V", q4) for q4 in range(4)]
        Sel = sb("Sel", [128, 16, 128], BF16)
        pidx = sb("pidx", [128, 1], I32)
        pidf = sb("pidf", [128, 1], F32)
        score = sb("score", [128, S], F32)
        self_ = score[:, 0:2048].rearrange("p (g t) -> p g t", g=16)
        sk4 = [("score", q) for q in range(4)]
        P.op("pool", lambda e: e.iota(self_, pattern=[[-8, 16], [1, 128]], base=0, channel_multiplier=0, allow_small_or_imprecise_dtypes=True), writes=sk4)
        P.op("pool", lambda e: e.iota(pidx[:], pattern=[[0, 1]], base=0, channel_multiplier=1), writes=["pidx"])
        P.op("dve", lambda e: e.tensor_scalar(out=pidx[:], in0=pidx[:], scalar1=4, scalar2=None,
                                               op0=ALU.arith_shift_right), reads=["pidx"], writes=["pidx"])
        P.op("dve", lambda e: e.tensor_copy(out=pidf[:], in_=pidx[:]), reads=["pidx"], writes=["pidf"])
        P.op("dve", lambda e: e.tensor_scalar(out=Sel[:], in0=self_, scalar1=pidf[:, 0:1], scalar2=None,
                                               op0=ALU.is_equal), reads=sk4 + ["pidf"], writes=["Sel"])
        qpos = sb("qpos", [128, 8], F32)
        P.dma("sp", qpos[:], K.qpos_own, writes=["qpos"])
        iwg = bias[:, 0:128]
        wcol = sb("wcol", [128, 128], F32)
        P.dma("sp", iwg, K.iw_d.rearrange("(g t) h -> g (t h)", t=8), writes=["bias"])
        A = [st.enter_context(nc.psum_tensor("A%d" % i, [128, 512], F32)) for i in range(2)]
        B = [st.enter_context(nc.psum_tensor("B%d" % i, [128, 512], F32)) for i in range(2)]
        C = st.enter_context(nc.psum_tensor("C3", [128, 8, 128], BF16))
        P.op("pe", lambda e: e.transpose(out=A[0][:, 0:128], in_=iwg, identity=identf),
             reads=["bias", "kposf"], writes=[("A", 0)])
        P.op("dve", lambda e: e.tensor_copy(out=wcol[:], in_=A[0][:, 0:128]), reads=[("A", 0)], writes=["wcol"])
        mask01 = sb("mask01", [128, S], BF16)
        maskT = sb("maskT", [128, 32, 128], BF16)
        R = [sb("R%d" % i, [128, 512], BF16) for i in range(2)]
        pexp = [sb("pexp%d" % i, [128, 512], BF16) for i in range(2)]
        pmk = [sb("pmk%d" % i, [128, 512], BF16) for i in range(2)]
        iqTs = sb("iqTs3", [64, 128, 16], BF16)
        qTs = sb("qTs3", [128, 8, 128], BF16)
        att = sb("att", [128, 8, 128], BF16)
        attTs = sb("attTs", [128, 8, 128], BF16)
        c2 = sb("c2", [128, NIT], F32)
        steps = sb("steps", [128, NIT], F32)
        sm = sb("sm3", [128, 8], F32)
        for k in range(NIT):
            P.op("pool", lambda e, k=k: e.memset(c2[:, k:k + 1], float(2.0 ** -(k + 1))), writes=["c2"])
        maskT2 = [maskT, sb("maskTb", [128, 32, 128], BF16)]
        Wbd = sb("Wbd", [128, 16, 128], BF16)
        qTs2 = [qTs, sb("qTs3b", [128, 8, 128], BF16)]
        rcp2 = sb("rcp2", [128, 2], F32)

        def stageA(i):
            nk = SLOT_NK[i]
            nb = nk // 4
            P.dma("sp", iqTs[:], K.iqT_d[:, i * 128:(i + 1) * 128, :], writes=["iqTs"])
            P.dma("sp", qTs2[i % 2][:], K.qT_d.rearrange("h p t -> p h t")[:, :, i * 128:(i + 1) * 128], writes=[("qTs", i % 2)])
            isteps = [(sbk, g) for sbk in range(nb) for g in range(16)]
            P.op("pool", lambda e, i=i: e.tensor_tensor(out=Wbd[:], in0=Sel[:],
                                                         in1=wcol[:, i * 16:(i + 1) * 16].unsqueeze(2).to_broadcast([128, 16, 128]),
                                                         op=ALU.mult), reads=["Sel", "wcol"], writes=["Wbd"])

            def dots(si):
                sbk, g = isteps[si]
                a_ = si % 2
                lhsT = iqTs[:, g * 8:(g + 1) * 8, :].rearrange("p t h -> p (t h)")
                P.op("pe", lambda e, a_=a_, lhsT=lhsT, sbk=sbk: e.matmul(
                    A[a_][:, :], lhsT=lhsT, rhs=ikT[:, sbk * 512:(sbk + 1) * 512], start=True, stop=True),
                    reads=["iqTs", "ikT"], writes=[("A", a_)])
            dots(0)
            for si, (sbk, g) in enumerate(isteps):
                a_ = si % 2
                bsl = sbk % 2
                if si + 1 < len(isteps):
                    dots(si + 1)
                if si % 2 == 0:
                    P.op("act", lambda e, a_=a_: e.activation(out=R[a_][:], in_=A[a_][:, :], func=AF.Relu),
                         reads=[("A", a_)], writes=[("R", a_)])
                else:
                    P.op("dve", lambda e, a_=a_: e.tensor_scalar(out=R[a_][:], in0=A[a_][:, :], scalar1=0.0, scalar2=None,
                                                                  op0=ALU.max), reads=[("A", a_)], writes=[("R", a_)])
                P.op("pe", lambda e, a_=a_, g=g, bsl=bsl: e.matmul(
                    B[bsl][:, :], lhsT=Wbd[:, g, :], rhs=R[a_][:], start=(g == 0), stop=(g == 15)),
                    reads=[("R", a_), "Wbd"], writes=[("B", bsl)])
                if g == 15:
                    P.op("dve", lambda e, bsl=bsl, sbk=sbk: e.tensor_copy(out=score[:, sbk * 512:(sbk + 1) * 512], in_=B[bsl][:, :]),
                         reads=[("B", bsl)], writes=[("score", sbk)])

        def stageBdve(i):
            nk = SLOT_NK[i]
            nb = nk // 4
            L = nk * 128
            sck = [("score", sbk) for sbk in range(nb)]
            P.op("dve", lambda e, L=L: e.tensor_reduce(out=sm[:, 0:1], in_=score[:, 0:L], axis=AX.X, op=ALU.max,
                                                        apply_absolute_value=True), reads=sck, writes=["sm0"])
            P.op("pool", lambda e, nb=nb: e.iota(kposf[:], pattern=[[1, 512]], base=(nb - 1) * 512, channel_multiplier=0,
                                                 allow_small_or_imprecise_dtypes=True), writes=["kposf"])
            P.op("dve", lambda e, i=i: e.tensor_scalar(out=bias[:], in0=kposf[:], scalar1=qpos[:, i:i + 1],
                                                        scalar2=-1e30, op0=ALU.is_gt, op1=ALU.mult),
                 reads=["kposf", "qpos"], writes=["bias"])
            P.op("dve", lambda e, nb=nb: e.tensor_tensor(out=score[:, (nb - 1) * 512:nb * 512],
                                                          in0=score[:, (nb - 1) * 512:nb * 512], in1=bias[:], op=ALU.add),
                 reads=["bias", ("score", nb - 1), "sm0"], writes=[("score", nb - 1)])
            P.op("dve", lambda e: e.tensor_scalar(out=sm[:, 1:2], in0=sm[:, 0:1], scalar1=-1.0, scalar2=-1.0,
                                                   op0=ALU.mult, op1=ALU.add), reads=["sm0"], writes=["lo"])
            P.op("dve", lambda e: e.tensor_scalar(out=sm[:, 5:6], in0=sm[:, 0:1], scalar1=2.0, scalar2=2.0,
                                                   op0=ALU.mult, op1=ALU.add), reads=["sm0"], writes=["d0"])
            P.op("dve", lambda e: e.tensor_scalar(out=steps[:], in0=c2[:], scalar1=sm[:, 5:6], scalar2=None,
                                                   op0=ALU.mult), reads=["d0", "c2"], writes=["steps"])
            P.op("dve", lambda e: e.tensor_tensor(out=sm[:, 2:3], in0=sm[:, 1:2], in1=steps[:, 0:1], op=ALU.add),
                 reads=["lo", "steps"], writes=["mid"])
            for k in range(NIT):
                P.op("dve", lambda e, L=L: e.tensor_scalar(out=mask01[:, 0:L], in0=score[:, 0:L], scalar1=sm[:, 2:3],
                                                            scalar2=None, op0=ALU.is_ge, op1=ALU.add,
                                                            accum_out=sm[:, 3:4]),
                     reads=sck + ["mid"], writes=["mask01", "cnt"])
                P.op("dve", lambda e: e.tensor_scalar(out=sm[:, 4:5], in0=sm[:, 3:4], scalar1=255.5, scalar2=-0.5,
                                                       op0=ALU.is_ge, op1=ALU.add), reads=["cnt"], writes=["inc"])
                P.op("dve", lambda e, k=k: e.scalar_tensor_tensor(out=sm[:, 2:3], in0=steps[:, k:k + 1], scalar=sm[:, 4:5],
                                                                   in1=sm[:, 2:3], op0=ALU.mult, op1=ALU.add),
                     reads=["inc", "steps", "mid"], writes=["mid"])
            P.op("dve", lambda e: e.scalar_tensor_tensor(out=sm[:, 1:2], in0=steps[:, NIT - 1:NIT], scalar=-0.5, in1=sm[:, 2:3],
                                                          op0=ALU.mult, op1=ALU.add), reads=["steps", "mid"], writes=["lo"])
            P.op("dve", lambda e, L=L: e.tensor_scalar(out=mask01[:, 0:L], in0=score[:, 0:L], scalar1=sm[:, 1:2],
                                                        scalar2=None, op0=ALU.is_ge), reads=sck + ["lo"], writes=["mask01"])

        def stageBpe(i):
            nk = SLOT_NK[i]
            mT = maskT2[i % 2]
            for kt in range(nk):
                P.op("pe", lambda e, kt=kt: e.transpose(out=C[:, kt % 8, :], in_=mask01[:, kt * 128:(kt + 1) * 128],
                                                         identity=ident[:]), reads=["mask01", "ident"], writes=["C"])
                if kt % 8 == 7 or kt == nk - 1:
                    k0 = (kt // 8) * 8
                    n8 = kt - k0 + 1
                    P.op("dve", lambda e, k0=k0, n8=n8, mT=mT: e.tensor_copy(out=mT[:, k0:k0 + n8, :], in_=C[:, 0:n8, :]),
                         reads=["C"], writes=[("maskT", i % 2, k0 // 8)])

        def stageC(i):
            nk = SLOT_NK[i]
            nb = nk // 4
            mT = maskT2[i % 2]
            qT_ = qTs2[i % 2]
            mk = [("maskT", i % 2, q) for q in range((nk + 7) // 8)]
            asteps = [(h, kg) for h in range(8) for kg in range(nb)]

            def qk(si):
                h, kg = asteps[si]
                a_ = si % 2
                for j4 in range(4):
                    kt = kg * 4 + j4
                    P.op("pe", lambda e, a_=a_, j4=j4, kt=kt, h=h: e.matmul(
                        A[a_][:, j4 * 128:(j4 + 1) * 128], lhsT=kT[:, h, kt * 128:(kt + 1) * 128], rhs=qT_[:, h, :],
                        start=True, stop=True), reads=kTk + [("qTs", i % 2)], writes=[("A", a_)])
            qk(0)
            for si, (h, kg) in enumerate(asteps):
                a_ = si % 2
                bsl = h % 2
                if si + 1 < len(asteps):
                    qk(si + 1)
                P.op("act", lambda e, a_=a_: e.activation(out=pexp[a_][:], in_=A[a_][:, :], func=AF.Exp,
                                                           scale=float(128 ** -0.5)),
                     reads=[("A", a_)], writes=[("pexp", a_)])
                P.op("pool", lambda e, a_=a_, kg=kg: e.tensor_tensor(
                    out=pmk[a_][:], in0=pexp[a_][:], in1=mT[:, kg * 4:(kg + 1) * 4, :].rearrange("p a t -> p (a t)"),
                    op=ALU.mult), reads=[("pexp", a_)] + mk, writes=[("pmk", a_)])
                for j4 in range(4):
                    kt = kg * 4 + j4
                    P.op("pe", lambda e, a_=a_, j4=j4, kt=kt, h=h, bsl=bsl, kg=kg, nb=nb: e.matmul(
                        B[bsl][:, 0:129], lhsT=pmk[a_][:, j4 * 128:(j4 + 1) * 128], rhs=V[:, kt, h * 129:(h + 1) * 129],
                        start=(kg == 0 and j4 == 0), stop=(kg == nb - 1 and j4 == 3)),
                        reads=[("pmk", a_)] + Vk, writes=[("B", bsl)])
                if kg == nb - 1:
                    P.op("act", lambda e, bsl=bsl: e.activation(out=rcp2[:, 0:1], in_=B[bsl][:, 128:129], func=AF.Ln),
                         reads=[("B", bsl)], writes=["rcpa"])
                    P.op("act", lambda e: e.activation(out=rcp2[:, 1:2], in_=rcp2[:, 0:1], func=AF.Exp, scale=-1.0),
                         reads=["rcpa"], writes=["rcpb"])
                    P.op("act", lambda e, bsl=bsl, h=h: e.activation(out=att[:, h, :], in_=B[bsl][:, 0:128], func=AF.Copy,
                                                                      scale=rcp2[:, 1:2]),
                         reads=[("B", bsl), "rcpb"], writes=["att"])
            for h in range(8):
                P.op("pe", lambda e, h=h: e.transpose(out=C[:, h, :], in_=att[:, h, :], identity=ident[:]),
                     reads=["att", "ident"], writes=["C"])
            P.op("act", lambda e: e.copy(out=attTs[:], in_=C[:]), reads=["C"], writes=["attTs"])
            P.dma("sp", K.attT_d.rearrange("h p t -> p h t")[:, :, i * 128:(i + 1) * 128], attTs[:],
                  reads=["attTs"], writes=[("attT_d", i)])

        stageA(0)
        stageBdve(0)
        stageBpe(0)
        for i in range(8):
            if i + 1 < 8:
                stageA(i + 1)
                stageBdve(i + 1)
            stageC(i)
            if i + 1 < 8:
                stageBpe(i + 1)
        P.flush()

RD = BF16
NCH = 64


def tok_shift(P, dst, raw, tmp, mu_ap, rk_raw, k_tmp, k_dst, n=128):
    P.op("pool", lambda e: e.tensor_tensor(out=tmp[0:n, 1:S], in0=raw[0:n, 0:S - 1], in1=raw[0:n, 1:S], op=ALU.subtract),
         reads=[rk_raw], writes=[k_tmp])
    P.op("pool", lambda e: e.tensor_scalar(out=tmp[0:n, 0:1], in0=raw[0:n, 0:1], scalar1=-1.0, scalar2=0.0,
                                            op0=ALU.mult, op1=ALU.add), reads=[rk_raw, k_tmp], writes=[k_tmp])
    P.op("dve", lambda e: e.scalar_tensor_tensor(out=dst[0:n, :], in0=tmp[0:n, :], scalar=mu_ap, in1=raw[0:n, :],
                                                  op0=ALU.mult, op1=ALU.add), reads=[rk_raw, k_tmp], writes=[k_dst])


def phase4b_rwkv_prep(K, cts=range(2)):
    nc, P = K.nc, K.P
    with contextlib.ExitStack() as st:
        def sb(name, shape, dt):
            return st.enter_context(nc.sbuf_tensor(name, shape, dt))
        txw = sb("txw", [96, S], BF16)
        xap = sb("xap", [96, S], BF16)
        sxg = sb("sxg", [128, 2, S], BF16)
        M01 = sb("M01", [128, S], BF16)
        wup = sb("wup", [96, 256], BF16)
        aup = sb("aup", [96, 256], BF16)
        gup = sb("gup", [128, 2, 256], BF16)
        wst = sb("wst4", [128, 2, 256], F32)
        bones = sb("bones", [128, 128], BF16)
        prm = sb("prm", [128, 12, 2], F32)
        mul = sb("mul", [128, 4], F32)
        PT = sb("PT", [128, S], F32)
        KK = sb("KK", [128, S], F32)
        KP = sb("KP", [128, S], F32)
        CL = sb("CL", [128, S], F32)
        RP = sb("RP", [128, S], BF16)
        VP = sb("VP", [128, S], BF16)
        AA = sb("AA", [128, S], BF16)
        K2 = sb("K2", [128, S], BF16)
        SQb = sb("SQb", [128, S], BF16)
        OUT = [sb("OUT%d" % i, [128, S], BF16) for i in range(2)]
        PCt = sb("PCt", [128, NCH], F32)
        ps = [st.enter_context(nc.psum_tensor("ps4_%d" % i, [128, 512], F32)) for i in range(4)]
        for i, ap in enumerate(K.rw_prm):
            P.dma("sp", prm[:, i, :], ap, writes=[("prm", i)])
        prk = [("prm", i) for i in range(10)]
        P.op("dve", lambda e: e.tensor_scalar(out=prm[:, 10, :], in0=prm[:, 6, :], scalar1=-1.0, scalar2=1.0,
                                               op0=ALU.mult, op1=ALU.add), reads=prk, writes=[("prm", 10)])
        prk = prk + [("prm", 10)]
        P.dma("sp", mul[:], K.rw_mul, writes=["mul"])
        P.op("pool", lambda e: e.memset(bones[:], 0.0), writes=["bones"])
        P.op("pool", lambda e: e.memset(bones[0:64, 0:64], 1.0), reads=["bones"], writes=["bones"])
        P.op("pool", lambda e: e.memset(bones[64:128, 64:128], 1.0), reads=["bones"], writes=["bones"])
        P.op("pool", lambda e: e.iota(PT[:].rearrange("p (c t) -> p c t", t=64), pattern=[[0, NCH], [1, 64]], base=0,
                                      channel_multiplier=0, allow_small_or_imprecise_dtypes=True), writes=["PT"])
        P.op("dve", lambda e: e.tensor_scalar(out=M01[:], in0=PT[:], scalar1=0.5, scalar2=None, op0=ALU.is_gt),
             reads=["PT"], writes=["M01"])
        P.dma("sp", wst[0:96, 0, :], K.rw_w_up, writes=["wst"])
        P.op("act", lambda e: e.copy(out=wup[:], in_=wst[0:96, 0, :]), reads=["wst"], writes=["wup"])
        P.dma("sp", wst[0:96, 1, :], K.rw_a_up, reads=[], writes=["wst1"])
        P.op("act", lambda e: e.copy(out=aup[:], in_=wst[0:96, 1, :]), reads=["wst1"], writes=["aup"])
        P.dma("sp", wst[:, :, :], K.rw_g_up.rearrange("(c p) n -> p c n", p=128), reads=[], writes=["wst", "wst1"])
        P.op("act", lambda e: e.copy(out=gup[:], in_=wst[:]), reads=["wst", "wst1"], writes=["gup"])
        for (r0, n, mcol, func, dst, kd) in ((768, 96, 0, AF.Tanh, txw[:, :], "txw"), (864, 96, 1, AF.Copy, xap[:, :], "xap"),
                                             (960, 128, 2, AF.Sigmoid, sxg[:, 0, :], "sxg0"),
                                             (1088, 128, 3, AF.Sigmoid, sxg[:, 1, :], "sxg1")):
            P.dma("sp", PT[0:n, :], K.yT_d[r0:r0 + n, :], writes=["PT"])
            tok_shift(P, KP, PT, KK, mul[0:n, mcol:mcol + 1], "PT", "KK", "KP", n=n)
            P.op("act", lambda e, n=n, func=func, dst=dst: e.activation(out=dst, in_=KP[0:n, :], func=func),
                 reads=["KP"], writes=[kd])
        lk = ["txw", "xap", "sxg0", "sxg1"]
        oc = 0
        for ct in cts:
            c0 = ct * 128
            P.dma("sp", PT[:], K.yT_d[c0:c0 + 128, :], writes=["PT"])
            tok_shift(P, RP, PT, KK, prm[:, 0, ct:ct + 1], "PT", "KK", "RP")
            P.dma("sp", PT[:], K.yT_d[256 + c0:256 + c0 + 128, :], writes=["PT"])
            tok_shift(P, KP, PT, KK, prm[:, 1, ct:ct + 1], "PT", "KK", "KP")
            P.dma("sp", PT[:], K.yT_d[512 + c0:512 + c0 + 128, :], writes=["PT"])
            tok_shift(P, VP, PT, KK, prm[:, 2, ct:ct + 1], "PT", "KK", "VP")
            P.dma("sp", K.vb_d[c0:c0 + 128, :], VP[:], reads=["VP"], writes=[("vb_d", ct)])
            for blk in range(8):
                bs = slice(blk * 512, (blk + 1) * 512)
                p0, p1, p2 = ps[0], ps[1], ps[2]
                P.op("pe", lambda e, bs=bs, c0=c0: e.matmul(ps[0][:, :], lhsT=wup[:, c0:c0 + 128], rhs=txw[:, bs],
                                                             start=True, stop=True), reads=["wup", "txw"], writes=[("ps4", 0)])
                P.op("act", lambda e, bs=bs, ct=ct: e.activation(out=CL[:, bs], in_=ps[0][:, :], func=AF.Sigmoid,
                                                                  bias=prm[:, 3, ct:ct + 1]),
                     reads=[("ps4", 0)] + prk, writes=["CL"])
                P.op("pe", lambda e, bs=bs, c0=c0: e.matmul(ps[1][:, :], lhsT=aup[:, c0:c0 + 128], rhs=xap[:, bs],
                                                             start=True, stop=True), reads=["aup", "xap"], writes=[("ps4", 1)])
                P.op("act", lambda e, bs=bs, ct=ct: e.activation(out=AA[:, bs], in_=ps[1][:, :], func=AF.Sigmoid,
                                                                  bias=prm[:, 4, ct:ct + 1]),
                     reads=[("ps4", 1)] + prk, writes=["AA"])
                for cc in range(2):
                    P.op("pe", lambda e, bs=bs, c0=c0, cc=cc: e.matmul(ps[2][:, :], lhsT=gup[:, cc, c0:c0 + 128],
                                                                       rhs=sxg[:, cc, bs], start=(cc == 0), stop=(cc == 1)),
                         reads=["gup", "sxg0", "sxg1"], writes=[("ps4", 2)])
                o = OUT[oc % 2]
                P.op("dve", lambda e, bs=bs, o=o: e.tensor_copy(out=o[:, bs], in_=ps[2][:, :]),
                     reads=[("ps4", 2)], writes=[("OUT", oc % 2)])
            P.dma("sp", K.G_d[c0:c0 + 128, :], OUT[oc % 2][:], reads=[("OUT", oc % 2)], writes=[("G_d", ct)])
            oc += 1
            P.op("dve", lambda e: e.tensor_scalar(out=CL[:], in0=CL[:], scalar1=-0.6065306597126334, scalar2=None,
                                                   op0=ALU.mult), reads=["CL"], writes=["CL"])
            P.op("dve", lambda e, ct=ct: e.tensor_scalar(out=KK[:], in0=KP[:], scalar1=prm[:, 5, ct:ct + 1], scalar2=None,
                                                          op0=ALU.mult), reads=["KP"] + prk, writes=["KK"])
            P.op("act", lambda e: e.activation(out=SQb[:], in_=KK[:], func=AF.Square), reads=["KK"], writes=["SQb"])
            for blk in range(8):
                bs = slice(blk * 512, (blk + 1) * 512)
                P.op("pe", lambda e, bs=bs: e.matmul(ps[3][:, :], lhsT=bones[:], rhs=SQb[:, bs], start=True, stop=True),
                     reads=["bones", "SQb"], writes=[("ps4", 3)])
                P.op("act", lambda e, bs=bs: e.activation(out=PT[:, bs], in_=ps[3][:, :], func=AF.Sqrt),
                     reads=[("ps4", 3)], writes=["PT"])
            P.op("dve", lambda e: e.tensor_scalar(out=PT[:], in0=PT[:], scalar1=1e-12, scalar2=None, op0=ALU.max),
                 reads=["PT"], writes=["PT"])
            P.op("dve", lambda e: e.reciprocal(out=PT[:], in_=PT[:]), reads=["PT"], writes=["PT"])
            P.op("dve", lambda e: e.tensor_tensor(out=KK[:], in0=KK[:], in1=PT[:], op=ALU.mult), reads=["KK", "PT"], writes=["KK"])
            P.op("dve", lambda e, ct=ct: e.tensor_scalar(out=PT[:], in0=AA[:], scalar1=prm[:, 6, ct:ct + 1],
                                                          scalar2=prm[:, 10, ct:ct + 1], op0=ALU.mult, op1=ALU.add),
                 reads=["AA", "PT"] + prk, writes=["PT"])
            P.op("dve", lambda e: e.tensor_tensor(out=K2[:], in0=KP[:], in1=PT[:], op=ALU.mult), reads=["KP", "PT"], writes=["K2"])
            P.op("dve", lambda e, ct=ct: e.scalar_tensor_tensor(out=SQb[:], in0=RP[:], scalar=prm[:, 7, ct:ct + 1], in1=K2[:],
                                                                 op0=ALU.mult, op1=ALU.mult),
                 reads=["RP", "K2", "SQb"] + prk, writes=["SQb"])
            o = OUT[oc % 2]
            for blk in range(8):
                bs = slice(blk * 512, (blk + 1) * 512)
                P.op("pe", lambda e, bs=bs: e.matmul(ps[3][:, :], lhsT=bones[:], rhs=SQb[:, bs], start=True, stop=True),
                     reads=["bones", "SQb"], writes=[("ps4", 3)])
                P.op("dve", lambda e, bs=bs, o=o: e.tensor_tensor(out=o[:, bs], in0=ps[3][:, :], in1=VP[:, bs], op=ALU.mult),
                     reads=[("ps4", 3), "VP"], writes=[("OUT", oc % 2)])
            P.dma("sp", K.BON_d[c0:c0 + 128, :], o[:], reads=[("OUT", oc % 2)], writes=[("BON_d", ct)])
            oc += 1
            P.op("dve", lambda e: e.tensor_tensor_scan(out=PT[:], data0=M01[:], data1=CL[:], initial=0.0,
                                                        op0=ALU.mult, op1=ALU.add), reads=["M01", "CL", "PT"], writes=["PT"])
            P.op("pool", lambda e: e.tensor_tensor(out=CL[:], in0=PT[:], in1=CL[:], op=ALU.subtract),
                 reads=["PT", "CL"], writes=["CL"])
            P.op("act", lambda e: e.activation(out=CL[:], in_=CL[:], func=AF.Exp), reads=["CL"], writes=["CL"])
            v3 = lambda t: t[:].rearrange("p (c t) -> p c t", t=64)
            o = OUT[oc % 2]
            P.op("dve", lambda e, o=o: e.scalar_tensor_tensor(out=o[:], in0=KK[:], scalar=-1.0, in1=CL[:],
                                                               op0=ALU.mult, op1=ALU.mult),
                 reads=["KK", "CL"], writes=[("OUT", oc % 2)])
            P.dma("sp", K.AH_d[c0:c0 + 128, :], o[:], reads=[("OUT", oc % 2)], writes=[("AH_d", ct)])
            oc += 1
            P.op("act", lambda e: e.activation(out=CL[:], in_=PT[:], func=AF.Exp), reads=["PT", "CL"], writes=["CL"])
            o = OUT[oc % 2]
            P.op("dve", lambda e, o=o: e.tensor_tensor(out=o[:], in0=RP[:], in1=CL[:], op=ALU.mult),
                 reads=["RP", "CL"], writes=[("OUT", oc % 2)])
            P.dma("sp", K.RH_d[c0:c0 + 128, :], o[:], reads=[("OUT", oc % 2)], writes=[("RH_d", ct)])
            oc += 1
            P.op("pool", lambda e: e.tensor_copy(out=PCt[:], in_=v3(CL)[:, :, 63]), reads=["CL"], writes=["PCt"])
            P.dma("sp", K.PC_d[c0:c0 + 128, :], PCt[:], reads=["PCt"], writes=[("PC_d", ct)])
            P.op("act", lambda e: e.activation(out=PT[:], in_=PT[:], func=AF.Exp, scale=-1.0), reads=["PT"], writes=["PT"])
            o = OUT[oc % 2]
            P.op("dve", lambda e, o=o: e.tensor_tensor(out=o[:], in0=K2[:], in1=PT[:], op=ALU.mult),
                 reads=["K2", "PT"], writes=[("OUT", oc % 2)])
            P.dma("sp", K.KH_d[c0:c0 + 128, :], o[:], reads=[("OUT", oc % 2)], writes=[("KH_d", ct)])
            oc += 1
            P.op("dve", lambda e: e.tensor_tensor(out=KK[:], in0=KK[:], in1=AA[:], op=ALU.mult), reads=["KK", "AA"], writes=["KK"])
            o = OUT[oc % 2]
            P.op("dve", lambda e, o=o: e.tensor_tensor(out=o[:], in0=KK[:], in1=PT[:], op=ALU.mult),
                 reads=["KK", "PT"], writes=[("OUT", oc % 2)])
            P.dma("sp", K.BH_d[c0:c0 + 128, :], o[:], reads=[("OUT", oc % 2)], writes=[("BH_d", ct)])
            oc += 1
        P.flush()

def phase4c_rwkv_scan(K, heads=range(4)):
    nc, P = K.nc, K.P
    with contextlib.ExitStack() as st:
        def sb(name, shape, dt):
            return st.enter_context(nc.sbuf_tensor(name, shape, dt))
        ident = sb("ident4", [128, 128], BF16)
        make_ident(K, P, ident)
        MaskG = sb("MaskG", [64, 4, 128], F32)
        MaskX = sb("MaskX", [64, 8, 64], F32)
        I8 = sb("I8", [64, 64], F32)
        ones = sb("ones4", [64, 64], F32)
        P.op("pool", lambda e: e.memset(ones[:], 1.0), writes=["ones"])
        for a in range(4):
            for cq in range(2):
                P.op("pool", lambda e, cq=cq, a=a: e.affine_select(
                    out=MaskG[:, a, cq * 64:(cq + 1) * 64], in_=ones[:], pattern=[[1, 64]],
                    compare_op=(ALU.is_gt if cq == 0 else ALU.is_ge), fill=0.0, base=0, channel_multiplier=-1),
                    reads=["ones"], writes=["MaskG"])
        for a in range(8):
            P.op("pool", lambda e, a=a: e.affine_select(out=MaskX[:, a, :], in_=ones[:], pattern=[[-1, 64]],
                                                         compare_op=ALU.is_gt, fill=0.0, base=0, channel_multiplier=1),
                 reads=["ones"], writes=["MaskX"])
        P.op("dve", lambda e: e.tensor_copy(out=I8[:], in_=ident[0:64, 0:64]), reads=["ident"], writes=["I8"])
        AH = sb("AH", [64, S], RD)
        RH = sb("RH", [64, S], RD)
        BH = sb("BH", [64, S], RD)
        KH = sb("KH", [64, S], RD)
        vb = sb("vb", [64, S], BF16)
        PC = sb("PC", [64, NCH], F32)
        ARh = sb("ARh", [64, NCH, 128], RD)
        BKh = sb("BKh", [64, NCH, 128], RD)
        GmB = sb("GmB", [64, NCH, 128], RD)
        GmK = sb("GmK", [64, NCH, 128], RD)
        Btok = sb("Btok", [64, NCH, 64], RD)
        Ktok = sb("Ktok", [64, NCH, 64], RD)
        Vtok = sb("Vtok", [64, NCH, 64], RD)
        X0 = sb("X0", [64, NCH, 64], RD)
        Pm = sb("Pm", [64, NCH, 64], RD)
        oT = sb("oT", [64, S], F32)
        Ast = sb("Ast", [64, 64], F32)
        Abf = sb("Abf", [64, 64], RD)
        Tt = sb("Tt", [64, 64], F32)
        Xs = sb("Xs", [64, 64], RD)
        Us = sb("Us", [64, 64], RD)
        PSb2 = [st.enter_context(nc.psum_tensor("PSb%d" % i, [128, 1024], BF16)) for i in range(2)]
        PS = [st.enter_context(nc.psum_tensor("PS%d" % i, [128, 512], F32)) for i in range(6)]
        v3 = lambda t: t[:].rearrange("p (c t) -> p c t", t=64)
        Nb = [v3(AH), v3(RH)]
        Xb = [v3(BH), v3(KH)]
        Nk = ["AH", "RH"]
        Xk = ["BH", "KH"]
        K.tcnt = 0
        for hd in heads:
            r0 = hd * 64
            P.dma("sp", AH[:], K.AH_d[r0:r0 + 64, :], writes=["AH"])
            P.dma("sp", RH[:], K.RH_d[r0:r0 + 64, :], writes=["RH"])
            P.dma("sp", BH[:], K.BH_d[r0:r0 + 64, :], writes=["BH"])
            P.dma("sp", KH[:], K.KH_d[r0:r0 + 64, :], writes=["KH"])
            P.dma("sp", vb[:], K.vb_d[r0:r0 + 64, :], writes=["vb"])
            P.dma("sp", PC[:], K.PC_d[r0:r0 + 64, :], writes=["PC"])
            P.op("dve", lambda e: e.tensor_copy(out=ARh[:, :, 0:64], in_=v3(AH)), reads=["AH"], writes=["ARh"])
            P.op("pool", lambda e: e.tensor_copy(out=ARh[:, :, 64:128], in_=v3(RH)), reads=["RH"], writes=["ARh"])
            P.op("dve", lambda e: e.tensor_copy(out=BKh[:, :, 0:64], in_=v3(BH)), reads=["BH"], writes=["BKh"])
            P.op("pool", lambda e: e.tensor_copy(out=BKh[:, :, 64:128], in_=v3(KH)), reads=["KH"], writes=["BKh"])
            for (src, srck, col0, dst, dk) in ((BKh, "BKh", 0, Btok, "Btok"), (BKh, "BKh", 64, Ktok, "Ktok"), (None, "vb", 0, Vtok, "Vtok")):
                for c16 in range(0, NCH, 16):
                    tb_ = K.tcnt % 2
                    K.tcnt += 1
                    PSb = PSb2[tb_]
                    for cc in range(16):
                        c = c16 + cc
                        in_ = vb[:, c * 64:(c + 1) * 64] if src is None else src[:, c, col0:col0 + 64]
                        P.op("pe", lambda e, cc=cc, in_=in_, PSb=PSb: e.transpose(out=PSb[0:64, cc * 64:(cc + 1) * 64], in_=in_,
                                                                                  identity=ident[0:64, 0:64]),
                             reads=[srck, "ident"], writes=[("PSb", tb_)])
                    P.op("act", lambda e, c16=c16, dst=dst, PSb=PSb: e.copy(out=dst[:, c16:c16 + 16, :].rearrange("p c k -> p (c k)"),
                                                                             in_=PSb[0:64, :]), reads=[("PSb", tb_)], writes=[dk])
            gi = 0
            for (col0, dst, dk) in ((0, GmB, "GmB"), (64, GmK, "GmK")):
                for c4 in range(0, NCH, 4):
                    b = gi % 2
                    gi += 1
                    for cc in range(4):
                        c = c4 + cc
                        P.op("pe", lambda e, c=c, cc=cc, b=b, col0=col0: e.matmul(
                            PS[b][0:64, cc * 128:(cc + 1) * 128], lhsT=BKh[:, c, col0:col0 + 64], rhs=ARh[:, c, :],
                            start=True, stop=True), reads=["BKh", "ARh"], writes=[("PS", b)])
                    P.op("dve", lambda e, c4=c4, dst=dst, b=b: e.tensor_tensor(
                        out=dst[:, c4:c4 + 4, :], in0=PS[b][0:64, :].rearrange("p (a t) -> p a t", t=128), in1=MaskG[:],
                        op=ALU.mult), reads=[("PS", b), "MaskG"], writes=[dk])
            for c8 in range(0, NCH, 8):
                for cc in range(8):
                    c = c8 + cc
                    P.op("pe", lambda e, c=c, cc=cc: e.matmul(PS[2][0:64, cc * 64:(cc + 1) * 64], lhsT=ARh[:, c, 0:64],
                                                               rhs=BKh[:, c, 0:64], start=True, stop=True),
                         reads=["ARh", "BKh"], writes=[("PS", 2)])
                P.op("dve", lambda e, c8=c8: e.tensor_tensor(
                    out=X0[:, c8:c8 + 8, :], in0=PS[2][0:64, :].rearrange("p (a t) -> p a t", t=64), in1=MaskX[:],
                    op=ALU.mult), reads=[("PS", 2), "MaskX"], writes=["X0"])
            N0 = GmB[:, :, 0:64]
            P.op("dve", lambda e, N0=N0: e.tensor_tensor(out=Pm[:], in0=N0, in1=I8[:].unsqueeze(1).to_broadcast([64, NCH, 64]),
                                                         op=ALU.add), reads=["GmB", "I8"], writes=["Pm"])
            curN, curNk = N0, "GmB"
            curX, curXk = X0[:], "X0"
            for lvl in range(1, 6):
                nX, nXk = Xb[lvl % 2], Xk[lvl % 2]
                nN, nNk = Nb[lvl % 2], Nk[lvl % 2]
                for c8 in range(0, NCH, 8):
                    pb_ = (c8 // 8) % 2
                    for cc in range(8):
                        c = c8 + cc
                        P.op("pe", lambda e, c=c, cc=cc, curN=curN, curX=curX, pb_=pb_: e.matmul(
                            PS[0 + pb_][0:64, cc * 64:(cc + 1) * 64], lhsT=curN[:, c, :], rhs=curX[:, c, :], start=True, stop=True),
                            reads=[curNk, curXk], writes=[("PS", 0 + pb_)])
                    P.op("act", lambda e, c8=c8, nX=nX, pb_=pb_: e.copy(out=nX[:, c8:c8 + 8, :],
                                                                in_=PS[0 + pb_][0:64, :].rearrange("p (a t) -> p a t", t=64)),
                         reads=[("PS", 0 + pb_)], writes=[nXk])
                    if lvl < 5:
                        for cc in range(8):
                            c = c8 + cc
                            P.op("pe", lambda e, c=c, cc=cc, curN=curN, curX=curX, pb_=pb_: e.matmul(
                                PS[2 + pb_][0:64, cc * 64:(cc + 1) * 64], lhsT=curX[:, c, :], rhs=curN[:, c, :], start=True, stop=True),
                                reads=[curNk, curXk], writes=[("PS", 2 + pb_)])
                        P.op("act", lambda e, c8=c8, nN=nN, pb_=pb_: e.copy(out=nN[:, c8:c8 + 8, :],
                                                                    in_=PS[2 + pb_][0:64, :].rearrange("p (a t) -> p a t", t=64)),
                             reads=[("PS", 2 + pb_)], writes=[nNk])
                    for cc in range(8):
                        c = c8 + cc
                        P.op("pe", lambda e, c=c, cc=cc, nX=nX, pb_=pb_: e.matmul(
                            PS[4 + pb_][0:64, cc * 64:(cc + 1) * 64], lhsT=nX[:, c, :], rhs=Pm[:, c, :], start=True, stop=True),
                            reads=[nXk, "Pm"], writes=[("PS", 4 + pb_)])
                    P.op("dve", lambda e, c8=c8, pb_=pb_: e.tensor_tensor(
                        out=Pm[:, c8:c8 + 8, :], in0=PS[4 + pb_][0:64, :].rearrange("p (a t) -> p a t", t=64),
                        in1=Pm[:, c8:c8 + 8, :], op=ALU.add), reads=[("PS", 4 + pb_), "Pm"], writes=["Pm"])
                curN, curNk, curX, curXk = nN, nNk, nX, nXk
            P.op("pool", lambda e: e.memset(Ast[:], 0.0), writes=["Ast"])
            P.op("pool", lambda e: e.memset(Abf[:], 0.0), writes=["Abf"])
            for c in range(NCH):
                P.op("pool", lambda e, c=c: e.tensor_scalar(out=Tt[:], in0=Ast[:], scalar1=PC[:, c:c + 1], scalar2=0.0,
                                                             op0=ALU.mult, op1=ALU.add), reads=["Ast", "PC"], writes=["Tt"])
                P.op("pe", lambda e, c=c: e.matmul(PS[0][0:64, 0:64], lhsT=ARh[:, c, 0:64], rhs=Abf[:], start=True, stop=False),
                     reads=["ARh", "Abf"], writes=[("PS", 0)])
                P.op("pe", lambda e, c=c: e.matmul(PS[0][0:64, 0:64], lhsT=GmK[:, c, 0:64], rhs=Vtok[:, c, :], start=False, stop=True),
                     reads=["GmK", "Vtok"], writes=[("PS", 0)])
                P.op("act", lambda e: e.copy(out=Xs[:], in_=PS[0][0:64, 0:64]), reads=[("PS", 0)], writes=["Xs"])
                P.op("pe", lambda e, c=c: e.matmul(PS[1][0:64, 0:64], lhsT=Pm[:, c, :], rhs=Xs[:], start=True, stop=True),
                     reads=["Pm", "Xs"], writes=[("PS", 1)])
                P.op("dve", lambda e: e.tensor_copy(out=Us[:], in_=PS[1][0:64, 0:64]), reads=[("PS", 1)], writes=["Us"])
                P.op("pe", lambda e, c=c: e.matmul(PS[4][0:64, 0:64], lhsT=Btok[:, c, :], rhs=Us[:], start=True, stop=False),
                     reads=["Btok", "Us"], writes=[("PS", 4)])
                P.op("pe", lambda e, c=c: e.matmul(PS[4][0:64, 0:64], lhsT=Ktok[:, c, :], rhs=Vtok[:, c, :], start=False, stop=True),
                     reads=["Ktok", "Vtok"], writes=[("PS", 4)])
                ob = 2 + (c % 2)
                P.op("pe", lambda e, c=c, ob=ob: e.matmul(PS[ob][0:64, 0:64], lhsT=Abf[:], rhs=ARh[:, c, 64:128], start=True, stop=False),
                     reads=["Abf", "ARh"], writes=[("PS", ob)])
                P.op("pe", lambda e, c=c, ob=ob: e.matmul(PS[ob][0:64, 0:64], lhsT=Us[:], rhs=GmB[:, c, 64:128], start=False, stop=False),
                     reads=["Us", "GmB"], writes=[("PS", ob)])
                P.op("pe", lambda e, c=c, ob=ob: e.matmul(PS[ob][0:64, 0:64], lhsT=Vtok[:, c, :], rhs=GmK[:, c, 64:128], start=False, stop=True),
                     reads=["Vtok", "GmK"], writes=[("PS", ob)])
                P.op("dve", lambda e, c=c: e.scalar_tensor_tensor(out=Abf[:], in0=PS[4][0:64, 0:64], scalar=PC[:, c:c + 1], in1=Tt[:],
                                                                   op0=ALU.mult, op1=ALU.add),
                     reads=[("PS", 4), "Tt", "PC"], writes=["Abf"])
                P.op("dve", lambda e, c=c: e.scalar_tensor_tensor(out=Ast[:], in0=PS[4][0:64, 0:64], scalar=PC[:, c:c + 1], in1=Tt[:],
                                                                   op0=ALU.mult, op1=ALU.add),
                     reads=[("PS", 4), "Tt", "PC"], writes=["Ast"])
                P.op("act", lambda e, c=c, ob=ob: e.copy(out=oT[:, c * 64:(c + 1) * 64], in_=PS[ob][0:64, 0:64]),
                     reads=[("PS", ob)], writes=[("oT", c // 8)])
            P.dma("sp", K.oT_d[r0:r0 + 64, :], oT[:], reads=[("oT", q) for q in range(8)], writes=[("oT_d", hd)])
        P.flush()

def phase4d_rwkv_post(K, cts=range(2)):
    nc, P = K.nc, K.P
    with contextlib.ExitStack() as st:
        def sb(name, shape, dt):
            return st.enter_context(nc.sbuf_tensor(name, shape, dt))
        ident = sb("ident4d", [128, 128], BF16)
        make_ident(K, P, ident)
        bonesf = sb("bonesf", [128, 128], F32)
        P.op("pool", lambda e: e.memset(bonesf[:], 0.0), writes=["bonesf"])
        P.op("pool", lambda e: e.memset(bonesf[0:64, 0:64], 1.0), reads=["bonesf"], writes=["bonesf"])
        P.op("pool", lambda e: e.memset(bonesf[64:128, 64:128], 1.0), reads=["bonesf"], writes=["bonesf"])
        prm = sb("prm4d", [128, 2, 2], F32)
        P.dma("sp", prm[:, 0, :], K.rw_prm[8], writes=["prm0"])
        P.dma("sp", prm[:, 1, :], K.rw_prm[9], writes=["prm1"])
        o = sb("o4d", [128, S], F32)
        osq = sb("osq", [128, S], F32)
        bon = sb("bon", [128, S], BF16)
        gg = sb("gg", [128, S], BF16)
        Mb = [sb("Mb%d" % i, [128, 512], F32) for i in range(2)]
        Vb = [sb("Vb%d" % i, [128, 512], F32) for i in range(2)]
        Yb = [sb("Yb%d" % i, [128, 512], F32) for i in range(2)]
        Ob = [sb("Ob%d" % i, [128, 512], BF16) for i in range(2)]
        Tk = [sb("Tk%d" % i, [128, 4, 128], BF16) for i in range(2)]
        ps = [st.enter_context(nc.psum_tensor("p4d_%d" % i, [128, 512], F32)) for i in range(4)]
        pst = [st.enter_context(nc.psum_tensor("p4dt_%d" % i, [128, 4, 128], BF16)) for i in range(2)]
        it = 0
        for ct in cts:
            c0 = ct * 128
            P.dma("sp", o[:], K.oT_d[c0:c0 + 128, :], writes=["o"])
            P.dma("sp", bon[:], K.BON_d[c0:c0 + 128, :], writes=["bon"])
            P.dma("sp", gg[:], K.G_d[c0:c0 + 128, :], writes=["gg"])
            P.op("act", lambda e: e.activation(out=osq[:], in_=o[:], func=AF.Square), reads=["o"], writes=["osq"])
            for blk in range(8):
                s2 = it % 2
                it += 1
                bs = slice(blk * 512, (blk + 1) * 512)
                P.op("pe", lambda e, bs=bs, s2=s2: e.matmul(ps[s2][:, :], lhsT=bonesf[:], rhs=o[:, bs], start=True, stop=True),
                     reads=["bonesf", "o"], writes=[("p4d", s2)])
                P.op("pe", lambda e, bs=bs, s2=s2: e.matmul(ps[2 + s2][:, :], lhsT=bonesf[:], rhs=osq[:, bs], start=True, stop=True),
                     reads=["bonesf", "osq"], writes=[("p4d", 2 + s2)])
                P.op("act", lambda e, s2=s2: e.activation(out=Mb[s2][:], in_=ps[s2][:, :], func=AF.Copy, scale=1.0 / 64),
                     reads=[("p4d", s2)], writes=[("Mb", s2)])
                P.op("pool", lambda e, s2=s2: e.tensor_tensor(out=Vb[s2][:], in0=Mb[s2][:], in1=Mb[s2][:], op=ALU.mult),
                     reads=[("Mb", s2)], writes=[("Vb", s2)])
                P.op("dve", lambda e, s2=s2: e.scalar_tensor_tensor(out=Vb[s2][:], in0=ps[2 + s2][:, :], scalar=1.0 / 64, in1=Vb[s2][:],
                                                                     op0=ALU.mult, op1=ALU.subtract),
                     reads=[("p4d", 2 + s2), ("Vb", s2)], writes=[("Vb", s2)])
                P.op("dve", lambda e, s2=s2: e.tensor_scalar(out=Vb[s2][:], in0=Vb[s2][:], scalar1=64e-5, scalar2=None, op0=ALU.add),
                     reads=[("Vb", s2)], writes=[("Vb", s2)])
                P.op("act", lambda e, s2=s2: e.activation(out=Vb[s2][:], in_=Vb[s2][:], func=AF.Sqrt),
                     reads=[("Vb", s2)], writes=[("Vb", s2)])
                P.op("dve", lambda e, s2=s2: e.reciprocal(out=Vb[s2][:], in_=Vb[s2][:]), reads=[("Vb", s2)], writes=[("Vb", s2)])
                P.op("pool", lambda e, s2=s2, bs=bs: e.tensor_tensor(out=Yb[s2][:], in0=o[:, bs], in1=Mb[s2][:], op=ALU.subtract),
                     reads=["o", ("Mb", s2)], writes=[("Yb", s2)])
                P.op("dve", lambda e, s2=s2: e.tensor_tensor(out=Yb[s2][:], in0=Yb[s2][:], in1=Vb[s2][:], op=ALU.mult),
                     reads=[("Yb", s2), ("Vb", s2)], writes=[("Yb", s2)])
                P.op("dve", lambda e, s2=s2, ct=ct: e.tensor_scalar(out=Yb[s2][:], in0=Yb[s2][:], scalar1=prm[:, 0, ct:ct + 1],
                                                                     scalar2=prm[:, 1, ct:ct + 1], op0=ALU.mult, op1=ALU.add),
                     reads=[("Yb", s2), "prm0", "prm1"], writes=[("Yb", s2)])
                P.op("pool", lambda e, s2=s2, bs=bs: e.tensor_tensor(out=Yb[s2][:], in0=Yb[s2][:], in1=bon[:, bs], op=ALU.add),
                     reads=[("Yb", s2), "bon"], writes=[("Yb", s2)])
                P.op("dve", lambda e, s2=s2, bs=bs: e.tensor_tensor(out=Ob[s2][:], in0=Yb[s2][:], in1=gg[:, bs], op=ALU.mult),
                     reads=[("Yb", s2), "gg"], writes=[("Ob", s2)])
                for q in range(4):
                    P.op("pe", lambda e, s2=s2, q=q: e.transpose(out=pst[s2][:, q, :], in_=Ob[s2][:, q * 128:(q + 1) * 128],
                                                                 identity=ident[:]),
                         reads=[("Ob", s2), "ident"], writes=[("p4dt", s2)])
                P.op("act", lambda e, s2=s2: e.copy(out=Tk[s2][:], in_=pst[s2][:]), reads=[("p4dt", s2)], writes=[("Tk", s2)])
                P.dma("sp", K.ro_loc_d[blk // 4].rearrange("(t p) c -> p t c", p=128)[:, (blk % 4) * 4:(blk % 4 + 1) * 4, c0:c0 + 128], Tk[s2][:],
                      reads=[("Tk", s2)], writes=[("ro_tok_d", ct, blk)])
        P.flush()


def phase4e_allgather(K):
    P = K.P
    for hh in range(2):
        P.coll(lambda e, hh=hh: e.collective_compute("AllGather", ALU.bypass, replica_groups=[[0, 1, 2, 3], [4, 5, 6, 7]],
                                                     ins=[K.ro_loc_d[hh].opt()], outs=[K.ro_all_d[hh].opt()]),
               reads=[("ro_loc", hh)], writes=[("ro_all", hh)])
    P.flush()


def phase5a_select(K):
    nc, P = K.nc, K.P
    with contextlib.ExitStack() as st:
        def sb(name, shape, dt):
            return st.enter_context(nc.sbuf_tensor(name, shape, dt))
        ro = sb("ro_tok", [128, 32, 1024], BF16)
        selT = sb("selT", [128, 32, 1024], BF16)
        qrow = sb("qrow", [128, 1024], F32)
        tki = sb("tki", [128, 32], I32)
        tkf = sb("tkf", [128, 32], F32)
        mo = [sb("mo%d" % i, [128, 512], BF16) for i in range(2)]
        at = sb("at5", [128, 8, 1024], BF16)
        ps = [st.enter_context(nc.psum_tensor("p5a_%d" % i, [128, 512], F32)) for i in range(2)]
        for q4 in range(4):
            for hh in range(2):
                P.dma("sp", ro[:, hh * 16:(hh + 1) * 16, q4 * 256:(q4 + 1) * 256],
                      K.ro_all_d[hh][q4 * 2048:(q4 + 1) * 2048, :].rearrange("(t p) c -> p t c", p=128), writes=[("ro", q4, hh)])
        rok = [("ro", q4, hh) for q4 in range(4) for hh in range(2)]
        P.dma("sp", qrow[:], bcast_rows(K.qpos_row, 1024), writes=["qrow"])
        P.op("pool", lambda e: e.iota(tki[:], pattern=[[128, 32]], base=0, channel_multiplier=1), writes=["tki"])
        P.op("dve", lambda e: e.tensor_copy(out=tkf[:], in_=tki[:]), reads=["tki"], writes=["tkf"])
        for T in range(32):
            P.op("dve", lambda e, T=T: e.tensor_scalar(out=selT[:, T, :], in0=qrow[:], scalar1=tkf[:, T:T + 1], scalar2=0.0,
                                                      op0=ALU.is_equal, op1=ALU.add), reads=["qrow", "tkf"], writes=[("selT", T)])
        sk = [("selT", T) for T in range(32)]
        P.dma("sp", at[:], K.attT_d.rearrange("h p t -> p h t"), writes=["at5"])
        P.dma("sp", K.mixT_d.rearrange("k p t -> p k t")[:, 0:8, :], at[:], reads=["at5"], writes=["mixa"])
        i = 0
        for m in range(8):
            for half in range(2):
                s2 = i % 2
                i += 1
                for T in range(32):
                    P.op("pe", lambda e, T=T, m=m, half=half, s2=s2: e.matmul(
                        ps[s2][:, :], lhsT=ro[:, T, m * 128:(m + 1) * 128], rhs=selT[:, T, half * 512:(half + 1) * 512],
                        start=(T == 0), stop=(T == 31)), reads=rok + sk, writes=[("p5a", s2)])
                P.op("act", lambda e, s2=s2: e.copy(out=mo[s2][:], in_=ps[s2][:, :]), reads=[("p5a", s2)], writes=[("mo", s2)])
                P.dma("sp", K.mixT_d[8 + m, :, half * 512:(half + 1) * 512], mo[s2][:], reads=[("mo", s2)], writes=[("mixr", m, half)])
        P.flush()


def phase5b_outproj(K):
    nc, P = K.nc, K.P
    with contextlib.ExitStack() as st:
        def sb(name, shape, dt):
            return st.enter_context(nc.sbuf_tensor(name, shape, dt))
        ident = sb("ident5", [128, 128], BF16)
        make_ident(K, P, ident)
        G2, SH2 = load_G_SH(K, P, st, 3, 4, K.norm2_g, "p5")
        GT1 = sb("GT1", [128, D], F32)
        P.dma("sp", GT1[:], bcast_rows(K.mod_d[2 * D:3 * D], D), writes=["GT1"])
        Wo = sb("Wo", [128, 16, D], BF16)
        stg = [sb("wstg5_%d" % i, [128, 4, 512], F32) for i in range(2)]
        wk = load_weight_bf16(K, P, stg, Wo, 0, K.w_out, D, "Wo")
        mixT = sb("mixT", [128, 16, 512], BF16)
        T = norm_tiles_alloc(K, st, "p5")
        x1 = T["xt"]
        hT = [sb("hT5_0", [128, 16, 512], BF16)] * 2
        xo = [sb("xo%d" % i, [128, D], F32) for i in range(2)]
        ps = [st.enter_context(nc.psum_tensor("p5b_%d" % i, [128, 512], F32)) for i in range(2)]
        ss, junk, hb, pT = T["ss"], T["junk"], T["hb"], T["pT"]
        gi = 0
        for blk in range(2):
            hs = 0
            P.dma("sp", mixT[:], K.mixT_d.rearrange("k p t -> p k t")[:, :, blk * 512:(blk + 1) * 512], writes=["mixT"])
            for ti in range(4):
                t = blk * 4 + ti
                xs = t % 2
                P.dma("sp", xo[xs][:], K.x_own[t * 128:(t + 1) * 128, :], writes=[("xo", xs)])
                for cg in range(4):
                    b = gi % 2
                    gi += 1
                    for k in range(16):
                        P.op("pe", lambda e, b=b, k=k, t=t, cg=cg: e.matmul(
                            ps[b][:, :], lhsT=mixT[:, k, (t % 4) * 128:(t % 4 + 1) * 128], rhs=Wo[:, k, cg * 512:(cg + 1) * 512],
                            start=(k == 0), stop=(k == 15)), reads=["mixT", ("Wo", cg * 512, 4 * (k // 4))], writes=[("p5b", b)])
                    cs = slice(cg * 512, (cg + 1) * 512)
                    P.op("dve", lambda e, b=b, xs=xs, cs=cs: e.tensor_tensor(out=x1[xs][:, cs], in0=ps[b][:, :], in1=GT1[:, cs], op=ALU.mult),
                         reads=[("p5b", b), "GT1"], writes=[("xt", xs)])
                    P.op("pool", lambda e, xs=xs, cs=cs: e.tensor_tensor(out=x1[xs][:, cs], in0=x1[xs][:, cs], in1=xo[xs][:, cs], op=ALU.add),
                         reads=[("xt", xs), ("xo", xs)], writes=[("xt", xs)])
                P.dma("sp", K.x1_d[t * 128:(t + 1) * 128, :], x1[xs][:], reads=[("xt", xs)], writes=[("x1_d", t)])
                P.op("act", lambda e, xs=xs: e.activation(out=junk[:], in_=x1[xs][:], func=AF.Square, accum_out=ss[:, 0:1]),
                     reads=[("xt", xs)], writes=["junk", "ss0"])
                P.op("dve", lambda e: e.tensor_scalar(out=ss[:, 1:2], in0=ss[:, 0:1], scalar1=1.0 / D, scalar2=1e-6,
                                                       op0=ALU.mult, op1=ALU.add), reads=["ss0"], writes=["ss1"])
                P.op("act", lambda e: e.activation(out=ss[:, 2:3], in_=ss[:, 1:2], func=AF.Sqrt), reads=["ss1"], writes=["ss2"])
                P.op("dve", lambda e: e.reciprocal(out=ss[:, 3:4], in_=ss[:, 2:3]), reads=["ss2"], writes=["ss3"])
                P.op("dve", lambda e, xs=xs: e.scalar_tensor_tensor(out=x1[xs][:], in0=x1[xs][:], scalar=ss[:, 3:4], in1=G2[:],
                                                                   op0=ALU.mult, op1=ALU.mult),
                     reads=[("xt", xs), "ss3", "G"], writes=[("xt", xs)])
                P.op("pool", lambda e, xs=xs: e.tensor_tensor(out=hb[xs][:], in0=x1[xs][:], in1=SH2[:], op=ALU.add),
                     reads=[("xt", xs), "SH"], writes=[("hb", xs)])
                for half in range(2):
                    for kk in range(8):
                        k = half * 8 + kk
                        P.op("pe", lambda e, k=k, kk=kk, half=half, xs=xs: e.transpose(
                            out=pT[half][:, kk, :], in_=hb[xs][:, k * 128:(k + 1) * 128], identity=ident[:]),
                            reads=[("hb", xs), "ident"], writes=[("pT", half)])
                    o_ = hT[hs][:, half * 8:(half + 1) * 8, ti * 128:(ti + 1) * 128]
                    if half == 0:
                        P.op("act", lambda e, o_=o_, half=half: e.copy(out=o_, in_=pT[half][:]), reads=[("pT", half)], writes=[("hT5", hs, ti, half)])
                    else:
                        P.op("dve", lambda e, o_=o_, half=half: e.tensor_copy(out=o_, in_=pT[half][:]), reads=[("pT", half)], writes=[("hT5", hs, ti, half)])
            P.dma("sp", K.h2T_d.rearrange("k p t -> p k t")[:, :, blk * 512:(blk + 1) * 512], hT[hs][:],
                  reads=[("hT5", hs, ti, half) for ti in range(4) for half in range(2)], writes=[("h2T_d", blk)])
        P.flush()


def phase5c_ffn(K):
    nc, P = K.nc, K.P
    NF = 5632 // 128
    with contextlib.ExitStack() as st:
        def sb(name, shape, dt):
            return st.enter_context(nc.sbuf_tensor(name, shape, dt))
        h2T = sb("h2T", [128, 16, OWN], BF16)
        P.dma("sp", h2T[:], K.h2T_d.rearrange("k p t -> p k t"), writes=["h2T"])
        ao = [sb("ao%d" % i, [128, 512], BF16) for i in range(2)]
        stg = [sb("wstg6_%d" % i, [128, 4, 512], F32) for i in range(4)]
        Wg = [sb("Wg%d" % i, [128, 16, 512], BF16) for i in range(2)]
        Wu = [sb("Wu%d" % i, [128, 16, 512], BF16) for i in range(2)]
        sg = [sb("sg%d" % i, [128, 512], F32) for i in range(2)]
        ps = [st.enter_context(nc.psum_tensor("p5c_%d" % i, [128, 512], F32)) for i in range(4)]
        gi = 0

        def load_group(fg, defer=None):
            ws = fg % 2
            load_weight_bf16(K, P, stg, Wg[ws], 0, K.w_ffn_gate[:, fg * 512:(fg + 1) * 512], 512, ("Wg", ws), defer=defer)
            load_weight_bf16(K, P, stg, Wu[ws], 0, K.w_ffn_up[:, fg * 512:(fg + 1) * 512], 512, ("Wu", ws), defer=defer)
        load_group(0)
        for fg in range(11):
            ws = fg % 2
            pend = []
            if fg + 1 < 11:
                load_group(fg + 1, defer=pend)
            for f4 in range(4):
                f = fg * 4 + f4
                for tb in range(2):
                    b = gi % 2
                    gi += 1
                    if pend:
                        pend.pop(0)()
                    for k in range(16):
                        P.op("pe", lambda e, b=b, k=k, f4=f4, tb=tb, ws=ws: e.matmul(
                            ps[b][:, :], lhsT=Wg[ws][:, k, f4 * 128:(f4 + 1) * 128], rhs=h2T[:, k, tb * 512:(tb + 1) * 512],
                            start=(k == 0), stop=(k == 15)), reads=["h2T", (("Wg", ws), 0, (k // 4) * 4)], writes=[("p5c", b)])
                    for k in range(16):
                        P.op("pe", lambda e, b=b, k=k, f4=f4, tb=tb, ws=ws: e.matmul(
                            ps[2 + b][:, :], lhsT=Wu[ws][:, k, f4 * 128:(f4 + 1) * 128], rhs=h2T[:, k, tb * 512:(tb + 1) * 512],
                            start=(k == 0), stop=(k == 15)), reads=["h2T", (("Wu", ws), 0, (k // 4) * 4)], writes=[("p5c", 2 + b)])
                    P.op("act", lambda e, b=b: e.activation(out=sg[b][:], in_=ps[b][:, :], func=AF.Silu),
                         reads=[("p5c", b)], writes=[("sg", b)])
                    P.op("dve", lambda e, b=b: e.tensor_tensor(out=ao[b][:], in0=ps[2 + b][:, :], in1=sg[b][:], op=ALU.mult),
                         reads=[("p5c", 2 + b), ("sg", b)], writes=[("ao", b)])
                    P.dma("sp", K.actT_d[f, :, tb * 512:(tb + 1) * 512], ao[b][:], reads=[("ao", b)], writes=[("actT_d", f, tb)])
        P.flush()
    with contextlib.ExitStack() as st:
        def sb(name, shape, dt):
            return st.enter_context(nc.sbuf_tensor(name, shape, dt))
        GT2 = sb("GT2", [128, D], F32)
        P.dma("sp", GT2[:], bcast_rows(K.mod_d[5 * D:6 * D], D), writes=["GT2"])
        actT = sb("actT", [128, NF, OWN], BF16)
        for q in range(4):
            P.dma("sp", actT[:, q * 11:(q + 1) * 11, :], K.actT_d.rearrange("f p t -> p f t")[:, q * 11:(q + 1) * 11, :], writes=[("actT", q)])
        ak = [("actT", q) for q in range(4)]
        stg = [sb("wstg7_%d" % i, [128, 4, 256], F32) for i in range(4)]
        ps = [st.enter_context(nc.psum_tensor("p5d_%d" % i, [128, 512], F32)) for i in range(2)]
        gi = 0
        Wd = [sb("Wd%d" % i, [128, NF, 256], BF16) for i in range(2)]
        x1 = [sb("x1_%d" % i, [128, 256], F32) for i in range(2)]
        yo = [sb("yo%d" % i, [128, 256], F32) for i in range(2)]
        wdv = K.w_ffn_down.rearrange("(k p) n -> p k n", p=128)
        engs = ["pool", "dve", "act"]

        def load_wd(cg, defer=None):
            wsl = cg % 2
            for k0 in range(0, NF, 4):
                if defer is not None:
                    defer.append(lambda k0=k0: load_wd_piece(cg, wsl, k0))
                else:
                    load_wd_piece(cg, wsl, k0)

        def load_wd_piece(cg, wsl, k0):
            if True:
                i = K.wcnt
                K.wcnt += 1
                sl = i % 4
                P.dma("sp", stg[sl][:, 0:4, 0:256], wdv[:, k0:k0 + 4, cg * 256:(cg + 1) * 256], writes=[("wstg", sl)])
                eng = engs[i % 3]
                o_ = Wd[wsl][:, k0:k0 + 4, :]
                if eng == "act":
                    P.op("act", lambda e, o_=o_, sl=sl: e.copy(out=o_, in_=stg[sl][:, 0:4, 0:256]), reads=[("wstg", sl)], writes=[("Wd", wsl, k0)])
                else:
                    P.op(eng, lambda e, o_=o_, sl=sl: e.tensor_copy(out=o_, in_=stg[sl][:, 0:4, 0:256]), reads=[("wstg", sl)], writes=[("Wd", wsl, k0)])
        load_wd(0)
        for cg in range(8):
            wsl = cg % 2
            cs = slice(cg * 256, (cg + 1) * 256)
            pend = []
            if cg + 1 < 8:
                load_wd(cg + 1, defer=pend)
            for t in range(8):
                b = gi % 2
                gi += 1
                if cg == 0 and t == 0:
                    P.dma("sp", x1[b][:], K.x1_d[0:128, cs], writes=[("x1", b)])
                nt_, ncg_ = (t + 1, cg) if t + 1 < 8 else (0, cg + 1)
                if ncg_ < 8:
                    P.dma("sp", x1[1 - b][:], K.x1_d[nt_ * 128:(nt_ + 1) * 128, ncg_ * 256:(ncg_ + 1) * 256], writes=[("x1", 1 - b)])
                for _ in range(2):
                    if pend:
                        pend.pop(0)()
                for f in range(NF):
                    P.op("pe", lambda e, b=b, f=f, t=t, wsl=wsl: e.matmul(ps[b][:, 0:256], lhsT=actT[:, f, t * 128:(t + 1) * 128], rhs=Wd[wsl][:, f, :],
                                                                          start=(f == 0), stop=(f == NF - 1)),
                         reads=[("actT", f // 11), ("Wd", wsl, (f // 4) * 4)], writes=[("p5c", b)])
                P.op("dve", lambda e, b=b, cs=cs: e.tensor_tensor(out=yo[b][:], in0=ps[b][:, 0:256], in1=GT2[:, cs], op=ALU.mult),
                     reads=[("p5c", b), "GT2"], writes=[("yo", b)])
                P.op("pool", lambda e, b=b: e.tensor_tensor(out=yo[b][:], in0=yo[b][:], in1=x1[b][:], op=ALU.add),
                     reads=[("yo", b), ("x1", b)], writes=[("yo", b)])
                P.dma("sp", K.out[t * 128:(t + 1) * 128, cs], yo[b][:], reads=[("yo", b)], writes=[("out", t, cg)])
        P.flush()


def phase_final_copy(K):
    nc, P = K.nc, K.P
    with contextlib.ExitStack() as st:
        xt = [st.enter_context(nc.sbuf_tensor("fx%d" % i, [128, D], F32)) for i in range(2)]
        for t in range(8):
            s = t % 2
            P.dma("sp", xt[s][:], K.x_own[t * 128:(t + 1) * 128, :], writes=[("fx", s)])
            P.dma("sp", K.out[t * 128:(t + 1) * 128, :], xt[s][:], reads=[("fx", s)], writes=[("out", t)])
        P.flush()


def own_tiles(j):
    r = []
    for m in range(4):
        r += [8 * m + j, 8 * m + 7 - j]
    return r


def build_program(debug=False, stages=99, cts=range(2), dbg_list=None, skip_att=False):
    nc = bass.Bass("TRN2", target_bir_lowering=False)
    K = Ctx()
    K.stages = stages
    K.cts = cts
    K.skip_att = skip_att
    K.nc = nc
    K.dbg = {}
    K.wcnt = 0

    def inp(name, shape, dt=F32):
        return nc.dram_tensor(name, list(shape), dt, kind="ExternalInput").ap()

    def scratch(name, shape, dt):
        return nc.dram_tensor(name, list(shape), dt, kind="Internal").ap()

    K.x_full = inp("x_full", [S, D])
    K.x_own = inp("x_own", [OWN, D])
    K.c_arr = inp("c_arr", [128, 16])
    K.pos_full = inp("pos_full", [128, 32], I32)
    K.invf_att = inp("invf_att", [128, 16])
    K.invf_idx = inp("invf_idx", [128, 8])
    K.w_ada = inp("w_ada", [D, 3072])
    K.b_ada = inp("b_ada", [3072])
    K.norm1_g = inp("norm1_g", [D])
    K.k_norm_g = inp("k_norm_g", [128])
    K.q_norm_g = inp("q_norm_g", [128])
    K.pos_own = inp("pos_own", [128, 8], I32)
    K.qpos_own = inp("qpos_own", [128, 8])
    K.w_in = inp("w_in", [D, 4176])
    K.rw_prm = [inp("rwp%d" % i, [128, 2]) for i in range(10)]
    K.w_in_rw = inp("w_in_rw", [D, 1216])
    K.rw_mul = inp("rw_mul", [128, 4])
    K.rw_w_up = inp("rw_w_up", [96, 256])
    K.rw_a_up = inp("rw_a_up", [96, 256])
    K.rw_g_up = inp("rw_g_up", [256, 256])
    K.qpos_row = inp("qpos_row", [OWN])
    K.w_out = inp("w_out", [D, D])
    K.norm2_g = inp("norm2_g", [D])
    K.w_ffn_gate = inp("w_ffn_gate", [D, 5632])
    K.w_ffn_up = inp("w_ffn_up", [D, 5632])
    K.w_ffn_down = inp("w_ffn_down", [5632, D])
    K.out = nc.dram_tensor("y_own", [OWN, D], F32, kind="ExternalOutput").ap()
    K.modq_d = scratch("modq_d", [1, 3072], F32)
    K.mod4_d = scratch("mod4_d", [4, 3072], F32)
    K.mod_d = K.mod4_d.rearrange("a n -> (a n)")
    K.hT_d = scratch("hT_d", [16, 128, S], BF16)
    K.kT_d = scratch("kT_d", [8, 128, S], BF16)
    K.v_d = scratch("v_d", [S, 8 * 129], BF16)
    K.ikT_d = scratch("ikT_d", [64, S], BF16)
    K.yT_d = scratch("yT_d", [1216, S], F32)
    K.qT_d = scratch("qT_d", [8, 128, OWN], BF16)
    K.iqT_d = scratch("iqT_d", [64, OWN, 16], BF16)
    K.iw_d = scratch("iw_d", [OWN, 16], F32)
    K.attT_d = scratch("attT_d", [8, 128, OWN], BF16)
    for nm in ("vb_d", "G_d", "BON_d", "AH_d", "RH_d", "BH_d", "KH_d"):
        setattr(K, nm, scratch(nm, [256, S], BF16))
    K.PC_d = scratch("PC_d", [256, NCH], F32)
    K.oT_d = scratch("oT_d", [256, S], F32)
    K.ro_loc_d = [scratch("ro_loc%d_d" % i, [2048, 256], BF16) for i in range(2)]
    K.ro_all_d = [scratch("ro_all%d_d" % i, [8192, 256], BF16) for i in range(2)]
    K.mixT_d = scratch("mixT_d", [16, 128, OWN], BF16)
    K.x1_d = scratch("x1_d", [OWN, D], F32)
    K.h2T_d = scratch("h2T_d", [16, 128, OWN], BF16)
    K.actT_d = scratch("actT_d", [44, 128, OWN], BF16)
    with contextlib.ExitStack() as stack:
        K.P = Prog(nc, stack)
        phase0_adaln(K)
        phase1_kv(K)
        if K.stages >= 2:
            phase1b_rwkv_proj(K)
        if K.stages >= 3 and not getattr(K, "skip_att", False):
            phase2_own_proj(K)
            phase3_attention(K)
        if K.stages >= 4:
            phase4b_rwkv_prep(K, cts=K.cts)
            if K.stages >= 5:
                phase4c_rwkv_scan(K, heads=[h for ct in K.cts for h in (2 * ct, 2 * ct + 1)])
        if K.stages >= 6:
            phase4d_rwkv_post(K, cts=K.cts)
            phase4e_allgather(K)
        if K.stages >= 7:
            phase5a_select(K)
            phase5b_outproj(K)
            phase5c_ffn(K)
        else:
            phase_final_copy(K)
        if debug:
            P = K.P
            allc = (("dbg_mixT", K.mixT_d, [16, 128, OWN], BF16), ("dbg_x1", K.x1_d, [OWN, D], F32),
                    ("dbg_oT", K.oT_d, [256, S], F32), ("dbg_AH", K.AH_d, [256, S], BF16), ("dbg_BH", K.BH_d, [256, S], BF16),
                    ("dbg_KH", K.KH_d, [256, S], BF16), ("dbg_RH", K.RH_d, [256, S], BF16), ("dbg_PC", K.PC_d, [256, NCH], F32),
                    ("dbg_G", K.G_d, [256, S], BF16), ("dbg_BON", K.BON_d, [256, S], BF16), ("dbg_vb", K.vb_d, [256, S], BF16),
                    ("dbg_yT", K.yT_d, [1216, S], F32), ("dbg_attT", K.attT_d, [8, 128, OWN], BF16),
                                     ("dbg_qT", K.qT_d, [8, 128, OWN], BF16), ("dbg_iqT", K.iqT_d, [64, OWN, 16], BF16),
                                     ("dbg_iw", K.iw_d, [OWN, 16], F32))
            for nm, src, shp, dt in allc:
                if dbg_list is not None and nm not in dbg_list:
                    continue
                o = dbg_out(K, nm, shp, dt)
                P.dma("sp", o, src, writes=[nm])
            P.flush()
    return nc, K


def make_in_maps(inputs, cores=range(8)):
    x = np.asarray(inputs["x"], dtype=np.float32)
    c = np.asarray(inputs["c"], dtype=np.float32)
    pos = np.asarray(inputs["positions"], dtype=np.int32)
    invf_att = (np.float32(500000.0) ** (-np.arange(16, dtype=np.float32) / np.float32(16))).astype(np.float32)
    invf_idx = (np.float32(500000.0) ** (-np.arange(8, dtype=np.float32) / np.float32(8))).astype(np.float32)
    mu = np.asarray(inputs["rwkv_mu"][0], dtype=np.float32)

    vecs = [mu[0:1024], mu[1024:2048], mu[2048:3072], inputs["rwkv_w0"][0], inputs["rwkv_a0"][0], inputs["rwkv_k_k"][0],
            inputs["rwkv_k_a"][0], np.asarray(inputs["rwkv_r_k"][0]).reshape(-1), inputs["rwkv_lnx_g"][0], inputs["rwkv_lnx_b"][0]]
    w_in_full = np.asarray(inputs["w_in"][0], dtype=np.float32)
    rw_mul = np.zeros((128, 4), np.float32)
    rw_mul[:96, 0] = mu[3072:3168]
    rw_mul[:96, 1] = mu[3168:3264]
    rw_mul[:, 2] = mu[3264:3392]
    rw_mul[:, 3] = mu[3392:3520]
    maps = []
    for core in cores:
        b, j = core // 4, core % 4
        ch = slice(256 * j, 256 * j + 256)
        rwp = {"rwp%d" % i: np.ascontiguousarray(np.asarray(v, dtype=np.float32)[ch].reshape(2, 128).T) for i, v in enumerate(vecs)}
        R0 = 4176
        w_in_rw = np.ascontiguousarray(np.concatenate([w_in_full[:, R0 + 256 * j:R0 + 256 * j + 256],
                                                       w_in_full[:, R0 + 1024 + 256 * j:R0 + 1024 + 256 * j + 256],
                                                       w_in_full[:, R0 + 2048 + 256 * j:R0 + 2048 + 256 * j + 256],
                                                       w_in_full[:, R0 + 3072:R0 + 3520]], axis=1))
        tiles = own_tiles(j)
        idx = np.concatenate([np.arange(t * 128, (t + 1) * 128) for t in tiles])
        maps.append({
            "x_full": np.ascontiguousarray(x[b]),
            "x_own": np.ascontiguousarray(x[b][idx]),
            "c_arr": np.ascontiguousarray(c[b].reshape(16, 128).T),
            "pos_full": np.ascontiguousarray(pos[b].reshape(32, 128).T),
            "invf_att": np.ascontiguousarray(np.broadcast_to(invf_att, (128, 16))),
            "invf_idx": np.ascontiguousarray(np.broadcast_to(invf_idx, (128, 8))),
            "w_ada": np.ascontiguousarray(np.asarray(inputs["w_ada"][0], dtype=np.float32)[:, 3072 * j:3072 * (j + 1)]),
            "b_ada": np.ascontiguousarray(np.asarray(inputs["b_ada"][0], dtype=np.float32)[3072 * j:3072 * (j + 1)]),
            "norm1_g": np.asarray(inputs["norm1_g"][0], dtype=np.float32),
            "k_norm_g": np.asarray(inputs["k_norm_g"][0], dtype=np.float32),
            "q_norm_g": np.asarray(inputs["q_norm_g"][0], dtype=np.float32),
            "pos_own": np.ascontiguousarray(pos[b][idx].reshape(8, 128).T),
            "qpos_own": np.ascontiguousarray(idx.astype(np.float32).reshape(8, 128).T),
            "w_in": np.ascontiguousarray(w_in_full[:, 0:4176]),
            "qpos_row": idx.astype(np.float32),
            "w_out": np.asarray(inputs["w_out"][0], dtype=np.float32),
            "norm2_g": np.asarray(inputs["norm2_g"][0], dtype=np.float32),
            "w_ffn_gate": np.asarray(inputs["w_ffn_gate"][0], dtype=np.float32),
            "w_ffn_up": np.asarray(inputs["w_ffn_up"][0], dtype=np.float32),
            "w_ffn_down": np.asarray(inputs["w_ffn_down"][0], dtype=np.float32),
            "rw_w_up": np.ascontiguousarray(np.asarray(inputs["rwkv_w_up"][0], dtype=np.float32)[:, ch]),
            "rw_a_up": np.ascontiguousarray(np.asarray(inputs["rwkv_a_up"][0], dtype=np.float32)[:, ch]),
            "rw_g_up": np.ascontiguousarray(np.asarray(inputs["rwkv_g_up"][0], dtype=np.float32)[:, ch]),
            "w_in_rw": w_in_rw,
            "rw_mul": rw_mul,
            **rwp,
        })
    return maps


def kernel(**inputs):
    nc, K = build_program(debug=False)
    maps = make_in_maps(inputs)
    res = run_bass_kernel_spmd(nc, maps, core_ids=list(range(8)))
    out = np.zeros((2, S, D), dtype=np.float32)
    for core in range(8):
        b, j = core // 4, core % 4
        y = res.results[core]["y_own"]
        for i, t in enumerate(own_tiles(j)):
            out[b, t * 128:(t + 1) * 128] = y[i * 128:(i + 1) * 128]
    return out
```
